# Optimizing a Trainium2 kernel written in Bass

```python
import math
import jax, jax.numpy as jnp
from jax import lax
import numpy as np

D_MODEL = 1024
BATCH = 8
SEQ = 2048
DEPTH = 1
DEC_BATCH = 128
DEC_SEQ = 1
PAST_LEN = 16384
PAGE_SIZE = 128

HEAD_DIM = 64
ML_HEADS = D_MODEL // (2 * HEAD_DIM)
RW_HEADS = D_MODEL // (2 * HEAD_DIM)
ML_W = ML_HEADS * HEAD_DIM
RW_W = RW_HEADS * HEAD_DIM
MIX_W = ML_W + RW_W
CONV_W = 4
MLSTM_CHUNK = 64
D_DECAY_LORA = 64
D_AAA_LORA = 64
D_GATE_LORA = 128
D_FF = 4 * D_MODEL
EPS = 1e-6
GN_EPS = 64e-5
ML_IN_W = 4 * ML_W + 2 * ML_HEADS
RW_IN_W = 3 * RW_W + D_DECAY_LORA + D_AAA_LORA + D_GATE_LORA
IN_W = ML_IN_W + RW_IN_W

kernel_name = "hymba_mlstm_rwkv7_decode_step"


def rms_norm(x, w):
    xf = x.astype(jnp.float32)
    y = xf * lax.rsqrt(jnp.mean(xf * xf, axis=-1, keepdims=True) + EPS)
    return (y * w.astype(jnp.float32)).astype(x.dtype)


def head_rms_norm(h, w, n_heads):
    B, T, W = h.shape
    hf = h.astype(jnp.float32).reshape(B, T, n_heads, W // n_heads)
    hf = hf * lax.rsqrt(jnp.mean(hf * hf, axis=-1, keepdims=True) + EPS)
    return (hf.reshape(B, T, W) * w.astype(jnp.float32)).astype(h.dtype)


def head_group_norm(y, w, b, n_heads):
    B, T, W = y.shape
    yf = y.astype(jnp.float32).reshape(B, T, n_heads, W // n_heads)
    mu = jnp.mean(yf, axis=-1, keepdims=True)
    d = yf - mu
    yf = d * lax.rsqrt(jnp.mean(d * d, axis=-1, keepdims=True) + GN_EPS)
    return (yf.reshape(B, T, W) * w.astype(jnp.float32) + b.astype(jnp.float32)).astype(y.dtype)


def causal_conv(u, buf, w, b):
    T = u.shape[1]
    ext = jnp.concatenate([buf.astype(u.dtype), u], axis=1)
    out = b
    for j in range(CONV_W):
        out = out + ext[:, j:j + T] * w[j]
    return out, ext[:, -(CONV_W - 1):]


def token_shift(p, prev, mu):
    shifted = jnp.concatenate([prev.astype(p.dtype), p[:, :-1]], axis=1)
    return p + mu * (shifted - p), p[:, -1:]


def mlstm_chunkwise(q, k, v, i_pre, logf, C0, n0, m0):
    B, T, H, Dh = q.shape
    L = math.gcd(T, MLSTM_CHUNK)
    NC = T // L
    f32 = lambda a: a.astype(jnp.float32)

    def to_chunks(a):
        return jnp.moveaxis(f32(a).reshape((B, NC, L) + a.shape[2:]), 1, 0)

    xs = tuple(to_chunks(a) for a in (q, k, v, i_pre, logf))
    causal = jnp.tril(jnp.ones((L, L), dtype=bool))[None, :, :, None]

    def step(carry, blk):
        C, n, m = carry
        qb, kb, vb, ib, fb = blk
        bcum = jnp.cumsum(fb, axis=1)
        dlog = bcum[:, :, None, :] - bcum[:, None, :, :] + ib[:, None, :, :]
        dlog = jnp.where(causal, dlog, -jnp.inf)
        inter = bcum + m[:, None, :]
        m_t = jnp.maximum(inter, jnp.max(dlog, axis=2))
        wts = jnp.exp(dlog - m_t[:, :, None, :])
        sc_inter = jnp.exp(inter - m_t)
        qk = jnp.einsum('bthd,bshd->btsh', qb, kb) * wts
        num = jnp.einsum('btsh,bshd->bthd', qk, vb) + sc_inter[..., None] * jnp.einsum('bthk,bhkv->bthv', qb, C)
        den = jnp.sum(qk, axis=2) + sc_inter * jnp.einsum('bthk,bhk->bth', qb, n)
        h = num / jnp.maximum(jnp.abs(den), jnp.exp(-m_t))[..., None]
        b_end = bcum[:, -1]
        g = b_end[:, None, :] - bcum + ib
        m_new = jnp.maximum(b_end + m, jnp.max(g, axis=1))
        ws = jnp.exp(g - m_new[:, None, :])
        dec = jnp.exp(b_end + m - m_new)
        C_new = dec[..., None, None] * C + jnp.einsum('bsh,bshk,bshv->bhkv', ws, kb, vb)
        n_new = dec[..., None] * n + jnp.einsum('bsh,bshk->bhk', ws, kb)
        return (C_new, n_new, m_new), h

    (C, n, m), hc = lax.scan(step, (f32(C0), f32(n0), f32(m0)), xs)
    h = jnp.moveaxis(hc, 0, 1).reshape(B, T, H, Dh)
    return h, C, n, m


def rwkv7_scan(r, w_log, k, v, kk, a, S0):
    def step(S, xs):
        rt, wt, kt, vt, kkt, at = xs
        sa = jnp.einsum('bhvk,bhk->bhv', S, -kkt)
        S = (S * jnp.exp(wt)[:, :, None, :]
             + sa[..., None] * (kkt * at)[:, :, None, :]
             + vt[..., None] * kt[:, :, None, :])
        return S, jnp.einsum('bhvk,bhk->bhv', S, rt)

    xs = tuple(jnp.moveaxis(t, 1, 0) for t in (r, w_log, k, v, kk, a))
    S, ys = lax.scan(step, S0.astype(jnp.float32), xs)
    return jnp.moveaxis(ys, 0, 1), S


def layer_forward(x, C0, n0, m0, conv0, S0, shift0,
                  norm_mix_w, w_in, mlstm_conv_w, mlstm_conv_b, mlstm_i_b, mlstm_f_b, mlstm_norm_w,
                  rw_mu, rw_w0, rw_w_up, rw_a0, rw_a_up, rw_g_up, rw_k_k, rw_k_a, rw_r_k,
                  rw_ln_w, rw_ln_b, w_out, norm_mlp_w, mlp_up, mlp_down):
    B, T, _ = x.shape
    dt = x.dtype
    xn = rms_norm(x, norm_mix_w)
    proj = xn @ w_in
    ml = proj[..., :ML_IN_W]
    rw = proj[..., ML_IN_W:]

    qk_pre = ml[..., :2 * ML_W]
    v_ml = ml[..., 2 * ML_W:3 * ML_W].reshape(B, T, ML_HEADS, HEAD_DIM)
    o_pre = ml[..., 3 * ML_W:4 * ML_W]
    i_pre = ml[..., 4 * ML_W:4 * ML_W + ML_HEADS].astype(jnp.float32) + mlstm_i_b.astype(jnp.float32)
    f_pre = ml[..., 4 * ML_W + ML_HEADS:].astype(jnp.float32) + mlstm_f_b.astype(jnp.float32)
    qk_c, conv_new = causal_conv(qk_pre, conv0, mlstm_conv_w, mlstm_conv_b)
    qk_c = jax.nn.silu(qk_c)
    q_ml = qk_c[..., :ML_W].reshape(B, T, ML_HEADS, HEAD_DIM)
    k_ml = qk_c[..., ML_W:].reshape(B, T, ML_HEADS, HEAD_DIM) * (HEAD_DIM ** -0.5)
    h_til, C_new, n_new, m_new = mlstm_chunkwise(q_ml, k_ml, v_ml, i_pre, jax.nn.log_sigmoid(f_pre), C0, n0, m0)
    h_ml = jax.nn.sigmoid(o_pre) * h_til.reshape(B, T, ML_W).astype(dt)
    h_ml = head_rms_norm(h_ml, mlstm_norm_w, ML_HEADS)

    rs, shift_new = token_shift(rw, shift0, rw_mu)
    r = rs[..., :RW_W]
    kr = rs[..., RW_W:2 * RW_W]
    vr = rs[..., 2 * RW_W:3 * RW_W]
    o = 3 * RW_W
    xw = rs[..., o:o + D_DECAY_LORA]
    o += D_DECAY_LORA
    xa = rs[..., o:o + D_AAA_LORA]
    o += D_AAA_LORA
    xg = rs[..., o:o + D_GATE_LORA]
    w_raw = (rw_w0 + jnp.tanh(xw) @ rw_w_up).astype(jnp.float32)
    w_log = -jnp.exp(-jax.nn.softplus(-w_raw) - 0.5)
    a = jax.nn.sigmoid((rw_a0 + xa @ rw_a_up).astype(jnp.float32))
    g = jax.nn.sigmoid(xg) @ rw_g_up
    kf = kr.astype(jnp.float32)
    kk = (kf * rw_k_k.astype(jnp.float32)).reshape(B, T, RW_HEADS, HEAD_DIM)
    kk = kk / jnp.maximum(jnp.sqrt(jnp.sum(kk * kk, axis=-1, keepdims=True)), 1e-12)
    k_eff = kf * (1.0 + (a - 1.0) * rw_k_a.astype(jnp.float32))
    heads = lambda t: t.astype(jnp.float32).reshape(B, T, RW_HEADS, HEAD_DIM)
    rh, kh, vh, wh, ah = heads(r), heads(k_eff), heads(vr), heads(w_log), heads(a)
    y, S_new = rwkv7_scan(rh, wh, kh, vh, kk, ah, S0)
    bonus = jnp.sum(rh * kh * rw_r_k.astype(jnp.float32), axis=-1, keepdims=True) * vh
    y_rw = (y + bonus).reshape(B, T, RW_W).astype(dt)
    y_rw = head_group_norm(y_rw, rw_ln_w, rw_ln_b, RW_HEADS) * g

    x = x + jnp.concatenate([h_ml, y_rw], axis=-1) @ w_out
    hid = jax.nn.relu(rms_norm(x, norm_mlp_w) @ mlp_up)
    x = x + jnp.square(hid) @ mlp_down
    sd = C0.dtype
    return (x, C_new.astype(sd), n_new.astype(sd), m_new.astype(sd), conv_new.astype(sd),
            S_new.astype(sd), shift_new.astype(sd))


def setup_inputs(seed: int = 0) -> dict:
    key = jax.random.key(seed)
    ks = iter(jax.random.split(key, 48))
    f32 = jnp.float32

    def nrm(shape, s):
        return jax.random.normal(next(ks), shape, f32) * s

    def unif(shape, lo, hi):
        return jax.random.uniform(next(ks), shape, f32, lo, hi)

    Ld = DEPTH
    f_bias = jnp.broadcast_to(jnp.linspace(3.0, 6.0, ML_HEADS, dtype=f32), (Ld, ML_HEADS))
    return {
        "x_prompt": nrm((BATCH, SEQ, D_MODEL), 1.0),
        "x_sample": nrm((DEC_BATCH, DEC_SEQ, D_MODEL), 1.0),
        "state_mlstm_C": nrm((Ld, DEC_BATCH, ML_HEADS, HEAD_DIM, HEAD_DIM), 0.1),
        "state_mlstm_n": nrm((Ld, DEC_BATCH, ML_HEADS, HEAD_DIM), 0.5),
        "state_mlstm_m": nrm((Ld, DEC_BATCH, ML_HEADS), 1.0),
        "state_mlstm_conv": nrm((Ld, DEC_BATCH, CONV_W - 1, 2 * ML_W), 1.0),
        "state_rwkv_S": nrm((Ld, DEC_BATCH, RW_HEADS, HEAD_DIM, HEAD_DIM), 0.2),
        "state_rwkv_shift": nrm((Ld, DEC_BATCH, 1, RW_IN_W), 1.0),
        "norm_mix_w": 1.0 + nrm((Ld, D_MODEL), 0.01),
        "w_in": nrm((Ld, D_MODEL, IN_W), D_MODEL ** -0.5),
        "mlstm_conv_w": nrm((Ld, CONV_W, 2 * ML_W), CONV_W ** -0.5),
        "mlstm_conv_b": nrm((Ld, 2 * ML_W), 0.01),
        "mlstm_i_b": nrm((Ld, ML_HEADS), 0.1),
        "mlstm_f_b": f_bias + nrm((Ld, ML_HEADS), 0.1),
        "mlstm_norm_w": 1.0 + nrm((Ld, ML_W), 0.01),
        "rw_mu": unif((Ld, RW_IN_W), 0.0, 1.0),
        "rw_w0": unif((Ld, RW_W), -6.0, -1.0),
        "rw_w_up": nrm((Ld, D_DECAY_LORA, RW_W), 0.1),
        "rw_a0": nrm((Ld, RW_W), 0.1),
        "rw_a_up": nrm((Ld, D_AAA_LORA, RW_W), 0.1),
        "rw_g_up": nrm((Ld, D_GATE_LORA, RW_W), D_GATE_LORA ** -0.5),
        "rw_k_k": 0.85 + nrm((Ld, RW_W), 0.05),
        "rw_k_a": 1.0 + nrm((Ld, RW_W), 0.05),
        "rw_r_k": nrm((Ld, RW_HEADS, HEAD_DIM), 0.1),
        "rw_ln_w": 1.0 + nrm((Ld, RW_W), 0.01),
        "rw_ln_b": nrm((Ld, RW_W), 0.01),
        "w_out": nrm((Ld, MIX_W, D_MODEL), MIX_W ** -0.5),
        "norm_mlp_w": 1.0 + nrm((Ld, D_MODEL), 0.01),
        "mlp_up": nrm((Ld, D_MODEL, D_FF), D_MODEL ** -0.5),
        "mlp_down": nrm((Ld, D_FF, D_MODEL), D_FF ** -0.5),
        "norm_f_w": 1.0 + nrm((D_MODEL,), 0.01),
    }


def reference(x_prompt, x_sample, state_mlstm_C, state_mlstm_n, state_mlstm_m, state_mlstm_conv,
              state_rwkv_S, state_rwkv_shift, norm_mix_w, w_in, mlstm_conv_w, mlstm_conv_b, mlstm_i_b,
              mlstm_f_b, mlstm_norm_w, rw_mu, rw_w0, rw_w_up, rw_a0, rw_a_up, rw_g_up, rw_k_k, rw_k_a,
              rw_r_k, rw_ln_w, rw_ln_b, w_out, norm_mlp_w, mlp_up, mlp_down, norm_f_w):
    Bp = x_prompt.shape[0]
    sd = state_mlstm_C.dtype
    xp, xs = x_prompt, x_sample
    pC, pn, pm, pconv, pS, pshift = [], [], [], [], [], []
    sC, sn, sm, sconv, sS, sshift = [], [], [], [], [], []
    for l in range(DEPTH):
        wl = (norm_mix_w[l], w_in[l], mlstm_conv_w[l], mlstm_conv_b[l], mlstm_i_b[l], mlstm_f_b[l],
              mlstm_norm_w[l], rw_mu[l], rw_w0[l], rw_w_up[l], rw_a0[l], rw_a_up[l], rw_g_up[l],
              rw_k_k[l], rw_k_a[l], rw_r_k[l], rw_ln_w[l], rw_ln_b[l], w_out[l], norm_mlp_w[l],
              mlp_up[l], mlp_down[l])
        outp = layer_forward(
            xp,
            jnp.zeros((Bp, ML_HEADS, HEAD_DIM, HEAD_DIM), sd),
            jnp.zeros((Bp, ML_HEADS, HEAD_DIM), sd),
            jnp.zeros((Bp, ML_HEADS), sd),
            jnp.zeros((Bp, CONV_W - 1, 2 * ML_W), sd),
            jnp.zeros((Bp, RW_HEADS, HEAD_DIM, HEAD_DIM), sd),
            jnp.zeros((Bp, 1, RW_IN_W), sd),
            *wl)
        xp = outp[0]
        pC.append(outp[1]); pn.append(outp[2]); pm.append(outp[3])
        pconv.append(outp[4]); pS.append(outp[5]); pshift.append(outp[6])
        outs = layer_forward(xs, state_mlstm_C[l], state_mlstm_n[l], state_mlstm_m[l],
                             state_mlstm_conv[l], state_rwkv_S[l], state_rwkv_shift[l], *wl)
        xs = outs[0]
        sC.append(outs[1]); sn.append(outs[2]); sm.append(outs[3])
        sconv.append(outs[4]); sS.append(outs[5]); sshift.append(outs[6])
    y_prompt = rms_norm(xp, norm_f_w)
    y_sample = rms_norm(xs, norm_f_w)
    return (y_prompt, y_sample,
            jnp.stack(pC), jnp.stack(pn), jnp.stack(pm), jnp.stack(pconv), jnp.stack(pS), jnp.stack(pshift),
            jnp.stack(sC), jnp.stack(sn), jnp.stack(sm), jnp.stack(sconv), jnp.stack(sS), jnp.stack(sshift))
```

```python
import contextlib
import numpy as np
import concourse.bass as bass
import concourse.mybir as mybir
from concourse.bass_utils import run_bass_kernel_spmd

F32 = mybir.dt.float32
BF16 = mybir.dt.bfloat16
AF = mybir.ActivationFunctionType
ALU = mybir.AluOpType
AX = mybir.AxisListType

D = 1024
T = 2048
NB = 16
NS = 16
INW = 3856
MLW = 2064
RWW = 1792
DFF = 4096
EPS = 1e-6
GN_EPS = 64e-5
C0 = 0.6065306597126334

OFF = {}
_o = 0
for _n, _w in [("mnw", 512), ("w0", 512), ("a0", 512), ("kk", 512), ("ka", 512), ("rk", 512),
               ("lnw", 512), ("lnb", 512), ("ifb", 16), ("nmw", 8), ("nmlp", 8), ("cw", 32), ("cb", 8),
               ("ident", 128), ("mui", 128), ("mus", 128), ("mls", 128), ("ones", 128),
               ("lsel", 128), ("rsel", 4)]:
    OFF[_n] = (_o, _o + _w)
    _o += _w
NA = _o
SOFF = {}
_o = 0
for _n, _w in [("mu_r", 64), ("mu_k", 64), ("mu_v", 64), ("cwq", 256), ("cwk", 256), ("cbq", 64), ("cbk", 64),
               ("mnw", 64), ("w0", 64), ("a0", 64), ("kk", 64), ("ka", 64), ("rk", 64), ("lnw", 64), ("lnb", 64),
               ("ib", 1), ("fb", 1)]:
    SOFF[_n] = (_o, _o + _w)
    _o += _w
NSP = _o


ALIAS = {"r_sb": "G0", "kf_sb": "G1", "vf": "G2", "osig": "G3", "hml": "G4", "nlrep": "G7", "wsig": "G3", "a_sb": "G4",
         "g_sb": "G5", "kap": "G6", "ktl": "G7", "bvec": "G8", "e1": "G9", "e2": "G10", "e3": "G11", "pcw": "G11",
         "ysb": "G11", "Fb": "G10", "cacc": "G11", "Ub": "Zb1", "tA": "G9", "tB": "G10", "plast": "G9", "junk": "xm", "ctmp": "xm", "xsb": "mix", "qks": "G11",
         "PTb": "Am0", "mixT": "TMbA", "TMb0": "TMbA", "TMb1": "TMbA", "TMb2": "TMbB", "TMb3": "TMbB",
         "Ss": "Cs", "lo3": "spj", "smix": "spj", "xt1": "xt0", "xnT1": "xnT0", "qkx1": "qkx0"}


class SemCtx:
    def __init__(self, nc):
        self.nc = nc
        self.es = contextlib.ExitStack()
        self.engs = ["pe", "act", "dve", "pool", "sp"]
        self.esem = {e: self.es.enter_context(nc.semaphore("s_" + e)) for e in self.engs}
        self.ecnt = {e: 0 for e in self.engs}
        self.bsem = self.es.enter_context(nc.semaphore("s_bar"))
        self.phase = 0
        self.gsem = {}
        self.gbase = {}

    def group_sem(self, g):
        if g not in self.gsem:
            self.gsem[g] = self.es.enter_context(self.nc.semaphore("g_%d" % len(self.gsem)))
            self.gbase[g] = 0
        return self.gsem[g]

    def close(self):
        self.es.close()


class Prog:
    max_ops = None

    def __init__(self, ctx):
        self.ctx = ctx
        self.nc = ctx.nc
        self.ops = []
        self.last_writer = {}
        self.readers = {}
        self.dma_groups = {}

    def op(self, eng, fn, reads=(), writes=(), dma_group=None, wait_total=False):
        if self.max_ops is not None and len(self.ops) >= self.max_ops:
            return None
        reads = [ALIAS.get(k, k) for k in reads]
        writes = [ALIAS.get(k, k) for k in writes]
        if eng != "pe":
            writes = writes + [k for k in reads if k.startswith("PS") and k not in writes]
        deps = set()
        for b in reads:
            if b in self.last_writer:
                deps.add(self.last_writer[b])
        for b in writes:
            if b in self.last_writer:
                deps.add(self.last_writer[b])
            for r in self.readers.get(b, ()):
                deps.add(r)
        idx = len(self.ops)
        if dma_group is not None:
            deps = {d for d in deps if self.ops[d]["dma"] != dma_group}
        o = dict(eng=eng, fn=fn, deps=sorted(deps), dma=dma_group, idx=idx)
        if dma_group is not None:
            g = self.dma_groups.setdefault(dma_group, dict(total=0, wait_total=wait_total))
            g["total"] += 1
            o["dma_cnt"] = g["total"]
        self.ops.append(o)
        for b in reads:
            self.readers.setdefault(b, []).append(idx)
        for b in writes:
            self.last_writer[b] = idx
            self.readers[b] = []
        return idx

    def finalize(self):
        nc = self.nc
        ctx = self.ctx
        ops = self.ops
        needed = set()
        for o in ops:
            for d in o["deps"]:
                p = ops[d]
                if p["dma"] is None:
                    if p["eng"] == "pe" and o["eng"] == "pe" and o["dma"] is None:
                        continue
                    needed.add(d)
        engs = ctx.engs
        last = {}
        for o in ops:
            if o["dma"] is None:
                last[o["eng"]] = o["idx"]
        needed |= set(last.values())
        cnt = dict(ctx.ecnt)
        for o in ops:
            if o["dma"] is None and o["idx"] in needed:
                cnt[o["eng"]] += 1
                o["sig"] = cnt[o["eng"]]
        for g in self.dma_groups:
            ctx.group_sem(g)
        phase = ctx.phase
        with nc.Block() as block:

            def emit_engine(ename, eng):
                known = {}
                if phase > 0:
                    eng.wait_ge(ctx.bsem, phase)
                for o in ops:
                    if o["eng"] != ename:
                        continue
                    for d in o["deps"]:
                        p = ops[d]
                        if p["dma"] is not None:
                            g = self.dma_groups[p["dma"]]
                            sem = ctx.gsem[p["dma"]]
                            val = ctx.gbase[p["dma"]] + 16 * (g["total"] if g["wait_total"] else p["dma_cnt"])
                            key = ("g", p["dma"])
                        else:
                            if p["eng"] == "pe" and ename == "pe" and o["dma"] is None:
                                continue
                            sem = ctx.esem[p["eng"]]
                            val = p["sig"]
                            key = ("e", p["eng"])
                        if known.get(key, 0) >= val:
                            continue
                        known[key] = val
                        eng.wait_ge(sem, val)
                    ins = o["fn"](eng)
                    if o["dma"] is not None:
                        ins.then_inc(ctx.gsem[o["dma"]], 16)
                    elif "sig" in o:
                        ins.then_inc(ctx.esem[ename], 1)
                if ename == "sp":
                    for e2 in engs:
                        if cnt[e2] > ctx.ecnt[e2]:
                            eng.wait_ge(ctx.esem[e2], cnt[e2])
                    for g, info in self.dma_groups.items():
                        eng.wait_ge(ctx.gsem[g], ctx.gbase[g] + 16 * info["total"])
                    eng.sem_inc(ctx.bsem, 1)

            @block.tensor
            def _(e):
                emit_engine("pe", e)

            @block.scalar
            def _(e):
                emit_engine("act", e)

            @block.vector
            def _(e):
                emit_engine("dve", e)

            @block.gpsimd
            def _(e):
                emit_engine("pool", e)

            @block.sync
            def _(e):
                emit_engine("sp", e)

        ctx.ecnt = cnt
        for g, info in self.dma_groups.items():
            ctx.gbase[g] += 16 * info["total"]
        ctx.phase += 1


STOP_EARLY = True


class _StopBuild(Exception):
    pass


def build_program(do_sample=True, debug=False):
    nc = bass.Bass("TRN2", target_bir_lowering=False)
    try:
        return _build_program(nc, do_sample, debug)
    except _StopBuild:
        return nc


def _build_program(nc, do_sample, debug):
    dbg = nc.dram_tensor("dbg", [128, 16, 512], F32, kind="ExternalOutput").ap() if debug else None
    din = lambda n, s: nc.dram_tensor(n, s, F32, kind="ExternalInput").ap()
    dout = lambda n, s: nc.dram_tensor(n, s, F32, kind="ExternalOutput").ap()
    xp = din("xp", [T, D])
    xs = din("xs", [NS, D])
    w_in = din("w_in", [D, INW])
    w_out = din("w_out", [D, D])
    mlp_up = din("mlp_up", [D, DFF])
    mlp_down = din("mlp_down", [DFF, D])
    packA_d = din("packA", [128, NA])
    mu_d = din("mu_b", [128, RWW])
    nfw_d = din("nfw_b", [128, D])
    wup_d = din("luw", [128, 512])
    gup_d = din("gup", [128, 512])
    spk_d = din("spack", [128, NSP])
    sC_d = din("sC", [128, 4096])
    sn_d = din("sn", [128, 64])
    sm_d = din("sm", [128, 1])
    sconv_d = din("sconv", [128, 2, 3, 64])
    sS_d = din("sS", [128, 4096])
    sshift_d = din("sshift", [128, 3, 64])
    sshl_d = din("sshl", [NS, 256])
    mul_d = din("mul", [NS, 256])

    yp = dout("yp", [T, D])
    ys = dout("ys", [NS, D])
    oC = dout("oC", [128, 4, 65])
    om = dout("om", [8, 1])
    oconv = dout("oconv", [128, 8, 3])
    oS = dout("oS", [128, 4, 64])
    oshift = dout("oshift", [1, RWW])
    osC = dout("osC", [128, 4096])
    osn = dout("osn", [128, 64])
    osm = dout("osm", [128, 1])
    osconv = dout("osconv", [128, 2, 3, 64])
    osS = dout("osS", [128, 4096])
    osshift = dout("osshift", [128, 3, 64])
    osshl = dout("osshl", [NS, 256])

    xmid_d = nc.dram_tensor("xmid_scr", [T + NS, D], F32, kind="Internal").ap()
    scr1 = nc.dram_tensor("scr1", [128, 8, 64], F32, kind="Internal").ap()
    scr2 = nc.dram_tensor("scr2", [128, 3, 64], F32, kind="Internal").ap()
    scr3 = nc.dram_tensor("scr3", [NS, 2, 8, 64], F32, kind="Internal").ap()

    ctx = SemCtx(nc)
    PP = [Prog(ctx)]
    es_res = contextlib.ExitStack()
    cur = [es_res]

    def TT(name, shape, dt=F32):
        return cur[0].enter_context(nc.sbuf_tensor("t_" + name, list(shape), dt))

    def dma(q, out, in_, reads, writes, group, wait_total=False):
        PP[0].op(q, lambda e: e.dma_start(out=out, in_=in_), reads=reads, writes=writes, dma_group=group, wait_total=wait_total)

    def dve(fn, r, w):
        PP[0].op("dve", fn, reads=r, writes=w)

    def act(fn, r, w):
        PP[0].op("act", fn, reads=r, writes=w)

    def pool(fn, r, w):
        PP[0].op("pool", fn, reads=r, writes=w)

    def pe(fn, r, w):
        PP[0].op("pe", fn, reads=r, writes=w)

    def mm(out, lhsT, rhs, start, stop, r, w):
        pe(lambda e: e.matmul(out, lhsT=lhsT, rhs=rhs, start=start, stop=stop), r, w)

    es_ps = contextlib.ExitStack()
    PS = [es_ps.enter_context(nc.psum_tensor("PS%d" % i, [128, 1024], F32)) for i in range(4)]
    PSB = [p.bitcast(BF16) for p in PS]
    psi = [0]

    def nps():
        i = psi[0] % 4
        psi[0] += 1
        return PS[i], PSB[i], "PS%d" % i

    wq = TT("wq", [128, 8, MLW], BF16)
    W1 = TT("W1", [128, 8, RWW], BF16)
    W2 = TT("W2", [128, 8, RWW], BF16)
    wout = TT("wout", [128, 8, D], BF16)
    luw = TT("luw", [128, 512], BF16)
    gup = TT("gup", [128, 512], BF16)
    pA = TT("pA", [128, NA], F32)
    identb = TT("identb", [128, 128], BF16)

    def PA(n):
        a, b = OFF[n]
        return pA[:, a:b]

    w_in_v = w_in.rearrange("(c p) n -> p c n", p=128)
    dma("sp", pA[:], packA_d, [], ["pA"], "init", True)
    for c in range(8):
        dma("pool", wq[:, c, :], w_in_v[:, c, 0:MLW], [], ["wq"], "init", True)
    dma("pool", luw[:], wup_d, [], ["luw"], "init", True)
    dma("pool", gup[:], gup_d, [], ["gup"], "init", True)
    w_out_v = w_out.rearrange("(c p) n -> p c n", p=128)
    for c in range(8):
        dma("pool", wout[:, c, :], w_out_v[:, c, :], [], ["wout"], "init", True)
    dve(lambda e: e.tensor_copy(out=identb[:], in_=PA("ident")), ["pA"], ["identb"])


    def _dbgdump(tag):
        if debug != tag:
            return
        dstg_ = cur[0].enter_context(nc.sbuf_tensor("t_dbgst%d" % tag, [128, 512], F32))
        def dd(slot, ap, key, n):
            dve(lambda e: e.tensor_copy(out=dstg_[:, 0:n], in_=ap), [key], ["dbgst"])
            dma("sp", dbg[:, slot, 0:n], dstg_[:, 0:n], ["dbgst"], [], "dbg")
        dd(0, PA("ident"), "pA", 128)
        dd(1, PA("mui"), "pA", 128)
        dd(2, PA("mnw"), "pA", 512)
        dd(3, PA("w0"), "pA", 512)
        dd(4, PA("lnb"), "pA", 512)
        PP[0].max_ops = len(PP[0].ops)
        PP[0].finalize()
        raise _StopBuild()
    _dbgdump(3)
    with contextlib.ExitStack() as es_prep:
        cur[0] = es_prep
        mub = TT("mub", [128, RWW], F32)
        omu = TT("omu", [128, RWW], F32)
        stg = [TT("stg%d" % i, [128, RWW], F32) for i in range(2)]
        dma("sp", mub[:], mu_d, [], ["mub"], "init", True)
        dve(lambda e: e.tensor_scalar(out=omu[:], in0=mub[:], scalar1=-1.0, scalar2=1.0, op0=ALU.mult, op1=ALU.add), ["mub"], ["omu"])
        for c in range(8):
            s = stg[c % 2]
            sk = "stg%d" % (c % 2)
            dma("sp", s[:], w_in_v[:, c, MLW:INW], [], [sk], sk)
            dve(lambda e, s=s, c=c: e.tensor_tensor(out=W1[:, c, :], in0=s[:], in1=omu[:], op=ALU.mult), [sk, "omu"], ["W1"])
            pool(lambda e, s=s, c=c: e.tensor_tensor(out=W2[:, c, :], in0=s[:], in1=mub[:], op=ALU.mult), [sk, "mub"], ["W2"])
        PP[0].finalize()
        PP[0] = Prog(ctx)

    def rmsnorm_T(xt, xk, nt, dstT, dstk, col0, wname, tmpb, tmpbk, junk, junkk, st, stk):
        act(lambda e: e.activation(out=junk[:nt, :], in_=xt[:nt, :], func=AF.Square, accum_out=st[:nt, 0:1]), [xk], [junkk, stk])
        act(lambda e: e.activation(out=st[:nt, 1:2], in_=st[:nt, 0:1], func=AF.Sqrt, bias=EPS, scale=1.0 / D), [stk], [stk])
        dve(lambda e: e.reciprocal(out=st[:nt, 2:3], in_=st[:nt, 1:2]), [stk], [stk])
        dve(lambda e: e.tensor_scalar_mul(out=tmpb[:nt, :], in0=xt[:nt, :], scalar1=st[:nt, 2:3]), [xk, stk], [tmpbk])
        ps, psb, pk = nps()
        for c in range(8):
            pe(lambda e, c=c: e.transpose(psb[:, c * 128:c * 128 + nt], tmpb[:nt, c * 128:(c + 1) * 128], identb[:nt, :nt]), [tmpbk, "identb"], [pk])
        a, b_ = OFF[wname]
        dve(lambda e: e.tensor_tensor(out=dstT[:, :, col0:col0 + nt],
                                      in0=psb[:, 0:1024].rearrange("p (c t) -> p c t", c=8)[:, :, 0:nt],
                                      in1=pA[:, a:b_].unsqueeze(2).to_broadcast([128, 8, nt]), op=ALU.mult), [pk, "pA"], [dstk])

    def head_ml(nt, hsrc, hk, osig, ok, mix, mixk, W):
        tA, tB, s8 = W["tA"], W["tB"], W["s8"]
        h3 = lambda t: t[:nt, :].rearrange("p (h d) -> p h d", h=8)
        bc = lambda t, c: t[:nt, c:c + 8].unsqueeze(2).to_broadcast([nt, 8, 64])
        dve(lambda e: e.tensor_tensor(out=tA[:nt, :], in0=hsrc[:nt, :], in1=osig[:nt, :], op=ALU.mult), [hk, ok], ["tA"])
        dve(lambda e: e.tensor_tensor(out=tB[:nt, :], in0=tA[:nt, :], in1=tA[:nt, :], op=ALU.mult), ["tA"], ["tB"])
        dve(lambda e: e.tensor_reduce(out=s8[:nt, 0:8], in_=h3(tB), axis=AX.X, op=ALU.add), ["tB"], ["s8"])
        act(lambda e: e.activation(out=s8[:nt, 8:16], in_=s8[:nt, 0:8], func=AF.Sqrt, bias=EPS, scale=1.0 / 64), ["s8"], ["s8"])
        dve(lambda e: e.reciprocal(out=s8[:nt, 16:24], in_=s8[:nt, 8:16]), ["s8"], ["s8"])
        dve(lambda e: e.tensor_tensor(out=h3(tB), in0=h3(tA), in1=bc(s8, 16), op=ALU.mult), ["tA", "s8"], ["tB"])
        dve(lambda e: e.tensor_tensor(out=mix[:nt, 0:512], in0=tB[:nt, :], in1=PA("mnw")[:nt, :], op=ALU.mult), ["tB", "pA"], [mixk])

    def head_rw(nt, ysrc, yk, bon, bonk, vf, vfk, g, gk, mix, mixk, W):
        tA, tB, s8 = W["tA"], W["tB"], W["s8"]
        h3 = lambda t: t[:nt, :].rearrange("p (h d) -> p h d", h=8)
        bc = lambda t, c: t[:nt, c:c + 8].unsqueeze(2).to_broadcast([nt, 8, 64])
        dve(lambda e: e.tensor_tensor(out=h3(tA), in0=h3(vf), in1=bc(bon, 0), op=ALU.mult), [vfk, bonk], ["tA"])
        dve(lambda e: e.tensor_tensor(out=tA[:nt, :], in0=tA[:nt, :], in1=ysrc[:nt, :], op=ALU.add), ["tA", yk], ["tA"])
        dve(lambda e: e.tensor_reduce(out=s8[:nt, 24:32], in_=h3(tA), axis=AX.X, op=ALU.add), ["tA"], ["s8"])
        dve(lambda e: e.tensor_scalar_mul(out=s8[:nt, 24:32], in0=s8[:nt, 24:32], scalar1=1.0 / 64), ["s8"], ["s8"])
        dve(lambda e: e.tensor_tensor(out=h3(tA), in0=h3(tA), in1=bc(s8, 24), op=ALU.subtract), ["tA", "s8"], ["tA"])
        dve(lambda e: e.tensor_tensor(out=tB[:nt, :], in0=tA[:nt, :], in1=tA[:nt, :], op=ALU.mult), ["tA"], ["tB"])
        dve(lambda e: e.tensor_reduce(out=s8[:nt, 32:40], in_=h3(tB), axis=AX.X, op=ALU.add), ["tB"], ["s8"])
        act(lambda e: e.activation(out=s8[:nt, 40:48], in_=s8[:nt, 32:40], func=AF.Sqrt, bias=GN_EPS, scale=1.0 / 64), ["s8"], ["s8"])
        dve(lambda e: e.reciprocal(out=s8[:nt, 48:56], in_=s8[:nt, 40:48]), ["s8"], ["s8"])
        dve(lambda e: e.tensor_tensor(out=h3(tB), in0=h3(tA), in1=bc(s8, 48), op=ALU.mult), ["tA", "s8"], ["tB"])
        dve(lambda e: e.tensor_tensor(out=tB[:nt, :], in0=tB[:nt, :], in1=PA("lnw")[:nt, :], op=ALU.mult), ["tB", "pA"], ["tB"])
        dve(lambda e: e.tensor_tensor(out=tB[:nt, :], in0=tB[:nt, :], in1=PA("lnb")[:nt, :], op=ALU.add), ["tB", "pA"], ["tB"])
        dve(lambda e: e.tensor_tensor(out=mix[:nt, 512:1024], in0=tB[:nt, :], in1=g[:nt, :], op=ALU.mult), ["tB", gk], [mixk])

    def out_proj(nt, mix, mixk, xt, xk, row0, W):
        mixT, xm = W["mixT"], W["xm"]
        ps, psb, pk = nps()
        for c in range(8):
            pe(lambda e, c=c: e.transpose(psb[:, c * 128:c * 128 + nt], mix[:nt, c * 128:(c + 1) * 128], identb[:nt, :nt]), [mixk, "identb"], [pk])
        act(lambda e: e.copy(out=mixT[:, :, 0:nt], in_=psb[:, 0:1024].rearrange("p (c t) -> p c t", c=8)[:, :, 0:nt]), [pk], ["mixT"])
        ps, psb, pk = nps()
        for n in range(2):
            for c in range(8):
                mm(ps[:nt, n * 512:(n + 1) * 512], mixT[:, c, 0:nt], wout[:, c, n * 512:(n + 1) * 512], c == 0, c == 7, ["mixT", "wout"], [pk])
        for a_ in range(2):
            dve(lambda e, a_=a_: e.tensor_tensor(out=xm[:nt, a_ * 512:(a_ + 1) * 512], in0=ps[:nt, a_ * 512:(a_ + 1) * 512], in1=xt[:nt, a_ * 512:(a_ + 1) * 512], op=ALU.add), [pk, xk], ["xm"])
        dma("pool", xmid_d[row0:row0 + nt, :], xm[:nt, :], ["xm"], ["xmid%d" % row0], "xm")

    with contextlib.ExitStack() as es1:
        cur[0] = es1
        W = {}
        Gbig = TT("Gbig", [128, 13, 512])
        G = [Gbig[:, i, :] for i in range(13)]
        W["s8"] = TT("s8", [128, 64])
        W["xm"] = TT("xm", [128, D])
        xt = [TT("xt0", [128, D])] * 2
        junk = W["xm"]
        st = TT("st", [128, 4])
        mix = TT("mix", [128, D], BF16)
        xsb = mix
        xnT = [TT("xnT0", [128, 8, 129], BF16)] * 2
        qkx = [TT("qkx0", [128, 8, 131])] * 2
        cacc = Gbig[:, 11:13, :].rearrange("p a (c t) -> p (a c) t", t=128)
        ctmp = W["xm"][:, :].rearrange("p (c t) -> p c t", c=8)
        qks = cacc
        qpb = TT("qpb", [128, 4, 128], BF16)
        kTb = TT("kTb", [128, 4, 128], BF16)
        ktm = TT("ktm", [128, 8, 64], BF16)
        vaug = TT("vaug", [128, 8, 65], BF16)
        gt = TT("gt", [128, 96])
        runmax = TT("runmax", [128, 8])
        nBc = TT("nBc", [128, 8])
        Cst = TT("Cst", [128, 4, 65])
        Cbf = TT("Cbf", [128, 4, 65], BF16)
        r_sb, kf_sb, vf = G[0], G[1], G[2]
        Fb = G[10].rearrange("p (j t) -> p j t", j=4)
        osig, hml = G[3], G[4]
        nlrep = G[7].rearrange("p (h d) -> p h d", h=8)
        wsig, a_sb, g_sb, kap, ktl, bvec, e1, e2, e3, pcw, ysb = G[3], G[4], G[5], G[6], G[7], G[8], G[9], G[10], G[11], G[12], G[11]
        W["tA"], W["tB"] = G[9], G[10]
        vrw = TT("vrw", [128, 8, 64], BF16)
        lor = TT("lor", [128, 256], BF16)
        lorT = TT("lorT", [128, 2, 128], BF16)
        r8 = TT("r8", [128, 32])
        bon = TT("bon", [128, 8])
        TMb = TT("TMb", [128, 4, 512], BF16)
        W["mixT"] = TMb[:, 0:2, :].rearrange("p a (c t) -> p (a c) t", t=128)
        Btz = TT("Btz", [128, 8, 128], BF16)
        Ktz = TT("Ktz", [128, 8, 128], BF16)
        FMt = TT("FMt", [128, 4, 4, 128], BF16)
        Am = [TT("Am%d" % i, [128, 8, 128], BF16) for i in range(3)]
        PTb = Am[0]
        Pw = [TT("Pw%d" % i, [128, 8, 128], BF16) for i in range(4)]
        Zb = [TT("Zb%d" % i, [128, 8, 64], BF16) for i in range(2)]
        Ub = Zb[1]
        Sst = TT("Sst", [128, 4, 64])
        Sbf = TT("Sbf", [128, 4, 64], BF16)
        WLfm = TT("WLfm", [128, 4])
        plast = G[9]

        pool(lambda e: e.memset(vaug[:], 1.0), [], ["vaug"])
        pool(lambda e: e.memset(Btz[:], 0.0), [], ["Btz"])
        pool(lambda e: e.memset(Ktz[:], 0.0), [], ["Ktz"])
        pool(lambda e: e.memset(Cst[:], 0.0), [], ["Cst"])
        pool(lambda e: e.memset(Cbf[:], 0.0), [], ["Cbf"])
        pool(lambda e: e.memset(Sst[:], 0.0), [], ["Sst"])
        pool(lambda e: e.memset(Sbf[:], 0.0), [], ["Sbf"])
        pool(lambda e: e.memset(runmax[:], -1e30), [], ["runmax"])
        pool(lambda e: e.memset(nBc[:], 0.0), [], ["nBc"])
        pool(lambda e: e.memset(xnT[0][:, :, 0:1], 0.0), [], ["xnT0"])
        pool(lambda e: e.memset(qkx[0][:, :, 0:3], 0.0), [], ["qkx0"])

        if debug == 2:
            dstg = TT("dbgstage", [128, 512]) if False else G[12]
            def ddump0(slot, ap, key, n):
                dve(lambda e: e.tensor_copy(out=dstg[:, 0:n], in_=ap), [key], ["pcw"])
                dma("sp", dbg[:, slot, 0:n], dstg[:, 0:n], ["pcw"], [], "dbg")
            ddump0(0, PA("ident"), "pA", 128)
            ddump0(1, PA("mui"), "pA", 128)
            ddump0(2, luw[:, :], "luw", 512)
            ddump0(3, gup[:, :], "gup", 512)
            ddump0(4, W1[:, 0, 0:512], "W1", 512)
            PP[0].max_ops = len(PP[0].ops)
            if STOP_EARLY:
                PP[0].finalize()
                raise _StopBuild()
        MUI = PA("mui")
        MUS = PA("mus")
        MLS = PA("mls")
        ONES = PA("ones")
        IDF = PA("ident")
        bc8 = lambda ap: ap.unsqueeze(2).to_broadcast([128, 8, 64])
        m8 = lambda m: m.unsqueeze(1).to_broadcast([128, 8, 128])
        v3 = lambda t: t[:].rearrange("p (h d) -> p h d", h=8)
        hoff = lambda h: (h % 2) * 512 + (h // 2) * 128

        for b in range(NB):
            x_ = xt[b % 2]
            xk = "xt%d" % (b % 2)
            xn = xnT[b % 2]
            xnk = "xnT%d" % (b % 2)
            qx = qkx[b % 2]
            qxk = "qkx%d" % (b % 2)
            dma("sp", x_[:], xp[b * 128:(b + 1) * 128, :], [], [xk], xk)
            rmsnorm_T(x_, xk, 128, xn, xnk, 1, "nmw", xsb, "xsb", junk, "junk", st, "st")
            cur_x = xn[:, :, 1:129]
            prv_x = xn[:, :, 0:128]

            ps, psb, pk = nps()
            for j in range(8):
                for c in range(8):
                    mm(ps[:, j * 128:(j + 1) * 128], wq[:, c, j * 128:(j + 1) * 128], cur_x[:, c, :], c == 0, c == 7, ["wq", xnk], [pk])
            for a_ in range(2):
                act(lambda e, ps=ps, qx=qx, a_=a_: e.copy(out=qx[:, 4 * a_:4 * a_ + 4, 3:131], in_=ps[:, a_ * 512:(a_ + 1) * 512].rearrange("p (j t) -> p j t", j=4)), [pk], [qxk])
            if b == NB - 1:
                dma("pool", oconv, qx[:, :, 128:131], [qxk], [], "fin")

            def tm_proj(col0, ncol, rw, ps_ap, pk):
                if not rw:
                    for c in range(8):
                        mm(ps_ap, cur_x[:, c, :], wq[:, c, col0:col0 + ncol], c == 0, c == 7, [xnk, "wq"], [pk])
                else:
                    for c in range(8):
                        mm(ps_ap, cur_x[:, c, :], W1[:, c, col0:col0 + ncol], c == 0, False, [xnk, "W1"], [pk])
                    for c in range(8):
                        mm(ps_ap, prv_x[:, c, :], W2[:, c, col0:col0 + ncol], False, c == 7, [xnk, "W2"], [pk])

            ps, psb, pk = nps()
            tm_proj(1024, 512, False, ps[:, 0:512], pk)
            tm_proj(1536, 512, False, ps[:, 512:1024], pk)
            act(lambda e, ps=ps: e.copy(out=vaug[:, :, 0:64], in_=ps[:, 0:512].rearrange("p (h d) -> p h d", h=8)), [pk], ["vaug"])
            act(lambda e, ps=ps: e.activation(out=osig[:], in_=ps[:, 512:1024], func=AF.Sigmoid), [pk], ["osig"])
            ps, psb, pk = nps()
            tm_proj(2048, 16, False, ps[:, 0:16], pk)
            tm_proj(0, 512, True, ps[:, 512:1024], pk)
            dve(lambda e, ps=ps: e.tensor_tensor(out=gt[:, 0:16], in0=ps[:, 0:16], in1=PA("ifb"), op=ALU.add), [pk, "pA"], ["gt"])
            act(lambda e, ps=ps: e.copy(out=r_sb[:], in_=ps[:, 512:1024]), [pk], ["r_sb"])
            ps, psb, pk = nps()
            tm_proj(512, 512, True, ps[:, 0:512], pk)
            tm_proj(1024, 512, True, ps[:, 512:1024], pk)
            act(lambda e, ps=ps: e.copy(out=kf_sb[:], in_=ps[:, 0:512]), [pk], ["kf_sb"])
            act(lambda e, ps=ps: e.copy(out=vf[:], in_=ps[:, 512:1024]), [pk], ["vf"])
            dve(lambda e, ps=ps: e.tensor_copy(out=vrw[:], in_=ps[:, 512:1024].rearrange("p (h d) -> p h d", h=8)), [pk], ["vrw"])
            ps, psb, pk = nps()
            tm_proj(1536, 256, True, ps[:, 0:256], pk)
            act(lambda e, ps=ps: e.activation(out=lor[:, 0:64], in_=ps[:, 0:64], func=AF.Tanh), [pk], ["lor"])
            act(lambda e, ps=ps: e.copy(out=lor[:, 64:128], in_=ps[:, 64:128]), [pk], ["lor"])
            act(lambda e, ps=ps: e.activation(out=lor[:, 128:256], in_=ps[:, 128:256], func=AF.Sigmoid), [pk], ["lor"])
            if b == NB - 1:
                lastc = xn[:, :, 128:129]
                for n0 in range(0, RWW, 512):
                    nn = min(512, RWW - n0)
                    ps2, _, pk2 = nps()
                    for c in range(8):
                        mm(ps2[0:1, 0:nn], lastc[:, c, :], W1[:, c, n0:n0 + nn], c == 0, False, [xnk, "W1"], [pk2])
                    for c in range(8):
                        mm(ps2[0:1, 0:nn], lastc[:, c, :], W2[:, c, n0:n0 + nn], False, c == 7, [xnk, "W2"], [pk2])
                    act(lambda e, ps2=ps2, n0=n0, nn=nn: e.copy(out=plast[0:1, 0:nn], in_=ps2[0:1, 0:nn]), [pk2], ["plast"])
                    dma("pool", oshift[:, n0:n0 + nn], plast[0:1, 0:nn], ["plast"], [], "fin")

            act(lambda e: e.activation(out=gt[:, 56:64], in_=gt[:, 8:16], func=AF.Exp, scale=-1.0), ["gt"], ["gt"])
            act(lambda e: e.activation(out=gt[:, 16:24], in_=gt[:, 56:64], func=AF.Ln, bias=1.0, scale=1.0), ["gt"], ["gt"])
            dve(lambda e: e.tensor_copy(out=nlrep[:], in_=bc8(gt[:, 16:24])), ["gt"], ["nlrep"])
            ps, psb, pk = nps()
            mm(ps[:, 0:8], MUI, gt[:, 16:24], True, True, ["pA", "gt"], [pk])
            mm(ps[:, 8:16], ONES, gt[:, 16:24], True, True, ["pA", "gt"], [pk])
            for j in range(4):
                mm(ps[:, 512 + j * 128:512 + (j + 1) * 128], nlrep[:, 2 * j:2 * j + 2, :].rearrange("p a d -> p (a d)"), MUI, True, True, ["nlrep", "pA"], [pk])
            dve(lambda e, ps=ps: e.tensor_tensor(out=gt[:, 24:32], in0=ps[:, 0:8], in1=gt[:, 0:8], op=ALU.add), [pk, "gt"], ["gt"])
            act(lambda e: e.activation(out=gt[:, 32:40], in_=gt[:, 24:32], func=AF.Exp), ["gt"], ["gt"])
            dve(lambda e, ps=ps: e.tensor_tensor(out=gt[:, 56:64], in0=gt[:, 24:32], in1=ps[:, 8:16], op=ALU.subtract), [pk, "gt"], ["gt"])
            act(lambda e: e.activation(out=gt[:, 40:48], in_=gt[:, 56:64], func=AF.Exp), ["gt"], ["gt"])
            act(lambda e, ps=ps: e.activation(out=gt[:, 48:56], in_=ps[:, 8:16], func=AF.Exp, scale=-1.0), [pk], ["gt"])
            act(lambda e, ps=ps: e.activation(out=Fb[:], in_=ps[:, 512:1024].rearrange("p (j t) -> p j t", j=4), func=AF.Exp, scale=-1.0), [pk], ["Fb"])
            dve(lambda e: e.tensor_tensor(out=gt[:, 56:64], in0=gt[:, 24:32], in1=nBc[:], op=ALU.add), ["gt", "nBc"], ["gt"])
            dve(lambda e: e.tensor_tensor(out=runmax[:], in0=runmax[:], in1=gt[:, 56:64], op=ALU.max), ["gt", "runmax"], ["runmax"])
            dve(lambda e, ps=ps: e.tensor_tensor(out=nBc[:], in0=nBc[:], in1=ps[:, 8:16], op=ALU.add), [pk, "nBc"], ["nBc"])

            cwv = PA("cw").rearrange("p (c j) -> p c j", j=4)
            wbc = lambda j: cwv[:, :, j:j + 1].to_broadcast([128, 8, 128])
            pool(lambda e, qx=qx: e.tensor_tensor(out=cacc[:], in0=qx[:, :, 3:131], in1=wbc(3), op=ALU.mult), [qxk, "pA"], ["cacc"])
            for j in range(3):
                pool(lambda e, qx=qx, j=j: e.tensor_tensor(out=ctmp[:], in0=qx[:, :, j:j + 128], in1=wbc(j), op=ALU.mult), [qxk, "pA"], ["ctmp"])
                pool(lambda e: e.tensor_tensor(out=cacc[:], in0=cacc[:], in1=ctmp[:], op=ALU.add), ["cacc", "ctmp"], ["cacc"])
            pool(lambda e: e.tensor_tensor(out=cacc[:], in0=cacc[:], in1=PA("cb").unsqueeze(2).to_broadcast([128, 8, 128]), op=ALU.add), ["cacc", "pA"], ["cacc"])
            act(lambda e: e.activation(out=qks[:], in_=cacc[:], func=AF.Silu), ["cacc"], ["qks"])
            dve(lambda e: e.tensor_tensor(out=qpb[:], in0=qks[:, 0:4, :], in1=Fb[:], op=ALU.mult), ["qks", "Fb"], ["qpb"])
            act(lambda e: e.activation(out=kTb[:], in_=qks[:, 4:8, :], func=AF.Copy, scale=0.125), ["qks"], ["kTb"])

            ps, psb, pk = nps()
            for j in range(4):
                pe(lambda e, j=j, psb=psb: e.transpose(psb[:, j * 128:(j + 1) * 128], kTb[:, j, :], identb[:]), ["kTb", "identb"], [pk])
            dve(lambda e, psb=psb: e.tensor_tensor(out=ktm[:], in0=psb[:, 0:512].rearrange("p (h d) -> p h d", h=8), in1=bc8(gt[:, 40:48]), op=ALU.mult), [pk, "gt"], ["ktm"])

            ps, psb, pk = nps()
            for h in range(8):
                j, hp = h // 2, h % 2
                sl = slice(hp * 64, hp * 64 + 64)
                mm(ps[:, hoff(h):hoff(h) + 128], kTb[sl, j, :], qpb[sl, j, :], True, True, ["kTb", "qpb"], [pk])
            for h in range(8):
                dve(lambda e, h=h, ps=ps: e.scalar_tensor_tensor(out=PTb[:, h, :], in0=ps[:, hoff(h):hoff(h) + 128], scalar=gt[:, 32 + h:33 + h], in1=MUI, op0=ALU.mult, op1=ALU.mult), [pk, "gt", "pA"], ["PTb"])
            ps, psb, pk = nps()
            psn = lambda ps, h: ps[:, (h // 4) * 512 + (h % 4) * 65:(h // 4) * 512 + (h % 4) * 65 + 65]
            for h in range(8):
                j, hp = h // 2, h % 2
                sl = slice(hp * 64, hp * 64 + 64)
                mm(psn(ps, h), PTb[:, h, :], vaug[:, h, :], True, False, ["PTb", "vaug"], [pk])
                mm(psn(ps, h), qpb[sl, j, :], Cbf[sl, j, :], False, True, ["qpb", "Cbf"], [pk])
            pn4 = ps[:, :].rearrange("p (a r) -> p a r", a=2)[:, :, 0:260].rearrange("p a (h d) -> p a h d", h=4)
            for a_ in range(2):
                act(lambda e, pn4=pn4, a_=a_: e.copy(out=r8[:, 4 * a_:4 * a_ + 4], in_=pn4[:, a_, :, 64]), [pk], ["r8"])
            dve(lambda e: e.scalar_tensor_tensor(out=r8[:, 8:16], in0=r8[:, 0:8], scalar=-1.0, in1=r8[:, 0:8], op0=ALU.mult, op1=ALU.max), ["r8"], ["r8"])
            dve(lambda e: e.tensor_scalar_max(out=r8[:, 8:16], in0=r8[:, 8:16], scalar1=1.0), ["r8"], ["r8"])
            dve(lambda e: e.reciprocal(out=r8[:, 16:24], in_=r8[:, 8:16]), ["r8"], ["r8"])
            for a_ in range(2):
                dve(lambda e, pn4=pn4, a_=a_: e.tensor_tensor(out=hml[:, a_ * 256:(a_ + 1) * 256].rearrange("p (h d) -> p h d", h=4), in0=pn4[:, a_, :, 0:64],
                                                          in1=r8[:, 16 + 4 * a_:20 + 4 * a_].unsqueeze(2).to_broadcast([128, 4, 64]), op=ALU.mult), [pk, "r8"], ["hml"])
            ps, psb, pk = nps()
            for h in range(8):
                j = h // 2
                mm(psn(ps, h), ktm[:, 2 * j:2 * j + 2, :].rearrange("p a d -> p (a d)"), vaug[:, h, :], True, True, ["ktm", "vaug"], [pk])
            pu4 = ps[:, :].rearrange("p (a r) -> p a r", a=2)[:, :, 0:260].rearrange("p a (h d) -> p a h d", h=4)
            for hp in range(2):
                sl = slice(hp * 64, hp * 64 + 64)
                decb = gt[sl, 48:56].rearrange("p (j q) -> p j q", q=2)[:, :, hp:hp + 1].to_broadcast([64, 4, 65])
                dve(lambda e, sl=sl, decb=decb: e.tensor_tensor(out=Cst[sl, :, :], in0=Cst[sl, :, :], in1=decb, op=ALU.mult), ["Cst", "gt"], ["Cst"])
                for a in range(2):
                    src = pu4[sl, a, hp::2, :]
                    dve(lambda e, sl=sl, a=a, src=src: e.tensor_tensor(out=Cst[sl, 2 * a:2 * a + 2, :], in0=Cst[sl, 2 * a:2 * a + 2, :], in1=src, op=ALU.add), [pk, "Cst"], ["Cst"])
            act(lambda e: e.copy(out=Cbf[:], in_=Cst[:]), ["Cst"], ["Cbf"])
            head_ml(128, hml, "hml", osig, "osig", mix, "mix", W)

            ps, psb, pk = nps()
            pe(lambda e, psb=psb: e.transpose(psb[:, 0:128], lor[:, 0:128], identb[:]), ["lor", "identb"], [pk])
            pe(lambda e, psb=psb: e.transpose(psb[:, 128:256], lor[:, 128:256], identb[:]), ["lor", "identb"], [pk])
            act(lambda e, psb=psb: e.copy(out=lorT[:], in_=psb[:, 0:256].rearrange("p (a t) -> p a t", a=2)), [pk], ["lorT"])
            ps, psb, pk = nps()
            mm(ps[:, 0:512], lorT[0:64, 0, :], luw[0:64, :], True, True, ["lorT", "luw"], [pk])
            mm(ps[:, 512:1024], lorT[64:128, 0, :], luw[64:128, :], True, True, ["lorT", "luw"], [pk])
            dve(lambda e, ps=ps: e.tensor_tensor(out=e1[:], in0=ps[:, 0:512], in1=PA("w0"), op=ALU.add), [pk, "pA"], ["e1"])
            act(lambda e: e.activation(out=wsig[:], in_=e1[:], func=AF.Sigmoid), ["e1"], ["wsig"])
            dve(lambda e, ps=ps: e.tensor_tensor(out=e2[:], in0=ps[:, 512:1024], in1=PA("a0"), op=ALU.add), [pk, "pA"], ["e2"])
            act(lambda e: e.activation(out=a_sb[:], in_=e2[:], func=AF.Sigmoid), ["e2"], ["a_sb"])
            ps, psb, pk = nps()
            mm(ps[:, 0:512], lorT[:, 1, :], gup[:, :], True, True, ["lorT", "gup"], [pk])
            act(lambda e, ps=ps: e.copy(out=g_sb[:], in_=ps[:, 0:512]), [pk], ["g_sb"])
            if debug and b == 0:
                dstg = G[12]
                def ddump(slot, ap, key, n):
                    dve(lambda e: e.tensor_copy(out=dstg[:, 0:n], in_=ap), [key], ["pcw"])
                    dma("pool", dbg[:, slot, 0:n], dstg[:, 0:n], ["pcw"], [], "dbg")
                ddump(15, lor[:, :], "lor", 256)
                ddump(13, lorT[:].rearrange("p a t -> p (a t)"), "lorT", 256)
                ddump(14, luw[:, :], "luw", 512)
                ddump(12, gup[:, :], "gup", 512)
                ddump(11, wsig[:, :], "wsig", 512)
                ddump(10, g_sb[:, :], "g_sb", 512)
                PP[0].max_ops = len(PP[0].ops)
            dve(lambda e: e.tensor_tensor(out=e1[:], in0=kf_sb[:], in1=PA("kk"), op=ALU.mult), ["kf_sb", "pA"], ["e1"])
            dve(lambda e: e.tensor_tensor(out=e2[:], in0=e1[:], in1=e1[:], op=ALU.mult), ["e1"], ["e2"])
            dve(lambda e: e.tensor_reduce(out=r8[:, 24:32], in_=v3(e2), axis=AX.X, op=ALU.add), ["e2"], ["r8"])
            dve(lambda e: e.tensor_scalar_max(out=r8[:, 24:32], in0=r8[:, 24:32], scalar1=1e-24), ["r8"], ["r8"])
            act(lambda e: e.activation(out=r8[:, 24:32], in_=r8[:, 24:32], func=AF.Sqrt), ["r8"], ["r8"])
            dve(lambda e: e.reciprocal(out=r8[:, 24:32], in_=r8[:, 24:32]), ["r8"], ["r8"])
            dve(lambda e: e.tensor_tensor(out=v3(kap), in0=v3(e1), in1=bc8(r8[:, 24:32]), op=ALU.mult), ["e1", "r8"], ["kap"])
            dve(lambda e: e.tensor_scalar_add(out=e2[:], in0=a_sb[:], scalar1=-1.0), ["a_sb"], ["e2"])
            dve(lambda e: e.tensor_tensor(out=e2[:], in0=e2[:], in1=PA("ka"), op=ALU.mult), ["e2", "pA"], ["e2"])
            dve(lambda e: e.tensor_tensor(out=e2[:], in0=e2[:], in1=kf_sb[:], op=ALU.mult), ["e2", "kf_sb"], ["e2"])
            dve(lambda e: e.tensor_tensor(out=ktl[:], in0=e2[:], in1=kf_sb[:], op=ALU.add), ["e2", "kf_sb"], ["ktl"])
            dve(lambda e: e.tensor_tensor(out=bvec[:], in0=a_sb[:], in1=kap[:], op=ALU.mult), ["a_sb", "kap"], ["bvec"])
            dve(lambda e: e.tensor_tensor(out=e2[:], in0=r_sb[:], in1=ktl[:], op=ALU.mult), ["r_sb", "ktl"], ["e2"])
            dve(lambda e: e.tensor_tensor(out=e2[:], in0=e2[:], in1=PA("rk"), op=ALU.mult), ["e2", "pA"], ["e2"])
            dve(lambda e: e.tensor_reduce(out=bon[:], in_=v3(e2), axis=AX.X, op=ALU.add), ["e2"], ["bon"])
            ps, psb, pk = nps()
            mm(ps[:, 0:512], MUI, wsig[:], True, True, ["pA", "wsig"], [pk])
            mm(ps[:, 512:1024], ONES, wsig[:], True, True, ["pA", "wsig"], [pk])
            act(lambda e, ps=ps: e.copy(out=pcw[:], in_=ps[:, 0:512]), [pk], ["pcw"])
            dve(lambda e: e.tensor_tensor(out=e1[:], in0=pcw[:], in1=wsig[:], op=ALU.subtract), ["pcw", "wsig"], ["e1"])
            act(lambda e: e.activation(out=e1[:], in_=e1[:], func=AF.Exp, scale=-C0), ["e1"], ["e1"])
            dve(lambda e: e.tensor_tensor(out=TMb[:, 0, :], in0=kap[:], in1=e1[:], op=ALU.mult), ["kap", "e1"], ["TMb0"])
            act(lambda e: e.activation(out=e2[:], in_=pcw[:], func=AF.Exp, scale=-C0), ["pcw"], ["e2"])
            dve(lambda e: e.tensor_tensor(out=TMb[:, 1, :], in0=r_sb[:], in1=e2[:], op=ALU.mult), ["r_sb", "e2"], ["TMb1"])
            act(lambda e: e.activation(out=e3[:], in_=pcw[:], func=AF.Exp, scale=C0), ["pcw"], ["e3"])
            dve(lambda e: e.tensor_tensor(out=TMb[:, 2, :], in0=bvec[:], in1=e3[:], op=ALU.mult), ["bvec", "e3"], ["TMb2"])
            dve(lambda e: e.tensor_tensor(out=TMb[:, 3, :], in0=ktl[:], in1=e3[:], op=ALU.mult), ["ktl", "e3"], ["TMb3"])
            dve(lambda e, ps=ps: e.tensor_tensor(out=e1[:], in0=ps[:, 512:1024], in1=pcw[:], op=ALU.subtract), [pk, "pcw"], ["e1"])
            act(lambda e: e.activation(out=e1[:], in_=e1[:], func=AF.Exp, scale=-C0), ["e1"], ["e1"])
            for hp in range(2):
                srcb = v3(bvec).rearrange("p (j q) d -> p j q d", q=2)[:, :, hp, :]
                srck = v3(ktl).rearrange("p (j q) d -> p j q d", q=2)[:, :, hp, :]
                wl = v3(e1).rearrange("p (j q) d -> p j q d", q=2)[:, :, hp, :]
                dstb = Btz[:].rearrange("p (j q) c -> p j q c", q=2)[:, :, hp, hp * 64:hp * 64 + 64]
                dstk = Ktz[:].rearrange("p (j q) c -> p j q c", q=2)[:, :, hp, hp * 64:hp * 64 + 64]
                dve(lambda e, srcb=srcb, wl=wl, dstb=dstb: e.tensor_tensor(out=dstb, in0=srcb, in1=wl, op=ALU.mult), ["bvec", "e1"], ["Btz"])
                dve(lambda e, srck=srck, wl=wl, dstk=dstk: e.tensor_tensor(out=dstk, in0=srck, in1=wl, op=ALU.mult), ["ktl", "e1"], ["Ktz"])
            ps2, _, pk2 = nps()
            for j in range(4):
                mm(ps2[:, j:j + 1], wsig[:, j * 128:(j + 1) * 128], ONES[:, 0:1], True, True, ["wsig", "pA"], [pk2])
            act(lambda e, ps2=ps2: e.activation(out=WLfm[:], in_=ps2[:, 0:4], func=AF.Exp, scale=-C0), [pk2], ["WLfm"])
            ps, psb, pk = nps()
            for w_ in range(4):
                for j in range(4):
                    pe(lambda e, w_=w_, j=j, psb=psb: e.transpose(psb[:, (w_ * 4 + j) * 128:(w_ * 4 + j + 1) * 128], TMb[:, w_, j * 128:(j + 1) * 128], identb[:]), ["TMb%d" % w_, "identb"], [pk])
            for w_ in range(4):
                eng_ = act if w_ % 2 == 0 else dve
                if w_ % 2 == 0:
                    act(lambda e, psb=psb, w_=w_: e.copy(out=FMt[:, w_, :, :], in_=psb[:, w_ * 512:(w_ + 1) * 512].rearrange("p (j t) -> p j t", j=4)), [pk], ["FMt"])
                else:
                    dve(lambda e, psb=psb, w_=w_: e.tensor_copy(out=FMt[:, w_, :, :], in_=psb[:, w_ * 512:(w_ + 1) * 512].rearrange("p (j t) -> p j t", j=4)), [pk], ["FMt"])
            KAP, RB, BB, KKB = 0, 1, 2, 3

            def amat(lw, rw_, dst, dk, mask, neg):
                ps, psb, pk = nps()
                for h in range(8):
                    j, hp = h // 2, h % 2
                    sl = slice(hp * 64, hp * 64 + 64)
                    mm(ps[:, hoff(h):hoff(h) + 128], FMt[sl, lw, j, :], FMt[sl, rw_, j, :], True, True, ["FMt"], [pk])
                psv = ps[:, :].rearrange("p (q j t) -> p q j t", q=2, j=4)
                dstv = dst[:].rearrange("p (j q) t -> p q j t", q=2)
                mk = mask.unsqueeze(1).unsqueeze(1).to_broadcast([128, 2, 4, 128])
                if neg:
                    mk3 = mask.unsqueeze(1).to_broadcast([128, 4, 128])
                    for q in range(2):
                        dve(lambda e, q=q: e.scalar_tensor_tensor(out=dstv[:, q], in0=psv[:, q], scalar=-1.0, in1=mk3, op0=ALU.mult, op1=ALU.mult), [pk, "pA"], [dk])
                else:
                    mk3 = mask.unsqueeze(1).to_broadcast([128, 4, 128])
                    for q in range(2):
                        dve(lambda e, q=q: e.tensor_tensor(out=dstv[:, q], in0=psv[:, q], in1=mk3, op=ALU.mult), [pk, "pA"], [dk])

            amat(BB, KAP, Pw[1], "Pw1", MUS, True)
            amat(KAP, BB, Pw[0], "Pw0", MLS, True)
            amat(KKB, KAP, Am[0], "Am0", MUS, False)
            amat(BB, RB, Am[1], "Am1", MUI, False)
            amat(KKB, RB, Am[2], "Am2", MUI, False)
            ps, psb, pk = nps()
            for h in range(8):
                j, hp = h // 2, h % 2
                sl = slice(hp * 64, hp * 64 + 64)
                mm(ps[:, h * 64:(h + 1) * 64], FMt[sl, KAP, j, :], Sbf[sl, j, :], True, False, ["FMt", "Sbf"], [pk])
                mm(ps[:, h * 64:(h + 1) * 64], Am[0][:, h, :], vrw[:, h, :], False, True, ["Am0", "vrw"], [pk])
            act(lambda e, ps=ps: e.copy(out=Zb[0][:], in_=ps[:, 0:512].rearrange("p (h d) -> p h d", h=8)), [pk], ["Zb0"])
            pi = 0
            zi = 0
            for lvl in range(7):
                Pc, PTc = Pw[pi], Pw[pi + 1]
                Pk, PTk = "Pw%d" % pi, "Pw%d" % (pi + 1)
                Zc, Zn = Zb[zi], Zb[1 - zi]
                ps, psb, pk = nps()
                for h in range(8):
                    mm(ps[:, h * 64:(h + 1) * 64], identb[:], Zc[:, h, :], True, False, ["identb", "Zb%d" % zi], [pk])
                    mm(ps[:, h * 64:(h + 1) * 64], PTc[:, h, :], Zc[:, h, :], False, True, [PTk, "Zb%d" % zi], [pk])
                if lvl < 6:
                    act(lambda e, ps=ps, Zn=Zn: e.copy(out=Zn[:], in_=ps[:, 0:512].rearrange("p (h d) -> p h d", h=8)), [pk], ["Zb%d" % (1 - zi)])
                    zi = 1 - zi
                    ni = 2 - pi
                    Pn, PTn = Pw[ni], Pw[ni + 1]
                    psA, _, pkA = nps()
                    for h in range(8):
                        mm(psA[:, h * 128:(h + 1) * 128], PTc[:, h, :], Pc[:, h, :], True, True, [PTk, Pk], [pkA])
                    for a_ in range(2):
                        dve(lambda e, psA=psA, Pn=Pn, a_=a_: e.tensor_copy(out=Pn[:, 4 * a_:4 * a_ + 4, :], in_=psA[:, a_ * 512:(a_ + 1) * 512].rearrange("p (h t) -> p h t", h=4)), [pkA], ["Pw%d" % ni])
                    psB, _, pkB = nps()
                    for h in range(8):
                        mm(psB[:, h * 128:(h + 1) * 128], Pc[:, h, :], PTc[:, h, :], True, True, [Pk, PTk], [pkB])
                    for a_ in range(2):
                        act(lambda e, psB=psB, PTn=PTn, a_=a_: e.copy(out=PTn[:, 4 * a_:4 * a_ + 4, :], in_=psB[:, a_ * 512:(a_ + 1) * 512].rearrange("p (h t) -> p h t", h=4)), [pkB], ["Pw%d" % (ni + 1)])
                    pi = ni
                else:
                    act(lambda e, ps=ps: e.activation(out=Ub[:], in_=ps[:, 0:512].rearrange("p (h d) -> p h d", h=8), func=AF.Copy, scale=-1.0), [pk], ["Ub"])
            ps, psb, pk = nps()
            for h in range(8):
                j, hp = h // 2, h % 2
                sl = slice(hp * 64, hp * 64 + 64)
                o_ = ps[:, h * 64:(h + 1) * 64]
                mm(o_, Am[1][:, h, :], Ub[:, h, :], True, False, ["Am1", "Ub"], [pk])
                mm(o_, Am[2][:, h, :], vrw[:, h, :], False, False, ["Am2", "vrw"], [pk])
                mm(o_, FMt[sl, RB, j, :], Sbf[sl, j, :], False, True, ["FMt", "Sbf"], [pk])
            act(lambda e, ps=ps: e.copy(out=ysb[:], in_=ps[:, 0:512]), [pk], ["ysb"])
            ps, psb, pk = nps()
            for j in range(4):
                o_ = ps[:, j * 64:(j + 1) * 64]
                mm(o_, Btz[:, 2 * j, :], Ub[:, 2 * j, :], True, False, ["Btz", "Ub"], [pk])
                mm(o_, Ktz[:, 2 * j, :], vrw[:, 2 * j, :], False, False, ["Ktz", "vrw"], [pk])
                mm(o_, Btz[:, 2 * j + 1, :], Ub[:, 2 * j + 1, :], False, False, ["Btz", "Ub"], [pk])
                mm(o_, Ktz[:, 2 * j + 1, :], vrw[:, 2 * j + 1, :], False, True, ["Ktz", "vrw"], [pk])
            dve(lambda e: e.tensor_tensor(out=Sst[:], in0=Sst[:], in1=WLfm[:].unsqueeze(2).to_broadcast([128, 4, 64]), op=ALU.mult), ["Sst", "WLfm"], ["Sst"])
            dve(lambda e, ps=ps: e.tensor_tensor(out=Sst[:], in0=Sst[:], in1=ps[:, 0:256].rearrange("p (j d) -> p j d", j=4), op=ALU.add), [pk, "Sst"], ["Sst"])
            act(lambda e: e.copy(out=Sbf[:], in_=Sst[:]), ["Sst"], ["Sbf"])
            if debug and b == 0:
                dtl = [("r_sb", r_sb), ("kf_sb", kf_sb), ("vf", vf), ("wsig", wsig), ("a_sb", a_sb), ("g_sb", g_sb), ("kap", kap), ("ktl", ktl), ("bvec", bvec), ("ysb", ysb)]
                for i_, (k_, t_) in enumerate(dtl):
                    dma("pool", dbg[:, i_, :], t_[:], [k_], [], "dbg")
                dma("pool", dbg[:, 10, 0:256], Sst[:].rearrange("p j d -> p (j d)"), ["Sst"], [], "dbg")
                dstg = G[9]
                dve(lambda e: e.tensor_copy(out=dstg[:], in_=Ub[:].rearrange("p h d -> p (h d)")), ["Ub"], ["e1"])
                dma("pool", dbg[:, 11, :], dstg[:], ["e1"], [], "dbg")
                dve(lambda e: e.tensor_copy(out=dstg[:], in_=Am[1][:, 0:4, :].rearrange("p h d -> p (h d)")), ["Am1"], ["e1"])
                dma("pool", dbg[:, 12, :], dstg[:], ["e1"], [], "dbg")
                dve(lambda e: e.tensor_copy(out=dstg[:], in_=TMb[:, 0, :]), ["TMb0"], ["e1"])
                dma("pool", dbg[:, 13, :], dstg[:], ["e1"], [], "dbg")
                dve(lambda e: e.tensor_copy(out=dstg[:], in_=TMb[:, 2, :]), ["TMb2"], ["e1"])
                dma("pool", dbg[:, 14, :], dstg[:], ["e1"], [], "dbg")
                PP[0].max_ops = len(PP[0].ops)
            head_rw(128, ysb, "ysb", bon, "bon", vf, "vf", g_sb, "g_sb", mix, "mix", W)
            out_proj(128, mix, "mix", x_, xk, b * 128, W)
            if b + 1 < NB:
                pool(lambda e, xn=xn: e.tensor_copy(out=xn[:, :, 0:1], in_=xn[:, :, 128:129]), [xnk], [xnk])
                pool(lambda e, qx=qx: e.tensor_copy(out=qx[:, :, 0:3], in_=qx[:, :, 128:131]), [qxk], [qxk])

        ps, psb, pk = nps()
        mm(ps[0:8, 0:128], runmax[:], IDF, True, True, ["runmax", "pA"], [pk])
        mm(ps[0:8, 128:256], nBc[:], IDF, True, True, ["nBc", "pA"], [pk])
        fs = TT("fs", [8, 16])
        dve(lambda e, ps=ps: e.tensor_reduce(out=fs[:, 0:1], in_=ps[0:8, 0:128], axis=AX.X, op=ALU.max), [pk], ["fs"])
        dve(lambda e: e.tensor_scalar_max(out=fs[:, 0:1], in0=fs[:, 0:1], scalar1=0.0), ["fs"], ["fs"])
        dve(lambda e, ps=ps: e.tensor_tensor(out=fs[:, 1:2], in0=fs[:, 0:1], in1=ps[0:8, 128:129], op=ALU.subtract), [pk, "fs"], ["fs"])
        dma("pool", om, fs[:, 1:2], ["fs"], [], "fin")
        act(lambda e: e.activation(out=fs[:, 2:3], in_=fs[:, 1:2], func=AF.Exp, scale=-1.0), ["fs"], ["fs"])
        dve(lambda e: e.tensor_scalar_mul(out=fs[:, 4:8], in0=pA[0:8, OFF["rsel"][0]:OFF["rsel"][1]], scalar1=fs[:, 2:3]), ["fs", "pA"], ["fs"])
        ps, psb, pk = nps()
        mm(ps[:, 0:4], pA[0:8, OFF["lsel"][0]:OFF["lsel"][1]], fs[:, 4:8], True, True, ["pA", "fs"], [pk])
        scb = TT("scb", [128, 4])
        act(lambda e, ps=ps: e.copy(out=scb[:], in_=ps[:, 0:4]), [pk], ["scb"])
        dve(lambda e: e.tensor_tensor(out=Cst[:], in0=Cst[:], in1=scb[:].unsqueeze(2).to_broadcast([128, 4, 65]), op=ALU.mult), ["Cst", "scb"], ["Cst"])
        dma("pool", oC, Cst[:], ["Cst"], [], "fin")
        dma("pool", oS, Sst[:], ["Sst"], [], "fin")
        PP[0].finalize()
        PP[0] = Prog(ctx)

    with contextlib.ExitStack() as es_s:
        cur[0] = es_s
        if do_sample:
            W = {}
            W["xm"] = TT("s_xm", [128, D])
            junk = W["xm"]
            st = TT("s_st", [128, 4])
            mix = TT("s_mix", [128, D], BF16)
            xsb = mix
            lor = TT("s_lor", [128, 256], BF16)
            lorT = TT("s_lorT", [128, 2, 128], BF16)
            W["mixT"] = TT("s_mixT", [128, 8, 128], BF16)
            sx = TT("sx", [NS, D])
            sxT = TT("sxT", [128, 8, NS], BF16)
            spj = TT("spj", [NS, INW])
            spk = TT("spk", [128, NSP])
            sl_t = TT("sl_t", [NS, 3, 256])
            Cs = TT("Cs", [128, 4096])
            Ss = Cs
            sn_t = TT("sn_t", [128, 64])
            sm_t = TT("sm_t", [128, 1])
            scv = TT("scv", [128, 2, 4, 64])
            ssh = TT("ssh", [128, 3, 64])
            dma("sp", sx[:], xs, [], ["sx"], "sin", True)
            dma("sp", spk[:], spk_d, [], ["spk"], "sin", True)
            dma("sp", sl_t[:, 0, :], sshl_d, [], ["sl_t"], "sin", True)
            dma("sp", sl_t[:, 1, :], mul_d, [], ["sl_t"], "sin", True)
            dma("sp", Cs[:], sC_d, [], ["Cs"], "sin", True)
            dma("sp", sn_t[:], sn_d, [], ["sn_t"], "sin", True)
            dma("sp", sm_t[:], sm_d, [], ["sm_t"], "sin", True)
            dma("sp", scv[:, :, 0:3, :], sconv_d, [], ["scv"], "sin", True)
            dma("sp", ssh[:], sshift_d, [], ["ssh"], "sin", True)
            rmsnorm_T(sx, "sx", NS, sxT, "sxT", 0, "nmw", xsb, "xsb", junk, "junk", st, "st")
            for n0 in range(0, INW, 512):
                nn = min(512, INW - n0)
                ps, psb, pk = nps()
                if n0 + nn <= MLW or n0 < MLW:
                    pass
                segs = []
                a0 = n0
                while a0 < n0 + nn:
                    if a0 < MLW:
                        a1 = min(n0 + nn, MLW)
                        segs.append((a0, a1, False))
                    else:
                        a1 = n0 + nn
                        segs.append((a0, a1, True))
                    a0 = a1
                for (a0, a1, rw) in segs:
                    o_ = ps[:NS, a0 - n0:a1 - n0]
                    if not rw:
                        for c in range(8):
                            mm(o_, sxT[:, c, :], wq[:, c, a0:a1], c == 0, c == 7, ["sxT", "wq"], [pk])
                    else:
                        for c in range(8):
                            mm(o_, sxT[:, c, :], W1[:, c, a0 - MLW:a1 - MLW], c == 0, False, ["sxT", "W1"], [pk])
                        for c in range(8):
                            mm(o_, sxT[:, c, :], W2[:, c, a0 - MLW:a1 - MLW], False, c == 7, ["sxT", "W2"], [pk])
                act(lambda e, ps=ps, n0=n0, nn=nn: e.copy(out=spj[:, n0:n0 + nn], in_=ps[:NS, 0:nn]), [pk], ["spj"])
            s1v = scr1.rearrange("(b h) a d -> b a h d", b=NS)
            for a_ in range(7):
                c0_ = a_ * 512 if a_ < 4 else MLW + (a_ - 4) * 512
                dma("pool", s1v[:, a_, :, :], spj[:, c0_:c0_ + 512].rearrange("p (h d) -> p h d", h=8), ["spj"], ["scr1"], "scrw1")
            A7 = TT("A7", [128, 8, 64])
            dma("sp", A7[:, 0:7, :], scr1[:, 0:7, :], ["scr1"], ["A7"], "scrr1")
            gif = TT("gif", [128, 2])
            s_if = nc.dram_tensor("scr_if", [2, 128], F32, kind="Internal").ap()
            for g_ in range(2):
                dma("pool", s_if[g_, :].rearrange("(b h) -> b h", b=NS), spj[:, 2048 + 8 * g_:2056 + 8 * g_], ["spj"], ["scr_if"], "scrwif")
            for g_ in range(2):
                dma("sp", gif[:, g_:g_ + 1], s_if[g_, :].rearrange("(p o) -> p o", o=1), ["scr_if"], ["gif"], "scrrif")
            SP_ = lambda n: spk[:, SOFF[n][0]:SOFF[n][1]]
            pl = spj[:, MLW + 1536:MLW + 1792]
            dma("pool", osshl, pl, ["spj"], [], "fin")
            dve(lambda e: e.tensor_tensor(out=sl_t[:, 2, :], in0=sl_t[:, 0, :], in1=pl, op=ALU.subtract), ["sl_t", "spj"], ["sl_t"])
            dve(lambda e: e.tensor_tensor(out=sl_t[:, 2, :], in0=sl_t[:, 2, :], in1=sl_t[:, 1, :], op=ALU.mult), ["sl_t"], ["sl_t"])
            dve(lambda e: e.tensor_tensor(out=sl_t[:, 2, :], in0=sl_t[:, 2, :], in1=pl, op=ALU.add), ["sl_t", "spj"], ["sl_t"])
            act(lambda e: e.activation(out=lor[:NS, 0:64], in_=sl_t[:, 2, 0:64], func=AF.Tanh), ["sl_t"], ["lor"])
            act(lambda e: e.copy(out=lor[:NS, 64:128], in_=sl_t[:, 2, 64:128]), ["sl_t"], ["lor"])
            act(lambda e: e.activation(out=lor[:NS, 128:256], in_=sl_t[:, 2, 128:256], func=AF.Sigmoid), ["sl_t"], ["lor"])
            ps, psb, pk = nps()
            pe(lambda e, psb=psb: e.transpose(psb[:, 0:NS], lor[:NS, 0:128], identb[:NS, :NS]), ["lor", "identb"], [pk])
            pe(lambda e, psb=psb: e.transpose(psb[:, 128:128 + NS], lor[:NS, 128:256], identb[:NS, :NS]), ["lor", "identb"], [pk])
            act(lambda e, psb=psb: e.copy(out=lorT[:, :, 0:NS], in_=psb[:, 0:256].rearrange("p (a t) -> p a t", a=2)[:, :, 0:NS]), [pk], ["lorT"])
            ps, psb, pk = nps()
            mm(ps[:NS, 0:512], lorT[0:64, 0, 0:NS], luw[0:64, :], True, True, ["lorT", "luw"], [pk])
            mm(ps[:NS, 512:1024], lorT[64:128, 0, 0:NS], luw[64:128, :], True, True, ["lorT", "luw"], [pk])
            ps2, _, pk2 = nps()
            mm(ps2[:NS, 0:512], lorT[:, 1, 0:NS], gup[:, :], True, True, ["lorT", "gup"], [pk2])
            lo3 = spj[:, 0:1536].rearrange("p (a n) -> p a n", a=3)
            for a_ in range(2):
                act(lambda e, ps=ps, a_=a_: e.copy(out=lo3[:, a_, :], in_=ps[:NS, a_ * 512:(a_ + 1) * 512]), [pk], ["lo3"])
            act(lambda e, ps2=ps2: e.copy(out=lo3[:, 2, :], in_=ps2[:NS, 0:512]), [pk2], ["lo3"])
            for a_ in range(3):
                dma("pool", scr2.rearrange("(b h) a d -> b a h d", b=NS)[:, a_, :, :], lo3[:, a_, :].rearrange("p (h d) -> p h d", h=8), ["lo3"], ["scr2"], "scrw2")
            L3 = TT("L3", [128, 3, 64])
            dma("sp", L3[:], scr2, ["scr2"], ["L3"], "scrr2")
            big = TT("big", [128, 4096])
            sv = TT("sv", [128, 64])
            pool(lambda e: e.tensor_copy(out=scv[:, :, 3, :], in_=A7[:, 0:2, :]), ["A7"], ["scv"])
            dma("pool", osconv, scv[:, :, 1:4, :], ["scv"], [], "fin")
            qk_s = TT("qk_s", [128, 2, 64])
            cwqk = lambda w_: spk[:, SOFF["cwq"][0] + w_ * 256:SOFF["cwq"][0] + (w_ + 1) * 256].rearrange("p (j d) -> p j d", j=4)
            for w_ in range(2):
                dve(lambda e, w_=w_: e.tensor_tensor(out=big[:, 0:256].rearrange("p (j d) -> p j d", j=4), in0=scv[:, w_, :, :], in1=cwqk(w_), op=ALU.mult), ["scv", "spk"], ["big"])
                dve(lambda e, w_=w_: e.tensor_reduce(out=qk_s[:, w_, :], in_=big[:, 0:256].rearrange("p (j d) -> p d j", j=4), axis=AX.X, op=ALU.add), ["big"], ["qk_s"])
            dve(lambda e: e.tensor_tensor(out=qk_s[:], in0=qk_s[:], in1=spk[:, SOFF["cbq"][0]:SOFF["cbk"][1]].rearrange("p (a d) -> p a d", a=2), op=ALU.add), ["qk_s", "spk"], ["qk_s"])
            act(lambda e: e.activation(out=qk_s[:], in_=qk_s[:], func=AF.Silu), ["qk_s"], ["qk_s"])
            act(lambda e: e.activation(out=qk_s[:, 1, :], in_=qk_s[:, 1, :], func=AF.Copy, scale=0.125), ["qk_s"], ["qk_s"])
            dve(lambda e: e.tensor_tensor(out=sv[:, 0:2], in0=gif[:], in1=spk[:, SOFF["ib"][0]:SOFF["fb"][1]], op=ALU.add), ["gif", "spk"], ["sv"])
            act(lambda e: e.activation(out=sv[:, 9:10], in_=sv[:, 1:2], func=AF.Exp, scale=-1.0), ["sv"], ["sv"])
            act(lambda e: e.activation(out=sv[:, 2:3], in_=sv[:, 9:10], func=AF.Ln, bias=1.0, scale=1.0), ["sv"], ["sv"])
            dve(lambda e: e.tensor_tensor(out=sv[:, 3:4], in0=sm_t[:], in1=sv[:, 2:3], op=ALU.subtract), ["sv", "sm_t"], ["sv"])
            dve(lambda e: e.tensor_tensor(out=sv[:, 4:5], in0=sv[:, 3:4], in1=sv[:, 0:1], op=ALU.max), ["sv"], ["sv"])
            dma("pool", osm, sv[:, 4:5], ["sv"], [], "fin")
            dve(lambda e: e.tensor_tensor(out=sv[:, 9:10], in0=sv[:, 0:1], in1=sv[:, 4:5], op=ALU.subtract), ["sv"], ["sv"])
            act(lambda e: e.activation(out=sv[:, 5:6], in_=sv[:, 9:10], func=AF.Exp), ["sv"], ["sv"])
            dve(lambda e: e.tensor_tensor(out=sv[:, 9:10], in0=sv[:, 3:4], in1=sv[:, 4:5], op=ALU.subtract), ["sv"], ["sv"])
            act(lambda e: e.activation(out=sv[:, 6:7], in_=sv[:, 9:10], func=AF.Exp), ["sv"], ["sv"])
            act(lambda e: e.activation(out=sv[:, 7:8], in_=sv[:, 4:5], func=AF.Exp, scale=-1.0), ["sv"], ["sv"])
            q_ = qk_s[:, 0, :]
            k_ = qk_s[:, 1, :]
            v_ = A7[:, 2, :]
            b3 = lambda t: t[:, :].rearrange("p (a c) -> p a c", a=64)
            pool(lambda e: e.tensor_tensor(out=b3(big), in0=k_.unsqueeze(2).to_broadcast([128, 64, 64]), in1=v_.unsqueeze(1).to_broadcast([128, 64, 64]), op=ALU.mult), ["qk_s", "A7"], ["big"])
            dve(lambda e: e.tensor_scalar_mul(out=Cs[:], in0=Cs[:], scalar1=sv[:, 6:7]), ["Cs", "sv"], ["Cs"])
            dve(lambda e: e.scalar_tensor_tensor(out=Cs[:], in0=big[:], scalar=sv[:, 5:6], in1=Cs[:], op0=ALU.mult, op1=ALU.add), ["big", "sv", "Cs"], ["Cs"])
            dma("pool", osC, Cs[:], ["Cs"], [], "fin")
            dve(lambda e: e.tensor_scalar_mul(out=sn_t[:], in0=sn_t[:], scalar1=sv[:, 6:7]), ["sn_t", "sv"], ["sn_t"])
            dve(lambda e: e.scalar_tensor_tensor(out=sn_t[:], in0=k_, scalar=sv[:, 5:6], in1=sn_t[:], op0=ALU.mult, op1=ALU.add), ["qk_s", "sv", "sn_t"], ["sn_t"])
            dma("pool", osn, sn_t[:], ["sn_t"], [], "fin")
            pool(lambda e: e.tensor_tensor(out=b3(big), in0=Cs[:, :].rearrange("p (k v) -> p v k", k=64), in1=q_.unsqueeze(1).to_broadcast([128, 64, 64]), op=ALU.mult), ["Cs", "qk_s"], ["big"])
            hs = TT("hs", [128, 2, 64])
            dve(lambda e: e.tensor_reduce(out=hs[:, 0, :], in_=b3(big), axis=AX.X, op=ALU.add), ["big"], ["hs"])
            dve(lambda e: e.tensor_tensor(out=sv[:, 16:80 - 16] if False else big[:, 0:64], in0=q_, in1=sn_t[:], op=ALU.mult), ["qk_s", "sn_t"], ["big"])
            dve(lambda e: e.tensor_reduce(out=sv[:, 8:9], in_=big[:, 0:64], axis=AX.X, op=ALU.add), ["big"], ["sv"])
            dve(lambda e: e.scalar_tensor_tensor(out=sv[:, 9:10], in0=sv[:, 8:9], scalar=-1.0, in1=sv[:, 8:9], op0=ALU.mult, op1=ALU.max), ["sv"], ["sv"])
            dve(lambda e: e.tensor_tensor(out=sv[:, 9:10], in0=sv[:, 9:10], in1=sv[:, 7:8], op=ALU.max), ["sv"], ["sv"])
            dve(lambda e: e.reciprocal(out=sv[:, 10:11], in_=sv[:, 9:10]), ["sv"], ["sv"])
            dve(lambda e: e.tensor_scalar_mul(out=hs[:, 0, :], in0=hs[:, 0, :], scalar1=sv[:, 10:11]), ["hs", "sv"], ["hs"])
            dma("pool", osshift, A7[:, 4:7, :], ["A7"], [], "fin")
            rk3 = TT("rk3", [128, 3, 64])
            mu3 = spk[:, SOFF["mu_r"][0]:SOFF["mu_v"][1]].rearrange("p (a d) -> p a d", a=3)
            dve(lambda e: e.tensor_tensor(out=rk3[:], in0=ssh[:], in1=A7[:, 4:7, :], op=ALU.subtract), ["ssh", "A7"], ["rk3"])
            dve(lambda e: e.tensor_tensor(out=rk3[:], in0=rk3[:], in1=mu3, op=ALU.mult), ["rk3", "spk"], ["rk3"])
            dve(lambda e: e.tensor_tensor(out=rk3[:], in0=rk3[:], in1=A7[:, 4:7, :], op=ALU.add), ["rk3", "A7"], ["rk3"])
            w8 = TT("w8", [128, 8, 64])
            dve(lambda e: e.tensor_tensor(out=w8[:, 0, :], in0=L3[:, 0, :], in1=SP_("w0"), op=ALU.add), ["L3", "spk"], ["w8"])
            act(lambda e: e.activation(out=w8[:, 0, :], in_=w8[:, 0, :], func=AF.Sigmoid), ["w8"], ["w8"])
            act(lambda e: e.activation(out=w8[:, 0, :], in_=w8[:, 0, :], func=AF.Exp, scale=-C0), ["w8"], ["w8"])
            dve(lambda e: e.tensor_tensor(out=w8[:, 1, :], in0=L3[:, 1, :], in1=SP_("a0"), op=ALU.add), ["L3", "spk"], ["w8"])
            act(lambda e: e.activation(out=w8[:, 1, :], in_=w8[:, 1, :], func=AF.Sigmoid), ["w8"], ["w8"])
            dve(lambda e: e.tensor_tensor(out=w8[:, 6, :], in0=rk3[:, 1, :], in1=SP_("kk"), op=ALU.mult), ["rk3", "spk"], ["w8"])
            dve(lambda e: e.tensor_tensor(out=w8[:, 7, :], in0=w8[:, 6, :], in1=w8[:, 6, :], op=ALU.mult), ["w8"], ["w8"])
            dve(lambda e: e.tensor_reduce(out=sv[:, 11:12], in_=w8[:, 7, :], axis=AX.X, op=ALU.add), ["w8"], ["sv"])
            dve(lambda e: e.tensor_scalar_max(out=sv[:, 11:12], in0=sv[:, 11:12], scalar1=1e-24), ["sv"], ["sv"])
            act(lambda e: e.activation(out=sv[:, 11:12], in_=sv[:, 11:12], func=AF.Sqrt), ["sv"], ["sv"])
            dve(lambda e: e.reciprocal(out=sv[:, 11:12], in_=sv[:, 11:12]), ["sv"], ["sv"])
            dve(lambda e: e.tensor_scalar_mul(out=w8[:, 3, :], in0=w8[:, 6, :], scalar1=sv[:, 11:12]), ["w8", "sv"], ["w8"])
            dve(lambda e: e.tensor_tensor(out=w8[:, 6, :], in0=w8[:, 1, :], in1=SP_("ka"), op=ALU.mult), ["w8", "spk"], ["w8"])
            dve(lambda e: e.tensor_tensor(out=w8[:, 6, :], in0=w8[:, 6, :], in1=SP_("ka"), op=ALU.subtract), ["w8", "spk"], ["w8"])
            dve(lambda e: e.tensor_scalar_add(out=w8[:, 6, :], in0=w8[:, 6, :], scalar1=1.0), ["w8"], ["w8"])
            dve(lambda e: e.tensor_tensor(out=w8[:, 4, :], in0=rk3[:, 1, :], in1=w8[:, 6, :], op=ALU.mult), ["rk3", "w8"], ["w8"])
            dve(lambda e: e.tensor_tensor(out=w8[:, 5, :], in0=w8[:, 1, :], in1=w8[:, 3, :], op=ALU.mult), ["w8"], ["w8"])
            dma("sp", Ss[:], sS_d, [], ["Ss"], "sin2")
            bk = lambda ap: ap.unsqueeze(1).to_broadcast([128, 64, 64])
            bv = lambda ap: ap.unsqueeze(2).to_broadcast([128, 64, 64])
            pool(lambda e: e.tensor_tensor(out=b3(big), in0=b3(Ss), in1=bk(w8[:, 3, :]), op=ALU.mult), ["Ss", "w8"], ["big"])
            dve(lambda e: e.tensor_reduce(out=w8[:, 7, :], in_=b3(big), axis=AX.X, op=ALU.add), ["big"], ["w8"])
            dve(lambda e: e.tensor_tensor(out=b3(Ss), in0=b3(Ss), in1=bk(w8[:, 0, :]), op=ALU.mult), ["Ss", "w8"], ["Ss"])
            pool(lambda e: e.tensor_tensor(out=b3(big), in0=bv(w8[:, 7, :]), in1=bk(w8[:, 5, :]), op=ALU.mult), ["w8"], ["big"])
            dve(lambda e: e.tensor_tensor(out=Ss[:], in0=Ss[:], in1=big[:], op=ALU.subtract), ["Ss", "big"], ["Ss"])
            pool(lambda e: e.tensor_tensor(out=b3(big), in0=bv(rk3[:, 2, :]), in1=bk(w8[:, 4, :]), op=ALU.mult), ["rk3", "w8"], ["big"])
            dve(lambda e: e.tensor_tensor(out=Ss[:], in0=Ss[:], in1=big[:], op=ALU.add), ["Ss", "big"], ["Ss"])
            dma("pool", osS, Ss[:], ["Ss"], [], "fin")
            pool(lambda e: e.tensor_tensor(out=b3(big), in0=b3(Ss), in1=bk(rk3[:, 0, :]), op=ALU.mult), ["Ss", "rk3"], ["big"])
            dve(lambda e: e.tensor_reduce(out=hs[:, 1, :], in_=b3(big), axis=AX.X, op=ALU.add), ["big"], ["hs"])
            dve(lambda e: e.tensor_tensor(out=w8[:, 6, :], in0=rk3[:, 0, :], in1=w8[:, 4, :], op=ALU.mult), ["rk3", "w8"], ["w8"])
            dve(lambda e: e.tensor_tensor(out=w8[:, 6, :], in0=w8[:, 6, :], in1=SP_("rk"), op=ALU.mult), ["w8", "spk"], ["w8"])
            dve(lambda e: e.tensor_reduce(out=sv[:, 12:13], in_=w8[:, 6, :], axis=AX.X, op=ALU.add), ["w8"], ["sv"])
            dve(lambda e: e.scalar_tensor_tensor(out=hs[:, 1, :], in0=rk3[:, 2, :], scalar=sv[:, 12:13], in1=hs[:, 1, :], op0=ALU.mult, op1=ALU.add), ["rk3", "sv", "hs"], ["hs"])
            act(lambda e: e.activation(out=w8[:, 6, :], in_=A7[:, 3, :], func=AF.Sigmoid), ["A7"], ["w8"])
            dve(lambda e: e.tensor_tensor(out=hs[:, 0, :], in0=hs[:, 0, :], in1=w8[:, 6, :], op=ALU.mult), ["hs", "w8"], ["hs"])
            dve(lambda e: e.tensor_tensor(out=w8[:, 7, :], in0=hs[:, 0, :], in1=hs[:, 0, :], op=ALU.mult), ["hs"], ["w8"])
            dve(lambda e: e.tensor_reduce(out=sv[:, 13:14], in_=w8[:, 7, :], axis=AX.X, op=ALU.add), ["w8"], ["sv"])
            act(lambda e: e.activation(out=sv[:, 13:14], in_=sv[:, 13:14], func=AF.Sqrt, bias=EPS, scale=1.0 / 64), ["sv"], ["sv"])
            dve(lambda e: e.reciprocal(out=sv[:, 13:14], in_=sv[:, 13:14]), ["sv"], ["sv"])
            dve(lambda e: e.scalar_tensor_tensor(out=hs[:, 0, :], in0=hs[:, 0, :], scalar=sv[:, 13:14], in1=SP_("mnw"), op0=ALU.mult, op1=ALU.mult), ["hs", "sv", "spk"], ["hs"])
            dve(lambda e: e.tensor_reduce(out=sv[:, 14:15], in_=hs[:, 1, :], axis=AX.X, op=ALU.add), ["hs"], ["sv"])
            dve(lambda e: e.tensor_scalar_mul(out=sv[:, 14:15], in0=sv[:, 14:15], scalar1=1.0 / 64), ["sv"], ["sv"])
            dve(lambda e: e.tensor_scalar_sub(out=hs[:, 1, :], in0=hs[:, 1, :], scalar1=sv[:, 14:15]), ["hs", "sv"], ["hs"])
            dve(lambda e: e.tensor_tensor(out=w8[:, 7, :], in0=hs[:, 1, :], in1=hs[:, 1, :], op=ALU.mult), ["hs"], ["w8"])
            dve(lambda e: e.tensor_reduce(out=sv[:, 15:16], in_=w8[:, 7, :], axis=AX.X, op=ALU.add), ["w8"], ["sv"])
            act(lambda e: e.activation(out=sv[:, 15:16], in_=sv[:, 15:16], func=AF.Sqrt, bias=GN_EPS, scale=1.0 / 64), ["sv"], ["sv"])
            dve(lambda e: e.reciprocal(out=sv[:, 15:16], in_=sv[:, 15:16]), ["sv"], ["sv"])
            dve(lambda e: e.scalar_tensor_tensor(out=hs[:, 1, :], in0=hs[:, 1, :], scalar=sv[:, 15:16], in1=SP_("lnw"), op0=ALU.mult, op1=ALU.mult), ["hs", "sv", "spk"], ["hs"])
            dve(lambda e: e.tensor_tensor(out=hs[:, 1, :], in0=hs[:, 1, :], in1=SP_("lnb"), op=ALU.add), ["hs", "spk"], ["hs"])
            dve(lambda e: e.tensor_tensor(out=hs[:, 1, :], in0=hs[:, 1, :], in1=L3[:, 2, :], op=ALU.mult), ["hs", "L3"], ["hs"])
            s3v = nc.dram_tensor("scr3b", [128, 2, 64], F32, kind="Internal").ap()
            dma("pool", s3v, hs[:], ["hs"], ["scr3b"], "scrw3")
            smix = spj[:, 2304:3328].rearrange("p (a h d) -> p a h d", a=2, h=8)
            for a_ in range(2):
                dma("sp", smix[:, a_, :, :], s3v.rearrange("(b h) a d -> b a h d", b=NS)[:, a_, :, :], ["scr3b"], ["smix"], "scrr3")
            act(lambda e: e.copy(out=mix[:NS, :], in_=smix[:].rearrange("p a h d -> p (a h d)")), ["smix"], ["mix"])
            out_proj(NS, mix, "mix", sx, "sx", T, W)
        PP[0].finalize()
        PP[0] = Prog(ctx)
    es_res.close()

    with contextlib.ExitStack() as es2:
        cur[0] = es2
        upb = TT("upb", [128, 8, DFF], BF16)
        dnb = TT("dnb", [128, 32, D], BF16)
        pA2 = TT("pA2", [128, NA], F32)
        nfw = TT("nfw", [128, D])
        identb2 = TT("identb2", [128, 128], BF16)
        up_v = mlp_up.rearrange("(c p) n -> p c n", p=128)
        dn_v = mlp_down.rearrange("(c p) n -> p c n", p=128)
        dma("sp", pA2[:], packA_d, [], ["pA2"], "init2", True)
        dma("sp", nfw[:], nfw_d, [], ["nfw"], "init2", True)
        for c in range(8):
            dma("pool", upb[:, c, :], up_v[:, c, :], [], ["upb"], "init2", True)
        for c in range(0, 32, 4):
            dma("pool", dnb[:, c:c + 4, :], dn_v[:, c:c + 4, :], [], ["dnb"], "init2", True)
        dve(lambda e: e.tensor_copy(out=identb2[:], in_=pA2[:, OFF["ident"][0]:OFF["ident"][1]]), ["pA2"], ["identb2"])
        xm2 = [TT("xm2_%d" % i, [128, D]) for i in range(2)]
        junk2 = TT("junk2", [128, D])
        st2 = TT("st2", [128, 8])
        xsb2 = TT("xsb2", [128, D], BF16)
        xn2T = TT("xn2T", [128, 8, 128], BF16)
        hT = TT("hT", [128, 32, 128], BF16)
        rl = [TT("rl%d" % i, [128, 1024]) for i in range(2)]
        yo = [TT("yo%d" % i, [128, D]) for i in range(2)]
        nmlp = pA2[:, OFF["nmlp"][0]:OFF["nmlp"][1]]
        for b in range(NB + 1):
            nt = 128 if b < NB else NS
            row0 = b * 128
            x_ = xm2[b % 2]
            xk = "xm2_%d" % (b % 2)
            dma("sp", x_[:nt, :], xmid_d[row0:row0 + nt, :], ["xmid%d" % row0], [xk], xk)
            act(lambda e, x_=x_, nt=nt: e.activation(out=junk2[:nt, :], in_=x_[:nt, :], func=AF.Square, accum_out=st2[:nt, 0:1]), [xk], ["junk2", "st2"])
            act(lambda e, nt=nt: e.activation(out=st2[:nt, 1:2], in_=st2[:nt, 0:1], func=AF.Sqrt, bias=EPS, scale=1.0 / D), ["st2"], ["st2"])
            dve(lambda e, nt=nt: e.reciprocal(out=st2[:nt, 2:3], in_=st2[:nt, 1:2]), ["st2"], ["st2"])
            dve(lambda e, x_=x_, nt=nt: e.tensor_scalar_mul(out=xsb2[:nt, :], in0=x_[:nt, :], scalar1=st2[:nt, 2:3]), [xk, "st2"], ["xsb2"])
            ps, psb, pk = nps()
            for c in range(8):
                pe(lambda e, c=c, psb=psb, nt=nt: e.transpose(psb[:, c * 128:c * 128 + nt], xsb2[:nt, c * 128:(c + 1) * 128], identb2[:nt, :nt]), ["xsb2", "identb2"], [pk])
            dve(lambda e, psb=psb, nt=nt: e.tensor_tensor(out=xn2T[:, :, 0:nt], in0=psb[:, 0:1024].rearrange("p (c t) -> p c t", c=8)[:, :, 0:nt],
                                                        in1=nmlp.unsqueeze(2).to_broadcast([128, 8, nt]), op=ALU.mult), [pk, "pA2"], ["xn2T"])
            for g8 in range(4):
                ps, psb, pk = nps()
                for jj in range(8):
                    j = g8 * 8 + jj
                    for c in range(8):
                        mm(ps[:, jj * 128:jj * 128 + nt], upb[:, c, j * 128:(j + 1) * 128], xn2T[:, c, 0:nt], c == 0, c == 7, ["upb", "xn2T"], [pk])
                r_ = rl[g8 % 2]
                rk_ = "rl%d" % (g8 % 2)
                psv = ps[:, :].rearrange("p (j t) -> p j t", j=8)[:, :, 0:nt]
                rv = r_[:, :].rearrange("p (j t) -> p j t", j=8)[:, :, 0:nt]
                for a_ in range(2):
                    act(lambda e, psv=psv, rv=rv, a_=a_: e.activation(out=rv[:, 4 * a_:4 * a_ + 4, :], in_=psv[:, 4 * a_:4 * a_ + 4, :], func=AF.Relu), [pk], [rk_])
                pool(lambda e, rv=rv, g8=g8, nt=nt: e.tensor_tensor(out=hT[:, g8 * 8:(g8 + 1) * 8, 0:nt], in0=rv, in1=rv, op=ALU.mult), [rk_], ["hT"])
            ps, psb, pk = nps()
            for n in range(2):
                for j in range(32):
                    mm(ps[:nt, n * 512:(n + 1) * 512], hT[:, j, 0:nt], dnb[:, j, n * 512:(n + 1) * 512], j == 0, j == 31, ["hT", "dnb"], [pk])
            y_ = yo[b % 2]
            yk = "yo%d" % (b % 2)
            for a_ in range(2):
                dve(lambda e, ps=ps, x_=x_, y_=y_, nt=nt, a_=a_: e.tensor_tensor(out=y_[:nt, a_ * 512:(a_ + 1) * 512], in0=ps[:nt, a_ * 512:(a_ + 1) * 512], in1=x_[:nt, a_ * 512:(a_ + 1) * 512], op=ALU.add), [pk, xk], [yk])
            act(lambda e, y_=y_, nt=nt: e.activation(out=junk2[:nt, :], in_=y_[:nt, :], func=AF.Square, accum_out=st2[:nt, 4:5]), [yk], ["junk2", "st2"])
            act(lambda e, nt=nt: e.activation(out=st2[:nt, 5:6], in_=st2[:nt, 4:5], func=AF.Sqrt, bias=EPS, scale=1.0 / D), ["st2"], ["st2"])
            dve(lambda e, nt=nt: e.reciprocal(out=st2[:nt, 6:7], in_=st2[:nt, 5:6]), ["st2"], ["st2"])
            dve(lambda e, y_=y_, nt=nt: e.scalar_tensor_tensor(out=y_[:nt, :], in0=y_[:nt, :], scalar=st2[:nt, 6:7], in1=nfw[:nt, :], op0=ALU.mult, op1=ALU.mult), [yk, "st2", "nfw"], [yk])
            if b < NB:
                dma("pool", yp[row0:row0 + nt, :], y_[:nt, :], [yk], [], yk)
            else:
                dma("pool", ys, y_[:nt, :], [yk], [], yk)
        PP[0].finalize()
    es_ps.close()
    ctx.close()
    return nc


_CACHE = {}


def _host_packs(inp, core):
    f = np.float32
    L = 0
    pa = np.zeros((128, NA), f)

    def put(n, arr):
        a, b = OFF[n]
        pa[:, a:b] = arr

    rep = lambda v: np.broadcast_to(np.asarray(v, f).reshape(1, -1), (128, np.asarray(v).size))
    put("mnw", rep(inp["mlstm_norm_w"][L]))
    put("w0", rep(inp["rw_w0"][L]))
    put("a0", rep(inp["rw_a0"][L]))
    put("kk", rep(inp["rw_k_k"][L]))
    put("ka", rep(inp["rw_k_a"][L]))
    put("rk", rep(inp["rw_r_k"][L].reshape(-1)))
    put("lnw", rep(inp["rw_ln_w"][L]))
    put("lnb", rep(inp["rw_ln_b"][L]))
    put("ifb", rep(np.concatenate([inp["mlstm_i_b"][L], inp["mlstm_f_b"][L]])))
    put("nmw", inp["norm_mix_w"][L].reshape(8, 128).T)
    put("nmlp", inp["norm_mlp_w"][L].reshape(8, 128).T)
    cw = inp["mlstm_conv_w"][L]
    put("cw", cw.reshape(4, 8, 128).transpose(2, 1, 0).reshape(128, 32))
    put("cb", inp["mlstm_conv_b"][L].reshape(8, 128).T)
    put("ident", np.eye(128, dtype=f))
    put("mui", np.triu(np.ones((128, 128), f), 0))
    put("mus", np.triu(np.ones((128, 128), f), 1))
    put("mls", np.tril(np.ones((128, 128), f), -1))
    put("ones", np.ones((128, 128), f))
    lsel = np.zeros((128, 128), f)
    rsel = np.zeros((128, 4), f)
    for h in range(8):
        lsel[h, (h % 2) * 64:(h % 2) * 64 + 64] = 1.0
        rsel[h, h // 2] = 1.0
    put("lsel", lsel)
    put("rsel", rsel)
    return pa


def _sample_pack(inp):
    f = np.float32
    L = 0
    sp = np.zeros((128, NSP), f)

    def bh(v512):
        return np.tile(np.asarray(v512, f).reshape(8, 64), (NS, 1))

    def put(n, arr):
        a, b = SOFF[n]
        sp[:, a:b] = arr

    mu = inp["rw_mu"][L]
    put("mu_r", bh(mu[0:512]))
    put("mu_k", bh(mu[512:1024]))
    put("mu_v", bh(mu[1024:1536]))
    cw = inp["mlstm_conv_w"][L]
    put("cwq", np.concatenate([bh(cw[j, 0:512]) for j in range(4)], axis=1))
    put("cwk", np.concatenate([bh(cw[j, 512:1024]) for j in range(4)], axis=1))
    cb = inp["mlstm_conv_b"][L]
    put("cbq", bh(cb[0:512]))
    put("cbk", bh(cb[512:1024]))
    put("mnw", bh(inp["mlstm_norm_w"][L]))
    put("w0", bh(inp["rw_w0"][L]))
    put("a0", bh(inp["rw_a0"][L]))
    put("kk", bh(inp["rw_k_k"][L]))
    put("ka", bh(inp["rw_k_a"][L]))
    put("rk", bh(inp["rw_r_k"][L].reshape(-1)))
    put("lnw", bh(inp["rw_ln_w"][L]))
    put("lnb", bh(inp["rw_ln_b"][L]))
    put("ib", np.tile(inp["mlstm_i_b"][L].reshape(8, 1), (NS, 1)))
    put("fb", np.tile(inp["mlstm_f_b"][L].reshape(8, 1), (NS, 1)))
    return sp


def kernel(**inp):
    f = np.float32
    inp = {k: np.asarray(v) for k, v in inp.items()}
    if "nc" not in _CACHE:
        _CACHE["nc"] = build_program()
    nc = _CACHE["nc"]
    L = 0
    pa = _host_packs(inp, 0)
    sp = _sample_pack(inp)
    mu = inp["rw_mu"][L]
    luw = np.concatenate([inp["rw_w_up"][L], inp["rw_a_up"][L]], axis=0).astype(f)
    common = {
        "w_in": np.ascontiguousarray(inp["w_in"][L], f),
        "w_out": np.ascontiguousarray(inp["w_out"][L], f),
        "mlp_up": np.ascontiguousarray(inp["mlp_up"][L], f),
        "mlp_down": np.ascontiguousarray(inp["mlp_down"][L], f),
        "packA": pa,
        "mu_b": np.ascontiguousarray(np.broadcast_to(mu.reshape(1, -1), (128, RWW)), f),
        "nfw_b": np.ascontiguousarray(np.broadcast_to(inp["norm_f_w"].reshape(1, -1), (128, D)), f),
        "luw": luw,
        "gup": np.ascontiguousarray(inp["rw_g_up"][L], f),
        "spack": sp,
        "mul": np.ascontiguousarray(np.broadcast_to(mu[1536:1792].reshape(1, -1), (NS, 256)), f),
    }
    in_maps = []
    for c in range(8):
        rs = slice(c * NS, (c + 1) * NS)
        m = dict(common)
        m["xp"] = np.ascontiguousarray(inp["x_prompt"][c], f)
        m["xs"] = np.ascontiguousarray(inp["x_sample"][rs, 0, :], f)
        m["sC"] = np.ascontiguousarray(inp["state_mlstm_C"][L, rs].reshape(128, 4096), f)
        m["sn"] = np.ascontiguousarray(inp["state_mlstm_n"][L, rs].reshape(128, 64), f)
        m["sm"] = np.ascontiguousarray(inp["state_mlstm_m"][L, rs].reshape(128, 1), f)
        cv = inp["state_mlstm_conv"][L, rs]
        m["sconv"] = np.ascontiguousarray(cv.reshape(NS, 3, 2, 8, 64).transpose(0, 3, 2, 1, 4).reshape(128, 2, 3, 64), f)
        m["sS"] = np.ascontiguousarray(inp["state_rwkv_S"][L, rs].reshape(128, 4096), f)
        sh = inp["state_rwkv_shift"][L, rs, 0, :]
        m["sshift"] = np.ascontiguousarray(sh[:, 0:1536].reshape(NS, 3, 8, 64).transpose(0, 2, 1, 3).reshape(128, 3, 64), f)
        m["sshl"] = np.ascontiguousarray(sh[:, 1536:1792], f)
        in_maps.append(m)
    res = run_bass_kernel_spmd(nc, in_maps, core_ids=list(range(8)))
    R = res.results
    y_prompt = np.stack([R[c]["yp"] for c in range(8)]).astype(f)
    y_sample = np.concatenate([R[c]["ys"] for c in range(8)], axis=0).reshape(128, 1, D).astype(f)
    pC = np.zeros((1, 8, 8, 64, 64), f)
    pn = np.zeros((1, 8, 8, 64), f)
    pm = np.zeros((1, 8, 8), f)
    pconv = np.zeros((1, 8, 3, 1024), f)
    pS = np.zeros((1, 8, 8, 64, 64), f)
    pshift = np.zeros((1, 8, 1, RWW), f)
    for c in range(8):
        oC = R[c]["oC"].reshape(2, 64, 4, 65)
        Ch = oC.transpose(2, 0, 1, 3).reshape(8, 64, 65)
        pC[0, c] = Ch[:, :, 0:64]
        pn[0, c] = Ch[:, :, 64]
        pm[0, c] = R[c]["om"].reshape(8)
        pconv[0, c] = R[c]["oconv"].transpose(2, 1, 0).reshape(3, 1024)
        oS = R[c]["oS"].reshape(2, 64, 4, 64)
        pS[0, c] = oS.transpose(2, 0, 3, 1).reshape(8, 64, 64)
        pshift[0, c, 0] = R[c]["oshift"].reshape(RWW)
    sC = np.concatenate([R[c]["osC"].reshape(NS, 8, 64, 64) for c in range(8)])[None].astype(f)
    sn = np.concatenate([R[c]["osn"].reshape(NS, 8, 64) for c in range(8)])[None].astype(f)
    sm = np.concatenate([R[c]["osm"].reshape(NS, 8) for c in range(8)])[None].astype(f)
    sconv = np.concatenate([R[c]["osconv"].reshape(NS, 8, 2, 3, 64).transpose(0, 3, 2, 1, 4).reshape(NS, 3, 1024) for c in range(8)])[None].astype(f)
    sS = np.concatenate([R[c]["osS"].reshape(NS, 8, 64, 64) for c in range(8)])[None].astype(f)
    sshift = np.concatenate([
        np.concatenate([R[c]["osshift"].reshape(NS, 8, 3, 64).transpose(0, 2, 1, 3).reshape(NS, 1536), R[c]["osshl"]], axis=1)
        for c in range(8)]).reshape(1, 128, 1, RWW).astype(f)
    return (y_prompt, y_sample, pC, pn, pm, pconv, pS, pshift, sC, sn, sm, sconv, sS, sshift)
```

```python
import contextlib
import numpy as np
import concourse.bass as bass
import concourse.mybir as mybir
from concourse.bass_utils import run_bass_kernel_spmd

F32 = mybir.dt.float32
BF16 = mybir.dt.bfloat16
AF = mybir.ActivationFunctionType
ALU = mybir.AluOpType
AX = mybir.AxisListType

D = 1024
T = 2048
NB = 16
NS = 16
INW = 3856
MLW = 2064
RWW = 1792
DFF = 4096
EPS = 1e-6
GN_EPS = 64e-5
C0 = 0.6065306597126334

OFF = {}
_o = 0
for _n, _w in [("mnw", 512), ("w0", 512), ("a0", 512), ("kk", 512), ("ka", 512), ("rk", 512),
               ("lnw", 512), ("lnb", 512), ("ifb", 16), ("nmw", 8), ("nmlp", 8), ("cw", 32), ("cb", 8),
               ("ident", 128), ("mui", 128), ("mus", 128), ("mls", 128), ("ones", 128),
               ("lsel", 128), ("rsel", 4)]:
    OFF[_n] = (_o, _o + _w)
    _o += _w
NA = _o
SOFF = {}
_o = 0
for _n, _w in [("mu_r", 64), ("mu_k", 64), ("mu_v", 64), ("cwq", 256), ("cwk", 256), ("cbq", 64), ("cbk", 64),
               ("mnw", 64), ("w0", 64), ("a0", 64), ("kk", 64), ("ka", 64), ("rk", 64), ("lnw", 64), ("lnb", 64),
               ("ib", 1), ("fb", 1)]:
    SOFF[_n] = (_o, _o + _w)
    _o += _w
NSP = _o


ALIAS = {"r_sb": "G0", "kf_sb": "G1", "vf": "G2", "osig": "G3", "hml": "G4", "nlrep": "G7", "wsig": "G3", "a_sb": "G4",
         "g_sb": "G5", "kap": "G6", "ktl": "G7", "bvec": "G8", "e1": "G9", "e2": "G10", "e3": "G11", "pcw": "G11",
         "ysb": "G11", "Fb": "G10", "cacc": "G11", "Ub": "Zb1", "tA": "G9", "tB": "G10", "plast": "G9", "junk": "xm", "ctmp": "xm", "xsb": "mix", "qks": "G11",
         "PTb": "Am0", "mixT": "TMbA", "TMb0": "TMbA", "TMb1": "TMbA", "TMb2": "TMbB", "TMb3": "TMbB",
         "Ss": "Cs", "lo3": "spj", "smix": "spj", "xt1": "xt0", "xnT1": "xnT0", "qkx1": "qkx0"}


class SemCtx:
    def __init__(self, nc):
        self.nc = nc
        self.es = contextlib.ExitStack()
        self.engs = ["pe", "act", "dve", "pool", "sp"]
        self.esem = {e: self.es.enter_context(nc.semaphore("s_" + e)) for e in self.engs}
        self.ecnt = {e: 0 for e in self.engs}
        self.bsem = self.es.enter_context(nc.semaphore("s_bar"))
        self.phase = 0
        self.gsem = {}
        self.gbase = {}

    def group_sem(self, g):
        if g not in self.gsem:
            self.gsem[g] = self.es.enter_context(self.nc.semaphore("g_%d" % len(self.gsem)))
            self.gbase[g] = 0
        return self.gsem[g]

    def close(self):
        self.es.close()


class Prog:
    max_ops = None

    def __init__(self, ctx):
        self.ctx = ctx
        self.nc = ctx.nc
        self.ops = []
        self.last_writer = {}
        self.readers = {}
        self.dma_groups = {}

    def op(self, eng, fn, reads=(), writes=(), dma_group=None, wait_total=False):
        if self.max_ops is not None and len(self.ops) >= self.max_ops:
            return None
        reads = [ALIAS.get(k, k) for k in reads]
        writes = [ALIAS.get(k, k) for k in writes]
        if eng != "pe":
            writes = writes + [k for k in reads if k.startswith("PS") and k not in writes]
        deps = set()
        for b in reads:
            if b in self.last_writer:
                deps.add(self.last_writer[b])
        for b in writes:
            if b in self.last_writer:
                deps.add(self.last_writer[b])
            for r in self.readers.get(b, ()):
                deps.add(r)
        idx = len(self.ops)
        if dma_group is not None:
            deps = {d for d in deps if self.ops[d]["dma"] != dma_group}
        o = dict(eng=eng, fn=fn, deps=sorted(deps), dma=dma_group, idx=idx)
        if dma_group is not None:
            g = self.dma_groups.setdefault(dma_group, dict(total=0, wait_total=wait_total))
            g["total"] += 1
            o["dma_cnt"] = g["total"]
        self.ops.append(o)
        for b in reads:
            self.readers.setdefault(b, []).append(idx)
        for b in writes:
            self.last_writer[b] = idx
            self.readers[b] = []
        return idx

    def finalize(self):
        nc = self.nc
        ctx = self.ctx
        ops = self.ops
        needed = set()
        for o in ops:
            best = {}
            rd = []
            for d in o["deps"]:
                p = ops[d]
                if p["dma"] is not None:
                    rd.append(d)
                else:
                    if p["eng"] == "pe" and o["eng"] == "pe" and o["dma"] is None:
                        continue
                    best[p["eng"]] = max(best.get(p["eng"], -1), d)
            rd.extend(best.values())
            o["deps"] = sorted(rd)
            for d in best.values():
                needed.add(d)
        engs = ctx.engs
        last = {}
        for o in ops:
            if o["dma"] is None:
                last[o["eng"]] = o["idx"]
        needed |= set(last.values())
        cnt = dict(ctx.ecnt)
        for o in ops:
            if o["dma"] is None and o["idx"] in needed:
                cnt[o["eng"]] += 1
                o["sig"] = cnt[o["eng"]]
        for g in self.dma_groups:
            ctx.group_sem(g)
        phase = ctx.phase
        with nc.Block() as block:

            def emit_engine(ename, eng):
                known = {}
                if phase > 0:
                    eng.wait_ge(ctx.bsem, phase)
                for o in ops:
                    if o["eng"] != ename:
                        continue
                    for d in o["deps"]:
                        p = ops[d]
                        if p["dma"] is not None:
                            g = self.dma_groups[p["dma"]]
                            sem = ctx.gsem[p["dma"]]
                            val = ctx.gbase[p["dma"]] + 16 * (g["total"] if g["wait_total"] else p["dma_cnt"])
                            key = ("g", p["dma"])
                        else:
                            if p["eng"] == "pe" and ename == "pe" and o["dma"] is None:
                                continue
                            sem = ctx.esem[p["eng"]]
                            val = p["sig"]
                            key = ("e", p["eng"])
                        if known.get(key, 0) >= val:
                            continue
                        known[key] = val
                        eng.wait_ge(sem, val)
                    ins = o["fn"](eng)
                    if o["dma"] is not None:
                        ins.then_inc(ctx.gsem[o["dma"]], 16)
                    elif "sig" in o:
                        ins.then_inc(ctx.esem[ename], 1)
                if ename == "sp":
                    for e2 in engs:
                        if cnt[e2] > ctx.ecnt[e2]:
                            eng.wait_ge(ctx.esem[e2], cnt[e2])
                    for g, info in self.dma_groups.items():
                        eng.wait_ge(ctx.gsem[g], ctx.gbase[g] + 16 * info["total"])
                    eng.sem_inc(ctx.bsem, 1)

            @block.tensor
            def _(e):
                emit_engine("pe", e)

            @block.scalar
            def _(e):
                emit_engine("act", e)

            @block.vector
            def _(e):
                emit_engine("dve", e)

            @block.gpsimd
            def _(e):
                emit_engine("pool", e)

            @block.sync
            def _(e):
                emit_engine("sp", e)

        ctx.ecnt = cnt
        for g, info in self.dma_groups.items():
            ctx.gbase[g] += 16 * info["total"]
        ctx.phase += 1


STOP_EARLY = True


class _StopBuild(Exception):
    pass


def build_program(do_sample=True, debug=False):
    nc = bass.Bass("TRN2", target_bir_lowering=False)
    try:
        return _build_program(nc, do_sample, debug)
    except _StopBuild:
        return nc


def _build_program(nc, do_sample, debug):
    dbg = nc.dram_tensor("dbg", [128, 16, 512], F32, kind="ExternalOutput").ap() if debug else None
    din = lambda n, s: nc.dram_tensor(n, s, F32, kind="ExternalInput").ap()
    dout = lambda n, s: nc.dram_tensor(n, s, F32, kind="ExternalOutput").ap()
    xp = din("xp", [T, D])
    xs = din("xs", [NS, D])
    w_in = din("w_in", [D, INW])
    w_out = din("w_out", [D, D])
    mlp_up = din("mlp_up", [D, DFF])
    mlp_down = din("mlp_down", [DFF, D])
    packA_d = din("packA", [128, NA])
    mu_d = din("mu_b", [128, RWW])
    nfw_d = din("nfw_b", [128, D])
    wup_d = din("luw", [128, 512])
    gup_d = din("gup", [128, 512])
    spk_d = din("spack", [128, NSP])
    sC_d = din("sC", [128, 4096])
    sn_d = din("sn", [128, 64])
    sm_d = din("sm", [128, 1])
    sconv_d = din("sconv", [128, 2, 3, 64])
    sS_d = din("sS", [128, 4096])
    sshift_d = din("sshift", [128, 3, 64])
    sshl_d = din("sshl", [NS, 256])
    mul_d = din("mul", [NS, 256])

    yp = dout("yp", [T, D])
    ys = dout("ys", [NS, D])
    oC = dout("oC", [128, 4, 65])
    om = dout("om", [8, 1])
    oconv = dout("oconv", [128, 8, 3])
    oS = dout("oS", [128, 4, 64])
    oshift = dout("oshift", [1, RWW])
    osC = dout("osC", [128, 4096])
    osn = dout("osn", [128, 64])
    osm = dout("osm", [128, 1])
    osconv = dout("osconv", [128, 2, 3, 64])
    osS = dout("osS", [128, 4096])
    osshift = dout("osshift", [128, 3, 64])
    osshl = dout("osshl", [NS, 256])

    xmid_d = nc.dram_tensor("xmid_scr", [T + NS, D], F32, kind="Internal").ap()
    scr1 = nc.dram_tensor("scr1", [128, 8, 64], F32, kind="Internal").ap()
    scr2 = nc.dram_tensor("scr2", [128, 3, 64], F32, kind="Internal").ap()
    scr3 = nc.dram_tensor("scr3", [NS, 2, 8, 64], F32, kind="Internal").ap()

    ctx = SemCtx(nc)
    PP = [Prog(ctx)]
    es_res = contextlib.ExitStack()
    cur = [es_res]

    def TT(name, shape, dt=F32):
        return cur[0].enter_context(nc.sbuf_tensor("t_" + name, list(shape), dt))

    def dma(q, out, in_, reads, writes, group, wait_total=False):
        PP[0].op(q, lambda e: e.dma_start(out=out, in_=in_), reads=reads, writes=writes, dma_group=group, wait_total=wait_total)

    def dve(fn, r, w):
        PP[0].op("dve", fn, reads=r, writes=w)

    def act(fn, r, w):
        PP[0].op("act", fn, reads=r, writes=w)

    def pool(fn, r, w):
        PP[0].op("pool", fn, reads=r, writes=w)

    def pe(fn, r, w):
        PP[0].op("pe", fn, reads=r, writes=w)

    def mm(out, lhsT, rhs, start, stop, r, w):
        pe(lambda e: e.matmul(out, lhsT=lhsT, rhs=rhs, start=start, stop=stop), r, w)

    es_ps = contextlib.ExitStack()
    PS = [es_ps.enter_context(nc.psum_tensor("PS%d" % i, [128, 1024], F32)) for i in range(4)]
    PSB = [p.bitcast(BF16) for p in PS]
    psi = [0]

    def nps():
        i = psi[0] % 4
        psi[0] += 1
        return PS[i], PSB[i], "PS%d" % i

    wq = TT("wq", [128, 8, MLW], BF16)
    W1 = TT("W1", [128, 8, RWW], BF16)
    W2 = TT("W2", [128, 8, RWW], BF16)
    wout = TT("wout", [128, 8, D], BF16)
    luw = TT("luw", [128, 512], BF16)
    gup = TT("gup", [128, 512], BF16)
    pA = TT("pA", [128, NA], F32)
    identb = TT("identb", [128, 128], BF16)

    def PA(n):
        a, b = OFF[n]
        return pA[:, a:b]

    w_in_v = w_in.rearrange("(c p) n -> p c n", p=128)
    dma("sp", pA[:], packA_d, [], ["pA"], "init", True)
    for c in range(8):
        dma("pool", wq[:, c, :], w_in_v[:, c, 0:MLW], [], ["wq"], "init", True)
    dma("pool", luw[:], wup_d, [], ["luw"], "init", True)
    dma("pool", gup[:], gup_d, [], ["gup"], "init", True)
    w_out_v = w_out.rearrange("(c p) n -> p c n", p=128)
    for c in range(8):
        dma("pool", wout[:, c, :], w_out_v[:, c, :], [], ["wout"], "init", True)
    dve(lambda e: e.tensor_copy(out=identb[:], in_=PA("ident")), ["pA"], ["identb"])


    def _dbgdump(tag):
        if debug != tag:
            return
        dstg_ = cur[0].enter_context(nc.sbuf_tensor("t_dbgst%d" % tag, [128, 512], F32))
        def dd(slot, ap, key, n):
            dve(lambda e: e.tensor_copy(out=dstg_[:, 0:n], in_=ap), [key], ["dbgst"])
            dma("sp", dbg[:, slot, 0:n], dstg_[:, 0:n], ["dbgst"], [], "dbg")
        dd(0, PA("ident"), "pA", 128)
        dd(1, PA("mui"), "pA", 128)
        dd(2, PA("mnw"), "pA", 512)
        dd(3, PA("w0"), "pA", 512)
        dd(4, PA("lnb"), "pA", 512)
        PP[0].max_ops = len(PP[0].ops)
        PP[0].finalize()
        raise _StopBuild()
    _dbgdump(3)
    with contextlib.ExitStack() as es_prep:
        cur[0] = es_prep
        mub = TT("mub", [128, RWW], F32)
        omu = TT("omu", [128, RWW], F32)
        stg = [TT("stg%d" % i, [128, RWW], F32) for i in range(2)]
        dma("sp", mub[:], mu_d, [], ["mub"], "init", True)
        dve(lambda e: e.tensor_scalar(out=omu[:], in0=mub[:], scalar1=-1.0, scalar2=1.0, op0=ALU.mult, op1=ALU.add), ["mub"], ["omu"])
        for c in range(8):
            s = stg[c % 2]
            sk = "stg%d" % (c % 2)
            dma("sp", s[:], w_in_v[:, c, MLW:INW], [], [sk], sk)
            dve(lambda e, s=s, c=c: e.tensor_tensor(out=W1[:, c, :], in0=s[:], in1=omu[:], op=ALU.mult), [sk, "omu"], ["W1"])
            pool(lambda e, s=s, c=c: e.tensor_tensor(out=W2[:, c, :], in0=s[:], in1=mub[:], op=ALU.mult), [sk, "mub"], ["W2"])
        PP[0].finalize()
        PP[0] = Prog(ctx)

    def rmsnorm_T(xt, xk, nt, dstT, dstk, col0, wname, tmpb, tmpbk, junk, junkk, st, stk):
        act(lambda e: e.activation(out=junk[:nt, :], in_=xt[:nt, :], func=AF.Square, accum_out=st[:nt, 0:1]), [xk], [junkk, stk])
        act(lambda e: e.activation(out=st[:nt, 1:2], in_=st[:nt, 0:1], func=AF.Sqrt, bias=EPS, scale=1.0 / D), [stk], [stk])
        dve(lambda e: e.reciprocal(out=st[:nt, 2:3], in_=st[:nt, 1:2]), [stk], [stk])
        dve(lambda e: e.tensor_scalar_mul(out=tmpb[:nt, :], in0=xt[:nt, :], scalar1=st[:nt, 2:3]), [xk, stk], [tmpbk])
        ps, psb, pk = nps()
        for c in range(8):
            pe(lambda e, c=c: e.transpose(psb[:, c * 128:c * 128 + nt], tmpb[:nt, c * 128:(c + 1) * 128], identb[:nt, :nt]), [tmpbk, "identb"], [pk])
        a, b_ = OFF[wname]
        dve(lambda e: e.tensor_tensor(out=dstT[:, :, col0:col0 + nt],
                                      in0=psb[:, 0:1024].rearrange("p (c t) -> p c t", c=8)[:, :, 0:nt],
                                      in1=pA[:, a:b_].unsqueeze(2).to_broadcast([128, 8, nt]), op=ALU.mult), [pk, "pA"], [dstk])

    def head_ml(nt, hsrc, hk, osig, ok, mix, mixk, W):
        tA, tB, s8 = W["tA"], W["tB"], W["s8"]
        h3 = lambda t: t[:nt, :].rearrange("p (h d) -> p h d", h=8)
        bc = lambda t, c: t[:nt, c:c + 8].unsqueeze(2).to_broadcast([nt, 8, 64])
        dve(lambda e: e.tensor_tensor(out=tA[:nt, :], in0=hsrc[:nt, :], in1=osig[:nt, :], op=ALU.mult), [hk, ok], ["tA"])
        dve(lambda e: e.tensor_tensor(out=tB[:nt, :], in0=tA[:nt, :], in1=tA[:nt, :], op=ALU.mult), ["tA"], ["tB"])
        dve(lambda e: e.tensor_reduce(out=s8[:nt, 0:8], in_=h3(tB), axis=AX.X, op=ALU.add), ["tB"], ["s8"])
        act(lambda e: e.activation(out=s8[:nt, 8:16], in_=s8[:nt, 0:8], func=AF.Sqrt, bias=EPS, scale=1.0 / 64), ["s8"], ["s8"])
        dve(lambda e: e.reciprocal(out=s8[:nt, 16:24], in_=s8[:nt, 8:16]), ["s8"], ["s8"])
        dve(lambda e: e.tensor_tensor(out=h3(tB), in0=h3(tA), in1=bc(s8, 16), op=ALU.mult), ["tA", "s8"], ["tB"])
        dve(lambda e: e.tensor_tensor(out=mix[:nt, 0:512], in0=tB[:nt, :], in1=PA("mnw")[:nt, :], op=ALU.mult), ["tB", "pA"], [mixk])

    def head_rw(nt, ysrc, yk, bon, bonk, vf, vfk, g, gk, mix, mixk, W):
        tA, tB, s8 = W["tA"], W["tB"], W["s8"]
        h3 = lambda t: t[:nt, :].rearrange("p (h d) -> p h d", h=8)
        bc = lambda t, c: t[:nt, c:c + 8].unsqueeze(2).to_broadcast([nt, 8, 64])
        dve(lambda e: e.tensor_tensor(out=h3(tA), in0=h3(vf), in1=bc(bon, 0), op=ALU.mult), [vfk, bonk], ["tA"])
        dve(lambda e: e.tensor_tensor(out=tA[:nt, :], in0=tA[:nt, :], in1=ysrc[:nt, :], op=ALU.add), ["tA", yk], ["tA"])
        dve(lambda e: e.tensor_reduce(out=s8[:nt, 24:32], in_=h3(tA), axis=AX.X, op=ALU.add), ["tA"], ["s8"])
        dve(lambda e: e.tensor_scalar_mul(out=s8[:nt, 24:32], in0=s8[:nt, 24:32], scalar1=1.0 / 64), ["s8"], ["s8"])
        dve(lambda e: e.tensor_tensor(out=h3(tA), in0=h3(tA), in1=bc(s8, 24), op=ALU.subtract), ["tA", "s8"], ["tA"])
        dve(lambda e: e.tensor_tensor(out=tB[:nt, :], in0=tA[:nt, :], in1=tA[:nt, :], op=ALU.mult), ["tA"], ["tB"])
        dve(lambda e: e.tensor_reduce(out=s8[:nt, 32:40], in_=h3(tB), axis=AX.X, op=ALU.add), ["tB"], ["s8"])
        act(lambda e: e.activation(out=s8[:nt, 40:48], in_=s8[:nt, 32:40], func=AF.Sqrt, bias=GN_EPS, scale=1.0 / 64), ["s8"], ["s8"])
        dve(lambda e: e.reciprocal(out=s8[:nt, 48:56], in_=s8[:nt, 40:48]), ["s8"], ["s8"])
        dve(lambda e: e.tensor_tensor(out=h3(tB), in0=h3(tA), in1=bc(s8, 48), op=ALU.mult), ["tA", "s8"], ["tB"])
        dve(lambda e: e.tensor_tensor(out=tB[:nt, :], in0=tB[:nt, :], in1=PA("lnw")[:nt, :], op=ALU.mult), ["tB", "pA"], ["tB"])
        dve(lambda e: e.tensor_tensor(out=tB[:nt, :], in0=tB[:nt, :], in1=PA("lnb")[:nt, :], op=ALU.add), ["tB", "pA"], ["tB"])
        dve(lambda e: e.tensor_tensor(out=mix[:nt, 512:1024], in0=tB[:nt, :], in1=g[:nt, :], op=ALU.mult), ["tB", gk], [mixk])

    def out_proj(nt, mix, mixk, xt, xk, row0, W):
        mixT, xm = W["mixT"], W["xm"]
        ps, psb, pk = nps()
        for c in range(8):
            pe(lambda e, c=c: e.transpose(psb[:, c * 128:c * 128 + nt], mix[:nt, c * 128:(c + 1) * 128], identb[:nt, :nt]), [mixk, "identb"], [pk])
        act(lambda e: e.copy(out=mixT[:, :, 0:nt], in_=psb[:, 0:1024].rearrange("p (c t) -> p c t", c=8)[:, :, 0:nt]), [pk], ["mixT"])
        ps, psb, pk = nps()
        for n in range(2):
            for c in range(8):
                mm(ps[:nt, n * 512:(n + 1) * 512], mixT[:, c, 0:nt], wout[:, c, n * 512:(n + 1) * 512], c == 0, c == 7, ["mixT", "wout"], [pk])
        for a_ in range(2):
            dve(lambda e, a_=a_: e.tensor_tensor(out=xm[:nt, a_ * 512:(a_ + 1) * 512], in0=ps[:nt, a_ * 512:(a_ + 1) * 512], in1=xt[:nt, a_ * 512:(a_ + 1) * 512], op=ALU.add), [pk, xk], ["xm"])
        dma("pool", xmid_d[row0:row0 + nt, :], xm[:nt, :], ["xm"], ["xmid%d" % row0], "xm")

    with contextlib.ExitStack() as es1:
        cur[0] = es1
        W = {}
        Gbig = TT("Gbig", [128, 13, 512])
        G = [Gbig[:, i, :] for i in range(13)]
        W["s8"] = TT("s8", [128, 64])
        W["xm"] = TT("xm", [128, D])
        xt = [TT("xt0", [128, D])] * 2
        junk = W["xm"]
        st = TT("st", [128, 4])
        mix = TT("mix", [128, D], BF16)
        xsb = mix
        xnT = [TT("xnT0", [128, 8, 129], BF16)] * 2
        qkx = [TT("qkx0", [128, 8, 131])] * 2
        cacc = Gbig[:, 11:13, :].rearrange("p a (c t) -> p (a c) t", t=128)
        ctmp = W["xm"][:, :].rearrange("p (c t) -> p c t", c=8)
        qks = cacc
        qpb = TT("qpb", [128, 4, 128], BF16)
        kTb = TT("kTb", [128, 4, 128], BF16)
        ktm = TT("ktm", [128, 8, 64], BF16)
        vaug = TT("vaug", [128, 8, 65], BF16)
        gt = TT("gt", [128, 96])
        runmax = TT("runmax", [128, 8])
        nBc = TT("nBc", [128, 8])
        Cst = TT("Cst", [128, 4, 65])
        Cbf = TT("Cbf", [128, 4, 65], BF16)
        r_sb, kf_sb, vf = G[0], G[1], G[2]
        Fb = G[10].rearrange("p (j t) -> p j t", j=4)
        osig, hml = G[3], G[4]
        nlrep = G[7].rearrange("p (h d) -> p h d", h=8)
        wsig, a_sb, g_sb, kap, ktl, bvec, e1, e2, e3, pcw, ysb = G[3], G[4], G[5], G[6], G[7], G[8], G[9], G[10], G[11], G[12], G[11]
        W["tA"], W["tB"] = G[9], G[10]
        vrw = TT("vrw", [128, 8, 64], BF16)
        lor = TT("lor", [128, 256], BF16)
        lorT = TT("lorT", [128, 2, 128], BF16)
        r8 = TT("r8", [128, 32])
        bon = TT("bon", [128, 8])
        TMb = TT("TMb", [128, 4, 512], BF16)
        W["mixT"] = TMb[:, 0:2, :].rearrange("p a (c t) -> p (a c) t", t=128)
        Btz = TT("Btz", [128, 8, 128], BF16)
        Ktz = TT("Ktz", [128, 8, 128], BF16)
        FMt = TT("FMt", [128, 4, 4, 128], BF16)
        Am = [TT("Am%d" % i, [128, 8, 128], BF16) for i in range(3)]
        PTb = Am[0]
        Pw = [TT("Pw%d" % i, [128, 8, 128], BF16) for i in range(4)]
        Zb = [TT("Zb%d" % i, [128, 8, 64], BF16) for i in range(2)]
        Ub = Zb[1]
        Sst = TT("Sst", [128, 4, 64])
        Sbf = TT("Sbf", [128, 4, 64], BF16)
        WLfm = TT("WLfm", [128, 4])
        plast = G[9]

        pool(lambda e: e.memset(vaug[:], 1.0), [], ["vaug"])
        pool(lambda e: e.memset(Btz[:], 0.0), [], ["Btz"])
        pool(lambda e: e.memset(Ktz[:], 0.0), [], ["Ktz"])
        pool(lambda e: e.memset(Cst[:], 0.0), [], ["Cst"])
        pool(lambda e: e.memset(Cbf[:], 0.0), [], ["Cbf"])
        pool(lambda e: e.memset(Sst[:], 0.0), [], ["Sst"])
        pool(lambda e: e.memset(Sbf[:], 0.0), [], ["Sbf"])
        pool(lambda e: e.memset(runmax[:], -1e30), [], ["runmax"])
        pool(lambda e: e.memset(nBc[:], 0.0), [], ["nBc"])
        pool(lambda e: e.memset(xnT[0][:, :, 0:1], 0.0), [], ["xnT0"])
        pool(lambda e: e.memset(qkx[0][:, :, 0:3], 0.0), [], ["qkx0"])

        if debug == 2:
            dstg = TT("dbgstage", [128, 512]) if False else G[12]
            def ddump0(slot, ap, key, n):
                dve(lambda e: e.tensor_copy(out=dstg[:, 0:n], in_=ap), [key], ["pcw"])
                dma("sp", dbg[:, slot, 0:n], dstg[:, 0:n], ["pcw"], [], "dbg")
            ddump0(0, PA("ident"), "pA", 128)
            ddump0(1, PA("mui"), "pA", 128)
            ddump0(2, luw[:, :], "luw", 512)
            ddump0(3, gup[:, :], "gup", 512)
            ddump0(4, W1[:, 0, 0:512], "W1", 512)
            PP[0].max_ops = len(PP[0].ops)
            if STOP_EARLY:
                PP[0].finalize()
                raise _StopBuild()
        MUI = PA("mui")
        MUS = PA("mus")
        MLS = PA("mls")
        ONES = PA("ones")
        IDF = PA("ident")
        bc8 = lambda ap: ap.unsqueeze(2).to_broadcast([128, 8, 64])
        m8 = lambda m: m.unsqueeze(1).to_broadcast([128, 8, 128])
        v3 = lambda t: t[:].rearrange("p (h d) -> p h d", h=8)
        hoff = lambda h: (h % 2) * 512 + (h // 2) * 128

        for b in range(NB):
            x_ = xt[b % 2]
            xk = "xt%d" % (b % 2)
            xn = xnT[b % 2]
            xnk = "xnT%d" % (b % 2)
            qx = qkx[b % 2]
            qxk = "qkx%d" % (b % 2)
            dma("sp", x_[:], xp[b * 128:(b + 1) * 128, :], [], [xk], xk)
            rmsnorm_T(x_, xk, 128, xn, xnk, 1, "nmw", xsb, "xsb", junk, "junk", st, "st")
            cur_x = xn[:, :, 1:129]
            prv_x = xn[:, :, 0:128]

            ps, psb, pk = nps()
            for j in range(8):
                for c in range(8):
                    mm(ps[:, j * 128:(j + 1) * 128], wq[:, c, j * 128:(j + 1) * 128], cur_x[:, c, :], c == 0, c == 7, ["wq", xnk], [pk])
            for a_ in range(2):
                act(lambda e, ps=ps, qx=qx, a_=a_: e.copy(out=qx[:, 4 * a_:4 * a_ + 4, 3:131], in_=ps[:, a_ * 512:(a_ + 1) * 512].rearrange("p (j t) -> p j t", j=4)), [pk], [qxk])
            if b == NB - 1:
                dma("pool", oconv, qx[:, :, 128:131], [qxk], [], "fin")

            def tm_proj(col0, ncol, rw, ps_ap, pk):
                if not rw:
                    for c in range(8):
                        mm(ps_ap, cur_x[:, c, :], wq[:, c, col0:col0 + ncol], c == 0, c == 7, [xnk, "wq"], [pk])
                else:
                    for c in range(8):
                        mm(ps_ap, cur_x[:, c, :], W1[:, c, col0:col0 + ncol], c == 0, False, [xnk, "W1"], [pk])
                    for c in range(8):
                        mm(ps_ap, prv_x[:, c, :], W2[:, c, col0:col0 + ncol], False, c == 7, [xnk, "W2"], [pk])

            ps, psb, pk = nps()
            tm_proj(1024, 512, False, ps[:, 0:512], pk)
            tm_proj(1536, 512, False, ps[:, 512:1024], pk)
            act(lambda e, ps=ps: e.copy(out=vaug[:, :, 0:64], in_=ps[:, 0:512].rearrange("p (h d) -> p h d", h=8)), [pk], ["vaug"])
            act(lambda e, ps=ps: e.activation(out=osig[:], in_=ps[:, 512:1024], func=AF.Sigmoid), [pk], ["osig"])
            ps, psb, pk = nps()
            tm_proj(2048, 16, False, ps[:, 0:16], pk)
            tm_proj(0, 512, True, ps[:, 512:1024], pk)
            dve(lambda e, ps=ps: e.tensor_tensor(out=gt[:, 0:16], in0=ps[:, 0:16], in1=PA("ifb"), op=ALU.add), [pk, "pA"], ["gt"])
            act(lambda e, ps=ps: e.copy(out=r_sb[:], in_=ps[:, 512:1024]), [pk], ["r_sb"])
            ps, psb, pk = nps()
            tm_proj(512, 512, True, ps[:, 0:512], pk)
            tm_proj(1024, 512, True, ps[:, 512:1024], pk)
            act(lambda e, ps=ps: e.copy(out=kf_sb[:], in_=ps[:, 0:512]), [pk], ["kf_sb"])
            act(lambda e, ps=ps: e.copy(out=vf[:], in_=ps[:, 512:1024]), [pk], ["vf"])
            dve(lambda e, ps=ps: e.tensor_copy(out=vrw[:], in_=ps[:, 512:1024].rearrange("p (h d) -> p h d", h=8)), [pk], ["vrw"])
            ps, psb, pk = nps()
            tm_proj(1536, 256, True, ps[:, 0:256], pk)
            act(lambda e, ps=ps: e.activation(out=lor[:, 0:64], in_=ps[:, 0:64], func=AF.Tanh), [pk], ["lor"])
            act(lambda e, ps=ps: e.copy(out=lor[:, 64:128], in_=ps[:, 64:128]), [pk], ["lor"])
            act(lambda e, ps=ps: e.activation(out=lor[:, 128:256], in_=ps[:, 128:256], func=AF.Sigmoid), [pk], ["lor"])
            if b == NB - 1:
                lastc = xn[:, :, 128:129]
                for n0 in range(0, RWW, 512):
                    nn = min(512, RWW - n0)
                    ps2, _, pk2 = nps()
                    for c in range(8):
                        mm(ps2[0:1, 0:nn], lastc[:, c, :], W1[:, c, n0:n0 + nn], c == 0, False, [xnk, "W1"], [pk2])
                    for c in range(8):
                        mm(ps2[0:1, 0:nn], lastc[:, c, :], W2[:, c, n0:n0 + nn], False, c == 7, [xnk, "W2"], [pk2])
                    act(lambda e, ps2=ps2, n0=n0, nn=nn: e.copy(out=plast[0:1, 0:nn], in_=ps2[0:1, 0:nn]), [pk2], ["plast"])
                    dma("pool", oshift[:, n0:n0 + nn], plast[0:1, 0:nn], ["plast"], [], "fin")

            act(lambda e: e.activation(out=gt[:, 56:64], in_=gt[:, 8:16], func=AF.Exp, scale=-1.0), ["gt"], ["gt"])
            act(lambda e: e.activation(out=gt[:, 16:24], in_=gt[:, 56:64], func=AF.Ln, bias=1.0, scale=1.0), ["gt"], ["gt"])
            dve(lambda e: e.tensor_copy(out=nlrep[:], in_=bc8(gt[:, 16:24])), ["gt"], ["nlrep"])
            ps, psb, pk = nps()
            mm(ps[:, 0:8], MUI, gt[:, 16:24], True, True, ["pA", "gt"], [pk])
            mm(ps[:, 8:16], ONES, gt[:, 16:24], True, True, ["pA", "gt"], [pk])
            for j in range(4):
                mm(ps[:, 512 + j * 128:512 + (j + 1) * 128], nlrep[:, 2 * j:2 * j + 2, :].rearrange("p a d -> p (a d)"), MUI, True, True, ["nlrep", "pA"], [pk])
            dve(lambda e, ps=ps: e.tensor_tensor(out=gt[:, 24:32], in0=ps[:, 0:8], in1=gt[:, 0:8], op=ALU.add), [pk, "gt"], ["gt"])
            act(lambda e: e.activation(out=gt[:, 32:40], in_=gt[:, 24:32], func=AF.Exp), ["gt"], ["gt"])
            dve(lambda e, ps=ps: e.tensor_tensor(out=gt[:, 56:64], in0=gt[:, 24:32], in1=ps[:, 8:16], op=ALU.subtract), [pk, "gt"], ["gt"])
            act(lambda e: e.activation(out=gt[:, 40:48], in_=gt[:, 56:64], func=AF.Exp), ["gt"], ["gt"])
            act(lambda e, ps=ps: e.activation(out=gt[:, 48:56], in_=ps[:, 8:16], func=AF.Exp, scale=-1.0), [pk], ["gt"])
            act(lambda e, ps=ps: e.activation(out=Fb[:], in_=ps[:, 512:1024].rearrange("p (j t) -> p j t", j=4), func=AF.Exp, scale=-1.0), [pk], ["Fb"])
            dve(lambda e: e.tensor_tensor(out=gt[:, 56:64], in0=gt[:, 24:32], in1=nBc[:], op=ALU.add), ["gt", "nBc"], ["gt"])
            dve(lambda e: e.tensor_tensor(out=runmax[:], in0=runmax[:], in1=gt[:, 56:64], op=ALU.max), ["gt", "runmax"], ["runmax"])
            dve(lambda e, ps=ps: e.tensor_tensor(out=nBc[:], in0=nBc[:], in1=ps[:, 8:16], op=ALU.add), [pk, "nBc"], ["nBc"])

            cwv = PA("cw").rearrange("p (c j) -> p c j", j=4)
            wbc = lambda j: cwv[:, :, j:j + 1].to_broadcast([128, 8, 128])
            pool(lambda e, qx=qx: e.tensor_tensor(out=cacc[:], in0=qx[:, :, 3:131], in1=wbc(3), op=ALU.mult), [qxk, "pA"], ["cacc"])
            for j in range(3):
                pool(lambda e, qx=qx, j=j: e.tensor_tensor(out=ctmp[:], in0=qx[:, :, j:j + 128], in1=wbc(j), op=ALU.mult), [qxk, "pA"], ["ctmp"])
                pool(lambda e: e.tensor_tensor(out=cacc[:], in0=cacc[:], in1=ctmp[:], op=ALU.add), ["cacc", "ctmp"], ["cacc"])
            pool(lambda e: e.tensor_tensor(out=cacc[:], in0=cacc[:], in1=PA("cb").unsqueeze(2).to_broadcast([128, 8, 128]), op=ALU.add), ["cacc", "pA"], ["cacc"])
            act(lambda e: e.activation(out=qks[:], in_=cacc[:], func=AF.Silu), ["cacc"], ["qks"])
            dve(lambda e: e.tensor_tensor(out=qpb[:], in0=qks[:, 0:4, :], in1=Fb[:], op=ALU.mult), ["qks", "Fb"], ["qpb"])
            act(lambda e: e.activation(out=kTb[:], in_=qks[:, 4:8, :], func=AF.Copy, scale=0.125), ["qks"], ["kTb"])

            ps, psb, pk = nps()
            for j in range(4):
                pe(lambda e, j=j, psb=psb: e.transpose(psb[:, j * 128:(j + 1) * 128], kTb[:, j, :], identb[:]), ["kTb", "identb"], [pk])
            dve(lambda e, psb=psb: e.tensor_tensor(out=ktm[:], in0=psb[:, 0:512].rearrange("p (h d) -> p h d", h=8), in1=bc8(gt[:, 40:48]), op=ALU.mult), [pk, "gt"], ["ktm"])

            ps, psb, pk = nps()
            for h in range(8):
                j, hp = h // 2, h % 2
                sl = slice(hp * 64, hp * 64 + 64)
                mm(ps[:, hoff(h):hoff(h) + 128], kTb[sl, j, :], qpb[sl, j, :], True, True, ["kTb", "qpb"], [pk])
            for h in range(8):
                dve(lambda e, h=h, ps=ps: e.scalar_tensor_tensor(out=PTb[:, h, :], in0=ps[:, hoff(h):hoff(h) + 128], scalar=gt[:, 32 + h:33 + h], in1=MUI, op0=ALU.mult, op1=ALU.mult), [pk, "gt", "pA"], ["PTb"])
            ps, psb, pk = nps()
            psn = lambda ps, h: ps[:, (h // 4) * 512 + (h % 4) * 65:(h // 4) * 512 + (h % 4) * 65 + 65]
            for h in range(8):
                j, hp = h // 2, h % 2
                sl = slice(hp * 64, hp * 64 + 64)
                mm(psn(ps, h), PTb[:, h, :], vaug[:, h, :], True, False, ["PTb", "vaug"], [pk])
                mm(psn(ps, h), qpb[sl, j, :], Cbf[sl, j, :], False, True, ["qpb", "Cbf"], [pk])
            pn4 = ps[:, :].rearrange("p (a r) -> p a r", a=2)[:, :, 0:260].rearrange("p a (h d) -> p a h d", h=4)
            for a_ in range(2):
                act(lambda e, pn4=pn4, a_=a_: e.copy(out=r8[:, 4 * a_:4 * a_ + 4], in_=pn4[:, a_, :, 64]), [pk], ["r8"])
            dve(lambda e: e.scalar_tensor_tensor(out=r8[:, 8:16], in0=r8[:, 0:8], scalar=-1.0, in1=r8[:, 0:8], op0=ALU.mult, op1=ALU.max), ["r8"], ["r8"])
            dve(lambda e: e.tensor_scalar_max(out=r8[:, 8:16], in0=r8[:, 8:16], scalar1=1.0), ["r8"], ["r8"])
            dve(lambda e: e.reciprocal(out=r8[:, 16:24], in_=r8[:, 8:16]), ["r8"], ["r8"])
            for a_ in range(2):
                dve(lambda e, pn4=pn4, a_=a_: e.tensor_tensor(out=hml[:, a_ * 256:(a_ + 1) * 256].rearrange("p (h d) -> p h d", h=4), in0=pn4[:, a_, :, 0:64],
                                                          in1=r8[:, 16 + 4 * a_:20 + 4 * a_].unsqueeze(2).to_broadcast([128, 4, 64]), op=ALU.mult), [pk, "r8"], ["hml"])
            ps, psb, pk = nps()
            for h in range(8):
                j = h // 2
                mm(psn(ps, h), ktm[:, 2 * j:2 * j + 2, :].rearrange("p a d -> p (a d)"), vaug[:, h, :], True, True, ["ktm", "vaug"], [pk])
            pu4 = ps[:, :].rearrange("p (a r) -> p a r", a=2)[:, :, 0:260].rearrange("p a (h d) -> p a h d", h=4)
            for hp in range(2):
                sl = slice(hp * 64, hp * 64 + 64)
                decb = gt[sl, 48:56].rearrange("p (j q) -> p j q", q=2)[:, :, hp:hp + 1].to_broadcast([64, 4, 65])
                dve(lambda e, sl=sl, decb=decb: e.tensor_tensor(out=Cst[sl, :, :], in0=Cst[sl, :, :], in1=decb, op=ALU.mult), ["Cst", "gt"], ["Cst"])
                for a in range(2):
                    src = pu4[sl, a, hp::2, :]
                    dve(lambda e, sl=sl, a=a, src=src: e.tensor_tensor(out=Cst[sl, 2 * a:2 * a + 2, :], in0=Cst[sl, 2 * a:2 * a + 2, :], in1=src, op=ALU.add), [pk, "Cst"], ["Cst"])
            act(lambda e: e.copy(out=Cbf[:], in_=Cst[:]), ["Cst"], ["Cbf"])
            head_ml(128, hml, "hml", osig, "osig", mix, "mix", W)

            ps, psb, pk = nps()
            pe(lambda e, psb=psb: e.transpose(psb[:, 0:128], lor[:, 0:128], identb[:]), ["lor", "identb"], [pk])
            pe(lambda e, psb=psb: e.transpose(psb[:, 128:256], lor[:, 128:256], identb[:]), ["lor", "identb"], [pk])
            act(lambda e, psb=psb: e.copy(out=lorT[:], in_=psb[:, 0:256].rearrange("p (a t) -> p a t", a=2)), [pk], ["lorT"])
            ps, psb, pk = nps()
            mm(ps[:, 0:512], lorT[0:64, 0, :], luw[0:64, :], True, True, ["lorT", "luw"], [pk])
            mm(ps[:, 512:1024], lorT[64:128, 0, :], luw[64:128, :], True, True, ["lorT", "luw"], [pk])
            dve(lambda e, ps=ps: e.tensor_tensor(out=e1[:], in0=ps[:, 0:512], in1=PA("w0"), op=ALU.add), [pk, "pA"], ["e1"])
            act(lambda e: e.activation(out=wsig[:], in_=e1[:], func=AF.Sigmoid), ["e1"], ["wsig"])
            dve(lambda e, ps=ps: e.tensor_tensor(out=e2[:], in0=ps[:, 512:1024], in1=PA("a0"), op=ALU.add), [pk, "pA"], ["e2"])
            act(lambda e: e.activation(out=a_sb[:], in_=e2[:], func=AF.Sigmoid), ["e2"], ["a_sb"])
            ps, psb, pk = nps()
            mm(ps[:, 0:512], lorT[:, 1, :], gup[:, :], True, True, ["lorT", "gup"], [pk])
            act(lambda e, ps=ps: e.copy(out=g_sb[:], in_=ps[:, 0:512]), [pk], ["g_sb"])
            if debug and b == 0:
                dstg = G[12]
                def ddump(slot, ap, key, n):
                    dve(lambda e: e.tensor_copy(out=dstg[:, 0:n], in_=ap), [key], ["pcw"])
                    dma("pool", dbg[:, slot, 0:n], dstg[:, 0:n], ["pcw"], [], "dbg")
                ddump(15, lor[:, :], "lor", 256)
                ddump(13, lorT[:].rearrange("p a t -> p (a t)"), "lorT", 256)
                ddump(14, luw[:, :], "luw", 512)
                ddump(12, gup[:, :], "gup", 512)
                ddump(11, wsig[:, :], "wsig", 512)
                ddump(10, g_sb[:, :], "g_sb", 512)
                PP[0].max_ops = len(PP[0].ops)
            dve(lambda e: e.tensor_tensor(out=e1[:], in0=kf_sb[:], in1=PA("kk"), op=ALU.mult), ["kf_sb", "pA"], ["e1"])
            dve(lambda e: e.tensor_tensor(out=e2[:], in0=e1[:], in1=e1[:], op=ALU.mult), ["e1"], ["e2"])
            dve(lambda e: e.tensor_reduce(out=r8[:, 24:32], in_=v3(e2), axis=AX.X, op=ALU.add), ["e2"], ["r8"])
            dve(lambda e: e.tensor_scalar_max(out=r8[:, 24:32], in0=r8[:, 24:32], scalar1=1e-24), ["r8"], ["r8"])
            act(lambda e: e.activation(out=r8[:, 24:32], in_=r8[:, 24:32], func=AF.Sqrt), ["r8"], ["r8"])
            dve(lambda e: e.reciprocal(out=r8[:, 24:32], in_=r8[:, 24:32]), ["r8"], ["r8"])
            dve(lambda e: e.tensor_tensor(out=v3(kap), in0=v3(e1), in1=bc8(r8[:, 24:32]), op=ALU.mult), ["e1", "r8"], ["kap"])
            dve(lambda e: e.tensor_scalar_add(out=e2[:], in0=a_sb[:], scalar1=-1.0), ["a_sb"], ["e2"])
            dve(lambda e: e.tensor_tensor(out=e2[:], in0=e2[:], in1=PA("ka"), op=ALU.mult), ["e2", "pA"], ["e2"])
            dve(lambda e: e.tensor_tensor(out=e2[:], in0=e2[:], in1=kf_sb[:], op=ALU.mult), ["e2", "kf_sb"], ["e2"])
            dve(lambda e: e.tensor_tensor(out=ktl[:], in0=e2[:], in1=kf_sb[:], op=ALU.add), ["e2", "kf_sb"], ["ktl"])
            dve(lambda e: e.tensor_tensor(out=bvec[:], in0=a_sb[:], in1=kap[:], op=ALU.mult), ["a_sb", "kap"], ["bvec"])
            dve(lambda e: e.tensor_tensor(out=e2[:], in0=r_sb[:], in1=ktl[:], op=ALU.mult), ["r_sb", "ktl"], ["e2"])
            dve(lambda e: e.tensor_tensor(out=e2[:], in0=e2[:], in1=PA("rk"), op=ALU.mult), ["e2", "pA"], ["e2"])
            dve(lambda e: e.tensor_reduce(out=bon[:], in_=v3(e2), axis=AX.X, op=ALU.add), ["e2"], ["bon"])
            ps, psb, pk = nps()
            mm(ps[:, 0:512], MUI, wsig[:], True, True, ["pA", "wsig"], [pk])
            mm(ps[:, 512:1024], ONES, wsig[:], True, True, ["pA", "wsig"], [pk])
            act(lambda e, ps=ps: e.copy(out=pcw[:], in_=ps[:, 0:512]), [pk], ["pcw"])
            dve(lambda e: e.tensor_tensor(out=e1[:], in0=pcw[:], in1=wsig[:], op=ALU.subtract), ["pcw", "wsig"], ["e1"])
            act(lambda e: e.activation(out=e1[:], in_=e1[:], func=AF.Exp, scale=-C0), ["e1"], ["e1"])
            dve(lambda e: e.tensor_tensor(out=TMb[:, 0, :], in0=kap[:], in1=e1[:], op=ALU.mult), ["kap", "e1"], ["TMb0"])
            act(lambda e: e.activation(out=e2[:], in_=pcw[:], func=AF.Exp, scale=-C0), ["pcw"], ["e2"])
            dve(lambda e: e.tensor_tensor(out=TMb[:, 1, :], in0=r_sb[:], in1=e2[:], op=ALU.mult), ["r_sb", "e2"], ["TMb1"])
            act(lambda e: e.activation(out=e3[:], in_=pcw[:], func=AF.Exp, scale=C0), ["pcw"], ["e3"])
            dve(lambda e: e.tensor_tensor(out=TMb[:, 2, :], in0=bvec[:], in1=e3[:], op=ALU.mult), ["bvec", "e3"], ["TMb2"])
            dve(lambda e: e.tensor_tensor(out=TMb[:, 3, :], in0=ktl[:], in1=e3[:], op=ALU.mult), ["ktl", "e3"], ["TMb3"])
            dve(lambda e, ps=ps: e.tensor_tensor(out=e1[:], in0=ps[:, 512:1024], in1=pcw[:], op=ALU.subtract), [pk, "pcw"], ["e1"])
            act(lambda e: e.activation(out=e1[:], in_=e1[:], func=AF.Exp, scale=-C0), ["e1"], ["e1"])
            for hp in range(2):
                srcb = v3(bvec).rearrange("p (j q) d -> p j q d", q=2)[:, :, hp, :]
                srck = v3(ktl).rearrange("p (j q) d -> p j q d", q=2)[:, :, hp, :]
                wl = v3(e1).rearrange("p (j q) d -> p j q d", q=2)[:, :, hp, :]
                dstb = Btz[:].rearrange("p (j q) c -> p j q c", q=2)[:, :, hp, hp * 64:hp * 64 + 64]
                dstk = Ktz[:].rearrange("p (j q) c -> p j q c", q=2)[:, :, hp, hp * 64:hp * 64 + 64]
                dve(lambda e, srcb=srcb, wl=wl, dstb=dstb: e.tensor_tensor(out=dstb, in0=srcb, in1=wl, op=ALU.mult), ["bvec", "e1"], ["Btz"])
                dve(lambda e, srck=srck, wl=wl, dstk=dstk: e.tensor_tensor(out=dstk, in0=srck, in1=wl, op=ALU.mult), ["ktl", "e1"], ["Ktz"])
            ps2, _, pk2 = nps()
            for j in range(4):
                mm(ps2[:, j:j + 1], wsig[:, j * 128:(j + 1) * 128], ONES[:, 0:1], True, True, ["wsig", "pA"], [pk2])
            act(lambda e, ps2=ps2: e.activation(out=WLfm[:], in_=ps2[:, 0:4], func=AF.Exp, scale=-C0), [pk2], ["WLfm"])
            ps, psb, pk = nps()
            for w_ in range(4):
                for j in range(4):
                    pe(lambda e, w_=w_, j=j, psb=psb: e.transpose(psb[:, (w_ * 4 + j) * 128:(w_ * 4 + j + 1) * 128], TMb[:, w_, j * 128:(j + 1) * 128], identb[:]), ["TMb%d" % w_, "identb"], [pk])
            for w_ in range(4):
                eng_ = act if w_ % 2 == 0 else dve
                if w_ % 2 == 0:
                    act(lambda e, psb=psb, w_=w_: e.copy(out=FMt[:, w_, :, :], in_=psb[:, w_ * 512:(w_ + 1) * 512].rearrange("p (j t) -> p j t", j=4)), [pk], ["FMt"])
                else:
                    dve(lambda e, psb=psb, w_=w_: e.tensor_copy(out=FMt[:, w_, :, :], in_=psb[:, w_ * 512:(w_ + 1) * 512].rearrange("p (j t) -> p j t", j=4)), [pk], ["FMt"])
            KAP, RB, BB, KKB = 0, 1, 2, 3

            def amat(lw, rw_, dst, dk, mask, neg):
                ps, psb, pk = nps()
                for h in range(8):
                    j, hp = h // 2, h % 2
                    sl = slice(hp * 64, hp * 64 + 64)
                    mm(ps[:, hoff(h):hoff(h) + 128], FMt[sl, lw, j, :], FMt[sl, rw_, j, :], True, True, ["FMt"], [pk])
                psv = ps[:, :].rearrange("p (q j t) -> p q j t", q=2, j=4)
                dstv = dst[:].rearrange("p (j q) t -> p q j t", q=2)
                mk = mask.unsqueeze(1).unsqueeze(1).to_broadcast([128, 2, 4, 128])
                if neg:
                    mk3 = mask.unsqueeze(1).to_broadcast([128, 4, 128])
                    for q in range(2):
                        dve(lambda e, q=q: e.scalar_tensor_tensor(out=dstv[:, q], in0=psv[:, q], scalar=-1.0, in1=mk3, op0=ALU.mult, op1=ALU.mult), [pk, "pA"], [dk])
                else:
                    mk3 = mask.unsqueeze(1).to_broadcast([128, 4, 128])
                    for q in range(2):
                        dve(lambda e, q=q: e.tensor_tensor(out=dstv[:, q], in0=psv[:, q], in1=mk3, op=ALU.mult), [pk, "pA"], [dk])

            amat(BB, KAP, Pw[1], "Pw1", MUS, True)
            amat(KAP, BB, Pw[0], "Pw0", MLS, True)
            amat(KKB, KAP, Am[0], "Am0", MUS, False)
            amat(BB, RB, Am[1], "Am1", MUI, False)
            amat(KKB, RB, Am[2], "Am2", MUI, False)
            ps, psb, pk = nps()
            for h in range(8):
                j, hp = h // 2, h % 2
                sl = slice(hp * 64, hp * 64 + 64)
                mm(ps[:, h * 64:(h + 1) * 64], FMt[sl, KAP, j, :], Sbf[sl, j, :], True, False, ["FMt", "Sbf"], [pk])
                mm(ps[:, h * 64:(h + 1) * 64], Am[0][:, h, :], vrw[:, h, :], False, True, ["Am0", "vrw"], [pk])
            act(lambda e, ps=ps: e.copy(out=Zb[0][:], in_=ps[:, 0:512].rearrange("p (h d) -> p h d", h=8)), [pk], ["Zb0"])
            pi = 0
            zi = 0
            for lvl in range(7):
                Pc, PTc = Pw[pi], Pw[pi + 1]
                Pk, PTk = "Pw%d" % pi, "Pw%d" % (pi + 1)
                Zc, Zn = Zb[zi], Zb[1 - zi]
                ps, psb, pk = nps()
                for h in range(8):
                    mm(ps[:, h * 64:(h + 1) * 64], identb[:], Zc[:, h, :], True, False, ["identb", "Zb%d" % zi], [pk])
                    mm(ps[:, h * 64:(h + 1) * 64], PTc[:, h, :], Zc[:, h, :], False, True, [PTk, "Zb%d" % zi], [pk])
                if lvl < 6:
                    act(lambda e, ps=ps, Zn=Zn: e.copy(out=Zn[:], in_=ps[:, 0:512].rearrange("p (h d) -> p h d", h=8)), [pk], ["Zb%d" % (1 - zi)])
                    zi = 1 - zi
                    ni = 2 - pi
                    Pn, PTn = Pw[ni], Pw[ni + 1]
                    psA, _, pkA = nps()
                    for h in range(8):
                        mm(psA[:, h * 128:(h + 1) * 128], PTc[:, h, :], Pc[:, h, :], True, True, [PTk, Pk], [pkA])
                    for a_ in range(2):
                        dve(lambda e, psA=psA, Pn=Pn, a_=a_: e.tensor_copy(out=Pn[:, 4 * a_:4 * a_ + 4, :], in_=psA[:, a_ * 512:(a_ + 1) * 512].rearrange("p (h t) -> p h t", h=4)), [pkA], ["Pw%d" % ni])
                    psB, _, pkB = nps()
                    for h in range(8):
                        mm(psB[:, h * 128:(h + 1) * 128], Pc[:, h, :], PTc[:, h, :], True, True, [Pk, PTk], [pkB])
                    for a_ in range(2):
                        act(lambda e, psB=psB, PTn=PTn, a_=a_: e.copy(out=PTn[:, 4 * a_:4 * a_ + 4, :], in_=psB[:, a_ * 512:(a_ + 1) * 512].rearrange("p (h t) -> p h t", h=4)), [pkB], ["Pw%d" % (ni + 1)])
                    pi = ni
                else:
                    act(lambda e, ps=ps: e.activation(out=Ub[:], in_=ps[:, 0:512].rearrange("p (h d) -> p h d", h=8), func=AF.Copy, scale=-1.0), [pk], ["Ub"])
            ps, psb, pk = nps()
            for h in range(8):
                j, hp = h // 2, h % 2
                sl = slice(hp * 64, hp * 64 + 64)
                o_ = ps[:, h * 64:(h + 1) * 64]
                mm(o_, Am[1][:, h, :], Ub[:, h, :], True, False, ["Am1", "Ub"], [pk])
                mm(o_, Am[2][:, h, :], vrw[:, h, :], False, False, ["Am2", "vrw"], [pk])
                mm(o_, FMt[sl, RB, j, :], Sbf[sl, j, :], False, True, ["FMt", "Sbf"], [pk])
            act(lambda e, ps=ps: e.copy(out=ysb[:], in_=ps[:, 0:512]), [pk], ["ysb"])
            ps, psb, pk = nps()
            for j in range(4):
                o_ = ps[:, j * 64:(j + 1) * 64]
                mm(o_, Btz[:, 2 * j, :], Ub[:, 2 * j, :], True, False, ["Btz", "Ub"], [pk])
                mm(o_, Ktz[:, 2 * j, :], vrw[:, 2 * j, :], False, False, ["Ktz", "vrw"], [pk])
                mm(o_, Btz[:, 2 * j + 1, :], Ub[:, 2 * j + 1, :], False, False, ["Btz", "Ub"], [pk])
                mm(o_, Ktz[:, 2 * j + 1, :], vrw[:, 2 * j + 1, :], False, True, ["Ktz", "vrw"], [pk])
            dve(lambda e: e.tensor_tensor(out=Sst[:], in0=Sst[:], in1=WLfm[:].unsqueeze(2).to_broadcast([128, 4, 64]), op=ALU.mult), ["Sst", "WLfm"], ["Sst"])
            dve(lambda e, ps=ps: e.tensor_tensor(out=Sst[:], in0=Sst[:], in1=ps[:, 0:256].rearrange("p (j d) -> p j d", j=4), op=ALU.add), [pk, "Sst"], ["Sst"])
            act(lambda e: e.copy(out=Sbf[:], in_=Sst[:]), ["Sst"], ["Sbf"])
            if debug and b == 0:
                dtl = [("r_sb", r_sb), ("kf_sb", kf_sb), ("vf", vf), ("wsig", wsig), ("a_sb", a_sb), ("g_sb", g_sb), ("kap", kap), ("ktl", ktl), ("bvec", bvec), ("ysb", ysb)]
                for i_, (k_, t_) in enumerate(dtl):
                    dma("pool", dbg[:, i_, :], t_[:], [k_], [], "dbg")
                dma("pool", dbg[:, 10, 0:256], Sst[:].rearrange("p j d -> p (j d)"), ["Sst"], [], "dbg")
                dstg = G[9]
                dve(lambda e: e.tensor_copy(out=dstg[:], in_=Ub[:].rearrange("p h d -> p (h d)")), ["Ub"], ["e1"])
                dma("pool", dbg[:, 11, :], dstg[:], ["e1"], [], "dbg")
                dve(lambda e: e.tensor_copy(out=dstg[:], in_=Am[1][:, 0:4, :].rearrange("p h d -> p (h d)")), ["Am1"], ["e1"])
                dma("pool", dbg[:, 12, :], dstg[:], ["e1"], [], "dbg")
                dve(lambda e: e.tensor_copy(out=dstg[:], in_=TMb[:, 0, :]), ["TMb0"], ["e1"])
                dma("pool", dbg[:, 13, :], dstg[:], ["e1"], [], "dbg")
                dve(lambda e: e.tensor_copy(out=dstg[:], in_=TMb[:, 2, :]), ["TMb2"], ["e1"])
                dma("pool", dbg[:, 14, :], dstg[:], ["e1"], [], "dbg")
                PP[0].max_ops = len(PP[0].ops)
            head_rw(128, ysb, "ysb", bon, "bon", vf, "vf", g_sb, "g_sb", mix, "mix", W)
            out_proj(128, mix, "mix", x_, xk, b * 128, W)
            if b + 1 < NB:
                pool(lambda e, xn=xn: e.tensor_copy(out=xn[:, :, 0:1], in_=xn[:, :, 128:129]), [xnk], [xnk])
                pool(lambda e, qx=qx: e.tensor_copy(out=qx[:, :, 0:3], in_=qx[:, :, 128:131]), [qxk], [qxk])

        ps, psb, pk = nps()
        mm(ps[0:8, 0:128], runmax[:], IDF, True, True, ["runmax", "pA"], [pk])
        mm(ps[0:8, 128:256], nBc[:], IDF, True, True, ["nBc", "pA"], [pk])
        fs = TT("fs", [8, 16])
        dve(lambda e, ps=ps: e.tensor_reduce(out=fs[:, 0:1], in_=ps[0:8, 0:128], axis=AX.X, op=ALU.max), [pk], ["fs"])
        dve(lambda e: e.tensor_scalar_max(out=fs[:, 0:1], in0=fs[:, 0:1], scalar1=0.0), ["fs"], ["fs"])
        dve(lambda e, ps=ps: e.tensor_tensor(out=fs[:, 1:2], in0=fs[:, 0:1], in1=ps[0:8, 128:129], op=ALU.subtract), [pk, "fs"], ["fs"])
        dma("pool", om, fs[:, 1:2], ["fs"], [], "fin")
        act(lambda e: e.activation(out=fs[:, 2:3], in_=fs[:, 1:2], func=AF.Exp, scale=-1.0), ["fs"], ["fs"])
        dve(lambda e: e.tensor_scalar_mul(out=fs[:, 4:8], in0=pA[0:8, OFF["rsel"][0]:OFF["rsel"][1]], scalar1=fs[:, 2:3]), ["fs", "pA"], ["fs"])
        ps, psb, pk = nps()
        mm(ps[:, 0:4], pA[0:8, OFF["lsel"][0]:OFF["lsel"][1]], fs[:, 4:8], True, True, ["pA", "fs"], [pk])
        scb = TT("scb", [128, 4])
        act(lambda e, ps=ps: e.copy(out=scb[:], in_=ps[:, 0:4]), [pk], ["scb"])
        dve(lambda e: e.tensor_tensor(out=Cst[:], in0=Cst[:], in1=scb[:].unsqueeze(2).to_broadcast([128, 4, 65]), op=ALU.mult), ["Cst", "scb"], ["Cst"])
        dma("pool", oC, Cst[:], ["Cst"], [], "fin")
        dma("pool", oS, Sst[:], ["Sst"], [], "fin")
        PP[0].finalize()
        PP[0] = Prog(ctx)

    with contextlib.ExitStack() as es_s:
        cur[0] = es_s
        if do_sample:
            W = {}
            W["xm"] = TT("s_xm", [128, D])
            junk = W["xm"]
            st = TT("s_st", [128, 4])
            mix = TT("s_mix", [128, D], BF16)
            xsb = mix
            lor = TT("s_lor", [128, 256], BF16)
            lorT = TT("s_lorT", [128, 2, 128], BF16)
            W["mixT"] = TT("s_mixT", [128, 8, 128], BF16)
            sx = TT("sx", [NS, D])
            sxT = TT("sxT", [128, 8, NS], BF16)
            spj = TT("spj", [NS, INW])
            spk = TT("spk", [128, NSP])
            sl_t = TT("sl_t", [NS, 3, 256])
            Cs = TT("Cs", [128, 4096])
            Ss = Cs
            sn_t = TT("sn_t", [128, 64])
            sm_t = TT("sm_t", [128, 1])
            scv = TT("scv", [128, 2, 4, 64])
            ssh = TT("ssh", [128, 3, 64])
            dma("sp", sx[:], xs, [], ["sx"], "sin", True)
            dma("sp", spk[:], spk_d, [], ["spk"], "sin", True)
            dma("sp", sl_t[:, 0, :], sshl_d, [], ["sl_t"], "sin", True)
            dma("sp", sl_t[:, 1, :], mul_d, [], ["sl_t"], "sin", True)
            dma("sp", Cs[:], sC_d, [], ["Cs"], "sin", True)
            dma("sp", sn_t[:], sn_d, [], ["sn_t"], "sin", True)
            dma("sp", sm_t[:], sm_d, [], ["sm_t"], "sin", True)
            dma("sp", scv[:, :, 0:3, :], sconv_d, [], ["scv"], "sin", True)
            dma("sp", ssh[:], sshift_d, [], ["ssh"], "sin", True)
            rmsnorm_T(sx, "sx", NS, sxT, "sxT", 0, "nmw", xsb, "xsb", junk, "junk", st, "st")
            for n0 in range(0, INW, 512):
                nn = min(512, INW - n0)
                ps, psb, pk = nps()
                if n0 + nn <= MLW or n0 < MLW:
                    pass
                segs = []
                a0 = n0
                while a0 < n0 + nn:
                    if a0 < MLW:
                        a1 = min(n0 + nn, MLW)
                        segs.append((a0, a1, False))
                    else:
                        a1 = n0 + nn
                        segs.append((a0, a1, True))
                    a0 = a1
                for (a0, a1, rw) in segs:
                    o_ = ps[:NS, a0 - n0:a1 - n0]
                    if not rw:
                        for c in range(8):
                            mm(o_, sxT[:, c, :], wq[:, c, a0:a1], c == 0, c == 7, ["sxT", "wq"], [pk])
                    else:
                        for c in range(8):
                            mm(o_, sxT[:, c, :], W1[:, c, a0 - MLW:a1 - MLW], c == 0, False, ["sxT", "W1"], [pk])
                        for c in range(8):
                            mm(o_, sxT[:, c, :], W2[:, c, a0 - MLW:a1 - MLW], False, c == 7, ["sxT", "W2"], [pk])
                act(lambda e, ps=ps, n0=n0, nn=nn: e.copy(out=spj[:, n0:n0 + nn], in_=ps[:NS, 0:nn]), [pk], ["spj"])
            s1v = scr1.rearrange("(b h) a d -> b a h d", b=NS)
            for a_ in range(7):
                c0_ = a_ * 512 if a_ < 4 else MLW + (a_ - 4) * 512
                dma("pool", s1v[:, a_, :, :], spj[:, c0_:c0_ + 512].rearrange("p (h d) -> p h d", h=8), ["spj"], ["scr1"], "scrw1")
            A7 = TT("A7", [128, 8, 64])
            dma("sp", A7[:, 0:7, :], scr1[:, 0:7, :], ["scr1"], ["A7"], "scrr1")
            gif = TT("gif", [128, 2])
            s_if = nc.dram_tensor("scr_if", [2, 128], F32, kind="Internal").ap()
            for g_ in range(2):
                dma("pool", s_if[g_, :].rearrange("(b h) -> b h", b=NS), spj[:, 2048 + 8 * g_:2056 + 8 * g_], ["spj"], ["scr_if"], "scrwif")
            for g_ in range(2):
                dma("sp", gif[:, g_:g_ + 1], s_if[g_, :].rearrange("(p o) -> p o", o=1), ["scr_if"], ["gif"], "scrrif")
            SP_ = lambda n: spk[:, SOFF[n][0]:SOFF[n][1]]
            pl = spj[:, MLW + 1536:MLW + 1792]
            dma("pool", osshl, pl, ["spj"], [], "fin")
            dve(lambda e: e.tensor_tensor(out=sl_t[:, 2, :], in0=sl_t[:, 0, :], in1=pl, op=ALU.subtract), ["sl_t", "spj"], ["sl_t"])
            dve(lambda e: e.tensor_tensor(out=sl_t[:, 2, :], in0=sl_t[:, 2, :], in1=sl_t[:, 1, :], op=ALU.mult), ["sl_t"], ["sl_t"])
            dve(lambda e: e.tensor_tensor(out=sl_t[:, 2, :], in0=sl_t[:, 2, :], in1=pl, op=ALU.add), ["sl_t", "spj"], ["sl_t"])
            act(lambda e: e.activation(out=lor[:NS, 0:64], in_=sl_t[:, 2, 0:64], func=AF.Tanh), ["sl_t"], ["lor"])
            act(lambda e: e.copy(out=lor[:NS, 64:128], in_=sl_t[:, 2, 64:128]), ["sl_t"], ["lor"])
            act(lambda e: e.activation(out=lor[:NS, 128:256], in_=sl_t[:, 2, 128:256], func=AF.Sigmoid), ["sl_t"], ["lor"])
            ps, psb, pk = nps()
            pe(lambda e, psb=psb: e.transpose(psb[:, 0:NS], lor[:NS, 0:128], identb[:NS, :NS]), ["lor", "identb"], [pk])
            pe(lambda e, psb=psb: e.transpose(psb[:, 128:128 + NS], lor[:NS, 128:256], identb[:NS, :NS]), ["lor", "identb"], [pk])
            act(lambda e, psb=psb: e.copy(out=lorT[:, :, 0:NS], in_=psb[:, 0:256].rearrange("p (a t) -> p a t", a=2)[:, :, 0:NS]), [pk], ["lorT"])
            ps, psb, pk = nps()
            mm(ps[:NS, 0:512], lorT[0:64, 0, 0:NS], luw[0:64, :], True, True, ["lorT", "luw"], [pk])
            mm(ps[:NS, 512:1024], lorT[64:128, 0, 0:NS], luw[64:128, :], True, True, ["lorT", "luw"], [pk])
            ps2, _, pk2 = nps()
            mm(ps2[:NS, 0:512], lorT[:, 1, 0:NS], gup[:, :], True, True, ["lorT", "gup"], [pk2])
            lo3 = spj[:, 0:1536].rearrange("p (a n) -> p a n", a=3)
            for a_ in range(2):
                act(lambda e, ps=ps, a_=a_: e.copy(out=lo3[:, a_, :], in_=ps[:NS, a_ * 512:(a_ + 1) * 512]), [pk], ["lo3"])
            act(lambda e, ps2=ps2: e.copy(out=lo3[:, 2, :], in_=ps2[:NS, 0:512]), [pk2], ["lo3"])
            for a_ in range(3):
                dma("pool", scr2.rearrange("(b h) a d -> b a h d", b=NS)[:, a_, :, :], lo3[:, a_, :].rearrange("p (h d) -> p h d", h=8), ["lo3"], ["scr2"], "scrw2")
            L3 = TT("L3", [128, 3, 64])
            dma("sp", L3[:], scr2, ["scr2"], ["L3"], "scrr2")
            big = TT("big", [128, 4096])
            sv = TT("sv", [128, 64])
            pool(lambda e: e.tensor_copy(out=scv[:, :, 3, :], in_=A7[:, 0:2, :]), ["A7"], ["scv"])
            dma("pool", osconv, scv[:, :, 1:4, :], ["scv"], [], "fin")
            qk_s = TT("qk_s", [128, 2, 64])
            cwqk = lambda w_: spk[:, SOFF["cwq"][0] + w_ * 256:SOFF["cwq"][0] + (w_ + 1) * 256].rearrange("p (j d) -> p j d", j=4)
            for w_ in range(2):
                dve(lambda e, w_=w_: e.tensor_tensor(out=big[:, 0:256].rearrange("p (j d) -> p j d", j=4), in0=scv[:, w_, :, :], in1=cwqk(w_), op=ALU.mult), ["scv", "spk"], ["big"])
                dve(lambda e, w_=w_: e.tensor_reduce(out=qk_s[:, w_, :], in_=big[:, 0:256].rearrange("p (j d) -> p d j", j=4), axis=AX.X, op=ALU.add), ["big"], ["qk_s"])
            dve(lambda e: e.tensor_tensor(out=qk_s[:], in0=qk_s[:], in1=spk[:, SOFF["cbq"][0]:SOFF["cbk"][1]].rearrange("p (a d) -> p a d", a=2), op=ALU.add), ["qk_s", "spk"], ["qk_s"])
            act(lambda e: e.activation(out=qk_s[:], in_=qk_s[:], func=AF.Silu), ["qk_s"], ["qk_s"])
            act(lambda e: e.activation(out=qk_s[:, 1, :], in_=qk_s[:, 1, :], func=AF.Copy, scale=0.125), ["qk_s"], ["qk_s"])
            dve(lambda e: e.tensor_tensor(out=sv[:, 0:2], in0=gif[:], in1=spk[:, SOFF["ib"][0]:SOFF["fb"][1]], op=ALU.add), ["gif", "spk"], ["sv"])
            act(lambda e: e.activation(out=sv[:, 9:10], in_=sv[:, 1:2], func=AF.Exp, scale=-1.0), ["sv"], ["sv"])
            act(lambda e: e.activation(out=sv[:, 2:3], in_=sv[:, 9:10], func=AF.Ln, bias=1.0, scale=1.0), ["sv"], ["sv"])
            dve(lambda e: e.tensor_tensor(out=sv[:, 3:4], in0=sm_t[:], in1=sv[:, 2:3], op=ALU.subtract), ["sv", "sm_t"], ["sv"])
            dve(lambda e: e.tensor_tensor(out=sv[:, 4:5], in0=sv[:, 3:4], in1=sv[:, 0:1], op=ALU.max), ["sv"], ["sv"])
            dma("pool", osm, sv[:, 4:5], ["sv"], [], "fin")
            dve(lambda e: e.tensor_tensor(out=sv[:, 9:10], in0=sv[:, 0:1], in1=sv[:, 4:5], op=ALU.subtract), ["sv"], ["sv"])
            act(lambda e: e.activation(out=sv[:, 5:6], in_=sv[:, 9:10], func=AF.Exp), ["sv"], ["sv"])
            dve(lambda e: e.tensor_tensor(out=sv[:, 9:10], in0=sv[:, 3:4], in1=sv[:, 4:5], op=ALU.subtract), ["sv"], ["sv"])
            act(lambda e: e.activation(out=sv[:, 6:7], in_=sv[:, 9:10], func=AF.Exp), ["sv"], ["sv"])
            act(lambda e: e.activation(out=sv[:, 7:8], in_=sv[:, 4:5], func=AF.Exp, scale=-1.0), ["sv"], ["sv"])
            q_ = qk_s[:, 0, :]
            k_ = qk_s[:, 1, :]
            v_ = A7[:, 2, :]
            b3 = lambda t: t[:, :].rearrange("p (a c) -> p a c", a=64)
            pool(lambda e: e.tensor_tensor(out=b3(big), in0=k_.unsqueeze(2).to_broadcast([128, 64, 64]), in1=v_.unsqueeze(1).to_broadcast([128, 64, 64]), op=ALU.mult), ["qk_s", "A7"], ["big"])
            dve(lambda e: e.tensor_scalar_mul(out=Cs[:], in0=Cs[:], scalar1=sv[:, 6:7]), ["Cs", "sv"], ["Cs"])
            dve(lambda e: e.scalar_tensor_tensor(out=Cs[:], in0=big[:], scalar=sv[:, 5:6], in1=Cs[:], op0=ALU.mult, op1=ALU.add), ["big", "sv", "Cs"], ["Cs"])
            dma("pool", osC, Cs[:], ["Cs"], [], "fin")
            dve(lambda e: e.tensor_scalar_mul(out=sn_t[:], in0=sn_t[:], scalar1=sv[:, 6:7]), ["sn_t", "sv"], ["sn_t"])
            dve(lambda e: e.scalar_tensor_tensor(out=sn_t[:], in0=k_, scalar=sv[:, 5:6], in1=sn_t[:], op0=ALU.mult, op1=ALU.add), ["qk_s", "sv", "sn_t"], ["sn_t"])
            dma("pool", osn, sn_t[:], ["sn_t"], [], "fin")
            pool(lambda e: e.tensor_tensor(out=b3(big), in0=Cs[:, :].rearrange("p (k v) -> p v k", k=64), in1=q_.unsqueeze(1).to_broadcast([128, 64, 64]), op=ALU.mult), ["Cs", "qk_s"], ["big"])
            hs = TT("hs", [128, 2, 64])
            dve(lambda e: e.tensor_reduce(out=hs[:, 0, :], in_=b3(big), axis=AX.X, op=ALU.add), ["big"], ["hs"])
            dve(lambda e: e.tensor_tensor(out=sv[:, 16:80 - 16] if False else big[:, 0:64], in0=q_, in1=sn_t[:], op=ALU.mult), ["qk_s", "sn_t"], ["big"])
            dve(lambda e: e.tensor_reduce(out=sv[:, 8:9], in_=big[:, 0:64], axis=AX.X, op=ALU.add), ["big"], ["sv"])
            dve(lambda e: e.scalar_tensor_tensor(out=sv[:, 9:10], in0=sv[:, 8:9], scalar=-1.0, in1=sv[:, 8:9], op0=ALU.mult, op1=ALU.max), ["sv"], ["sv"])
            dve(lambda e: e.tensor_tensor(out=sv[:, 9:10], in0=sv[:, 9:10], in1=sv[:, 7:8], op=ALU.max), ["sv"], ["sv"])
            dve(lambda e: e.reciprocal(out=sv[:, 10:11], in_=sv[:, 9:10]), ["sv"], ["sv"])
            dve(lambda e: e.tensor_scalar_mul(out=hs[:, 0, :], in0=hs[:, 0, :], scalar1=sv[:, 10:11]), ["hs", "sv"], ["hs"])
            dma("pool", osshift, A7[:, 4:7, :], ["A7"], [], "fin")
            rk3 = TT("rk3", [128, 3, 64])
            mu3 = spk[:, SOFF["mu_r"][0]:SOFF["mu_v"][1]].rearrange("p (a d) -> p a d", a=3)
            dve(lambda e: e.tensor_tensor(out=rk3[:], in0=ssh[:], in1=A7[:, 4:7, :], op=ALU.subtract), ["ssh", "A7"], ["rk3"])
            dve(lambda e: e.tensor_tensor(out=rk3[:], in0=rk3[:], in1=mu3, op=ALU.mult), ["rk3", "spk"], ["rk3"])
            dve(lambda e: e.tensor_tensor(out=rk3[:], in0=rk3[:], in1=A7[:, 4:7, :], op=ALU.add), ["rk3", "A7"], ["rk3"])
            w8 = TT("w8", [128, 8, 64])
            dve(lambda e: e.tensor_tensor(out=w8[:, 0, :], in0=L3[:, 0, :], in1=SP_("w0"), op=ALU.add), ["L3", "spk"], ["w8"])
            act(lambda e: e.activation(out=w8[:, 0, :], in_=w8[:, 0, :], func=AF.Sigmoid), ["w8"], ["w8"])
            act(lambda e: e.activation(out=w8[:, 0, :], in_=w8[:, 0, :], func=AF.Exp, scale=-C0), ["w8"], ["w8"])
            dve(lambda e: e.tensor_tensor(out=w8[:, 1, :], in0=L3[:, 1, :], in1=SP_("a0"), op=ALU.add), ["L3", "spk"], ["w8"])
            act(lambda e: e.activation(out=w8[:, 1, :], in_=w8[:, 1, :], func=AF.Sigmoid), ["w8"], ["w8"])
            dve(lambda e: e.tensor_tensor(out=w8[:, 6, :], in0=rk3[:, 1, :], in1=SP_("kk"), op=ALU.mult), ["rk3", "spk"], ["w8"])
            dve(lambda e: e.tensor_tensor(out=w8[:, 7, :], in0=w8[:, 6, :], in1=w8[:, 6, :], op=ALU.mult), ["w8"], ["w8"])
            dve(lambda e: e.tensor_reduce(out=sv[:, 11:12], in_=w8[:, 7, :], axis=AX.X, op=ALU.add), ["w8"], ["sv"])
            dve(lambda e: e.tensor_scalar_max(out=sv[:, 11:12], in0=sv[:, 11:12], scalar1=1e-24), ["sv"], ["sv"])
            act(lambda e: e.activation(out=sv[:, 11:12], in_=sv[:, 11:12], func=AF.Sqrt), ["sv"], ["sv"])
            dve(lambda e: e.reciprocal(out=sv[:, 11:12], in_=sv[:, 11:12]), ["sv"], ["sv"])
            dve(lambda e: e.tensor_scalar_mul(out=w8[:, 3, :], in0=w8[:, 6, :], scalar1=sv[:, 11:12]), ["w8", "sv"], ["w8"])
            dve(lambda e: e.tensor_tensor(out=w8[:, 6, :], in0=w8[:, 1, :], in1=SP_("ka"), op=ALU.mult), ["w8", "spk"], ["w8"])
            dve(lambda e: e.tensor_tensor(out=w8[:, 6, :], in0=w8[:, 6, :], in1=SP_("ka"), op=ALU.subtract), ["w8", "spk"], ["w8"])
            dve(lambda e: e.tensor_scalar_add(out=w8[:, 6, :], in0=w8[:, 6, :], scalar1=1.0), ["w8"], ["w8"])
            dve(lambda e: e.tensor_tensor(out=w8[:, 4, :], in0=rk3[:, 1, :], in1=w8[:, 6, :], op=ALU.mult), ["rk3", "w8"], ["w8"])
            dve(lambda e: e.tensor_tensor(out=w8[:, 5, :], in0=w8[:, 1, :], in1=w8[:, 3, :], op=ALU.mult), ["w8"], ["w8"])
            dma("sp", Ss[:], sS_d, [], ["Ss"], "sin2")
            bk = lambda ap: ap.unsqueeze(1).to_broadcast([128, 64, 64])
            bv = lambda ap: ap.unsqueeze(2).to_broadcast([128, 64, 64])
            pool(lambda e: e.tensor_tensor(out=b3(big), in0=b3(Ss), in1=bk(w8[:, 3, :]), op=ALU.mult), ["Ss", "w8"], ["big"])
            dve(lambda e: e.tensor_reduce(out=w8[:, 7, :], in_=b3(big), axis=AX.X, op=ALU.add), ["big"], ["w8"])
            dve(lambda e: e.tensor_tensor(out=b3(Ss), in0=b3(Ss), in1=bk(w8[:, 0, :]), op=ALU.mult), ["Ss", "w8"], ["Ss"])
            pool(lambda e: e.tensor_tensor(out=b3(big), in0=bv(w8[:, 7, :]), in1=bk(w8[:, 5, :]), op=ALU.mult), ["w8"], ["big"])
            dve(lambda e: e.tensor_tensor(out=Ss[:], in0=Ss[:], in1=big[:], op=ALU.subtract), ["Ss", "big"], ["Ss"])
            pool(lambda e: e.tensor_tensor(out=b3(big), in0=bv(rk3[:, 2, :]), in1=bk(w8[:, 4, :]), op=ALU.mult), ["rk3", "w8"], ["big"])
            dve(lambda e: e.tensor_tensor(out=Ss[:], in0=Ss[:], in1=big[:], op=ALU.add), ["Ss", "big"], ["Ss"])
            dma("pool", osS, Ss[:], ["Ss"], [], "fin")
            pool(lambda e: e.tensor_tensor(out=b3(big), in0=b3(Ss), in1=bk(rk3[:, 0, :]), op=ALU.mult), ["Ss", "rk3"], ["big"])
            dve(lambda e: e.tensor_reduce(out=hs[:, 1, :], in_=b3(big), axis=AX.X, op=ALU.add), ["big"], ["hs"])
            dve(lambda e: e.tensor_tensor(out=w8[:, 6, :], in0=rk3[:, 0, :], in1=w8[:, 4, :], op=ALU.mult), ["rk3", "w8"], ["w8"])
            dve(lambda e: e.tensor_tensor(out=w8[:, 6, :], in0=w8[:, 6, :], in1=SP_("rk"), op=ALU.mult), ["w8", "spk"], ["w8"])
            dve(lambda e: e.tensor_reduce(out=sv[:, 12:13], in_=w8[:, 6, :], axis=AX.X, op=ALU.add), ["w8"], ["sv"])
            dve(lambda e: e.scalar_tensor_tensor(out=hs[:, 1, :], in0=rk3[:, 2, :], scalar=sv[:, 12:13], in1=hs[:, 1, :], op0=ALU.mult, op1=ALU.add), ["rk3", "sv", "hs"], ["hs"])
            act(lambda e: e.activation(out=w8[:, 6, :], in_=A7[:, 3, :], func=AF.Sigmoid), ["A7"], ["w8"])
            dve(lambda e: e.tensor_tensor(out=hs[:, 0, :], in0=hs[:, 0, :], in1=w8[:, 6, :], op=ALU.mult), ["hs", "w8"], ["hs"])
            dve(lambda e: e.tensor_tensor(out=w8[:, 7, :], in0=hs[:, 0, :], in1=hs[:, 0, :], op=ALU.mult), ["hs"], ["w8"])
            dve(lambda e: e.tensor_reduce(out=sv[:, 13:14], in_=w8[:, 7, :], axis=AX.X, op=ALU.add), ["w8"], ["sv"])
            act(lambda e: e.activation(out=sv[:, 13:14], in_=sv[:, 13:14], func=AF.Sqrt, bias=EPS, scale=1.0 / 64), ["sv"], ["sv"])
            dve(lambda e: e.reciprocal(out=sv[:, 13:14], in_=sv[:, 13:14]), ["sv"], ["sv"])
            dve(lambda e: e.scalar_tensor_tensor(out=hs[:, 0, :], in0=hs[:, 0, :], scalar=sv[:, 13:14], in1=SP_("mnw"), op0=ALU.mult, op1=ALU.mult), ["hs", "sv", "spk"], ["hs"])
            dve(lambda e: e.tensor_reduce(out=sv[:, 14:15], in_=hs[:, 1, :], axis=AX.X, op=ALU.add), ["hs"], ["sv"])
            dve(lambda e: e.tensor_scalar_mul(out=sv[:, 14:15], in0=sv[:, 14:15], scalar1=1.0 / 64), ["sv"], ["sv"])
            dve(lambda e: e.tensor_scalar_sub(out=hs[:, 1, :], in0=hs[:, 1, :], scalar1=sv[:, 14:15]), ["hs", "sv"], ["hs"])
            dve(lambda e: e.tensor_tensor(out=w8[:, 7, :], in0=hs[:, 1, :], in1=hs[:, 1, :], op=ALU.mult), ["hs"], ["w8"])
            dve(lambda e: e.tensor_reduce(out=sv[:, 15:16], in_=w8[:, 7, :], axis=AX.X, op=ALU.add), ["w8"], ["sv"])
            act(lambda e: e.activation(out=sv[:, 15:16], in_=sv[:, 15:16], func=AF.Sqrt, bias=GN_EPS, scale=1.0 / 64), ["sv"], ["sv"])
            dve(lambda e: e.reciprocal(out=sv[:, 15:16], in_=sv[:, 15:16]), ["sv"], ["sv"])
            dve(lambda e: e.scalar_tensor_tensor(out=hs[:, 1, :], in0=hs[:, 1, :], scalar=sv[:, 15:16], in1=SP_("lnw"), op0=ALU.mult, op1=ALU.mult), ["hs", "sv", "spk"], ["hs"])
            dve(lambda e: e.tensor_tensor(out=hs[:, 1, :], in0=hs[:, 1, :], in1=SP_("lnb"), op=ALU.add), ["hs", "spk"], ["hs"])
            dve(lambda e: e.tensor_tensor(out=hs[:, 1, :], in0=hs[:, 1, :], in1=L3[:, 2, :], op=ALU.mult), ["hs", "L3"], ["hs"])
            s3v = nc.dram_tensor("scr3b", [128, 2, 64], F32, kind="Internal").ap()
            dma("pool", s3v, hs[:], ["hs"], ["scr3b"], "scrw3")
            smix = spj[:, 2304:3328].rearrange("p (a h d) -> p a h d", a=2, h=8)
            for a_ in range(2):
                dma("sp", smix[:, a_, :, :], s3v.rearrange("(b h) a d -> b a h d", b=NS)[:, a_, :, :], ["scr3b"], ["smix"], "scrr3")
            act(lambda e: e.copy(out=mix[:NS, :], in_=smix[:].rearrange("p a h d -> p (a h d)")), ["smix"], ["mix"])
            out_proj(NS, mix, "mix", sx, "sx", T, W)
        PP[0].finalize()
        PP[0] = Prog(ctx)
    es_res.close()

    with contextlib.ExitStack() as es2:
        cur[0] = es2
        upb = TT("upb", [128, 8, DFF], BF16)
        dnb = TT("dnb", [128, 32, D], BF16)
        pA2 = TT("pA2", [128, NA], F32)
        nfw = TT("nfw", [128, D])
        identb2 = TT("identb2", [128, 128], BF16)
        up_v = mlp_up.rearrange("(c p) n -> p c n", p=128)
        dn_v = mlp_down.rearrange("(c p) n -> p c n", p=128)
        dma("sp", pA2[:], packA_d, [], ["pA2"], "init2", True)
        dma("sp", nfw[:], nfw_d, [], ["nfw"], "init2", True)
        for c in range(8):
            dma("pool", upb[:, c, :], up_v[:, c, :], [], ["upb"], "init2", True)
        for c in range(0, 32, 4):
            dma("pool", dnb[:, c:c + 4, :], dn_v[:, c:c + 4, :], [], ["dnb"], "init2", True)
        dve(lambda e: e.tensor_copy(out=identb2[:], in_=pA2[:, OFF["ident"][0]:OFF["ident"][1]]), ["pA2"], ["identb2"])
        xm2 = [TT("xm2_%d" % i, [128, D]) for i in range(2)]
        junk2 = TT("junk2", [128, D])
        st2 = TT("st2", [128, 8])
        xsb2 = TT("xsb2", [128, D], BF16)
        xn2T = TT("xn2T", [128, 8, 128], BF16)
        hT = TT("hT", [128, 32, 128], BF16)
        rl = [TT("rl%d" % i, [128, 1024]) for i in range(2)]
        yo = [TT("yo%d" % i, [128, D]) for i in range(2)]
        nmlp = pA2[:, OFF["nmlp"][0]:OFF["nmlp"][1]]
        for b in range(NB + 1):
            nt = 128 if b < NB else NS
            row0 = b * 128
            x_ = xm2[b % 2]
            xk = "xm2_%d" % (b % 2)
            dma("sp", x_[:nt, :], xmid_d[row0:row0 + nt, :], ["xmid%d" % row0], [xk], xk)
            act(lambda e, x_=x_, nt=nt: e.activation(out=junk2[:nt, :], in_=x_[:nt, :], func=AF.Square, accum_out=st2[:nt, 0:1]), [xk], ["junk2", "st2"])
            act(lambda e, nt=nt: e.activation(out=st2[:nt, 1:2], in_=st2[:nt, 0:1], func=AF.Sqrt, bias=EPS, scale=1.0 / D), ["st2"], ["st2"])
            dve(lambda e, nt=nt: e.reciprocal(out=st2[:nt, 2:3], in_=st2[:nt, 1:2]), ["st2"], ["st2"])
            dve(lambda e, x_=x_, nt=nt: e.tensor_scalar_mul(out=xsb2[:nt, :], in0=x_[:nt, :], scalar1=st2[:nt, 2:3]), [xk, "st2"], ["xsb2"])
            ps, psb, pk = nps()
            for c in range(8):
                pe(lambda e, c=c, psb=psb, nt=nt: e.transpose(psb[:, c * 128:c * 128 + nt], xsb2[:nt, c * 128:(c + 1) * 128], identb2[:nt, :nt]), ["xsb2", "identb2"], [pk])
            dve(lambda e, psb=psb, nt=nt: e.tensor_tensor(out=xn2T[:, :, 0:nt], in0=psb[:, 0:1024].rearrange("p (c t) -> p c t", c=8)[:, :, 0:nt],
                                                        in1=nmlp.unsqueeze(2).to_broadcast([128, 8, nt]), op=ALU.mult), [pk, "pA2"], ["xn2T"])
            for g8 in range(4):
                ps, psb, pk = nps()
                for jj in range(8):
                    j = g8 * 8 + jj
                    for c in range(8):
                        mm(ps[:, jj * 128:jj * 128 + nt], upb[:, c, j * 128:(j + 1) * 128], xn2T[:, c, 0:nt], c == 0, c == 7, ["upb", "xn2T"], [pk])
                r_ = rl[g8 % 2]
                rk_ = "rl%d" % (g8 % 2)
                psv = ps[:, :].rearrange("p (j t) -> p j t", j=8)[:, :, 0:nt]
                rv = r_[:, :].rearrange("p (j t) -> p j t", j=8)[:, :, 0:nt]
                for a_ in range(2):
                    act(lambda e, psv=psv, rv=rv, a_=a_: e.activation(out=rv[:, 4 * a_:4 * a_ + 4, :], in_=psv[:, 4 * a_:4 * a_ + 4, :], func=AF.Relu), [pk], [rk_])
                pool(lambda e, rv=rv, g8=g8, nt=nt: e.tensor_tensor(out=hT[:, g8 * 8:(g8 + 1) * 8, 0:nt], in0=rv, in1=rv, op=ALU.mult), [rk_], ["hT"])
            ps, psb, pk = nps()
            for n in range(2):
                for j in range(32):
                    mm(ps[:nt, n * 512:(n + 1) * 512], hT[:, j, 0:nt], dnb[:, j, n * 512:(n + 1) * 512], j == 0, j == 31, ["hT", "dnb"], [pk])
            y_ = yo[b % 2]
            yk = "yo%d" % (b % 2)
            for a_ in range(2):
                dve(lambda e, ps=ps, x_=x_, y_=y_, nt=nt, a_=a_: e.tensor_tensor(out=y_[:nt, a_ * 512:(a_ + 1) * 512], in0=ps[:nt, a_ * 512:(a_ + 1) * 512], in1=x_[:nt, a_ * 512:(a_ + 1) * 512], op=ALU.add), [pk, xk], [yk])
            act(lambda e, y_=y_, nt=nt: e.activation(out=junk2[:nt, :], in_=y_[:nt, :], func=AF.Square, accum_out=st2[:nt, 4:5]), [yk], ["junk2", "st2"])
            act(lambda e, nt=nt: e.activation(out=st2[:nt, 5:6], in_=st2[:nt, 4:5], func=AF.Sqrt, bias=EPS, scale=1.0 / D), ["st2"], ["st2"])
            dve(lambda e, nt=nt: e.reciprocal(out=st2[:nt, 6:7], in_=st2[:nt, 5:6]), ["st2"], ["st2"])
            dve(lambda e, y_=y_, nt=nt: e.scalar_tensor_tensor(out=y_[:nt, :], in0=y_[:nt, :], scalar=st2[:nt, 6:7], in1=nfw[:nt, :], op0=ALU.mult, op1=ALU.mult), [yk, "st2", "nfw"], [yk])
            if b < NB:
                dma("pool", yp[row0:row0 + nt, :], y_[:nt, :], [yk], [], yk)
            else:
                dma("pool", ys, y_[:nt, :], [yk], [], yk)
        PP[0].finalize()
    es_ps.close()
    ctx.close()
    return nc


_CACHE = {}


def _host_packs(inp, core):
    f = np.float32
    L = 0
    pa = np.zeros((128, NA), f)

    def put(n, arr):
        a, b = OFF[n]
        pa[:, a:b] = arr

    rep = lambda v: np.broadcast_to(np.asarray(v, f).reshape(1, -1), (128, np.asarray(v).size))
    put("mnw", rep(inp["mlstm_norm_w"][L]))
    put("w0", rep(inp["rw_w0"][L]))
    put("a0", rep(inp["rw_a0"][L]))
    put("kk", rep(inp["rw_k_k"][L]))
    put("ka", rep(inp["rw_k_a"][L]))
    put("rk", rep(inp["rw_r_k"][L].reshape(-1)))
    put("lnw", rep(inp["rw_ln_w"][L]))
    put("lnb", rep(inp["rw_ln_b"][L]))
    put("ifb", rep(np.concatenate([inp["mlstm_i_b"][L], inp["mlstm_f_b"][L]])))
    put("nmw", inp["norm_mix_w"][L].reshape(8, 128).T)
    put("nmlp", inp["norm_mlp_w"][L].reshape(8, 128).T)
    cw = inp["mlstm_conv_w"][L]
    put("cw", cw.reshape(4, 8, 128).transpose(2, 1, 0).reshape(128, 32))
    put("cb", inp["mlstm_conv_b"][L].reshape(8, 128).T)
    put("ident", np.eye(128, dtype=f))
    put("mui", np.triu(np.ones((128, 128), f), 0))
    put("mus", np.triu(np.ones((128, 128), f), 1))
    put("mls", np.tril(np.ones((128, 128), f), -1))
    put("ones", np.ones((128, 128), f))
    lsel = np.zeros((128, 128), f)
    rsel = np.zeros((128, 4), f)
    for h in range(8):
        lsel[h, (h % 2) * 64:(h % 2) * 64 + 64] = 1.0
        rsel[h, h // 2] = 1.0
    put("lsel", lsel)
    put("rsel", rsel)
    return pa


def _sample_pack(inp):
    f = np.float32
    L = 0
    sp = np.zeros((128, NSP), f)

    def bh(v512):
        return np.tile(np.asarray(v512, f).reshape(8, 64), (NS, 1))

    def put(n, arr):
        a, b = SOFF[n]
        sp[:, a:b] = arr

    mu = inp["rw_mu"][L]
    put("mu_r", bh(mu[0:512]))
    put("mu_k", bh(mu[512:1024]))
    put("mu_v", bh(mu[1024:1536]))
    cw = inp["mlstm_conv_w"][L]
    put("cwq", np.concatenate([bh(cw[j, 0:512]) for j in range(4)], axis=1))
    put("cwk", np.concatenate([bh(cw[j, 512:1024]) for j in range(4)], axis=1))
    cb = inp["mlstm_conv_b"][L]
    put("cbq", bh(cb[0:512]))
    put("cbk", bh(cb[512:1024]))
    put("mnw", bh(inp["mlstm_norm_w"][L]))
    put("w0", bh(inp["rw_w0"][L]))
    put("a0", bh(inp["rw_a0"][L]))
    put("kk", bh(inp["rw_k_k"][L]))
    put("ka", bh(inp["rw_k_a"][L]))
    put("rk", bh(inp["rw_r_k"][L].reshape(-1)))
    put("lnw", bh(inp["rw_ln_w"][L]))
    put("lnb", bh(inp["rw_ln_b"][L]))
    put("ib", np.tile(inp["mlstm_i_b"][L].reshape(8, 1), (NS, 1)))
    put("fb", np.tile(inp["mlstm_f_b"][L].reshape(8, 1), (NS, 1)))
    return sp


def kernel(**inp):
    f = np.float32
    inp = {k: np.asarray(v) for k, v in inp.items()}
    if "nc" not in _CACHE:
        _CACHE["nc"] = build_program()
    nc = _CACHE["nc"]
    L = 0
    pa = _host_packs(inp, 0)
    sp = _sample_pack(inp)
    mu = inp["rw_mu"][L]
    luw = np.concatenate([inp["rw_w_up"][L], inp["rw_a_up"][L]], axis=0).astype(f)
    common = {
        "w_in": np.ascontiguousarray(inp["w_in"][L], f),
        "w_out": np.ascontiguousarray(inp["w_out"][L], f),
        "mlp_up": np.ascontiguousarray(inp["mlp_up"][L], f),
        "mlp_down": np.ascontiguousarray(inp["mlp_down"][L], f),
        "packA": pa,
        "mu_b": np.ascontiguousarray(np.broadcast_to(mu.reshape(1, -1), (128, RWW)), f),
        "nfw_b": np.ascontiguousarray(np.broadcast_to(inp["norm_f_w"].reshape(1, -1), (128, D)), f),
        "luw": luw,
        "gup": np.ascontiguousarray(inp["rw_g_up"][L], f),
        "spack": sp,
        "mul": np.ascontiguousarray(np.broadcast_to(mu[1536:1792].reshape(1, -1), (NS, 256)), f),
    }
    in_maps = []
    for c in range(8):
        rs = slice(c * NS, (c + 1) * NS)
        m = dict(common)
        m["xp"] = np.ascontiguousarray(inp["x_prompt"][c], f)
        m["xs"] = np.ascontiguousarray(inp["x_sample"][rs, 0, :], f)
        m["sC"] = np.ascontiguousarray(inp["state_mlstm_C"][L, rs].reshape(128, 4096), f)
        m["sn"] = np.ascontiguousarray(inp["state_mlstm_n"][L, rs].reshape(128, 64), f)
        m["sm"] = np.ascontiguousarray(inp["state_mlstm_m"][L, rs].reshape(128, 1), f)
        cv = inp["state_mlstm_conv"][L, rs]
        m["sconv"] = np.ascontiguousarray(cv.reshape(NS, 3, 2, 8, 64).transpose(0, 3, 2, 1, 4).reshape(128, 2, 3, 64), f)
        m["sS"] = np.ascontiguousarray(inp["state_rwkv_S"][L, rs].reshape(128, 4096), f)
        sh = inp["state_rwkv_shift"][L, rs, 0, :]
        m["sshift"] = np.ascontiguousarray(sh[:, 0:1536].reshape(NS, 3, 8, 64).transpose(0, 2, 1, 3).reshape(128, 3, 64), f)
        m["sshl"] = np.ascontiguousarray(sh[:, 1536:1792], f)
        in_maps.append(m)
    res = run_bass_kernel_spmd(nc, in_maps, core_ids=list(range(8)))
    R = res.results
    y_prompt = np.stack([R[c]["yp"] for c in range(8)]).astype(f)
    y_sample = np.concatenate([R[c]["ys"] for c in range(8)], axis=0).reshape(128, 1, D).astype(f)
    pC = np.zeros((1, 8, 8, 64, 64), f)
    pn = np.zeros((1, 8, 8, 64), f)
    pm = np.zeros((1, 8, 8), f)
    pconv = np.zeros((1, 8, 3, 1024), f)
    pS = np.zeros((1, 8, 8, 64, 64), f)
    pshift = np.zeros((1, 8, 1, RWW), f)
    for c in range(8):
        oC = R[c]["oC"].reshape(2, 64, 4, 65)
        Ch = oC.transpose(2, 0, 1, 3).reshape(8, 64, 65)
        pC[0, c] = Ch[:, :, 0:64]
        pn[0, c] = Ch[:, :, 64]
        pm[0, c] = R[c]["om"].reshape(8)
        pconv[0, c] = R[c]["oconv"].transpose(2, 1, 0).reshape(3, 1024)
        oS = R[c]["oS"].reshape(2, 64, 4, 64)
        pS[0, c] = oS.transpose(2, 0, 3, 1).reshape(8, 64, 64)
        pshift[0, c, 0] = R[c]["oshift"].reshape(RWW)
    sC = np.concatenate([R[c]["osC"].reshape(NS, 8, 64, 64) for c in range(8)])[None].astype(f)
    sn = np.concatenate([R[c]["osn"].reshape(NS, 8, 64) for c in range(8)])[None].astype(f)
    sm = np.concatenate([R[c]["osm"].reshape(NS, 8) for c in range(8)])[None].astype(f)
    sconv = np.concatenate([R[c]["osconv"].reshape(NS, 8, 2, 3, 64).transpose(0, 3, 2, 1, 4).reshape(NS, 3, 1024) for c in range(8)])[None].astype(f)
    sS = np.concatenate([R[c]["osS"].reshape(NS, 8, 64, 64) for c in range(8)])[None].astype(f)
    sshift = np.concatenate([
        np.concatenate([R[c]["osshift"].reshape(NS, 8, 3, 64).transpose(0, 2, 1, 3).reshape(NS, 1536), R[c]["osshl"]], axis=1)
        for c in range(8)]).reshape(1, 128, 1, RWW).astype(f)
    return (y_prompt, y_sample, pC, pn, pm, pconv, pS, pshift, sC, sn, sm, sconv, sS, sshift)
```

```python
import contextlib
import numpy as np
import concourse.bass as bass
import concourse.mybir as mybir
from concourse.bass_utils import run_bass_kernel_spmd

F32 = mybir.dt.float32
BF16 = mybir.dt.bfloat16
AF = mybir.ActivationFunctionType
ALU = mybir.AluOpType
AX = mybir.AxisListType

D = 1024
T = 2048
NB = 16
NS = 16
INW = 3856
MLW = 2064
RWW = 1792
DFF = 4096
EPS = 1e-6
GN_EPS = 64e-5
C0 = 0.6065306597126334

OFF = {}
_o = 0
for _n, _w in [("mnw", 512), ("w0", 512), ("a0", 512), ("kk", 512), ("ka", 512), ("rk", 512),
               ("lnw", 512), ("lnb", 512), ("ifb", 16), ("nmw", 8), ("nmlp", 8), ("cw", 32), ("cb", 8),
               ("ident", 128), ("mui", 128), ("mus", 128), ("mls", 128), ("ones", 128),
               ("lsel", 128), ("rsel", 4)]:
    OFF[_n] = (_o, _o + _w)
    _o += _w
NA = _o
SOFF = {}
_o = 0
for _n, _w in [("mu_r", 64), ("mu_k", 64), ("mu_v", 64), ("cwq", 256), ("cwk", 256), ("cbq", 64), ("cbk", 64),
               ("mnw", 64), ("w0", 64), ("a0", 64), ("kk", 64), ("ka", 64), ("rk", 64), ("lnw", 64), ("lnb", 64),
               ("ib", 1), ("fb", 1)]:
    SOFF[_n] = (_o, _o + _w)
    _o += _w
NSP = _o


ALIAS = {"r_sb": "G0", "kf_sb": "G1", "vf": "G2", "osig": "G3", "hml": "G4", "nlrep": "G7", "wsig": "G3", "a_sb": "G4",
         "g_sb": "G5", "kap": "G6", "ktl": "G7", "bvec": "G8", "e1": "G9", "e2": "G10", "e3": "G11", "pcw": "G11",
         "ysb": "G11", "Fb": "G10", "cacc": "G11", "Ub": "Zb1", "tA": "G9", "tB": "G10", "plast": "G9", "junk": "xm", "ctmp": "xm", "xsb": "mix", "qks": "G11",
         "PTb": "Am0", "mixT": "TMbA", "TMb0": "TMbA", "TMb1": "TMbA", "TMb2": "TMbB", "TMb3": "TMbB",
         "Ss": "Cs", "lo3": "spj", "smix": "spj", "xt1": "xt0", "xnT1": "xnT0", "qkx1": "qkx0"}


class SemCtx:
    def __init__(self, nc):
        self.nc = nc
        self.es = contextlib.ExitStack()
        self.engs = ["pe", "act", "dve", "pool", "sp"]
        self.esem = {e: self.es.enter_context(nc.semaphore("s_" + e)) for e in self.engs}
        self.ecnt = {e: 0 for e in self.engs}
        self.bsem = self.es.enter_context(nc.semaphore("s_bar"))
        self.phase = 0
        self.gsem = {}
        self.gbase = {}

    def group_sem(self, g):
        if g not in self.gsem:
            self.gsem[g] = self.es.enter_context(self.nc.semaphore("g_%d" % len(self.gsem)))
            self.gbase[g] = 0
        return self.gsem[g]

    def close(self):
        self.es.close()


class Prog:
    max_ops = None

    def __init__(self, ctx):
        self.ctx = ctx
        self.nc = ctx.nc
        self.ops = []
        self.last_writer = {}
        self.readers = {}
        self.dma_groups = {}

    def op(self, eng, fn, reads=(), writes=(), dma_group=None, wait_total=False):
        if self.max_ops is not None and len(self.ops) >= self.max_ops:
            return None
        reads = [ALIAS.get(k, k) for k in reads]
        writes = [ALIAS.get(k, k) for k in writes]
        if eng != "pe":
            writes = writes + [k for k in reads if k.startswith("PS") and k not in writes]
        deps = set()
        for b in reads:
            if b in self.last_writer:
                deps.add(self.last_writer[b])
        for b in writes:
            if b in self.last_writer:
                deps.add(self.last_writer[b])
            for r in self.readers.get(b, ()):
                deps.add(r)
        idx = len(self.ops)
        if dma_group is not None:
            deps = {d for d in deps if self.ops[d]["dma"] != dma_group}
        o = dict(eng=eng, fn=fn, deps=sorted(deps), dma=dma_group, idx=idx)
        if dma_group is not None:
            g = self.dma_groups.setdefault(dma_group, dict(total=0, wait_total=wait_total))
            g["total"] += 1
            o["dma_cnt"] = g["total"]
        self.ops.append(o)
        for b in reads:
            self.readers.setdefault(b, []).append(idx)
        for b in writes:
            self.last_writer[b] = idx
            self.readers[b] = []
        return idx

    def finalize(self):
        nc = self.nc
        ctx = self.ctx
        ops = self.ops
        needed = set()
        for o in ops:
            best = {}
            rd = []
            for d in o["deps"]:
                p = ops[d]
                if p["dma"] is not None:
                    rd.append(d)
                else:
                    if p["eng"] == "pe" and o["eng"] == "pe" and o["dma"] is None:
                        continue
                    best[p["eng"]] = max(best.get(p["eng"], -1), d)
            rd.extend(best.values())
            o["deps"] = sorted(rd)
            for d in best.values():
                needed.add(d)
        engs = ctx.engs
        last = {}
        for o in ops:
            if o["dma"] is None:
                last[o["eng"]] = o["idx"]
        needed |= set(last.values())
        cnt = dict(ctx.ecnt)
        for o in ops:
            if o["dma"] is None and o["idx"] in needed:
                cnt[o["eng"]] += 1
                o["sig"] = cnt[o["eng"]]
        for g in self.dma_groups:
            ctx.group_sem(g)
        phase = ctx.phase
        with nc.Block() as block:

            def emit_engine(ename, eng):
                known = {}
                if phase > 0:
                    eng.wait_ge(ctx.bsem, phase)
                for o in ops:
                    if o["eng"] != ename:
                        continue
                    for d in o["deps"]:
                        p = ops[d]
                        if p["dma"] is not None:
                            g = self.dma_groups[p["dma"]]
                            sem = ctx.gsem[p["dma"]]
                            val = ctx.gbase[p["dma"]] + 16 * (g["total"] if g["wait_total"] else p["dma_cnt"])
                            key = ("g", p["dma"])
                        else:
                            if p["eng"] == "pe" and ename == "pe" and o["dma"] is None:
                                continue
                            sem = ctx.esem[p["eng"]]
                            val = p["sig"]
                            key = ("e", p["eng"])
                        if known.get(key, 0) >= val:
                            continue
                        known[key] = val
                        eng.wait_ge(sem, val)
                    ins = o["fn"](eng)
                    if o["dma"] is not None:
                        ins.then_inc(ctx.gsem[o["dma"]], 16)
                    elif "sig" in o:
                        ins.then_inc(ctx.esem[ename], 1)
                if ename == "sp":
                    for e2 in engs:
                        if cnt[e2] > ctx.ecnt[e2]:
                            eng.wait_ge(ctx.esem[e2], cnt[e2])
                    for g, info in self.dma_groups.items():
                        eng.wait_ge(ctx.gsem[g], ctx.gbase[g] + 16 * info["total"])
                    eng.sem_inc(ctx.bsem, 1)

            @block.tensor
            def _(e):
                emit_engine("pe", e)

            @block.scalar
            def _(e):
                emit_engine("act", e)

            @block.vector
            def _(e):
                emit_engine("dve", e)

            @block.gpsimd
            def _(e):
                emit_engine("pool", e)

            @block.sync
            def _(e):
                emit_engine("sp", e)

        ctx.ecnt = cnt
        for g, info in self.dma_groups.items():
            ctx.gbase[g] += 16 * info["total"]
        ctx.phase += 1


STOP_EARLY = True


class _StopBuild(Exception):
    pass


def build_program(do_sample=True, debug=False):
    nc = bass.Bass("TRN2", target_bir_lowering=False)
    try:
        return _build_program(nc, do_sample, debug)
    except _StopBuild:
        return nc


def _build_program(nc, do_sample, debug):
    dbg = nc.dram_tensor("dbg", [128, 16, 512], F32, kind="ExternalOutput").ap() if debug else None
    din = lambda n, s: nc.dram_tensor(n, s, F32, kind="ExternalInput").ap()
    dout = lambda n, s: nc.dram_tensor(n, s, F32, kind="ExternalOutput").ap()
    xp = din("xp", [T, D])
    xs = din("xs", [NS, D])
    w_in = din("w_in", [D, INW])
    w_out = din("w_out", [D, D])
    mlp_up = din("mlp_up", [D, DFF])
    mlp_down = din("mlp_down", [DFF, D])
    packA_d = din("packA", [128, NA])
    mu_d = din("mu_b", [128, RWW])
    nfw_d = din("nfw_b", [128, D])
    wup_d = din("luw", [128, 512])
    gup_d = din("gup", [128, 512])
    spk_d = din("spack", [128, NSP])
    sC_d = din("sC", [128, 4096])
    sn_d = din("sn", [128, 64])
    sm_d = din("sm", [128, 1])
    sconv_d = din("sconv", [128, 2, 3, 64])
    sS_d = din("sS", [128, 4096])
    sshift_d = din("sshift", [128, 3, 64])
    sshl_d = din("sshl", [NS, 256])
    mul_d = din("mul", [NS, 256])

    yp = dout("yp", [T, D])
    ys = dout("ys", [NS, D])
    oC = dout("oC", [128, 4, 65])
    om = dout("om", [8, 1])
    oconv = dout("oconv", [128, 8, 3])
    oS = dout("oS", [128, 4, 64])
    oshift = dout("oshift", [1, RWW])
    osC = dout("osC", [128, 4096])
    osn = dout("osn", [128, 64])
    osm = dout("osm", [128, 1])
    osconv = dout("osconv", [128, 2, 3, 64])
    osS = dout("osS", [128, 4096])
    osshift = dout("osshift", [128, 3, 64])
    osshl = dout("osshl", [NS, 256])

    xmid_d = nc.dram_tensor("xmid_scr", [T + NS, D], F32, kind="Internal").ap()
    scr1 = nc.dram_tensor("scr1", [128, 8, 64], F32, kind="Internal").ap()
    scr2 = nc.dram_tensor("scr2", [128, 3, 64], F32, kind="Internal").ap()
    scr3 = nc.dram_tensor("scr3", [NS, 2, 8, 64], F32, kind="Internal").ap()

    ctx = SemCtx(nc)
    PP = [Prog(ctx)]
    es_res = contextlib.ExitStack()
    cur = [es_res]

    def TT(name, shape, dt=F32):
        return cur[0].enter_context(nc.sbuf_tensor("t_" + name, list(shape), dt))

    def dma(q, out, in_, reads, writes, group, wait_total=False):
        PP[0].op(q, lambda e: e.dma_start(out=out, in_=in_), reads=reads, writes=writes, dma_group=group, wait_total=wait_total)

    def dve(fn, r, w):
        PP[0].op("dve", fn, reads=r, writes=w)

    def act(fn, r, w):
        PP[0].op("act", fn, reads=r, writes=w)

    def pool(fn, r, w):
        PP[0].op("pool", fn, reads=r, writes=w)

    def pe(fn, r, w):
        PP[0].op("pe", fn, reads=r, writes=w)

    def mm(out, lhsT, rhs, start, stop, r, w):
        pe(lambda e: e.matmul(out, lhsT=lhsT, rhs=rhs, start=start, stop=stop), r, w)

    es_ps = contextlib.ExitStack()
    PS = [es_ps.enter_context(nc.psum_tensor("PS%d" % i, [128, 1024], F32)) for i in range(4)]
    PSB = [p.bitcast(BF16) for p in PS]
    psi = [0]

    def nps():
        i = psi[0] % 4
        psi[0] += 1
        return PS[i], PSB[i], "PS%d" % i

    wq = TT("wq", [128, 8, MLW], BF16)
    W1 = TT("W1", [128, 8, RWW], BF16)
    W2 = TT("W2", [128, 8, RWW], BF16)
    wout = TT("wout", [128, 8, D], BF16)
    luw = TT("luw", [128, 512], BF16)
    gup = TT("gup", [128, 512], BF16)
    pA = TT("pA", [128, NA], F32)
    identb = TT("identb", [128, 128], BF16)

    def PA(n):
        a, b = OFF[n]
        return pA[:, a:b]

    w_in_v = w_in.rearrange("(c p) n -> p c n", p=128)
    dma("sp", pA[:], packA_d, [], ["pA"], "init", True)
    for c in range(8):
        dma("pool", wq[:, c, :], w_in_v[:, c, 0:MLW], [], ["wq"], "init", True)
    dma("pool", luw[:], wup_d, [], ["luw"], "init", True)
    dma("pool", gup[:], gup_d, [], ["gup"], "init", True)
    w_out_v = w_out.rearrange("(c p) n -> p c n", p=128)
    for c in range(8):
        dma("pool", wout[:, c, :], w_out_v[:, c, :], [], ["wout"], "init", True)
    dve(lambda e: e.tensor_copy(out=identb[:], in_=PA("ident")), ["pA"], ["identb"])


    def _dbgdump(tag):
        if debug != tag:
            return
        dstg_ = cur[0].enter_context(nc.sbuf_tensor("t_dbgst%d" % tag, [128, 512], F32))
        def dd(slot, ap, key, n):
            dve(lambda e: e.tensor_copy(out=dstg_[:, 0:n], in_=ap), [key], ["dbgst"])
            dma("sp", dbg[:, slot, 0:n], dstg_[:, 0:n], ["dbgst"], [], "dbg")
        dd(0, PA("ident"), "pA", 128)
        dd(1, PA("mui"), "pA", 128)
        dd(2, PA("mnw"), "pA", 512)
        dd(3, PA("w0"), "pA", 512)
        dd(4, PA("lnb"), "pA", 512)
        PP[0].max_ops = len(PP[0].ops)
        PP[0].finalize()
        raise _StopBuild()
    _dbgdump(3)
    with contextlib.ExitStack() as es_prep:
        cur[0] = es_prep
        mub = TT("mub", [128, RWW], F32)
        omu = TT("omu", [128, RWW], F32)
        stg = [TT("stg%d" % i, [128, RWW], F32) for i in range(2)]
        dma("sp", mub[:], mu_d, [], ["mub"], "init", True)
        dve(lambda e: e.tensor_scalar(out=omu[:], in0=mub[:], scalar1=-1.0, scalar2=1.0, op0=ALU.mult, op1=ALU.add), ["mub"], ["omu"])
        for c in range(8):
            s = stg[c % 2]
            sk = "stg%d" % (c % 2)
            dma("sp", s[:], w_in_v[:, c, MLW:INW], [], [sk], sk)
            dve(lambda e, s=s, c=c: e.tensor_tensor(out=W1[:, c, :], in0=s[:], in1=omu[:], op=ALU.mult), [sk, "omu"], ["W1"])
            pool(lambda e, s=s, c=c: e.tensor_tensor(out=W2[:, c, :], in0=s[:], in1=mub[:], op=ALU.mult), [sk, "mub"], ["W2"])
        PP[0].finalize()
        PP[0] = Prog(ctx)

    def rmsnorm_T(xt, xk, nt, dstT, dstk, col0, wname, tmpb, tmpbk, junk, junkk, st, stk):
        act(lambda e: e.activation(out=junk[:nt, :], in_=xt[:nt, :], func=AF.Square, accum_out=st[:nt, 0:1]), [xk], [junkk, stk])
        act(lambda e: e.activation(out=st[:nt, 1:2], in_=st[:nt, 0:1], func=AF.Sqrt, bias=EPS, scale=1.0 / D), [stk], [stk])
        dve(lambda e: e.reciprocal(out=st[:nt, 2:3], in_=st[:nt, 1:2]), [stk], [stk])
        dve(lambda e: e.tensor_scalar_mul(out=tmpb[:nt, :], in0=xt[:nt, :], scalar1=st[:nt, 2:3]), [xk, stk], [tmpbk])
        ps, psb, pk = nps()
        for c in range(8):
            pe(lambda e, c=c: e.transpose(psb[:, c * 128:c * 128 + nt], tmpb[:nt, c * 128:(c + 1) * 128], identb[:nt, :nt]), [tmpbk, "identb"], [pk])
        a, b_ = OFF[wname]
        dve(lambda e: e.tensor_tensor(out=dstT[:, :, col0:col0 + nt],
                                      in0=psb[:, 0:1024].rearrange("p (c t) -> p c t", c=8)[:, :, 0:nt],
                                      in1=pA[:, a:b_].unsqueeze(2).to_broadcast([128, 8, nt]), op=ALU.mult), [pk, "pA"], [dstk])

    def head_ml(nt, hsrc, hk, osig, ok, mix, mixk, W):
        tA, tB, s8 = W["tA"], W["tB"], W["s8"]
        h3 = lambda t: t[:nt, :].rearrange("p (h d) -> p h d", h=8)
        bc = lambda t, c: t[:nt, c:c + 8].unsqueeze(2).to_broadcast([nt, 8, 64])
        dve(lambda e: e.tensor_tensor(out=tA[:nt, :], in0=hsrc[:nt, :], in1=osig[:nt, :], op=ALU.mult), [hk, ok], ["tA"])
        dve(lambda e: e.tensor_tensor(out=tB[:nt, :], in0=tA[:nt, :], in1=tA[:nt, :], op=ALU.mult), ["tA"], ["tB"])
        dve(lambda e: e.tensor_reduce(out=s8[:nt, 0:8], in_=h3(tB), axis=AX.X, op=ALU.add), ["tB"], ["s8"])
        act(lambda e: e.activation(out=s8[:nt, 8:16], in_=s8[:nt, 0:8], func=AF.Sqrt, bias=EPS, scale=1.0 / 64), ["s8"], ["s8"])
        dve(lambda e: e.reciprocal(out=s8[:nt, 16:24], in_=s8[:nt, 8:16]), ["s8"], ["s8"])
        dve(lambda e: e.tensor_tensor(out=h3(tB), in0=h3(tA), in1=bc(s8, 16), op=ALU.mult), ["tA", "s8"], ["tB"])
        dve(lambda e: e.tensor_tensor(out=mix[:nt, 0:512], in0=tB[:nt, :], in1=PA("mnw")[:nt, :], op=ALU.mult), ["tB", "pA"], [mixk])

    def head_rw(nt, ysrc, yk, bon, bonk, vf, vfk, g, gk, mix, mixk, W):
        tA, tB, s8 = W["tA"], W["tB"], W["s8"]
        h3 = lambda t: t[:nt, :].rearrange("p (h d) -> p h d", h=8)
        bc = lambda t, c: t[:nt, c:c + 8].unsqueeze(2).to_broadcast([nt, 8, 64])
        dve(lambda e: e.tensor_tensor(out=h3(tA), in0=h3(vf), in1=bc(bon, 0), op=ALU.mult), [vfk, bonk], ["tA"])
        dve(lambda e: e.tensor_tensor(out=tA[:nt, :], in0=tA[:nt, :], in1=ysrc[:nt, :], op=ALU.add), ["tA", yk], ["tA"])
        dve(lambda e: e.tensor_reduce(out=s8[:nt, 24:32], in_=h3(tA), axis=AX.X, op=ALU.add), ["tA"], ["s8"])
        dve(lambda e: e.tensor_scalar_mul(out=s8[:nt, 24:32], in0=s8[:nt, 24:32], scalar1=1.0 / 64), ["s8"], ["s8"])
        dve(lambda e: e.tensor_tensor(out=h3(tA), in0=h3(tA), in1=bc(s8, 24), op=ALU.subtract), ["tA", "s8"], ["tA"])
        dve(lambda e: e.tensor_tensor(out=tB[:nt, :], in0=tA[:nt, :], in1=tA[:nt, :], op=ALU.mult), ["tA"], ["tB"])
        dve(lambda e: e.tensor_reduce(out=s8[:nt, 32:40], in_=h3(tB), axis=AX.X, op=ALU.add), ["tB"], ["s8"])
        act(lambda e: e.activation(out=s8[:nt, 40:48], in_=s8[:nt, 32:40], func=AF.Sqrt, bias=GN_EPS, scale=1.0 / 64), ["s8"], ["s8"])
        dve(lambda e: e.reciprocal(out=s8[:nt, 48:56], in_=s8[:nt, 40:48]), ["s8"], ["s8"])
        dve(lambda e: e.tensor_tensor(out=h3(tB), in0=h3(tA), in1=bc(s8, 48), op=ALU.mult), ["tA", "s8"], ["tB"])
        dve(lambda e: e.tensor_tensor(out=tB[:nt, :], in0=tB[:nt, :], in1=PA("lnw")[:nt, :], op=ALU.mult), ["tB", "pA"], ["tB"])
        dve(lambda e: e.tensor_tensor(out=tB[:nt, :], in0=tB[:nt, :], in1=PA("lnb")[:nt, :], op=ALU.add), ["tB", "pA"], ["tB"])
        dve(lambda e: e.tensor_tensor(out=mix[:nt, 512:1024], in0=tB[:nt, :], in1=g[:nt, :], op=ALU.mult), ["tB", gk], [mixk])

    def out_proj(nt, mix, mixk, xt, xk, row0, W):
        mixT, xm = W["mixT"], W["xm"]
        ps, psb, pk = nps()
        for c in range(8):
            pe(lambda e, c=c: e.transpose(psb[:, c * 128:c * 128 + nt], mix[:nt, c * 128:(c + 1) * 128], identb[:nt, :nt]), [mixk, "identb"], [pk])
        act(lambda e: e.copy(out=mixT[:, :, 0:nt], in_=psb[:, 0:1024].rearrange("p (c t) -> p c t", c=8)[:, :, 0:nt]), [pk], ["mixT"])
        ps, psb, pk = nps()
        for n in range(2):
            for c in range(8):
                mm(ps[:nt, n * 512:(n + 1) * 512], mixT[:, c, 0:nt], wout[:, c, n * 512:(n + 1) * 512], c == 0, c == 7, ["mixT", "wout"], [pk])
        for a_ in range(2):
            dve(lambda e, a_=a_: e.tensor_tensor(out=xm[:nt, a_ * 512:(a_ + 1) * 512], in0=ps[:nt, a_ * 512:(a_ + 1) * 512], in1=xt[:nt, a_ * 512:(a_ + 1) * 512], op=ALU.add), [pk, xk], ["xm"])
        dma("pool", xmid_d[row0:row0 + nt, :], xm[:nt, :], ["xm"], ["xmid%d" % row0], "xm")

    with contextlib.ExitStack() as es1:
        cur[0] = es1
        W = {}
        Gbig = TT("Gbig", [128, 13, 512])
        G = [Gbig[:, i, :] for i in range(13)]
        W["s8"] = TT("s8", [128, 64])
        W["xm"] = TT("xm", [128, D])
        xt = [TT("xt0", [128, D])] * 2
        junk = W["xm"]
        st = TT("st", [128, 4])
        mix = TT("mix", [128, D], BF16)
        xsb = mix
        xnT = [TT("xnT0", [128, 8, 129], BF16)] * 2
        qkx = [TT("qkx0", [128, 8, 131])] * 2
        cacc = Gbig[:, 11:13, :].rearrange("p a (c t) -> p (a c) t", t=128)
        ctmp = W["xm"][:, :].rearrange("p (c t) -> p c t", c=8)
        qks = cacc
        qpb = TT("qpb", [128, 4, 128], BF16)
        kTb = TT("kTb", [128, 4, 128], BF16)
        ktm = TT("ktm", [128, 8, 64], BF16)
        vaug = TT("vaug", [128, 8, 65], BF16)
        gt = TT("gt", [128, 96])
        runmax = TT("runmax", [128, 8])
        nBc = TT("nBc", [128, 8])
        Cst = TT("Cst", [128, 4, 65])
        Cbf = TT("Cbf", [128, 4, 65], BF16)
        r_sb, kf_sb, vf = G[0], G[1], G[2]
        Fb = G[10].rearrange("p (j t) -> p j t", j=4)
        osig, hml = G[3], G[4]
        nlrep = G[7].rearrange("p (h d) -> p h d", h=8)
        wsig, a_sb, g_sb, kap, ktl, bvec, e1, e2, e3, pcw, ysb = G[3], G[4], G[5], G[6], G[7], G[8], G[9], G[10], G[11], G[12], G[11]
        W["tA"], W["tB"] = G[9], G[10]
        vrw = TT("vrw", [128, 8, 64], BF16)
        lor = TT("lor", [128, 256], BF16)
        lorT = TT("lorT", [128, 2, 128], BF16)
        r8 = TT("r8", [128, 32])
        bon = TT("bon", [128, 8])
        TMb = TT("TMb", [128, 4, 512], BF16)
        W["mixT"] = TMb[:, 0:2, :].rearrange("p a (c t) -> p (a c) t", t=128)
        Btz = TT("Btz", [128, 8, 128], BF16)
        Ktz = TT("Ktz", [128, 8, 128], BF16)
        FMt = TT("FMt", [128, 4, 4, 128], BF16)
        Am = [TT("Am%d" % i, [128, 8, 128], BF16) for i in range(3)]
        PTb = Am[0]
        Pw = [TT("Pw%d" % i, [128, 8, 128], BF16) for i in range(4)]
        Zb = [TT("Zb%d" % i, [128, 8, 64], BF16) for i in range(2)]
        Ub = Zb[1]
        Sst = TT("Sst", [128, 4, 64])
        Sbf = TT("Sbf", [128, 4, 64], BF16)
        WLfm = TT("WLfm", [128, 4])
        plast = G[9]

        pool(lambda e: e.memset(vaug[:], 1.0), [], ["vaug"])
        pool(lambda e: e.memset(Btz[:], 0.0), [], ["Btz"])
        pool(lambda e: e.memset(Ktz[:], 0.0), [], ["Ktz"])
        pool(lambda e: e.memset(Cst[:], 0.0), [], ["Cst"])
        pool(lambda e: e.memset(Cbf[:], 0.0), [], ["Cbf"])
        pool(lambda e: e.memset(Sst[:], 0.0), [], ["Sst"])
        pool(lambda e: e.memset(Sbf[:], 0.0), [], ["Sbf"])
        pool(lambda e: e.memset(runmax[:], -1e30), [], ["runmax"])
        pool(lambda e: e.memset(nBc[:], 0.0), [], ["nBc"])
        pool(lambda e: e.memset(xnT[0][:, :, 0:1], 0.0), [], ["xnT0"])
        pool(lambda e: e.memset(qkx[0][:, :, 0:3], 0.0), [], ["qkx0"])

        if debug == 2:
            dstg = TT("dbgstage", [128, 512]) if False else G[12]
            def ddump0(slot, ap, key, n):
                dve(lambda e: e.tensor_copy(out=dstg[:, 0:n], in_=ap), [key], ["pcw"])
                dma("sp", dbg[:, slot, 0:n], dstg[:, 0:n], ["pcw"], [], "dbg")
            ddump0(0, PA("ident"), "pA", 128)
            ddump0(1, PA("mui"), "pA", 128)
            ddump0(2, luw[:, :], "luw", 512)
            ddump0(3, gup[:, :], "gup", 512)
            ddump0(4, W1[:, 0, 0:512], "W1", 512)
            PP[0].max_ops = len(PP[0].ops)
            if STOP_EARLY:
                PP[0].finalize()
                raise _StopBuild()
        MUI = PA("mui")
        MUS = PA("mus")
        MLS = PA("mls")
        ONES = PA("ones")
        IDF = PA("ident")
        bc8 = lambda ap: ap.unsqueeze(2).to_broadcast([128, 8, 64])
        m8 = lambda m: m.unsqueeze(1).to_broadcast([128, 8, 128])
        v3 = lambda t: t[:].rearrange("p (h d) -> p h d", h=8)
        hoff = lambda h: (h % 2) * 512 + (h // 2) * 128

        for b in range(NB):
            x_ = xt[b % 2]
            xk = "xt%d" % (b % 2)
            xn = xnT[b % 2]
            xnk = "xnT%d" % (b % 2)
            qx = qkx[b % 2]
            qxk = "qkx%d" % (b % 2)
            dma("sp", x_[:], xp[b * 128:(b + 1) * 128, :], [], [xk], xk)
            rmsnorm_T(x_, xk, 128, xn, xnk, 1, "nmw", xsb, "xsb", junk, "junk", st, "st")
            cur_x = xn[:, :, 1:129]
            prv_x = xn[:, :, 0:128]

            ps, psb, pk = nps()
            for j in range(8):
                for c in range(8):
                    mm(ps[:, j * 128:(j + 1) * 128], wq[:, c, j * 128:(j + 1) * 128], cur_x[:, c, :], c == 0, c == 7, ["wq", xnk], [pk])
            for a_ in range(2):
                act(lambda e, ps=ps, qx=qx, a_=a_: e.copy(out=qx[:, 4 * a_:4 * a_ + 4, 3:131], in_=ps[:, a_ * 512:(a_ + 1) * 512].rearrange("p (j t) -> p j t", j=4)), [pk], [qxk])
            if b == NB - 1:
                dma("pool", oconv, qx[:, :, 128:131], [qxk], [], "fin")

            def tm_proj(col0, ncol, rw, ps_ap, pk):
                if not rw:
                    for c in range(8):
                        mm(ps_ap, cur_x[:, c, :], wq[:, c, col0:col0 + ncol], c == 0, c == 7, [xnk, "wq"], [pk])
                else:
                    for c in range(8):
                        mm(ps_ap, cur_x[:, c, :], W1[:, c, col0:col0 + ncol], c == 0, False, [xnk, "W1"], [pk])
                    for c in range(8):
                        mm(ps_ap, prv_x[:, c, :], W2[:, c, col0:col0 + ncol], False, c == 7, [xnk, "W2"], [pk])

            ps, psb, pk = nps()
            tm_proj(1024, 512, False, ps[:, 0:512], pk)
            tm_proj(1536, 512, False, ps[:, 512:1024], pk)
            act(lambda e, ps=ps: e.copy(out=vaug[:, :, 0:64], in_=ps[:, 0:512].rearrange("p (h d) -> p h d", h=8)), [pk], ["vaug"])
            act(lambda e, ps=ps: e.activation(out=osig[:], in_=ps[:, 512:1024], func=AF.Sigmoid), [pk], ["osig"])
            ps, psb, pk = nps()
            tm_proj(2048, 16, False, ps[:, 0:16], pk)
            tm_proj(0, 512, True, ps[:, 512:1024], pk)
            dve(lambda e, ps=ps: e.tensor_tensor(out=gt[:, 0:16], in0=ps[:, 0:16], in1=PA("ifb"), op=ALU.add), [pk, "pA"], ["gt"])
            act(lambda e, ps=ps: e.copy(out=r_sb[:], in_=ps[:, 512:1024]), [pk], ["r_sb"])
            ps, psb, pk = nps()
            tm_proj(512, 512, True, ps[:, 0:512], pk)
            tm_proj(1024, 512, True, ps[:, 512:1024], pk)
            act(lambda e, ps=ps: e.copy(out=kf_sb[:], in_=ps[:, 0:512]), [pk], ["kf_sb"])
            act(lambda e, ps=ps: e.copy(out=vf[:], in_=ps[:, 512:1024]), [pk], ["vf"])
            dve(lambda e, ps=ps: e.tensor_copy(out=vrw[:], in_=ps[:, 512:1024].rearrange("p (h d) -> p h d", h=8)), [pk], ["vrw"])
            ps, psb, pk = nps()
            tm_proj(1536, 256, True, ps[:, 0:256], pk)
            act(lambda e, ps=ps: e.activation(out=lor[:, 0:64], in_=ps[:, 0:64], func=AF.Tanh), [pk], ["lor"])
            act(lambda e, ps=ps: e.copy(out=lor[:, 64:128], in_=ps[:, 64:128]), [pk], ["lor"])
            act(lambda e, ps=ps: e.activation(out=lor[:, 128:256], in_=ps[:, 128:256], func=AF.Sigmoid), [pk], ["lor"])
            if b == NB - 1:
                lastc = xn[:, :, 128:129]
                for n0 in range(0, RWW, 512):
                    nn = min(512, RWW - n0)
                    ps2, _, pk2 = nps()
                    for c in range(8):
                        mm(ps2[0:1, 0:nn], lastc[:, c, :], W1[:, c, n0:n0 + nn], c == 0, False, [xnk, "W1"], [pk2])
                    for c in range(8):
                        mm(ps2[0:1, 0:nn], lastc[:, c, :], W2[:, c, n0:n0 + nn], False, c == 7, [xnk, "W2"], [pk2])
                    act(lambda e, ps2=ps2, n0=n0, nn=nn: e.copy(out=plast[0:1, 0:nn], in_=ps2[0:1, 0:nn]), [pk2], ["plast"])
                    dma("pool", oshift[:, n0:n0 + nn], plast[0:1, 0:nn], ["plast"], [], "fin")

            act(lambda e: e.activation(out=gt[:, 56:64], in_=gt[:, 8:16], func=AF.Exp, scale=-1.0), ["gt"], ["gt"])
            act(lambda e: e.activation(out=gt[:, 16:24], in_=gt[:, 56:64], func=AF.Ln, bias=1.0, scale=1.0), ["gt"], ["gt"])
            dve(lambda e: e.tensor_copy(out=nlrep[:], in_=bc8(gt[:, 16:24])), ["gt"], ["nlrep"])
            ps, psb, pk = nps()
            mm(ps[:, 0:8], MUI, gt[:, 16:24], True, True, ["pA", "gt"], [pk])
            mm(ps[:, 8:16], ONES, gt[:, 16:24], True, True, ["pA", "gt"], [pk])
            for j in range(4):
                mm(ps[:, 512 + j * 128:512 + (j + 1) * 128], nlrep[:, 2 * j:2 * j + 2, :].rearrange("p a d -> p (a d)"), MUI, True, True, ["nlrep", "pA"], [pk])
            dve(lambda e, ps=ps: e.tensor_tensor(out=gt[:, 24:32], in0=ps[:, 0:8], in1=gt[:, 0:8], op=ALU.add), [pk, "gt"], ["gt"])
            act(lambda e: e.activation(out=gt[:, 32:40], in_=gt[:, 24:32], func=AF.Exp), ["gt"], ["gt"])
            dve(lambda e, ps=ps: e.tensor_tensor(out=gt[:, 56:64], in0=gt[:, 24:32], in1=ps[:, 8:16], op=ALU.subtract), [pk, "gt"], ["gt"])
            act(lambda e: e.activation(out=gt[:, 40:48], in_=gt[:, 56:64], func=AF.Exp), ["gt"], ["gt"])
            act(lambda e, ps=ps: e.activation(out=gt[:, 48:56], in_=ps[:, 8:16], func=AF.Exp, scale=-1.0), [pk], ["gt"])
            act(lambda e, ps=ps: e.activation(out=Fb[:], in_=ps[:, 512:1024].rearrange("p (j t) -> p j t", j=4), func=AF.Exp, scale=-1.0), [pk], ["Fb"])
            dve(lambda e: e.tensor_tensor(out=gt[:, 56:64], in0=gt[:, 24:32], in1=nBc[:], op=ALU.add), ["gt", "nBc"], ["gt"])
            dve(lambda e: e.tensor_tensor(out=runmax[:], in0=runmax[:], in1=gt[:, 56:64], op=ALU.max), ["gt", "runmax"], ["runmax"])
            dve(lambda e, ps=ps: e.tensor_tensor(out=nBc[:], in0=nBc[:], in1=ps[:, 8:16], op=ALU.add), [pk, "nBc"], ["nBc"])

            cwv = PA("cw").rearrange("p (c j) -> p c j", j=4)
            wbc = lambda j: cwv[:, :, j:j + 1].to_broadcast([128, 8, 128])
            pool(lambda e, qx=qx: e.tensor_tensor(out=cacc[:], in0=qx[:, :, 3:131], in1=wbc(3), op=ALU.mult), [qxk, "pA"], ["cacc"])
            for j in range(3):
                pool(lambda e, qx=qx, j=j: e.tensor_tensor(out=ctmp[:], in0=qx[:, :, j:j + 128], in1=wbc(j), op=ALU.mult), [qxk, "pA"], ["ctmp"])
                pool(lambda e: e.tensor_tensor(out=cacc[:], in0=cacc[:], in1=ctmp[:], op=ALU.add), ["cacc", "ctmp"], ["cacc"])
            pool(lambda e: e.tensor_tensor(out=cacc[:], in0=cacc[:], in1=PA("cb").unsqueeze(2).to_broadcast([128, 8, 128]), op=ALU.add), ["cacc", "pA"], ["cacc"])
            act(lambda e: e.activation(out=qks[:], in_=cacc[:], func=AF.Silu), ["cacc"], ["qks"])
            dve(lambda e: e.tensor_tensor(out=qpb[:], in0=qks[:, 0:4, :], in1=Fb[:], op=ALU.mult), ["qks", "Fb"], ["qpb"])
            act(lambda e: e.activation(out=kTb[:], in_=qks[:, 4:8, :], func=AF.Copy, scale=0.125), ["qks"], ["kTb"])

            ps, psb, pk = nps()
            for j in range(4):
                pe(lambda e, j=j, psb=psb: e.transpose(psb[:, j * 128:(j + 1) * 128], kTb[:, j, :], identb[:]), ["kTb", "identb"], [pk])
            dve(lambda e, psb=psb: e.tensor_tensor(out=ktm[:], in0=psb[:, 0:512].rearrange("p (h d) -> p h d", h=8), in1=bc8(gt[:, 40:48]), op=ALU.mult), [pk, "gt"], ["ktm"])

            ps, psb, pk = nps()
            for h in range(8):
                j, hp = h // 2, h % 2
                sl = slice(hp * 64, hp * 64 + 64)
                mm(ps[:, hoff(h):hoff(h) + 128], kTb[sl, j, :], qpb[sl, j, :], True, True, ["kTb", "qpb"], [pk])
            for h in range(8):
                dve(lambda e, h=h, ps=ps: e.scalar_tensor_tensor(out=PTb[:, h, :], in0=ps[:, hoff(h):hoff(h) + 128], scalar=gt[:, 32 + h:33 + h], in1=MUI, op0=ALU.mult, op1=ALU.mult), [pk, "gt", "pA"], ["PTb"])
            ps, psb, pk = nps()
            psn = lambda ps, h: ps[:, (h // 4) * 512 + (h % 4) * 65:(h // 4) * 512 + (h % 4) * 65 + 65]
            for h in range(8):
                j, hp = h // 2, h % 2
                sl = slice(hp * 64, hp * 64 + 64)
                mm(psn(ps, h), PTb[:, h, :], vaug[:, h, :], True, False, ["PTb", "vaug"], [pk])
                mm(psn(ps, h), qpb[sl, j, :], Cbf[sl, j, :], False, True, ["qpb", "Cbf"], [pk])
            pn4 = ps[:, :].rearrange("p (a r) -> p a r", a=2)[:, :, 0:260].rearrange("p a (h d) -> p a h d", h=4)
            for a_ in range(2):
                act(lambda e, pn4=pn4, a_=a_: e.copy(out=r8[:, 4 * a_:4 * a_ + 4], in_=pn4[:, a_, :, 64]), [pk], ["r8"])
            dve(lambda e: e.scalar_tensor_tensor(out=r8[:, 8:16], in0=r8[:, 0:8], scalar=-1.0, in1=r8[:, 0:8], op0=ALU.mult, op1=ALU.max), ["r8"], ["r8"])
            dve(lambda e: e.tensor_scalar_max(out=r8[:, 8:16], in0=r8[:, 8:16], scalar1=1.0), ["r8"], ["r8"])
            dve(lambda e: e.reciprocal(out=r8[:, 16:24], in_=r8[:, 8:16]), ["r8"], ["r8"])
            for a_ in range(2):
                dve(lambda e, pn4=pn4, a_=a_: e.tensor_tensor(out=hml[:, a_ * 256:(a_ + 1) * 256].rearrange("p (h d) -> p h d", h=4), in0=pn4[:, a_, :, 0:64],
                                                          in1=r8[:, 16 + 4 * a_:20 + 4 * a_].unsqueeze(2).to_broadcast([128, 4, 64]), op=ALU.mult), [pk, "r8"], ["hml"])
            ps, psb, pk = nps()
            for h in range(8):
                j = h // 2
                mm(psn(ps, h), ktm[:, 2 * j:2 * j + 2, :].rearrange("p a d -> p (a d)"), vaug[:, h, :], True, True, ["ktm", "vaug"], [pk])
            pu4 = ps[:, :].rearrange("p (a r) -> p a r", a=2)[:, :, 0:260].rearrange("p a (h d) -> p a h d", h=4)
            for hp in range(2):
                sl = slice(hp * 64, hp * 64 + 64)
                decb = gt[sl, 48:56].rearrange("p (j q) -> p j q", q=2)[:, :, hp:hp + 1].to_broadcast([64, 4, 65])
                dve(lambda e, sl=sl, decb=decb: e.tensor_tensor(out=Cst[sl, :, :], in0=Cst[sl, :, :], in1=decb, op=ALU.mult), ["Cst", "gt"], ["Cst"])
                for a in range(2):
                    src = pu4[sl, a, hp::2, :]
                    dve(lambda e, sl=sl, a=a, src=src: e.tensor_tensor(out=Cst[sl, 2 * a:2 * a + 2, :], in0=Cst[sl, 2 * a:2 * a + 2, :], in1=src, op=ALU.add), [pk, "Cst"], ["Cst"])
            act(lambda e: e.copy(out=Cbf[:], in_=Cst[:]), ["Cst"], ["Cbf"])
            head_ml(128, hml, "hml", osig, "osig", mix, "mix", W)

            ps, psb, pk = nps()
            pe(lambda e, psb=psb: e.transpose(psb[:, 0:128], lor[:, 0:128], identb[:]), ["lor", "identb"], [pk])
            pe(lambda e, psb=psb: e.transpose(psb[:, 128:256], lor[:, 128:256], identb[:]), ["lor", "identb"], [pk])
            act(lambda e, psb=psb: e.copy(out=lorT[:], in_=psb[:, 0:256].rearrange("p (a t) -> p a t", a=2)), [pk], ["lorT"])
            ps, psb, pk = nps()
            mm(ps[:, 0:512], lorT[0:64, 0, :], luw[0:64, :], True, True, ["lorT", "luw"], [pk])
            mm(ps[:, 512:1024], lorT[64:128, 0, :], luw[64:128, :], True, True, ["lorT", "luw"], [pk])
            dve(lambda e, ps=ps: e.tensor_tensor(out=e1[:], in0=ps[:, 0:512], in1=PA("w0"), op=ALU.add), [pk, "pA"], ["e1"])
            act(lambda e: e.activation(out=wsig[:], in_=e1[:], func=AF.Sigmoid), ["e1"], ["wsig"])
            dve(lambda e, ps=ps: e.tensor_tensor(out=e2[:], in0=ps[:, 512:1024], in1=PA("a0"), op=ALU.add), [pk, "pA"], ["e2"])
            act(lambda e: e.activation(out=a_sb[:], in_=e2[:], func=AF.Sigmoid), ["e2"], ["a_sb"])
            ps, psb, pk = nps()
            mm(ps[:, 0:512], lorT[:, 1, :], gup[:, :], True, True, ["lorT", "gup"], [pk])
            act(lambda e, ps=ps: e.copy(out=g_sb[:], in_=ps[:, 0:512]), [pk], ["g_sb"])
            if debug and b == 0:
                dstg = G[12]
                def ddump(slot, ap, key, n):
                    dve(lambda e: e.tensor_copy(out=dstg[:, 0:n], in_=ap), [key], ["pcw"])
                    dma("pool", dbg[:, slot, 0:n], dstg[:, 0:n], ["pcw"], [], "dbg")
                ddump(15, lor[:, :], "lor", 256)
                ddump(13, lorT[:].rearrange("p a t -> p (a t)"), "lorT", 256)
                ddump(14, luw[:, :], "luw", 512)
                ddump(12, gup[:, :], "gup", 512)
                ddump(11, wsig[:, :], "wsig", 512)
                ddump(10, g_sb[:, :], "g_sb", 512)
                PP[0].max_ops = len(PP[0].ops)
            dve(lambda e: e.tensor_tensor(out=e1[:], in0=kf_sb[:], in1=PA("kk"), op=ALU.mult), ["kf_sb", "pA"], ["e1"])
            dve(lambda e: e.tensor_tensor(out=e2[:], in0=e1[:], in1=e1[:], op=ALU.mult), ["e1"], ["e2"])
            dve(lambda e: e.tensor_reduce(out=r8[:, 24:32], in_=v3(e2), axis=AX.X, op=ALU.add), ["e2"], ["r8"])
            dve(lambda e: e.tensor_scalar_max(out=r8[:, 24:32], in0=r8[:, 24:32], scalar1=1e-24), ["r8"], ["r8"])
            act(lambda e: e.activation(out=r8[:, 24:32], in_=r8[:, 24:32], func=AF.Sqrt), ["r8"], ["r8"])
            dve(lambda e: e.reciprocal(out=r8[:, 24:32], in_=r8[:, 24:32]), ["r8"], ["r8"])
            dve(lambda e: e.tensor_tensor(out=v3(kap), in0=v3(e1), in1=bc8(r8[:, 24:32]), op=ALU.mult), ["e1", "r8"], ["kap"])
            dve(lambda e: e.tensor_scalar_add(out=e2[:], in0=a_sb[:], scalar1=-1.0), ["a_sb"], ["e2"])
            dve(lambda e: e.tensor_tensor(out=e2[:], in0=e2[:], in1=PA("ka"), op=ALU.mult), ["e2", "pA"], ["e2"])
            dve(lambda e: e.tensor_tensor(out=e2[:], in0=e2[:], in1=kf_sb[:], op=ALU.mult), ["e2", "kf_sb"], ["e2"])
            dve(lambda e: e.tensor_tensor(out=ktl[:], in0=e2[:], in1=kf_sb[:], op=ALU.add), ["e2", "kf_sb"], ["ktl"])
            dve(lambda e: e.tensor_tensor(out=bvec[:], in0=a_sb[:], in1=kap[:], op=ALU.mult), ["a_sb", "kap"], ["bvec"])
            dve(lambda e: e.tensor_tensor(out=e2[:], in0=r_sb[:], in1=ktl[:], op=ALU.mult), ["r_sb", "ktl"], ["e2"])
            dve(lambda e: e.tensor_tensor(out=e2[:], in0=e2[:], in1=PA("rk"), op=ALU.mult), ["e2", "pA"], ["e2"])
            dve(lambda e: e.tensor_reduce(out=bon[:], in_=v3(e2), axis=AX.X, op=ALU.add), ["e2"], ["bon"])
            ps, psb, pk = nps()
            mm(ps[:, 0:512], MUI, wsig[:], True, True, ["pA", "wsig"], [pk])
            mm(ps[:, 512:1024], ONES, wsig[:], True, True, ["pA", "wsig"], [pk])
            act(lambda e, ps=ps: e.copy(out=pcw[:], in_=ps[:, 0:512]), [pk], ["pcw"])
            dve(lambda e: e.tensor_tensor(out=e1[:], in0=pcw[:], in1=wsig[:], op=ALU.subtract), ["pcw", "wsig"], ["e1"])
            act(lambda e: e.activation(out=e1[:], in_=e1[:], func=AF.Exp, scale=-C0), ["e1"], ["e1"])
            dve(lambda e: e.tensor_tensor(out=TMb[:, 0, :], in0=kap[:], in1=e1[:], op=ALU.mult), ["kap", "e1"], ["TMb0"])
            act(lambda e: e.activation(out=e2[:], in_=pcw[:], func=AF.Exp, scale=-C0), ["pcw"], ["e2"])
            dve(lambda e: e.tensor_tensor(out=TMb[:, 1, :], in0=r_sb[:], in1=e2[:], op=ALU.mult), ["r_sb", "e2"], ["TMb1"])
            act(lambda e: e.activation(out=e3[:], in_=pcw[:], func=AF.Exp, scale=C0), ["pcw"], ["e3"])
            dve(lambda e: e.tensor_tensor(out=TMb[:, 2, :], in0=bvec[:], in1=e3[:], op=ALU.mult), ["bvec", "e3"], ["TMb2"])
            dve(lambda e: e.tensor_tensor(out=TMb[:, 3, :], in0=ktl[:], in1=e3[:], op=ALU.mult), ["ktl", "e3"], ["TMb3"])
            dve(lambda e, ps=ps: e.tensor_tensor(out=e1[:], in0=ps[:, 512:1024], in1=pcw[:], op=ALU.subtract), [pk, "pcw"], ["e1"])
            act(lambda e: e.activation(out=e1[:], in_=e1[:], func=AF.Exp, scale=-C0), ["e1"], ["e1"])
            for hp in range(2):
                srcb = v3(bvec).rearrange("p (j q) d -> p j q d", q=2)[:, :, hp, :]
                srck = v3(ktl).rearrange("p (j q) d -> p j q d", q=2)[:, :, hp, :]
                wl = v3(e1).rearrange("p (j q) d -> p j q d", q=2)[:, :, hp, :]
                dstb = Btz[:].rearrange("p (j q) c -> p j q c", q=2)[:, :, hp, hp * 64:hp * 64 + 64]
                dstk = Ktz[:].rearrange("p (j q) c -> p j q c", q=2)[:, :, hp, hp * 64:hp * 64 + 64]
                dve(lambda e, srcb=srcb, wl=wl, dstb=dstb: e.tensor_tensor(out=dstb, in0=srcb, in1=wl, op=ALU.mult), ["bvec", "e1"], ["Btz"])
                dve(lambda e, srck=srck, wl=wl, dstk=dstk: e.tensor_tensor(out=dstk, in0=srck, in1=wl, op=ALU.mult), ["ktl", "e1"], ["Ktz"])
            ps2, _, pk2 = nps()
            for j in range(4):
                mm(ps2[:, j:j + 1], wsig[:, j * 128:(j + 1) * 128], ONES[:, 0:1], True, True, ["wsig", "pA"], [pk2])
            act(lambda e, ps2=ps2: e.activation(out=WLfm[:], in_=ps2[:, 0:4], func=AF.Exp, scale=-C0), [pk2], ["WLfm"])
            ps, psb, pk = nps()
            for w_ in range(4):
                for j in range(4):
                    pe(lambda e, w_=w_, j=j, psb=psb: e.transpose(psb[:, (w_ * 4 + j) * 128:(w_ * 4 + j + 1) * 128], TMb[:, w_, j * 128:(j + 1) * 128], identb[:]), ["TMb%d" % w_, "identb"], [pk])
            for w_ in range(4):
                eng_ = act if w_ % 2 == 0 else dve
                if w_ % 2 == 0:
                    act(lambda e, psb=psb, w_=w_: e.copy(out=FMt[:, w_, :, :], in_=psb[:, w_ * 512:(w_ + 1) * 512].rearrange("p (j t) -> p j t", j=4)), [pk], ["FMt"])
                else:
                    dve(lambda e, psb=psb, w_=w_: e.tensor_copy(out=FMt[:, w_, :, :], in_=psb[:, w_ * 512:(w_ + 1) * 512].rearrange("p (j t) -> p j t", j=4)), [pk], ["FMt"])
            KAP, RB, BB, KKB = 0, 1, 2, 3

            def amat(lw, rw_, dst, dk, mask, neg):
                ps, psb, pk = nps()
                for h in range(8):
                    j, hp = h // 2, h % 2
                    sl = slice(hp * 64, hp * 64 + 64)
                    mm(ps[:, hoff(h):hoff(h) + 128], FMt[sl, lw, j, :], FMt[sl, rw_, j, :], True, True, ["FMt"], [pk])
                psv = ps[:, :].rearrange("p (q j t) -> p q j t", q=2, j=4)
                dstv = dst[:].rearrange("p (j q) t -> p q j t", q=2)
                mk = mask.unsqueeze(1).unsqueeze(1).to_broadcast([128, 2, 4, 128])
                if neg:
                    mk3 = mask.unsqueeze(1).to_broadcast([128, 4, 128])
                    for q in range(2):
                        dve(lambda e, q=q: e.scalar_tensor_tensor(out=dstv[:, q], in0=psv[:, q], scalar=-1.0, in1=mk3, op0=ALU.mult, op1=ALU.mult), [pk, "pA"], [dk])
                else:
                    mk3 = mask.unsqueeze(1).to_broadcast([128, 4, 128])
                    for q in range(2):
                        dve(lambda e, q=q: e.tensor_tensor(out=dstv[:, q], in0=psv[:, q], in1=mk3, op=ALU.mult), [pk, "pA"], [dk])

            amat(BB, KAP, Pw[1], "Pw1", MUS, True)
            amat(KAP, BB, Pw[0], "Pw0", MLS, True)
            amat(KKB, KAP, Am[0], "Am0", MUS, False)
            amat(BB, RB, Am[1], "Am1", MUI, False)
            amat(KKB, RB, Am[2], "Am2", MUI, False)
            ps, psb, pk = nps()
            for h in range(8):
                j, hp = h // 2, h % 2
                sl = slice(hp * 64, hp * 64 + 64)
                mm(ps[:, h * 64:(h + 1) * 64], FMt[sl, KAP, j, :], Sbf[sl, j, :], True, False, ["FMt", "Sbf"], [pk])
                mm(ps[:, h * 64:(h + 1) * 64], Am[0][:, h, :], vrw[:, h, :], False, True, ["Am0", "vrw"], [pk])
            act(lambda e, ps=ps: e.copy(out=Zb[0][:], in_=ps[:, 0:512].rearrange("p (h d) -> p h d", h=8)), [pk], ["Zb0"])
            pi = 0
            zi = 0
            for lvl in range(7):
                Pc, PTc = Pw[pi], Pw[pi + 1]
                Pk, PTk = "Pw%d" % pi, "Pw%d" % (pi + 1)
                Zc, Zn = Zb[zi], Zb[1 - zi]
                ps, psb, pk = nps()
                for h in range(8):
                    mm(ps[:, h * 64:(h + 1) * 64], identb[:], Zc[:, h, :], True, False, ["identb", "Zb%d" % zi], [pk])
                    mm(ps[:, h * 64:(h + 1) * 64], PTc[:, h, :], Zc[:, h, :], False, True, [PTk, "Zb%d" % zi], [pk])
                if lvl < 6:
                    act(lambda e, ps=ps, Zn=Zn: e.copy(out=Zn[:], in_=ps[:, 0:512].rearrange("p (h d) -> p h d", h=8)), [pk], ["Zb%d" % (1 - zi)])
                    zi = 1 - zi
                    ni = 2 - pi
                    Pn, PTn = Pw[ni], Pw[ni + 1]
                    psA, _, pkA = nps()
                    for h in range(8):
                        mm(psA[:, h * 128:(h + 1) * 128], PTc[:, h, :], Pc[:, h, :], True, True, [PTk, Pk], [pkA])
                    for a_ in range(2):
                        dve(lambda e, psA=psA, Pn=Pn, a_=a_: e.tensor_copy(out=Pn[:, 4 * a_:4 * a_ + 4, :], in_=psA[:, a_ * 512:(a_ + 1) * 512].rearrange("p (h t) -> p h t", h=4)), [pkA], ["Pw%d" % ni])
                    psB, _, pkB = nps()
                    for h in range(8):
                        mm(psB[:, h * 128:(h + 1) * 128], Pc[:, h, :], PTc[:, h, :], True, True, [Pk, PTk], [pkB])
                    for a_ in range(2):
                        act(lambda e, psB=psB, PTn=PTn, a_=a_: e.copy(out=PTn[:, 4 * a_:4 * a_ + 4, :], in_=psB[:, a_ * 512:(a_ + 1) * 512].rearrange("p (h t) -> p h t", h=4)), [pkB], ["Pw%d" % (ni + 1)])
                    pi = ni
                else:
                    act(lambda e, ps=ps: e.activation(out=Ub[:], in_=ps[:, 0:512].rearrange("p (h d) -> p h d", h=8), func=AF.Copy, scale=-1.0), [pk], ["Ub"])
            ps, psb, pk = nps()
            for h in range(8):
                j, hp = h // 2, h % 2
                sl = slice(hp * 64, hp * 64 + 64)
                o_ = ps[:, h * 64:(h + 1) * 64]
                mm(o_, Am[1][:, h, :], Ub[:, h, :], True, False, ["Am1", "Ub"], [pk])
                mm(o_, Am[2][:, h, :], vrw[:, h, :], False, False, ["Am2", "vrw"], [pk])
                mm(o_, FMt[sl, RB, j, :], Sbf[sl, j, :], False, True, ["FMt", "Sbf"], [pk])
            act(lambda e, ps=ps: e.copy(out=ysb[:], in_=ps[:, 0:512]), [pk], ["ysb"])
            ps, psb, pk = nps()
            for j in range(4):
                o_ = ps[:, j * 64:(j + 1) * 64]
                mm(o_, Btz[:, 2 * j, :], Ub[:, 2 * j, :], True, False, ["Btz", "Ub"], [pk])
                mm(o_, Ktz[:, 2 * j, :], vrw[:, 2 * j, :], False, False, ["Ktz", "vrw"], [pk])
                mm(o_, Btz[:, 2 * j + 1, :], Ub[:, 2 * j + 1, :], False, False, ["Btz", "Ub"], [pk])
                mm(o_, Ktz[:, 2 * j + 1, :], vrw[:, 2 * j + 1, :], False, True, ["Ktz", "vrw"], [pk])
            dve(lambda e: e.tensor_tensor(out=Sst[:], in0=Sst[:], in1=WLfm[:].unsqueeze(2).to_broadcast([128, 4, 64]), op=ALU.mult), ["Sst", "WLfm"], ["Sst"])
            dve(lambda e, ps=ps: e.tensor_tensor(out=Sst[:], in0=Sst[:], in1=ps[:, 0:256].rearrange("p (j d) -> p j d", j=4), op=ALU.add), [pk, "Sst"], ["Sst"])
            act(lambda e: e.copy(out=Sbf[:], in_=Sst[:]), ["Sst"], ["Sbf"])
            if debug and b == 0:
                dtl = [("r_sb", r_sb), ("kf_sb", kf_sb), ("vf", vf), ("wsig", wsig), ("a_sb", a_sb), ("g_sb", g_sb), ("kap", kap), ("ktl", ktl), ("bvec", bvec), ("ysb", ysb)]
                for i_, (k_, t_) in enumerate(dtl):
                    dma("pool", dbg[:, i_, :], t_[:], [k_], [], "dbg")
                dma("pool", dbg[:, 10, 0:256], Sst[:].rearrange("p j d -> p (j d)"), ["Sst"], [], "dbg")
                dstg = G[9]
                dve(lambda e: e.tensor_copy(out=dstg[:], in_=Ub[:].rearrange("p h d -> p (h d)")), ["Ub"], ["e1"])
                dma("pool", dbg[:, 11, :], dstg[:], ["e1"], [], "dbg")
                dve(lambda e: e.tensor_copy(out=dstg[:], in_=Am[1][:, 0:4, :].rearrange("p h d -> p (h d)")), ["Am1"], ["e1"])
                dma("pool", dbg[:, 12, :], dstg[:], ["e1"], [], "dbg")
                dve(lambda e: e.tensor_copy(out=dstg[:], in_=TMb[:, 0, :]), ["TMb0"], ["e1"])
                dma("pool", dbg[:, 13, :], dstg[:], ["e1"], [], "dbg")
                dve(lambda e: e.tensor_copy(out=dstg[:], in_=TMb[:, 2, :]), ["TMb2"], ["e1"])
                dma("pool", dbg[:, 14, :], dstg[:], ["e1"], [], "dbg")
                PP[0].max_ops = len(PP[0].ops)
            head_rw(128, ysb, "ysb", bon, "bon", vf, "vf", g_sb, "g_sb", mix, "mix", W)
            out_proj(128, mix, "mix", x_, xk, b * 128, W)
            if b + 1 < NB:
                pool(lambda e, xn=xn: e.tensor_copy(out=xn[:, :, 0:1], in_=xn[:, :, 128:129]), [xnk], [xnk])
                pool(lambda e, qx=qx: e.tensor_copy(out=qx[:, :, 0:3], in_=qx[:, :, 128:131]), [qxk], [qxk])

        ps, psb, pk = nps()
        mm(ps[0:8, 0:128], runmax[:], IDF, True, True, ["runmax", "pA"], [pk])
        mm(ps[0:8, 128:256], nBc[:], IDF, True, True, ["nBc", "pA"], [pk])
        fs = TT("fs", [8, 16])
        dve(lambda e, ps=ps: e.tensor_reduce(out=fs[:, 0:1], in_=ps[0:8, 0:128], axis=AX.X, op=ALU.max), [pk], ["fs"])
        dve(lambda e: e.tensor_scalar_max(out=fs[:, 0:1], in0=fs[:, 0:1], scalar1=0.0), ["fs"], ["fs"])
        dve(lambda e, ps=ps: e.tensor_tensor(out=fs[:, 1:2], in0=fs[:, 0:1], in1=ps[0:8, 128:129], op=ALU.subtract), [pk, "fs"], ["fs"])
        dma("pool", om, fs[:, 1:2], ["fs"], [], "fin")
        act(lambda e: e.activation(out=fs[:, 2:3], in_=fs[:, 1:2], func=AF.Exp, scale=-1.0), ["fs"], ["fs"])
        dve(lambda e: e.tensor_scalar_mul(out=fs[:, 4:8], in0=pA[0:8, OFF["rsel"][0]:OFF["rsel"][1]], scalar1=fs[:, 2:3]), ["fs", "pA"], ["fs"])
        ps, psb, pk = nps()
        mm(ps[:, 0:4], pA[0:8, OFF["lsel"][0]:OFF["lsel"][1]], fs[:, 4:8], True, True, ["pA", "fs"], [pk])
        scb = TT("scb", [128, 4])
        act(lambda e, ps=ps: e.copy(out=scb[:], in_=ps[:, 0:4]), [pk], ["scb"])
        dve(lambda e: e.tensor_tensor(out=Cst[:], in0=Cst[:], in1=scb[:].unsqueeze(2).to_broadcast([128, 4, 65]), op=ALU.mult), ["Cst", "scb"], ["Cst"])
        dma("pool", oC, Cst[:], ["Cst"], [], "fin")
        dma("pool", oS, Sst[:], ["Sst"], [], "fin")
        PP[0].finalize()
        PP[0] = Prog(ctx)

    with contextlib.ExitStack() as es_s:
        cur[0] = es_s
        if do_sample:
            W = {}
            W["xm"] = TT("s_xm", [128, D])
            junk = W["xm"]
            st = TT("s_st", [128, 4])
            mix = TT("s_mix", [128, D], BF16)
            xsb = mix
            lor = TT("s_lor", [128, 256], BF16)
            lorT = TT("s_lorT", [128, 2, 128], BF16)
            W["mixT"] = TT("s_mixT", [128, 8, 128], BF16)
            sx = TT("sx", [NS, D])
            sxT = TT("sxT", [128, 8, NS], BF16)
            spj = TT("spj", [NS, INW])
            spk = TT("spk", [128, NSP])
            sl_t = TT("sl_t", [NS, 3, 256])
            Cs = TT("Cs", [128, 4096])
            Ss = Cs
            sn_t = TT("sn_t", [128, 64])
            sm_t = TT("sm_t", [128, 1])
            scv = TT("scv", [128, 2, 4, 64])
            ssh = TT("ssh", [128, 3, 64])
            dma("sp", sx[:], xs, [], ["sx"], "sin", True)
            dma("sp", spk[:], spk_d, [], ["spk"], "sin", True)
            dma("sp", sl_t[:, 0, :], sshl_d, [], ["sl_t"], "sin", True)
            dma("sp", sl_t[:, 1, :], mul_d, [], ["sl_t"], "sin", True)
            dma("sp", Cs[:], sC_d, [], ["Cs"], "sin", True)
            dma("sp", sn_t[:], sn_d, [], ["sn_t"], "sin", True)
            dma("sp", sm_t[:], sm_d, [], ["sm_t"], "sin", True)
            dma("sp", scv[:, :, 0:3, :], sconv_d, [], ["scv"], "sin", True)
            dma("sp", ssh[:], sshift_d, [], ["ssh"], "sin", True)
            rmsnorm_T(sx, "sx", NS, sxT, "sxT", 0, "nmw", xsb, "xsb", junk, "junk", st, "st")
            for n0 in range(0, INW, 512):
                nn = min(512, INW - n0)
                ps, psb, pk = nps()
                if n0 + nn <= MLW or n0 < MLW:
                    pass
                segs = []
                a0 = n0
                while a0 < n0 + nn:
                    if a0 < MLW:
                        a1 = min(n0 + nn, MLW)
                        segs.append((a0, a1, False))
                    else:
                        a1 = n0 + nn
                        segs.append((a0, a1, True))
                    a0 = a1
                for (a0, a1, rw) in segs:
                    o_ = ps[:NS, a0 - n0:a1 - n0]
                    if not rw:
                        for c in range(8):
                            mm(o_, sxT[:, c, :], wq[:, c, a0:a1], c == 0, c == 7, ["sxT", "wq"], [pk])
                    else:
                        for c in range(8):
                            mm(o_, sxT[:, c, :], W1[:, c, a0 - MLW:a1 - MLW], c == 0, False, ["sxT", "W1"], [pk])
                        for c in range(8):
                            mm(o_, sxT[:, c, :], W2[:, c, a0 - MLW:a1 - MLW], False, c == 7, ["sxT", "W2"], [pk])
                act(lambda e, ps=ps, n0=n0, nn=nn: e.copy(out=spj[:, n0:n0 + nn], in_=ps[:NS, 0:nn]), [pk], ["spj"])
            s1v = scr1.rearrange("(b h) a d -> b a h d", b=NS)
            for a_ in range(7):
                c0_ = a_ * 512 if a_ < 4 else MLW + (a_ - 4) * 512
                dma("pool", s1v[:, a_, :, :], spj[:, c0_:c0_ + 512].rearrange("p (h d) -> p h d", h=8), ["spj"], ["scr1"], "scrw1")
            A7 = TT("A7", [128, 8, 64])
            dma("sp", A7[:, 0:7, :], scr1[:, 0:7, :], ["scr1"], ["A7"], "scrr1")
            gif = TT("gif", [128, 2])
            s_if = nc.dram_tensor("scr_if", [2, 128], F32, kind="Internal").ap()
            for g_ in range(2):
                dma("pool", s_if[g_, :].rearrange("(b h) -> b h", b=NS), spj[:, 2048 + 8 * g_:2056 + 8 * g_], ["spj"], ["scr_if"], "scrwif")
            for g_ in range(2):
                dma("sp", gif[:, g_:g_ + 1], s_if[g_, :].rearrange("(p o) -> p o", o=1), ["scr_if"], ["gif"], "scrrif")
            SP_ = lambda n: spk[:, SOFF[n][0]:SOFF[n][1]]
            pl = spj[:, MLW + 1536:MLW + 1792]
            dma("pool", osshl, pl, ["spj"], [], "fin")
            dve(lambda e: e.tensor_tensor(out=sl_t[:, 2, :], in0=sl_t[:, 0, :], in1=pl, op=ALU.subtract), ["sl_t", "spj"], ["sl_t"])
            dve(lambda e: e.tensor_tensor(out=sl_t[:, 2, :], in0=sl_t[:, 2, :], in1=sl_t[:, 1, :], op=ALU.mult), ["sl_t"], ["sl_t"])
            dve(lambda e: e.tensor_tensor(out=sl_t[:, 2, :], in0=sl_t[:, 2, :], in1=pl, op=ALU.add), ["sl_t", "spj"], ["sl_t"])
            act(lambda e: e.activation(out=lor[:NS, 0:64], in_=sl_t[:, 2, 0:64], func=AF.Tanh), ["sl_t"], ["lor"])
            act(lambda e: e.copy(out=lor[:NS, 64:128], in_=sl_t[:, 2, 64:128]), ["sl_t"], ["lor"])
            act(lambda e: e.activation(out=lor[:NS, 128:256], in_=sl_t[:, 2, 128:256], func=AF.Sigmoid), ["sl_t"], ["lor"])
            ps, psb, pk = nps()
            pe(lambda e, psb=psb: e.transpose(psb[:, 0:NS], lor[:NS, 0:128], identb[:NS, :NS]), ["lor", "identb"], [pk])
            pe(lambda e, psb=psb: e.transpose(psb[:, 128:128 + NS], lor[:NS, 128:256], identb[:NS, :NS]), ["lor", "identb"], [pk])
            act(lambda e, psb=psb: e.copy(out=lorT[:, :, 0:NS], in_=psb[:, 0:256].rearrange("p (a t) -> p a t", a=2)[:, :, 0:NS]), [pk], ["lorT"])
            ps, psb, pk = nps()
            mm(ps[:NS, 0:512], lorT[0:64, 0, 0:NS], luw[0:64, :], True, True, ["lorT", "luw"], [pk])
            mm(ps[:NS, 512:1024], lorT[64:128, 0, 0:NS], luw[64:128, :], True, True, ["lorT", "luw"], [pk])
            ps2, _, pk2 = nps()
            mm(ps2[:NS, 0:512], lorT[:, 1, 0:NS], gup[:, :], True, True, ["lorT", "gup"], [pk2])
            lo3 = spj[:, 0:1536].rearrange("p (a n) -> p a n", a=3)
            for a_ in range(2):
                act(lambda e, ps=ps, a_=a_: e.copy(out=lo3[:, a_, :], in_=ps[:NS, a_ * 512:(a_ + 1) * 512]), [pk], ["lo3"])
            act(lambda e, ps2=ps2: e.copy(out=lo3[:, 2, :], in_=ps2[:NS, 0:512]), [pk2], ["lo3"])
            for a_ in range(3):
                dma("pool", scr2.rearrange("(b h) a d -> b a h d", b=NS)[:, a_, :, :], lo3[:, a_, :].rearrange("p (h d) -> p h d", h=8), ["lo3"], ["scr2"], "scrw2")
            L3 = TT("L3", [128, 3, 64])
            dma("sp", L3[:], scr2, ["scr2"], ["L3"], "scrr2")
            big = TT("big", [128, 4096])
            sv = TT("sv", [128, 64])
            pool(lambda e: e.tensor_copy(out=scv[:, :, 3, :], in_=A7[:, 0:2, :]), ["A7"], ["scv"])
            dma("pool", osconv, scv[:, :, 1:4, :], ["scv"], [], "fin")
            qk_s = TT("qk_s", [128, 2, 64])
            cwqk = lambda w_: spk[:, SOFF["cwq"][0] + w_ * 256:SOFF["cwq"][0] + (w_ + 1) * 256].rearrange("p (j d) -> p j d", j=4)
            for w_ in range(2):
                dve(lambda e, w_=w_: e.tensor_tensor(out=big[:, 0:256].rearrange("p (j d) -> p j d", j=4), in0=scv[:, w_, :, :], in1=cwqk(w_), op=ALU.mult), ["scv", "spk"], ["big"])
                dve(lambda e, w_=w_: e.tensor_reduce(out=qk_s[:, w_, :], in_=big[:, 0:256].rearrange("p (j d) -> p d j", j=4), axis=AX.X, op=ALU.add), ["big"], ["qk_s"])
            dve(lambda e: e.tensor_tensor(out=qk_s[:], in0=qk_s[:], in1=spk[:, SOFF["cbq"][0]:SOFF["cbk"][1]].rearrange("p (a d) -> p a d", a=2), op=ALU.add), ["qk_s", "spk"], ["qk_s"])
            act(lambda e: e.activation(out=qk_s[:], in_=qk_s[:], func=AF.Silu), ["qk_s"], ["qk_s"])
            act(lambda e: e.activation(out=qk_s[:, 1, :], in_=qk_s[:, 1, :], func=AF.Copy, scale=0.125), ["qk_s"], ["qk_s"])
            dve(lambda e: e.tensor_tensor(out=sv[:, 0:2], in0=gif[:], in1=spk[:, SOFF["ib"][0]:SOFF["fb"][1]], op=ALU.add), ["gif", "spk"], ["sv"])
            act(lambda e: e.activation(out=sv[:, 9:10], in_=sv[:, 1:2], func=AF.Exp, scale=-1.0), ["sv"], ["sv"])
            act(lambda e: e.activation(out=sv[:, 2:3], in_=sv[:, 9:10], func=AF.Ln, bias=1.0, scale=1.0), ["sv"], ["sv"])
            dve(lambda e: e.tensor_tensor(out=sv[:, 3:4], in0=sm_t[:], in1=sv[:, 2:3], op=ALU.subtract), ["sv", "sm_t"], ["sv"])
            dve(lambda e: e.tensor_tensor(out=sv[:, 4:5], in0=sv[:, 3:4], in1=sv[:, 0:1], op=ALU.max), ["sv"], ["sv"])
            dma("pool", osm, sv[:, 4:5], ["sv"], [], "fin")
            dve(lambda e: e.tensor_tensor(out=sv[:, 9:10], in0=sv[:, 0:1], in1=sv[:, 4:5], op=ALU.subtract), ["sv"], ["sv"])
            act(lambda e: e.activation(out=sv[:, 5:6], in_=sv[:, 9:10], func=AF.Exp), ["sv"], ["sv"])
            dve(lambda e: e.tensor_tensor(out=sv[:, 9:10], in0=sv[:, 3:4], in1=sv[:, 4:5], op=ALU.subtract), ["sv"], ["sv"])
            act(lambda e: e.activation(out=sv[:, 6:7], in_=sv[:, 9:10], func=AF.Exp), ["sv"], ["sv"])
            act(lambda e: e.activation(out=sv[:, 7:8], in_=sv[:, 4:5], func=AF.Exp, scale=-1.0), ["sv"], ["sv"])
            q_ = qk_s[:, 0, :]
            k_ = qk_s[:, 1, :]
            v_ = A7[:, 2, :]
            b3 = lambda t: t[:, :].rearrange("p (a c) -> p a c", a=64)
            pool(lambda e: e.tensor_tensor(out=b3(big), in0=k_.unsqueeze(2).to_broadcast([128, 64, 64]), in1=v_.unsqueeze(1).to_broadcast([128, 64, 64]), op=ALU.mult), ["qk_s", "A7"], ["big"])
            dve(lambda e: e.tensor_scalar_mul(out=Cs[:], in0=Cs[:], scalar1=sv[:, 6:7]), ["Cs", "sv"], ["Cs"])
            dve(lambda e: e.scalar_tensor_tensor(out=Cs[:], in0=big[:], scalar=sv[:, 5:6], in1=Cs[:], op0=ALU.mult, op1=ALU.add), ["big", "sv", "Cs"], ["Cs"])
            dma("pool", osC, Cs[:], ["Cs"], [], "fin")
            dve(lambda e: e.tensor_scalar_mul(out=sn_t[:], in0=sn_t[:], scalar1=sv[:, 6:7]), ["sn_t", "sv"], ["sn_t"])
            dve(lambda e: e.scalar_tensor_tensor(out=sn_t[:], in0=k_, scalar=sv[:, 5:6], in1=sn_t[:], op0=ALU.mult, op1=ALU.add), ["qk_s", "sv", "sn_t"], ["sn_t"])
            dma("pool", osn, sn_t[:], ["sn_t"], [], "fin")
            pool(lambda e: e.tensor_tensor(out=b3(big), in0=Cs[:, :].rearrange("p (k v) -> p v k", k=64), in1=q_.unsqueeze(1).to_broadcast([128, 64, 64]), op=ALU.mult), ["Cs", "qk_s"], ["big"])
            hs = TT("hs", [128, 2, 64])
            dve(lambda e: e.tensor_reduce(out=hs[:, 0, :], in_=b3(big), axis=AX.X, op=ALU.add), ["big"], ["hs"])
            dve(lambda e: e.tensor_tensor(out=sv[:, 16:80 - 16] if False else big[:, 0:64], in0=q_, in1=sn_t[:], op=ALU.mult), ["qk_s", "sn_t"], ["big"])
            dve(lambda e: e.tensor_reduce(out=sv[:, 8:9], in_=big[:, 0:64], axis=AX.X, op=ALU.add), ["big"], ["sv"])
            dve(lambda e: e.scalar_tensor_tensor(out=sv[:, 9:10], in0=sv[:, 8:9], scalar=-1.0, in1=sv[:, 8:9], op0=ALU.mult, op1=ALU.max), ["sv"], ["sv"])
            dve(lambda e: e.tensor_tensor(out=sv[:, 9:10], in0=sv[:, 9:10], in1=sv[:, 7:8], op=ALU.max), ["sv"], ["sv"])
            dve(lambda e: e.reciprocal(out=sv[:, 10:11], in_=sv[:, 9:10]), ["sv"], ["sv"])
            dve(lambda e: e.tensor_scalar_mul(out=hs[:, 0, :], in0=hs[:, 0, :], scalar1=sv[:, 10:11]), ["hs", "sv"], ["hs"])
            dma("pool", osshift, A7[:, 4:7, :], ["A7"], [], "fin")
            rk3 = TT("rk3", [128, 3, 64])
            mu3 = spk[:, SOFF["mu_r"][0]:SOFF["mu_v"][1]].rearrange("p (a d) -> p a d", a=3)
            dve(lambda e: e.tensor_tensor(out=rk3[:], in0=ssh[:], in1=A7[:, 4:7, :], op=ALU.subtract), ["ssh", "A7"], ["rk3"])
            dve(lambda e: e.tensor_tensor(out=rk3[:], in0=rk3[:], in1=mu3, op=ALU.mult), ["rk3", "spk"], ["rk3"])
            dve(lambda e: e.tensor_tensor(out=rk3[:], in0=rk3[:], in1=A7[:, 4:7, :], op=ALU.add), ["rk3", "A7"], ["rk3"])
            w8 = TT("w8", [128, 8, 64])
            dve(lambda e: e.tensor_tensor(out=w8[:, 0, :], in0=L3[:, 0, :], in1=SP_("w0"), op=ALU.add), ["L3", "spk"], ["w8"])
            act(lambda e: e.activation(out=w8[:, 0, :], in_=w8[:, 0, :], func=AF.Sigmoid), ["w8"], ["w8"])
            act(lambda e: e.activation(out=w8[:, 0, :], in_=w8[:, 0, :], func=AF.Exp, scale=-C0), ["w8"], ["w8"])
            dve(lambda e: e.tensor_tensor(out=w8[:, 1, :], in0=L3[:, 1, :], in1=SP_("a0"), op=ALU.add), ["L3", "spk"], ["w8"])
            act(lambda e: e.activation(out=w8[:, 1, :], in_=w8[:, 1, :], func=AF.Sigmoid), ["w8"], ["w8"])
            dve(lambda e: e.tensor_tensor(out=w8[:, 6, :], in0=rk3[:, 1, :], in1=SP_("kk"), op=ALU.mult), ["rk3", "spk"], ["w8"])
            dve(lambda e: e.tensor_tensor(out=w8[:, 7, :], in0=w8[:, 6, :], in1=w8[:, 6, :], op=ALU.mult), ["w8"], ["w8"])
            dve(lambda e: e.tensor_reduce(out=sv[:, 11:12], in_=w8[:, 7, :], axis=AX.X, op=ALU.add), ["w8"], ["sv"])
            dve(lambda e: e.tensor_scalar_max(out=sv[:, 11:12], in0=sv[:, 11:12], scalar1=1e-24), ["sv"], ["sv"])
            act(lambda e: e.activation(out=sv[:, 11:12], in_=sv[:, 11:12], func=AF.Sqrt), ["sv"], ["sv"])
            dve(lambda e: e.reciprocal(out=sv[:, 11:12], in_=sv[:, 11:12]), ["sv"], ["sv"])
            dve(lambda e: e.tensor_scalar_mul(out=w8[:, 3, :], in0=w8[:, 6, :], scalar1=sv[:, 11:12]), ["w8", "sv"], ["w8"])
            dve(lambda e: e.tensor_tensor(out=w8[:, 6, :], in0=w8[:, 1, :], in1=SP_("ka"), op=ALU.mult), ["w8", "spk"], ["w8"])
            dve(lambda e: e.tensor_tensor(out=w8[:, 6, :], in0=w8[:, 6, :], in1=SP_("ka"), op=ALU.subtract), ["w8", "spk"], ["w8"])
            dve(lambda e: e.tensor_scalar_add(out=w8[:, 6, :], in0=w8[:, 6, :], scalar1=1.0), ["w8"], ["w8"])
            dve(lambda e: e.tensor_tensor(out=w8[:, 4, :], in0=rk3[:, 1, :], in1=w8[:, 6, :], op=ALU.mult), ["rk3", "w8"], ["w8"])
            dve(lambda e: e.tensor_tensor(out=w8[:, 5, :], in0=w8[:, 1, :], in1=w8[:, 3, :], op=ALU.mult), ["w8"], ["w8"])
            dma("sp", Ss[:], sS_d, [], ["Ss"], "sin2")
            bk = lambda ap: ap.unsqueeze(1).to_broadcast([128, 64, 64])
            bv = lambda ap: ap.unsqueeze(2).to_broadcast([128, 64, 64])
            pool(lambda e: e.tensor_tensor(out=b3(big), in0=b3(Ss), in1=bk(w8[:, 3, :]), op=ALU.mult), ["Ss", "w8"], ["big"])
            dve(lambda e: e.tensor_reduce(out=w8[:, 7, :], in_=b3(big), axis=AX.X, op=ALU.add), ["big"], ["w8"])
            dve(lambda e: e.tensor_tensor(out=b3(Ss), in0=b3(Ss), in1=bk(w8[:, 0, :]), op=ALU.mult), ["Ss", "w8"], ["Ss"])
            pool(lambda e: e.tensor_tensor(out=b3(big), in0=bv(w8[:, 7, :]), in1=bk(w8[:, 5, :]), op=ALU.mult), ["w8"], ["big"])
            dve(lambda e: e.tensor_tensor(out=Ss[:], in0=Ss[:], in1=big[:], op=ALU.subtract), ["Ss", "big"], ["Ss"])
            pool(lambda e: e.tensor_tensor(out=b3(big), in0=bv(rk3[:, 2, :]), in1=bk(w8[:, 4, :]), op=ALU.mult), ["rk3", "w8"], ["big"])
            dve(lambda e: e.tensor_tensor(out=Ss[:], in0=Ss[:], in1=big[:], op=ALU.add), ["Ss", "big"], ["Ss"])
            dma("pool", osS, Ss[:], ["Ss"], [], "fin")
            pool(lambda e: e.tensor_tensor(out=b3(big), in0=b3(Ss), in1=bk(rk3[:, 0, :]), op=ALU.mult), ["Ss", "rk3"], ["big"])
            dve(lambda e: e.tensor_reduce(out=hs[:, 1, :], in_=b3(big), axis=AX.X, op=ALU.add), ["big"], ["hs"])
            dve(lambda e: e.tensor_tensor(out=w8[:, 6, :], in0=rk3[:, 0, :], in1=w8[:, 4, :], op=ALU.mult), ["rk3", "w8"], ["w8"])
            dve(lambda e: e.tensor_tensor(out=w8[:, 6, :], in0=w8[:, 6, :], in1=SP_("rk"), op=ALU.mult), ["w8", "spk"], ["w8"])
            dve(lambda e: e.tensor_reduce(out=sv[:, 12:13], in_=w8[:, 6, :], axis=AX.X, op=ALU.add), ["w8"], ["sv"])
            dve(lambda e: e.scalar_tensor_tensor(out=hs[:, 1, :], in0=rk3[:, 2, :], scalar=sv[:, 12:13], in1=hs[:, 1, :], op0=ALU.mult, op1=ALU.add), ["rk3", "sv", "hs"], ["hs"])
            act(lambda e: e.activation(out=w8[:, 6, :], in_=A7[:, 3, :], func=AF.Sigmoid), ["A7"], ["w8"])
            dve(lambda e: e.tensor_tensor(out=hs[:, 0, :], in0=hs[:, 0, :], in1=w8[:, 6, :], op=ALU.mult), ["hs", "w8"], ["hs"])
            dve(lambda e: e.tensor_tensor(out=w8[:, 7, :], in0=hs[:, 0, :], in1=hs[:, 0, :], op=ALU.mult), ["hs"], ["w8"])
            dve(lambda e: e.tensor_reduce(out=sv[:, 13:14], in_=w8[:, 7, :], axis=AX.X, op=ALU.add), ["w8"], ["sv"])
            act(lambda e: e.activation(out=sv[:, 13:14], in_=sv[:, 13:14], func=AF.Sqrt, bias=EPS, scale=1.0 / 64), ["sv"], ["sv"])
            dve(lambda e: e.reciprocal(out=sv[:, 13:14], in_=sv[:, 13:14]), ["sv"], ["sv"])
            dve(lambda e: e.scalar_tensor_tensor(out=hs[:, 0, :], in0=hs[:, 0, :], scalar=sv[:, 13:14], in1=SP_("mnw"), op0=ALU.mult, op1=ALU.mult), ["hs", "sv", "spk"], ["hs"])
            dve(lambda e: e.tensor_reduce(out=sv[:, 14:15], in_=hs[:, 1, :], axis=AX.X, op=ALU.add), ["hs"], ["sv"])
            dve(lambda e: e.tensor_scalar_mul(out=sv[:, 14:15], in0=sv[:, 14:15], scalar1=1.0 / 64), ["sv"], ["sv"])
            dve(lambda e: e.tensor_scalar_sub(out=hs[:, 1, :], in0=hs[:, 1, :], scalar1=sv[:, 14:15]), ["hs", "sv"], ["hs"])
            dve(lambda e: e.tensor_tensor(out=w8[:, 7, :], in0=hs[:, 1, :], in1=hs[:, 1, :], op=ALU.mult), ["hs"], ["w8"])
            dve(lambda e: e.tensor_reduce(out=sv[:, 15:16], in_=w8[:, 7, :], axis=AX.X, op=ALU.add), ["w8"], ["sv"])
            act(lambda e: e.activation(out=sv[:, 15:16], in_=sv[:, 15:16], func=AF.Sqrt, bias=GN_EPS, scale=1.0 / 64), ["sv"], ["sv"])
            dve(lambda e: e.reciprocal(out=sv[:, 15:16], in_=sv[:, 15:16]), ["sv"], ["sv"])
            dve(lambda e: e.scalar_tensor_tensor(out=hs[:, 1, :], in0=hs[:, 1, :], scalar=sv[:, 15:16], in1=SP_("lnw"), op0=ALU.mult, op1=ALU.mult), ["hs", "sv", "spk"], ["hs"])
            dve(lambda e: e.tensor_tensor(out=hs[:, 1, :], in0=hs[:, 1, :], in1=SP_("lnb"), op=ALU.add), ["hs", "spk"], ["hs"])
            dve(lambda e: e.tensor_tensor(out=hs[:, 1, :], in0=hs[:, 1, :], in1=L3[:, 2, :], op=ALU.mult), ["hs", "L3"], ["hs"])
            s3v = nc.dram_tensor("scr3b", [128, 2, 64], F32, kind="Internal").ap()
            dma("pool", s3v, hs[:], ["hs"], ["scr3b"], "scrw3")
            smix = spj[:, 2304:3328].rearrange("p (a h d) -> p a h d", a=2, h=8)
            for a_ in range(2):
                dma("sp", smix[:, a_, :, :], s3v.rearrange("(b h) a d -> b a h d", b=NS)[:, a_, :, :], ["scr3b"], ["smix"], "scrr3")
            act(lambda e: e.copy(out=mix[:NS, :], in_=smix[:].rearrange("p a h d -> p (a h d)")), ["smix"], ["mix"])
            out_proj(NS, mix, "mix", sx, "sx", T, W)
        PP[0].finalize()
        PP[0] = Prog(ctx)
    es_res.close()

    with contextlib.ExitStack() as es2:
        cur[0] = es2
        upb = TT("upb", [128, 8, DFF], BF16)
        dnb = TT("dnb", [128, 32, D], BF16)
        pB2 = TT("pB2", [128, 136], F32)
        nfw = TT("nfw", [128, D])
        identb2 = TT("identb2", [128, 128], BF16)
        up_v = mlp_up.rearrange("(c p) n -> p c n", p=128)
        dn_v = mlp_down.rearrange("(c p) n -> p c n", p=128)
        dma("sp", pB2[:, 0:128], packA_d[:, OFF["ident"][0]:OFF["ident"][1]], [], ["pB2"], "init2", True)
        dma("sp", pB2[:, 128:136], packA_d[:, OFF["nmlp"][0]:OFF["nmlp"][1]], [], ["pB2"], "init2", True)
        dma("sp", nfw[:], nfw_d, [], ["nfw"], "init2", True)
        for g8 in range(8):
            dma("pool", upb[:, :, g8 * 512:(g8 + 1) * 512], up_v[:, :, g8 * 512:(g8 + 1) * 512], [], ["upb%d" % g8], "up%d" % g8)
        for g8 in range(8):
            dma("pool", dnb[:, g8 * 4:(g8 + 1) * 4, :], dn_v[:, g8 * 4:(g8 + 1) * 4, :], [], ["dnb%d" % g8], "dn%d" % g8)
        dve(lambda e: e.tensor_copy(out=identb2[:], in_=pB2[:, 0:128]), ["pB2"], ["identb2"])
        x4 = TT("x4", [128, 4, D])
        junk2 = TT("junk2", [128, D])
        st2 = TT("st2", [128, 8])
        xsb2 = TT("xsb2", [128, D], BF16)
        xn2T = TT("xn2T", [128, 8, 512], BF16)
        hT = TT("hT", [128, 32, 512], BF16)
        rl = [TT("rl%d" % i, [128, 512]) for i in range(2)]
        nmlp = pB2[:, 128:136]
        xm_v = xmid_d[0:T, :].rearrange("(s p) d -> p s d", p=128)
        yp_v = yp.rearrange("(s p) d -> p s d", p=128)
        nsb_all = [(sb * 4, 4, 128) for sb in range(4)] + [(16, 1, NS)]
        for (s0, nsub, nt) in nsb_all:
            ntt = nsub * nt if nsub == 4 else nt
            if nsub == 4:
                dma("sp", x4[:], xm_v[:, s0:s0 + 4, :], ["xmid"], ["x4"], "x4")
            else:
                dma("sp", x4[:nt, 0, :], xmid_d[T:T + nt, :], ["xmid"], ["x4"], "x4")
            for si in range(nsub):
                act(lambda e, si=si, nt=nt: e.activation(out=junk2[:nt, :], in_=x4[:nt, si, :], func=AF.Square, accum_out=st2[:nt, 0:1]), ["x4"], ["junk2", "st2"])
                act(lambda e, nt=nt: e.activation(out=st2[:nt, 1:2], in_=st2[:nt, 0:1], func=AF.Sqrt, bias=EPS, scale=1.0 / D), ["st2"], ["st2"])
                dve(lambda e, nt=nt: e.reciprocal(out=st2[:nt, 2:3], in_=st2[:nt, 1:2]), ["st2"], ["st2"])
                dve(lambda e, si=si, nt=nt: e.tensor_scalar_mul(out=xsb2[:nt, :], in0=x4[:nt, si, :], scalar1=st2[:nt, 2:3]), ["x4", "st2"], ["xsb2"])
                ps, psb, pk = nps()
                for c in range(8):
                    pe(lambda e, c=c, psb=psb, nt=nt: e.transpose(psb[:, c * 128:c * 128 + nt], xsb2[:nt, c * 128:(c + 1) * 128], identb2[:nt, :nt]), ["xsb2", "identb2"], [pk])
                dve(lambda e, psb=psb, nt=nt, si=si: e.tensor_tensor(out=xn2T[:, :, si * 128:si * 128 + nt], in0=psb[:, 0:1024].rearrange("p (c t) -> p c t", c=8)[:, :, 0:nt],
                                                                   in1=nmlp.unsqueeze(2).to_broadcast([128, 8, nt]), op=ALU.mult), [pk, "pB2"], ["xn2T"])
            for j2 in range(16):
                ps, psb, pk = nps()
                for jj in range(2):
                    j = j2 * 2 + jj
                    for c in range(8):
                        mm(ps[:, jj * 512:jj * 512 + ntt], upb[:, c, j * 128:(j + 1) * 128], xn2T[:, c, 0:ntt], c == 0, c == 7, ["upb%d" % (j // 4), "xn2T"], [pk])
                for jj in range(2):
                    j = j2 * 2 + jj
                    r_ = rl[jj]
                    rk_ = "rl%d" % jj
                    act(lambda e, ps=ps, r_=r_, jj=jj, ntt=ntt: e.activation(out=r_[:, 0:ntt], in_=ps[:, jj * 512:jj * 512 + ntt], func=AF.Relu), [pk], [rk_])
                    if jj == 0:
                        dve(lambda e, r_=r_, j=j, ntt=ntt: e.tensor_tensor(out=hT[:, j, 0:ntt], in0=r_[:, 0:ntt], in1=r_[:, 0:ntt], op=ALU.mult), [rk_], ["hT"])
                    else:
                        pool(lambda e, r_=r_, j=j, ntt=ntt: e.tensor_tensor(out=hT[:, j, 0:ntt], in0=r_[:, 0:ntt], in1=r_[:, 0:ntt], op=ALU.mult), [rk_], ["hT"])
            for si in range(nsub):
                ps, psb, pk = nps()
                for n in range(2):
                    for j in range(32):
                        mm(ps[:nt, n * 512:(n + 1) * 512], hT[:, j, si * 128:si * 128 + nt], dnb[:, j, n * 512:(n + 1) * 512], j == 0, j == 31, ["hT", "dnb%d" % (j // 4)], [pk])
                for a_ in range(2):
                    dve(lambda e, ps=ps, si=si, nt=nt, a_=a_: e.tensor_tensor(out=x4[:nt, si, a_ * 512:(a_ + 1) * 512], in0=ps[:nt, a_ * 512:(a_ + 1) * 512], in1=x4[:nt, si, a_ * 512:(a_ + 1) * 512], op=ALU.add), [pk, "x4"], ["x4"])
                act(lambda e, si=si, nt=nt: e.activation(out=junk2[:nt, :], in_=x4[:nt, si, :], func=AF.Square, accum_out=st2[:nt, 4:5]), ["x4"], ["junk2", "st2"])
                act(lambda e, nt=nt: e.activation(out=st2[:nt, 5:6], in_=st2[:nt, 4:5], func=AF.Sqrt, bias=EPS, scale=1.0 / D), ["st2"], ["st2"])
                dve(lambda e, nt=nt: e.reciprocal(out=st2[:nt, 6:7], in_=st2[:nt, 5:6]), ["st2"], ["st2"])
                dve(lambda e, si=si, nt=nt: e.scalar_tensor_tensor(out=x4[:nt, si, :], in0=x4[:nt, si, :], scalar=st2[:nt, 6:7], in1=nfw[:nt, :], op0=ALU.mult, op1=ALU.mult), ["x4", "st2", "nfw"], ["x4"])
            if nsub == 4:
                dma("pool", yp_v[:, s0:s0 + 4, :], x4[:], ["x4"], [], "yo")
            else:
                dma("pool", ys, x4[:nt, 0, :], ["x4"], [], "yo")
        PP[0].finalize()
    es_ps.close()
    ctx.close()
    return nc


_CACHE = {}


def _host_packs(inp, core):
    f = np.float32
    L = 0
    pa = np.zeros((128, NA), f)

    def put(n, arr):
        a, b = OFF[n]
        pa[:, a:b] = arr

    rep = lambda v: np.broadcast_to(np.asarray(v, f).reshape(1, -1), (128, np.asarray(v).size))
    put("mnw", rep(inp["mlstm_norm_w"][L]))
    put("w0", rep(inp["rw_w0"][L]))
    put("a0", rep(inp["rw_a0"][L]))
    put("kk", rep(inp["rw_k_k"][L]))
    put("ka", rep(inp["rw_k_a"][L]))
    put("rk", rep(inp["rw_r_k"][L].reshape(-1)))
    put("lnw", rep(inp["rw_ln_w"][L]))
    put("lnb", rep(inp["rw_ln_b"][L]))
    put("ifb", rep(np.concatenate([inp["mlstm_i_b"][L], inp["mlstm_f_b"][L]])))
    put("nmw", inp["norm_mix_w"][L].reshape(8, 128).T)
    put("nmlp", inp["norm_mlp_w"][L].reshape(8, 128).T)
    cw = inp["mlstm_conv_w"][L]
    put("cw", cw.reshape(4, 8, 128).transpose(2, 1, 0).reshape(128, 32))
    put("cb", inp["mlstm_conv_b"][L].reshape(8, 128).T)
    put("ident", np.eye(128, dtype=f))
    put("mui", np.triu(np.ones((128, 128), f), 0))
    put("mus", np.triu(np.ones((128, 128), f), 1))
    put("mls", np.tril(np.ones((128, 128), f), -1))
    put("ones", np.ones((128, 128), f))
    lsel = np.zeros((128, 128), f)
    rsel = np.zeros((128, 4), f)
    for h in range(8):
        lsel[h, (h % 2) * 64:(h % 2) * 64 + 64] = 1.0
        rsel[h, h // 2] = 1.0
    put("lsel", lsel)
    put("rsel", rsel)
    return pa


def _sample_pack(inp):
    f = np.float32
    L = 0
    sp = np.zeros((128, NSP), f)

    def bh(v512):
        return np.tile(np.asarray(v512, f).reshape(8, 64), (NS, 1))

    def put(n, arr):
        a, b = SOFF[n]
        sp[:, a:b] = arr

    mu = inp["rw_mu"][L]
    put("mu_r", bh(mu[0:512]))
    put("mu_k", bh(mu[512:1024]))
    put("mu_v", bh(mu[1024:1536]))
    cw = inp["mlstm_conv_w"][L]
    put("cwq", np.concatenate([bh(cw[j, 0:512]) for j in range(4)], axis=1))
    put("cwk", np.concatenate([bh(cw[j, 512:1024]) for j in range(4)], axis=1))
    cb = inp["mlstm_conv_b"][L]
    put("cbq", bh(cb[0:512]))
    put("cbk", bh(cb[512:1024]))
    put("mnw", bh(inp["mlstm_norm_w"][L]))
    put("w0", bh(inp["rw_w0"][L]))
    put("a0", bh(inp["rw_a0"][L]))
    put("kk", bh(inp["rw_k_k"][L]))
    put("ka", bh(inp["rw_k_a"][L]))
    put("rk", bh(inp["rw_r_k"][L].reshape(-1)))
    put("lnw", bh(inp["rw_ln_w"][L]))
    put("lnb", bh(inp["rw_ln_b"][L]))
    put("ib", np.tile(inp["mlstm_i_b"][L].reshape(8, 1), (NS, 1)))
    put("fb", np.tile(inp["mlstm_f_b"][L].reshape(8, 1), (NS, 1)))
    return sp


def kernel(**inp):
    f = np.float32
    inp = {k: np.asarray(v) for k, v in inp.items()}
    if "nc" not in _CACHE:
        _CACHE["nc"] = build_program()
    nc = _CACHE["nc"]
    L = 0
    pa = _host_packs(inp, 0)
    sp = _sample_pack(inp)
    mu = inp["rw_mu"][L]
    luw = np.concatenate([inp["rw_w_up"][L], inp["rw_a_up"][L]], axis=0).astype(f)
    common = {
        "w_in": np.ascontiguousarray(inp["w_in"][L], f),
        "w_out": np.ascontiguousarray(inp["w_out"][L], f),
        "mlp_up": np.ascontiguousarray(inp["mlp_up"][L], f),
        "mlp_down": np.ascontiguousarray(inp["mlp_down"][L], f),
        "packA": pa,
        "mu_b": np.ascontiguousarray(np.broadcast_to(mu.reshape(1, -1), (128, RWW)), f),
        "nfw_b": np.ascontiguousarray(np.broadcast_to(inp["norm_f_w"].reshape(1, -1), (128, D)), f),
        "luw": luw,
        "gup": np.ascontiguousarray(inp["rw_g_up"][L], f),
        "spack": sp,
        "mul": np.ascontiguousarray(np.broadcast_to(mu[1536:1792].reshape(1, -1), (NS, 256)), f),
    }
    in_maps = []
    for c in range(8):
        rs = slice(c * NS, (c + 1) * NS)
        m = dict(common)
        m["xp"] = np.ascontiguousarray(inp["x_prompt"][c], f)
        m["xs"] = np.ascontiguousarray(inp["x_sample"][rs, 0, :], f)
        m["sC"] = np.ascontiguousarray(inp["state_mlstm_C"][L, rs].reshape(128, 4096), f)
        m["sn"] = np.ascontiguousarray(inp["state_mlstm_n"][L, rs].reshape(128, 64), f)
        m["sm"] = np.ascontiguousarray(inp["state_mlstm_m"][L, rs].reshape(128, 1), f)
        cv = inp["state_mlstm_conv"][L, rs]
        m["sconv"] = np.ascontiguousarray(cv.reshape(NS, 3, 2, 8, 64).transpose(0, 3, 2, 1, 4).reshape(128, 2, 3, 64), f)
        m["sS"] = np.ascontiguousarray(inp["state_rwkv_S"][L, rs].reshape(128, 4096), f)
        sh = inp["state_rwkv_shift"][L, rs, 0, :]
        m["sshift"] = np.ascontiguousarray(sh[:, 0:1536].reshape(NS, 3, 8, 64).transpose(0, 2, 1, 3).reshape(128, 3, 64), f)
        m["sshl"] = np.ascontiguousarray(sh[:, 1536:1792], f)
        in_maps.append(m)
    res = run_bass_kernel_spmd(nc, in_maps, core_ids=list(range(8)))
    R = res.results
    y_prompt = np.stack([R[c]["yp"] for c in range(8)]).astype(f)
    y_sample = np.concatenate([R[c]["ys"] for c in range(8)], axis=0).reshape(128, 1, D).astype(f)
    pC = np.zeros((1, 8, 8, 64, 64), f)
    pn = np.zeros((1, 8, 8, 64), f)
    pm = np.zeros((1, 8, 8), f)
    pconv = np.zeros((1, 8, 3, 1024), f)
    pS = np.zeros((1, 8, 8, 64, 64), f)
    pshift = np.zeros((1, 8, 1, RWW), f)
    for c in range(8):
        oC = R[c]["oC"].reshape(2, 64, 4, 65)
        Ch = oC.transpose(2, 0, 1, 3).reshape(8, 64, 65)
        pC[0, c] = Ch[:, :, 0:64]
        pn[0, c] = Ch[:, :, 64]
        pm[0, c] = R[c]["om"].reshape(8)
        pconv[0, c] = R[c]["oconv"].transpose(2, 1, 0).reshape(3, 1024)
        oS = R[c]["oS"].reshape(2, 64, 4, 64)
        pS[0, c] = oS.transpose(2, 0, 3, 1).reshape(8, 64, 64)
        pshift[0, c, 0] = R[c]["oshift"].reshape(RWW)
    sC = np.concatenate([R[c]["osC"].reshape(NS, 8, 64, 64) for c in range(8)])[None].astype(f)
    sn = np.concatenate([R[c]["osn"].reshape(NS, 8, 64) for c in range(8)])[None].astype(f)
    sm = np.concatenate([R[c]["osm"].reshape(NS, 8) for c in range(8)])[None].astype(f)
    sconv = np.concatenate([R[c]["osconv"].reshape(NS, 8, 2, 3, 64).transpose(0, 3, 2, 1, 4).reshape(NS, 3, 1024) for c in range(8)])[None].astype(f)
    sS = np.concatenate([R[c]["osS"].reshape(NS, 8, 64, 64) for c in range(8)])[None].astype(f)
    sshift = np.concatenate([
        np.concatenate([R[c]["osshift"].reshape(NS, 8, 3, 64).transpose(0, 2, 1, 3).reshape(NS, 1536), R[c]["osshl"]], axis=1)
        for c in range(8)]).reshape(1, 128, 1, RWW).astype(f)
    return (y_prompt, y_sample, pC, pn, pm, pconv, pS, pshift, sC, sn, sm, sconv, sS, sshift)
```

```python
import contextlib
import numpy as np
import concourse.bass as bass
import concourse.mybir as mybir
from concourse.bass_utils import run_bass_kernel_spmd

F32 = mybir.dt.float32
BF16 = mybir.dt.bfloat16
AF = mybir.ActivationFunctionType
ALU = mybir.AluOpType
AX = mybir.AxisListType

D = 1024
T = 2048
NB = 16
NS = 16
INW = 3856
MLW = 2064
RWW = 1792
DFF = 4096
EPS = 1e-6
GN_EPS = 64e-5
C0 = 0.6065306597126334

OFF = {}
_o = 0
for _n, _w in [("mnw", 512), ("w0", 512), ("a0", 512), ("kk", 512), ("ka", 512), ("rk", 512),
               ("lnw", 512), ("lnb", 512), ("ifb", 16), ("nmw", 8), ("nmlp", 8), ("cw", 32), ("cb", 8),
               ("ident", 128), ("mui", 128), ("mus", 128), ("mls", 128), ("ones", 128),
               ("lsel", 128), ("rsel", 4)]:
    OFF[_n] = (_o, _o + _w)
    _o += _w
NA = _o
SOFF = {}
_o = 0
for _n, _w in [("mu_r", 64), ("mu_k", 64), ("mu_v", 64), ("cwq", 256), ("cwk", 256), ("cbq", 64), ("cbk", 64),
               ("mnw", 64), ("w0", 64), ("a0", 64), ("kk", 64), ("ka", 64), ("rk", 64), ("lnw", 64), ("lnb", 64),
               ("ib", 1), ("fb", 1)]:
    SOFF[_n] = (_o, _o + _w)
    _o += _w
NSP = _o


ALIAS = {"r_sb": "G0", "kf_sb": "G1", "vf": "G2", "wsig": "G3", "a_sb": "G4", "g_sb": "G5", "kap": "G6", "ktl": "G7",
         "bvec": "G8", "e1": "G9", "e2": "G10", "e3": "G11", "pcw": "G12", "ysb": "G11", "tA": "G9", "tB": "G10", "plast": "G9",
         "osig": "G13", "hml": "G14", "nlrep": "G15", "Fb": "G15", "cacc": "G16", "qks": "G16", "tAm": "G18", "tBm": "G19",
         "junk": "xm", "ctmp": "xm", "xsb": "mix", "mixT": "TMbA", "TMb0": "TMbA", "TMb1": "TMbA",
         "TMb2": "TMbB", "TMb3": "TMbB", "Ub": "Zb1", "Ss": "Cs", "lo3": "spj", "smix": "spj",
         "xt1": "xt0", "xnT1": "xnT0", "qkx1": "qkx0"}


class SemCtx:
    def __init__(self, nc):
        self.nc = nc
        self.es = contextlib.ExitStack()
        self.engs = ["pe", "act", "dve", "pool", "sp"]
        self.esem = {e: self.es.enter_context(nc.semaphore("s_" + e)) for e in self.engs}
        self.ecnt = {e: 0 for e in self.engs}
        self.bsem = self.es.enter_context(nc.semaphore("s_bar"))
        self.phase = 0
        self.gsem = {}
        self.gbase = {}

    def group_sem(self, g):
        if g not in self.gsem:
            self.gsem[g] = self.es.enter_context(self.nc.semaphore("g_%d" % len(self.gsem)))
            self.gbase[g] = 0
        return self.gsem[g]

    def close(self):
        self.es.close()


class Prog:
    max_ops = None

    def __init__(self, ctx):
        self.ctx = ctx
        self.nc = ctx.nc
        self.ops = []
        self.last_writer = {}
        self.readers = {}
        self.dma_groups = {}

    def op(self, eng, fn, reads=(), writes=(), dma_group=None, wait_total=False):
        if self.max_ops is not None and len(self.ops) >= self.max_ops:
            return None
        reads = [ALIAS.get(k, k) for k in reads]
        writes = [ALIAS.get(k, k) for k in writes]
        if eng != "pe":
            writes = writes + [k for k in reads if k.startswith("PS") and k not in writes]
        deps = set()
        for b in reads:
            if b in self.last_writer:
                deps.add(self.last_writer[b])
        for b in writes:
            if b in self.last_writer:
                deps.add(self.last_writer[b])
            for r in self.readers.get(b, ()):
                deps.add(r)
        idx = len(self.ops)
        if dma_group is not None:
            deps = {d for d in deps if self.ops[d]["dma"] != dma_group}
        o = dict(eng=eng, fn=fn, deps=sorted(deps), dma=dma_group, idx=idx)
        if dma_group is not None:
            g = self.dma_groups.setdefault(dma_group, dict(total=0, wait_total=wait_total))
            g["total"] += 1
            o["dma_cnt"] = g["total"]
        self.ops.append(o)
        for b in reads:
            self.readers.setdefault(b, []).append(idx)
        for b in writes:
            self.last_writer[b] = idx
            self.readers[b] = []
        return idx

    def finalize(self):
        nc = self.nc
        ctx = self.ctx
        ops = self.ops
        needed = set()
        for o in ops:
            best = {}
            rd = []
            for d in o["deps"]:
                p = ops[d]
                if p["dma"] is not None:
                    rd.append(d)
                else:
                    if p["eng"] == "pe" and o["eng"] == "pe" and o["dma"] is None:
                        continue
                    best[p["eng"]] = max(best.get(p["eng"], -1), d)
            rd.extend(best.values())
            o["deps"] = sorted(rd)
            for d in best.values():
                needed.add(d)
        engs = ctx.engs
        last = {}
        for o in ops:
            if o["dma"] is None:
                last[o["eng"]] = o["idx"]
        needed |= set(last.values())
        cnt = dict(ctx.ecnt)
        for o in ops:
            if o["dma"] is None and o["idx"] in needed:
                cnt[o["eng"]] += 1
                o["sig"] = cnt[o["eng"]]
        for g in self.dma_groups:
            ctx.group_sem(g)
        phase = ctx.phase
        with nc.Block() as block:

            def emit_engine(ename, eng):
                known = {}
                if phase > 0:
                    eng.wait_ge(ctx.bsem, phase)
                for o in ops:
                    if o["eng"] != ename:
                        continue
                    for d in o["deps"]:
                        p = ops[d]
                        if p["dma"] is not None:
                            g = self.dma_groups[p["dma"]]
                            sem = ctx.gsem[p["dma"]]
                            val = ctx.gbase[p["dma"]] + 16 * (g["total"] if g["wait_total"] else p["dma_cnt"])
                            key = ("g", p["dma"])
                        else:
                            if p["eng"] == "pe" and ename == "pe" and o["dma"] is None:
                                continue
                            sem = ctx.esem[p["eng"]]
                            val = p["sig"]
                            key = ("e", p["eng"])
                        if known.get(key, 0) >= val:
                            continue
                        known[key] = val
                        eng.wait_ge(sem, val)
                    ins = o["fn"](eng)
                    if o["dma"] is not None:
                        ins.then_inc(ctx.gsem[o["dma"]], 16)
                    elif "sig" in o:
                        ins.then_inc(ctx.esem[ename], 1)
                if ename == "sp":
                    for e2 in engs:
                        if cnt[e2] > ctx.ecnt[e2]:
                            eng.wait_ge(ctx.esem[e2], cnt[e2])
                    for g, info in self.dma_groups.items():
                        eng.wait_ge(ctx.gsem[g], ctx.gbase[g] + 16 * info["total"])
                    eng.sem_inc(ctx.bsem, 1)

            @block.tensor
            def _(e):
                emit_engine("pe", e)

            @block.scalar
            def _(e):
                emit_engine("act", e)

            @block.vector
            def _(e):
                emit_engine("dve", e)

            @block.gpsimd
            def _(e):
                emit_engine("pool", e)

            @block.sync
            def _(e):
                emit_engine("sp", e)

        ctx.ecnt = cnt
        for g, info in self.dma_groups.items():
            ctx.gbase[g] += 16 * info["total"]
        ctx.phase += 1


STOP_EARLY = True


class _StopBuild(Exception):
    pass


def build_program(do_sample=True, debug=False):
    nc = bass.Bass("TRN2", target_bir_lowering=False)
    try:
        return _build_program(nc, do_sample, debug)
    except _StopBuild:
        return nc


def _build_program(nc, do_sample, debug):
    dbg = nc.dram_tensor("dbg", [128, 16, 512], F32, kind="ExternalOutput").ap() if debug else None
    din = lambda n, s: nc.dram_tensor(n, s, F32, kind="ExternalInput").ap()
    dout = lambda n, s: nc.dram_tensor(n, s, F32, kind="ExternalOutput").ap()
    xp = din("xp", [T, D])
    xs = din("xs", [NS, D])
    w_in = din("w_in", [D, INW])
    w_out = din("w_out", [D, D])
    mlp_up = din("mlp_up", [D, DFF])
    mlp_down = din("mlp_down", [DFF, D])
    packA_d = din("packA", [128, NA])
    mu_d = din("mu_b", [128, RWW])
    nfw_d = din("nfw_b", [128, D])
    wup_d = din("luw", [128, 512])
    gup_d = din("gup", [128, 512])
    spk_d = din("spack", [128, NSP])
    sC_d = din("sC", [128, 4096])
    sn_d = din("sn", [128, 64])
    sm_d = din("sm", [128, 1])
    sconv_d = din("sconv", [128, 2, 3, 64])
    sS_d = din("sS", [128, 4096])
    sshift_d = din("sshift", [128, 3, 64])
    sshl_d = din("sshl", [NS, 256])
    mul_d = din("mul", [NS, 256])

    yp = dout("yp", [T, D])
    ys = dout("ys", [NS, D])
    oC = dout("oC", [128, 4, 65])
    om = dout("om", [8, 1])
    oconv = dout("oconv", [128, 8, 3])
    oS = dout("oS", [128, 4, 64])
    oshift = dout("oshift", [1, RWW])
    osC = dout("osC", [128, 4096])
    osn = dout("osn", [128, 64])
    osm = dout("osm", [128, 1])
    osconv = dout("osconv", [128, 2, 3, 64])
    osS = dout("osS", [128, 4096])
    osshift = dout("osshift", [128, 3, 64])
    osshl = dout("osshl", [NS, 256])

    xmid_d = nc.dram_tensor("xmid_scr", [T + NS, D], F32, kind="Internal").ap()
    scr1 = nc.dram_tensor("scr1", [128, 8, 64], F32, kind="Internal").ap()
    scr2 = nc.dram_tensor("scr2", [128, 3, 64], F32, kind="Internal").ap()
    scr3 = nc.dram_tensor("scr3", [NS, 2, 8, 64], F32, kind="Internal").ap()

    ctx = SemCtx(nc)
    PP = [Prog(ctx)]
    es_res = contextlib.ExitStack()
    cur = [es_res]

    def TT(name, shape, dt=F32):
        return cur[0].enter_context(nc.sbuf_tensor("t_" + name, list(shape), dt))

    def dma(q, out, in_, reads, writes, group, wait_total=False):
        PP[0].op(q, lambda e: e.dma_start(out=out, in_=in_), reads=reads, writes=writes, dma_group=group, wait_total=wait_total)

    def dve(fn, r, w):
        PP[0].op("dve", fn, reads=r, writes=w)

    def act(fn, r, w):
        PP[0].op("act", fn, reads=r, writes=w)

    def pool(fn, r, w):
        PP[0].op("pool", fn, reads=r, writes=w)

    def pe(fn, r, w):
        PP[0].op("pe", fn, reads=r, writes=w)

    def mm(out, lhsT, rhs, start, stop, r, w):
        pe(lambda e: e.matmul(out, lhsT=lhsT, rhs=rhs, start=start, stop=stop), r, w)

    es_ps = contextlib.ExitStack()
    PS = [es_ps.enter_context(nc.psum_tensor("PS%d" % i, [128, 1024], F32)) for i in range(4)]
    PSB = [p.bitcast(BF16) for p in PS]
    psi = [0]

    def nps():
        i = psi[0] % 4
        psi[0] += 1
        return PS[i], PSB[i], "PS%d" % i

    wq = TT("wq", [128, 8, INW], BF16)
    mub = TT("mub", [128, RWW], F32)
    wout = TT("wout", [128, 8, D], BF16)
    luw = TT("luw", [128, 512], BF16)
    gup = TT("gup", [128, 512], BF16)
    pA = TT("pA", [128, NA], F32)
    identb = TT("identb", [128, 128], BF16)

    def PA(n):
        a, b = OFF[n]
        return pA[:, a:b]

    w_in_v = w_in.rearrange("(c p) n -> p c n", p=128)
    dma("sp", pA[:], packA_d, [], ["pA"], "init", True)
    for c in range(8):
        dma("pool", wq[:, c, :], w_in_v[:, c, :], [], ["wq"], "init", True)
    dma("pool", luw[:], wup_d, [], ["luw"], "init", True)
    dma("pool", gup[:], gup_d, [], ["gup"], "init", True)
    w_out_v = w_out.rearrange("(c p) n -> p c n", p=128)
    for c in range(8):
        dma("pool", wout[:, c, :], w_out_v[:, c, :], [], ["wout"], "init", True)
    dve(lambda e: e.tensor_copy(out=identb[:], in_=PA("ident")), ["pA"], ["identb"])


    def _dbgdump(tag):
        if debug != tag:
            return
        dstg_ = cur[0].enter_context(nc.sbuf_tensor("t_dbgst%d" % tag, [128, 512], F32))
        def dd(slot, ap, key, n):
            dve(lambda e: e.tensor_copy(out=dstg_[:, 0:n], in_=ap), [key], ["dbgst"])
            dma("sp", dbg[:, slot, 0:n], dstg_[:, 0:n], ["dbgst"], [], "dbg")
        dd(0, PA("ident"), "pA", 128)
        dd(1, PA("mui"), "pA", 128)
        dd(2, PA("mnw"), "pA", 512)
        dd(3, PA("w0"), "pA", 512)
        dd(4, PA("lnb"), "pA", 512)
        PP[0].max_ops = len(PP[0].ops)
        PP[0].finalize()
        raise _StopBuild()
    _dbgdump(3)
    dma("sp", mub[:], mu_d, [], ["mub"], "init", True)
    PP[0].finalize()
    PP[0] = Prog(ctx)

    def rmsnorm_T(xt, xk, nt, dstT, dstk, col0, wname, tmpb, tmpbk, junk, junkk, st, stk):
        act(lambda e: e.activation(out=junk[:nt, :], in_=xt[:nt, :], func=AF.Square, accum_out=st[:nt, 0:1]), [xk], [junkk, stk])
        act(lambda e: e.activation(out=st[:nt, 1:2], in_=st[:nt, 0:1], func=AF.Sqrt, bias=EPS, scale=1.0 / D), [stk], [stk])
        dve(lambda e: e.reciprocal(out=st[:nt, 2:3], in_=st[:nt, 1:2]), [stk], [stk])
        dve(lambda e: e.tensor_scalar_mul(out=tmpb[:nt, :], in0=xt[:nt, :], scalar1=st[:nt, 2:3]), [xk, stk], [tmpbk])
        ps, psb, pk = nps()
        for c in range(8):
            pe(lambda e, c=c: e.transpose(psb[:, c * 128:c * 128 + nt], tmpb[:nt, c * 128:(c + 1) * 128], identb[:nt, :nt]), [tmpbk, "identb"], [pk])
        a, b_ = OFF[wname]
        dve(lambda e: e.tensor_tensor(out=dstT[:, :, col0:col0 + nt],
                                      in0=psb[:, 0:1024].rearrange("p (c t) -> p c t", c=8)[:, :, 0:nt],
                                      in1=pA[:, a:b_].unsqueeze(2).to_broadcast([128, 8, nt]), op=ALU.mult), [pk, "pA"], [dstk])

    def head_ml(nt, hsrc, hk, osig, ok, mix, mixk, W, sfx=""):
        tA, tB, s8 = W["tA"], W["tB"], W["s8"]
        h3 = lambda t: t[:nt, :].rearrange("p (h d) -> p h d", h=8)
        bc = lambda t, c: t[:nt, c:c + 8].unsqueeze(2).to_broadcast([nt, 8, 64])
        dve(lambda e: e.tensor_tensor(out=tA[:nt, :], in0=hsrc[:nt, :], in1=osig[:nt, :], op=ALU.mult), [hk, ok], ["tA" + sfx])
        dve(lambda e: e.tensor_tensor(out=tB[:nt, :], in0=tA[:nt, :], in1=tA[:nt, :], op=ALU.mult), ["tA" + sfx], ["tB" + sfx])
        dve(lambda e: e.tensor_reduce(out=s8[:nt, 0:8], in_=h3(tB), axis=AX.X, op=ALU.add), ["tB" + sfx], ["s8" + sfx])
        act(lambda e: e.activation(out=s8[:nt, 8:16], in_=s8[:nt, 0:8], func=AF.Sqrt, bias=EPS, scale=1.0 / 64), ["s8" + sfx], ["s8" + sfx])
        dve(lambda e: e.reciprocal(out=s8[:nt, 16:24], in_=s8[:nt, 8:16]), ["s8" + sfx], ["s8" + sfx])
        dve(lambda e: e.tensor_tensor(out=h3(tB), in0=h3(tA), in1=bc(s8, 16), op=ALU.mult), ["tA" + sfx, "s8" + sfx], ["tB" + sfx])
        dve(lambda e: e.tensor_tensor(out=mix[:nt, 0:512], in0=tB[:nt, :], in1=PA("mnw")[:nt, :], op=ALU.mult), ["tB" + sfx, "pA"], [mixk])

    def head_rw(nt, ysrc, yk, bon, bonk, vf, vfk, g, gk, mix, mixk, W):
        tA, tB, s8 = W["tA"], W["tB"], W["s8"]
        h3 = lambda t: t[:nt, :].rearrange("p (h d) -> p h d", h=8)
        bc = lambda t, c: t[:nt, c:c + 8].unsqueeze(2).to_broadcast([nt, 8, 64])
        dve(lambda e: e.tensor_tensor(out=h3(tA), in0=h3(vf), in1=bc(bon, 0), op=ALU.mult), [vfk, bonk], ["tA"])
        dve(lambda e: e.tensor_tensor(out=tA[:nt, :], in0=tA[:nt, :], in1=ysrc[:nt, :], op=ALU.add), ["tA", yk], ["tA"])
        dve(lambda e: e.tensor_reduce(out=s8[:nt, 24:32], in_=h3(tA), axis=AX.X, op=ALU.add), ["tA"], ["s8"])
        dve(lambda e: e.tensor_scalar_mul(out=s8[:nt, 24:32], in0=s8[:nt, 24:32], scalar1=1.0 / 64), ["s8"], ["s8"])
        dve(lambda e: e.tensor_tensor(out=h3(tA), in0=h3(tA), in1=bc(s8, 24), op=ALU.subtract), ["tA", "s8"], ["tA"])
        dve(lambda e: e.tensor_tensor(out=tB[:nt, :], in0=tA[:nt, :], in1=tA[:nt, :], op=ALU.mult), ["tA"], ["tB"])
        dve(lambda e: e.tensor_reduce(out=s8[:nt, 32:40], in_=h3(tB), axis=AX.X, op=ALU.add), ["tB"], ["s8"])
        act(lambda e: e.activation(out=s8[:nt, 40:48], in_=s8[:nt, 32:40], func=AF.Sqrt, bias=GN_EPS, scale=1.0 / 64), ["s8"], ["s8"])
        dve(lambda e: e.reciprocal(out=s8[:nt, 48:56], in_=s8[:nt, 40:48]), ["s8"], ["s8"])
        dve(lambda e: e.tensor_tensor(out=h3(tB), in0=h3(tA), in1=bc(s8, 48), op=ALU.mult), ["tA", "s8"], ["tB"])
        dve(lambda e: e.tensor_tensor(out=tB[:nt, :], in0=tB[:nt, :], in1=PA("lnw")[:nt, :], op=ALU.mult), ["tB", "pA"], ["tB"])
        dve(lambda e: e.tensor_tensor(out=tB[:nt, :], in0=tB[:nt, :], in1=PA("lnb")[:nt, :], op=ALU.add), ["tB", "pA"], ["tB"])
        dve(lambda e: e.tensor_tensor(out=mix[:nt, 512:1024], in0=tB[:nt, :], in1=g[:nt, :], op=ALU.mult), ["tB", gk], [mixk])

    def out_proj(nt, mix, mixk, xt, xk, row0, W):
        mixT, xm = W["mixT"], W["xm"]
        ps, psb, pk = nps()
        for c in range(8):
            pe(lambda e, c=c: e.transpose(psb[:, c * 128:c * 128 + nt], mix[:nt, c * 128:(c + 1) * 128], identb[:nt, :nt]), [mixk, "identb"], [pk])
        act(lambda e: e.copy(out=mixT[:, :, 0:nt], in_=psb[:, 0:1024].rearrange("p (c t) -> p c t", c=8)[:, :, 0:nt]), [pk], ["mixT"])
        ps, psb, pk = nps()
        for n in range(2):
            for c in range(8):
                mm(ps[:nt, n * 512:(n + 1) * 512], mixT[:, c, 0:nt], wout[:, c, n * 512:(n + 1) * 512], c == 0, c == 7, ["mixT", "wout"], [pk])
        for a_ in range(2):
            dve(lambda e, a_=a_: e.tensor_tensor(out=xm[:nt, a_ * 512:(a_ + 1) * 512], in0=ps[:nt, a_ * 512:(a_ + 1) * 512], in1=xt[:nt, a_ * 512:(a_ + 1) * 512], op=ALU.add), [pk, xk], ["xm"])
        dma("pool", xmid_d[row0:row0 + nt, :], xm[:nt, :], ["xm"], ["xmid%d" % row0], "xm")

    with contextlib.ExitStack() as es1:
        cur[0] = es1
        W = {}
        Gbig = TT("Gbig", [128, 20, 512])
        G = [Gbig[:, i, :] for i in range(20)]
        W["s8"] = TT("s8", [128, 64])
        W["xm"] = TT("xm", [128, D])
        xt = [TT("xt0", [128, D])] * 2
        junk = W["xm"]
        st = TT("st", [128, 4])
        mix = TT("mix", [128, D], BF16)
        xsb = mix
        xnT = [TT("xnT0", [128, 8, 129], BF16)] * 2
        dxT = TT("dxT", [128, 8, 128], BF16)
        qkx = [TT("qkx0", [128, 8, 131])] * 2
        cacc = Gbig[:, 16:18, :].rearrange("p a (c t) -> p (a c) t", t=128)
        ctmp = W["xm"][:, :].rearrange("p (c t) -> p c t", c=8)
        qks = cacc
        qpb = TT("qpb", [128, 4, 128], BF16)
        kTb = TT("kTb", [128, 4, 128], BF16)
        ktm = TT("ktm", [128, 8, 64], BF16)
        vaug = TT("vaug", [128, 8, 65], BF16)
        gt = TT("gt", [128, 96])
        runmax = TT("runmax", [128, 8])
        nBc = TT("nBc", [128, 8])
        Cst = TT("Cst", [128, 4, 65])
        Cbf = TT("Cbf", [128, 4, 65], BF16)
        r_sb, kf_sb, vf = G[0], G[1], G[2]
        Fb = G[15].rearrange("p (j t) -> p j t", j=4)
        osig, hml = G[13], G[14]
        nlrep = G[15].rearrange("p (h d) -> p h d", h=8)
        wsig, a_sb, g_sb, kap, ktl, bvec, e1, e2, e3, pcw, ysb = G[3], G[4], G[5], G[6], G[7], G[8], G[9], G[10], G[11], G[12], G[11]
        W["tA"], W["tB"] = G[9], G[10]
        WM = {"tA": G[18], "tB": G[19], "s8": TT("s8m", [128, 64])}
        r8b = TT("r8b", [128, 8])
        vrw = TT("vrw", [128, 8, 64], BF16)
        lor = TT("lor", [128, 256], BF16)
        lorT = TT("lorT", [128, 2, 128], BF16)
        r8 = TT("r8", [128, 32])
        bon = TT("bon", [128, 8])
        TMb = TT("TMb", [128, 4, 512], BF16)
        W["mixT"] = TMb[:, 0:2, :].rearrange("p a (c t) -> p (a c) t", t=128)
        Btz = TT("Btz", [128, 8, 128], BF16)
        Ktz = TT("Ktz", [128, 8, 128], BF16)
        FMt = TT("FMt", [128, 4, 4, 128], BF16)
        Am = [TT("Am%d" % i, [128, 8, 128], BF16) for i in range(3)]
        PTb = TT("PTb", [128, 8, 128], BF16)
        Pw = [TT("Pw%d" % i, [128, 8, 128], BF16) for i in range(4)]
        Zb = [TT("Zb%d" % i, [128, 8, 64], BF16) for i in range(2)]
        Ub = Zb[1]
        Sst = TT("Sst", [128, 4, 64])
        Sbf = TT("Sbf", [128, 4, 64], BF16)
        WLfm = TT("WLfm", [128, 4])
        plast = G[9]

        pool(lambda e: e.memset(vaug[:], 1.0), [], ["vaug"])
        pool(lambda e: e.memset(Btz[:], 0.0), [], ["Btz"])
        pool(lambda e: e.memset(Ktz[:], 0.0), [], ["Ktz"])
        pool(lambda e: e.memset(Cst[:], 0.0), [], ["Cst"])
        pool(lambda e: e.memset(Cbf[:], 0.0), [], ["Cbf"])
        pool(lambda e: e.memset(Sst[:], 0.0), [], ["Sst"])
        pool(lambda e: e.memset(Sbf[:], 0.0), [], ["Sbf"])
        pool(lambda e: e.memset(runmax[:], -1e30), [], ["runmax"])
        pool(lambda e: e.memset(nBc[:], 0.0), [], ["nBc"])
        pool(lambda e: e.memset(xnT[0][:, :, 0:1], 0.0), [], ["xnT0"])
        pool(lambda e: e.memset(qkx[0][:, :, 0:3], 0.0), [], ["qkx0"])

        if debug == 2:
            dstg = TT("dbgstage", [128, 512]) if False else G[12]
            def ddump0(slot, ap, key, n):
                dve(lambda e: e.tensor_copy(out=dstg[:, 0:n], in_=ap), [key], ["pcw"])
                dma("sp", dbg[:, slot, 0:n], dstg[:, 0:n], ["pcw"], [], "dbg")
            ddump0(0, PA("ident"), "pA", 128)
            ddump0(1, PA("mui"), "pA", 128)
            ddump0(2, luw[:, :], "luw", 512)
            ddump0(3, gup[:, :], "gup", 512)
            ddump0(4, W1[:, 0, 0:512], "W1", 512)
            PP[0].max_ops = len(PP[0].ops)
            if STOP_EARLY:
                PP[0].finalize()
                raise _StopBuild()
        MUI = PA("mui")
        MUS = PA("mus")
        MLS = PA("mls")
        ONES = PA("ones")
        IDF = PA("ident")
        bc8 = lambda ap: ap.unsqueeze(2).to_broadcast([128, 8, 64])
        m8 = lambda m: m.unsqueeze(1).to_broadcast([128, 8, 128])
        v3 = lambda t: t[:].rearrange("p (h d) -> p h d", h=8)
        hoff = lambda h: (h % 2) * 512 + (h // 2) * 128

        for b in range(NB):
            x_ = xt[b % 2]
            xk = "xt%d" % (b % 2)
            xn = xnT[b % 2]
            xnk = "xnT%d" % (b % 2)
            qx = qkx[b % 2]
            qxk = "qkx%d" % (b % 2)
            dma("sp", x_[:], xp[b * 128:(b + 1) * 128, :], [], [xk], xk)
            rmsnorm_T(x_, xk, 128, xn, xnk, 1, "nmw", xsb, "xsb", junk, "junk", st, "st")
            cur_x = xn[:, :, 1:129]
            prv_x = xn[:, :, 0:128]
            dve(lambda e, xn=xn: e.tensor_tensor(out=dxT[:], in0=xn[:, :, 0:128], in1=xn[:, :, 1:129], op=ALU.subtract), [xnk], ["dxT"])

            ps, psb, pk = nps()
            for j in range(8):
                for c in range(8):
                    mm(ps[:, j * 128:(j + 1) * 128], wq[:, c, j * 128:(j + 1) * 128], cur_x[:, c, :], c == 0, c == 7, ["wq", xnk], [pk])
            for a_ in range(2):
                act(lambda e, ps=ps, qx=qx, a_=a_: e.copy(out=qx[:, 4 * a_:4 * a_ + 4, 3:131], in_=ps[:, a_ * 512:(a_ + 1) * 512].rearrange("p (j t) -> p j t", j=4)), [pk], [qxk])
            if b == NB - 1:
                dma("pool", oconv, qx[:, :, 128:131], [qxk], [], "fin")

            def tm_plain(col0, ncol, ps_ap, pk):
                for c in range(8):
                    mm(ps_ap, cur_x[:, c, :], wq[:, c, col0:col0 + ncol], c == 0, c == 7, [xnk, "wq"], [pk])

            def tm_shift(col0, ncol, dst, dk):
                ps, psb, pk = nps()
                for c in range(8):
                    mm(ps[:, 0:ncol], cur_x[:, c, :], wq[:, c, MLW + col0:MLW + col0 + ncol], c == 0, c == 7, [xnk, "wq"], [pk])
                for c in range(8):
                    mm(ps[:, 512:512 + ncol], dxT[:, c, :], wq[:, c, MLW + col0:MLW + col0 + ncol], c == 0, c == 7, ["dxT", "wq"], [pk])
                dve(lambda e, ps=ps: e.tensor_tensor(out=dst, in0=ps[:, 512:512 + ncol], in1=mub[:, col0:col0 + ncol], op=ALU.mult), [pk, "mub"], [dk])
                dve(lambda e, ps=ps: e.tensor_tensor(out=dst, in0=dst, in1=ps[:, 0:ncol], op=ALU.add), [pk, dk], [dk])

            ps, psb, pk = nps()
            tm_plain(1024, 512, ps[:, 0:512], pk)
            tm_plain(1536, 512, ps[:, 512:1024], pk)
            act(lambda e, ps=ps: e.copy(out=vaug[:, :, 0:64], in_=ps[:, 0:512].rearrange("p (h d) -> p h d", h=8)), [pk], ["vaug"])
            act(lambda e, ps=ps: e.activation(out=osig[:], in_=ps[:, 512:1024], func=AF.Sigmoid), [pk], ["osig"])
            ps, psb, pk = nps()
            tm_plain(2048, 16, ps[:, 0:16], pk)
            dve(lambda e, ps=ps: e.tensor_tensor(out=gt[:, 0:16], in0=ps[:, 0:16], in1=PA("ifb"), op=ALU.add), [pk, "pA"], ["gt"])
            tm_shift(0, 512, r_sb[:], "r_sb")
            tm_shift(512, 512, kf_sb[:], "kf_sb")
            tm_shift(1024, 512, vf[:], "vf")
            pool(lambda e: e.tensor_copy(out=vrw[:], in_=vf[:].rearrange("p (h d) -> p h d", h=8)), ["vf"], ["vrw"])
            ltmp = G[18]
            tm_shift(1536, 256, ltmp[:, 0:256], "tAm")
            act(lambda e: e.activation(out=lor[:, 0:64], in_=ltmp[:, 0:64], func=AF.Tanh), ["tAm"], ["lor"])
            act(lambda e: e.copy(out=lor[:, 64:128], in_=ltmp[:, 64:128]), ["tAm"], ["lor"])
            act(lambda e: e.activation(out=lor[:, 128:256], in_=ltmp[:, 128:256], func=AF.Sigmoid), ["tAm"], ["lor"])
            if b == NB - 1:
                lastc = xn[:, :, 128:129]
                for n0 in range(0, RWW, 512):
                    nn = min(512, RWW - n0)
                    ps2, _, pk2 = nps()
                    for c in range(8):
                        mm(ps2[0:1, 0:nn], lastc[:, c, :], wq[:, c, MLW + n0:MLW + n0 + nn], c == 0, c == 7, [xnk, "wq"], [pk2])
                    act(lambda e, ps2=ps2, n0=n0, nn=nn: e.copy(out=plast[0:1, 0:nn], in_=ps2[0:1, 0:nn]), [pk2], ["plast"])
                    dma("pool", oshift[:, n0:n0 + nn], plast[0:1, 0:nn], ["plast"], [], "fin")

            def ml_stage():
                act(lambda e: e.activation(out=gt[:, 56:64], in_=gt[:, 8:16], func=AF.Exp, scale=-1.0), ["gt"], ["gt"])
                act(lambda e: e.activation(out=gt[:, 16:24], in_=gt[:, 56:64], func=AF.Ln, bias=1.0, scale=1.0), ["gt"], ["gt"])
                dve(lambda e: e.tensor_copy(out=nlrep[:], in_=bc8(gt[:, 16:24])), ["gt"], ["nlrep"])
                ps, psb, pk = nps()
                mm(ps[:, 0:8], MUI, gt[:, 16:24], True, True, ["pA", "gt"], [pk])
                mm(ps[:, 8:16], ONES, gt[:, 16:24], True, True, ["pA", "gt"], [pk])
                for j in range(4):
                    mm(ps[:, 512 + j * 128:512 + (j + 1) * 128], nlrep[:, 2 * j:2 * j + 2, :].rearrange("p a d -> p (a d)"), MUI, True, True, ["nlrep", "pA"], [pk])
                dve(lambda e, ps=ps: e.tensor_tensor(out=gt[:, 24:32], in0=ps[:, 0:8], in1=gt[:, 0:8], op=ALU.add), [pk, "gt"], ["gt"])
                act(lambda e: e.activation(out=gt[:, 32:40], in_=gt[:, 24:32], func=AF.Exp), ["gt"], ["gt"])
                dve(lambda e, ps=ps: e.tensor_tensor(out=gt[:, 56:64], in0=gt[:, 24:32], in1=ps[:, 8:16], op=ALU.subtract), [pk, "gt"], ["gt"])
                act(lambda e: e.activation(out=gt[:, 40:48], in_=gt[:, 56:64], func=AF.Exp), ["gt"], ["gt"])
                act(lambda e, ps=ps: e.activation(out=gt[:, 48:56], in_=ps[:, 8:16], func=AF.Exp, scale=-1.0), [pk], ["gt"])
                act(lambda e, ps=ps: e.activation(out=Fb[:], in_=ps[:, 512:1024].rearrange("p (j t) -> p j t", j=4), func=AF.Exp, scale=-1.0), [pk], ["Fb"])
                dve(lambda e: e.tensor_tensor(out=gt[:, 56:64], in0=gt[:, 24:32], in1=nBc[:], op=ALU.add), ["gt", "nBc"], ["gt"])
                dve(lambda e: e.tensor_tensor(out=runmax[:], in0=runmax[:], in1=gt[:, 56:64], op=ALU.max), ["gt", "runmax"], ["runmax"])
                dve(lambda e, ps=ps: e.tensor_tensor(out=nBc[:], in0=nBc[:], in1=ps[:, 8:16], op=ALU.add), [pk, "nBc"], ["nBc"])

                yield
                cwv = PA("cw").rearrange("p (c j) -> p c j", j=4)
                wbc = lambda j: cwv[:, :, j:j + 1].to_broadcast([128, 8, 128])
                pool(lambda e, qx=qx: e.tensor_tensor(out=cacc[:], in0=qx[:, :, 3:131], in1=wbc(3), op=ALU.mult), [qxk, "pA"], ["cacc"])
                for j in range(3):
                    pool(lambda e, qx=qx, j=j: e.tensor_tensor(out=ctmp[:], in0=qx[:, :, j:j + 128], in1=wbc(j), op=ALU.mult), [qxk, "pA"], ["ctmp"])
                    pool(lambda e: e.tensor_tensor(out=cacc[:], in0=cacc[:], in1=ctmp[:], op=ALU.add), ["cacc", "ctmp"], ["cacc"])
                pool(lambda e: e.tensor_tensor(out=cacc[:], in0=cacc[:], in1=PA("cb").unsqueeze(2).to_broadcast([128, 8, 128]), op=ALU.add), ["cacc", "pA"], ["cacc"])
                act(lambda e: e.activation(out=qks[:], in_=cacc[:], func=AF.Silu), ["cacc"], ["qks"])
                dve(lambda e: e.tensor_tensor(out=qpb[:], in0=qks[:, 0:4, :], in1=Fb[:], op=ALU.mult), ["qks", "Fb"], ["qpb"])
                act(lambda e: e.activation(out=kTb[:], in_=qks[:, 4:8, :], func=AF.Copy, scale=0.125), ["qks"], ["kTb"])

                yield
                ps, psb, pk = nps()
                for j in range(4):
                    pe(lambda e, j=j, psb=psb: e.transpose(psb[:, j * 128:(j + 1) * 128], kTb[:, j, :], identb[:]), ["kTb", "identb"], [pk])
                dve(lambda e, psb=psb: e.tensor_tensor(out=ktm[:], in0=psb[:, 0:512].rearrange("p (h d) -> p h d", h=8), in1=bc8(gt[:, 40:48]), op=ALU.mult), [pk, "gt"], ["ktm"])

                yield
                ps, psb, pk = nps()
                for h in range(8):
                    j, hp = h // 2, h % 2
                    sl = slice(hp * 64, hp * 64 + 64)
                    mm(ps[:, hoff(h):hoff(h) + 128], kTb[sl, j, :], qpb[sl, j, :], True, True, ["kTb", "qpb"], [pk])
                for h in range(8):
                    dve(lambda e, h=h, ps=ps: e.scalar_tensor_tensor(out=PTb[:, h, :], in0=ps[:, hoff(h):hoff(h) + 128], scalar=gt[:, 32 + h:33 + h], in1=MUI, op0=ALU.mult, op1=ALU.mult), [pk, "gt", "pA"], ["PTb"])
                yield
                ps, psb, pk = nps()
                psn = lambda ps, h: ps[:, (h // 4) * 512 + (h % 4) * 65:(h // 4) * 512 + (h % 4) * 65 + 65]
                for h in range(8):
                    j, hp = h // 2, h % 2
                    sl = slice(hp * 64, hp * 64 + 64)
                    mm(psn(ps, h), PTb[:, h, :], vaug[:, h, :], True, False, ["PTb", "vaug"], [pk])
                    mm(psn(ps, h), qpb[sl, j, :], Cbf[sl, j, :], False, True, ["qpb", "Cbf"], [pk])
                pn4 = ps[:, :].rearrange("p (a r) -> p a r", a=2)[:, :, 0:260].rearrange("p a (h d) -> p a h d", h=4)
                for a_ in range(2):
                    act(lambda e, pn4=pn4, a_=a_: e.copy(out=r8[:, 4 * a_:4 * a_ + 4], in_=pn4[:, a_, :, 64]), [pk], ["r8"])
                dve(lambda e: e.scalar_tensor_tensor(out=r8[:, 8:16], in0=r8[:, 0:8], scalar=-1.0, in1=r8[:, 0:8], op0=ALU.mult, op1=ALU.max), ["r8"], ["r8"])
                dve(lambda e: e.tensor_scalar_max(out=r8[:, 8:16], in0=r8[:, 8:16], scalar1=1.0), ["r8"], ["r8"])
                dve(lambda e: e.reciprocal(out=r8[:, 16:24], in_=r8[:, 8:16]), ["r8"], ["r8"])
                for a_ in range(2):
                    dve(lambda e, pn4=pn4, a_=a_: e.tensor_tensor(out=hml[:, a_ * 256:(a_ + 1) * 256].rearrange("p (h d) -> p h d", h=4), in0=pn4[:, a_, :, 0:64],
                                                              in1=r8[:, 16 + 4 * a_:20 + 4 * a_].unsqueeze(2).to_broadcast([128, 4, 64]), op=ALU.mult), [pk, "r8"], ["hml"])
                yield
                ps, psb, pk = nps()
                for h in range(8):
                    j = h // 2
                    mm(psn(ps, h), ktm[:, 2 * j:2 * j + 2, :].rearrange("p a d -> p (a d)"), vaug[:, h, :], True, True, ["ktm", "vaug"], [pk])
                pu4 = ps[:, :].rearrange("p (a r) -> p a r", a=2)[:, :, 0:260].rearrange("p a (h d) -> p a h d", h=4)
                for hp in range(2):
                    sl = slice(hp * 64, hp * 64 + 64)
                    decb = gt[sl, 48:56].rearrange("p (j q) -> p j q", q=2)[:, :, hp:hp + 1].to_broadcast([64, 4, 65])
                    dve(lambda e, sl=sl, decb=decb: e.tensor_tensor(out=Cst[sl, :, :], in0=Cst[sl, :, :], in1=decb, op=ALU.mult), ["Cst", "gt"], ["Cst"])
                    for a in range(2):
                        src = pu4[sl, a, hp::2, :]
                        dve(lambda e, sl=sl, a=a, src=src: e.tensor_tensor(out=Cst[sl, 2 * a:2 * a + 2, :], in0=Cst[sl, 2 * a:2 * a + 2, :], in1=src, op=ALU.add), [pk, "Cst"], ["Cst"])
                act(lambda e: e.copy(out=Cbf[:], in_=Cst[:]), ["Cst"], ["Cbf"])
                head_ml(128, hml, "hml", osig, "osig", mix, "mix", WM, "m")


                yield
            def rw_stage():
                ps, psb, pk = nps()
                pe(lambda e, psb=psb: e.transpose(psb[:, 0:128], lor[:, 0:128], identb[:]), ["lor", "identb"], [pk])
                pe(lambda e, psb=psb: e.transpose(psb[:, 128:256], lor[:, 128:256], identb[:]), ["lor", "identb"], [pk])
                act(lambda e, psb=psb: e.copy(out=lorT[:], in_=psb[:, 0:256].rearrange("p (a t) -> p a t", a=2)), [pk], ["lorT"])
                ps, psb, pk = nps()
                mm(ps[:, 0:512], lorT[0:64, 0, :], luw[0:64, :], True, True, ["lorT", "luw"], [pk])
                mm(ps[:, 512:1024], lorT[64:128, 0, :], luw[64:128, :], True, True, ["lorT", "luw"], [pk])
                dve(lambda e, ps=ps: e.tensor_tensor(out=e1[:], in0=ps[:, 0:512], in1=PA("w0"), op=ALU.add), [pk, "pA"], ["e1"])
                act(lambda e: e.activation(out=wsig[:], in_=e1[:], func=AF.Sigmoid), ["e1"], ["wsig"])
                dve(lambda e, ps=ps: e.tensor_tensor(out=e2[:], in0=ps[:, 512:1024], in1=PA("a0"), op=ALU.add), [pk, "pA"], ["e2"])
                act(lambda e: e.activation(out=a_sb[:], in_=e2[:], func=AF.Sigmoid), ["e2"], ["a_sb"])
                ps, psb, pk = nps()
                mm(ps[:, 0:512], lorT[:, 1, :], gup[:, :], True, True, ["lorT", "gup"], [pk])
                act(lambda e, ps=ps: e.copy(out=g_sb[:], in_=ps[:, 0:512]), [pk], ["g_sb"])
                yield
                dve(lambda e: e.tensor_tensor(out=e1[:], in0=kf_sb[:], in1=PA("kk"), op=ALU.mult), ["kf_sb", "pA"], ["e1"])
                dve(lambda e: e.tensor_tensor(out=e2[:], in0=e1[:], in1=e1[:], op=ALU.mult), ["e1"], ["e2"])
                dve(lambda e: e.tensor_reduce(out=r8b[:, 0:8], in_=v3(e2), axis=AX.X, op=ALU.add), ["e2"], ["r8b"])
                dve(lambda e: e.tensor_scalar_max(out=r8b[:, 0:8], in0=r8b[:, 0:8], scalar1=1e-24), ["r8b"], ["r8b"])
                act(lambda e: e.activation(out=r8b[:, 0:8], in_=r8b[:, 0:8], func=AF.Sqrt), ["r8b"], ["r8b"])
                dve(lambda e: e.reciprocal(out=r8b[:, 0:8], in_=r8b[:, 0:8]), ["r8b"], ["r8b"])
                dve(lambda e: e.tensor_tensor(out=v3(kap), in0=v3(e1), in1=bc8(r8b[:, 0:8]), op=ALU.mult), ["e1", "r8b"], ["kap"])
                dve(lambda e: e.tensor_scalar_add(out=e2[:], in0=a_sb[:], scalar1=-1.0), ["a_sb"], ["e2"])
                dve(lambda e: e.tensor_tensor(out=e2[:], in0=e2[:], in1=PA("ka"), op=ALU.mult), ["e2", "pA"], ["e2"])
                dve(lambda e: e.tensor_tensor(out=e2[:], in0=e2[:], in1=kf_sb[:], op=ALU.mult), ["e2", "kf_sb"], ["e2"])
                dve(lambda e: e.tensor_tensor(out=ktl[:], in0=e2[:], in1=kf_sb[:], op=ALU.add), ["e2", "kf_sb"], ["ktl"])
                dve(lambda e: e.tensor_tensor(out=bvec[:], in0=a_sb[:], in1=kap[:], op=ALU.mult), ["a_sb", "kap"], ["bvec"])
                dve(lambda e: e.tensor_tensor(out=e2[:], in0=r_sb[:], in1=ktl[:], op=ALU.mult), ["r_sb", "ktl"], ["e2"])
                dve(lambda e: e.tensor_tensor(out=e2[:], in0=e2[:], in1=PA("rk"), op=ALU.mult), ["e2", "pA"], ["e2"])
                dve(lambda e: e.tensor_reduce(out=bon[:], in_=v3(e2), axis=AX.X, op=ALU.add), ["e2"], ["bon"])
                yield
                ps, psb, pk = nps()
                mm(ps[:, 0:512], MUI, wsig[:], True, True, ["pA", "wsig"], [pk])
                mm(ps[:, 512:1024], ONES, wsig[:], True, True, ["pA", "wsig"], [pk])
                act(lambda e, ps=ps: e.copy(out=pcw[:], in_=ps[:, 0:512]), [pk], ["pcw"])
                dve(lambda e: e.tensor_tensor(out=e1[:], in0=pcw[:], in1=wsig[:], op=ALU.subtract), ["pcw", "wsig"], ["e1"])
                act(lambda e: e.activation(out=e1[:], in_=e1[:], func=AF.Exp, scale=-C0), ["e1"], ["e1"])
                dve(lambda e: e.tensor_tensor(out=TMb[:, 0, :], in0=kap[:], in1=e1[:], op=ALU.mult), ["kap", "e1"], ["TMb0"])
                act(lambda e: e.activation(out=e2[:], in_=pcw[:], func=AF.Exp, scale=-C0), ["pcw"], ["e2"])
                dve(lambda e: e.tensor_tensor(out=TMb[:, 1, :], in0=r_sb[:], in1=e2[:], op=ALU.mult), ["r_sb", "e2"], ["TMb1"])
                act(lambda e: e.activation(out=e3[:], in_=pcw[:], func=AF.Exp, scale=C0), ["pcw"], ["e3"])
                dve(lambda e: e.tensor_tensor(out=TMb[:, 2, :], in0=bvec[:], in1=e3[:], op=ALU.mult), ["bvec", "e3"], ["TMb2"])
                dve(lambda e: e.tensor_tensor(out=TMb[:, 3, :], in0=ktl[:], in1=e3[:], op=ALU.mult), ["ktl", "e3"], ["TMb3"])
                dve(lambda e, ps=ps: e.tensor_tensor(out=e1[:], in0=ps[:, 512:1024], in1=pcw[:], op=ALU.subtract), [pk, "pcw"], ["e1"])
                act(lambda e: e.activation(out=e1[:], in_=e1[:], func=AF.Exp, scale=-C0), ["e1"], ["e1"])
                for hp in range(2):
                    srcb = v3(bvec).rearrange("p (j q) d -> p j q d", q=2)[:, :, hp, :]
                    srck = v3(ktl).rearrange("p (j q) d -> p j q d", q=2)[:, :, hp, :]
                    wl = v3(e1).rearrange("p (j q) d -> p j q d", q=2)[:, :, hp, :]
                    dstb = Btz[:].rearrange("p (j q) c -> p j q c", q=2)[:, :, hp, hp * 64:hp * 64 + 64]
                    dstk = Ktz[:].rearrange("p (j q) c -> p j q c", q=2)[:, :, hp, hp * 64:hp * 64 + 64]
                    dve(lambda e, srcb=srcb, wl=wl, dstb=dstb: e.tensor_tensor(out=dstb, in0=srcb, in1=wl, op=ALU.mult), ["bvec", "e1"], ["Btz"])
                    dve(lambda e, srck=srck, wl=wl, dstk=dstk: e.tensor_tensor(out=dstk, in0=srck, in1=wl, op=ALU.mult), ["ktl", "e1"], ["Ktz"])
                ps2, _, pk2 = nps()
                for j in range(4):
                    mm(ps2[:, j:j + 1], wsig[:, j * 128:(j + 1) * 128], ONES[:, 0:1], True, True, ["wsig", "pA"], [pk2])
                act(lambda e, ps2=ps2: e.activation(out=WLfm[:], in_=ps2[:, 0:4], func=AF.Exp, scale=-C0), [pk2], ["WLfm"])
                yield
                ps, psb, pk = nps()
                for w_ in range(4):
                    for j in range(4):
                        pe(lambda e, w_=w_, j=j, psb=psb: e.transpose(psb[:, (w_ * 4 + j) * 128:(w_ * 4 + j + 1) * 128], TMb[:, w_, j * 128:(j + 1) * 128], identb[:]), ["TMb%d" % w_, "identb"], [pk])
                for w_ in range(4):
                    eng_ = act if w_ % 2 == 0 else dve
                    if w_ % 2 == 0:
                        act(lambda e, psb=psb, w_=w_: e.copy(out=FMt[:, w_, :, :], in_=psb[:, w_ * 512:(w_ + 1) * 512].rearrange("p (j t) -> p j t", j=4)), [pk], ["FMt"])
                    else:
                        dve(lambda e, psb=psb, w_=w_: e.tensor_copy(out=FMt[:, w_, :, :], in_=psb[:, w_ * 512:(w_ + 1) * 512].rearrange("p (j t) -> p j t", j=4)), [pk], ["FMt"])
                KAP, RB, BB, KKB = 0, 1, 2, 3

                def amat(lw, rw_, dst, dk, mask, neg):
                    ps, psb, pk = nps()
                    for h in range(8):
                        j, hp = h // 2, h % 2
                        sl = slice(hp * 64, hp * 64 + 64)
                        mm(ps[:, hoff(h):hoff(h) + 128], FMt[sl, lw, j, :], FMt[sl, rw_, j, :], True, True, ["FMt"], [pk])
                    psv = ps[:, :].rearrange("p (q j t) -> p q j t", q=2, j=4)
                    dstv = dst[:].rearrange("p (j q) t -> p q j t", q=2)
                    mk = mask.unsqueeze(1).unsqueeze(1).to_broadcast([128, 2, 4, 128])
                    if neg:
                        mk3 = mask.unsqueeze(1).to_broadcast([128, 4, 128])
                        for q in range(2):
                            dve(lambda e, q=q: e.scalar_tensor_tensor(out=dstv[:, q], in0=psv[:, q], scalar=-1.0, in1=mk3, op0=ALU.mult, op1=ALU.mult), [pk, "pA"], [dk])
                    else:
                        mk3 = mask.unsqueeze(1).to_broadcast([128, 4, 128])
                        for q in range(2):
                            dve(lambda e, q=q: e.tensor_tensor(out=dstv[:, q], in0=psv[:, q], in1=mk3, op=ALU.mult), [pk, "pA"], [dk])

                amat(BB, KAP, Pw[1], "Pw1", MUS, True)
                amat(KAP, BB, Pw[0], "Pw0", MLS, True)
                amat(KKB, KAP, Am[0], "Am0", MUS, False)
                amat(BB, RB, Am[1], "Am1", MUI, False)
                amat(KKB, RB, Am[2], "Am2", MUI, False)
                yield
                ps, psb, pk = nps()
                for h in range(8):
                    j, hp = h // 2, h % 2
                    sl = slice(hp * 64, hp * 64 + 64)
                    mm(ps[:, h * 64:(h + 1) * 64], FMt[sl, KAP, j, :], Sbf[sl, j, :], True, False, ["FMt", "Sbf"], [pk])
                    mm(ps[:, h * 64:(h + 1) * 64], Am[0][:, h, :], vrw[:, h, :], False, True, ["Am0", "vrw"], [pk])
                act(lambda e, ps=ps: e.copy(out=Zb[0][:], in_=ps[:, 0:512].rearrange("p (h d) -> p h d", h=8)), [pk], ["Zb0"])
                pi = 0
                zi = 0
                for lvl in range(7):
                    yield
                    Pc, PTc = Pw[pi], Pw[pi + 1]
                    Pk, PTk = "Pw%d" % pi, "Pw%d" % (pi + 1)
                    Zc, Zn = Zb[zi], Zb[1 - zi]
                    ps, psb, pk = nps()
                    for h in range(8):
                        mm(ps[:, h * 64:(h + 1) * 64], identb[:], Zc[:, h, :], True, False, ["identb", "Zb%d" % zi], [pk])
                        mm(ps[:, h * 64:(h + 1) * 64], PTc[:, h, :], Zc[:, h, :], False, True, [PTk, "Zb%d" % zi], [pk])
                    if lvl < 6:
                        act(lambda e, ps=ps, Zn=Zn: e.copy(out=Zn[:], in_=ps[:, 0:512].rearrange("p (h d) -> p h d", h=8)), [pk], ["Zb%d" % (1 - zi)])
                        zi = 1 - zi
                        ni = 2 - pi
                        Pn, PTn = Pw[ni], Pw[ni + 1]
                        psA, _, pkA = nps()
                        for h in range(8):
                            mm(psA[:, h * 128:(h + 1) * 128], PTc[:, h, :], Pc[:, h, :], True, True, [PTk, Pk], [pkA])
                        for a_ in range(2):
                            dve(lambda e, psA=psA, Pn=Pn, a_=a_: e.tensor_copy(out=Pn[:, 4 * a_:4 * a_ + 4, :], in_=psA[:, a_ * 512:(a_ + 1) * 512].rearrange("p (h t) -> p h t", h=4)), [pkA], ["Pw%d" % ni])
                        psB, _, pkB = nps()
                        for h in range(8):
                            mm(psB[:, h * 128:(h + 1) * 128], Pc[:, h, :], PTc[:, h, :], True, True, [Pk, PTk], [pkB])
                        for a_ in range(2):
                            act(lambda e, psB=psB, PTn=PTn, a_=a_: e.copy(out=PTn[:, 4 * a_:4 * a_ + 4, :], in_=psB[:, a_ * 512:(a_ + 1) * 512].rearrange("p (h t) -> p h t", h=4)), [pkB], ["Pw%d" % (ni + 1)])
                        pi = ni
                    else:
                        act(lambda e, ps=ps: e.activation(out=Ub[:], in_=ps[:, 0:512].rearrange("p (h d) -> p h d", h=8), func=AF.Copy, scale=-1.0), [pk], ["Ub"])
                yield
                ps, psb, pk = nps()
                for h in range(8):
                    j, hp = h // 2, h % 2
                    sl = slice(hp * 64, hp * 64 + 64)
                    o_ = ps[:, h * 64:(h + 1) * 64]
                    mm(o_, Am[1][:, h, :], Ub[:, h, :], True, False, ["Am1", "Ub"], [pk])
                    mm(o_, Am[2][:, h, :], vrw[:, h, :], False, False, ["Am2", "vrw"], [pk])
                    mm(o_, FMt[sl, RB, j, :], Sbf[sl, j, :], False, True, ["FMt", "Sbf"], [pk])
                act(lambda e, ps=ps: e.copy(out=ysb[:], in_=ps[:, 0:512]), [pk], ["ysb"])
                yield
                ps, psb, pk = nps()
                for j in range(4):
                    o_ = ps[:, j * 64:(j + 1) * 64]
                    mm(o_, Btz[:, 2 * j, :], Ub[:, 2 * j, :], True, False, ["Btz", "Ub"], [pk])
                    mm(o_, Ktz[:, 2 * j, :], vrw[:, 2 * j, :], False, False, ["Ktz", "vrw"], [pk])
                    mm(o_, Btz[:, 2 * j + 1, :], Ub[:, 2 * j + 1, :], False, False, ["Btz", "Ub"], [pk])
                    mm(o_, Ktz[:, 2 * j + 1, :], vrw[:, 2 * j + 1, :], False, True, ["Ktz", "vrw"], [pk])
                dve(lambda e: e.tensor_tensor(out=Sst[:], in0=Sst[:], in1=WLfm[:].unsqueeze(2).to_broadcast([128, 4, 64]), op=ALU.mult), ["Sst", "WLfm"], ["Sst"])
                dve(lambda e, ps=ps: e.tensor_tensor(out=Sst[:], in0=Sst[:], in1=ps[:, 0:256].rearrange("p (j d) -> p j d", j=4), op=ALU.add), [pk, "Sst"], ["Sst"])
                act(lambda e: e.copy(out=Sbf[:], in_=Sst[:]), ["Sst"], ["Sbf"])
                yield
            gens = [ml_stage(), rw_stage()]
            while gens:
                for g_ in list(gens):
                    try:
                        next(g_)
                    except StopIteration:
                        gens.remove(g_)
            head_rw(128, ysb, "ysb", bon, "bon", vf, "vf", g_sb, "g_sb", mix, "mix", W)
            out_proj(128, mix, "mix", x_, xk, b * 128, W)
            if b + 1 < NB:
                pool(lambda e, xn=xn: e.tensor_copy(out=xn[:, :, 0:1], in_=xn[:, :, 128:129]), [xnk], [xnk])
                pool(lambda e, qx=qx: e.tensor_copy(out=qx[:, :, 0:3], in_=qx[:, :, 128:131]), [qxk], [qxk])

        ps, psb, pk = nps()
        mm(ps[0:8, 0:128], runmax[:], IDF, True, True, ["runmax", "pA"], [pk])
        mm(ps[0:8, 128:256], nBc[:], IDF, True, True, ["nBc", "pA"], [pk])
        fs = TT("fs", [8, 16])
        dve(lambda e, ps=ps: e.tensor_reduce(out=fs[:, 0:1], in_=ps[0:8, 0:128], axis=AX.X, op=ALU.max), [pk], ["fs"])
        dve(lambda e: e.tensor_scalar_max(out=fs[:, 0:1], in0=fs[:, 0:1], scalar1=0.0), ["fs"], ["fs"])
        dve(lambda e, ps=ps: e.tensor_tensor(out=fs[:, 1:2], in0=fs[:, 0:1], in1=ps[0:8, 128:129], op=ALU.subtract), [pk, "fs"], ["fs"])
        dma("pool", om, fs[:, 1:2], ["fs"], [], "fin")
        act(lambda e: e.activation(out=fs[:, 2:3], in_=fs[:, 1:2], func=AF.Exp, scale=-1.0), ["fs"], ["fs"])
        dve(lambda e: e.tensor_scalar_mul(out=fs[:, 4:8], in0=pA[0:8, OFF["rsel"][0]:OFF["rsel"][1]], scalar1=fs[:, 2:3]), ["fs", "pA"], ["fs"])
        ps, psb, pk = nps()
        mm(ps[:, 0:4], pA[0:8, OFF["lsel"][0]:OFF["lsel"][1]], fs[:, 4:8], True, True, ["pA", "fs"], [pk])
        scb = TT("scb", [128, 4])
        act(lambda e, ps=ps: e.copy(out=scb[:], in_=ps[:, 0:4]), [pk], ["scb"])
        dve(lambda e: e.tensor_tensor(out=Cst[:], in0=Cst[:], in1=scb[:].unsqueeze(2).to_broadcast([128, 4, 65]), op=ALU.mult), ["Cst", "scb"], ["Cst"])
        dma("pool", oC, Cst[:], ["Cst"], [], "fin")
        dma("pool", oS, Sst[:], ["Sst"], [], "fin")
        PP[0].finalize()
        PP[0] = Prog(ctx)

    with contextlib.ExitStack() as es_s:
        cur[0] = es_s
        if do_sample:
            W = {}
            W["xm"] = TT("s_xm", [128, D])
            junk = W["xm"]
            st = TT("s_st", [128, 4])
            mix = TT("s_mix", [128, D], BF16)
            xsb = mix
            lor = TT("s_lor", [128, 256], BF16)
            lorT = TT("s_lorT", [128, 2, 128], BF16)
            W["mixT"] = TT("s_mixT", [128, 8, 128], BF16)
            sx = TT("sx", [NS, D])
            sxT = TT("sxT", [128, 8, NS], BF16)
            spj = TT("spj", [NS, INW])
            spk = TT("spk", [128, NSP])
            sl_t = TT("sl_t", [NS, 3, 256])
            Cs = TT("Cs", [128, 4096])
            Ss = Cs
            sn_t = TT("sn_t", [128, 64])
            sm_t = TT("sm_t", [128, 1])
            scv = TT("scv", [128, 2, 4, 64])
            ssh = TT("ssh", [128, 3, 64])
            dma("sp", sx[:], xs, [], ["sx"], "sin", True)
            dma("sp", spk[:], spk_d, [], ["spk"], "sin", True)
            dma("sp", sl_t[:, 0, :], sshl_d, [], ["sl_t"], "sin", True)
            dma("sp", sl_t[:, 1, :], mul_d, [], ["sl_t"], "sin", True)
            dma("sp", Cs[:], sC_d, [], ["Cs"], "sin", True)
            dma("sp", sn_t[:], sn_d, [], ["sn_t"], "sin", True)
            dma("sp", sm_t[:], sm_d, [], ["sm_t"], "sin", True)
            dma("sp", scv[:, :, 0:3, :], sconv_d, [], ["scv"], "sin", True)
            dma("sp", ssh[:], sshift_d, [], ["ssh"], "sin", True)
            rmsnorm_T(sx, "sx", NS, sxT, "sxT", 0, "nmw", xsb, "xsb", junk, "junk", st, "st")
            for n0 in range(0, INW, 512):
                nn = min(512, INW - n0)
                ps, psb, pk = nps()
                for c in range(8):
                    mm(ps[:NS, 0:nn], sxT[:, c, :], wq[:, c, n0:n0 + nn], c == 0, c == 7, ["sxT", "wq"], [pk])
                act(lambda e, ps=ps, n0=n0, nn=nn: e.copy(out=spj[:, n0:n0 + nn], in_=ps[:NS, 0:nn]), [pk], ["spj"])
            s1v = scr1.rearrange("(b h) a d -> b a h d", b=NS)
            for a_ in range(7):
                c0_ = a_ * 512 if a_ < 4 else MLW + (a_ - 4) * 512
                dma("pool", s1v[:, a_, :, :], spj[:, c0_:c0_ + 512].rearrange("p (h d) -> p h d", h=8), ["spj"], ["scr1"], "scrw1")
            A7 = TT("A7", [128, 8, 64])
            dma("sp", A7[:, 0:7, :], scr1[:, 0:7, :], ["scr1"], ["A7"], "scrr1")
            gif = TT("gif", [128, 2])
            s_if = nc.dram_tensor("scr_if", [2, 128], F32, kind="Internal").ap()
            for g_ in range(2):
                dma("pool", s_if[g_, :].rearrange("(b h) -> b h", b=NS), spj[:, 2048 + 8 * g_:2056 + 8 * g_], ["spj"], ["scr_if"], "scrwif")
            for g_ in range(2):
                dma("sp", gif[:, g_:g_ + 1], s_if[g_, :].rearrange("(p o) -> p o", o=1), ["scr_if"], ["gif"], "scrrif")
            SP_ = lambda n: spk[:, SOFF[n][0]:SOFF[n][1]]
            pl = spj[:, MLW + 1536:MLW + 1792]
            dma("pool", osshl, pl, ["spj"], [], "fin")
            dve(lambda e: e.tensor_tensor(out=sl_t[:, 2, :], in0=sl_t[:, 0, :], in1=pl, op=ALU.subtract), ["sl_t", "spj"], ["sl_t"])
            dve(lambda e: e.tensor_tensor(out=sl_t[:, 2, :], in0=sl_t[:, 2, :], in1=sl_t[:, 1, :], op=ALU.mult), ["sl_t"], ["sl_t"])
            dve(lambda e: e.tensor_tensor(out=sl_t[:, 2, :], in0=sl_t[:, 2, :], in1=pl, op=ALU.add), ["sl_t", "spj"], ["sl_t"])
            act(lambda e: e.activation(out=lor[:NS, 0:64], in_=sl_t[:, 2, 0:64], func=AF.Tanh), ["sl_t"], ["lor"])
            act(lambda e: e.copy(out=lor[:NS, 64:128], in_=sl_t[:, 2, 64:128]), ["sl_t"], ["lor"])
            act(lambda e: e.activation(out=lor[:NS, 128:256], in_=sl_t[:, 2, 128:256], func=AF.Sigmoid), ["sl_t"], ["lor"])
            ps, psb, pk = nps()
            pe(lambda e, psb=psb: e.transpose(psb[:, 0:NS], lor[:NS, 0:128], identb[:NS, :NS]), ["lor", "identb"], [pk])
            pe(lambda e, psb=psb: e.transpose(psb[:, 128:128 + NS], lor[:NS, 128:256], identb[:NS, :NS]), ["lor", "identb"], [pk])
            act(lambda e, psb=psb: e.copy(out=lorT[:, :, 0:NS], in_=psb[:, 0:256].rearrange("p (a t) -> p a t", a=2)[:, :, 0:NS]), [pk], ["lorT"])
            ps, psb, pk = nps()
            mm(ps[:NS, 0:512], lorT[0:64, 0, 0:NS], luw[0:64, :], True, True, ["lorT", "luw"], [pk])
            mm(ps[:NS, 512:1024], lorT[64:128, 0, 0:NS], luw[64:128, :], True, True, ["lorT", "luw"], [pk])
            ps2, _, pk2 = nps()
            mm(ps2[:NS, 0:512], lorT[:, 1, 0:NS], gup[:, :], True, True, ["lorT", "gup"], [pk2])
            lo3 = spj[:, 0:1536].rearrange("p (a n) -> p a n", a=3)
            for a_ in range(2):
                act(lambda e, ps=ps, a_=a_: e.copy(out=lo3[:, a_, :], in_=ps[:NS, a_ * 512:(a_ + 1) * 512]), [pk], ["lo3"])
            act(lambda e, ps2=ps2: e.copy(out=lo3[:, 2, :], in_=ps2[:NS, 0:512]), [pk2], ["lo3"])
            for a_ in range(3):
                dma("pool", scr2.rearrange("(b h) a d -> b a h d", b=NS)[:, a_, :, :], lo3[:, a_, :].rearrange("p (h d) -> p h d", h=8), ["lo3"], ["scr2"], "scrw2")
            L3 = TT("L3", [128, 3, 64])
            dma("sp", L3[:], scr2, ["scr2"], ["L3"], "scrr2")
            big = TT("big", [128, 4096])
            sv = TT("sv", [128, 64])
            pool(lambda e: e.tensor_copy(out=scv[:, :, 3, :], in_=A7[:, 0:2, :]), ["A7"], ["scv"])
            dma("pool", osconv, scv[:, :, 1:4, :], ["scv"], [], "fin")
            qk_s = TT("qk_s", [128, 2, 64])
            cwqk = lambda w_: spk[:, SOFF["cwq"][0] + w_ * 256:SOFF["cwq"][0] + (w_ + 1) * 256].rearrange("p (j d) -> p j d", j=4)
            for w_ in range(2):
                dve(lambda e, w_=w_: e.tensor_tensor(out=big[:, 0:256].rearrange("p (j d) -> p j d", j=4), in0=scv[:, w_, :, :], in1=cwqk(w_), op=ALU.mult), ["scv", "spk"], ["big"])
                dve(lambda e, w_=w_: e.tensor_reduce(out=qk_s[:, w_, :], in_=big[:, 0:256].rearrange("p (j d) -> p d j", j=4), axis=AX.X, op=ALU.add), ["big"], ["qk_s"])
            dve(lambda e: e.tensor_tensor(out=qk_s[:], in0=qk_s[:], in1=spk[:, SOFF["cbq"][0]:SOFF["cbk"][1]].rearrange("p (a d) -> p a d", a=2), op=ALU.add), ["qk_s", "spk"], ["qk_s"])
            act(lambda e: e.activation(out=qk_s[:], in_=qk_s[:], func=AF.Silu), ["qk_s"], ["qk_s"])
            act(lambda e: e.activation(out=qk_s[:, 1, :], in_=qk_s[:, 1, :], func=AF.Copy, scale=0.125), ["qk_s"], ["qk_s"])
            dve(lambda e: e.tensor_tensor(out=sv[:, 0:2], in0=gif[:], in1=spk[:, SOFF["ib"][0]:SOFF["fb"][1]], op=ALU.add), ["gif", "spk"], ["sv"])
            act(lambda e: e.activation(out=sv[:, 9:10], in_=sv[:, 1:2], func=AF.Exp, scale=-1.0), ["sv"], ["sv"])
            act(lambda e: e.activation(out=sv[:, 2:3], in_=sv[:, 9:10], func=AF.Ln, bias=1.0, scale=1.0), ["sv"], ["sv"])
            dve(lambda e: e.tensor_tensor(out=sv[:, 3:4], in0=sm_t[:], in1=sv[:, 2:3], op=ALU.subtract), ["sv", "sm_t"], ["sv"])
            dve(lambda e: e.tensor_tensor(out=sv[:, 4:5], in0=sv[:, 3:4], in1=sv[:, 0:1], op=ALU.max), ["sv"], ["sv"])
            dma("pool", osm, sv[:, 4:5], ["sv"], [], "fin")
            dve(lambda e: e.tensor_tensor(out=sv[:, 9:10], in0=sv[:, 0:1], in1=sv[:, 4:5], op=ALU.subtract), ["sv"], ["sv"])
            act(lambda e: e.activation(out=sv[:, 5:6], in_=sv[:, 9:10], func=AF.Exp), ["sv"], ["sv"])
            dve(lambda e: e.tensor_tensor(out=sv[:, 9:10], in0=sv[:, 3:4], in1=sv[:, 4:5], op=ALU.subtract), ["sv"], ["sv"])
            act(lambda e: e.activation(out=sv[:, 6:7], in_=sv[:, 9:10], func=AF.Exp), ["sv"], ["sv"])
            act(lambda e: e.activation(out=sv[:, 7:8], in_=sv[:, 4:5], func=AF.Exp, scale=-1.0), ["sv"], ["sv"])
            q_ = qk_s[:, 0, :]
            k_ = qk_s[:, 1, :]
            v_ = A7[:, 2, :]
            b3 = lambda t: t[:, :].rearrange("p (a c) -> p a c", a=64)
            pool(lambda e: e.tensor_tensor(out=b3(big), in0=k_.unsqueeze(2).to_broadcast([128, 64, 64]), in1=v_.unsqueeze(1).to_broadcast([128, 64, 64]), op=ALU.mult), ["qk_s", "A7"], ["big"])
            dve(lambda e: e.tensor_scalar_mul(out=Cs[:], in0=Cs[:], scalar1=sv[:, 6:7]), ["Cs", "sv"], ["Cs"])
            dve(lambda e: e.scalar_tensor_tensor(out=Cs[:], in0=big[:], scalar=sv[:, 5:6], in1=Cs[:], op0=ALU.mult, op1=ALU.add), ["big", "sv", "Cs"], ["Cs"])
            dma("pool", osC, Cs[:], ["Cs"], [], "fin")
            dve(lambda e: e.tensor_scalar_mul(out=sn_t[:], in0=sn_t[:], scalar1=sv[:, 6:7]), ["sn_t", "sv"], ["sn_t"])
            dve(lambda e: e.scalar_tensor_tensor(out=sn_t[:], in0=k_, scalar=sv[:, 5:6], in1=sn_t[:], op0=ALU.mult, op1=ALU.add), ["qk_s", "sv", "sn_t"], ["sn_t"])
            dma("pool", osn, sn_t[:], ["sn_t"], [], "fin")
            pool(lambda e: e.tensor_tensor(out=b3(big), in0=Cs[:, :].rearrange("p (k v) -> p v k", k=64), in1=q_.unsqueeze(1).to_broadcast([128, 64, 64]), op=ALU.mult), ["Cs", "qk_s"], ["big"])
            hs = TT("hs", [128, 2, 64])
            dve(lambda e: e.tensor_reduce(out=hs[:, 0, :], in_=b3(big), axis=AX.X, op=ALU.add), ["big"], ["hs"])
            dve(lambda e: e.tensor_tensor(out=sv[:, 16:80 - 16] if False else big[:, 0:64], in0=q_, in1=sn_t[:], op=ALU.mult), ["qk_s", "sn_t"], ["big"])
            dve(lambda e: e.tensor_reduce(out=sv[:, 8:9], in_=big[:, 0:64], axis=AX.X, op=ALU.add), ["big"], ["sv"])
            dve(lambda e: e.scalar_tensor_tensor(out=sv[:, 9:10], in0=sv[:, 8:9], scalar=-1.0, in1=sv[:, 8:9], op0=ALU.mult, op1=ALU.max), ["sv"], ["sv"])
            dve(lambda e: e.tensor_tensor(out=sv[:, 9:10], in0=sv[:, 9:10], in1=sv[:, 7:8], op=ALU.max), ["sv"], ["sv"])
            dve(lambda e: e.reciprocal(out=sv[:, 10:11], in_=sv[:, 9:10]), ["sv"], ["sv"])
            dve(lambda e: e.tensor_scalar_mul(out=hs[:, 0, :], in0=hs[:, 0, :], scalar1=sv[:, 10:11]), ["hs", "sv"], ["hs"])
            dma("pool", osshift, A7[:, 4:7, :], ["A7"], [], "fin")
            rk3 = TT("rk3", [128, 3, 64])
            mu3 = spk[:, SOFF["mu_r"][0]:SOFF["mu_v"][1]].rearrange("p (a d) -> p a d", a=3)
            dve(lambda e: e.tensor_tensor(out=rk3[:], in0=ssh[:], in1=A7[:, 4:7, :], op=ALU.subtract), ["ssh", "A7"], ["rk3"])
            dve(lambda e: e.tensor_tensor(out=rk3[:], in0=rk3[:], in1=mu3, op=ALU.mult), ["rk3", "spk"], ["rk3"])
            dve(lambda e: e.tensor_tensor(out=rk3[:], in0=rk3[:], in1=A7[:, 4:7, :], op=ALU.add), ["rk3", "A7"], ["rk3"])
            w8 = TT("w8", [128, 8, 64])
            dve(lambda e: e.tensor_tensor(out=w8[:, 0, :], in0=L3[:, 0, :], in1=SP_("w0"), op=ALU.add), ["L3", "spk"], ["w8"])
            act(lambda e: e.activation(out=w8[:, 0, :], in_=w8[:, 0, :], func=AF.Sigmoid), ["w8"], ["w8"])
            act(lambda e: e.activation(out=w8[:, 0, :], in_=w8[:, 0, :], func=AF.Exp, scale=-C0), ["w8"], ["w8"])
            dve(lambda e: e.tensor_tensor(out=w8[:, 1, :], in0=L3[:, 1, :], in1=SP_("a0"), op=ALU.add), ["L3", "spk"], ["w8"])
            act(lambda e: e.activation(out=w8[:, 1, :], in_=w8[:, 1, :], func=AF.Sigmoid), ["w8"], ["w8"])
            dve(lambda e: e.tensor_tensor(out=w8[:, 6, :], in0=rk3[:, 1, :], in1=SP_("kk"), op=ALU.mult), ["rk3", "spk"], ["w8"])
            dve(lambda e: e.tensor_tensor(out=w8[:, 7, :], in0=w8[:, 6, :], in1=w8[:, 6, :], op=ALU.mult), ["w8"], ["w8"])
            dve(lambda e: e.tensor_reduce(out=sv[:, 11:12], in_=w8[:, 7, :], axis=AX.X, op=ALU.add), ["w8"], ["sv"])
            dve(lambda e: e.tensor_scalar_max(out=sv[:, 11:12], in0=sv[:, 11:12], scalar1=1e-24), ["sv"], ["sv"])
            act(lambda e: e.activation(out=sv[:, 11:12], in_=sv[:, 11:12], func=AF.Sqrt), ["sv"], ["sv"])
            dve(lambda e: e.reciprocal(out=sv[:, 11:12], in_=sv[:, 11:12]), ["sv"], ["sv"])
            dve(lambda e: e.tensor_scalar_mul(out=w8[:, 3, :], in0=w8[:, 6, :], scalar1=sv[:, 11:12]), ["w8", "sv"], ["w8"])
            dve(lambda e: e.tensor_tensor(out=w8[:, 6, :], in0=w8[:, 1, :], in1=SP_("ka"), op=ALU.mult), ["w8", "spk"], ["w8"])
            dve(lambda e: e.tensor_tensor(out=w8[:, 6, :], in0=w8[:, 6, :], in1=SP_("ka"), op=ALU.subtract), ["w8", "spk"], ["w8"])
            dve(lambda e: e.tensor_scalar_add(out=w8[:, 6, :], in0=w8[:, 6, :], scalar1=1.0), ["w8"], ["w8"])
            dve(lambda e: e.tensor_tensor(out=w8[:, 4, :], in0=rk3[:, 1, :], in1=w8[:, 6, :], op=ALU.mult), ["rk3", "w8"], ["w8"])
            dve(lambda e: e.tensor_tensor(out=w8[:, 5, :], in0=w8[:, 1, :], in1=w8[:, 3, :], op=ALU.mult), ["w8"], ["w8"])
            dma("sp", Ss[:], sS_d, [], ["Ss"], "sin2")
            bk = lambda ap: ap.unsqueeze(1).to_broadcast([128, 64, 64])
            bv = lambda ap: ap.unsqueeze(2).to_broadcast([128, 64, 64])
            pool(lambda e: e.tensor_tensor(out=b3(big), in0=b3(Ss), in1=bk(w8[:, 3, :]), op=ALU.mult), ["Ss", "w8"], ["big"])
            dve(lambda e: e.tensor_reduce(out=w8[:, 7, :], in_=b3(big), axis=AX.X, op=ALU.add), ["big"], ["w8"])
            dve(lambda e: e.tensor_tensor(out=b3(Ss), in0=b3(Ss), in1=bk(w8[:, 0, :]), op=ALU.mult), ["Ss", "w8"], ["Ss"])
            pool(lambda e: e.tensor_tensor(out=b3(big), in0=bv(w8[:, 7, :]), in1=bk(w8[:, 5, :]), op=ALU.mult), ["w8"], ["big"])
            dve(lambda e: e.tensor_tensor(out=Ss[:], in0=Ss[:], in1=big[:], op=ALU.subtract), ["Ss", "big"], ["Ss"])
            pool(lambda e: e.tensor_tensor(out=b3(big), in0=bv(rk3[:, 2, :]), in1=bk(w8[:, 4, :]), op=ALU.mult), ["rk3", "w8"], ["big"])
            dve(lambda e: e.tensor_tensor(out=Ss[:], in0=Ss[:], in1=big[:], op=ALU.add), ["Ss", "big"], ["Ss"])
            dma("pool", osS, Ss[:], ["Ss"], [], "fin")
            pool(lambda e: e.tensor_tensor(out=b3(big), in0=b3(Ss), in1=bk(rk3[:, 0, :]), op=ALU.mult), ["Ss", "rk3"], ["big"])
            dve(lambda e: e.tensor_reduce(out=hs[:, 1, :], in_=b3(big), axis=AX.X, op=ALU.add), ["big"], ["hs"])
            dve(lambda e: e.tensor_tensor(out=w8[:, 6, :], in0=rk3[:, 0, :], in1=w8[:, 4, :], op=ALU.mult), ["rk3", "w8"], ["w8"])
            dve(lambda e: e.tensor_tensor(out=w8[:, 6, :], in0=w8[:, 6, :], in1=SP_("rk"), op=ALU.mult), ["w8", "spk"], ["w8"])
            dve(lambda e: e.tensor_reduce(out=sv[:, 12:13], in_=w8[:, 6, :], axis=AX.X, op=ALU.add), ["w8"], ["sv"])
            dve(lambda e: e.scalar_tensor_tensor(out=hs[:, 1, :], in0=rk3[:, 2, :], scalar=sv[:, 12:13], in1=hs[:, 1, :], op0=ALU.mult, op1=ALU.add), ["rk3", "sv", "hs"], ["hs"])
            act(lambda e: e.activation(out=w8[:, 6, :], in_=A7[:, 3, :], func=AF.Sigmoid), ["A7"], ["w8"])
            dve(lambda e: e.tensor_tensor(out=hs[:, 0, :], in0=hs[:, 0, :], in1=w8[:, 6, :], op=ALU.mult), ["hs", "w8"], ["hs"])
            dve(lambda e: e.tensor_tensor(out=w8[:, 7, :], in0=hs[:, 0, :], in1=hs[:, 0, :], op=ALU.mult), ["hs"], ["w8"])
            dve(lambda e: e.tensor_reduce(out=sv[:, 13:14], in_=w8[:, 7, :], axis=AX.X, op=ALU.add), ["w8"], ["sv"])
            act(lambda e: e.activation(out=sv[:, 13:14], in_=sv[:, 13:14], func=AF.Sqrt, bias=EPS, scale=1.0 / 64), ["sv"], ["sv"])
            dve(lambda e: e.reciprocal(out=sv[:, 13:14], in_=sv[:, 13:14]), ["sv"], ["sv"])
            dve(lambda e: e.scalar_tensor_tensor(out=hs[:, 0, :], in0=hs[:, 0, :], scalar=sv[:, 13:14], in1=SP_("mnw"), op0=ALU.mult, op1=ALU.mult), ["hs", "sv", "spk"], ["hs"])
            dve(lambda e: e.tensor_reduce(out=sv[:, 14:15], in_=hs[:, 1, :], axis=AX.X, op=ALU.add), ["hs"], ["sv"])
            dve(lambda e: e.tensor_scalar_mul(out=sv[:, 14:15], in0=sv[:, 14:15], scalar1=1.0 / 64), ["sv"], ["sv"])
            dve(lambda e: e.tensor_scalar_sub(out=hs[:, 1, :], in0=hs[:, 1, :], scalar1=sv[:, 14:15]), ["hs", "sv"], ["hs"])
            dve(lambda e: e.tensor_tensor(out=w8[:, 7, :], in0=hs[:, 1, :], in1=hs[:, 1, :], op=ALU.mult), ["hs"], ["w8"])
            dve(lambda e: e.tensor_reduce(out=sv[:, 15:16], in_=w8[:, 7, :], axis=AX.X, op=ALU.add), ["w8"], ["sv"])
            act(lambda e: e.activation(out=sv[:, 15:16], in_=sv[:, 15:16], func=AF.Sqrt, bias=GN_EPS, scale=1.0 / 64), ["sv"], ["sv"])
            dve(lambda e: e.reciprocal(out=sv[:, 15:16], in_=sv[:, 15:16]), ["sv"], ["sv"])
            dve(lambda e: e.scalar_tensor_tensor(out=hs[:, 1, :], in0=hs[:, 1, :], scalar=sv[:, 15:16], in1=SP_("lnw"), op0=ALU.mult, op1=ALU.mult), ["hs", "sv", "spk"], ["hs"])
            dve(lambda e: e.tensor_tensor(out=hs[:, 1, :], in0=hs[:, 1, :], in1=SP_("lnb"), op=ALU.add), ["hs", "spk"], ["hs"])
            dve(lambda e: e.tensor_tensor(out=hs[:, 1, :], in0=hs[:, 1, :], in1=L3[:, 2, :], op=ALU.mult), ["hs", "L3"], ["hs"])
            s3v = nc.dram_tensor("scr3b", [128, 2, 64], F32, kind="Internal").ap()
            dma("pool", s3v, hs[:], ["hs"], ["scr3b"], "scrw3")
            smix = spj[:, 2304:3328].rearrange("p (a h d) -> p a h d", a=2, h=8)
            for a_ in range(2):
                dma("sp", smix[:, a_, :, :], s3v.rearrange("(b h) a d -> b a h d", b=NS)[:, a_, :, :], ["scr3b"], ["smix"], "scrr3")
            act(lambda e: e.copy(out=mix[:NS, :], in_=smix[:].rearrange("p a h d -> p (a h d)")), ["smix"], ["mix"])
            out_proj(NS, mix, "mix", sx, "sx", T, W)
        PP[0].finalize()
        PP[0] = Prog(ctx)
    es_res.close()

    with contextlib.ExitStack() as es2:
        cur[0] = es2
        upb = TT("upb", [128, 8, DFF], BF16)
        dnb = TT("dnb", [128, 32, D], BF16)
        pB2 = TT("pB2", [128, 136], F32)
        nfw = TT("nfw", [128, D])
        identb2 = TT("identb2", [128, 128], BF16)
        up_v = mlp_up.rearrange("(c p) n -> p c n", p=128)
        dn_v = mlp_down.rearrange("(c p) n -> p c n", p=128)
        dma("sp", pB2[:, 0:128], packA_d[:, OFF["ident"][0]:OFF["ident"][1]], [], ["pB2"], "init2", True)
        dma("sp", pB2[:, 128:136], packA_d[:, OFF["nmlp"][0]:OFF["nmlp"][1]], [], ["pB2"], "init2", True)
        dma("sp", nfw[:], nfw_d, [], ["nfw"], "init2", True)
        for g8 in range(8):
            dma("pool", upb[:, :, g8 * 512:(g8 + 1) * 512], up_v[:, :, g8 * 512:(g8 + 1) * 512], [], ["upb%d" % g8], "up%d" % g8)
        for g8 in range(8):
            dma("pool", dnb[:, g8 * 4:(g8 + 1) * 4, :], dn_v[:, g8 * 4:(g8 + 1) * 4, :], [], ["dnb%d" % g8], "dn%d" % g8)
        dve(lambda e: e.tensor_copy(out=identb2[:], in_=pB2[:, 0:128]), ["pB2"], ["identb2"])
        x4 = TT("x4", [128, 4, D])
        junk2 = TT("junk2", [128, D])
        st2 = TT("st2", [128, 8])
        xsb2 = TT("xsb2", [128, D], BF16)
        xn2T = TT("xn2T", [128, 8, 512], BF16)
        hT = TT("hT", [128, 32, 512], BF16)
        rl = [TT("rl%d" % i, [128, 512]) for i in range(2)]
        nmlp = pB2[:, 128:136]
        xm_v = xmid_d[0:T, :].rearrange("(s p) d -> p s d", p=128)
        yp_v = yp.rearrange("(s p) d -> p s d", p=128)
        nsb_all = [(sb * 4, 4, 128) for sb in range(4)] + [(16, 1, NS)]
        for (s0, nsub, nt) in nsb_all:
            ntt = nsub * nt if nsub == 4 else nt
            if nsub == 4:
                dma("sp", x4[:], xm_v[:, s0:s0 + 4, :], ["xmid"], ["x4"], "x4")
            else:
                dma("sp", x4[:nt, 0, :], xmid_d[T:T + nt, :], ["xmid"], ["x4"], "x4")
            for si in range(nsub):
                act(lambda e, si=si, nt=nt: e.activation(out=junk2[:nt, :], in_=x4[:nt, si, :], func=AF.Square, accum_out=st2[:nt, 0:1]), ["x4"], ["junk2", "st2"])
                act(lambda e, nt=nt: e.activation(out=st2[:nt, 1:2], in_=st2[:nt, 0:1], func=AF.Sqrt, bias=EPS, scale=1.0 / D), ["st2"], ["st2"])
                dve(lambda e, nt=nt: e.reciprocal(out=st2[:nt, 2:3], in_=st2[:nt, 1:2]), ["st2"], ["st2"])
                dve(lambda e, si=si, nt=nt: e.tensor_scalar_mul(out=xsb2[:nt, :], in0=x4[:nt, si, :], scalar1=st2[:nt, 2:3]), ["x4", "st2"], ["xsb2"])
                ps, psb, pk = nps()
                for c in range(8):
                    pe(lambda e, c=c, psb=psb, nt=nt: e.transpose(psb[:, c * 128:c * 128 + nt], xsb2[:nt, c * 128:(c + 1) * 128], identb2[:nt, :nt]), ["xsb2", "identb2"], [pk])
                dve(lambda e, psb=psb, nt=nt, si=si: e.tensor_tensor(out=xn2T[:, :, si * 128:si * 128 + nt], in0=psb[:, 0:1024].rearrange("p (c t) -> p c t", c=8)[:, :, 0:nt],
                                                                   in1=nmlp.unsqueeze(2).to_broadcast([128, 8, nt]), op=ALU.mult), [pk, "pB2"], ["xn2T"])
            for j2 in range(16):
                ps, psb, pk = nps()
                for jj in range(2):
                    j = j2 * 2 + jj
                    for c in range(8):
                        mm(ps[:, jj * 512:jj * 512 + ntt], upb[:, c, j * 128:(j + 1) * 128], xn2T[:, c, 0:ntt], c == 0, c == 7, ["upb%d" % (j // 4), "xn2T"], [pk])
                for jj in range(2):
                    j = j2 * 2 + jj
                    r_ = rl[jj]
                    rk_ = "rl%d" % jj
                    act(lambda e, ps=ps, r_=r_, jj=jj, ntt=ntt: e.activation(out=r_[:, 0:ntt], in_=ps[:, jj * 512:jj * 512 + ntt], func=AF.Relu), [pk], [rk_])
                    if jj == 0:
                        dve(lambda e, r_=r_, j=j, ntt=ntt: e.tensor_tensor(out=hT[:, j, 0:ntt], in0=r_[:, 0:ntt], in1=r_[:, 0:ntt], op=ALU.mult), [rk_], ["hT"])
                    else:
                        pool(lambda e, r_=r_, j=j, ntt=ntt: e.tensor_tensor(out=hT[:, j, 0:ntt], in0=r_[:, 0:ntt], in1=r_[:, 0:ntt], op=ALU.mult), [rk_], ["hT"])
            for si in range(nsub):
                ps, psb, pk = nps()
                for n in range(2):
                    for j in range(32):
                        mm(ps[:nt, n * 512:(n + 1) * 512], hT[:, j, si * 128:si * 128 + nt], dnb[:, j, n * 512:(n + 1) * 512], j == 0, j == 31, ["hT", "dnb%d" % (j // 4)], [pk])
                for a_ in range(2):
                    dve(lambda e, ps=ps, si=si, nt=nt, a_=a_: e.tensor_tensor(out=x4[:nt, si, a_ * 512:(a_ + 1) * 512], in0=ps[:nt, a_ * 512:(a_ + 1) * 512], in1=x4[:nt, si, a_ * 512:(a_ + 1) * 512], op=ALU.add), [pk, "x4"], ["x4"])
                act(lambda e, si=si, nt=nt: e.activation(out=junk2[:nt, :], in_=x4[:nt, si, :], func=AF.Square, accum_out=st2[:nt, 4:5]), ["x4"], ["junk2", "st2"])
                act(lambda e, nt=nt: e.activation(out=st2[:nt, 5:6], in_=st2[:nt, 4:5], func=AF.Sqrt, bias=EPS, scale=1.0 / D), ["st2"], ["st2"])
                dve(lambda e, nt=nt: e.reciprocal(out=st2[:nt, 6:7], in_=st2[:nt, 5:6]), ["st2"], ["st2"])
                dve(lambda e, si=si, nt=nt: e.scalar_tensor_tensor(out=x4[:nt, si, :], in0=x4[:nt, si, :], scalar=st2[:nt, 6:7], in1=nfw[:nt, :], op0=ALU.mult, op1=ALU.mult), ["x4", "st2", "nfw"], ["x4"])
            if nsub == 4:
                dma("pool", yp_v[:, s0:s0 + 4, :], x4[:], ["x4"], [], "yo")
            else:
                dma("pool", ys, x4[:nt, 0, :], ["x4"], [], "yo")
        PP[0].finalize()
    es_ps.close()
    ctx.close()
    return nc


_CACHE = {}


def _host_packs(inp, core):
    f = np.float32
    L = 0
    pa = np.zeros((128, NA), f)

    def put(n, arr):
        a, b = OFF[n]
        pa[:, a:b] = arr

    rep = lambda v: np.broadcast_to(np.asarray(v, f).reshape(1, -1), (128, np.asarray(v).size))
    put("mnw", rep(inp["mlstm_norm_w"][L]))
    put("w0", rep(inp["rw_w0"][L]))
    put("a0", rep(inp["rw_a0"][L]))
    put("kk", rep(inp["rw_k_k"][L]))
    put("ka", rep(inp["rw_k_a"][L]))
    put("rk", rep(inp["rw_r_k"][L].reshape(-1)))
    put("lnw", rep(inp["rw_ln_w"][L]))
    put("lnb", rep(inp["rw_ln_b"][L]))
    put("ifb", rep(np.concatenate([inp["mlstm_i_b"][L], inp["mlstm_f_b"][L]])))
    put("nmw", inp["norm_mix_w"][L].reshape(8, 128).T)
    put("nmlp", inp["norm_mlp_w"][L].reshape(8, 128).T)
    cw = inp["mlstm_conv_w"][L]
    put("cw", cw.reshape(4, 8, 128).transpose(2, 1, 0).reshape(128, 32))
    put("cb", inp["mlstm_conv_b"][L].reshape(8, 128).T)
    put("ident", np.eye(128, dtype=f))
    put("mui", np.triu(np.ones((128, 128), f), 0))
    put("mus", np.triu(np.ones((128, 128), f), 1))
    put("mls", np.tril(np.ones((128, 128), f), -1))
    put("ones", np.ones((128, 128), f))
    lsel = np.zeros((128, 128), f)
    rsel = np.zeros((128, 4), f)
    for h in range(8):
        lsel[h, (h % 2) * 64:(h % 2) * 64 + 64] = 1.0
        rsel[h, h // 2] = 1.0
    put("lsel", lsel)
    put("rsel", rsel)
    return pa


def _sample_pack(inp):
    f = np.float32
    L = 0
    sp = np.zeros((128, NSP), f)

    def bh(v512):
        return np.tile(np.asarray(v512, f).reshape(8, 64), (NS, 1))

    def put(n, arr):
        a, b = SOFF[n]
        sp[:, a:b] = arr

    mu = inp["rw_mu"][L]
    put("mu_r", bh(mu[0:512]))
    put("mu_k", bh(mu[512:1024]))
    put("mu_v", bh(mu[1024:1536]))
    cw = inp["mlstm_conv_w"][L]
    put("cwq", np.concatenate([bh(cw[j, 0:512]) for j in range(4)], axis=1))
    put("cwk", np.concatenate([bh(cw[j, 512:1024]) for j in range(4)], axis=1))
    cb = inp["mlstm_conv_b"][L]
    put("cbq", bh(cb[0:512]))
    put("cbk", bh(cb[512:1024]))
    put("mnw", bh(inp["mlstm_norm_w"][L]))
    put("w0", bh(inp["rw_w0"][L]))
    put("a0", bh(inp["rw_a0"][L]))
    put("kk", bh(inp["rw_k_k"][L]))
    put("ka", bh(inp["rw_k_a"][L]))
    put("rk", bh(inp["rw_r_k"][L].reshape(-1)))
    put("lnw", bh(inp["rw_ln_w"][L]))
    put("lnb", bh(inp["rw_ln_b"][L]))
    put("ib", np.tile(inp["mlstm_i_b"][L].reshape(8, 1), (NS, 1)))
    put("fb", np.tile(inp["mlstm_f_b"][L].reshape(8, 1), (NS, 1)))
    return sp


def kernel(**inp):
    f = np.float32
    inp = {k: np.asarray(v) for k, v in inp.items()}
    if "nc" not in _CACHE:
        _CACHE["nc"] = build_program()
    nc = _CACHE["nc"]
    L = 0
    pa = _host_packs(inp, 0)
    sp = _sample_pack(inp)
    mu = inp["rw_mu"][L]
    luw = np.concatenate([inp["rw_w_up"][L], inp["rw_a_up"][L]], axis=0).astype(f)
    common = {
        "w_in": np.ascontiguousarray(inp["w_in"][L], f),
        "w_out": np.ascontiguousarray(inp["w_out"][L], f),
        "mlp_up": np.ascontiguousarray(inp["mlp_up"][L], f),
        "mlp_down": np.ascontiguousarray(inp["mlp_down"][L], f),
        "packA": pa,
        "mu_b": np.ascontiguousarray(np.broadcast_to(mu.reshape(1, -1), (128, RWW)), f),
        "nfw_b": np.ascontiguousarray(np.broadcast_to(inp["norm_f_w"].reshape(1, -1), (128, D)), f),
        "luw": luw,
        "gup": np.ascontiguousarray(inp["rw_g_up"][L], f),
        "spack": sp,
        "mul": np.ascontiguousarray(np.broadcast_to(mu[1536:1792].reshape(1, -1), (NS, 256)), f),
    }
    in_maps = []
    for c in range(8):
        rs = slice(c * NS, (c + 1) * NS)
        m = dict(common)
        m["xp"] = np.ascontiguousarray(inp["x_prompt"][c], f)
        m["xs"] = np.ascontiguousarray(inp["x_sample"][rs, 0, :], f)
        m["sC"] = np.ascontiguousarray(inp["state_mlstm_C"][L, rs].reshape(128, 4096), f)
        m["sn"] = np.ascontiguousarray(inp["state_mlstm_n"][L, rs].reshape(128, 64), f)
        m["sm"] = np.ascontiguousarray(inp["state_mlstm_m"][L, rs].reshape(128, 1), f)
        cv = inp["state_mlstm_conv"][L, rs]
        m["sconv"] = np.ascontiguousarray(cv.reshape(NS, 3, 2, 8, 64).transpose(0, 3, 2, 1, 4).reshape(128, 2, 3, 64), f)
        m["sS"] = np.ascontiguousarray(inp["state_rwkv_S"][L, rs].reshape(128, 4096), f)
        sh = inp["state_rwkv_shift"][L, rs, 0, :]
        m["sshift"] = np.ascontiguousarray(sh[:, 0:1536].reshape(NS, 3, 8, 64).transpose(0, 2, 1, 3).reshape(128, 3, 64), f)
        m["sshl"] = np.ascontiguousarray(sh[:, 1536:1792], f)
        in_maps.append(m)
    res = run_bass_kernel_spmd(nc, in_maps, core_ids=list(range(8)))
    R = res.results
    y_prompt = np.stack([R[c]["yp"] for c in range(8)]).astype(f)
    y_sample = np.concatenate([R[c]["ys"] for c in range(8)], axis=0).reshape(128, 1, D).astype(f)
    pC = np.zeros((1, 8, 8, 64, 64), f)
    pn = np.zeros((1, 8, 8, 64), f)
    pm = np.zeros((1, 8, 8), f)
    pconv = np.zeros((1, 8, 3, 1024), f)
    pS = np.zeros((1, 8, 8, 64, 64), f)
    pshift = np.zeros((1, 8, 1, RWW), f)
    for c in range(8):
        oC = R[c]["oC"].reshape(2, 64, 4, 65)
        Ch = oC.transpose(2, 0, 1, 3).reshape(8, 64, 65)
        pC[0, c] = Ch[:, :, 0:64]
        pn[0, c] = Ch[:, :, 64]
        pm[0, c] = R[c]["om"].reshape(8)
        pconv[0, c] = R[c]["oconv"].transpose(2, 1, 0).reshape(3, 1024)
        oS = R[c]["oS"].reshape(2, 64, 4, 64)
        pS[0, c] = oS.transpose(2, 0, 3, 1).reshape(8, 64, 64)
        pshift[0, c, 0] = R[c]["oshift"].reshape(RWW)
    sC = np.concatenate([R[c]["osC"].reshape(NS, 8, 64, 64) for c in range(8)])[None].astype(f)
    sn = np.concatenate([R[c]["osn"].reshape(NS, 8, 64) for c in range(8)])[None].astype(f)
    sm = np.concatenate([R[c]["osm"].reshape(NS, 8) for c in range(8)])[None].astype(f)
    sconv = np.concatenate([R[c]["osconv"].reshape(NS, 8, 2, 3, 64).transpose(0, 3, 2, 1, 4).reshape(NS, 3, 1024) for c in range(8)])[None].astype(f)
    sS = np.concatenate([R[c]["osS"].reshape(NS, 8, 64, 64) for c in range(8)])[None].astype(f)
    sshift = np.concatenate([
        np.concatenate([R[c]["osshift"].reshape(NS, 8, 3, 64).transpose(0, 2, 1, 3).reshape(NS, 1536), R[c]["osshl"]], axis=1)
        for c in range(8)]).reshape(1, 128, 1, RWW).astype(f)
    return (y_prompt, y_sample, pC, pn, pm, pconv, pS, pshift, sC, sn, sm, sconv, sS, sshift)
```

```python
import contextlib
import numpy as np
import concourse.bass as bass
import concourse.mybir as mybir
from concourse.bass_utils import run_bass_kernel_spmd

F32 = mybir.dt.float32
BF16 = mybir.dt.bfloat16
AF = mybir.ActivationFunctionType
ALU = mybir.AluOpType
AX = mybir.AxisListType

D = 1024
T = 2048
NB = 16
NS = 16
INW = 3856
MLW = 2064
RWW = 1792
DFF = 4096
EPS = 1e-6
GN_EPS = 64e-5
C0 = 0.6065306597126334

OFF = {}
_o = 0
for _n, _w in [("mnw", 512), ("w0", 512), ("a0", 512), ("kk", 512), ("ka", 512), ("rk", 512),
               ("lnw", 512), ("lnb", 512), ("ifb", 16), ("nmw", 8), ("nmlp", 8), ("cw", 32), ("cb", 8),
               ("ident", 128), ("mui", 128), ("mus", 128), ("mls", 128), ("ones", 128),
               ("lsel", 128), ("rsel", 4)]:
    OFF[_n] = (_o, _o + _w)
    _o += _w
NA = _o
SOFF = {}
_o = 0
for _n, _w in [("mu_r", 64), ("mu_k", 64), ("mu_v", 64), ("cwq", 256), ("cwk", 256), ("cbq", 64), ("cbk", 64),
               ("mnw", 64), ("w0", 64), ("a0", 64), ("kk", 64), ("ka", 64), ("rk", 64), ("lnw", 64), ("lnb", 64),
               ("ib", 1), ("fb", 1)]:
    SOFF[_n] = (_o, _o + _w)
    _o += _w
NSP = _o


ALIAS = {"r_sb": "G0", "kf_sb": "G1", "vf": "G2", "wsig": "G3", "a_sb": "G4", "g_sb": "G5", "kap": "G6", "ktl": "G7",
         "bvec": "G8", "e1": "G9", "e2": "G10", "e3": "G11", "pcw": "G12", "ysb": "G11", "tA": "G9", "tB": "G10", "plast": "G9",
         "osig": "G13", "hml": "G14", "nlrep": "G15", "Fb": "G15", "cacc": "G16", "qks": "G16", "tAm": "G18", "tBm": "G19",
         "junk": "xm", "ctmp": "xm", "xsb": "mix", "mixT": "TMbA", "TMb0": "TMbA", "TMb1": "TMbA",
         "TMb2": "TMbB", "TMb3": "TMbB", "Ub": "Zb1", "Ss": "Cs", "lo3": "spj", "smix": "spj",
         "xt1": "xt0", "xnT1": "xnT0", "qkx1": "qkx0"}


class SemCtx:
    def __init__(self, nc):
        self.nc = nc
        self.es = contextlib.ExitStack()
        self.engs = ["pe", "act", "dve", "pool", "sp"]
        self.esem = {e: self.es.enter_context(nc.semaphore("s_" + e)) for e in self.engs}
        self.ecnt = {e: 0 for e in self.engs}
        self.bsem = self.es.enter_context(nc.semaphore("s_bar"))
        self.phase = 0
        self.gsem = {}
        self.gbase = {}

    def group_sem(self, g):
        if g not in self.gsem:
            self.gsem[g] = self.es.enter_context(self.nc.semaphore("g_%d" % len(self.gsem)))
            self.gbase[g] = 0
        return self.gsem[g]

    def close(self):
        self.es.close()


class Prog:
    max_ops = None

    def __init__(self, ctx):
        self.ctx = ctx
        self.nc = ctx.nc
        self.ops = []
        self.last_writer = {}
        self.readers = {}
        self.dma_groups = {}

    def op(self, eng, fn, reads=(), writes=(), dma_group=None, wait_total=False):
        if self.max_ops is not None and len(self.ops) >= self.max_ops:
            return None
        reads = [ALIAS.get(k, k) for k in reads]
        writes = [ALIAS.get(k, k) for k in writes]
        if eng != "pe":
            writes = writes + [k for k in reads if k.startswith("PS") and k not in writes]
        deps = set()
        for b in reads:
            if b in self.last_writer:
                deps.add(self.last_writer[b])
        for b in writes:
            if b in self.last_writer:
                deps.add(self.last_writer[b])
            for r in self.readers.get(b, ()):
                deps.add(r)
        idx = len(self.ops)
        if dma_group is not None:
            deps = {d for d in deps if self.ops[d]["dma"] != dma_group}
        o = dict(eng=eng, fn=fn, deps=sorted(deps), dma=dma_group, idx=idx)
        if dma_group is not None:
            g = self.dma_groups.setdefault(dma_group, dict(total=0, wait_total=wait_total))
            g["total"] += 1
            o["dma_cnt"] = g["total"]
        self.ops.append(o)
        for b in reads:
            self.readers.setdefault(b, []).append(idx)
        for b in writes:
            self.last_writer[b] = idx
            self.readers[b] = []
        return idx

    def finalize(self):
        nc = self.nc
        ctx = self.ctx
        ops = self.ops
        needed = set()
        for o in ops:
            best = {}
            rd = []
            for d in o["deps"]:
                p = ops[d]
                if p["dma"] is not None:
                    rd.append(d)
                else:
                    if p["eng"] == "pe" and o["eng"] == "pe" and o["dma"] is None:
                        continue
                    best[p["eng"]] = max(best.get(p["eng"], -1), d)
            rd.extend(best.values())
            o["deps"] = sorted(rd)
            for d in best.values():
                needed.add(d)
        engs = ctx.engs
        last = {}
        for o in ops:
            if o["dma"] is None:
                last[o["eng"]] = o["idx"]
        needed |= set(last.values())
        cnt = dict(ctx.ecnt)
        for o in ops:
            if o["dma"] is None and o["idx"] in needed:
                cnt[o["eng"]] += 1
                o["sig"] = cnt[o["eng"]]
        for g in self.dma_groups:
            ctx.group_sem(g)
        phase = ctx.phase
        with nc.Block() as block:

            def emit_engine(ename, eng):
                known = {}
                if phase > 0:
                    eng.wait_ge(ctx.bsem, phase)
                for o in ops:
                    if o["eng"] != ename:
                        continue
                    for d in o["deps"]:
                        p = ops[d]
                        if p["dma"] is not None:
                            g = self.dma_groups[p["dma"]]
                            sem = ctx.gsem[p["dma"]]
                            val = ctx.gbase[p["dma"]] + 16 * (g["total"] if g["wait_total"] else p["dma_cnt"])
                            key = ("g", p["dma"])
                        else:
                            if p["eng"] == "pe" and ename == "pe" and o["dma"] is None:
                                continue
                            sem = ctx.esem[p["eng"]]
                            val = p["sig"]
                            key = ("e", p["eng"])
                        if known.get(key, 0) >= val:
                            continue
                        known[key] = val
                        eng.wait_ge(sem, val)
                    ins = o["fn"](eng)
                    if o["dma"] is not None:
                        ins.then_inc(ctx.gsem[o["dma"]], 16)
                    elif "sig" in o:
                        ins.then_inc(ctx.esem[ename], 1)
                if ename == "sp":
                    for e2 in engs:
                        if cnt[e2] > ctx.ecnt[e2]:
                            eng.wait_ge(ctx.esem[e2], cnt[e2])
                    for g, info in self.dma_groups.items():
                        eng.wait_ge(ctx.gsem[g], ctx.gbase[g] + 16 * info["total"])
                    eng.sem_inc(ctx.bsem, 1)

            @block.tensor
            def _(e):
                emit_engine("pe", e)

            @block.scalar
            def _(e):
                emit_engine("act", e)

            @block.vector
            def _(e):
                emit_engine("dve", e)

            @block.gpsimd
            def _(e):
                emit_engine("pool", e)

            @block.sync
            def _(e):
                emit_engine("sp", e)

        ctx.ecnt = cnt
        for g, info in self.dma_groups.items():
            ctx.gbase[g] += 16 * info["total"]
        ctx.phase += 1


STOP_EARLY = True


class _StopBuild(Exception):
    pass


def build_program(do_sample=True, debug=False):
    nc = bass.Bass("TRN2", target_bir_lowering=False)
    try:
        return _build_program(nc, do_sample, debug)
    except _StopBuild:
        return nc


def _build_program(nc, do_sample, debug):
    dbg = nc.dram_tensor("dbg", [128, 16, 512], F32, kind="ExternalOutput").ap() if debug else None
    din = lambda n, s: nc.dram_tensor(n, s, F32, kind="ExternalInput").ap()
    dout = lambda n, s: nc.dram_tensor(n, s, F32, kind="ExternalOutput").ap()
    xp = din("xp", [T, D])
    xs = din("xs", [NS, D])
    w_in = din("w_in", [D, INW])
    w_out = din("w_out", [D, D])
    mlp_up = din("mlp_up", [D, DFF])
    mlp_down = din("mlp_down", [DFF, D])
    packA_d = din("packA", [128, NA])
    mu_d = din("mu_b", [128, RWW])
    nfw_d = din("nfw_b", [128, D])
    wup_d = din("luw", [128, 512])
    gup_d = din("gup", [128, 512])
    spk_d = din("spack", [128, NSP])
    sC_d = din("sC", [128, 4096])
    sn_d = din("sn", [128, 64])
    sm_d = din("sm", [128, 1])
    sconv_d = din("sconv", [128, 2, 3, 64])
    sS_d = din("sS", [128, 4096])
    sshift_d = din("sshift", [128, 3, 64])
    sshl_d = din("sshl", [NS, 256])
    mul_d = din("mul", [NS, 256])

    yp = dout("yp", [T, D])
    ys = dout("ys", [NS, D])
    oC = dout("oC", [128, 4, 65])
    om = dout("om", [8, 1])
    oconv = dout("oconv", [128, 8, 3])
    oS = dout("oS", [128, 4, 64])
    oshift = dout("oshift", [1, RWW])
    osC = dout("osC", [128, 4096])
    osn = dout("osn", [128, 64])
    osm = dout("osm", [128, 1])
    osconv = dout("osconv", [128, 2, 3, 64])
    osS = dout("osS", [128, 4096])
    osshift = dout("osshift", [128, 3, 64])
    osshl = dout("osshl", [NS, 256])

    mix_d = nc.dram_tensor("mix_scr", [T + NS, D], BF16, kind="Internal").ap()
    scr1 = nc.dram_tensor("scr1", [128, 8, 64], F32, kind="Internal").ap()
    scr2 = nc.dram_tensor("scr2", [128, 3, 64], F32, kind="Internal").ap()
    scr3 = nc.dram_tensor("scr3", [NS, 2, 8, 64], F32, kind="Internal").ap()

    ctx = SemCtx(nc)
    PP = [Prog(ctx)]
    es_res = contextlib.ExitStack()
    cur = [es_res]

    def TT(name, shape, dt=F32):
        return cur[0].enter_context(nc.sbuf_tensor("t_" + name, list(shape), dt))

    def dma(q, out, in_, reads, writes, group, wait_total=False):
        PP[0].op(q, lambda e: e.dma_start(out=out, in_=in_), reads=reads, writes=writes, dma_group=group, wait_total=wait_total)

    def dve(fn, r, w):
        PP[0].op("dve", fn, reads=r, writes=w)

    def act(fn, r, w):
        PP[0].op("act", fn, reads=r, writes=w)

    def pool(fn, r, w):
        PP[0].op("pool", fn, reads=r, writes=w)

    def pe(fn, r, w):
        PP[0].op("pe", fn, reads=r, writes=w)

    def mm(out, lhsT, rhs, start, stop, r, w):
        pe(lambda e: e.matmul(out, lhsT=lhsT, rhs=rhs, start=start, stop=stop), r, w)

    es_ps = contextlib.ExitStack()
    PS = [es_ps.enter_context(nc.psum_tensor("PS%d" % i, [128, 1024], F32)) for i in range(4)]
    PSB = [p.bitcast(BF16) for p in PS]
    psi = [0]

    def nps():
        i = psi[0] % 4
        psi[0] += 1
        return PS[i], PSB[i], "PS%d" % i

    wq = TT("wq", [128, 8, INW], BF16)
    mub = TT("mub", [128, RWW], F32)
    luw = TT("luw", [128, 512], BF16)
    gup = TT("gup", [128, 512], BF16)
    pA = TT("pA", [128, NA], F32)
    identb = TT("identb", [128, 128], BF16)

    def PA(n):
        a, b = OFF[n]
        return pA[:, a:b]

    w_in_v = w_in.rearrange("(c p) n -> p c n", p=128)
    dma("sp", pA[:], packA_d, [], ["pA"], "init", True)
    for c in range(8):
        dma("pool", wq[:, c, :], w_in_v[:, c, :], [], ["wq"], "init", True)
    dma("pool", luw[:], wup_d, [], ["luw"], "init", True)
    dma("pool", gup[:], gup_d, [], ["gup"], "init", True)
    w_out_v = w_out.rearrange("(c p) n -> p c n", p=128)
    dve(lambda e: e.tensor_copy(out=identb[:], in_=PA("ident")), ["pA"], ["identb"])


    def _dbgdump(tag):
        if debug != tag:
            return
        dstg_ = cur[0].enter_context(nc.sbuf_tensor("t_dbgst%d" % tag, [128, 512], F32))
        def dd(slot, ap, key, n):
            dve(lambda e: e.tensor_copy(out=dstg_[:, 0:n], in_=ap), [key], ["dbgst"])
            dma("sp", dbg[:, slot, 0:n], dstg_[:, 0:n], ["dbgst"], [], "dbg")
        dd(0, PA("ident"), "pA", 128)
        dd(1, PA("mui"), "pA", 128)
        dd(2, PA("mnw"), "pA", 512)
        dd(3, PA("w0"), "pA", 512)
        dd(4, PA("lnb"), "pA", 512)
        PP[0].max_ops = len(PP[0].ops)
        PP[0].finalize()
        raise _StopBuild()
    _dbgdump(3)
    dma("sp", mub[:], mu_d, [], ["mub"], "init", True)
    PP[0].finalize()
    PP[0] = Prog(ctx)

    def rmsnorm_T(xt, xk, nt, dstT, dstk, col0, wname, tmpb, tmpbk, junk, junkk, st, stk):
        act(lambda e: e.activation(out=junk[:nt, :], in_=xt[:nt, :], func=AF.Square, accum_out=st[:nt, 0:1]), [xk], [junkk, stk])
        act(lambda e: e.activation(out=st[:nt, 1:2], in_=st[:nt, 0:1], func=AF.Sqrt, bias=EPS, scale=1.0 / D), [stk], [stk])
        dve(lambda e: e.reciprocal(out=st[:nt, 2:3], in_=st[:nt, 1:2]), [stk], [stk])
        dve(lambda e: e.tensor_scalar_mul(out=tmpb[:nt, :], in0=xt[:nt, :], scalar1=st[:nt, 2:3]), [xk, stk], [tmpbk])
        ps, psb, pk = nps()
        for c in range(8):
            pe(lambda e, c=c: e.transpose(psb[:, c * 128:c * 128 + nt], tmpb[:nt, c * 128:(c + 1) * 128], identb[:nt, :nt]), [tmpbk, "identb"], [pk])
        a, b_ = OFF[wname]
        dve(lambda e: e.tensor_tensor(out=dstT[:, :, col0:col0 + nt],
                                      in0=psb[:, 0:1024].rearrange("p (c t) -> p c t", c=8)[:, :, 0:nt],
                                      in1=pA[:, a:b_].unsqueeze(2).to_broadcast([128, 8, nt]), op=ALU.mult), [pk, "pA"], [dstk])

    def head_ml(nt, hsrc, hk, osig, ok, mix, mixk, W, sfx=""):
        tA, tB, s8 = W["tA"], W["tB"], W["s8"]
        h3 = lambda t: t[:nt, :].rearrange("p (h d) -> p h d", h=8)
        bc = lambda t, c: t[:nt, c:c + 8].unsqueeze(2).to_broadcast([nt, 8, 64])
        dve(lambda e: e.tensor_tensor(out=tA[:nt, :], in0=hsrc[:nt, :], in1=osig[:nt, :], op=ALU.mult), [hk, ok], ["tA" + sfx])
        dve(lambda e: e.tensor_tensor(out=tB[:nt, :], in0=tA[:nt, :], in1=tA[:nt, :], op=ALU.mult), ["tA" + sfx], ["tB" + sfx])
        dve(lambda e: e.tensor_reduce(out=s8[:nt, 0:8], in_=h3(tB), axis=AX.X, op=ALU.add), ["tB" + sfx], ["s8" + sfx])
        act(lambda e: e.activation(out=s8[:nt, 8:16], in_=s8[:nt, 0:8], func=AF.Sqrt, bias=EPS, scale=1.0 / 64), ["s8" + sfx], ["s8" + sfx])
        dve(lambda e: e.reciprocal(out=s8[:nt, 16:24], in_=s8[:nt, 8:16]), ["s8" + sfx], ["s8" + sfx])
        dve(lambda e: e.tensor_tensor(out=h3(tB), in0=h3(tA), in1=bc(s8, 16), op=ALU.mult), ["tA" + sfx, "s8" + sfx], ["tB" + sfx])
        dve(lambda e: e.tensor_tensor(out=mix[:nt, 0:512], in0=tB[:nt, :], in1=PA("mnw")[:nt, :], op=ALU.mult), ["tB" + sfx, "pA"], [mixk])

    def head_rw(nt, ysrc, yk, bon, bonk, vf, vfk, g, gk, mix, mixk, W):
        tA, tB, s8 = W["tA"], W["tB"], W["s8"]
        h3 = lambda t: t[:nt, :].rearrange("p (h d) -> p h d", h=8)
        bc = lambda t, c: t[:nt, c:c + 8].unsqueeze(2).to_broadcast([nt, 8, 64])
        dve(lambda e: e.tensor_tensor(out=h3(tA), in0=h3(vf), in1=bc(bon, 0), op=ALU.mult), [vfk, bonk], ["tA"])
        dve(lambda e: e.tensor_tensor(out=tA[:nt, :], in0=tA[:nt, :], in1=ysrc[:nt, :], op=ALU.add), ["tA", yk], ["tA"])
        dve(lambda e: e.tensor_reduce(out=s8[:nt, 24:32], in_=h3(tA), axis=AX.X, op=ALU.add), ["tA"], ["s8"])
        dve(lambda e: e.tensor_scalar_mul(out=s8[:nt, 24:32], in0=s8[:nt, 24:32], scalar1=1.0 / 64), ["s8"], ["s8"])
        dve(lambda e: e.tensor_tensor(out=h3(tA), in0=h3(tA), in1=bc(s8, 24), op=ALU.subtract), ["tA", "s8"], ["tA"])
        dve(lambda e: e.tensor_tensor(out=tB[:nt, :], in0=tA[:nt, :], in1=tA[:nt, :], op=ALU.mult), ["tA"], ["tB"])
        dve(lambda e: e.tensor_reduce(out=s8[:nt, 32:40], in_=h3(tB), axis=AX.X, op=ALU.add), ["tB"], ["s8"])
        act(lambda e: e.activation(out=s8[:nt, 40:48], in_=s8[:nt, 32:40], func=AF.Sqrt, bias=GN_EPS, scale=1.0 / 64), ["s8"], ["s8"])
        dve(lambda e: e.reciprocal(out=s8[:nt, 48:56], in_=s8[:nt, 40:48]), ["s8"], ["s8"])
        dve(lambda e: e.tensor_tensor(out=h3(tB), in0=h3(tA), in1=bc(s8, 48), op=ALU.mult), ["tA", "s8"], ["tB"])
        dve(lambda e: e.tensor_tensor(out=tB[:nt, :], in0=tB[:nt, :], in1=PA("lnw")[:nt, :], op=ALU.mult), ["tB", "pA"], ["tB"])
        dve(lambda e: e.tensor_tensor(out=tB[:nt, :], in0=tB[:nt, :], in1=PA("lnb")[:nt, :], op=ALU.add), ["tB", "pA"], ["tB"])
        dve(lambda e: e.tensor_tensor(out=mix[:nt, 512:1024], in0=tB[:nt, :], in1=g[:nt, :], op=ALU.mult), ["tB", gk], [mixk])

    def out_proj(nt, mix, mixk, xt, xk, row0, W):
        dma("pool", mix_d[row0:row0 + nt, :], mix[:nt, :], [mixk], [], "mixst")

    with contextlib.ExitStack() as es1:
        cur[0] = es1
        W = {}
        Gbig = TT("Gbig", [128, 20, 512])
        G = [Gbig[:, i, :] for i in range(20)]
        W["s8"] = TT("s8", [128, 64])
        W["xm"] = TT("xm", [128, D])
        xt = [TT("xt0", [128, D])] * 2
        junk = W["xm"]
        st = TT("st", [128, 4])
        mix = TT("mix", [128, D], BF16)
        xsb = mix
        xnT = [TT("xnT0", [128, 8, 129], BF16)] * 2
        dxT = TT("dxT", [128, 8, 128], BF16)
        qkx = [TT("qkx0", [128, 8, 131])] * 2
        cacc = Gbig[:, 16:18, :].rearrange("p a (c t) -> p (a c) t", t=128)
        ctmp = W["xm"][:, :].rearrange("p (c t) -> p c t", c=8)
        qks = cacc
        qpb = TT("qpb", [128, 4, 128], BF16)
        kTb = TT("kTb", [128, 4, 128], BF16)
        ktm = TT("ktm", [128, 8, 64], BF16)
        vaug = TT("vaug", [128, 8, 65], BF16)
        gt = TT("gt", [128, 96])
        runmax = TT("runmax", [128, 8])
        nBc = TT("nBc", [128, 8])
        Cst = TT("Cst", [128, 4, 65])
        Cbf = TT("Cbf", [128, 4, 65], BF16)
        r_sb, kf_sb, vf = G[0], G[1], G[2]
        Fb = G[15].rearrange("p (j t) -> p j t", j=4)
        osig, hml = G[13], G[14]
        nlrep = G[15].rearrange("p (h d) -> p h d", h=8)
        wsig, a_sb, g_sb, kap, ktl, bvec, e1, e2, e3, pcw, ysb = G[3], G[4], G[5], G[6], G[7], G[8], G[9], G[10], G[11], G[12], G[11]
        W["tA"], W["tB"] = G[9], G[10]
        WM = {"tA": G[18], "tB": G[19], "s8": TT("s8m", [128, 64])}
        r8b = TT("r8b", [128, 8])
        vrw = TT("vrw", [128, 8, 64], BF16)
        lor = TT("lor", [128, 256], BF16)
        lorT = TT("lorT", [128, 2, 128], BF16)
        r8 = TT("r8", [128, 32])
        bon = TT("bon", [128, 8])
        TMb = TT("TMb", [128, 4, 512], BF16)
        W["mixT"] = TMb[:, 0:2, :].rearrange("p a (c t) -> p (a c) t", t=128)
        Btz = TT("Btz", [128, 8, 128], BF16)
        Ktz = TT("Ktz", [128, 8, 128], BF16)
        FMt = TT("FMt", [128, 4, 4, 128], BF16)
        Am = [TT("Am%d" % i, [128, 8, 128], BF16) for i in range(3)]
        PTb = TT("PTb", [128, 8, 128], BF16)
        Pw = [TT("Pw%d" % i, [128, 8, 128], BF16) for i in range(4)]
        Zb = [TT("Zb%d" % i, [128, 8, 64], BF16) for i in range(2)]
        Ub = Zb[1]
        Sst = TT("Sst", [128, 4, 64])
        Sbf = TT("Sbf", [128, 4, 64], BF16)
        WLfm = TT("WLfm", [128, 4])
        plast = G[9]

        pool(lambda e: e.memset(vaug[:], 1.0), [], ["vaug"])
        pool(lambda e: e.memset(Btz[:], 0.0), [], ["Btz"])
        pool(lambda e: e.memset(Ktz[:], 0.0), [], ["Ktz"])
        pool(lambda e: e.memset(Cst[:], 0.0), [], ["Cst"])
        pool(lambda e: e.memset(Cbf[:], 0.0), [], ["Cbf"])
        pool(lambda e: e.memset(Sst[:], 0.0), [], ["Sst"])
        pool(lambda e: e.memset(Sbf[:], 0.0), [], ["Sbf"])
        pool(lambda e: e.memset(runmax[:], -1e30), [], ["runmax"])
        pool(lambda e: e.memset(nBc[:], 0.0), [], ["nBc"])
        pool(lambda e: e.memset(xnT[0][:, :, 0:1], 0.0), [], ["xnT0"])
        pool(lambda e: e.memset(qkx[0][:, :, 0:3], 0.0), [], ["qkx0"])

        if debug == 2:
            dstg = TT("dbgstage", [128, 512]) if False else G[12]
            def ddump0(slot, ap, key, n):
                dve(lambda e: e.tensor_copy(out=dstg[:, 0:n], in_=ap), [key], ["pcw"])
                dma("sp", dbg[:, slot, 0:n], dstg[:, 0:n], ["pcw"], [], "dbg")
            ddump0(0, PA("ident"), "pA", 128)
            ddump0(1, PA("mui"), "pA", 128)
            ddump0(2, luw[:, :], "luw", 512)
            ddump0(3, gup[:, :], "gup", 512)
            ddump0(4, W1[:, 0, 0:512], "W1", 512)
            PP[0].max_ops = len(PP[0].ops)
            if STOP_EARLY:
                PP[0].finalize()
                raise _StopBuild()
        MUI = PA("mui")
        MUS = PA("mus")
        MLS = PA("mls")
        ONES = PA("ones")
        IDF = PA("ident")
        bc8 = lambda ap: ap.unsqueeze(2).to_broadcast([128, 8, 64])
        m8 = lambda m: m.unsqueeze(1).to_broadcast([128, 8, 128])
        v3 = lambda t: t[:].rearrange("p (h d) -> p h d", h=8)
        hoff = lambda h: (h % 2) * 512 + (h // 2) * 128

        for b in range(NB):
            x_ = xt[b % 2]
            xk = "xt%d" % (b % 2)
            xn = xnT[b % 2]
            xnk = "xnT%d" % (b % 2)
            qx = qkx[b % 2]
            qxk = "qkx%d" % (b % 2)
            dma("sp", x_[:], xp[b * 128:(b + 1) * 128, :], [], [xk], xk)
            rmsnorm_T(x_, xk, 128, xn, xnk, 1, "nmw", xsb, "xsb", junk, "junk", st, "st")
            cur_x = xn[:, :, 1:129]
            prv_x = xn[:, :, 0:128]
            dve(lambda e, xn=xn: e.tensor_tensor(out=dxT[:], in0=xn[:, :, 0:128], in1=xn[:, :, 1:129], op=ALU.subtract), [xnk], ["dxT"])

            ps, psb, pk = nps()
            for j in range(8):
                for c in range(8):
                    mm(ps[:, j * 128:(j + 1) * 128], wq[:, c, j * 128:(j + 1) * 128], cur_x[:, c, :], c == 0, c == 7, ["wq", xnk], [pk])
            for a_ in range(2):
                act(lambda e, ps=ps, qx=qx, a_=a_: e.copy(out=qx[:, 4 * a_:4 * a_ + 4, 3:131], in_=ps[:, a_ * 512:(a_ + 1) * 512].rearrange("p (j t) -> p j t", j=4)), [pk], [qxk])
            if b == NB - 1:
                dma("pool", oconv, qx[:, :, 128:131], [qxk], [], "fin")

            def tm_plain(col0, ncol, ps_ap, pk):
                for c in range(8):
                    mm(ps_ap, cur_x[:, c, :], wq[:, c, col0:col0 + ncol], c == 0, c == 7, [xnk, "wq"], [pk])

            def tm_shift(col0, ncol, dst, dk):
                ps, psb, pk = nps()
                for c in range(8):
                    mm(ps[:, 0:ncol], cur_x[:, c, :], wq[:, c, MLW + col0:MLW + col0 + ncol], c == 0, c == 7, [xnk, "wq"], [pk])
                for c in range(8):
                    mm(ps[:, 512:512 + ncol], dxT[:, c, :], wq[:, c, MLW + col0:MLW + col0 + ncol], c == 0, c == 7, ["dxT", "wq"], [pk])
                dve(lambda e, ps=ps: e.tensor_tensor(out=dst, in0=ps[:, 512:512 + ncol], in1=mub[:, col0:col0 + ncol], op=ALU.mult), [pk, "mub"], [dk])
                dve(lambda e, ps=ps: e.tensor_tensor(out=dst, in0=dst, in1=ps[:, 0:ncol], op=ALU.add), [pk, dk], [dk])

            ps, psb, pk = nps()
            tm_plain(1024, 512, ps[:, 0:512], pk)
            tm_plain(1536, 512, ps[:, 512:1024], pk)
            act(lambda e, ps=ps: e.copy(out=vaug[:, :, 0:64], in_=ps[:, 0:512].rearrange("p (h d) -> p h d", h=8)), [pk], ["vaug"])
            act(lambda e, ps=ps: e.activation(out=osig[:], in_=ps[:, 512:1024], func=AF.Sigmoid), [pk], ["osig"])
            ps, psb, pk = nps()
            tm_plain(2048, 16, ps[:, 0:16], pk)
            dve(lambda e, ps=ps: e.tensor_tensor(out=gt[:, 0:16], in0=ps[:, 0:16], in1=PA("ifb"), op=ALU.add), [pk, "pA"], ["gt"])
            tm_shift(0, 512, r_sb[:], "r_sb")
            tm_shift(512, 512, kf_sb[:], "kf_sb")
            tm_shift(1024, 512, vf[:], "vf")
            pool(lambda e: e.tensor_copy(out=vrw[:], in_=vf[:].rearrange("p (h d) -> p h d", h=8)), ["vf"], ["vrw"])
            ltmp = G[18]
            tm_shift(1536, 256, ltmp[:, 0:256], "tAm")
            act(lambda e: e.activation(out=lor[:, 0:64], in_=ltmp[:, 0:64], func=AF.Tanh), ["tAm"], ["lor"])
            act(lambda e: e.copy(out=lor[:, 64:128], in_=ltmp[:, 64:128]), ["tAm"], ["lor"])
            act(lambda e: e.activation(out=lor[:, 128:256], in_=ltmp[:, 128:256], func=AF.Sigmoid), ["tAm"], ["lor"])
            if b == NB - 1:
                lastc = xn[:, :, 128:129]
                for n0 in range(0, RWW, 512):
                    nn = min(512, RWW - n0)
                    ps2, _, pk2 = nps()
                    for c in range(8):
                        mm(ps2[0:1, 0:nn], lastc[:, c, :], wq[:, c, MLW + n0:MLW + n0 + nn], c == 0, c == 7, [xnk, "wq"], [pk2])
                    act(lambda e, ps2=ps2, n0=n0, nn=nn: e.copy(out=plast[0:1, 0:nn], in_=ps2[0:1, 0:nn]), [pk2], ["plast"])
                    dma("pool", oshift[:, n0:n0 + nn], plast[0:1, 0:nn], ["plast"], [], "fin")

            def ml_stage():
                act(lambda e: e.activation(out=gt[:, 56:64], in_=gt[:, 8:16], func=AF.Exp, scale=-1.0), ["gt"], ["gt"])
                act(lambda e: e.activation(out=gt[:, 16:24], in_=gt[:, 56:64], func=AF.Ln, bias=1.0, scale=1.0), ["gt"], ["gt"])
                dve(lambda e: e.tensor_copy(out=nlrep[:], in_=bc8(gt[:, 16:24])), ["gt"], ["nlrep"])
                ps, psb, pk = nps()
                mm(ps[:, 0:8], MUI, gt[:, 16:24], True, True, ["pA", "gt"], [pk])
                mm(ps[:, 8:16], ONES, gt[:, 16:24], True, True, ["pA", "gt"], [pk])
                for j in range(4):
                    mm(ps[:, 512 + j * 128:512 + (j + 1) * 128], nlrep[:, 2 * j:2 * j + 2, :].rearrange("p a d -> p (a d)"), MUI, True, True, ["nlrep", "pA"], [pk])
                dve(lambda e, ps=ps: e.tensor_tensor(out=gt[:, 24:32], in0=ps[:, 0:8], in1=gt[:, 0:8], op=ALU.add), [pk, "gt"], ["gt"])
                act(lambda e: e.activation(out=gt[:, 32:40], in_=gt[:, 24:32], func=AF.Exp), ["gt"], ["gt"])
                dve(lambda e, ps=ps: e.tensor_tensor(out=gt[:, 56:64], in0=gt[:, 24:32], in1=ps[:, 8:16], op=ALU.subtract), [pk, "gt"], ["gt"])
                act(lambda e: e.activation(out=gt[:, 40:48], in_=gt[:, 56:64], func=AF.Exp), ["gt"], ["gt"])
                act(lambda e, ps=ps: e.activation(out=gt[:, 48:56], in_=ps[:, 8:16], func=AF.Exp, scale=-1.0), [pk], ["gt"])
                act(lambda e, ps=ps: e.activation(out=Fb[:], in_=ps[:, 512:1024].rearrange("p (j t) -> p j t", j=4), func=AF.Exp, scale=-1.0), [pk], ["Fb"])
                dve(lambda e: e.tensor_tensor(out=gt[:, 56:64], in0=gt[:, 24:32], in1=nBc[:], op=ALU.add), ["gt", "nBc"], ["gt"])
                dve(lambda e: e.tensor_tensor(out=runmax[:], in0=runmax[:], in1=gt[:, 56:64], op=ALU.max), ["gt", "runmax"], ["runmax"])
                dve(lambda e, ps=ps: e.tensor_tensor(out=nBc[:], in0=nBc[:], in1=ps[:, 8:16], op=ALU.add), [pk, "nBc"], ["nBc"])

                yield
                cwv = PA("cw").rearrange("p (c j) -> p c j", j=4)
                wbc = lambda j: cwv[:, :, j:j + 1].to_broadcast([128, 8, 128])
                pool(lambda e, qx=qx: e.tensor_tensor(out=cacc[:], in0=qx[:, :, 3:131], in1=wbc(3), op=ALU.mult), [qxk, "pA"], ["cacc"])
                for j in range(3):
                    pool(lambda e, qx=qx, j=j: e.tensor_tensor(out=ctmp[:], in0=qx[:, :, j:j + 128], in1=wbc(j), op=ALU.mult), [qxk, "pA"], ["ctmp"])
                    pool(lambda e: e.tensor_tensor(out=cacc[:], in0=cacc[:], in1=ctmp[:], op=ALU.add), ["cacc", "ctmp"], ["cacc"])
                pool(lambda e: e.tensor_tensor(out=cacc[:], in0=cacc[:], in1=PA("cb").unsqueeze(2).to_broadcast([128, 8, 128]), op=ALU.add), ["cacc", "pA"], ["cacc"])
                act(lambda e: e.activation(out=qks[:], in_=cacc[:], func=AF.Silu), ["cacc"], ["qks"])
                dve(lambda e: e.tensor_tensor(out=qpb[:], in0=qks[:, 0:4, :], in1=Fb[:], op=ALU.mult), ["qks", "Fb"], ["qpb"])
                act(lambda e: e.activation(out=kTb[:], in_=qks[:, 4:8, :], func=AF.Copy, scale=0.125), ["qks"], ["kTb"])

                yield
                ps, psb, pk = nps()
                for j in range(4):
                    pe(lambda e, j=j, psb=psb: e.transpose(psb[:, j * 128:(j + 1) * 128], kTb[:, j, :], identb[:]), ["kTb", "identb"], [pk])
                dve(lambda e, psb=psb: e.tensor_tensor(out=ktm[:], in0=psb[:, 0:512].rearrange("p (h d) -> p h d", h=8), in1=bc8(gt[:, 40:48]), op=ALU.mult), [pk, "gt"], ["ktm"])

                yield
                ps, psb, pk = nps()
                for h in range(8):
                    j, hp = h // 2, h % 2
                    sl = slice(hp * 64, hp * 64 + 64)
                    mm(ps[:, hoff(h):hoff(h) + 128], kTb[sl, j, :], qpb[sl, j, :], True, True, ["kTb", "qpb"], [pk])
                for h in range(8):
                    dve(lambda e, h=h, ps=ps: e.scalar_tensor_tensor(out=PTb[:, h, :], in0=ps[:, hoff(h):hoff(h) + 128], scalar=gt[:, 32 + h:33 + h], in1=MUI, op0=ALU.mult, op1=ALU.mult), [pk, "gt", "pA"], ["PTb"])
                yield
                ps, psb, pk = nps()
                psn = lambda ps, h: ps[:, (h // 4) * 512 + (h % 4) * 65:(h // 4) * 512 + (h % 4) * 65 + 65]
                for h in range(8):
                    j, hp = h // 2, h % 2
                    sl = slice(hp * 64, hp * 64 + 64)
                    mm(psn(ps, h), PTb[:, h, :], vaug[:, h, :], True, False, ["PTb", "vaug"], [pk])
                    mm(psn(ps, h), qpb[sl, j, :], Cbf[sl, j, :], False, True, ["qpb", "Cbf"], [pk])
                pn4 = ps[:, :].rearrange("p (a r) -> p a r", a=2)[:, :, 0:260].rearrange("p a (h d) -> p a h d", h=4)
                for a_ in range(2):
                    act(lambda e, pn4=pn4, a_=a_: e.copy(out=r8[:, 4 * a_:4 * a_ + 4], in_=pn4[:, a_, :, 64]), [pk], ["r8"])
                dve(lambda e: e.scalar_tensor_tensor(out=r8[:, 8:16], in0=r8[:, 0:8], scalar=-1.0, in1=r8[:, 0:8], op0=ALU.mult, op1=ALU.max), ["r8"], ["r8"])
                dve(lambda e: e.tensor_scalar_max(out=r8[:, 8:16], in0=r8[:, 8:16], scalar1=1.0), ["r8"], ["r8"])
                dve(lambda e: e.reciprocal(out=r8[:, 16:24], in_=r8[:, 8:16]), ["r8"], ["r8"])
                for a_ in range(2):
                    dve(lambda e, pn4=pn4, a_=a_: e.tensor_tensor(out=hml[:, a_ * 256:(a_ + 1) * 256].rearrange("p (h d) -> p h d", h=4), in0=pn4[:, a_, :, 0:64],
                                                              in1=r8[:, 16 + 4 * a_:20 + 4 * a_].unsqueeze(2).to_broadcast([128, 4, 64]), op=ALU.mult), [pk, "r8"], ["hml"])
                yield
                ps, psb, pk = nps()
                for h in range(8):
                    j = h // 2
                    mm(psn(ps, h), ktm[:, 2 * j:2 * j + 2, :].rearrange("p a d -> p (a d)"), vaug[:, h, :], True, True, ["ktm", "vaug"], [pk])
                pu4 = ps[:, :].rearrange("p (a r) -> p a r", a=2)[:, :, 0:260].rearrange("p a (h d) -> p a h d", h=4)
                for hp in range(2):
                    sl = slice(hp * 64, hp * 64 + 64)
                    decb = gt[sl, 48:56].rearrange("p (j q) -> p j q", q=2)[:, :, hp:hp + 1].to_broadcast([64, 4, 65])
                    dve(lambda e, sl=sl, decb=decb: e.tensor_tensor(out=Cst[sl, :, :], in0=Cst[sl, :, :], in1=decb, op=ALU.mult), ["Cst", "gt"], ["Cst"])
                    for a in range(2):
                        src = pu4[sl, a, hp::2, :]
                        dve(lambda e, sl=sl, a=a, src=src: e.tensor_tensor(out=Cst[sl, 2 * a:2 * a + 2, :], in0=Cst[sl, 2 * a:2 * a + 2, :], in1=src, op=ALU.add), [pk, "Cst"], ["Cst"])
                act(lambda e: e.copy(out=Cbf[:], in_=Cst[:]), ["Cst"], ["Cbf"])
                head_ml(128, hml, "hml", osig, "osig", mix, "mix", WM, "m")


                yield
            def rw_stage():
                ps, psb, pk = nps()
                pe(lambda e, psb=psb: e.transpose(psb[:, 0:128], lor[:, 0:128], identb[:]), ["lor", "identb"], [pk])
                pe(lambda e, psb=psb: e.transpose(psb[:, 128:256], lor[:, 128:256], identb[:]), ["lor", "identb"], [pk])
                act(lambda e, psb=psb: e.copy(out=lorT[:], in_=psb[:, 0:256].rearrange("p (a t) -> p a t", a=2)), [pk], ["lorT"])
                ps, psb, pk = nps()
                mm(ps[:, 0:512], lorT[0:64, 0, :], luw[0:64, :], True, True, ["lorT", "luw"], [pk])
                mm(ps[:, 512:1024], lorT[64:128, 0, :], luw[64:128, :], True, True, ["lorT", "luw"], [pk])
                dve(lambda e, ps=ps: e.tensor_tensor(out=e1[:], in0=ps[:, 0:512], in1=PA("w0"), op=ALU.add), [pk, "pA"], ["e1"])
                act(lambda e: e.activation(out=wsig[:], in_=e1[:], func=AF.Sigmoid), ["e1"], ["wsig"])
                dve(lambda e, ps=ps: e.tensor_tensor(out=e2[:], in0=ps[:, 512:1024], in1=PA("a0"), op=ALU.add), [pk, "pA"], ["e2"])
                act(lambda e: e.activation(out=a_sb[:], in_=e2[:], func=AF.Sigmoid), ["e2"], ["a_sb"])
                ps, psb, pk = nps()
                mm(ps[:, 0:512], lorT[:, 1, :], gup[:, :], True, True, ["lorT", "gup"], [pk])
                act(lambda e, ps=ps: e.copy(out=g_sb[:], in_=ps[:, 0:512]), [pk], ["g_sb"])
                yield
                dve(lambda e: e.tensor_tensor(out=e1[:], in0=kf_sb[:], in1=PA("kk"), op=ALU.mult), ["kf_sb", "pA"], ["e1"])
                dve(lambda e: e.tensor_tensor(out=e2[:], in0=e1[:], in1=e1[:], op=ALU.mult), ["e1"], ["e2"])
                dve(lambda e: e.tensor_reduce(out=r8b[:, 0:8], in_=v3(e2), axis=AX.X, op=ALU.add), ["e2"], ["r8b"])
                dve(lambda e: e.tensor_scalar_max(out=r8b[:, 0:8], in0=r8b[:, 0:8], scalar1=1e-24), ["r8b"], ["r8b"])
                act(lambda e: e.activation(out=r8b[:, 0:8], in_=r8b[:, 0:8], func=AF.Sqrt), ["r8b"], ["r8b"])
                dve(lambda e: e.reciprocal(out=r8b[:, 0:8], in_=r8b[:, 0:8]), ["r8b"], ["r8b"])
                dve(lambda e: e.tensor_tensor(out=v3(kap), in0=v3(e1), in1=bc8(r8b[:, 0:8]), op=ALU.mult), ["e1", "r8b"], ["kap"])
                dve(lambda e: e.tensor_scalar_add(out=e2[:], in0=a_sb[:], scalar1=-1.0), ["a_sb"], ["e2"])
                dve(lambda e: e.tensor_tensor(out=e2[:], in0=e2[:], in1=PA("ka"), op=ALU.mult), ["e2", "pA"], ["e2"])
                dve(lambda e: e.tensor_tensor(out=e2[:], in0=e2[:], in1=kf_sb[:], op=ALU.mult), ["e2", "kf_sb"], ["e2"])
                dve(lambda e: e.tensor_tensor(out=ktl[:], in0=e2[:], in1=kf_sb[:], op=ALU.add), ["e2", "kf_sb"], ["ktl"])
                dve(lambda e: e.tensor_tensor(out=bvec[:], in0=a_sb[:], in1=kap[:], op=ALU.mult), ["a_sb", "kap"], ["bvec"])
                dve(lambda e: e.tensor_tensor(out=e2[:], in0=r_sb[:], in1=ktl[:], op=ALU.mult), ["r_sb", "ktl"], ["e2"])
                dve(lambda e: e.tensor_tensor(out=e2[:], in0=e2[:], in1=PA("rk"), op=ALU.mult), ["e2", "pA"], ["e2"])
                dve(lambda e: e.tensor_reduce(out=bon[:], in_=v3(e2), axis=AX.X, op=ALU.add), ["e2"], ["bon"])
                yield
                ps, psb, pk = nps()
                mm(ps[:, 0:512], MUI, wsig[:], True, True, ["pA", "wsig"], [pk])
                mm(ps[:, 512:1024], ONES, wsig[:], True, True, ["pA", "wsig"], [pk])
                act(lambda e, ps=ps: e.copy(out=pcw[:], in_=ps[:, 0:512]), [pk], ["pcw"])
                dve(lambda e: e.tensor_tensor(out=e1[:], in0=pcw[:], in1=wsig[:], op=ALU.subtract), ["pcw", "wsig"], ["e1"])
                act(lambda e: e.activation(out=e1[:], in_=e1[:], func=AF.Exp, scale=-C0), ["e1"], ["e1"])
                dve(lambda e: e.tensor_tensor(out=TMb[:, 0, :], in0=kap[:], in1=e1[:], op=ALU.mult), ["kap", "e1"], ["TMb0"])
                act(lambda e: e.activation(out=e2[:], in_=pcw[:], func=AF.Exp, scale=-C0), ["pcw"], ["e2"])
                dve(lambda e: e.tensor_tensor(out=TMb[:, 1, :], in0=r_sb[:], in1=e2[:], op=ALU.mult), ["r_sb", "e2"], ["TMb1"])
                act(lambda e: e.activation(out=e3[:], in_=pcw[:], func=AF.Exp, scale=C0), ["pcw"], ["e3"])
                dve(lambda e: e.tensor_tensor(out=TMb[:, 2, :], in0=bvec[:], in1=e3[:], op=ALU.mult), ["bvec", "e3"], ["TMb2"])
                dve(lambda e: e.tensor_tensor(out=TMb[:, 3, :], in0=ktl[:], in1=e3[:], op=ALU.mult), ["ktl", "e3"], ["TMb3"])
                dve(lambda e, ps=ps: e.tensor_tensor(out=e1[:], in0=ps[:, 512:1024], in1=pcw[:], op=ALU.subtract), [pk, "pcw"], ["e1"])
                act(lambda e: e.activation(out=e1[:], in_=e1[:], func=AF.Exp, scale=-C0), ["e1"], ["e1"])
                for hp in range(2):
                    srcb = v3(bvec).rearrange("p (j q) d -> p j q d", q=2)[:, :, hp, :]
                    srck = v3(ktl).rearrange("p (j q) d -> p j q d", q=2)[:, :, hp, :]
                    wl = v3(e1).rearrange("p (j q) d -> p j q d", q=2)[:, :, hp, :]
                    dstb = Btz[:].rearrange("p (j q) c -> p j q c", q=2)[:, :, hp, hp * 64:hp * 64 + 64]
                    dstk = Ktz[:].rearrange("p (j q) c -> p j q c", q=2)[:, :, hp, hp * 64:hp * 64 + 64]
                    dve(lambda e, srcb=srcb, wl=wl, dstb=dstb: e.tensor_tensor(out=dstb, in0=srcb, in1=wl, op=ALU.mult), ["bvec", "e1"], ["Btz"])
                    dve(lambda e, srck=srck, wl=wl, dstk=dstk: e.tensor_tensor(out=dstk, in0=srck, in1=wl, op=ALU.mult), ["ktl", "e1"], ["Ktz"])
                ps2, _, pk2 = nps()
                for j in range(4):
                    mm(ps2[:, j:j + 1], wsig[:, j * 128:(j + 1) * 128], ONES[:, 0:1], True, True, ["wsig", "pA"], [pk2])
                act(lambda e, ps2=ps2: e.activation(out=WLfm[:], in_=ps2[:, 0:4], func=AF.Exp, scale=-C0), [pk2], ["WLfm"])
                yield
                ps, psb, pk = nps()
                for w_ in range(4):
                    for j in range(4):
                        pe(lambda e, w_=w_, j=j, psb=psb: e.transpose(psb[:, (w_ * 4 + j) * 128:(w_ * 4 + j + 1) * 128], TMb[:, w_, j * 128:(j + 1) * 128], identb[:]), ["TMb%d" % w_, "identb"], [pk])
                for w_ in range(4):
                    eng_ = act if w_ % 2 == 0 else dve
                    if w_ % 2 == 0:
                        act(lambda e, psb=psb, w_=w_: e.copy(out=FMt[:, w_, :, :], in_=psb[:, w_ * 512:(w_ + 1) * 512].rearrange("p (j t) -> p j t", j=4)), [pk], ["FMt"])
                    else:
                        dve(lambda e, psb=psb, w_=w_: e.tensor_copy(out=FMt[:, w_, :, :], in_=psb[:, w_ * 512:(w_ + 1) * 512].rearrange("p (j t) -> p j t", j=4)), [pk], ["FMt"])
                KAP, RB, BB, KKB = 0, 1, 2, 3

                def amat(lw, rw_, dst, dk, mask, neg):
                    ps, psb, pk = nps()
                    for h in range(8):
                        j, hp = h // 2, h % 2
                        sl = slice(hp * 64, hp * 64 + 64)
                        mm(ps[:, hoff(h):hoff(h) + 128], FMt[sl, lw, j, :], FMt[sl, rw_, j, :], True, True, ["FMt"], [pk])
                    psv = ps[:, :].rearrange("p (q j t) -> p q j t", q=2, j=4)
                    dstv = dst[:].rearrange("p (j q) t -> p q j t", q=2)
                    mk = mask.unsqueeze(1).unsqueeze(1).to_broadcast([128, 2, 4, 128])
                    if neg:
                        mk3 = mask.unsqueeze(1).to_broadcast([128, 4, 128])
                        for q in range(2):
                            dve(lambda e, q=q: e.scalar_tensor_tensor(out=dstv[:, q], in0=psv[:, q], scalar=-1.0, in1=mk3, op0=ALU.mult, op1=ALU.mult), [pk, "pA"], [dk])
                    else:
                        mk3 = mask.unsqueeze(1).to_broadcast([128, 4, 128])
                        for q in range(2):
                            dve(lambda e, q=q: e.tensor_tensor(out=dstv[:, q], in0=psv[:, q], in1=mk3, op=ALU.mult), [pk, "pA"], [dk])

                amat(BB, KAP, Pw[1], "Pw1", MUS, True)
                amat(KAP, BB, Pw[0], "Pw0", MLS, True)
                amat(KKB, KAP, Am[0], "Am0", MUS, False)
                amat(BB, RB, Am[1], "Am1", MUI, False)
                amat(KKB, RB, Am[2], "Am2", MUI, False)
                yield
                ps, psb, pk = nps()
                for h in range(8):
                    j, hp = h // 2, h % 2
                    sl = slice(hp * 64, hp * 64 + 64)
                    mm(ps[:, h * 64:(h + 1) * 64], FMt[sl, KAP, j, :], Sbf[sl, j, :], True, False, ["FMt", "Sbf"], [pk])
                    mm(ps[:, h * 64:(h + 1) * 64], Am[0][:, h, :], vrw[:, h, :], False, True, ["Am0", "vrw"], [pk])
                act(lambda e, ps=ps: e.copy(out=Zb[0][:], in_=ps[:, 0:512].rearrange("p (h d) -> p h d", h=8)), [pk], ["Zb0"])
                pi = 0
                zi = 0
                for lvl in range(7):
                    yield
                    Pc, PTc = Pw[pi], Pw[pi + 1]
                    Pk, PTk = "Pw%d" % pi, "Pw%d" % (pi + 1)
                    Zc, Zn = Zb[zi], Zb[1 - zi]
                    ps, psb, pk = nps()
                    for h in range(8):
                        mm(ps[:, h * 64:(h + 1) * 64], identb[:], Zc[:, h, :], True, False, ["identb", "Zb%d" % zi], [pk])
                        mm(ps[:, h * 64:(h + 1) * 64], PTc[:, h, :], Zc[:, h, :], False, True, [PTk, "Zb%d" % zi], [pk])
                    if lvl < 6:
                        act(lambda e, ps=ps, Zn=Zn: e.copy(out=Zn[:], in_=ps[:, 0:512].rearrange("p (h d) -> p h d", h=8)), [pk], ["Zb%d" % (1 - zi)])
                        zi = 1 - zi
                        ni = 2 - pi
                        Pn, PTn = Pw[ni], Pw[ni + 1]
                        psA, _, pkA = nps()
                        for h in range(8):
                            mm(psA[:, h * 128:(h + 1) * 128], PTc[:, h, :], Pc[:, h, :], True, True, [PTk, Pk], [pkA])
                        for a_ in range(2):
                            dve(lambda e, psA=psA, Pn=Pn, a_=a_: e.tensor_copy(out=Pn[:, 4 * a_:4 * a_ + 4, :], in_=psA[:, a_ * 512:(a_ + 1) * 512].rearrange("p (h t) -> p h t", h=4)), [pkA], ["Pw%d" % ni])
                        psB, _, pkB = nps()
                        for h in range(8):
                            mm(psB[:, h * 128:(h + 1) * 128], Pc[:, h, :], PTc[:, h, :], True, True, [Pk, PTk], [pkB])
                        for a_ in range(2):
                            act(lambda e, psB=psB, PTn=PTn, a_=a_: e.copy(out=PTn[:, 4 * a_:4 * a_ + 4, :], in_=psB[:, a_ * 512:(a_ + 1) * 512].rearrange("p (h t) -> p h t", h=4)), [pkB], ["Pw%d" % (ni + 1)])
                        pi = ni
                    else:
                        act(lambda e, ps=ps: e.activation(out=Ub[:], in_=ps[:, 0:512].rearrange("p (h d) -> p h d", h=8), func=AF.Copy, scale=-1.0), [pk], ["Ub"])
                yield
                ps, psb, pk = nps()
                for h in range(8):
                    j, hp = h // 2, h % 2
                    sl = slice(hp * 64, hp * 64 + 64)
                    o_ = ps[:, h * 64:(h + 1) * 64]
                    mm(o_, Am[1][:, h, :], Ub[:, h, :], True, False, ["Am1", "Ub"], [pk])
                    mm(o_, Am[2][:, h, :], vrw[:, h, :], False, False, ["Am2", "vrw"], [pk])
                    mm(o_, FMt[sl, RB, j, :], Sbf[sl, j, :], False, True, ["FMt", "Sbf"], [pk])
                act(lambda e, ps=ps: e.copy(out=ysb[:], in_=ps[:, 0:512]), [pk], ["ysb"])
                yield
                ps, psb, pk = nps()
                for j in range(4):
                    o_ = ps[:, j * 64:(j + 1) * 64]
                    mm(o_, Btz[:, 2 * j, :], Ub[:, 2 * j, :], True, False, ["Btz", "Ub"], [pk])
                    mm(o_, Ktz[:, 2 * j, :], vrw[:, 2 * j, :], False, False, ["Ktz", "vrw"], [pk])
                    mm(o_, Btz[:, 2 * j + 1, :], Ub[:, 2 * j + 1, :], False, False, ["Btz", "Ub"], [pk])
                    mm(o_, Ktz[:, 2 * j + 1, :], vrw[:, 2 * j + 1, :], False, True, ["Ktz", "vrw"], [pk])
                dve(lambda e: e.tensor_tensor(out=Sst[:], in0=Sst[:], in1=WLfm[:].unsqueeze(2).to_broadcast([128, 4, 64]), op=ALU.mult), ["Sst", "WLfm"], ["Sst"])
                dve(lambda e, ps=ps: e.tensor_tensor(out=Sst[:], in0=Sst[:], in1=ps[:, 0:256].rearrange("p (j d) -> p j d", j=4), op=ALU.add), [pk, "Sst"], ["Sst"])
                act(lambda e: e.copy(out=Sbf[:], in_=Sst[:]), ["Sst"], ["Sbf"])
                yield
            gens = [ml_stage(), rw_stage()]
            while gens:
                for g_ in list(gens):
                    try:
                        next(g_)
                    except StopIteration:
                        gens.remove(g_)
            head_rw(128, ysb, "ysb", bon, "bon", vf, "vf", g_sb, "g_sb", mix, "mix", W)
            out_proj(128, mix, "mix", x_, xk, b * 128, W)
            if b + 1 < NB:
                pool(lambda e, xn=xn: e.tensor_copy(out=xn[:, :, 0:1], in_=xn[:, :, 128:129]), [xnk], [xnk])
                pool(lambda e, qx=qx: e.tensor_copy(out=qx[:, :, 0:3], in_=qx[:, :, 128:131]), [qxk], [qxk])

        ps, psb, pk = nps()
        mm(ps[0:8, 0:128], runmax[:], IDF, True, True, ["runmax", "pA"], [pk])
        mm(ps[0:8, 128:256], nBc[:], IDF, True, True, ["nBc", "pA"], [pk])
        fs = TT("fs", [8, 16])
        dve(lambda e, ps=ps: e.tensor_reduce(out=fs[:, 0:1], in_=ps[0:8, 0:128], axis=AX.X, op=ALU.max), [pk], ["fs"])
        dve(lambda e: e.tensor_scalar_max(out=fs[:, 0:1], in0=fs[:, 0:1], scalar1=0.0), ["fs"], ["fs"])
        dve(lambda e, ps=ps: e.tensor_tensor(out=fs[:, 1:2], in0=fs[:, 0:1], in1=ps[0:8, 128:129], op=ALU.subtract), [pk, "fs"], ["fs"])
        dma("pool", om, fs[:, 1:2], ["fs"], [], "fin")
        act(lambda e: e.activation(out=fs[:, 2:3], in_=fs[:, 1:2], func=AF.Exp, scale=-1.0), ["fs"], ["fs"])
        dve(lambda e: e.tensor_scalar_mul(out=fs[:, 4:8], in0=pA[0:8, OFF["rsel"][0]:OFF["rsel"][1]], scalar1=fs[:, 2:3]), ["fs", "pA"], ["fs"])
        ps, psb, pk = nps()
        mm(ps[:, 0:4], pA[0:8, OFF["lsel"][0]:OFF["lsel"][1]], fs[:, 4:8], True, True, ["pA", "fs"], [pk])
        scb = TT("scb", [128, 4])
        act(lambda e, ps=ps: e.copy(out=scb[:], in_=ps[:, 0:4]), [pk], ["scb"])
        dve(lambda e: e.tensor_tensor(out=Cst[:], in0=Cst[:], in1=scb[:].unsqueeze(2).to_broadcast([128, 4, 65]), op=ALU.mult), ["Cst", "scb"], ["Cst"])
        dma("pool", oC, Cst[:], ["Cst"], [], "fin")
        dma("pool", oS, Sst[:], ["Sst"], [], "fin")
        PP[0].finalize()
        PP[0] = Prog(ctx)

    with contextlib.ExitStack() as es_s:
        cur[0] = es_s
        if do_sample:
            W = {}
            W["xm"] = TT("s_xm", [128, D])
            junk = W["xm"]
            st = TT("s_st", [128, 4])
            mix = TT("s_mix", [128, D], BF16)
            xsb = mix
            lor = TT("s_lor", [128, 256], BF16)
            lorT = TT("s_lorT", [128, 2, 128], BF16)
            W["mixT"] = TT("s_mixT", [128, 8, 128], BF16)
            sx = TT("sx", [NS, D])
            sxT = TT("sxT", [128, 8, NS], BF16)
            spj = TT("spj", [NS, INW])
            spk = TT("spk", [128, NSP])
            sl_t = TT("sl_t", [NS, 3, 256])
            Cs = TT("Cs", [128, 4096])
            Ss = Cs
            sn_t = TT("sn_t", [128, 64])
            sm_t = TT("sm_t", [128, 1])
            scv = TT("scv", [128, 2, 4, 64])
            ssh = TT("ssh", [128, 3, 64])
            dma("sp", sx[:], xs, [], ["sx"], "sin", True)
            dma("sp", spk[:], spk_d, [], ["spk"], "sin", True)
            dma("sp", sl_t[:, 0, :], sshl_d, [], ["sl_t"], "sin", True)
            dma("sp", sl_t[:, 1, :], mul_d, [], ["sl_t"], "sin", True)
            dma("sp", Cs[:], sC_d, [], ["Cs"], "sin", True)
            dma("sp", sn_t[:], sn_d, [], ["sn_t"], "sin", True)
            dma("sp", sm_t[:], sm_d, [], ["sm_t"], "sin", True)
            dma("sp", scv[:, :, 0:3, :], sconv_d, [], ["scv"], "sin", True)
            dma("sp", ssh[:], sshift_d, [], ["ssh"], "sin", True)
            rmsnorm_T(sx, "sx", NS, sxT, "sxT", 0, "nmw", xsb, "xsb", junk, "junk", st, "st")
            for n0 in range(0, INW, 512):
                nn = min(512, INW - n0)
                ps, psb, pk = nps()
                for c in range(8):
                    mm(ps[:NS, 0:nn], sxT[:, c, :], wq[:, c, n0:n0 + nn], c == 0, c == 7, ["sxT", "wq"], [pk])
                act(lambda e, ps=ps, n0=n0, nn=nn: e.copy(out=spj[:, n0:n0 + nn], in_=ps[:NS, 0:nn]), [pk], ["spj"])
            s1v = scr1.rearrange("(b h) a d -> b a h d", b=NS)
            for a_ in range(7):
                c0_ = a_ * 512 if a_ < 4 else MLW + (a_ - 4) * 512
                dma("pool", s1v[:, a_, :, :], spj[:, c0_:c0_ + 512].rearrange("p (h d) -> p h d", h=8), ["spj"], ["scr1"], "scrw1")
            A7 = TT("A7", [128, 8, 64])
            dma("sp", A7[:, 0:7, :], scr1[:, 0:7, :], ["scr1"], ["A7"], "scrr1")
            gif = TT("gif", [128, 2])
            s_if = nc.dram_tensor("scr_if", [2, 128], F32, kind="Internal").ap()
            for g_ in range(2):
                dma("pool", s_if[g_, :].rearrange("(b h) -> b h", b=NS), spj[:, 2048 + 8 * g_:2056 + 8 * g_], ["spj"], ["scr_if"], "scrwif")
            for g_ in range(2):
                dma("sp", gif[:, g_:g_ + 1], s_if[g_, :].rearrange("(p o) -> p o", o=1), ["scr_if"], ["gif"], "scrrif")
            SP_ = lambda n: spk[:, SOFF[n][0]:SOFF[n][1]]
            pl = spj[:, MLW + 1536:MLW + 1792]
            dma("pool", osshl, pl, ["spj"], [], "fin")
            dve(lambda e: e.tensor_tensor(out=sl_t[:, 2, :], in0=sl_t[:, 0, :], in1=pl, op=ALU.subtract), ["sl_t", "spj"], ["sl_t"])
            dve(lambda e: e.tensor_tensor(out=sl_t[:, 2, :], in0=sl_t[:, 2, :], in1=sl_t[:, 1, :], op=ALU.mult), ["sl_t"], ["sl_t"])
            dve(lambda e: e.tensor_tensor(out=sl_t[:, 2, :], in0=sl_t[:, 2, :], in1=pl, op=ALU.add), ["sl_t", "spj"], ["sl_t"])
            act(lambda e: e.activation(out=lor[:NS, 0:64], in_=sl_t[:, 2, 0:64], func=AF.Tanh), ["sl_t"], ["lor"])
            act(lambda e: e.copy(out=lor[:NS, 64:128], in_=sl_t[:, 2, 64:128]), ["sl_t"], ["lor"])
            act(lambda e: e.activation(out=lor[:NS, 128:256], in_=sl_t[:, 2, 128:256], func=AF.Sigmoid), ["sl_t"], ["lor"])
            ps, psb, pk = nps()
            pe(lambda e, psb=psb: e.transpose(psb[:, 0:NS], lor[:NS, 0:128], identb[:NS, :NS]), ["lor", "identb"], [pk])
            pe(lambda e, psb=psb: e.transpose(psb[:, 128:128 + NS], lor[:NS, 128:256], identb[:NS, :NS]), ["lor", "identb"], [pk])
            act(lambda e, psb=psb: e.copy(out=lorT[:, :, 0:NS], in_=psb[:, 0:256].rearrange("p (a t) -> p a t", a=2)[:, :, 0:NS]), [pk], ["lorT"])
            ps, psb, pk = nps()
            mm(ps[:NS, 0:512], lorT[0:64, 0, 0:NS], luw[0:64, :], True, True, ["lorT", "luw"], [pk])
            mm(ps[:NS, 512:1024], lorT[64:128, 0, 0:NS], luw[64:128, :], True, True, ["lorT", "luw"], [pk])
            ps2, _, pk2 = nps()
            mm(ps2[:NS, 0:512], lorT[:, 1, 0:NS], gup[:, :], True, True, ["lorT", "gup"], [pk2])
            lo3 = spj[:, 0:1536].rearrange("p (a n) -> p a n", a=3)
            for a_ in range(2):
                act(lambda e, ps=ps, a_=a_: e.copy(out=lo3[:, a_, :], in_=ps[:NS, a_ * 512:(a_ + 1) * 512]), [pk], ["lo3"])
            act(lambda e, ps2=ps2: e.copy(out=lo3[:, 2, :], in_=ps2[:NS, 0:512]), [pk2], ["lo3"])
            for a_ in range(3):
                dma("pool", scr2.rearrange("(b h) a d -> b a h d", b=NS)[:, a_, :, :], lo3[:, a_, :].rearrange("p (h d) -> p h d", h=8), ["lo3"], ["scr2"], "scrw2")
            L3 = TT("L3", [128, 3, 64])
            dma("sp", L3[:], scr2, ["scr2"], ["L3"], "scrr2")
            big = TT("big", [128, 4096])
            sv = TT("sv", [128, 64])
            pool(lambda e: e.tensor_copy(out=scv[:, :, 3, :], in_=A7[:, 0:2, :]), ["A7"], ["scv"])
            dma("pool", osconv, scv[:, :, 1:4, :], ["scv"], [], "fin")
            qk_s = TT("qk_s", [128, 2, 64])
            cwqk = lambda w_: spk[:, SOFF["cwq"][0] + w_ * 256:SOFF["cwq"][0] + (w_ + 1) * 256].rearrange("p (j d) -> p j d", j=4)
            for w_ in range(2):
                dve(lambda e, w_=w_: e.tensor_tensor(out=big[:, 0:256].rearrange("p (j d) -> p j d", j=4), in0=scv[:, w_, :, :], in1=cwqk(w_), op=ALU.mult), ["scv", "spk"], ["big"])
                dve(lambda e, w_=w_: e.tensor_reduce(out=qk_s[:, w_, :], in_=big[:, 0:256].rearrange("p (j d) -> p d j", j=4), axis=AX.X, op=ALU.add), ["big"], ["qk_s"])
            dve(lambda e: e.tensor_tensor(out=qk_s[:], in0=qk_s[:], in1=spk[:, SOFF["cbq"][0]:SOFF["cbk"][1]].rearrange("p (a d) -> p a d", a=2), op=ALU.add), ["qk_s", "spk"], ["qk_s"])
            act(lambda e: e.activation(out=qk_s[:], in_=qk_s[:], func=AF.Silu), ["qk_s"], ["qk_s"])
            act(lambda e: e.activation(out=qk_s[:, 1, :], in_=qk_s[:, 1, :], func=AF.Copy, scale=0.125), ["qk_s"], ["qk_s"])
            dve(lambda e: e.tensor_tensor(out=sv[:, 0:2], in0=gif[:], in1=spk[:, SOFF["ib"][0]:SOFF["fb"][1]], op=ALU.add), ["gif", "spk"], ["sv"])
            act(lambda e: e.activation(out=sv[:, 9:10], in_=sv[:, 1:2], func=AF.Exp, scale=-1.0), ["sv"], ["sv"])
            act(lambda e: e.activation(out=sv[:, 2:3], in_=sv[:, 9:10], func=AF.Ln, bias=1.0, scale=1.0), ["sv"], ["sv"])
            dve(lambda e: e.tensor_tensor(out=sv[:, 3:4], in0=sm_t[:], in1=sv[:, 2:3], op=ALU.subtract), ["sv", "sm_t"], ["sv"])
            dve(lambda e: e.tensor_tensor(out=sv[:, 4:5], in0=sv[:, 3:4], in1=sv[:, 0:1], op=ALU.max), ["sv"], ["sv"])
            dma("pool", osm, sv[:, 4:5], ["sv"], [], "fin")
            dve(lambda e: e.tensor_tensor(out=sv[:, 9:10], in0=sv[:, 0:1], in1=sv[:, 4:5], op=ALU.subtract), ["sv"], ["sv"])
            act(lambda e: e.activation(out=sv[:, 5:6], in_=sv[:, 9:10], func=AF.Exp), ["sv"], ["sv"])
            dve(lambda e: e.tensor_tensor(out=sv[:, 9:10], in0=sv[:, 3:4], in1=sv[:, 4:5], op=ALU.subtract), ["sv"], ["sv"])
            act(lambda e: e.activation(out=sv[:, 6:7], in_=sv[:, 9:10], func=AF.Exp), ["sv"], ["sv"])
            act(lambda e: e.activation(out=sv[:, 7:8], in_=sv[:, 4:5], func=AF.Exp, scale=-1.0), ["sv"], ["sv"])
            q_ = qk_s[:, 0, :]
            k_ = qk_s[:, 1, :]
            v_ = A7[:, 2, :]
            b3 = lambda t: t[:, :].rearrange("p (a c) -> p a c", a=64)
            pool(lambda e: e.tensor_tensor(out=b3(big), in0=k_.unsqueeze(2).to_broadcast([128, 64, 64]), in1=v_.unsqueeze(1).to_broadcast([128, 64, 64]), op=ALU.mult), ["qk_s", "A7"], ["big"])
            dve(lambda e: e.tensor_scalar_mul(out=Cs[:], in0=Cs[:], scalar1=sv[:, 6:7]), ["Cs", "sv"], ["Cs"])
            dve(lambda e: e.scalar_tensor_tensor(out=Cs[:], in0=big[:], scalar=sv[:, 5:6], in1=Cs[:], op0=ALU.mult, op1=ALU.add), ["big", "sv", "Cs"], ["Cs"])
            dma("pool", osC, Cs[:], ["Cs"], [], "fin")
            dve(lambda e: e.tensor_scalar_mul(out=sn_t[:], in0=sn_t[:], scalar1=sv[:, 6:7]), ["sn_t", "sv"], ["sn_t"])
            dve(lambda e: e.scalar_tensor_tensor(out=sn_t[:], in0=k_, scalar=sv[:, 5:6], in1=sn_t[:], op0=ALU.mult, op1=ALU.add), ["qk_s", "sv", "sn_t"], ["sn_t"])
            dma("pool", osn, sn_t[:], ["sn_t"], [], "fin")
            pool(lambda e: e.tensor_tensor(out=b3(big), in0=Cs[:, :].rearrange("p (k v) -> p v k", k=64), in1=q_.unsqueeze(1).to_broadcast([128, 64, 64]), op=ALU.mult), ["Cs", "qk_s"], ["big"])
            hs = TT("hs", [128, 2, 64])
            dve(lambda e: e.tensor_reduce(out=hs[:, 0, :], in_=b3(big), axis=AX.X, op=ALU.add), ["big"], ["hs"])
            dve(lambda e: e.tensor_tensor(out=sv[:, 16:80 - 16] if False else big[:, 0:64], in0=q_, in1=sn_t[:], op=ALU.mult), ["qk_s", "sn_t"], ["big"])
            dve(lambda e: e.tensor_reduce(out=sv[:, 8:9], in_=big[:, 0:64], axis=AX.X, op=ALU.add), ["big"], ["sv"])
            dve(lambda e: e.scalar_tensor_tensor(out=sv[:, 9:10], in0=sv[:, 8:9], scalar=-1.0, in1=sv[:, 8:9], op0=ALU.mult, op1=ALU.max), ["sv"], ["sv"])
            dve(lambda e: e.tensor_tensor(out=sv[:, 9:10], in0=sv[:, 9:10], in1=sv[:, 7:8], op=ALU.max), ["sv"], ["sv"])
            dve(lambda e: e.reciprocal(out=sv[:, 10:11], in_=sv[:, 9:10]), ["sv"], ["sv"])
            dve(lambda e: e.tensor_scalar_mul(out=hs[:, 0, :], in0=hs[:, 0, :], scalar1=sv[:, 10:11]), ["hs", "sv"], ["hs"])
            dma("pool", osshift, A7[:, 4:7, :], ["A7"], [], "fin")
            rk3 = TT("rk3", [128, 3, 64])
            mu3 = spk[:, SOFF["mu_r"][0]:SOFF["mu_v"][1]].rearrange("p (a d) -> p a d", a=3)
            dve(lambda e: e.tensor_tensor(out=rk3[:], in0=ssh[:], in1=A7[:, 4:7, :], op=ALU.subtract), ["ssh", "A7"], ["rk3"])
            dve(lambda e: e.tensor_tensor(out=rk3[:], in0=rk3[:], in1=mu3, op=ALU.mult), ["rk3", "spk"], ["rk3"])
            dve(lambda e: e.tensor_tensor(out=rk3[:], in0=rk3[:], in1=A7[:, 4:7, :], op=ALU.add), ["rk3", "A7"], ["rk3"])
            w8 = TT("w8", [128, 8, 64])
            dve(lambda e: e.tensor_tensor(out=w8[:, 0, :], in0=L3[:, 0, :], in1=SP_("w0"), op=ALU.add), ["L3", "spk"], ["w8"])
            act(lambda e: e.activation(out=w8[:, 0, :], in_=w8[:, 0, :], func=AF.Sigmoid), ["w8"], ["w8"])
            act(lambda e: e.activation(out=w8[:, 0, :], in_=w8[:, 0, :], func=AF.Exp, scale=-C0), ["w8"], ["w8"])
            dve(lambda e: e.tensor_tensor(out=w8[:, 1, :], in0=L3[:, 1, :], in1=SP_("a0"), op=ALU.add), ["L3", "spk"], ["w8"])
            act(lambda e: e.activation(out=w8[:, 1, :], in_=w8[:, 1, :], func=AF.Sigmoid), ["w8"], ["w8"])
            dve(lambda e: e.tensor_tensor(out=w8[:, 6, :], in0=rk3[:, 1, :], in1=SP_("kk"), op=ALU.mult), ["rk3", "spk"], ["w8"])
            dve(lambda e: e.tensor_tensor(out=w8[:, 7, :], in0=w8[:, 6, :], in1=w8[:, 6, :], op=ALU.mult), ["w8"], ["w8"])
            dve(lambda e: e.tensor_reduce(out=sv[:, 11:12], in_=w8[:, 7, :], axis=AX.X, op=ALU.add), ["w8"], ["sv"])
            dve(lambda e: e.tensor_scalar_max(out=sv[:, 11:12], in0=sv[:, 11:12], scalar1=1e-24), ["sv"], ["sv"])
            act(lambda e: e.activation(out=sv[:, 11:12], in_=sv[:, 11:12], func=AF.Sqrt), ["sv"], ["sv"])
            dve(lambda e: e.reciprocal(out=sv[:, 11:12], in_=sv[:, 11:12]), ["sv"], ["sv"])
            dve(lambda e: e.tensor_scalar_mul(out=w8[:, 3, :], in0=w8[:, 6, :], scalar1=sv[:, 11:12]), ["w8", "sv"], ["w8"])
            dve(lambda e: e.tensor_tensor(out=w8[:, 6, :], in0=w8[:, 1, :], in1=SP_("ka"), op=ALU.mult), ["w8", "spk"], ["w8"])
            dve(lambda e: e.tensor_tensor(out=w8[:, 6, :], in0=w8[:, 6, :], in1=SP_("ka"), op=ALU.subtract), ["w8", "spk"], ["w8"])
            dve(lambda e: e.tensor_scalar_add(out=w8[:, 6, :], in0=w8[:, 6, :], scalar1=1.0), ["w8"], ["w8"])
            dve(lambda e: e.tensor_tensor(out=w8[:, 4, :], in0=rk3[:, 1, :], in1=w8[:, 6, :], op=ALU.mult), ["rk3", "w8"], ["w8"])
            dve(lambda e: e.tensor_tensor(out=w8[:, 5, :], in0=w8[:, 1, :], in1=w8[:, 3, :], op=ALU.mult), ["w8"], ["w8"])
            dma("sp", Ss[:], sS_d, [], ["Ss"], "sin2")
            bk = lambda ap: ap.unsqueeze(1).to_broadcast([128, 64, 64])
            bv = lambda ap: ap.unsqueeze(2).to_broadcast([128, 64, 64])
            pool(lambda e: e.tensor_tensor(out=b3(big), in0=b3(Ss), in1=bk(w8[:, 3, :]), op=ALU.mult), ["Ss", "w8"], ["big"])
            dve(lambda e: e.tensor_reduce(out=w8[:, 7, :], in_=b3(big), axis=AX.X, op=ALU.add), ["big"], ["w8"])
            dve(lambda e: e.tensor_tensor(out=b3(Ss), in0=b3(Ss), in1=bk(w8[:, 0, :]), op=ALU.mult), ["Ss", "w8"], ["Ss"])
            pool(lambda e: e.tensor_tensor(out=b3(big), in0=bv(w8[:, 7, :]), in1=bk(w8[:, 5, :]), op=ALU.mult), ["w8"], ["big"])
            dve(lambda e: e.tensor_tensor(out=Ss[:], in0=Ss[:], in1=big[:], op=ALU.subtract), ["Ss", "big"], ["Ss"])
            pool(lambda e: e.tensor_tensor(out=b3(big), in0=bv(rk3[:, 2, :]), in1=bk(w8[:, 4, :]), op=ALU.mult), ["rk3", "w8"], ["big"])
            dve(lambda e: e.tensor_tensor(out=Ss[:], in0=Ss[:], in1=big[:], op=ALU.add), ["Ss", "big"], ["Ss"])
            dma("pool", osS, Ss[:], ["Ss"], [], "fin")
            pool(lambda e: e.tensor_tensor(out=b3(big), in0=b3(Ss), in1=bk(rk3[:, 0, :]), op=ALU.mult), ["Ss", "rk3"], ["big"])
            dve(lambda e: e.tensor_reduce(out=hs[:, 1, :], in_=b3(big), axis=AX.X, op=ALU.add), ["big"], ["hs"])
            dve(lambda e: e.tensor_tensor(out=w8[:, 6, :], in0=rk3[:, 0, :], in1=w8[:, 4, :], op=ALU.mult), ["rk3", "w8"], ["w8"])
            dve(lambda e: e.tensor_tensor(out=w8[:, 6, :], in0=w8[:, 6, :], in1=SP_("rk"), op=ALU.mult), ["w8", "spk"], ["w8"])
            dve(lambda e: e.tensor_reduce(out=sv[:, 12:13], in_=w8[:, 6, :], axis=AX.X, op=ALU.add), ["w8"], ["sv"])
            dve(lambda e: e.scalar_tensor_tensor(out=hs[:, 1, :], in0=rk3[:, 2, :], scalar=sv[:, 12:13], in1=hs[:, 1, :], op0=ALU.mult, op1=ALU.add), ["rk3", "sv", "hs"], ["hs"])
            act(lambda e: e.activation(out=w8[:, 6, :], in_=A7[:, 3, :], func=AF.Sigmoid), ["A7"], ["w8"])
            dve(lambda e: e.tensor_tensor(out=hs[:, 0, :], in0=hs[:, 0, :], in1=w8[:, 6, :], op=ALU.mult), ["hs", "w8"], ["hs"])
            dve(lambda e: e.tensor_tensor(out=w8[:, 7, :], in0=hs[:, 0, :], in1=hs[:, 0, :], op=ALU.mult), ["hs"], ["w8"])
            dve(lambda e: e.tensor_reduce(out=sv[:, 13:14], in_=w8[:, 7, :], axis=AX.X, op=ALU.add), ["w8"], ["sv"])
            act(lambda e: e.activation(out=sv[:, 13:14], in_=sv[:, 13:14], func=AF.Sqrt, bias=EPS, scale=1.0 / 64), ["sv"], ["sv"])
            dve(lambda e: e.reciprocal(out=sv[:, 13:14], in_=sv[:, 13:14]), ["sv"], ["sv"])
            dve(lambda e: e.scalar_tensor_tensor(out=hs[:, 0, :], in0=hs[:, 0, :], scalar=sv[:, 13:14], in1=SP_("mnw"), op0=ALU.mult, op1=ALU.mult), ["hs", "sv", "spk"], ["hs"])
            dve(lambda e: e.tensor_reduce(out=sv[:, 14:15], in_=hs[:, 1, :], axis=AX.X, op=ALU.add), ["hs"], ["sv"])
            dve(lambda e: e.tensor_scalar_mul(out=sv[:, 14:15], in0=sv[:, 14:15], scalar1=1.0 / 64), ["sv"], ["sv"])
            dve(lambda e: e.tensor_scalar_sub(out=hs[:, 1, :], in0=hs[:, 1, :], scalar1=sv[:, 14:15]), ["hs", "sv"], ["hs"])
            dve(lambda e: e.tensor_tensor(out=w8[:, 7, :], in0=hs[:, 1, :], in1=hs[:, 1, :], op=ALU.mult), ["hs"], ["w8"])
            dve(lambda e: e.tensor_reduce(out=sv[:, 15:16], in_=w8[:, 7, :], axis=AX.X, op=ALU.add), ["w8"], ["sv"])
            act(lambda e: e.activation(out=sv[:, 15:16], in_=sv[:, 15:16], func=AF.Sqrt, bias=GN_EPS, scale=1.0 / 64), ["sv"], ["sv"])
            dve(lambda e: e.reciprocal(out=sv[:, 15:16], in_=sv[:, 15:16]), ["sv"], ["sv"])
            dve(lambda e: e.scalar_tensor_tensor(out=hs[:, 1, :], in0=hs[:, 1, :], scalar=sv[:, 15:16], in1=SP_("lnw"), op0=ALU.mult, op1=ALU.mult), ["hs", "sv", "spk"], ["hs"])
            dve(lambda e: e.tensor_tensor(out=hs[:, 1, :], in0=hs[:, 1, :], in1=SP_("lnb"), op=ALU.add), ["hs", "spk"], ["hs"])
            dve(lambda e: e.tensor_tensor(out=hs[:, 1, :], in0=hs[:, 1, :], in1=L3[:, 2, :], op=ALU.mult), ["hs", "L3"], ["hs"])
            s3v = nc.dram_tensor("scr3b", [128, 2, 64], F32, kind="Internal").ap()
            dma("pool", s3v, hs[:], ["hs"], ["scr3b"], "scrw3")
            smix = spj[:, 2304:3328].rearrange("p (a h d) -> p a h d", a=2, h=8)
            for a_ in range(2):
                dma("sp", smix[:, a_, :, :], s3v.rearrange("(b h) a d -> b a h d", b=NS)[:, a_, :, :], ["scr3b"], ["smix"], "scrr3")
            act(lambda e: e.copy(out=mix[:NS, :], in_=smix[:].rearrange("p a h d -> p (a h d)")), ["smix"], ["mix"])
            out_proj(NS, mix, "mix", sx, "sx", T, W)
        PP[0].finalize()
        PP[0] = Prog(ctx)
    es_res.close()

    with contextlib.ExitStack() as es2:
        cur[0] = es2
        upb = TT("upb", [128, 8, DFF], BF16)
        dnb = TT("dnb", [128, 32, D], BF16)
        pB2 = TT("pB2", [128, 136], F32)
        nfw = TT("nfw", [128, D])
        identb2 = TT("identb2", [128, 128], BF16)
        wout = TT("wout", [128, 8, D], BF16)
        mixin = TT("mixin", [128, D], BF16)
        for c in range(0, 8, 4):
            dma("pool", wout[:, c:c + 4, :], w_out_v[:, c:c + 4, :], [], ["wout"], "wout")
        up_v = mlp_up.rearrange("(c p) n -> p c n", p=128)
        dn_v = mlp_down.rearrange("(c p) n -> p c n", p=128)
        dma("sp", pB2[:, 0:128], packA_d[:, OFF["ident"][0]:OFF["ident"][1]], [], ["pB2"], "init2", True)
        dma("sp", pB2[:, 128:136], packA_d[:, OFF["nmlp"][0]:OFF["nmlp"][1]], [], ["pB2"], "init2", True)
        dma("sp", nfw[:], nfw_d, [], ["nfw"], "init2", True)
        for g8 in range(8):
            dma("pool", upb[:, :, g8 * 512:(g8 + 1) * 512], up_v[:, :, g8 * 512:(g8 + 1) * 512], [], ["upb%d" % g8], "up%d" % g8)
        for g8 in range(8):
            dma("pool", dnb[:, g8 * 4:(g8 + 1) * 4, :], dn_v[:, g8 * 4:(g8 + 1) * 4, :], [], ["dnb%d" % g8], "dn%d" % g8)
        dve(lambda e: e.tensor_copy(out=identb2[:], in_=pB2[:, 0:128]), ["pB2"], ["identb2"])
        NSUB = 2
        NTT = NSUB * 128
        xsb2 = TT("xsb2", [128, D], BF16)
        xb = [TT("xb%d" % i, [128, NSUB, D]) for i in range(2)]
        st2 = TT("st2", [128, 8])
        xn2 = [TT("xn2T%d" % i, [128, 8, NTT], BF16) for i in range(2)]
        hT = TT("hT", [128, 32, NTT], BF16)
        junk2 = TT("junk2", [128, D], BF16)
        rl = [TT("rl%d" % i, [128, 512]) for i in range(2)]
        nmlp = pB2[:, 128:136]
        xm_v = xp.rearrange("(s p) d -> p s d", p=128)
        yp_v = yp.rearrange("(s p) d -> p s d", p=128)
        sbs = [(sb * NSUB, NSUB, 128) for sb in range(NB // NSUB)] + [(NB, 1, NS)]

        def front(i):
            s0, nsub, nt = sbs[i]
            x4 = xb[i % 2]
            xk = "xb%d" % (i % 2)
            xn2T = xn2[i % 2]
            xnk = "xn2T%d" % (i % 2)
            if nsub == NSUB:
                dma("sp", x4[:], xm_v[:, s0:s0 + NSUB, :], [], [xk], xk)
            else:
                dma("sp", x4[:nt, 0, :], xs, [], [xk], xk)
            for si in range(nsub):
                r0_ = (s0 + si) * 128 if nsub == NSUB else T
                dma("sp", mixin[:nt, :], mix_d[r0_:r0_ + nt, :], [], ["mixin"], "mixin")
                ps, psb, pk = nps()
                for c in range(8):
                    pe(lambda e, c=c, psb=psb: e.transpose(psb[:, c * 128:c * 128 + nt], mixin[:nt, c * 128:(c + 1) * 128], identb2[:nt, :nt]), ["mixin", "identb2"], [pk])
                act(lambda e, psb=psb, si=si: e.copy(out=xn2T[:, :, si * 128:si * 128 + nt], in_=psb[:, 0:1024].rearrange("p (c t) -> p c t", c=8)[:, :, 0:nt]), [pk], [xnk])
                ps, psb, pk = nps()
                for n in range(2):
                    for c in range(8):
                        mm(ps[:nt, n * 512:(n + 1) * 512], xn2T[:, c, si * 128:si * 128 + nt], wout[:, c, n * 512:(n + 1) * 512], c == 0, c == 7, [xnk, "wout"], [pk])
                for a_ in range(2):
                    dve(lambda e, ps=ps, si=si, a_=a_: e.tensor_tensor(out=x4[:nt, si, a_ * 512:(a_ + 1) * 512], in0=ps[:nt, a_ * 512:(a_ + 1) * 512], in1=x4[:nt, si, a_ * 512:(a_ + 1) * 512], op=ALU.add), [pk, xk], [xk])
                act(lambda e, si=si: e.activation(out=junk2[:nt, :], in_=x4[:nt, si, :], func=AF.Square, accum_out=st2[:nt, 0:1]), [xk], ["junk2", "st2"])
                act(lambda e: e.activation(out=st2[:nt, 1:2], in_=st2[:nt, 0:1], func=AF.Sqrt, bias=EPS, scale=1.0 / D), ["st2"], ["st2"])
                dve(lambda e: e.reciprocal(out=st2[:nt, 2:3], in_=st2[:nt, 1:2]), ["st2"], ["st2"])
                dve(lambda e, si=si: e.tensor_scalar_mul(out=xsb2[:nt, :], in0=x4[:nt, si, :], scalar1=st2[:nt, 2:3]), [xk, "st2"], ["xsb2"])
                ps, psb, pk = nps()
                for c in range(8):
                    pe(lambda e, c=c, psb=psb: e.transpose(psb[:, c * 128:c * 128 + nt], xsb2[:nt, c * 128:(c + 1) * 128], identb2[:nt, :nt]), ["xsb2", "identb2"], [pk])
                dve(lambda e, psb=psb, si=si: e.tensor_tensor(out=xn2T[:, :, si * 128:si * 128 + nt], in0=psb[:, 0:1024].rearrange("p (c t) -> p c t", c=8)[:, :, 0:nt],
                                                             in1=nmlp.unsqueeze(2).to_broadcast([128, 8, nt]), op=ALU.mult), [pk, "pB2"], [xnk])

        def up(i):
            s0, nsub, nt = sbs[i]
            ntt = nsub * nt if nsub == NSUB else nt
            xn2T = xn2[i % 2]
            xnk = "xn2T%d" % (i % 2)
            per = 512 // NTT
            for j2 in range(32 // (2 * per)):
                ps, psb, pk = nps()
                for jj in range(2 * per):
                    j = j2 * 2 * per + jj
                    for c in range(8):
                        mm(ps[:, jj * NTT:jj * NTT + ntt], upb[:, c, j * 128:(j + 1) * 128], xn2T[:, c, 0:ntt], c == 0, c == 7, ["upb%d" % (j // 4), xnk], [pk])
                for bk in range(2):
                    r_ = rl[bk]
                    rk_ = "rl%d" % bk
                    j0 = j2 * 2 * per + bk * per
                    psv = ps[:, bk * 512:(bk + 1) * 512].rearrange("p (j t) -> p j t", j=per)[:, :, 0:ntt]
                    rv = r_[:, :].rearrange("p (j t) -> p j t", j=per)[:, :, 0:ntt]
                    act(lambda e, psv=psv, rv=rv: e.activation(out=rv, in_=psv, func=AF.Relu), [pk], [rk_])
                    if bk == 0:
                        dve(lambda e, rv=rv, j0=j0: e.tensor_tensor(out=hT[:, j0:j0 + per, 0:ntt], in0=rv, in1=rv, op=ALU.mult), [rk_], ["hT"])
                    else:
                        pool(lambda e, rv=rv, j0=j0: e.tensor_tensor(out=hT[:, j0:j0 + per, 0:ntt], in0=rv, in1=rv, op=ALU.mult), [rk_], ["hT"])

        def down(i):
            s0, nsub, nt = sbs[i]
            x4 = xb[i % 2]
            xk = "xb%d" % (i % 2)
            for si in range(nsub):
                ps, psb, pk = nps()
                for n in range(2):
                    for j in range(32):
                        mm(ps[:nt, n * 512:(n + 1) * 512], hT[:, j, si * 128:si * 128 + nt], dnb[:, j, n * 512:(n + 1) * 512], j == 0, j == 31, ["hT", "dnb%d" % (j // 4)], [pk])
                for a_ in range(2):
                    dve(lambda e, ps=ps, si=si, a_=a_: e.tensor_tensor(out=x4[:nt, si, a_ * 512:(a_ + 1) * 512], in0=ps[:nt, a_ * 512:(a_ + 1) * 512], in1=x4[:nt, si, a_ * 512:(a_ + 1) * 512], op=ALU.add), [pk, xk], [xk])
                act(lambda e, si=si: e.activation(out=junk2[:nt, :], in_=x4[:nt, si, :], func=AF.Square, accum_out=st2[:nt, 4:5]), [xk], ["junk2", "st2"])
                act(lambda e: e.activation(out=st2[:nt, 5:6], in_=st2[:nt, 4:5], func=AF.Sqrt, bias=EPS, scale=1.0 / D), ["st2"], ["st2"])
                dve(lambda e: e.reciprocal(out=st2[:nt, 6:7], in_=st2[:nt, 5:6]), ["st2"], ["st2"])
                dve(lambda e, si=si: e.scalar_tensor_tensor(out=x4[:nt, si, :], in0=x4[:nt, si, :], scalar=st2[:nt, 6:7], in1=nfw[:nt, :], op0=ALU.mult, op1=ALU.mult), [xk, "st2", "nfw"], [xk])
            if nsub == NSUB:
                dma("pool", yp_v[:, s0:s0 + NSUB, :], x4[:], [xk], [], "yo%d" % (i % 2))
            else:
                dma("pool", ys, x4[:nt, 0, :], [xk], [], "yo%d" % (i % 2))

        front(0)
        for i in range(len(sbs)):
            up(i)
            if i + 1 < len(sbs):
                front(i + 1)
            down(i)
        PP[0].finalize()
    es_ps.close()
    ctx.close()
    return nc


_CACHE = {}


def _host_packs(inp, core):
    f = np.float32
    L = 0
    pa = np.zeros((128, NA), f)

    def put(n, arr):
        a, b = OFF[n]
        pa[:, a:b] = arr

    rep = lambda v: np.broadcast_to(np.asarray(v, f).reshape(1, -1), (128, np.asarray(v).size))
    put("mnw", rep(inp["mlstm_norm_w"][L]))
    put("w0", rep(inp["rw_w0"][L]))
    put("a0", rep(inp["rw_a0"][L]))
    put("kk", rep(inp["rw_k_k"][L]))
    put("ka", rep(inp["rw_k_a"][L]))
    put("rk", rep(inp["rw_r_k"][L].reshape(-1)))
    put("lnw", rep(inp["rw_ln_w"][L]))
    put("lnb", rep(inp["rw_ln_b"][L]))
    put("ifb", rep(np.concatenate([inp["mlstm_i_b"][L], inp["mlstm_f_b"][L]])))
    put("nmw", inp["norm_mix_w"][L].reshape(8, 128).T)
    put("nmlp", inp["norm_mlp_w"][L].reshape(8, 128).T)
    cw = inp["mlstm_conv_w"][L]
    put("cw", cw.reshape(4, 8, 128).transpose(2, 1, 0).reshape(128, 32))
    put("cb", inp["mlstm_conv_b"][L].reshape(8, 128).T)
    put("ident", np.eye(128, dtype=f))
    put("mui", np.triu(np.ones((128, 128), f), 0))
    put("mus", np.triu(np.ones((128, 128), f), 1))
    put("mls", np.tril(np.ones((128, 128), f), -1))
    put("ones", np.ones((128, 128), f))
    lsel = np.zeros((128, 128), f)
    rsel = np.zeros((128, 4), f)
    for h in range(8):
        lsel[h, (h % 2) * 64:(h % 2) * 64 + 64] = 1.0
        rsel[h, h // 2] = 1.0
    put("lsel", lsel)
    put("rsel", rsel)
    return pa


def _sample_pack(inp):
    f = np.float32
    L = 0
    sp = np.zeros((128, NSP), f)

    def bh(v512):
        return np.tile(np.asarray(v512, f).reshape(8, 64), (NS, 1))

    def put(n, arr):
        a, b = SOFF[n]
        sp[:, a:b] = arr

    mu = inp["rw_mu"][L]
    put("mu_r", bh(mu[0:512]))
    put("mu_k", bh(mu[512:1024]))
    put("mu_v", bh(mu[1024:1536]))
    cw = inp["mlstm_conv_w"][L]
    put("cwq", np.concatenate([bh(cw[j, 0:512]) for j in range(4)], axis=1))
    put("cwk", np.concatenate([bh(cw[j, 512:1024]) for j in range(4)], axis=1))
    cb = inp["mlstm_conv_b"][L]
    put("cbq", bh(cb[0:512]))
    put("cbk", bh(cb[512:1024]))
    put("mnw", bh(inp["mlstm_norm_w"][L]))
    put("w0", bh(inp["rw_w0"][L]))
    put("a0", bh(inp["rw_a0"][L]))
    put("kk", bh(inp["rw_k_k"][L]))
    put("ka", bh(inp["rw_k_a"][L]))
    put("rk", bh(inp["rw_r_k"][L].reshape(-1)))
    put("lnw", bh(inp["rw_ln_w"][L]))
    put("lnb", bh(inp["rw_ln_b"][L]))
    put("ib", np.tile(inp["mlstm_i_b"][L].reshape(8, 1), (NS, 1)))
    put("fb", np.tile(inp["mlstm_f_b"][L].reshape(8, 1), (NS, 1)))
    return sp


def kernel(**inp):
    f = np.float32
    inp = {k: np.asarray(v) for k, v in inp.items()}
    if "nc" not in _CACHE:
        _CACHE["nc"] = build_program()
    nc = _CACHE["nc"]
    L = 0
    pa = _host_packs(inp, 0)
    sp = _sample_pack(inp)
    mu = inp["rw_mu"][L]
    luw = np.concatenate([inp["rw_w_up"][L], inp["rw_a_up"][L]], axis=0).astype(f)
    common = {
        "w_in": np.ascontiguousarray(inp["w_in"][L], f),
        "w_out": np.ascontiguousarray(inp["w_out"][L], f),
        "mlp_up": np.ascontiguousarray(inp["mlp_up"][L], f),
        "mlp_down": np.ascontiguousarray(inp["mlp_down"][L], f),
        "packA": pa,
        "mu_b": np.ascontiguousarray(np.broadcast_to(mu.reshape(1, -1), (128, RWW)), f),
        "nfw_b": np.ascontiguousarray(np.broadcast_to(inp["norm_f_w"].reshape(1, -1), (128, D)), f),
        "luw": luw,
        "gup": np.ascontiguousarray(inp["rw_g_up"][L], f),
        "spack": sp,
        "mul": np.ascontiguousarray(np.broadcast_to(mu[1536:1792].reshape(1, -1), (NS, 256)), f),
    }
    in_maps = []
    for c in range(8):
        rs = slice(c * NS, (c + 1) * NS)
        m = dict(common)
        m["xp"] = np.ascontiguousarray(inp["x_prompt"][c], f)
        m["xs"] = np.ascontiguousarray(inp["x_sample"][rs, 0, :], f)
        m["sC"] = np.ascontiguousarray(inp["state_mlstm_C"][L, rs].reshape(128, 4096), f)
        m["sn"] = np.ascontiguousarray(inp["state_mlstm_n"][L, rs].reshape(128, 64), f)
        m["sm"] = np.ascontiguousarray(inp["state_mlstm_m"][L, rs].reshape(128, 1), f)
        cv = inp["state_mlstm_conv"][L, rs]
        m["sconv"] = np.ascontiguousarray(cv.reshape(NS, 3, 2, 8, 64).transpose(0, 3, 2, 1, 4).reshape(128, 2, 3, 64), f)
        m["sS"] = np.ascontiguousarray(inp["state_rwkv_S"][L, rs].reshape(128, 4096), f)
        sh = inp["state_rwkv_shift"][L, rs, 0, :]
        m["sshift"] = np.ascontiguousarray(sh[:, 0:1536].reshape(NS, 3, 8, 64).transpose(0, 2, 1, 3).reshape(128, 3, 64), f)
        m["sshl"] = np.ascontiguousarray(sh[:, 1536:1792], f)
        in_maps.append(m)
    res = run_bass_kernel_spmd(nc, in_maps, core_ids=list(range(8)))
    R = res.results
    y_prompt = np.stack([R[c]["yp"] for c in range(8)]).astype(f)
    y_sample = np.concatenate([R[c]["ys"] for c in range(8)], axis=0).reshape(128, 1, D).astype(f)
    pC = np.zeros((1, 8, 8, 64, 64), f)
    pn = np.zeros((1, 8, 8, 64), f)
    pm = np.zeros((1, 8, 8), f)
    pconv = np.zeros((1, 8, 3, 1024), f)
    pS = np.zeros((1, 8, 8, 64, 64), f)
    pshift = np.zeros((1, 8, 1, RWW), f)
    for c in range(8):
        oC = R[c]["oC"].reshape(2, 64, 4, 65)
        Ch = oC.transpose(2, 0, 1, 3).reshape(8, 64, 65)
        pC[0, c] = Ch[:, :, 0:64]
        pn[0, c] = Ch[:, :, 64]
        pm[0, c] = R[c]["om"].reshape(8)
        pconv[0, c] = R[c]["oconv"].transpose(2, 1, 0).reshape(3, 1024)
        oS = R[c]["oS"].reshape(2, 64, 4, 64)
        pS[0, c] = oS.transpose(2, 0, 3, 1).reshape(8, 64, 64)
        pshift[0, c, 0] = R[c]["oshift"].reshape(RWW)
    sC = np.concatenate([R[c]["osC"].reshape(NS, 8, 64, 64) for c in range(8)])[None].astype(f)
    sn = np.concatenate([R[c]["osn"].reshape(NS, 8, 64) for c in range(8)])[None].astype(f)
    sm = np.concatenate([R[c]["osm"].reshape(NS, 8) for c in range(8)])[None].astype(f)
    sconv = np.concatenate([R[c]["osconv"].reshape(NS, 8, 2, 3, 64).transpose(0, 3, 2, 1, 4).reshape(NS, 3, 1024) for c in range(8)])[None].astype(f)
    sS = np.concatenate([R[c]["osS"].reshape(NS, 8, 64, 64) for c in range(8)])[None].astype(f)
    sshift = np.concatenate([
        np.concatenate([R[c]["osshift"].reshape(NS, 8, 3, 64).transpose(0, 2, 1, 3).reshape(NS, 1536), R[c]["osshl"]], axis=1)
        for c in range(8)]).reshape(1, 128, 1, RWW).astype(f)
    return (y_prompt, y_sample, pC, pn, pm, pconv, pS, pshift, sC, sn, sm, sconv, sS, sshift)
```

```python
import contextlib
import numpy as np
import concourse.bass as bass
import concourse.mybir as mybir
from concourse.bass_utils import run_bass_kernel_spmd

F32 = mybir.dt.float32
BF16 = mybir.dt.bfloat16
AF = mybir.ActivationFunctionType
ALU = mybir.AluOpType
AX = mybir.AxisListType

D = 1024
T = 2048
NB = 16
NS = 16
INW = 3856
MLW = 2064
RWW = 1792
DFF = 4096
EPS = 1e-6
GN_EPS = 64e-5
C0 = 0.6065306597126334

OFF = {}
_o = 0
for _n, _w in [("mnw", 512), ("w0", 512), ("a0", 512), ("kk", 512), ("ka", 512), ("rk", 512),
               ("lnw", 512), ("lnb", 512), ("ifb", 16), ("nmw", 8), ("nmlp", 8), ("cw", 32), ("cb", 8),
               ("ident", 128), ("mui", 128), ("mus", 128), ("mls", 128), ("ones", 128),
               ("lsel", 128), ("rsel", 4)]:
    OFF[_n] = (_o, _o + _w)
    _o += _w
NA = _o
SOFF = {}
_o = 0
for _n, _w in [("mu_r", 64), ("mu_k", 64), ("mu_v", 64), ("cwq", 256), ("cwk", 256), ("cbq", 64), ("cbk", 64),
               ("mnw", 64), ("w0", 64), ("a0", 64), ("kk", 64), ("ka", 64), ("rk", 64), ("lnw", 64), ("lnb", 64),
               ("ib", 1), ("fb", 1)]:
    SOFF[_n] = (_o, _o + _w)
    _o += _w
NSP = _o


ALIAS = {"r_sb0": "G0", "kf_sb0": "G1", "vf0": "G2", "osig0": "G13", "r_sb1": "G20", "kf_sb1": "G21", "vf1": "G22", "osig1": "G23",
         "wsig": "G3", "a_sb": "G4", "g_sb": "G5", "kap": "G6", "ktl": "G7",
         "bvec": "G8", "e1": "G9", "e2": "G10", "e3": "G11", "pcw": "G12", "ysb": "G11", "tA": "G9", "tB": "G10", "plast": "G16",
         "hml": "G14", "nlrep": "G15", "Fb": "G15", "cacc": "G16", "qks": "G16", "tAm": "G18", "tBm": "G19",
         "junk": "xm", "ctmp": "xm", "mixT": "TMbA", "TMb0": "TMbA", "TMb1": "TMbA",
         "TMb2": "TMbB", "TMb3": "TMbB", "Ub": "Zb1", "Ss": "Cs", "lo3": "spj", "smix": "spj",
         "xt1": "xt0", "xnT1": "xnT0", "qkx1": "qkx0"}
PARKEYS = {"r_sb", "kf_sb", "vf", "osig", "vaug", "gt", "qpb", "kTb", "ktm", "vrw", "lor"}
CURP = [0]


class SemCtx:
    def __init__(self, nc):
        self.nc = nc
        self.es = contextlib.ExitStack()
        self.engs = ["pe", "act", "dve", "pool", "sp"]
        self.esem = {e: self.es.enter_context(nc.semaphore("s_" + e)) for e in self.engs}
        self.ecnt = {e: 0 for e in self.engs}
        self.bsem = self.es.enter_context(nc.semaphore("s_bar"))
        self.phase = 0
        self.gsem = {}
        self.gbase = {}

    def group_sem(self, g):
        if g not in self.gsem:
            self.gsem[g] = self.es.enter_context(self.nc.semaphore("g_%d" % len(self.gsem)))
            self.gbase[g] = 0
        return self.gsem[g]

    def close(self):
        self.es.close()


class Prog:
    max_ops = None

    def __init__(self, ctx):
        self.ctx = ctx
        self.nc = ctx.nc
        self.ops = []
        self.last_writer = {}
        self.readers = {}
        self.dma_groups = {}

    def op(self, eng, fn, reads=(), writes=(), dma_group=None, wait_total=False):
        if self.max_ops is not None and len(self.ops) >= self.max_ops:
            return None
        reads = [(k + str(CURP[0])) if k in PARKEYS else k for k in reads]
        writes = [(k + str(CURP[0])) if k in PARKEYS else k for k in writes]
        reads = [ALIAS.get(k, k) for k in reads]
        writes = [ALIAS.get(k, k) for k in writes]
        if eng != "pe":
            writes = writes + [k for k in reads if k.startswith("PS") and k not in writes]
        deps = set()
        for b in reads:
            if b in self.last_writer:
                deps.add(self.last_writer[b])
        for b in writes:
            if b in self.last_writer:
                deps.add(self.last_writer[b])
            for r in self.readers.get(b, ()):
                deps.add(r)
        idx = len(self.ops)
        if dma_group is not None:
            deps = {d for d in deps if self.ops[d]["dma"] != dma_group}
        o = dict(eng=eng, fn=fn, deps=sorted(deps), dma=dma_group, idx=idx)
        if dma_group is not None:
            g = self.dma_groups.setdefault(dma_group, dict(total=0, wait_total=wait_total))
            g["total"] += 1
            o["dma_cnt"] = g["total"]
        self.ops.append(o)
        for b in reads:
            self.readers.setdefault(b, []).append(idx)
        for b in writes:
            self.last_writer[b] = idx
            self.readers[b] = []
        return idx

    def finalize(self):
        nc = self.nc
        ctx = self.ctx
        ops = self.ops
        needed = set()
        for o in ops:
            best = {}
            rd = []
            for d in o["deps"]:
                p = ops[d]
                if p["dma"] is not None:
                    rd.append(d)
                else:
                    if p["eng"] == "pe" and o["eng"] == "pe" and o["dma"] is None:
                        continue
                    best[p["eng"]] = max(best.get(p["eng"], -1), d)
            rd.extend(best.values())
            o["deps"] = sorted(rd)
            for d in best.values():
                needed.add(d)
        engs = ctx.engs
        last = {}
        for o in ops:
            if o["dma"] is None:
                last[o["eng"]] = o["idx"]
        needed |= set(last.values())
        cnt = dict(ctx.ecnt)
        for o in ops:
            if o["dma"] is None and o["idx"] in needed:
                cnt[o["eng"]] += 1
                o["sig"] = cnt[o["eng"]]
        for g in self.dma_groups:
            ctx.group_sem(g)
        phase = ctx.phase
        with nc.Block() as block:

            def emit_engine(ename, eng):
                known = {}
                if phase > 0:
                    eng.wait_ge(ctx.bsem, phase)
                for o in ops:
                    if o["eng"] != ename:
                        continue
                    for d in o["deps"]:
                        p = ops[d]
                        if p["dma"] is not None:
                            g = self.dma_groups[p["dma"]]
                            sem = ctx.gsem[p["dma"]]
                            val = ctx.gbase[p["dma"]] + 16 * (g["total"] if g["wait_total"] else p["dma_cnt"])
                            key = ("g", p["dma"])
                        else:
                            if p["eng"] == "pe" and ename == "pe" and o["dma"] is None:
                                continue
                            sem = ctx.esem[p["eng"]]
                            val = p["sig"]
                            key = ("e", p["eng"])
                        if known.get(key, 0) >= val:
                            continue
                        known[key] = val
                        eng.wait_ge(sem, val)
                    ins = o["fn"](eng)
                    if o["dma"] is not None:
                        ins.then_inc(ctx.gsem[o["dma"]], 16)
                    elif "sig" in o:
                        ins.then_inc(ctx.esem[ename], 1)
                if ename == "sp":
                    for e2 in engs:
                        if cnt[e2] > ctx.ecnt[e2]:
                            eng.wait_ge(ctx.esem[e2], cnt[e2])
                    for g, info in self.dma_groups.items():
                        eng.wait_ge(ctx.gsem[g], ctx.gbase[g] + 16 * info["total"])
                    eng.sem_inc(ctx.bsem, 1)

            @block.tensor
            def _(e):
                emit_engine("pe", e)

            @block.scalar
            def _(e):
                emit_engine("act", e)

            @block.vector
            def _(e):
                emit_engine("dve", e)

            @block.gpsimd
            def _(e):
                emit_engine("pool", e)

            @block.sync
            def _(e):
                emit_engine("sp", e)

        ctx.ecnt = cnt
        for g, info in self.dma_groups.items():
            ctx.gbase[g] += 16 * info["total"]
        ctx.phase += 1


STOP_EARLY = True


class _StopBuild(Exception):
    pass


def build_program(do_sample=True, debug=False):
    nc = bass.Bass("TRN2", target_bir_lowering=False)
    try:
        return _build_program(nc, do_sample, debug)
    except _StopBuild:
        return nc


def _build_program(nc, do_sample, debug):
    dbg = nc.dram_tensor("dbg", [128, 16, 512], F32, kind="ExternalOutput").ap() if debug else None
    din = lambda n, s: nc.dram_tensor(n, s, F32, kind="ExternalInput").ap()
    dout = lambda n, s: nc.dram_tensor(n, s, F32, kind="ExternalOutput").ap()
    xp = din("xp", [T, D])
    xs = din("xs", [NS, D])
    w_in = din("w_in", [D, INW])
    w_out = din("w_out", [D, D])
    mlp_up = din("mlp_up", [D, DFF])
    mlp_down = din("mlp_down", [DFF, D])
    packA_d = din("packA", [128, NA])
    mu_d = din("mu_b", [128, RWW])
    nfw_d = din("nfw_b", [128, D])
    wup_d = din("luw", [128, 512])
    gup_d = din("gup", [128, 512])
    spk_d = din("spack", [128, NSP])
    sC_d = din("sC", [128, 4096])
    sn_d = din("sn", [128, 64])
    sm_d = din("sm", [128, 1])
    sconv_d = din("sconv", [128, 2, 3, 64])
    sS_d = din("sS", [128, 4096])
    sshift_d = din("sshift", [128, 3, 64])
    sshl_d = din("sshl", [NS, 256])
    mul_d = din("mul", [NS, 256])

    yp = dout("yp", [T, D])
    ys = dout("ys", [NS, D])
    oC = dout("oC", [128, 4, 65])
    om = dout("om", [8, 1])
    oconv = dout("oconv", [128, 8, 3])
    oS = dout("oS", [128, 4, 64])
    oshift = dout("oshift", [1, RWW])
    osC = dout("osC", [128, 4096])
    osn = dout("osn", [128, 64])
    osm = dout("osm", [128, 1])
    osconv = dout("osconv", [128, 2, 3, 64])
    osS = dout("osS", [128, 4096])
    osshift = dout("osshift", [128, 3, 64])
    osshl = dout("osshl", [NS, 256])

    mix_d = nc.dram_tensor("mix_scr", [T + NS, D], BF16, kind="Internal").ap()
    scr1 = nc.dram_tensor("scr1", [128, 8, 64], F32, kind="Internal").ap()
    scr2 = nc.dram_tensor("scr2", [128, 3, 64], F32, kind="Internal").ap()
    scr3 = nc.dram_tensor("scr3", [NS, 2, 8, 64], F32, kind="Internal").ap()

    ctx = SemCtx(nc)
    PP = [Prog(ctx)]
    es_res = contextlib.ExitStack()
    cur = [es_res]

    def TT(name, shape, dt=F32):
        return cur[0].enter_context(nc.sbuf_tensor("t_" + name, list(shape), dt))

    def dma(q, out, in_, reads, writes, group, wait_total=False):
        PP[0].op(q, lambda e: e.dma_start(out=out, in_=in_), reads=reads, writes=writes, dma_group=group, wait_total=wait_total)

    def dve(fn, r, w):
        PP[0].op("dve", fn, reads=r, writes=w)

    def act(fn, r, w):
        PP[0].op("act", fn, reads=r, writes=w)

    def pool(fn, r, w):
        PP[0].op("pool", fn, reads=r, writes=w)

    def pe(fn, r, w):
        PP[0].op("pe", fn, reads=r, writes=w)

    def mm(out, lhsT, rhs, start, stop, r, w):
        pe(lambda e: e.matmul(out, lhsT=lhsT, rhs=rhs, start=start, stop=stop), r, w)

    es_ps = contextlib.ExitStack()
    PS = [es_ps.enter_context(nc.psum_tensor("PS%d" % i, [128, 1024], F32)) for i in range(4)]
    PSB = [p.bitcast(BF16) for p in PS]
    psi = [0]

    def nps():
        i = psi[0] % 4
        psi[0] += 1
        return PS[i], PSB[i], "PS%d" % i

    wq = TT("wq", [128, 8, INW], BF16)
    mub = TT("mub", [128, RWW], F32)
    luw = TT("luw", [128, 512], BF16)
    gup = TT("gup", [128, 512], BF16)
    pA = TT("pA", [128, NA], F32)
    identb = TT("identb", [128, 128], BF16)

    def PA(n):
        a, b = OFF[n]
        return pA[:, a:b]

    w_in_v = w_in.rearrange("(c p) n -> p c n", p=128)
    dma("sp", pA[:], packA_d, [], ["pA"], "init", True)
    for c in range(8):
        dma("pool", wq[:, c, :], w_in_v[:, c, :], [], ["wq"], "init", True)
    dma("pool", luw[:], wup_d, [], ["luw"], "init", True)
    dma("pool", gup[:], gup_d, [], ["gup"], "init", True)
    w_out_v = w_out.rearrange("(c p) n -> p c n", p=128)
    dve(lambda e: e.tensor_copy(out=identb[:], in_=PA("ident")), ["pA"], ["identb"])


    def _dbgdump(tag):
        if debug != tag:
            return
        dstg_ = cur[0].enter_context(nc.sbuf_tensor("t_dbgst%d" % tag, [128, 512], F32))
        def dd(slot, ap, key, n):
            dve(lambda e: e.tensor_copy(out=dstg_[:, 0:n], in_=ap), [key], ["dbgst"])
            dma("sp", dbg[:, slot, 0:n], dstg_[:, 0:n], ["dbgst"], [], "dbg")
        dd(0, PA("ident"), "pA", 128)
        dd(1, PA("mui"), "pA", 128)
        dd(2, PA("mnw"), "pA", 512)
        dd(3, PA("w0"), "pA", 512)
        dd(4, PA("lnb"), "pA", 512)
        PP[0].max_ops = len(PP[0].ops)
        PP[0].finalize()
        raise _StopBuild()
    _dbgdump(3)
    dma("sp", mub[:], mu_d, [], ["mub"], "init", True)
    PP[0].finalize()
    PP[0] = Prog(ctx)

    def rmsnorm_T(xt, xk, nt, dstT, dstk, col0, wname, tmpb, tmpbk, junk, junkk, st, stk):
        act(lambda e: e.activation(out=junk[:nt, :], in_=xt[:nt, :], func=AF.Square, accum_out=st[:nt, 0:1]), [xk], [junkk, stk])
        act(lambda e: e.activation(out=st[:nt, 1:2], in_=st[:nt, 0:1], func=AF.Sqrt, bias=EPS, scale=1.0 / D), [stk], [stk])
        dve(lambda e: e.reciprocal(out=st[:nt, 2:3], in_=st[:nt, 1:2]), [stk], [stk])
        dve(lambda e: e.tensor_scalar_mul(out=tmpb[:nt, :], in0=xt[:nt, :], scalar1=st[:nt, 2:3]), [xk, stk], [tmpbk])
        ps, psb, pk = nps()
        for c in range(8):
            pe(lambda e, c=c: e.transpose(psb[:, c * 128:c * 128 + nt], tmpb[:nt, c * 128:(c + 1) * 128], identb[:nt, :nt]), [tmpbk, "identb"], [pk])
        a, b_ = OFF[wname]
        dve(lambda e: e.tensor_tensor(out=dstT[:, :, col0:col0 + nt],
                                      in0=psb[:, 0:1024].rearrange("p (c t) -> p c t", c=8)[:, :, 0:nt],
                                      in1=pA[:, a:b_].unsqueeze(2).to_broadcast([128, 8, nt]), op=ALU.mult), [pk, "pA"], [dstk])

    def head_ml(nt, hsrc, hk, osig, ok, mix, mixk, W, sfx=""):
        tA, tB, s8 = W["tA"], W["tB"], W["s8"]
        h3 = lambda t: t[:nt, :].rearrange("p (h d) -> p h d", h=8)
        bc = lambda t, c: t[:nt, c:c + 8].unsqueeze(2).to_broadcast([nt, 8, 64])
        dve(lambda e: e.tensor_tensor(out=tA[:nt, :], in0=hsrc[:nt, :], in1=osig[:nt, :], op=ALU.mult), [hk, ok], ["tA" + sfx])
        dve(lambda e: e.tensor_tensor(out=tB[:nt, :], in0=tA[:nt, :], in1=tA[:nt, :], op=ALU.mult), ["tA" + sfx], ["tB" + sfx])
        dve(lambda e: e.tensor_reduce(out=s8[:nt, 0:8], in_=h3(tB), axis=AX.X, op=ALU.add), ["tB" + sfx], ["s8" + sfx])
        act(lambda e: e.activation(out=s8[:nt, 8:16], in_=s8[:nt, 0:8], func=AF.Sqrt, bias=EPS, scale=1.0 / 64), ["s8" + sfx], ["s8" + sfx])
        dve(lambda e: e.reciprocal(out=s8[:nt, 16:24], in_=s8[:nt, 8:16]), ["s8" + sfx], ["s8" + sfx])
        dve(lambda e: e.tensor_tensor(out=h3(tB), in0=h3(tA), in1=bc(s8, 16), op=ALU.mult), ["tA" + sfx, "s8" + sfx], ["tB" + sfx])
        dve(lambda e: e.tensor_tensor(out=mix[:nt, 0:512], in0=tB[:nt, :], in1=PA("mnw")[:nt, :], op=ALU.mult), ["tB" + sfx, "pA"], [mixk])

    def head_rw(nt, ysrc, yk, bon, bonk, vf, vfk, g, gk, mix, mixk, W):
        tA, tB, s8 = W["tA"], W["tB"], W["s8"]
        h3 = lambda t: t[:nt, :].rearrange("p (h d) -> p h d", h=8)
        bc = lambda t, c: t[:nt, c:c + 8].unsqueeze(2).to_broadcast([nt, 8, 64])
        dve(lambda e: e.tensor_tensor(out=h3(tA), in0=h3(vf), in1=bc(bon, 0), op=ALU.mult), [vfk, bonk], ["tA"])
        dve(lambda e: e.tensor_tensor(out=tA[:nt, :], in0=tA[:nt, :], in1=ysrc[:nt, :], op=ALU.add), ["tA", yk], ["tA"])
        dve(lambda e: e.tensor_reduce(out=s8[:nt, 24:32], in_=h3(tA), axis=AX.X, op=ALU.add), ["tA"], ["s8"])
        dve(lambda e: e.tensor_scalar_mul(out=s8[:nt, 24:32], in0=s8[:nt, 24:32], scalar1=1.0 / 64), ["s8"], ["s8"])
        dve(lambda e: e.tensor_tensor(out=h3(tA), in0=h3(tA), in1=bc(s8, 24), op=ALU.subtract), ["tA", "s8"], ["tA"])
        dve(lambda e: e.tensor_tensor(out=tB[:nt, :], in0=tA[:nt, :], in1=tA[:nt, :], op=ALU.mult), ["tA"], ["tB"])
        dve(lambda e: e.tensor_reduce(out=s8[:nt, 32:40], in_=h3(tB), axis=AX.X, op=ALU.add), ["tB"], ["s8"])
        act(lambda e: e.activation(out=s8[:nt, 40:48], in_=s8[:nt, 32:40], func=AF.Sqrt, bias=GN_EPS, scale=1.0 / 64), ["s8"], ["s8"])
        dve(lambda e: e.reciprocal(out=s8[:nt, 48:56], in_=s8[:nt, 40:48]), ["s8"], ["s8"])
        dve(lambda e: e.tensor_tensor(out=h3(tB), in0=h3(tA), in1=bc(s8, 48), op=ALU.mult), ["tA", "s8"], ["tB"])
        dve(lambda e: e.tensor_tensor(out=tB[:nt, :], in0=tB[:nt, :], in1=PA("lnw")[:nt, :], op=ALU.mult), ["tB", "pA"], ["tB"])
        dve(lambda e: e.tensor_tensor(out=tB[:nt, :], in0=tB[:nt, :], in1=PA("lnb")[:nt, :], op=ALU.add), ["tB", "pA"], ["tB"])
        dve(lambda e: e.tensor_tensor(out=mix[:nt, 512:1024], in0=tB[:nt, :], in1=g[:nt, :], op=ALU.mult), ["tB", gk], [mixk])

    def out_proj(nt, mix, mixk, xt, xk, row0, W):
        dma("pool", mix_d[row0:row0 + nt, :], mix[:nt, :], [mixk], [], "mixst")

    with contextlib.ExitStack() as es1:
        cur[0] = es1
        W = {}
        Gbig = TT("Gbig", [128, 24, 512])
        G = [Gbig[:, i, :] for i in range(24)]
        W["s8"] = TT("s8", [128, 64])
        W["xm"] = TT("xm", [128, D])
        xt = [TT("xt0", [128, D])] * 2
        junk = W["xm"]
        st = TT("st", [128, 4])
        mix = TT("mix", [128, D], BF16)
        xsb = TT("xsb", [128, D], BF16)
        xnT = [TT("xnT0", [128, 8, 129], BF16)] * 2
        dxT = TT("dxT", [128, 8, 128], BF16)
        qkx = [TT("qkx0", [128, 8, 131])] * 2
        cacc = Gbig[:, 16:18, :].rearrange("p a (c t) -> p (a c) t", t=128)
        ctmp = W["xm"][:, :].rearrange("p (c t) -> p c t", c=8)
        qks = cacc
        QPB = [TT("qpb%d" % i, [128, 4, 128], BF16) for i in range(2)]
        KTB = [TT("kTb%d" % i, [128, 4, 128], BF16) for i in range(2)]
        KTM = [TT("ktm%d" % i, [128, 8, 64], BF16) for i in range(2)]
        VAUG = [TT("vaug%d" % i, [128, 8, 65], BF16) for i in range(2)]
        GT = [TT("gt%d" % i, [128, 96]) for i in range(2)]
        runmax = TT("runmax", [128, 8])
        nBc = TT("nBc", [128, 8])
        Cst = TT("Cst", [128, 4, 65])
        Cbf = TT("Cbf", [128, 4, 65], BF16)
        Fb = G[15].rearrange("p (j t) -> p j t", j=4)
        hml = G[14]
        nlrep = G[15].rearrange("p (h d) -> p h d", h=8)
        wsig, a_sb, g_sb, kap, ktl, bvec, e1, e2, e3, pcw, ysb = G[3], G[4], G[5], G[6], G[7], G[8], G[9], G[10], G[11], G[12], G[11]
        W["tA"], W["tB"] = G[9], G[10]
        WM = {"tA": G[18], "tB": G[19], "s8": TT("s8m", [128, 64])}
        r8b = TT("r8b", [128, 8])
        VRW = [TT("vrw%d" % i, [128, 8, 64], BF16) for i in range(2)]
        LOR = [TT("lor%d" % i, [128, 256], BF16) for i in range(2)]
        lorT = TT("lorT", [128, 2, 128], BF16)
        r8 = TT("r8", [128, 32])
        bon = TT("bon", [128, 8])
        TMb = TT("TMb", [128, 4, 512], BF16)
        W["mixT"] = TMb[:, 0:2, :].rearrange("p a (c t) -> p (a c) t", t=128)
        Btz = TT("Btz", [128, 8, 128], BF16)
        Ktz = TT("Ktz", [128, 8, 128], BF16)
        FMt = TT("FMt", [128, 4, 4, 128], BF16)
        Am = [TT("Am%d" % i, [128, 8, 128], BF16) for i in range(3)]
        PTb = TT("PTb", [128, 8, 128], BF16)
        Pw = [TT("Pw%d" % i, [128, 8, 128], BF16) for i in range(4)]
        Zb = [TT("Zb%d" % i, [128, 8, 64], BF16) for i in range(2)]
        Ub = Zb[1]
        Sst = TT("Sst", [128, 4, 64])
        Sbf = TT("Sbf", [128, 4, 64], BF16)
        WLfm = TT("WLfm", [128, 4])
        plast = G[17]

        for p_ in range(2):
            pool(lambda e, p_=p_: e.memset(VAUG[p_][:], 1.0), [], ["vaug%d" % p_])
        pool(lambda e: e.memset(Btz[:], 0.0), [], ["Btz"])
        pool(lambda e: e.memset(Ktz[:], 0.0), [], ["Ktz"])
        pool(lambda e: e.memset(Cst[:], 0.0), [], ["Cst"])
        pool(lambda e: e.memset(Cbf[:], 0.0), [], ["Cbf"])
        pool(lambda e: e.memset(Sst[:], 0.0), [], ["Sst"])
        pool(lambda e: e.memset(Sbf[:], 0.0), [], ["Sbf"])
        pool(lambda e: e.memset(runmax[:], -1e30), [], ["runmax"])
        pool(lambda e: e.memset(nBc[:], 0.0), [], ["nBc"])
        pool(lambda e: e.memset(xnT[0][:, :, 0:1], 0.0), [], ["xnT0"])
        pool(lambda e: e.memset(qkx[0][:, :, 0:3], 0.0), [], ["qkx0"])

        if debug == 2:
            dstg = TT("dbgstage", [128, 512]) if False else G[12]
            def ddump0(slot, ap, key, n):
                dve(lambda e: e.tensor_copy(out=dstg[:, 0:n], in_=ap), [key], ["pcw"])
                dma("sp", dbg[:, slot, 0:n], dstg[:, 0:n], ["pcw"], [], "dbg")
            ddump0(0, PA("ident"), "pA", 128)
            ddump0(1, PA("mui"), "pA", 128)
            ddump0(2, luw[:, :], "luw", 512)
            ddump0(3, gup[:, :], "gup", 512)
            ddump0(4, W1[:, 0, 0:512], "W1", 512)
            PP[0].max_ops = len(PP[0].ops)
            if STOP_EARLY:
                PP[0].finalize()
                raise _StopBuild()
        MUI = PA("mui")
        MUS = PA("mus")
        MLS = PA("mls")
        ONES = PA("ones")
        IDF = PA("ident")
        bc8 = lambda ap: ap.unsqueeze(2).to_broadcast([128, 8, 64])
        m8 = lambda m: m.unsqueeze(1).to_broadcast([128, 8, 128])
        v3 = lambda t: t[:].rearrange("p (h d) -> p h d", h=8)
        hoff = lambda h: (h % 2) * 512 + (h // 2) * 128

        def make_block(b):
            p_ = b % 2
            r_sb, kf_sb, vf, osig = G[0 + 20 * p_] if p_ == 0 else G[20], G[1] if p_ == 0 else G[21], G[2] if p_ == 0 else G[22], G[13] if p_ == 0 else G[23]
            vaug, gt, qpb, kTb, ktm, vrw, lor = VAUG[p_], GT[p_], QPB[p_], KTB[p_], KTM[p_], VRW[p_], LOR[p_]

            def front_stage():
                x_ = xt[b % 2]
                xk = "xt%d" % (b % 2)
                xn = xnT[b % 2]
                xnk = "xnT%d" % (b % 2)
                qx = qkx[b % 2]
                qxk = "qkx%d" % (b % 2)
                dma("sp", x_[:], xp[b * 128:(b + 1) * 128, :], [], [xk], xk)
                rmsnorm_T(x_, xk, 128, xn, xnk, 1, "nmw", xsb, "xsb", junk, "junk", st, "st")
                cur_x = xn[:, :, 1:129]
                prv_x = xn[:, :, 0:128]
                dve(lambda e, xn=xn: e.tensor_tensor(out=dxT[:], in0=xn[:, :, 0:128], in1=xn[:, :, 1:129], op=ALU.subtract), [xnk], ["dxT"])

                yield
                ps, psb, pk = nps()
                for j in range(8):
                    for c in range(8):
                        mm(ps[:, j * 128:(j + 1) * 128], wq[:, c, j * 128:(j + 1) * 128], cur_x[:, c, :], c == 0, c == 7, ["wq", xnk], [pk])
                for a_ in range(2):
                    act(lambda e, ps=ps, qx=qx, a_=a_: e.copy(out=qx[:, 4 * a_:4 * a_ + 4, 3:131], in_=ps[:, a_ * 512:(a_ + 1) * 512].rearrange("p (j t) -> p j t", j=4)), [pk], [qxk])
                if b == NB - 1:
                    dma("pool", oconv, qx[:, :, 128:131], [qxk], [], "fin")

                yield
                def tm_plain(col0, ncol, ps_ap, pk):
                    for c in range(8):
                        mm(ps_ap, cur_x[:, c, :], wq[:, c, col0:col0 + ncol], c == 0, c == 7, [xnk, "wq"], [pk])

                def tm_shift(col0, ncol, dst, dk):
                    ps, psb, pk = nps()
                    for c in range(8):
                        mm(ps[:, 0:ncol], cur_x[:, c, :], wq[:, c, MLW + col0:MLW + col0 + ncol], c == 0, c == 7, [xnk, "wq"], [pk])
                    for c in range(8):
                        mm(ps[:, 512:512 + ncol], dxT[:, c, :], wq[:, c, MLW + col0:MLW + col0 + ncol], c == 0, c == 7, ["dxT", "wq"], [pk])
                    dve(lambda e, ps=ps: e.tensor_tensor(out=dst, in0=ps[:, 512:512 + ncol], in1=mub[:, col0:col0 + ncol], op=ALU.mult), [pk, "mub"], [dk])
                    dve(lambda e, ps=ps: e.tensor_tensor(out=dst, in0=dst, in1=ps[:, 0:ncol], op=ALU.add), [pk, dk], [dk])

                ps, psb, pk = nps()
                tm_plain(1024, 512, ps[:, 0:512], pk)
                tm_plain(1536, 512, ps[:, 512:1024], pk)
                act(lambda e, ps=ps: e.copy(out=vaug[:, :, 0:64], in_=ps[:, 0:512].rearrange("p (h d) -> p h d", h=8)), [pk], ["vaug"])
                act(lambda e, ps=ps: e.activation(out=osig[:], in_=ps[:, 512:1024], func=AF.Sigmoid), [pk], ["osig"])
                ps, psb, pk = nps()
                tm_plain(2048, 16, ps[:, 0:16], pk)
                dve(lambda e, ps=ps: e.tensor_tensor(out=gt[:, 0:16], in0=ps[:, 0:16], in1=PA("ifb"), op=ALU.add), [pk, "pA"], ["gt"])
                tm_shift(0, 512, r_sb[:], "r_sb")
                tm_shift(512, 512, kf_sb[:], "kf_sb")
                tm_shift(1024, 512, vf[:], "vf")
                pool(lambda e: e.tensor_copy(out=vrw[:], in_=vf[:].rearrange("p (h d) -> p h d", h=8)), ["vf"], ["vrw"])
                ltmp = G[16]
                tm_shift(1536, 256, ltmp[:, 0:256], "cacc")
                act(lambda e: e.activation(out=lor[:, 0:64], in_=ltmp[:, 0:64], func=AF.Tanh), ["cacc"], ["lor"])
                act(lambda e: e.copy(out=lor[:, 64:128], in_=ltmp[:, 64:128]), ["cacc"], ["lor"])
                act(lambda e: e.activation(out=lor[:, 128:256], in_=ltmp[:, 128:256], func=AF.Sigmoid), ["cacc"], ["lor"])
                if b == NB - 1:
                    lastc = xn[:, :, 128:129]
                    for n0 in range(0, RWW, 512):
                        nn = min(512, RWW - n0)
                        ps2, _, pk2 = nps()
                        for c in range(8):
                            mm(ps2[0:1, 0:nn], lastc[:, c, :], wq[:, c, MLW + n0:MLW + n0 + nn], c == 0, c == 7, [xnk, "wq"], [pk2])
                        act(lambda e, ps2=ps2, n0=n0, nn=nn: e.copy(out=plast[0:1, 0:nn], in_=ps2[0:1, 0:nn]), [pk2], ["plast"])
                        dma("pool", oshift[:, n0:n0 + nn], plast[0:1, 0:nn], ["plast"], [], "fin")

                act(lambda e: e.activation(out=gt[:, 56:64], in_=gt[:, 8:16], func=AF.Exp, scale=-1.0), ["gt"], ["gt"])
                act(lambda e: e.activation(out=gt[:, 16:24], in_=gt[:, 56:64], func=AF.Ln, bias=1.0, scale=1.0), ["gt"], ["gt"])
                dve(lambda e: e.tensor_copy(out=nlrep[:], in_=bc8(gt[:, 16:24])), ["gt"], ["nlrep"])
                ps, psb, pk = nps()
                mm(ps[:, 0:8], MUI, gt[:, 16:24], True, True, ["pA", "gt"], [pk])
                mm(ps[:, 8:16], ONES, gt[:, 16:24], True, True, ["pA", "gt"], [pk])
                for j in range(4):
                    mm(ps[:, 512 + j * 128:512 + (j + 1) * 128], nlrep[:, 2 * j:2 * j + 2, :].rearrange("p a d -> p (a d)"), MUI, True, True, ["nlrep", "pA"], [pk])
                dve(lambda e, ps=ps: e.tensor_tensor(out=gt[:, 24:32], in0=ps[:, 0:8], in1=gt[:, 0:8], op=ALU.add), [pk, "gt"], ["gt"])
                act(lambda e: e.activation(out=gt[:, 32:40], in_=gt[:, 24:32], func=AF.Exp), ["gt"], ["gt"])
                dve(lambda e, ps=ps: e.tensor_tensor(out=gt[:, 56:64], in0=gt[:, 24:32], in1=ps[:, 8:16], op=ALU.subtract), [pk, "gt"], ["gt"])
                act(lambda e: e.activation(out=gt[:, 40:48], in_=gt[:, 56:64], func=AF.Exp), ["gt"], ["gt"])
                act(lambda e, ps=ps: e.activation(out=gt[:, 48:56], in_=ps[:, 8:16], func=AF.Exp, scale=-1.0), [pk], ["gt"])
                act(lambda e, ps=ps: e.activation(out=Fb[:], in_=ps[:, 512:1024].rearrange("p (j t) -> p j t", j=4), func=AF.Exp, scale=-1.0), [pk], ["Fb"])
                dve(lambda e: e.tensor_tensor(out=gt[:, 56:64], in0=gt[:, 24:32], in1=nBc[:], op=ALU.add), ["gt", "nBc"], ["gt"])
                dve(lambda e: e.tensor_tensor(out=runmax[:], in0=runmax[:], in1=gt[:, 56:64], op=ALU.max), ["gt", "runmax"], ["runmax"])
                dve(lambda e, ps=ps: e.tensor_tensor(out=nBc[:], in0=nBc[:], in1=ps[:, 8:16], op=ALU.add), [pk, "nBc"], ["nBc"])

                yield
                cwv = PA("cw").rearrange("p (c j) -> p c j", j=4)
                wbc = lambda j: cwv[:, :, j:j + 1].to_broadcast([128, 8, 128])
                pool(lambda e, qx=qx: e.tensor_tensor(out=cacc[:], in0=qx[:, :, 3:131], in1=wbc(3), op=ALU.mult), [qxk, "pA"], ["cacc"])
                for j in range(3):
                    pool(lambda e, qx=qx, j=j: e.tensor_tensor(out=ctmp[:], in0=qx[:, :, j:j + 128], in1=wbc(j), op=ALU.mult), [qxk, "pA"], ["ctmp"])
                    pool(lambda e: e.tensor_tensor(out=cacc[:], in0=cacc[:], in1=ctmp[:], op=ALU.add), ["cacc", "ctmp"], ["cacc"])
                pool(lambda e: e.tensor_tensor(out=cacc[:], in0=cacc[:], in1=PA("cb").unsqueeze(2).to_broadcast([128, 8, 128]), op=ALU.add), ["cacc", "pA"], ["cacc"])
                act(lambda e: e.activation(out=qks[:], in_=cacc[:], func=AF.Silu), ["cacc"], ["qks"])
                dve(lambda e: e.tensor_tensor(out=qpb[:], in0=qks[:, 0:4, :], in1=Fb[:], op=ALU.mult), ["qks", "Fb"], ["qpb"])
                act(lambda e: e.activation(out=kTb[:], in_=qks[:, 4:8, :], func=AF.Copy, scale=0.125), ["qks"], ["kTb"])

                yield
                ps, psb, pk = nps()
                for j in range(4):
                    pe(lambda e, j=j, psb=psb: e.transpose(psb[:, j * 128:(j + 1) * 128], kTb[:, j, :], identb[:]), ["kTb", "identb"], [pk])
                dve(lambda e, psb=psb: e.tensor_tensor(out=ktm[:], in0=psb[:, 0:512].rearrange("p (h d) -> p h d", h=8), in1=bc8(gt[:, 40:48]), op=ALU.mult), [pk, "gt"], ["ktm"])

                yield
                yield
                if b + 1 < NB:
                    pool(lambda e, xn=xn: e.tensor_copy(out=xn[:, :, 0:1], in_=xn[:, :, 128:129]), [xnk], [xnk])
                    pool(lambda e, qx=qx: e.tensor_copy(out=qx[:, :, 0:3], in_=qx[:, :, 128:131]), [qxk], [qxk])
                yield

            def ml_stage():
                ps, psb, pk = nps()
                for h in range(8):
                    j, hp = h // 2, h % 2
                    sl = slice(hp * 64, hp * 64 + 64)
                    mm(ps[:, hoff(h):hoff(h) + 128], kTb[sl, j, :], qpb[sl, j, :], True, True, ["kTb", "qpb"], [pk])
                for h in range(8):
                    dve(lambda e, h=h, ps=ps: e.scalar_tensor_tensor(out=PTb[:, h, :], in0=ps[:, hoff(h):hoff(h) + 128], scalar=gt[:, 32 + h:33 + h], in1=MUI, op0=ALU.mult, op1=ALU.mult), [pk, "gt", "pA"], ["PTb"])
                yield
                ps, psb, pk = nps()
                psn = lambda ps, h: ps[:, (h // 4) * 512 + (h % 4) * 65:(h // 4) * 512 + (h % 4) * 65 + 65]
                for h in range(8):
                    j, hp = h // 2, h % 2
                    sl = slice(hp * 64, hp * 64 + 64)
                    mm(psn(ps, h), PTb[:, h, :], vaug[:, h, :], True, False, ["PTb", "vaug"], [pk])
                    mm(psn(ps, h), qpb[sl, j, :], Cbf[sl, j, :], False, True, ["qpb", "Cbf"], [pk])
                pn4 = ps[:, :].rearrange("p (a r) -> p a r", a=2)[:, :, 0:260].rearrange("p a (h d) -> p a h d", h=4)
                for a_ in range(2):
                    act(lambda e, pn4=pn4, a_=a_: e.copy(out=r8[:, 4 * a_:4 * a_ + 4], in_=pn4[:, a_, :, 64]), [pk], ["r8"])
                dve(lambda e: e.scalar_tensor_tensor(out=r8[:, 8:16], in0=r8[:, 0:8], scalar=-1.0, in1=r8[:, 0:8], op0=ALU.mult, op1=ALU.max), ["r8"], ["r8"])
                dve(lambda e: e.tensor_scalar_max(out=r8[:, 8:16], in0=r8[:, 8:16], scalar1=1.0), ["r8"], ["r8"])
                dve(lambda e: e.reciprocal(out=r8[:, 16:24], in_=r8[:, 8:16]), ["r8"], ["r8"])
                for a_ in range(2):
                    dve(lambda e, pn4=pn4, a_=a_: e.tensor_tensor(out=hml[:, a_ * 256:(a_ + 1) * 256].rearrange("p (h d) -> p h d", h=4), in0=pn4[:, a_, :, 0:64],
                                                              in1=r8[:, 16 + 4 * a_:20 + 4 * a_].unsqueeze(2).to_broadcast([128, 4, 64]), op=ALU.mult), [pk, "r8"], ["hml"])
                yield
                ps, psb, pk = nps()
                for h in range(8):
                    j = h // 2
                    mm(psn(ps, h), ktm[:, 2 * j:2 * j + 2, :].rearrange("p a d -> p (a d)"), vaug[:, h, :], True, True, ["ktm", "vaug"], [pk])
                pu4 = ps[:, :].rearrange("p (a r) -> p a r", a=2)[:, :, 0:260].rearrange("p a (h d) -> p a h d", h=4)
                for hp in range(2):
                    sl = slice(hp * 64, hp * 64 + 64)
                    decb = gt[sl, 48:56].rearrange("p (j q) -> p j q", q=2)[:, :, hp:hp + 1].to_broadcast([64, 4, 65])
                    dve(lambda e, sl=sl, decb=decb: e.tensor_tensor(out=Cst[sl, :, :], in0=Cst[sl, :, :], in1=decb, op=ALU.mult), ["Cst", "gt"], ["Cst"])
                    for a in range(2):
                        src = pu4[sl, a, hp::2, :]
                        dve(lambda e, sl=sl, a=a, src=src: e.tensor_tensor(out=Cst[sl, 2 * a:2 * a + 2, :], in0=Cst[sl, 2 * a:2 * a + 2, :], in1=src, op=ALU.add), [pk, "Cst"], ["Cst"])
                act(lambda e: e.copy(out=Cbf[:], in_=Cst[:]), ["Cst"], ["Cbf"])
                head_ml(128, hml, "hml", osig, "osig", mix, "mix", WM, "m")


                yield
            def rw_stage():
                ps, psb, pk = nps()
                pe(lambda e, psb=psb: e.transpose(psb[:, 0:128], lor[:, 0:128], identb[:]), ["lor", "identb"], [pk])
                pe(lambda e, psb=psb: e.transpose(psb[:, 128:256], lor[:, 128:256], identb[:]), ["lor", "identb"], [pk])
                act(lambda e, psb=psb: e.copy(out=lorT[:], in_=psb[:, 0:256].rearrange("p (a t) -> p a t", a=2)), [pk], ["lorT"])
                ps, psb, pk = nps()
                mm(ps[:, 0:512], lorT[0:64, 0, :], luw[0:64, :], True, True, ["lorT", "luw"], [pk])
                mm(ps[:, 512:1024], lorT[64:128, 0, :], luw[64:128, :], True, True, ["lorT", "luw"], [pk])
                dve(lambda e, ps=ps: e.tensor_tensor(out=e1[:], in0=ps[:, 0:512], in1=PA("w0"), op=ALU.add), [pk, "pA"], ["e1"])
                act(lambda e: e.activation(out=wsig[:], in_=e1[:], func=AF.Sigmoid), ["e1"], ["wsig"])
                dve(lambda e, ps=ps: e.tensor_tensor(out=e2[:], in0=ps[:, 512:1024], in1=PA("a0"), op=ALU.add), [pk, "pA"], ["e2"])
                act(lambda e: e.activation(out=a_sb[:], in_=e2[:], func=AF.Sigmoid), ["e2"], ["a_sb"])
                ps, psb, pk = nps()
                mm(ps[:, 0:512], lorT[:, 1, :], gup[:, :], True, True, ["lorT", "gup"], [pk])
                act(lambda e, ps=ps: e.copy(out=g_sb[:], in_=ps[:, 0:512]), [pk], ["g_sb"])
                yield
                dve(lambda e: e.tensor_tensor(out=e1[:], in0=kf_sb[:], in1=PA("kk"), op=ALU.mult), ["kf_sb", "pA"], ["e1"])
                dve(lambda e: e.tensor_tensor(out=e2[:], in0=e1[:], in1=e1[:], op=ALU.mult), ["e1"], ["e2"])
                dve(lambda e: e.tensor_reduce(out=r8b[:, 0:8], in_=v3(e2), axis=AX.X, op=ALU.add), ["e2"], ["r8b"])
                dve(lambda e: e.tensor_scalar_max(out=r8b[:, 0:8], in0=r8b[:, 0:8], scalar1=1e-24), ["r8b"], ["r8b"])
                act(lambda e: e.activation(out=r8b[:, 0:8], in_=r8b[:, 0:8], func=AF.Sqrt), ["r8b"], ["r8b"])
                dve(lambda e: e.reciprocal(out=r8b[:, 0:8], in_=r8b[:, 0:8]), ["r8b"], ["r8b"])
                dve(lambda e: e.tensor_tensor(out=v3(kap), in0=v3(e1), in1=bc8(r8b[:, 0:8]), op=ALU.mult), ["e1", "r8b"], ["kap"])
                dve(lambda e: e.tensor_scalar_add(out=e2[:], in0=a_sb[:], scalar1=-1.0), ["a_sb"], ["e2"])
                dve(lambda e: e.tensor_tensor(out=e2[:], in0=e2[:], in1=PA("ka"), op=ALU.mult), ["e2", "pA"], ["e2"])
                dve(lambda e: e.tensor_tensor(out=e2[:], in0=e2[:], in1=kf_sb[:], op=ALU.mult), ["e2", "kf_sb"], ["e2"])
                dve(lambda e: e.tensor_tensor(out=ktl[:], in0=e2[:], in1=kf_sb[:], op=ALU.add), ["e2", "kf_sb"], ["ktl"])
                dve(lambda e: e.tensor_tensor(out=bvec[:], in0=a_sb[:], in1=kap[:], op=ALU.mult), ["a_sb", "kap"], ["bvec"])
                dve(lambda e: e.tensor_tensor(out=e2[:], in0=r_sb[:], in1=ktl[:], op=ALU.mult), ["r_sb", "ktl"], ["e2"])
                dve(lambda e: e.tensor_tensor(out=e2[:], in0=e2[:], in1=PA("rk"), op=ALU.mult), ["e2", "pA"], ["e2"])
                dve(lambda e: e.tensor_reduce(out=bon[:], in_=v3(e2), axis=AX.X, op=ALU.add), ["e2"], ["bon"])
                yield
                ps, psb, pk = nps()
                mm(ps[:, 0:512], MUI, wsig[:], True, True, ["pA", "wsig"], [pk])
                mm(ps[:, 512:1024], ONES, wsig[:], True, True, ["pA", "wsig"], [pk])
                act(lambda e, ps=ps: e.copy(out=pcw[:], in_=ps[:, 0:512]), [pk], ["pcw"])
                dve(lambda e: e.tensor_tensor(out=e1[:], in0=pcw[:], in1=wsig[:], op=ALU.subtract), ["pcw", "wsig"], ["e1"])
                act(lambda e: e.activation(out=e1[:], in_=e1[:], func=AF.Exp, scale=-C0), ["e1"], ["e1"])
                dve(lambda e: e.tensor_tensor(out=TMb[:, 0, :], in0=kap[:], in1=e1[:], op=ALU.mult), ["kap", "e1"], ["TMb0"])
                act(lambda e: e.activation(out=e2[:], in_=pcw[:], func=AF.Exp, scale=-C0), ["pcw"], ["e2"])
                dve(lambda e: e.tensor_tensor(out=TMb[:, 1, :], in0=r_sb[:], in1=e2[:], op=ALU.mult), ["r_sb", "e2"], ["TMb1"])
                act(lambda e: e.activation(out=e3[:], in_=pcw[:], func=AF.Exp, scale=C0), ["pcw"], ["e3"])
                dve(lambda e: e.tensor_tensor(out=TMb[:, 2, :], in0=bvec[:], in1=e3[:], op=ALU.mult), ["bvec", "e3"], ["TMb2"])
                dve(lambda e: e.tensor_tensor(out=TMb[:, 3, :], in0=ktl[:], in1=e3[:], op=ALU.mult), ["ktl", "e3"], ["TMb3"])
                dve(lambda e, ps=ps: e.tensor_tensor(out=e1[:], in0=ps[:, 512:1024], in1=pcw[:], op=ALU.subtract), [pk, "pcw"], ["e1"])
                act(lambda e: e.activation(out=e1[:], in_=e1[:], func=AF.Exp, scale=-C0), ["e1"], ["e1"])
                for hp in range(2):
                    srcb = v3(bvec).rearrange("p (j q) d -> p j q d", q=2)[:, :, hp, :]
                    srck = v3(ktl).rearrange("p (j q) d -> p j q d", q=2)[:, :, hp, :]
                    wl = v3(e1).rearrange("p (j q) d -> p j q d", q=2)[:, :, hp, :]
                    dstb = Btz[:].rearrange("p (j q) c -> p j q c", q=2)[:, :, hp, hp * 64:hp * 64 + 64]
                    dstk = Ktz[:].rearrange("p (j q) c -> p j q c", q=2)[:, :, hp, hp * 64:hp * 64 + 64]
                    dve(lambda e, srcb=srcb, wl=wl, dstb=dstb: e.tensor_tensor(out=dstb, in0=srcb, in1=wl, op=ALU.mult), ["bvec", "e1"], ["Btz"])
                    dve(lambda e, srck=srck, wl=wl, dstk=dstk: e.tensor_tensor(out=dstk, in0=srck, in1=wl, op=ALU.mult), ["ktl", "e1"], ["Ktz"])
                ps2, _, pk2 = nps()
                for j in range(4):
                    mm(ps2[:, j:j + 1], wsig[:, j * 128:(j + 1) * 128], ONES[:, 0:1], True, True, ["wsig", "pA"], [pk2])
                act(lambda e, ps2=ps2: e.activation(out=WLfm[:], in_=ps2[:, 0:4], func=AF.Exp, scale=-C0), [pk2], ["WLfm"])
                yield
                ps, psb, pk = nps()
                for w_ in range(4):
                    for j in range(4):
                        pe(lambda e, w_=w_, j=j, psb=psb: e.transpose(psb[:, (w_ * 4 + j) * 128:(w_ * 4 + j + 1) * 128], TMb[:, w_, j * 128:(j + 1) * 128], identb[:]), ["TMb%d" % w_, "identb"], [pk])
                for w_ in range(4):
                    eng_ = act if w_ % 2 == 0 else dve
                    if w_ % 2 == 0:
                        act(lambda e, psb=psb, w_=w_: e.copy(out=FMt[:, w_, :, :], in_=psb[:, w_ * 512:(w_ + 1) * 512].rearrange("p (j t) -> p j t", j=4)), [pk], ["FMt"])
                    else:
                        dve(lambda e, psb=psb, w_=w_: e.tensor_copy(out=FMt[:, w_, :, :], in_=psb[:, w_ * 512:(w_ + 1) * 512].rearrange("p (j t) -> p j t", j=4)), [pk], ["FMt"])
                KAP, RB, BB, KKB = 0, 1, 2, 3

                def amat(lw, rw_, dst, dk, mask, neg):
                    ps, psb, pk = nps()
                    for h in range(8):
                        j, hp = h // 2, h % 2
                        sl = slice(hp * 64, hp * 64 + 64)
                        mm(ps[:, hoff(h):hoff(h) + 128], FMt[sl, lw, j, :], FMt[sl, rw_, j, :], True, True, ["FMt"], [pk])
                    psv = ps[:, :].rearrange("p (q j t) -> p q j t", q=2, j=4)
                    dstv = dst[:].rearrange("p (j q) t -> p q j t", q=2)
                    mk = mask.unsqueeze(1).unsqueeze(1).to_broadcast([128, 2, 4, 128])
                    if neg:
                        mk3 = mask.unsqueeze(1).to_broadcast([128, 4, 128])
                        for q in range(2):
                            dve(lambda e, q=q: e.scalar_tensor_tensor(out=dstv[:, q], in0=psv[:, q], scalar=-1.0, in1=mk3, op0=ALU.mult, op1=ALU.mult), [pk, "pA"], [dk])
                    else:
                        mk3 = mask.unsqueeze(1).to_broadcast([128, 4, 128])
                        for q in range(2):
                            dve(lambda e, q=q: e.tensor_tensor(out=dstv[:, q], in0=psv[:, q], in1=mk3, op=ALU.mult), [pk, "pA"], [dk])

                amat(BB, KAP, Pw[1], "Pw1", MUS, True)
                amat(KAP, BB, Pw[0], "Pw0", MLS, True)
                amat(KKB, KAP, Am[0], "Am0", MUS, False)
                amat(BB, RB, Am[1], "Am1", MUI, False)
                amat(KKB, RB, Am[2], "Am2", MUI, False)
                yield
                ps, psb, pk = nps()
                for h in range(8):
                    j, hp = h // 2, h % 2
                    sl = slice(hp * 64, hp * 64 + 64)
                    mm(ps[:, h * 64:(h + 1) * 64], FMt[sl, KAP, j, :], Sbf[sl, j, :], True, False, ["FMt", "Sbf"], [pk])
                    mm(ps[:, h * 64:(h + 1) * 64], Am[0][:, h, :], vrw[:, h, :], False, True, ["Am0", "vrw"], [pk])
                act(lambda e, ps=ps: e.copy(out=Zb[0][:], in_=ps[:, 0:512].rearrange("p (h d) -> p h d", h=8)), [pk], ["Zb0"])
                pi = 0
                zi = 0
                for lvl in range(7):
                    yield
                    Pc, PTc = Pw[pi], Pw[pi + 1]
                    Pk, PTk = "Pw%d" % pi, "Pw%d" % (pi + 1)
                    Zc, Zn = Zb[zi], Zb[1 - zi]
                    ps, psb, pk = nps()
                    for h in range(8):
                        mm(ps[:, h * 64:(h + 1) * 64], identb[:], Zc[:, h, :], True, False, ["identb", "Zb%d" % zi], [pk])
                        mm(ps[:, h * 64:(h + 1) * 64], PTc[:, h, :], Zc[:, h, :], False, True, [PTk, "Zb%d" % zi], [pk])
                    if lvl < 6:
                        act(lambda e, ps=ps, Zn=Zn: e.copy(out=Zn[:], in_=ps[:, 0:512].rearrange("p (h d) -> p h d", h=8)), [pk], ["Zb%d" % (1 - zi)])
                        zi = 1 - zi
                        ni = 2 - pi
                        Pn, PTn = Pw[ni], Pw[ni + 1]
                        psA, _, pkA = nps()
                        for h in range(8):
                            mm(psA[:, h * 128:(h + 1) * 128], PTc[:, h, :], Pc[:, h, :], True, True, [PTk, Pk], [pkA])
                        for a_ in range(2):
                            dve(lambda e, psA=psA, Pn=Pn, a_=a_: e.tensor_copy(out=Pn[:, 4 * a_:4 * a_ + 4, :], in_=psA[:, a_ * 512:(a_ + 1) * 512].rearrange("p (h t) -> p h t", h=4)), [pkA], ["Pw%d" % ni])
                        psB, _, pkB = nps()
                        for h in range(8):
                            mm(psB[:, h * 128:(h + 1) * 128], Pc[:, h, :], PTc[:, h, :], True, True, [Pk, PTk], [pkB])
                        for a_ in range(2):
                            act(lambda e, psB=psB, PTn=PTn, a_=a_: e.copy(out=PTn[:, 4 * a_:4 * a_ + 4, :], in_=psB[:, a_ * 512:(a_ + 1) * 512].rearrange("p (h t) -> p h t", h=4)), [pkB], ["Pw%d" % (ni + 1)])
                        pi = ni
                    else:
                        act(lambda e, ps=ps: e.activation(out=Ub[:], in_=ps[:, 0:512].rearrange("p (h d) -> p h d", h=8), func=AF.Copy, scale=-1.0), [pk], ["Ub"])
                yield
                ps, psb, pk = nps()
                for h in range(8):
                    j, hp = h // 2, h % 2
                    sl = slice(hp * 64, hp * 64 + 64)
                    o_ = ps[:, h * 64:(h + 1) * 64]
                    mm(o_, Am[1][:, h, :], Ub[:, h, :], True, False, ["Am1", "Ub"], [pk])
                    mm(o_, Am[2][:, h, :], vrw[:, h, :], False, False, ["Am2", "vrw"], [pk])
                    mm(o_, FMt[sl, RB, j, :], Sbf[sl, j, :], False, True, ["FMt", "Sbf"], [pk])
                act(lambda e, ps=ps: e.copy(out=ysb[:], in_=ps[:, 0:512]), [pk], ["ysb"])
                yield
                ps, psb, pk = nps()
                for j in range(4):
                    o_ = ps[:, j * 64:(j + 1) * 64]
                    mm(o_, Btz[:, 2 * j, :], Ub[:, 2 * j, :], True, False, ["Btz", "Ub"], [pk])
                    mm(o_, Ktz[:, 2 * j, :], vrw[:, 2 * j, :], False, False, ["Ktz", "vrw"], [pk])
                    mm(o_, Btz[:, 2 * j + 1, :], Ub[:, 2 * j + 1, :], False, False, ["Btz", "Ub"], [pk])
                    mm(o_, Ktz[:, 2 * j + 1, :], vrw[:, 2 * j + 1, :], False, True, ["Ktz", "vrw"], [pk])
                dve(lambda e: e.tensor_tensor(out=Sst[:], in0=Sst[:], in1=WLfm[:].unsqueeze(2).to_broadcast([128, 4, 64]), op=ALU.mult), ["Sst", "WLfm"], ["Sst"])
                dve(lambda e, ps=ps: e.tensor_tensor(out=Sst[:], in0=Sst[:], in1=ps[:, 0:256].rearrange("p (j d) -> p j d", j=4), op=ALU.add), [pk, "Sst"], ["Sst"])
                act(lambda e: e.copy(out=Sbf[:], in_=Sst[:]), ["Sst"], ["Sbf"])
                yield
            def tail():
                head_rw(128, ysb, "ysb", bon, "bon", vf, "vf", g_sb, "g_sb", mix, "mix", W)
                out_proj(128, mix, "mix", None, None, b * 128, W)

            return front_stage, ml_stage, rw_stage, tail, p_

        def run_gens(gl):
            gl = list(gl)
            while gl:
                for item in list(gl):
                    CURP[0] = item[1]
                    try:
                        next(item[0])
                    except StopIteration:
                        gl.remove(item)

        blocks = [make_block(b) for b in range(NB)]
        run_gens([(blocks[0][0](), 0)])
        for b in range(NB):
            fr, ml_, rw_, tl, p_ = blocks[b]
            gl = [(ml_(), p_), (rw_(), p_)]
            if b + 1 < NB:
                gl.append((blocks[b + 1][0](), (b + 1) % 2))
            run_gens(gl)
            CURP[0] = p_
            tl()
        CURP[0] = 0

        ps, psb, pk = nps()
        mm(ps[0:8, 0:128], runmax[:], IDF, True, True, ["runmax", "pA"], [pk])
        mm(ps[0:8, 128:256], nBc[:], IDF, True, True, ["nBc", "pA"], [pk])
        fs = TT("fs", [8, 16])
        dve(lambda e, ps=ps: e.tensor_reduce(out=fs[:, 0:1], in_=ps[0:8, 0:128], axis=AX.X, op=ALU.max), [pk], ["fs"])
        dve(lambda e: e.tensor_scalar_max(out=fs[:, 0:1], in0=fs[:, 0:1], scalar1=0.0), ["fs"], ["fs"])
        dve(lambda e, ps=ps: e.tensor_tensor(out=fs[:, 1:2], in0=fs[:, 0:1], in1=ps[0:8, 128:129], op=ALU.subtract), [pk, "fs"], ["fs"])
        dma("pool", om, fs[:, 1:2], ["fs"], [], "fin")
        act(lambda e: e.activation(out=fs[:, 2:3], in_=fs[:, 1:2], func=AF.Exp, scale=-1.0), ["fs"], ["fs"])
        dve(lambda e: e.tensor_scalar_mul(out=fs[:, 4:8], in0=pA[0:8, OFF["rsel"][0]:OFF["rsel"][1]], scalar1=fs[:, 2:3]), ["fs", "pA"], ["fs"])
        ps, psb, pk = nps()
        mm(ps[:, 0:4], pA[0:8, OFF["lsel"][0]:OFF["lsel"][1]], fs[:, 4:8], True, True, ["pA", "fs"], [pk])
        scb = TT("scb", [128, 4])
        act(lambda e, ps=ps: e.copy(out=scb[:], in_=ps[:, 0:4]), [pk], ["scb"])
        dve(lambda e: e.tensor_tensor(out=Cst[:], in0=Cst[:], in1=scb[:].unsqueeze(2).to_broadcast([128, 4, 65]), op=ALU.mult), ["Cst", "scb"], ["Cst"])
        dma("pool", oC, Cst[:], ["Cst"], [], "fin")
        dma("pool", oS, Sst[:], ["Sst"], [], "fin")
        PP[0].finalize()
        PP[0] = Prog(ctx)

    with contextlib.ExitStack() as es_s:
        cur[0] = es_s
        if do_sample:
            W = {}
            W["xm"] = TT("s_xm", [128, D])
            junk = W["xm"]
            st = TT("s_st", [128, 4])
            mix = TT("s_mix", [128, D], BF16)
            xsb = mix
            lor = TT("s_lor", [128, 256], BF16)
            lorT = TT("s_lorT", [128, 2, 128], BF16)
            W["mixT"] = TT("s_mixT", [128, 8, 128], BF16)
            sx = TT("sx", [NS, D])
            sxT = TT("sxT", [128, 8, NS], BF16)
            spj = TT("spj", [NS, INW])
            spk = TT("spk", [128, NSP])
            sl_t = TT("sl_t", [NS, 3, 256])
            Cs = TT("Cs", [128, 4096])
            Ss = Cs
            sn_t = TT("sn_t", [128, 64])
            sm_t = TT("sm_t", [128, 1])
            scv = TT("scv", [128, 2, 4, 64])
            ssh = TT("ssh", [128, 3, 64])
            dma("sp", sx[:], xs, [], ["sx"], "sin", True)
            dma("sp", spk[:], spk_d, [], ["spk"], "sin", True)
            dma("sp", sl_t[:, 0, :], sshl_d, [], ["sl_t"], "sin", True)
            dma("sp", sl_t[:, 1, :], mul_d, [], ["sl_t"], "sin", True)
            dma("sp", Cs[:], sC_d, [], ["Cs"], "sin", True)
            dma("sp", sn_t[:], sn_d, [], ["sn_t"], "sin", True)
            dma("sp", sm_t[:], sm_d, [], ["sm_t"], "sin", True)
            dma("sp", scv[:, :, 0:3, :], sconv_d, [], ["scv"], "sin", True)
            dma("sp", ssh[:], sshift_d, [], ["ssh"], "sin", True)
            rmsnorm_T(sx, "sx", NS, sxT, "sxT", 0, "nmw", xsb, "xsb", junk, "junk", st, "st")
            for n0 in range(0, INW, 512):
                nn = min(512, INW - n0)
                ps, psb, pk = nps()
                for c in range(8):
                    mm(ps[:NS, 0:nn], sxT[:, c, :], wq[:, c, n0:n0 + nn], c == 0, c == 7, ["sxT", "wq"], [pk])
                act(lambda e, ps=ps, n0=n0, nn=nn: e.copy(out=spj[:, n0:n0 + nn], in_=ps[:NS, 0:nn]), [pk], ["spj"])
            s1v = scr1.rearrange("(b h) a d -> b a h d", b=NS)
            for a_ in range(7):
                c0_ = a_ * 512 if a_ < 4 else MLW + (a_ - 4) * 512
                dma("pool", s1v[:, a_, :, :], spj[:, c0_:c0_ + 512].rearrange("p (h d) -> p h d", h=8), ["spj"], ["scr1"], "scrw1")
            A7 = TT("A7", [128, 8, 64])
            dma("sp", A7[:, 0:7, :], scr1[:, 0:7, :], ["scr1"], ["A7"], "scrr1")
            gif = TT("gif", [128, 2])
            s_if = nc.dram_tensor("scr_if", [2, 128], F32, kind="Internal").ap()
            for g_ in range(2):
                dma("pool", s_if[g_, :].rearrange("(b h) -> b h", b=NS), spj[:, 2048 + 8 * g_:2056 + 8 * g_], ["spj"], ["scr_if"], "scrwif")
            for g_ in range(2):
                dma("sp", gif[:, g_:g_ + 1], s_if[g_, :].rearrange("(p o) -> p o", o=1), ["scr_if"], ["gif"], "scrrif")
            SP_ = lambda n: spk[:, SOFF[n][0]:SOFF[n][1]]
            pl = spj[:, MLW + 1536:MLW + 1792]
            dma("pool", osshl, pl, ["spj"], [], "fin")
            dve(lambda e: e.tensor_tensor(out=sl_t[:, 2, :], in0=sl_t[:, 0, :], in1=pl, op=ALU.subtract), ["sl_t", "spj"], ["sl_t"])
            dve(lambda e: e.tensor_tensor(out=sl_t[:, 2, :], in0=sl_t[:, 2, :], in1=sl_t[:, 1, :], op=ALU.mult), ["sl_t"], ["sl_t"])
            dve(lambda e: e.tensor_tensor(out=sl_t[:, 2, :], in0=sl_t[:, 2, :], in1=pl, op=ALU.add), ["sl_t", "spj"], ["sl_t"])
            act(lambda e: e.activation(out=lor[:NS, 0:64], in_=sl_t[:, 2, 0:64], func=AF.Tanh), ["sl_t"], ["lor"])
            act(lambda e: e.copy(out=lor[:NS, 64:128], in_=sl_t[:, 2, 64:128]), ["sl_t"], ["lor"])
            act(lambda e: e.activation(out=lor[:NS, 128:256], in_=sl_t[:, 2, 128:256], func=AF.Sigmoid), ["sl_t"], ["lor"])
            ps, psb, pk = nps()
            pe(lambda e, psb=psb: e.transpose(psb[:, 0:NS], lor[:NS, 0:128], identb[:NS, :NS]), ["lor", "identb"], [pk])
            pe(lambda e, psb=psb: e.transpose(psb[:, 128:128 + NS], lor[:NS, 128:256], identb[:NS, :NS]), ["lor", "identb"], [pk])
            act(lambda e, psb=psb: e.copy(out=lorT[:, :, 0:NS], in_=psb[:, 0:256].rearrange("p (a t) -> p a t", a=2)[:, :, 0:NS]), [pk], ["lorT"])
            ps, psb, pk = nps()
            mm(ps[:NS, 0:512], lorT[0:64, 0, 0:NS], luw[0:64, :], True, True, ["lorT", "luw"], [pk])
            mm(ps[:NS, 512:1024], lorT[64:128, 0, 0:NS], luw[64:128, :], True, True, ["lorT", "luw"], [pk])
            ps2, _, pk2 = nps()
            mm(ps2[:NS, 0:512], lorT[:, 1, 0:NS], gup[:, :], True, True, ["lorT", "gup"], [pk2])
            lo3 = spj[:, 0:1536].rearrange("p (a n) -> p a n", a=3)
            for a_ in range(2):
                act(lambda e, ps=ps, a_=a_: e.copy(out=lo3[:, a_, :], in_=ps[:NS, a_ * 512:(a_ + 1) * 512]), [pk], ["lo3"])
            act(lambda e, ps2=ps2: e.copy(out=lo3[:, 2, :], in_=ps2[:NS, 0:512]), [pk2], ["lo3"])
            for a_ in range(3):
                dma("pool", scr2.rearrange("(b h) a d -> b a h d", b=NS)[:, a_, :, :], lo3[:, a_, :].rearrange("p (h d) -> p h d", h=8), ["lo3"], ["scr2"], "scrw2")
            L3 = TT("L3", [128, 3, 64])
            dma("sp", L3[:], scr2, ["scr2"], ["L3"], "scrr2")
            big = TT("big", [128, 4096])
            sv = TT("sv", [128, 64])
            pool(lambda e: e.tensor_copy(out=scv[:, :, 3, :], in_=A7[:, 0:2, :]), ["A7"], ["scv"])
            dma("pool", osconv, scv[:, :, 1:4, :], ["scv"], [], "fin")
            qk_s = TT("qk_s", [128, 2, 64])
            cwqk = lambda w_: spk[:, SOFF["cwq"][0] + w_ * 256:SOFF["cwq"][0] + (w_ + 1) * 256].rearrange("p (j d) -> p j d", j=4)
            for w_ in range(2):
                dve(lambda e, w_=w_: e.tensor_tensor(out=big[:, 0:256].rearrange("p (j d) -> p j d", j=4), in0=scv[:, w_, :, :], in1=cwqk(w_), op=ALU.mult), ["scv", "spk"], ["big"])
                dve(lambda e, w_=w_: e.tensor_reduce(out=qk_s[:, w_, :], in_=big[:, 0:256].rearrange("p (j d) -> p d j", j=4), axis=AX.X, op=ALU.add), ["big"], ["qk_s"])
            dve(lambda e: e.tensor_tensor(out=qk_s[:], in0=qk_s[:], in1=spk[:, SOFF["cbq"][0]:SOFF["cbk"][1]].rearrange("p (a d) -> p a d", a=2), op=ALU.add), ["qk_s", "spk"], ["qk_s"])
            act(lambda e: e.activation(out=qk_s[:], in_=qk_s[:], func=AF.Silu), ["qk_s"], ["qk_s"])
            act(lambda e: e.activation(out=qk_s[:, 1, :], in_=qk_s[:, 1, :], func=AF.Copy, scale=0.125), ["qk_s"], ["qk_s"])
            dve(lambda e: e.tensor_tensor(out=sv[:, 0:2], in0=gif[:], in1=spk[:, SOFF["ib"][0]:SOFF["fb"][1]], op=ALU.add), ["gif", "spk"], ["sv"])
            act(lambda e: e.activation(out=sv[:, 9:10], in_=sv[:, 1:2], func=AF.Exp, scale=-1.0), ["sv"], ["sv"])
            act(lambda e: e.activation(out=sv[:, 2:3], in_=sv[:, 9:10], func=AF.Ln, bias=1.0, scale=1.0), ["sv"], ["sv"])
            dve(lambda e: e.tensor_tensor(out=sv[:, 3:4], in0=sm_t[:], in1=sv[:, 2:3], op=ALU.subtract), ["sv", "sm_t"], ["sv"])
            dve(lambda e: e.tensor_tensor(out=sv[:, 4:5], in0=sv[:, 3:4], in1=sv[:, 0:1], op=ALU.max), ["sv"], ["sv"])
            dma("pool", osm, sv[:, 4:5], ["sv"], [], "fin")
            dve(lambda e: e.tensor_tensor(out=sv[:, 9:10], in0=sv[:, 0:1], in1=sv[:, 4:5], op=ALU.subtract), ["sv"], ["sv"])
            act(lambda e: e.activation(out=sv[:, 5:6], in_=sv[:, 9:10], func=AF.Exp), ["sv"], ["sv"])
            dve(lambda e: e.tensor_tensor(out=sv[:, 9:10], in0=sv[:, 3:4], in1=sv[:, 4:5], op=ALU.subtract), ["sv"], ["sv"])
            act(lambda e: e.activation(out=sv[:, 6:7], in_=sv[:, 9:10], func=AF.Exp), ["sv"], ["sv"])
            act(lambda e: e.activation(out=sv[:, 7:8], in_=sv[:, 4:5], func=AF.Exp, scale=-1.0), ["sv"], ["sv"])
            q_ = qk_s[:, 0, :]
            k_ = qk_s[:, 1, :]
            v_ = A7[:, 2, :]
            b3 = lambda t: t[:, :].rearrange("p (a c) -> p a c", a=64)
            pool(lambda e: e.tensor_tensor(out=b3(big), in0=k_.unsqueeze(2).to_broadcast([128, 64, 64]), in1=v_.unsqueeze(1).to_broadcast([128, 64, 64]), op=ALU.mult), ["qk_s", "A7"], ["big"])
            dve(lambda e: e.tensor_scalar_mul(out=Cs[:], in0=Cs[:], scalar1=sv[:, 6:7]), ["Cs", "sv"], ["Cs"])
            dve(lambda e: e.scalar_tensor_tensor(out=Cs[:], in0=big[:], scalar=sv[:, 5:6], in1=Cs[:], op0=ALU.mult, op1=ALU.add), ["big", "sv", "Cs"], ["Cs"])
            dma("pool", osC, Cs[:], ["Cs"], [], "fin")
            dve(lambda e: e.tensor_scalar_mul(out=sn_t[:], in0=sn_t[:], scalar1=sv[:, 6:7]), ["sn_t", "sv"], ["sn_t"])
            dve(lambda e: e.scalar_tensor_tensor(out=sn_t[:], in0=k_, scalar=sv[:, 5:6], in1=sn_t[:], op0=ALU.mult, op1=ALU.add), ["qk_s", "sv", "sn_t"], ["sn_t"])
            dma("pool", osn, sn_t[:], ["sn_t"], [], "fin")
            pool(lambda e: e.tensor_tensor(out=b3(big), in0=Cs[:, :].rearrange("p (k v) -> p v k", k=64), in1=q_.unsqueeze(1).to_broadcast([128, 64, 64]), op=ALU.mult), ["Cs", "qk_s"], ["big"])
            hs = TT("hs", [128, 2, 64])
            dve(lambda e: e.tensor_reduce(out=hs[:, 0, :], in_=b3(big), axis=AX.X, op=ALU.add), ["big"], ["hs"])
            dve(lambda e: e.tensor_tensor(out=sv[:, 16:80 - 16] if False else big[:, 0:64], in0=q_, in1=sn_t[:], op=ALU.mult), ["qk_s", "sn_t"], ["big"])
            dve(lambda e: e.tensor_reduce(out=sv[:, 8:9], in_=big[:, 0:64], axis=AX.X, op=ALU.add), ["big"], ["sv"])
            dve(lambda e: e.scalar_tensor_tensor(out=sv[:, 9:10], in0=sv[:, 8:9], scalar=-1.0, in1=sv[:, 8:9], op0=ALU.mult, op1=ALU.max), ["sv"], ["sv"])
            dve(lambda e: e.tensor_tensor(out=sv[:, 9:10], in0=sv[:, 9:10], in1=sv[:, 7:8], op=ALU.max), ["sv"], ["sv"])
            dve(lambda e: e.reciprocal(out=sv[:, 10:11], in_=sv[:, 9:10]), ["sv"], ["sv"])
            dve(lambda e: e.tensor_scalar_mul(out=hs[:, 0, :], in0=hs[:, 0, :], scalar1=sv[:, 10:11]), ["hs", "sv"], ["hs"])
            dma("pool", osshift, A7[:, 4:7, :], ["A7"], [], "fin")
            rk3 = TT("rk3", [128, 3, 64])
            mu3 = spk[:, SOFF["mu_r"][0]:SOFF["mu_v"][1]].rearrange("p (a d) -> p a d", a=3)
            dve(lambda e: e.tensor_tensor(out=rk3[:], in0=ssh[:], in1=A7[:, 4:7, :], op=ALU.subtract), ["ssh", "A7"], ["rk3"])
            dve(lambda e: e.tensor_tensor(out=rk3[:], in0=rk3[:], in1=mu3, op=ALU.mult), ["rk3", "spk"], ["rk3"])
            dve(lambda e: e.tensor_tensor(out=rk3[:], in0=rk3[:], in1=A7[:, 4:7, :], op=ALU.add), ["rk3", "A7"], ["rk3"])
            w8 = TT("w8", [128, 8, 64])
            dve(lambda e: e.tensor_tensor(out=w8[:, 0, :], in0=L3[:, 0, :], in1=SP_("w0"), op=ALU.add), ["L3", "spk"], ["w8"])
            act(lambda e: e.activation(out=w8[:, 0, :], in_=w8[:, 0, :], func=AF.Sigmoid), ["w8"], ["w8"])
            act(lambda e: e.activation(out=w8[:, 0, :], in_=w8[:, 0, :], func=AF.Exp, scale=-C0), ["w8"], ["w8"])
            dve(lambda e: e.tensor_tensor(out=w8[:, 1, :], in0=L3[:, 1, :], in1=SP_("a0"), op=ALU.add), ["L3", "spk"], ["w8"])
            act(lambda e: e.activation(out=w8[:, 1, :], in_=w8[:, 1, :], func=AF.Sigmoid), ["w8"], ["w8"])
            dve(lambda e: e.tensor_tensor(out=w8[:, 6, :], in0=rk3[:, 1, :], in1=SP_("kk"), op=ALU.mult), ["rk3", "spk"], ["w8"])
            dve(lambda e: e.tensor_tensor(out=w8[:, 7, :], in0=w8[:, 6, :], in1=w8[:, 6, :], op=ALU.mult), ["w8"], ["w8"])
            dve(lambda e: e.tensor_reduce(out=sv[:, 11:12], in_=w8[:, 7, :], axis=AX.X, op=ALU.add), ["w8"], ["sv"])
            dve(lambda e: e.tensor_scalar_max(out=sv[:, 11:12], in0=sv[:, 11:12], scalar1=1e-24), ["sv"], ["sv"])
            act(lambda e: e.activation(out=sv[:, 11:12], in_=sv[:, 11:12], func=AF.Sqrt), ["sv"], ["sv"])
            dve(lambda e: e.reciprocal(out=sv[:, 11:12], in_=sv[:, 11:12]), ["sv"], ["sv"])
            dve(lambda e: e.tensor_scalar_mul(out=w8[:, 3, :], in0=w8[:, 6, :], scalar1=sv[:, 11:12]), ["w8", "sv"], ["w8"])
            dve(lambda e: e.tensor_tensor(out=w8[:, 6, :], in0=w8[:, 1, :], in1=SP_("ka"), op=ALU.mult), ["w8", "spk"], ["w8"])
            dve(lambda e: e.tensor_tensor(out=w8[:, 6, :], in0=w8[:, 6, :], in1=SP_("ka"), op=ALU.subtract), ["w8", "spk"], ["w8"])
            dve(lambda e: e.tensor_scalar_add(out=w8[:, 6, :], in0=w8[:, 6, :], scalar1=1.0), ["w8"], ["w8"])
            dve(lambda e: e.tensor_tensor(out=w8[:, 4, :], in0=rk3[:, 1, :], in1=w8[:, 6, :], op=ALU.mult), ["rk3", "w8"], ["w8"])
            dve(lambda e: e.tensor_tensor(out=w8[:, 5, :], in0=w8[:, 1, :], in1=w8[:, 3, :], op=ALU.mult), ["w8"], ["w8"])
            dma("sp", Ss[:], sS_d, [], ["Ss"], "sin2")
            bk = lambda ap: ap.unsqueeze(1).to_broadcast([128, 64, 64])
            bv = lambda ap: ap.unsqueeze(2).to_broadcast([128, 64, 64])
            pool(lambda e: e.tensor_tensor(out=b3(big), in0=b3(Ss), in1=bk(w8[:, 3, :]), op=ALU.mult), ["Ss", "w8"], ["big"])
            dve(lambda e: e.tensor_reduce(out=w8[:, 7, :], in_=b3(big), axis=AX.X, op=ALU.add), ["big"], ["w8"])
            dve(lambda e: e.tensor_tensor(out=b3(Ss), in0=b3(Ss), in1=bk(w8[:, 0, :]), op=ALU.mult), ["Ss", "w8"], ["Ss"])
            pool(lambda e: e.tensor_tensor(out=b3(big), in0=bv(w8[:, 7, :]), in1=bk(w8[:, 5, :]), op=ALU.mult), ["w8"], ["big"])
            dve(lambda e: e.tensor_tensor(out=Ss[:], in0=Ss[:], in1=big[:], op=ALU.subtract), ["Ss", "big"], ["Ss"])
            pool(lambda e: e.tensor_tensor(out=b3(big), in0=bv(rk3[:, 2, :]), in1=bk(w8[:, 4, :]), op=ALU.mult), ["rk3", "w8"], ["big"])
            dve(lambda e: e.tensor_tensor(out=Ss[:], in0=Ss[:], in1=big[:], op=ALU.add), ["Ss", "big"], ["Ss"])
            dma("pool", osS, Ss[:], ["Ss"], [], "fin")
            pool(lambda e: e.tensor_tensor(out=b3(big), in0=b3(Ss), in1=bk(rk3[:, 0, :]), op=ALU.mult), ["Ss", "rk3"], ["big"])
            dve(lambda e: e.tensor_reduce(out=hs[:, 1, :], in_=b3(big), axis=AX.X, op=ALU.add), ["big"], ["hs"])
            dve(lambda e: e.tensor_tensor(out=w8[:, 6, :], in0=rk3[:, 0, :], in1=w8[:, 4, :], op=ALU.mult), ["rk3", "w8"], ["w8"])
            dve(lambda e: e.tensor_tensor(out=w8[:, 6, :], in0=w8[:, 6, :], in1=SP_("rk"), op=ALU.mult), ["w8", "spk"], ["w8"])
            dve(lambda e: e.tensor_reduce(out=sv[:, 12:13], in_=w8[:, 6, :], axis=AX.X, op=ALU.add), ["w8"], ["sv"])
            dve(lambda e: e.scalar_tensor_tensor(out=hs[:, 1, :], in0=rk3[:, 2, :], scalar=sv[:, 12:13], in1=hs[:, 1, :], op0=ALU.mult, op1=ALU.add), ["rk3", "sv", "hs"], ["hs"])
            act(lambda e: e.activation(out=w8[:, 6, :], in_=A7[:, 3, :], func=AF.Sigmoid), ["A7"], ["w8"])
            dve(lambda e: e.tensor_tensor(out=hs[:, 0, :], in0=hs[:, 0, :], in1=w8[:, 6, :], op=ALU.mult), ["hs", "w8"], ["hs"])
            dve(lambda e: e.tensor_tensor(out=w8[:, 7, :], in0=hs[:, 0, :], in1=hs[:, 0, :], op=ALU.mult), ["hs"], ["w8"])
            dve(lambda e: e.tensor_reduce(out=sv[:, 13:14], in_=w8[:, 7, :], axis=AX.X, op=ALU.add), ["w8"], ["sv"])
            act(lambda e: e.activation(out=sv[:, 13:14], in_=sv[:, 13:14], func=AF.Sqrt, bias=EPS, scale=1.0 / 64), ["sv"], ["sv"])
            dve(lambda e: e.reciprocal(out=sv[:, 13:14], in_=sv[:, 13:14]), ["sv"], ["sv"])
            dve(lambda e: e.scalar_tensor_tensor(out=hs[:, 0, :], in0=hs[:, 0, :], scalar=sv[:, 13:14], in1=SP_("mnw"), op0=ALU.mult, op1=ALU.mult), ["hs", "sv", "spk"], ["hs"])
            dve(lambda e: e.tensor_reduce(out=sv[:, 14:15], in_=hs[:, 1, :], axis=AX.X, op=ALU.add), ["hs"], ["sv"])
            dve(lambda e: e.tensor_scalar_mul(out=sv[:, 14:15], in0=sv[:, 14:15], scalar1=1.0 / 64), ["sv"], ["sv"])
            dve(lambda e: e.tensor_scalar_sub(out=hs[:, 1, :], in0=hs[:, 1, :], scalar1=sv[:, 14:15]), ["hs", "sv"], ["hs"])
            dve(lambda e: e.tensor_tensor(out=w8[:, 7, :], in0=hs[:, 1, :], in1=hs[:, 1, :], op=ALU.mult), ["hs"], ["w8"])
            dve(lambda e: e.tensor_reduce(out=sv[:, 15:16], in_=w8[:, 7, :], axis=AX.X, op=ALU.add), ["w8"], ["sv"])
            act(lambda e: e.activation(out=sv[:, 15:16], in_=sv[:, 15:16], func=AF.Sqrt, bias=GN_EPS, scale=1.0 / 64), ["sv"], ["sv"])
            dve(lambda e: e.reciprocal(out=sv[:, 15:16], in_=sv[:, 15:16]), ["sv"], ["sv"])
            dve(lambda e: e.scalar_tensor_tensor(out=hs[:, 1, :], in0=hs[:, 1, :], scalar=sv[:, 15:16], in1=SP_("lnw"), op0=ALU.mult, op1=ALU.mult), ["hs", "sv", "spk"], ["hs"])
            dve(lambda e: e.tensor_tensor(out=hs[:, 1, :], in0=hs[:, 1, :], in1=SP_("lnb"), op=ALU.add), ["hs", "spk"], ["hs"])
            dve(lambda e: e.tensor_tensor(out=hs[:, 1, :], in0=hs[:, 1, :], in1=L3[:, 2, :], op=ALU.mult), ["hs", "L3"], ["hs"])
            s3v = nc.dram_tensor("scr3b", [128, 2, 64], F32, kind="Internal").ap()
            dma("pool", s3v, hs[:], ["hs"], ["scr3b"], "scrw3")
            smix = spj[:, 2304:3328].rearrange("p (a h d) -> p a h d", a=2, h=8)
            for a_ in range(2):
                dma("sp", smix[:, a_, :, :], s3v.rearrange("(b h) a d -> b a h d", b=NS)[:, a_, :, :], ["scr3b"], ["smix"], "scrr3")
            act(lambda e: e.copy(out=mix[:NS, :], in_=smix[:].rearrange("p a h d -> p (a h d)")), ["smix"], ["mix"])
            out_proj(NS, mix, "mix", sx, "sx", T, W)
        PP[0].finalize()
        PP[0] = Prog(ctx)
    es_res.close()

    with contextlib.ExitStack() as es2:
        cur[0] = es2
        upb = TT("upb", [128, 8, DFF], BF16)
        dnb = TT("dnb", [128, 32, D], BF16)
        pB2 = TT("pB2", [128, 136], F32)
        nfw = TT("nfw", [128, D])
        identb2 = TT("identb2", [128, 128], BF16)
        wout = TT("wout", [128, 8, D], BF16)
        mixin = TT("mixin", [128, D], BF16)
        for c in range(0, 8, 4):
            dma("pool", wout[:, c:c + 4, :], w_out_v[:, c:c + 4, :], [], ["wout"], "wout")
        up_v = mlp_up.rearrange("(c p) n -> p c n", p=128)
        dn_v = mlp_down.rearrange("(c p) n -> p c n", p=128)
        dma("sp", pB2[:, 0:128], packA_d[:, OFF["ident"][0]:OFF["ident"][1]], [], ["pB2"], "init2", True)
        dma("sp", pB2[:, 128:136], packA_d[:, OFF["nmlp"][0]:OFF["nmlp"][1]], [], ["pB2"], "init2", True)
        dma("sp", nfw[:], nfw_d, [], ["nfw"], "init2", True)
        for g8 in range(8):
            dma("pool", upb[:, :, g8 * 512:(g8 + 1) * 512], up_v[:, :, g8 * 512:(g8 + 1) * 512], [], ["upb%d" % g8], "up%d" % g8)
        for g8 in range(8):
            dma("pool", dnb[:, g8 * 4:(g8 + 1) * 4, :], dn_v[:, g8 * 4:(g8 + 1) * 4, :], [], ["dnb%d" % g8], "dn%d" % g8)
        dve(lambda e: e.tensor_copy(out=identb2[:], in_=pB2[:, 0:128]), ["pB2"], ["identb2"])
        NSUB = 2
        NTT = NSUB * 128
        xsb2 = TT("xsb2", [128, D], BF16)
        xb = [TT("xb%d" % i, [128, NSUB, D]) for i in range(2)]
        st2 = TT("st2", [128, 8])
        xn2 = [TT("xn2T%d" % i, [128, 8, NTT], BF16) for i in range(2)]
        hT = TT("hT", [128, 32, NTT], BF16)
        junk2 = TT("junk2", [128, D], BF16)
        rl = [TT("rl%d" % i, [128, 512]) for i in range(2)]
        nmlp = pB2[:, 128:136]
        xm_v = xp.rearrange("(s p) d -> p s d", p=128)
        yp_v = yp.rearrange("(s p) d -> p s d", p=128)
        sbs = [(sb * NSUB, NSUB, 128) for sb in range(NB // NSUB)] + [(NB, 1, NS)]

        def front(i):
            s0, nsub, nt = sbs[i]
            x4 = xb[i % 2]
            xk = "xb%d" % (i % 2)
            xn2T = xn2[i % 2]
            xnk = "xn2T%d" % (i % 2)
            if nsub == NSUB:
                dma("sp", x4[:], xm_v[:, s0:s0 + NSUB, :], [], [xk], xk)
            else:
                dma("sp", x4[:nt, 0, :], xs, [], [xk], xk)
            for si in range(nsub):
                r0_ = (s0 + si) * 128 if nsub == NSUB else T
                dma("sp", mixin[:nt, :], mix_d[r0_:r0_ + nt, :], [], ["mixin"], "mixin")
                ps, psb, pk = nps()
                for c in range(8):
                    pe(lambda e, c=c, psb=psb: e.transpose(psb[:, c * 128:c * 128 + nt], mixin[:nt, c * 128:(c + 1) * 128], identb2[:nt, :nt]), ["mixin", "identb2"], [pk])
                act(lambda e, psb=psb, si=si: e.copy(out=xn2T[:, :, si * 128:si * 128 + nt], in_=psb[:, 0:1024].rearrange("p (c t) -> p c t", c=8)[:, :, 0:nt]), [pk], [xnk])
                ps, psb, pk = nps()
                for n in range(2):
                    for c in range(8):
                        mm(ps[:nt, n * 512:(n + 1) * 512], xn2T[:, c, si * 128:si * 128 + nt], wout[:, c, n * 512:(n + 1) * 512], c == 0, c == 7, [xnk, "wout"], [pk])
                for a_ in range(2):
                    dve(lambda e, ps=ps, si=si, a_=a_: e.tensor_tensor(out=x4[:nt, si, a_ * 512:(a_ + 1) * 512], in0=ps[:nt, a_ * 512:(a_ + 1) * 512], in1=x4[:nt, si, a_ * 512:(a_ + 1) * 512], op=ALU.add), [pk, xk], [xk])
                act(lambda e, si=si: e.activation(out=junk2[:nt, :], in_=x4[:nt, si, :], func=AF.Square, accum_out=st2[:nt, 0:1]), [xk], ["junk2", "st2"])
                act(lambda e: e.activation(out=st2[:nt, 1:2], in_=st2[:nt, 0:1], func=AF.Sqrt, bias=EPS, scale=1.0 / D), ["st2"], ["st2"])
                dve(lambda e: e.reciprocal(out=st2[:nt, 2:3], in_=st2[:nt, 1:2]), ["st2"], ["st2"])
                dve(lambda e, si=si: e.tensor_scalar_mul(out=xsb2[:nt, :], in0=x4[:nt, si, :], scalar1=st2[:nt, 2:3]), [xk, "st2"], ["xsb2"])
                ps, psb, pk = nps()
                for c in range(8):
                    pe(lambda e, c=c, psb=psb: e.transpose(psb[:, c * 128:c * 128 + nt], xsb2[:nt, c * 128:(c + 1) * 128], identb2[:nt, :nt]), ["xsb2", "identb2"], [pk])
                dve(lambda e, psb=psb, si=si: e.tensor_tensor(out=xn2T[:, :, si * 128:si * 128 + nt], in0=psb[:, 0:1024].rearrange("p (c t) -> p c t", c=8)[:, :, 0:nt],
                                                             in1=nmlp.unsqueeze(2).to_broadcast([128, 8, nt]), op=ALU.mult), [pk, "pB2"], [xnk])

        def up(i):
            s0, nsub, nt = sbs[i]
            ntt = nsub * nt if nsub == NSUB else nt
            xn2T = xn2[i % 2]
            xnk = "xn2T%d" % (i % 2)
            per = 512 // NTT
            for j2 in range(32 // (2 * per)):
                ps, psb, pk = nps()
                for jj in range(2 * per):
                    j = j2 * 2 * per + jj
                    for c in range(8):
                        mm(ps[:, jj * NTT:jj * NTT + ntt], upb[:, c, j * 128:(j + 1) * 128], xn2T[:, c, 0:ntt], c == 0, c == 7, ["upb%d" % (j // 4), xnk], [pk])
                for bk in range(2):
                    r_ = rl[bk]
                    rk_ = "rl%d" % bk
                    j0 = j2 * 2 * per + bk * per
                    psv = ps[:, bk * 512:(bk + 1) * 512].rearrange("p (j t) -> p j t", j=per)[:, :, 0:ntt]
                    rv = r_[:, :].rearrange("p (j t) -> p j t", j=per)[:, :, 0:ntt]
                    act(lambda e, psv=psv, rv=rv: e.activation(out=rv, in_=psv, func=AF.Relu), [pk], [rk_])
                    if bk == 0:
                        dve(lambda e, rv=rv, j0=j0: e.tensor_tensor(out=hT[:, j0:j0 + per, 0:ntt], in0=rv, in1=rv, op=ALU.mult), [rk_], ["hT"])
                    else:
                        pool(lambda e, rv=rv, j0=j0: e.tensor_tensor(out=hT[:, j0:j0 + per, 0:ntt], in0=rv, in1=rv, op=ALU.mult), [rk_], ["hT"])

        def down(i):
            s0, nsub, nt = sbs[i]
            x4 = xb[i % 2]
            xk = "xb%d" % (i % 2)
            for si in range(nsub):
                ps, psb, pk = nps()
                for n in range(2):
                    for j in range(32):
                        mm(ps[:nt, n * 512:(n + 1) * 512], hT[:, j, si * 128:si * 128 + nt], dnb[:, j, n * 512:(n + 1) * 512], j == 0, j == 31, ["hT", "dnb%d" % (j // 4)], [pk])
                for a_ in range(2):
                    dve(lambda e, ps=ps, si=si, a_=a_: e.tensor_tensor(out=x4[:nt, si, a_ * 512:(a_ + 1) * 512], in0=ps[:nt, a_ * 512:(a_ + 1) * 512], in1=x4[:nt, si, a_ * 512:(a_ + 1) * 512], op=ALU.add), [pk, xk], [xk])
                act(lambda e, si=si: e.activation(out=junk2[:nt, :], in_=x4[:nt, si, :], func=AF.Square, accum_out=st2[:nt, 4:5]), [xk], ["junk2", "st2"])
                act(lambda e: e.activation(out=st2[:nt, 5:6], in_=st2[:nt, 4:5], func=AF.Sqrt, bias=EPS, scale=1.0 / D), ["st2"], ["st2"])
                dve(lambda e: e.reciprocal(out=st2[:nt, 6:7], in_=st2[:nt, 5:6]), ["st2"], ["st2"])
                dve(lambda e, si=si: e.scalar_tensor_tensor(out=x4[:nt, si, :], in0=x4[:nt, si, :], scalar=st2[:nt, 6:7], in1=nfw[:nt, :], op0=ALU.mult, op1=ALU.mult), [xk, "st2", "nfw"], [xk])
            if nsub == NSUB:
                dma("pool", yp_v[:, s0:s0 + NSUB, :], x4[:], [xk], [], "yo%d" % (i % 2))
            else:
                dma("pool", ys, x4[:nt, 0, :], [xk], [], "yo%d" % (i % 2))

        front(0)
        for i in range(len(sbs)):
            up(i)
            if i + 1 < len(sbs):
                front(i + 1)
            down(i)
        PP[0].finalize()
    es_ps.close()
    ctx.close()
    return nc


_CACHE = {}


def _host_packs(inp, core):
    f = np.float32
    L = 0
    pa = np.zeros((128, NA), f)

    def put(n, arr):
        a, b = OFF[n]
        pa[:, a:b] = arr

    rep = lambda v: np.broadcast_to(np.asarray(v, f).reshape(1, -1), (128, np.asarray(v).size))
    put("mnw", rep(inp["mlstm_norm_w"][L]))
    put("w0", rep(inp["rw_w0"][L]))
    put("a0", rep(inp["rw_a0"][L]))
    put("kk", rep(inp["rw_k_k"][L]))
    put("ka", rep(inp["rw_k_a"][L]))
    put("rk", rep(inp["rw_r_k"][L].reshape(-1)))
    put("lnw", rep(inp["rw_ln_w"][L]))
    put("lnb", rep(inp["rw_ln_b"][L]))
    put("ifb", rep(np.concatenate([inp["mlstm_i_b"][L], inp["mlstm_f_b"][L]])))
    put("nmw", inp["norm_mix_w"][L].reshape(8, 128).T)
    put("nmlp", inp["norm_mlp_w"][L].reshape(8, 128).T)
    cw = inp["mlstm_conv_w"][L]
    put("cw", cw.reshape(4, 8, 128).transpose(2, 1, 0).reshape(128, 32))
    put("cb", inp["mlstm_conv_b"][L].reshape(8, 128).T)
    put("ident", np.eye(128, dtype=f))
    put("mui", np.triu(np.ones((128, 128), f), 0))
    put("mus", np.triu(np.ones((128, 128), f), 1))
    put("mls", np.tril(np.ones((128, 128), f), -1))
    put("ones", np.ones((128, 128), f))
    lsel = np.zeros((128, 128), f)
    rsel = np.zeros((128, 4), f)
    for h in range(8):
        lsel[h, (h % 2) * 64:(h % 2) * 64 + 64] = 1.0
        rsel[h, h // 2] = 1.0
    put("lsel", lsel)
    put("rsel", rsel)
    return pa


def _sample_pack(inp):
    f = np.float32
    L = 0
    sp = np.zeros((128, NSP), f)

    def bh(v512):
        return np.tile(np.asarray(v512, f).reshape(8, 64), (NS, 1))

    def put(n, arr):
        a, b = SOFF[n]
        sp[:, a:b] = arr

    mu = inp["rw_mu"][L]
    put("mu_r", bh(mu[0:512]))
    put("mu_k", bh(mu[512:1024]))
    put("mu_v", bh(mu[1024:1536]))
    cw = inp["mlstm_conv_w"][L]
    put("cwq", np.concatenate([bh(cw[j, 0:512]) for j in range(4)], axis=1))
    put("cwk", np.concatenate([bh(cw[j, 512:1024]) for j in range(4)], axis=1))
    cb = inp["mlstm_conv_b"][L]
    put("cbq", bh(cb[0:512]))
    put("cbk", bh(cb[512:1024]))
    put("mnw", bh(inp["mlstm_norm_w"][L]))
    put("w0", bh(inp["rw_w0"][L]))
    put("a0", bh(inp["rw_a0"][L]))
    put("kk", bh(inp["rw_k_k"][L]))
    put("ka", bh(inp["rw_k_a"][L]))
    put("rk", bh(inp["rw_r_k"][L].reshape(-1)))
    put("lnw", bh(inp["rw_ln_w"][L]))
    put("lnb", bh(inp["rw_ln_b"][L]))
    put("ib", np.tile(inp["mlstm_i_b"][L].reshape(8, 1), (NS, 1)))
    put("fb", np.tile(inp["mlstm_f_b"][L].reshape(8, 1), (NS, 1)))
    return sp


def kernel(**inp):
    f = np.float32
    inp = {k: np.asarray(v) for k, v in inp.items()}
    if "nc" not in _CACHE:
        _CACHE["nc"] = build_program()
    nc = _CACHE["nc"]
    L = 0
    pa = _host_packs(inp, 0)
    sp = _sample_pack(inp)
    mu = inp["rw_mu"][L]
    luw = np.concatenate([inp["rw_w_up"][L], inp["rw_a_up"][L]], axis=0).astype(f)
    common = {
        "w_in": np.ascontiguousarray(inp["w_in"][L], f),
        "w_out": np.ascontiguousarray(inp["w_out"][L], f),
        "mlp_up": np.ascontiguousarray(inp["mlp_up"][L], f),
        "mlp_down": np.ascontiguousarray(inp["mlp_down"][L], f),
        "packA": pa,
        "mu_b": np.ascontiguousarray(np.broadcast_to(mu.reshape(1, -1), (128, RWW)), f),
        "nfw_b": np.ascontiguousarray(np.broadcast_to(inp["norm_f_w"].reshape(1, -1), (128, D)), f),
        "luw": luw,
        "gup": np.ascontiguousarray(inp["rw_g_up"][L], f),
        "spack": sp,
        "mul": np.ascontiguousarray(np.broadcast_to(mu[1536:1792].reshape(1, -1), (NS, 256)), f),
    }
    in_maps = []
    for c in range(8):
        rs = slice(c * NS, (c + 1) * NS)
        m = dict(common)
        m["xp"] = np.ascontiguousarray(inp["x_prompt"][c], f)
        m["xs"] = np.ascontiguousarray(inp["x_sample"][rs, 0, :], f)
        m["sC"] = np.ascontiguousarray(inp["state_mlstm_C"][L, rs].reshape(128, 4096), f)
        m["sn"] = np.ascontiguousarray(inp["state_mlstm_n"][L, rs].reshape(128, 64), f)
        m["sm"] = np.ascontiguousarray(inp["state_mlstm_m"][L, rs].reshape(128, 1), f)
        cv = inp["state_mlstm_conv"][L, rs]
        m["sconv"] = np.ascontiguousarray(cv.reshape(NS, 3, 2, 8, 64).transpose(0, 3, 2, 1, 4).reshape(128, 2, 3, 64), f)
        m["sS"] = np.ascontiguousarray(inp["state_rwkv_S"][L, rs].reshape(128, 4096), f)
        sh = inp["state_rwkv_shift"][L, rs, 0, :]
        m["sshift"] = np.ascontiguousarray(sh[:, 0:1536].reshape(NS, 3, 8, 64).transpose(0, 2, 1, 3).reshape(128, 3, 64), f)
        m["sshl"] = np.ascontiguousarray(sh[:, 1536:1792], f)
        in_maps.append(m)
    res = run_bass_kernel_spmd(nc, in_maps, core_ids=list(range(8)))
    R = res.results
    y_prompt = np.stack([R[c]["yp"] for c in range(8)]).astype(f)
    y_sample = np.concatenate([R[c]["ys"] for c in range(8)], axis=0).reshape(128, 1, D).astype(f)
    pC = np.zeros((1, 8, 8, 64, 64), f)
    pn = np.zeros((1, 8, 8, 64), f)
    pm = np.zeros((1, 8, 8), f)
    pconv = np.zeros((1, 8, 3, 1024), f)
    pS = np.zeros((1, 8, 8, 64, 64), f)
    pshift = np.zeros((1, 8, 1, RWW), f)
    for c in range(8):
        oC = R[c]["oC"].reshape(2, 64, 4, 65)
        Ch = oC.transpose(2, 0, 1, 3).reshape(8, 64, 65)
        pC[0, c] = Ch[:, :, 0:64]
        pn[0, c] = Ch[:, :, 64]
        pm[0, c] = R[c]["om"].reshape(8)
        pconv[0, c] = R[c]["oconv"].transpose(2, 1, 0).reshape(3, 1024)
        oS = R[c]["oS"].reshape(2, 64, 4, 64)
        pS[0, c] = oS.transpose(2, 0, 3, 1).reshape(8, 64, 64)
        pshift[0, c, 0] = R[c]["oshift"].reshape(RWW)
    sC = np.concatenate([R[c]["osC"].reshape(NS, 8, 64, 64) for c in range(8)])[None].astype(f)
    sn = np.concatenate([R[c]["osn"].reshape(NS, 8, 64) for c in range(8)])[None].astype(f)
    sm = np.concatenate([R[c]["osm"].reshape(NS, 8) for c in range(8)])[None].astype(f)
    sconv = np.concatenate([R[c]["osconv"].reshape(NS, 8, 2, 3, 64).transpose(0, 3, 2, 1, 4).reshape(NS, 3, 1024) for c in range(8)])[None].astype(f)
    sS = np.concatenate([R[c]["osS"].reshape(NS, 8, 64, 64) for c in range(8)])[None].astype(f)
    sshift = np.concatenate([
        np.concatenate([R[c]["osshift"].reshape(NS, 8, 3, 64).transpose(0, 2, 1, 3).reshape(NS, 1536), R[c]["osshl"]], axis=1)
        for c in range(8)]).reshape(1, 128, 1, RWW).astype(f)
    return (y_prompt, y_sample, pC, pn, pm, pconv, pS, pshift, sC, sn, sm, sconv, sS, sshift)
```

```python
import contextlib
import numpy as np
import concourse.bass as bass
import concourse.mybir as mybir
from concourse.bass_utils import run_bass_kernel_spmd

F32 = mybir.dt.float32
BF16 = mybir.dt.bfloat16
AF = mybir.ActivationFunctionType
ALU = mybir.AluOpType
AX = mybir.AxisListType

D = 1024
T = 2048
NB = 16
NS = 16
INW = 3856
MLW = 2064
RWW = 1792
DFF = 4096
EPS = 1e-6
GN_EPS = 64e-5
C0 = 0.6065306597126334

OFF = {}
_o = 0
for _n, _w in [("mnw", 512), ("w0", 512), ("a0", 512), ("kk", 512), ("ka", 512), ("rk", 512),
               ("lnw", 512), ("lnb", 512), ("ifb", 16), ("nmw", 8), ("nmlp", 8), ("cw", 32), ("cb", 8),
               ("ident", 128), ("mui", 128), ("mus", 128), ("mls", 128), ("ones", 128),
               ("lsel", 128), ("rsel", 4)]:
    OFF[_n] = (_o, _o + _w)
    _o += _w
NA = _o
SOFF = {}
_o = 0
for _n, _w in [("mu_r", 64), ("mu_k", 64), ("mu_v", 64), ("cwq", 256), ("cwk", 256), ("cbq", 64), ("cbk", 64),
               ("mnw", 64), ("w0", 64), ("a0", 64), ("kk", 64), ("ka", 64), ("rk", 64), ("lnw", 64), ("lnb", 64),
               ("ib", 1), ("fb", 1)]:
    SOFF[_n] = (_o, _o + _w)
    _o += _w
NSP = _o


ALIAS = {"r_sb0": "G0", "kf_sb0": "G1", "vf0": "G2", "osig0": "G13", "r_sb1": "G20", "kf_sb1": "G21", "vf1": "G22", "osig1": "G23",
         "wsig": "G3", "a_sb": "G4", "g_sb": "G5", "kap": "G6", "ktl": "G7",
         "bvec": "G8", "e1": "G9", "e2": "G10", "e3": "G11", "pcw": "G12", "ysb": "G11", "tA": "G9", "tB": "G10", "plast": "G16",
         "hml": "G14", "nlrep": "G15", "Fb": "G15", "cacc": "G16", "qks": "G16", "tAm": "G18", "tBm": "G19",
         "junk": "xm", "ctmp": "xm", "mixT": "TMbA", "TMb0": "TMbA", "TMb1": "TMbA",
         "TMb2": "TMbB", "TMb3": "TMbB", "Ub": "Zb1", "Ss": "Cs", "lo3": "spj", "smix": "spj",
         "xt1": "xt0", "xnT1": "xnT0", "qkx1": "qkx0"}
PARKEYS = {"r_sb", "kf_sb", "vf", "osig", "vaug", "gt", "qpb", "kTb", "ktm", "vrw", "lor"}
CURP = [0]


class SemCtx:
    def __init__(self, nc):
        self.nc = nc
        self.es = contextlib.ExitStack()
        self.engs = ["pe", "act", "dve", "pool", "sp"]
        self.esem = {e: self.es.enter_context(nc.semaphore("s_" + e)) for e in self.engs}
        self.ecnt = {e: 0 for e in self.engs}
        self.bsem = self.es.enter_context(nc.semaphore("s_bar"))
        self.phase = 0
        self.gsem = {}
        self.gbase = {}

    def group_sem(self, g):
        if g not in self.gsem:
            self.gsem[g] = self.es.enter_context(self.nc.semaphore("g_%d" % len(self.gsem)))
            self.gbase[g] = 0
        return self.gsem[g]

    def close(self):
        self.es.close()


class Prog:
    max_ops = None

    def __init__(self, ctx):
        self.ctx = ctx
        self.nc = ctx.nc
        self.ops = []
        self.last_writer = {}
        self.readers = {}
        self.dma_groups = {}

    def op(self, eng, fn, reads=(), writes=(), dma_group=None, wait_total=False):
        if self.max_ops is not None and len(self.ops) >= self.max_ops:
            return None
        reads = [(k + str(CURP[0])) if k in PARKEYS else k for k in reads]
        writes = [(k + str(CURP[0])) if k in PARKEYS else k for k in writes]
        reads = [ALIAS.get(k, k) for k in reads]
        writes = [ALIAS.get(k, k) for k in writes]
        if eng != "pe":
            writes = writes + [k for k in reads if k.startswith("PS") and k not in writes]
        deps = set()
        for b in reads:
            if b in self.last_writer:
                deps.add(self.last_writer[b])
        for b in writes:
            if b in self.last_writer:
                deps.add(self.last_writer[b])
            for r in self.readers.get(b, ()):
                deps.add(r)
        idx = len(self.ops)
        if dma_group is not None:
            deps = {d for d in deps if self.ops[d]["dma"] != dma_group}
        o = dict(eng=eng, fn=fn, deps=sorted(deps), dma=dma_group, idx=idx)
        if dma_group is not None:
            g = self.dma_groups.setdefault(dma_group, dict(total=0, wait_total=wait_total))
            g["total"] += 1
            o["dma_cnt"] = g["total"]
        self.ops.append(o)
        for b in reads:
            self.readers.setdefault(b, []).append(idx)
        for b in writes:
            self.last_writer[b] = idx
            self.readers[b] = []
        return idx

    def finalize(self):
        nc = self.nc
        ctx = self.ctx
        ops = self.ops
        needed = set()
        for o in ops:
            best = {}
            rd = []
            for d in o["deps"]:
                p = ops[d]
                if p["dma"] is not None:
                    rd.append(d)
                else:
                    if p["eng"] == "pe" and o["eng"] == "pe" and o["dma"] is None:
                        continue
                    best[p["eng"]] = max(best.get(p["eng"], -1), d)
            rd.extend(best.values())
            o["deps"] = sorted(rd)
            for d in best.values():
                needed.add(d)
        engs = ctx.engs
        last = {}
        for o in ops:
            if o["dma"] is None:
                last[o["eng"]] = o["idx"]
        needed |= set(last.values())
        cnt = dict(ctx.ecnt)
        for o in ops:
            if o["dma"] is None and o["idx"] in needed:
                cnt[o["eng"]] += 1
                o["sig"] = cnt[o["eng"]]
        for g in self.dma_groups:
            ctx.group_sem(g)
        phase = ctx.phase
        with nc.Block() as block:

            def emit_engine(ename, eng):
                known = {}
                if phase > 0:
                    eng.wait_ge(ctx.bsem, phase)
                for o in ops:
                    if o["eng"] != ename:
                        continue
                    for d in o["deps"]:
                        p = ops[d]
                        if p["dma"] is not None:
                            g = self.dma_groups[p["dma"]]
                            sem = ctx.gsem[p["dma"]]
                            val = ctx.gbase[p["dma"]] + 16 * (g["total"] if g["wait_total"] else p["dma_cnt"])
                            key = ("g", p["dma"])
                        else:
                            if p["eng"] == "pe" and ename == "pe" and o["dma"] is None:
                                continue
                            sem = ctx.esem[p["eng"]]
                            val = p["sig"]
                            key = ("e", p["eng"])
                        if known.get(key, 0) >= val:
                            continue
                        known[key] = val
                        eng.wait_ge(sem, val)
                    ins = o["fn"](eng)
                    if o["dma"] is not None:
                        ins.then_inc(ctx.gsem[o["dma"]], 16)
                    elif "sig" in o:
                        ins.then_inc(ctx.esem[ename], 1)
                if ename == "sp":
                    for e2 in engs:
                        if cnt[e2] > ctx.ecnt[e2]:
                            eng.wait_ge(ctx.esem[e2], cnt[e2])
                    for g, info in self.dma_groups.items():
                        eng.wait_ge(ctx.gsem[g], ctx.gbase[g] + 16 * info["total"])
                    eng.sem_inc(ctx.bsem, 1)

            @block.tensor
            def _(e):
                emit_engine("pe", e)

            @block.scalar
            def _(e):
                emit_engine("act", e)

            @block.vector
            def _(e):
                emit_engine("dve", e)

            @block.gpsimd
            def _(e):
                emit_engine("pool", e)

            @block.sync
            def _(e):
                emit_engine("sp", e)

        ctx.ecnt = cnt
        for g, info in self.dma_groups.items():
            ctx.gbase[g] += 16 * info["total"]
        ctx.phase += 1


STOP_EARLY = True


class _StopBuild(Exception):
    pass


def build_program(do_sample=True, debug=False):
    nc = bass.Bass("TRN2", target_bir_lowering=False)
    try:
        return _build_program(nc, do_sample, debug)
    except _StopBuild:
        return nc


def _build_program(nc, do_sample, debug):
    dbg = nc.dram_tensor("dbg", [128, 16, 512], F32, kind="ExternalOutput").ap() if debug else None
    din = lambda n, s: nc.dram_tensor(n, s, F32, kind="ExternalInput").ap()
    dout = lambda n, s: nc.dram_tensor(n, s, F32, kind="ExternalOutput").ap()
    xp = din("xp", [T, D])
    xs = din("xs", [NS, D])
    w_in = din("w_in", [D, INW])
    w_out = din("w_out", [D, D])
    mlp_up = din("mlp_up", [D, DFF])
    mlp_down = din("mlp_down", [DFF, D])
    packA_d = din("packA", [128, NA])
    mu_d = din("mu_b", [128, RWW])
    nfw_d = din("nfw_b", [128, D])
    wup_d = din("luw", [128, 512])
    gup_d = din("gup", [128, 512])
    spk_d = din("spack", [128, NSP])
    sC_d = din("sC", [128, 4096])
    sn_d = din("sn", [128, 64])
    sm_d = din("sm", [128, 1])
    sconv_d = din("sconv", [128, 2, 3, 64])
    sS_d = din("sS", [128, 4096])
    sshift_d = din("sshift", [128, 3, 64])
    sshl_d = din("sshl", [NS, 256])
    mul_d = din("mul", [NS, 256])

    yp = dout("yp", [T, D])
    ys = dout("ys", [NS, D])
    oC = dout("oC", [128, 4, 65])
    om = dout("om", [8, 1])
    oconv = dout("oconv", [128, 8, 3])
    oS = dout("oS", [128, 4, 64])
    oshift = dout("oshift", [1, RWW])
    osC = dout("osC", [128, 4096])
    osn = dout("osn", [128, 64])
    osm = dout("osm", [128, 1])
    osconv = dout("osconv", [128, 2, 3, 64])
    osS = dout("osS", [128, 4096])
    osshift = dout("osshift", [128, 3, 64])
    osshl = dout("osshl", [NS, 256])

    mix_d = nc.dram_tensor("mix_scr", [T + NS, D], BF16, kind="Internal").ap()
    scr1 = nc.dram_tensor("scr1", [128, 8, 64], F32, kind="Internal").ap()
    scr2 = nc.dram_tensor("scr2", [128, 3, 64], F32, kind="Internal").ap()
    scr3 = nc.dram_tensor("scr3", [NS, 2, 8, 64], F32, kind="Internal").ap()

    ctx = SemCtx(nc)
    PP = [Prog(ctx)]
    es_res = contextlib.ExitStack()
    cur = [es_res]

    def TT(name, shape, dt=F32):
        return cur[0].enter_context(nc.sbuf_tensor("t_" + name, list(shape), dt))

    def dma(q, out, in_, reads, writes, group, wait_total=False):
        PP[0].op(q, lambda e: e.dma_start(out=out, in_=in_), reads=reads, writes=writes, dma_group=group, wait_total=wait_total)

    def dve(fn, r, w):
        PP[0].op("dve", fn, reads=r, writes=w)

    def act(fn, r, w):
        PP[0].op("act", fn, reads=r, writes=w)

    def pool(fn, r, w):
        PP[0].op("pool", fn, reads=r, writes=w)

    def pe(fn, r, w):
        PP[0].op("pe", fn, reads=r, writes=w)

    def mm(out, lhsT, rhs, start, stop, r, w):
        pe(lambda e: e.matmul(out, lhsT=lhsT, rhs=rhs, start=start, stop=stop), r, w)

    es_ps = contextlib.ExitStack()
    PS = [es_ps.enter_context(nc.psum_tensor("PS%d" % i, [128, 1024], F32)) for i in range(4)]
    PSB = [p.bitcast(BF16) for p in PS]
    psi = {"A": 0, "F": 0, "B": 0}
    POOLS = {"A": [0, 1, 2, 3], "F": [0, 1], "B": [2, 3]}
    CURPOOL = ["A"]

    def nps():
        pl = CURPOOL[0]
        lst = POOLS[pl]
        i = lst[psi[pl] % len(lst)]
        psi[pl] += 1
        return PS[i], PSB[i], "PS%d" % i

    wq = TT("wq", [128, 8, INW], BF16)
    mub = TT("mub", [128, RWW], F32)
    luw = TT("luw", [128, 512], BF16)
    gup = TT("gup", [128, 512], BF16)
    pA = TT("pA", [128, NA], F32)
    identb = TT("identb", [128, 128], BF16)

    def PA(n):
        a, b = OFF[n]
        return pA[:, a:b]

    w_in_v = w_in.rearrange("(c p) n -> p c n", p=128)
    dma("sp", pA[:], packA_d, [], ["pA"], "init", True)
    for c in range(8):
        dma("pool", wq[:, c, :], w_in_v[:, c, :], [], ["wq"], "init", True)
    dma("pool", luw[:], wup_d, [], ["luw"], "init", True)
    dma("pool", gup[:], gup_d, [], ["gup"], "init", True)
    w_out_v = w_out.rearrange("(c p) n -> p c n", p=128)
    dve(lambda e: e.tensor_copy(out=identb[:], in_=PA("ident")), ["pA"], ["identb"])


    def _dbgdump(tag):
        if debug != tag:
            return
        dstg_ = cur[0].enter_context(nc.sbuf_tensor("t_dbgst%d" % tag, [128, 512], F32))
        def dd(slot, ap, key, n):
            dve(lambda e: e.tensor_copy(out=dstg_[:, 0:n], in_=ap), [key], ["dbgst"])
            dma("sp", dbg[:, slot, 0:n], dstg_[:, 0:n], ["dbgst"], [], "dbg")
        dd(0, PA("ident"), "pA", 128)
        dd(1, PA("mui"), "pA", 128)
        dd(2, PA("mnw"), "pA", 512)
        dd(3, PA("w0"), "pA", 512)
        dd(4, PA("lnb"), "pA", 512)
        PP[0].max_ops = len(PP[0].ops)
        PP[0].finalize()
        raise _StopBuild()
    _dbgdump(3)
    dma("sp", mub[:], mu_d, [], ["mub"], "init", True)
    PP[0].finalize()
    PP[0] = Prog(ctx)

    def rmsnorm_T(xt, xk, nt, dstT, dstk, col0, wname, tmpb, tmpbk, junk, junkk, st, stk):
        act(lambda e: e.activation(out=junk[:nt, :], in_=xt[:nt, :], func=AF.Square, accum_out=st[:nt, 0:1]), [xk], [junkk, stk])
        act(lambda e: e.activation(out=st[:nt, 1:2], in_=st[:nt, 0:1], func=AF.Sqrt, bias=EPS, scale=1.0 / D), [stk], [stk])
        dve(lambda e: e.reciprocal(out=st[:nt, 2:3], in_=st[:nt, 1:2]), [stk], [stk])
        dve(lambda e: e.tensor_scalar_mul(out=tmpb[:nt, :], in0=xt[:nt, :], scalar1=st[:nt, 2:3]), [xk, stk], [tmpbk])
        ps, psb, pk = nps()
        for c in range(8):
            pe(lambda e, c=c: e.transpose(psb[:, c * 128:c * 128 + nt], tmpb[:nt, c * 128:(c + 1) * 128], identb[:nt, :nt]), [tmpbk, "identb"], [pk])
        a, b_ = OFF[wname]
        dve(lambda e: e.tensor_tensor(out=dstT[:, :, col0:col0 + nt],
                                      in0=psb[:, 0:1024].rearrange("p (c t) -> p c t", c=8)[:, :, 0:nt],
                                      in1=pA[:, a:b_].unsqueeze(2).to_broadcast([128, 8, nt]), op=ALU.mult), [pk, "pA"], [dstk])

    def head_ml(nt, hsrc, hk, osig, ok, mix, mixk, W, sfx=""):
        tA, tB, s8 = W["tA"], W["tB"], W["s8"]
        h3 = lambda t: t[:nt, :].rearrange("p (h d) -> p h d", h=8)
        bc = lambda t, c: t[:nt, c:c + 8].unsqueeze(2).to_broadcast([nt, 8, 64])
        dve(lambda e: e.tensor_tensor(out=tA[:nt, :], in0=hsrc[:nt, :], in1=osig[:nt, :], op=ALU.mult), [hk, ok], ["tA" + sfx])
        dve(lambda e: e.tensor_tensor(out=tB[:nt, :], in0=tA[:nt, :], in1=tA[:nt, :], op=ALU.mult), ["tA" + sfx], ["tB" + sfx])
        dve(lambda e: e.tensor_reduce(out=s8[:nt, 0:8], in_=h3(tB), axis=AX.X, op=ALU.add), ["tB" + sfx], ["s8" + sfx])
        act(lambda e: e.activation(out=s8[:nt, 8:16], in_=s8[:nt, 0:8], func=AF.Sqrt, bias=EPS, scale=1.0 / 64), ["s8" + sfx], ["s8" + sfx])
        dve(lambda e: e.reciprocal(out=s8[:nt, 16:24], in_=s8[:nt, 8:16]), ["s8" + sfx], ["s8" + sfx])
        dve(lambda e: e.tensor_tensor(out=h3(tB), in0=h3(tA), in1=bc(s8, 16), op=ALU.mult), ["tA" + sfx, "s8" + sfx], ["tB" + sfx])
        dve(lambda e: e.tensor_tensor(out=mix[:nt, 0:512], in0=tB[:nt, :], in1=PA("mnw")[:nt, :], op=ALU.mult), ["tB" + sfx, "pA"], [mixk])

    def head_rw(nt, ysrc, yk, bon, bonk, vf, vfk, g, gk, mix, mixk, W):
        tA, tB, s8 = W["tA"], W["tB"], W["s8"]
        h3 = lambda t: t[:nt, :].rearrange("p (h d) -> p h d", h=8)
        bc = lambda t, c: t[:nt, c:c + 8].unsqueeze(2).to_broadcast([nt, 8, 64])
        dve(lambda e: e.tensor_tensor(out=h3(tA), in0=h3(vf), in1=bc(bon, 0), op=ALU.mult), [vfk, bonk], ["tA"])
        dve(lambda e: e.tensor_tensor(out=tA[:nt, :], in0=tA[:nt, :], in1=ysrc[:nt, :], op=ALU.add), ["tA", yk], ["tA"])
        dve(lambda e: e.tensor_reduce(out=s8[:nt, 24:32], in_=h3(tA), axis=AX.X, op=ALU.add), ["tA"], ["s8"])
        dve(lambda e: e.tensor_scalar_mul(out=s8[:nt, 24:32], in0=s8[:nt, 24:32], scalar1=1.0 / 64), ["s8"], ["s8"])
        dve(lambda e: e.tensor_tensor(out=h3(tA), in0=h3(tA), in1=bc(s8, 24), op=ALU.subtract), ["tA", "s8"], ["tA"])
        dve(lambda e: e.tensor_tensor(out=tB[:nt, :], in0=tA[:nt, :], in1=tA[:nt, :], op=ALU.mult), ["tA"], ["tB"])
        dve(lambda e: e.tensor_reduce(out=s8[:nt, 32:40], in_=h3(tB), axis=AX.X, op=ALU.add), ["tB"], ["s8"])
        act(lambda e: e.activation(out=s8[:nt, 40:48], in_=s8[:nt, 32:40], func=AF.Sqrt, bias=GN_EPS, scale=1.0 / 64), ["s8"], ["s8"])
        dve(lambda e: e.reciprocal(out=s8[:nt, 48:56], in_=s8[:nt, 40:48]), ["s8"], ["s8"])
        dve(lambda e: e.tensor_tensor(out=h3(tB), in0=h3(tA), in1=bc(s8, 48), op=ALU.mult), ["tA", "s8"], ["tB"])
        dve(lambda e: e.tensor_tensor(out=tB[:nt, :], in0=tB[:nt, :], in1=PA("lnw")[:nt, :], op=ALU.mult), ["tB", "pA"], ["tB"])
        dve(lambda e: e.tensor_tensor(out=tB[:nt, :], in0=tB[:nt, :], in1=PA("lnb")[:nt, :], op=ALU.add), ["tB", "pA"], ["tB"])
        dve(lambda e: e.tensor_tensor(out=mix[:nt, 512:1024], in0=tB[:nt, :], in1=g[:nt, :], op=ALU.mult), ["tB", gk], [mixk])

    def out_proj(nt, mix, mixk, xt, xk, row0, W):
        dma("pool", mix_d[row0:row0 + nt, :], mix[:nt, :], [mixk], [], "mixst")

    with contextlib.ExitStack() as es1:
        cur[0] = es1
        W = {}
        Gbig = TT("Gbig", [128, 24, 512])
        G = [Gbig[:, i, :] for i in range(24)]
        W["s8"] = TT("s8", [128, 64])
        W["xm"] = TT("xm", [128, D])
        xt = [TT("xt0", [128, D])] * 2
        junk = W["xm"]
        st = TT("st", [128, 4])
        mix = TT("mix", [128, D], BF16)
        xsb = TT("xsb", [128, D], BF16)
        xnT = [TT("xnT0", [128, 8, 129], BF16)] * 2
        dxT = TT("dxT", [128, 8, 128], BF16)
        qkx = [TT("qkx0", [128, 8, 131])] * 2
        cacc = Gbig[:, 16:18, :].rearrange("p a (c t) -> p (a c) t", t=128)
        ctmp = W["xm"][:, :].rearrange("p (c t) -> p c t", c=8)
        qks = cacc
        QPB = [TT("qpb%d" % i, [128, 4, 128], BF16) for i in range(2)]
        KTB = [TT("kTb%d" % i, [128, 4, 128], BF16) for i in range(2)]
        KTM = [TT("ktm%d" % i, [128, 8, 64], BF16) for i in range(2)]
        VAUG = [TT("vaug%d" % i, [128, 8, 65], BF16) for i in range(2)]
        GT = [TT("gt%d" % i, [128, 96]) for i in range(2)]
        runmax = TT("runmax", [128, 8])
        nBc = TT("nBc", [128, 8])
        Cst = TT("Cst", [128, 4, 65])
        Cbf = TT("Cbf", [128, 4, 65], BF16)
        Fb = G[15].rearrange("p (j t) -> p j t", j=4)
        hml = G[14]
        nlrep = G[15].rearrange("p (h d) -> p h d", h=8)
        wsig, a_sb, g_sb, kap, ktl, bvec, e1, e2, e3, pcw, ysb = G[3], G[4], G[5], G[6], G[7], G[8], G[9], G[10], G[11], G[12], G[11]
        W["tA"], W["tB"] = G[9], G[10]
        WM = {"tA": G[18], "tB": G[19], "s8": TT("s8m", [128, 64])}
        r8b = TT("r8b", [128, 8])
        VRW = [TT("vrw%d" % i, [128, 8, 64], BF16) for i in range(2)]
        LOR = [TT("lor%d" % i, [128, 256], BF16) for i in range(2)]
        lorT = TT("lorT", [128, 2, 128], BF16)
        r8 = TT("r8", [128, 32])
        bon = TT("bon", [128, 8])
        TMb = TT("TMb", [128, 4, 512], BF16)
        W["mixT"] = TMb[:, 0:2, :].rearrange("p a (c t) -> p (a c) t", t=128)
        Btz = TT("Btz", [128, 8, 128], BF16)
        Ktz = TT("Ktz", [128, 8, 128], BF16)
        FMt = TT("FMt", [128, 4, 4, 128], BF16)
        Am = [TT("Am%d" % i, [128, 8, 128], BF16) for i in range(3)]
        PTb = TT("PTb", [128, 8, 128], BF16)
        Pw = [TT("Pw%d" % i, [128, 8, 128], BF16) for i in range(4)]
        Zb = [TT("Zb%d" % i, [128, 8, 64], BF16) for i in range(2)]
        Ub = Zb[1]
        Sst = TT("Sst", [128, 4, 64])
        Sbf = TT("Sbf", [128, 4, 64], BF16)
        WLfm = TT("WLfm", [128, 4])
        plast = G[17]

        for p_ in range(2):
            pool(lambda e, p_=p_: e.memset(VAUG[p_][:], 1.0), [], ["vaug%d" % p_])
        pool(lambda e: e.memset(Btz[:], 0.0), [], ["Btz"])
        pool(lambda e: e.memset(Ktz[:], 0.0), [], ["Ktz"])
        pool(lambda e: e.memset(Cst[:], 0.0), [], ["Cst"])
        pool(lambda e: e.memset(Cbf[:], 0.0), [], ["Cbf"])
        pool(lambda e: e.memset(Sst[:], 0.0), [], ["Sst"])
        pool(lambda e: e.memset(Sbf[:], 0.0), [], ["Sbf"])
        pool(lambda e: e.memset(runmax[:], -1e30), [], ["runmax"])
        pool(lambda e: e.memset(nBc[:], 0.0), [], ["nBc"])
        pool(lambda e: e.memset(xnT[0][:, :, 0:1], 0.0), [], ["xnT0"])
        pool(lambda e: e.memset(qkx[0][:, :, 0:3], 0.0), [], ["qkx0"])

        if debug == 2:
            dstg = TT("dbgstage", [128, 512]) if False else G[12]
            def ddump0(slot, ap, key, n):
                dve(lambda e: e.tensor_copy(out=dstg[:, 0:n], in_=ap), [key], ["pcw"])
                dma("sp", dbg[:, slot, 0:n], dstg[:, 0:n], ["pcw"], [], "dbg")
            ddump0(0, PA("ident"), "pA", 128)
            ddump0(1, PA("mui"), "pA", 128)
            ddump0(2, luw[:, :], "luw", 512)
            ddump0(3, gup[:, :], "gup", 512)
            ddump0(4, W1[:, 0, 0:512], "W1", 512)
            PP[0].max_ops = len(PP[0].ops)
            if STOP_EARLY:
                PP[0].finalize()
                raise _StopBuild()
        MUI = PA("mui")
        MUS = PA("mus")
        MLS = PA("mls")
        ONES = PA("ones")
        IDF = PA("ident")
        bc8 = lambda ap: ap.unsqueeze(2).to_broadcast([128, 8, 64])
        m8 = lambda m: m.unsqueeze(1).to_broadcast([128, 8, 128])
        v3 = lambda t: t[:].rearrange("p (h d) -> p h d", h=8)
        hoff = lambda h: (h % 2) * 512 + (h // 2) * 128

        def make_block(b):
            p_ = b % 2
            r_sb, kf_sb, vf, osig = G[0 + 20 * p_] if p_ == 0 else G[20], G[1] if p_ == 0 else G[21], G[2] if p_ == 0 else G[22], G[13] if p_ == 0 else G[23]
            vaug, gt, qpb, kTb, ktm, vrw, lor = VAUG[p_], GT[p_], QPB[p_], KTB[p_], KTM[p_], VRW[p_], LOR[p_]

            def front_stage():
                x_ = xt[b % 2]
                xk = "xt%d" % (b % 2)
                xn = xnT[b % 2]
                xnk = "xnT%d" % (b % 2)
                qx = qkx[b % 2]
                qxk = "qkx%d" % (b % 2)
                dma("sp", x_[:], xp[b * 128:(b + 1) * 128, :], [], [xk], xk)
                rmsnorm_T(x_, xk, 128, xn, xnk, 1, "nmw", xsb, "xsb", junk, "junk", st, "st")
                cur_x = xn[:, :, 1:129]
                prv_x = xn[:, :, 0:128]
                dve(lambda e, xn=xn: e.tensor_tensor(out=dxT[:], in0=xn[:, :, 0:128], in1=xn[:, :, 1:129], op=ALU.subtract), [xnk], ["dxT"])

                yield
                ps, psb, pk = nps()
                for j in range(8):
                    for c in range(8):
                        mm(ps[:, j * 128:(j + 1) * 128], wq[:, c, j * 128:(j + 1) * 128], cur_x[:, c, :], c == 0, c == 7, ["wq", xnk], [pk])
                for a_ in range(2):
                    act(lambda e, ps=ps, qx=qx, a_=a_: e.copy(out=qx[:, 4 * a_:4 * a_ + 4, 3:131], in_=ps[:, a_ * 512:(a_ + 1) * 512].rearrange("p (j t) -> p j t", j=4)), [pk], [qxk])
                if b == NB - 1:
                    dma("pool", oconv, qx[:, :, 128:131], [qxk], [], "fin")

                yield
                def tm_plain(col0, ncol, ps_ap, pk):
                    for c in range(8):
                        mm(ps_ap, cur_x[:, c, :], wq[:, c, col0:col0 + ncol], c == 0, c == 7, [xnk, "wq"], [pk])

                def tm_shift(col0, ncol, dst, dk):
                    ps, psb, pk = nps()
                    for c in range(8):
                        mm(ps[:, 0:ncol], cur_x[:, c, :], wq[:, c, MLW + col0:MLW + col0 + ncol], c == 0, c == 7, [xnk, "wq"], [pk])
                    for c in range(8):
                        mm(ps[:, 512:512 + ncol], dxT[:, c, :], wq[:, c, MLW + col0:MLW + col0 + ncol], c == 0, c == 7, ["dxT", "wq"], [pk])
                    dve(lambda e, ps=ps: e.tensor_tensor(out=dst, in0=ps[:, 512:512 + ncol], in1=mub[:, col0:col0 + ncol], op=ALU.mult), [pk, "mub"], [dk])
                    dve(lambda e, ps=ps: e.tensor_tensor(out=dst, in0=dst, in1=ps[:, 0:ncol], op=ALU.add), [pk, dk], [dk])

                ps, psb, pk = nps()
                tm_plain(1024, 512, ps[:, 0:512], pk)
                tm_plain(1536, 512, ps[:, 512:1024], pk)
                act(lambda e, ps=ps: e.copy(out=vaug[:, :, 0:64], in_=ps[:, 0:512].rearrange("p (h d) -> p h d", h=8)), [pk], ["vaug"])
                act(lambda e, ps=ps: e.activation(out=osig[:], in_=ps[:, 512:1024], func=AF.Sigmoid), [pk], ["osig"])
                ps, psb, pk = nps()
                tm_plain(2048, 16, ps[:, 0:16], pk)
                dve(lambda e, ps=ps: e.tensor_tensor(out=gt[:, 0:16], in0=ps[:, 0:16], in1=PA("ifb"), op=ALU.add), [pk, "pA"], ["gt"])
                tm_shift(0, 512, r_sb[:], "r_sb")
                tm_shift(512, 512, kf_sb[:], "kf_sb")
                tm_shift(1024, 512, vf[:], "vf")
                pool(lambda e: e.tensor_copy(out=vrw[:], in_=vf[:].rearrange("p (h d) -> p h d", h=8)), ["vf"], ["vrw"])
                ltmp = G[16]
                tm_shift(1536, 256, ltmp[:, 0:256], "cacc")
                act(lambda e: e.activation(out=lor[:, 0:64], in_=ltmp[:, 0:64], func=AF.Tanh), ["cacc"], ["lor"])
                act(lambda e: e.copy(out=lor[:, 64:128], in_=ltmp[:, 64:128]), ["cacc"], ["lor"])
                act(lambda e: e.activation(out=lor[:, 128:256], in_=ltmp[:, 128:256], func=AF.Sigmoid), ["cacc"], ["lor"])
                if b == NB - 1:
                    lastc = xn[:, :, 128:129]
                    for n0 in range(0, RWW, 512):
                        nn = min(512, RWW - n0)
                        ps2, _, pk2 = nps()
                        for c in range(8):
                            mm(ps2[0:1, 0:nn], lastc[:, c, :], wq[:, c, MLW + n0:MLW + n0 + nn], c == 0, c == 7, [xnk, "wq"], [pk2])
                        act(lambda e, ps2=ps2, n0=n0, nn=nn: e.copy(out=plast[0:1, 0:nn], in_=ps2[0:1, 0:nn]), [pk2], ["plast"])
                        dma("pool", oshift[:, n0:n0 + nn], plast[0:1, 0:nn], ["plast"], [], "fin")

                act(lambda e: e.activation(out=gt[:, 56:64], in_=gt[:, 8:16], func=AF.Exp, scale=-1.0), ["gt"], ["gt"])
                act(lambda e: e.activation(out=gt[:, 16:24], in_=gt[:, 56:64], func=AF.Ln, bias=1.0, scale=1.0), ["gt"], ["gt"])
                dve(lambda e: e.tensor_copy(out=nlrep[:], in_=bc8(gt[:, 16:24])), ["gt"], ["nlrep"])
                ps, psb, pk = nps()
                mm(ps[:, 0:8], MUI, gt[:, 16:24], True, True, ["pA", "gt"], [pk])
                mm(ps[:, 8:16], ONES, gt[:, 16:24], True, True, ["pA", "gt"], [pk])
                for j in range(4):
                    mm(ps[:, 512 + j * 128:512 + (j + 1) * 128], nlrep[:, 2 * j:2 * j + 2, :].rearrange("p a d -> p (a d)"), MUI, True, True, ["nlrep", "pA"], [pk])
                dve(lambda e, ps=ps: e.tensor_tensor(out=gt[:, 24:32], in0=ps[:, 0:8], in1=gt[:, 0:8], op=ALU.add), [pk, "gt"], ["gt"])
                act(lambda e: e.activation(out=gt[:, 32:40], in_=gt[:, 24:32], func=AF.Exp), ["gt"], ["gt"])
                dve(lambda e, ps=ps: e.tensor_tensor(out=gt[:, 56:64], in0=gt[:, 24:32], in1=ps[:, 8:16], op=ALU.subtract), [pk, "gt"], ["gt"])
                act(lambda e: e.activation(out=gt[:, 40:48], in_=gt[:, 56:64], func=AF.Exp), ["gt"], ["gt"])
                act(lambda e, ps=ps: e.activation(out=gt[:, 48:56], in_=ps[:, 8:16], func=AF.Exp, scale=-1.0), [pk], ["gt"])
                act(lambda e, ps=ps: e.activation(out=Fb[:], in_=ps[:, 512:1024].rearrange("p (j t) -> p j t", j=4), func=AF.Exp, scale=-1.0), [pk], ["Fb"])
                dve(lambda e: e.tensor_tensor(out=gt[:, 56:64], in0=gt[:, 24:32], in1=nBc[:], op=ALU.add), ["gt", "nBc"], ["gt"])
                dve(lambda e: e.tensor_tensor(out=runmax[:], in0=runmax[:], in1=gt[:, 56:64], op=ALU.max), ["gt", "runmax"], ["runmax"])
                dve(lambda e, ps=ps: e.tensor_tensor(out=nBc[:], in0=nBc[:], in1=ps[:, 8:16], op=ALU.add), [pk, "nBc"], ["nBc"])

                yield
                cwv = PA("cw").rearrange("p (c j) -> p c j", j=4)
                wbc = lambda j: cwv[:, :, j:j + 1].to_broadcast([128, 8, 128])
                pool(lambda e, qx=qx: e.tensor_tensor(out=cacc[:], in0=qx[:, :, 3:131], in1=wbc(3), op=ALU.mult), [qxk, "pA"], ["cacc"])
                for j in range(3):
                    pool(lambda e, qx=qx, j=j: e.tensor_tensor(out=ctmp[:], in0=qx[:, :, j:j + 128], in1=wbc(j), op=ALU.mult), [qxk, "pA"], ["ctmp"])
                    pool(lambda e: e.tensor_tensor(out=cacc[:], in0=cacc[:], in1=ctmp[:], op=ALU.add), ["cacc", "ctmp"], ["cacc"])
                pool(lambda e: e.tensor_tensor(out=cacc[:], in0=cacc[:], in1=PA("cb").unsqueeze(2).to_broadcast([128, 8, 128]), op=ALU.add), ["cacc", "pA"], ["cacc"])
                act(lambda e: e.activation(out=qks[:], in_=cacc[:], func=AF.Silu), ["cacc"], ["qks"])
                dve(lambda e: e.tensor_tensor(out=qpb[:], in0=qks[:, 0:4, :], in1=Fb[:], op=ALU.mult), ["qks", "Fb"], ["qpb"])
                act(lambda e: e.activation(out=kTb[:], in_=qks[:, 4:8, :], func=AF.Copy, scale=0.125), ["qks"], ["kTb"])

                yield
                ps, psb, pk = nps()
                for j in range(4):
                    pe(lambda e, j=j, psb=psb: e.transpose(psb[:, j * 128:(j + 1) * 128], kTb[:, j, :], identb[:]), ["kTb", "identb"], [pk])
                dve(lambda e, psb=psb: e.tensor_tensor(out=ktm[:], in0=psb[:, 0:512].rearrange("p (h d) -> p h d", h=8), in1=bc8(gt[:, 40:48]), op=ALU.mult), [pk, "gt"], ["ktm"])

                yield
                yield
                if b + 1 < NB:
                    pool(lambda e, xn=xn: e.tensor_copy(out=xn[:, :, 0:1], in_=xn[:, :, 128:129]), [xnk], [xnk])
                    pool(lambda e, qx=qx: e.tensor_copy(out=qx[:, :, 0:3], in_=qx[:, :, 128:131]), [qxk], [qxk])
                yield

            def ml_stage():
                ps, psb, pk = nps()
                for h in range(8):
                    j, hp = h // 2, h % 2
                    sl = slice(hp * 64, hp * 64 + 64)
                    mm(ps[:, hoff(h):hoff(h) + 128], kTb[sl, j, :], qpb[sl, j, :], True, True, ["kTb", "qpb"], [pk])
                for h in range(8):
                    dve(lambda e, h=h, ps=ps: e.scalar_tensor_tensor(out=PTb[:, h, :], in0=ps[:, hoff(h):hoff(h) + 128], scalar=gt[:, 32 + h:33 + h], in1=MUI, op0=ALU.mult, op1=ALU.mult), [pk, "gt", "pA"], ["PTb"])
                yield
                ps, psb, pk = nps()
                psn = lambda ps, h: ps[:, (h // 4) * 512 + (h % 4) * 65:(h // 4) * 512 + (h % 4) * 65 + 65]
                for h in range(8):
                    j, hp = h // 2, h % 2
                    sl = slice(hp * 64, hp * 64 + 64)
                    mm(psn(ps, h), PTb[:, h, :], vaug[:, h, :], True, False, ["PTb", "vaug"], [pk])
                    mm(psn(ps, h), qpb[sl, j, :], Cbf[sl, j, :], False, True, ["qpb", "Cbf"], [pk])
                pn4 = ps[:, :].rearrange("p (a r) -> p a r", a=2)[:, :, 0:260].rearrange("p a (h d) -> p a h d", h=4)
                for a_ in range(2):
                    act(lambda e, pn4=pn4, a_=a_: e.copy(out=r8[:, 4 * a_:4 * a_ + 4], in_=pn4[:, a_, :, 64]), [pk], ["r8"])
                dve(lambda e: e.scalar_tensor_tensor(out=r8[:, 8:16], in0=r8[:, 0:8], scalar=-1.0, in1=r8[:, 0:8], op0=ALU.mult, op1=ALU.max), ["r8"], ["r8"])
                dve(lambda e: e.tensor_scalar_max(out=r8[:, 8:16], in0=r8[:, 8:16], scalar1=1.0), ["r8"], ["r8"])
                dve(lambda e: e.reciprocal(out=r8[:, 16:24], in_=r8[:, 8:16]), ["r8"], ["r8"])
                for a_ in range(2):
                    dve(lambda e, pn4=pn4, a_=a_: e.tensor_tensor(out=hml[:, a_ * 256:(a_ + 1) * 256].rearrange("p (h d) -> p h d", h=4), in0=pn4[:, a_, :, 0:64],
                                                              in1=r8[:, 16 + 4 * a_:20 + 4 * a_].unsqueeze(2).to_broadcast([128, 4, 64]), op=ALU.mult), [pk, "r8"], ["hml"])
                yield
                ps, psb, pk = nps()
                for h in range(8):
                    j = h // 2
                    mm(psn(ps, h), ktm[:, 2 * j:2 * j + 2, :].rearrange("p a d -> p (a d)"), vaug[:, h, :], True, True, ["ktm", "vaug"], [pk])
                pu4 = ps[:, :].rearrange("p (a r) -> p a r", a=2)[:, :, 0:260].rearrange("p a (h d) -> p a h d", h=4)
                for hp in range(2):
                    sl = slice(hp * 64, hp * 64 + 64)
                    decb = gt[sl, 48:56].rearrange("p (j q) -> p j q", q=2)[:, :, hp:hp + 1].to_broadcast([64, 4, 65])
                    dve(lambda e, sl=sl, decb=decb: e.tensor_tensor(out=Cst[sl, :, :], in0=Cst[sl, :, :], in1=decb, op=ALU.mult), ["Cst", "gt"], ["Cst"])
                    for a in range(2):
                        src = pu4[sl, a, hp::2, :]
                        dve(lambda e, sl=sl, a=a, src=src: e.tensor_tensor(out=Cst[sl, 2 * a:2 * a + 2, :], in0=Cst[sl, 2 * a:2 * a + 2, :], in1=src, op=ALU.add), [pk, "Cst"], ["Cst"])
                act(lambda e: e.copy(out=Cbf[:], in_=Cst[:]), ["Cst"], ["Cbf"])
                head_ml(128, hml, "hml", osig, "osig", mix, "mix", WM, "m")


                yield
            def rw_stage():
                ps, psb, pk = nps()
                pe(lambda e, psb=psb: e.transpose(psb[:, 0:128], lor[:, 0:128], identb[:]), ["lor", "identb"], [pk])
                pe(lambda e, psb=psb: e.transpose(psb[:, 128:256], lor[:, 128:256], identb[:]), ["lor", "identb"], [pk])
                act(lambda e, psb=psb: e.copy(out=lorT[:], in_=psb[:, 0:256].rearrange("p (a t) -> p a t", a=2)), [pk], ["lorT"])
                ps, psb, pk = nps()
                mm(ps[:, 0:512], lorT[0:64, 0, :], luw[0:64, :], True, True, ["lorT", "luw"], [pk])
                mm(ps[:, 512:1024], lorT[64:128, 0, :], luw[64:128, :], True, True, ["lorT", "luw"], [pk])
                dve(lambda e, ps=ps: e.tensor_tensor(out=e1[:], in0=ps[:, 0:512], in1=PA("w0"), op=ALU.add), [pk, "pA"], ["e1"])
                act(lambda e: e.activation(out=wsig[:], in_=e1[:], func=AF.Sigmoid), ["e1"], ["wsig"])
                dve(lambda e, ps=ps: e.tensor_tensor(out=e2[:], in0=ps[:, 512:1024], in1=PA("a0"), op=ALU.add), [pk, "pA"], ["e2"])
                act(lambda e: e.activation(out=a_sb[:], in_=e2[:], func=AF.Sigmoid), ["e2"], ["a_sb"])
                ps, psb, pk = nps()
                mm(ps[:, 0:512], lorT[:, 1, :], gup[:, :], True, True, ["lorT", "gup"], [pk])
                act(lambda e, ps=ps: e.copy(out=g_sb[:], in_=ps[:, 0:512]), [pk], ["g_sb"])
                yield
                dve(lambda e: e.tensor_tensor(out=e1[:], in0=kf_sb[:], in1=PA("kk"), op=ALU.mult), ["kf_sb", "pA"], ["e1"])
                dve(lambda e: e.tensor_tensor(out=e2[:], in0=e1[:], in1=e1[:], op=ALU.mult), ["e1"], ["e2"])
                dve(lambda e: e.tensor_reduce(out=r8b[:, 0:8], in_=v3(e2), axis=AX.X, op=ALU.add), ["e2"], ["r8b"])
                dve(lambda e: e.tensor_scalar_max(out=r8b[:, 0:8], in0=r8b[:, 0:8], scalar1=1e-24), ["r8b"], ["r8b"])
                act(lambda e: e.activation(out=r8b[:, 0:8], in_=r8b[:, 0:8], func=AF.Sqrt), ["r8b"], ["r8b"])
                dve(lambda e: e.reciprocal(out=r8b[:, 0:8], in_=r8b[:, 0:8]), ["r8b"], ["r8b"])
                dve(lambda e: e.tensor_tensor(out=v3(kap), in0=v3(e1), in1=bc8(r8b[:, 0:8]), op=ALU.mult), ["e1", "r8b"], ["kap"])
                dve(lambda e: e.tensor_scalar_add(out=e2[:], in0=a_sb[:], scalar1=-1.0), ["a_sb"], ["e2"])
                dve(lambda e: e.tensor_tensor(out=e2[:], in0=e2[:], in1=PA("ka"), op=ALU.mult), ["e2", "pA"], ["e2"])
                dve(lambda e: e.tensor_tensor(out=e2[:], in0=e2[:], in1=kf_sb[:], op=ALU.mult), ["e2", "kf_sb"], ["e2"])
                dve(lambda e: e.tensor_tensor(out=ktl[:], in0=e2[:], in1=kf_sb[:], op=ALU.add), ["e2", "kf_sb"], ["ktl"])
                dve(lambda e: e.tensor_tensor(out=bvec[:], in0=a_sb[:], in1=kap[:], op=ALU.mult), ["a_sb", "kap"], ["bvec"])
                dve(lambda e: e.tensor_tensor(out=e2[:], in0=r_sb[:], in1=ktl[:], op=ALU.mult), ["r_sb", "ktl"], ["e2"])
                dve(lambda e: e.tensor_tensor(out=e2[:], in0=e2[:], in1=PA("rk"), op=ALU.mult), ["e2", "pA"], ["e2"])
                dve(lambda e: e.tensor_reduce(out=bon[:], in_=v3(e2), axis=AX.X, op=ALU.add), ["e2"], ["bon"])
                yield
                ps, psb, pk = nps()
                mm(ps[:, 0:512], MUI, wsig[:], True, True, ["pA", "wsig"], [pk])
                mm(ps[:, 512:1024], ONES, wsig[:], True, True, ["pA", "wsig"], [pk])
                act(lambda e, ps=ps: e.copy(out=pcw[:], in_=ps[:, 0:512]), [pk], ["pcw"])
                dve(lambda e: e.tensor_tensor(out=e1[:], in0=pcw[:], in1=wsig[:], op=ALU.subtract), ["pcw", "wsig"], ["e1"])
                act(lambda e: e.activation(out=e1[:], in_=e1[:], func=AF.Exp, scale=-C0), ["e1"], ["e1"])
                dve(lambda e: e.tensor_tensor(out=TMb[:, 0, :], in0=kap[:], in1=e1[:], op=ALU.mult), ["kap", "e1"], ["TMb0"])
                act(lambda e: e.activation(out=e2[:], in_=pcw[:], func=AF.Exp, scale=-C0), ["pcw"], ["e2"])
                dve(lambda e: e.tensor_tensor(out=TMb[:, 1, :], in0=r_sb[:], in1=e2[:], op=ALU.mult), ["r_sb", "e2"], ["TMb1"])
                act(lambda e: e.activation(out=e3[:], in_=pcw[:], func=AF.Exp, scale=C0), ["pcw"], ["e3"])
                dve(lambda e: e.tensor_tensor(out=TMb[:, 2, :], in0=bvec[:], in1=e3[:], op=ALU.mult), ["bvec", "e3"], ["TMb2"])
                dve(lambda e: e.tensor_tensor(out=TMb[:, 3, :], in0=ktl[:], in1=e3[:], op=ALU.mult), ["ktl", "e3"], ["TMb3"])
                dve(lambda e, ps=ps: e.tensor_tensor(out=e1[:], in0=ps[:, 512:1024], in1=pcw[:], op=ALU.subtract), [pk, "pcw"], ["e1"])
                act(lambda e: e.activation(out=e1[:], in_=e1[:], func=AF.Exp, scale=-C0), ["e1"], ["e1"])
                for hp in range(2):
                    srcb = v3(bvec).rearrange("p (j q) d -> p j q d", q=2)[:, :, hp, :]
                    srck = v3(ktl).rearrange("p (j q) d -> p j q d", q=2)[:, :, hp, :]
                    wl = v3(e1).rearrange("p (j q) d -> p j q d", q=2)[:, :, hp, :]
                    dstb = Btz[:].rearrange("p (j q) c -> p j q c", q=2)[:, :, hp, hp * 64:hp * 64 + 64]
                    dstk = Ktz[:].rearrange("p (j q) c -> p j q c", q=2)[:, :, hp, hp * 64:hp * 64 + 64]
                    dve(lambda e, srcb=srcb, wl=wl, dstb=dstb: e.tensor_tensor(out=dstb, in0=srcb, in1=wl, op=ALU.mult), ["bvec", "e1"], ["Btz"])
                    dve(lambda e, srck=srck, wl=wl, dstk=dstk: e.tensor_tensor(out=dstk, in0=srck, in1=wl, op=ALU.mult), ["ktl", "e1"], ["Ktz"])
                ps2, _, pk2 = nps()
                for j in range(4):
                    mm(ps2[:, j:j + 1], wsig[:, j * 128:(j + 1) * 128], ONES[:, 0:1], True, True, ["wsig", "pA"], [pk2])
                act(lambda e, ps2=ps2: e.activation(out=WLfm[:], in_=ps2[:, 0:4], func=AF.Exp, scale=-C0), [pk2], ["WLfm"])
                yield
                ps, psb, pk = nps()
                for w_ in range(4):
                    for j in range(4):
                        pe(lambda e, w_=w_, j=j, psb=psb: e.transpose(psb[:, (w_ * 4 + j) * 128:(w_ * 4 + j + 1) * 128], TMb[:, w_, j * 128:(j + 1) * 128], identb[:]), ["TMb%d" % w_, "identb"], [pk])
                for w_ in range(4):
                    eng_ = act if w_ % 2 == 0 else dve
                    if w_ % 2 == 0:
                        act(lambda e, psb=psb, w_=w_: e.copy(out=FMt[:, w_, :, :], in_=psb[:, w_ * 512:(w_ + 1) * 512].rearrange("p (j t) -> p j t", j=4)), [pk], ["FMt"])
                    else:
                        dve(lambda e, psb=psb, w_=w_: e.tensor_copy(out=FMt[:, w_, :, :], in_=psb[:, w_ * 512:(w_ + 1) * 512].rearrange("p (j t) -> p j t", j=4)), [pk], ["FMt"])
                KAP, RB, BB, KKB = 0, 1, 2, 3

                def amat(lw, rw_, dst, dk, mask, neg):
                    ps, psb, pk = nps()
                    for h in range(8):
                        j, hp = h // 2, h % 2
                        sl = slice(hp * 64, hp * 64 + 64)
                        mm(ps[:, hoff(h):hoff(h) + 128], FMt[sl, lw, j, :], FMt[sl, rw_, j, :], True, True, ["FMt"], [pk])
                    psv = ps[:, :].rearrange("p (q j t) -> p q j t", q=2, j=4)
                    dstv = dst[:].rearrange("p (j q) t -> p q j t", q=2)
                    mk = mask.unsqueeze(1).unsqueeze(1).to_broadcast([128, 2, 4, 128])
                    if neg:
                        mk3 = mask.unsqueeze(1).to_broadcast([128, 4, 128])
                        for q in range(2):
                            dve(lambda e, q=q: e.scalar_tensor_tensor(out=dstv[:, q], in0=psv[:, q], scalar=-1.0, in1=mk3, op0=ALU.mult, op1=ALU.mult), [pk, "pA"], [dk])
                    else:
                        mk3 = mask.unsqueeze(1).to_broadcast([128, 4, 128])
                        for q in range(2):
                            dve(lambda e, q=q: e.tensor_tensor(out=dstv[:, q], in0=psv[:, q], in1=mk3, op=ALU.mult), [pk, "pA"], [dk])

                amat(BB, KAP, Pw[1], "Pw1", MUS, True)
                amat(KAP, BB, Pw[0], "Pw0", MLS, True)
                amat(KKB, KAP, Am[0], "Am0", MUS, False)
                amat(BB, RB, Am[1], "Am1", MUI, False)
                amat(KKB, RB, Am[2], "Am2", MUI, False)
                yield
                ps, psb, pk = nps()
                for h in range(8):
                    j, hp = h // 2, h % 2
                    sl = slice(hp * 64, hp * 64 + 64)
                    mm(ps[:, h * 64:(h + 1) * 64], FMt[sl, KAP, j, :], Sbf[sl, j, :], True, False, ["FMt", "Sbf"], [pk])
                    mm(ps[:, h * 64:(h + 1) * 64], Am[0][:, h, :], vrw[:, h, :], False, True, ["Am0", "vrw"], [pk])
                act(lambda e, ps=ps: e.copy(out=Zb[0][:], in_=ps[:, 0:512].rearrange("p (h d) -> p h d", h=8)), [pk], ["Zb0"])
                pi = 0
                zi = 0
                for lvl in range(7):
                    yield
                    Pc, PTc = Pw[pi], Pw[pi + 1]
                    Pk, PTk = "Pw%d" % pi, "Pw%d" % (pi + 1)
                    Zc, Zn = Zb[zi], Zb[1 - zi]
                    ps, psb, pk = nps()
                    for h in range(8):
                        mm(ps[:, h * 64:(h + 1) * 64], identb[:], Zc[:, h, :], True, False, ["identb", "Zb%d" % zi], [pk])
                        mm(ps[:, h * 64:(h + 1) * 64], PTc[:, h, :], Zc[:, h, :], False, True, [PTk, "Zb%d" % zi], [pk])
                    if lvl < 6:
                        act(lambda e, ps=ps, Zn=Zn: e.copy(out=Zn[:], in_=ps[:, 0:512].rearrange("p (h d) -> p h d", h=8)), [pk], ["Zb%d" % (1 - zi)])
                        zi = 1 - zi
                        ni = 2 - pi
                        Pn, PTn = Pw[ni], Pw[ni + 1]
                        psA, _, pkA = nps()
                        for h in range(8):
                            mm(psA[:, h * 128:(h + 1) * 128], PTc[:, h, :], Pc[:, h, :], True, True, [PTk, Pk], [pkA])
                        for a_ in range(2):
                            dve(lambda e, psA=psA, Pn=Pn, a_=a_: e.tensor_copy(out=Pn[:, 4 * a_:4 * a_ + 4, :], in_=psA[:, a_ * 512:(a_ + 1) * 512].rearrange("p (h t) -> p h t", h=4)), [pkA], ["Pw%d" % ni])
                        psB, _, pkB = nps()
                        for h in range(8):
                            mm(psB[:, h * 128:(h + 1) * 128], Pc[:, h, :], PTc[:, h, :], True, True, [Pk, PTk], [pkB])
                        for a_ in range(2):
                            act(lambda e, psB=psB, PTn=PTn, a_=a_: e.copy(out=PTn[:, 4 * a_:4 * a_ + 4, :], in_=psB[:, a_ * 512:(a_ + 1) * 512].rearrange("p (h t) -> p h t", h=4)), [pkB], ["Pw%d" % (ni + 1)])
                        pi = ni
                    else:
                        act(lambda e, ps=ps: e.activation(out=Ub[:], in_=ps[:, 0:512].rearrange("p (h d) -> p h d", h=8), func=AF.Copy, scale=-1.0), [pk], ["Ub"])
                yield
                ps, psb, pk = nps()
                for h in range(8):
                    j, hp = h // 2, h % 2
                    sl = slice(hp * 64, hp * 64 + 64)
                    o_ = ps[:, h * 64:(h + 1) * 64]
                    mm(o_, Am[1][:, h, :], Ub[:, h, :], True, False, ["Am1", "Ub"], [pk])
                    mm(o_, Am[2][:, h, :], vrw[:, h, :], False, False, ["Am2", "vrw"], [pk])
                    mm(o_, FMt[sl, RB, j, :], Sbf[sl, j, :], False, True, ["FMt", "Sbf"], [pk])
                act(lambda e, ps=ps: e.copy(out=ysb[:], in_=ps[:, 0:512]), [pk], ["ysb"])
                yield
                ps, psb, pk = nps()
                for j in range(4):
                    o_ = ps[:, j * 64:(j + 1) * 64]
                    mm(o_, Btz[:, 2 * j, :], Ub[:, 2 * j, :], True, False, ["Btz", "Ub"], [pk])
                    mm(o_, Ktz[:, 2 * j, :], vrw[:, 2 * j, :], False, False, ["Ktz", "vrw"], [pk])
                    mm(o_, Btz[:, 2 * j + 1, :], Ub[:, 2 * j + 1, :], False, False, ["Btz", "Ub"], [pk])
                    mm(o_, Ktz[:, 2 * j + 1, :], vrw[:, 2 * j + 1, :], False, True, ["Ktz", "vrw"], [pk])
                dve(lambda e: e.tensor_tensor(out=Sst[:], in0=Sst[:], in1=WLfm[:].unsqueeze(2).to_broadcast([128, 4, 64]), op=ALU.mult), ["Sst", "WLfm"], ["Sst"])
                dve(lambda e, ps=ps: e.tensor_tensor(out=Sst[:], in0=Sst[:], in1=ps[:, 0:256].rearrange("p (j d) -> p j d", j=4), op=ALU.add), [pk, "Sst"], ["Sst"])
                act(lambda e: e.copy(out=Sbf[:], in_=Sst[:]), ["Sst"], ["Sbf"])
                yield
            def tail():
                head_rw(128, ysb, "ysb", bon, "bon", vf, "vf", g_sb, "g_sb", mix, "mix", W)
                out_proj(128, mix, "mix", None, None, b * 128, W)

            return front_stage, ml_stage, rw_stage, tail, p_

        def run_gens(gl):
            gl = list(gl)
            while gl:
                for item in list(gl):
                    CURP[0] = item[1]
                    CURPOOL[0] = item[3] if len(item) > 3 else "A"
                    for _ in range(item[2] if len(item) > 2 else 1):
                        try:
                            next(item[0])
                        except StopIteration:
                            gl.remove(item)
                            break

        blocks = [make_block(b) for b in range(NB)]
        run_gens([(blocks[0][0](), 0, 1, "A")])
        for b in range(NB):
            fr, ml_, rw_, tl, p_ = blocks[b]
            gl = [(rw_(), p_, 1, "A"), (ml_(), p_, 1, "A")]
            if b + 1 < NB:
                gl.append((blocks[b + 1][0](), (b + 1) % 2, 1, "A"))
            run_gens(gl)
            CURP[0] = p_
            CURPOOL[0] = "A"
            tl()
        CURP[0] = 0

        ps, psb, pk = nps()
        mm(ps[0:8, 0:128], runmax[:], IDF, True, True, ["runmax", "pA"], [pk])
        mm(ps[0:8, 128:256], nBc[:], IDF, True, True, ["nBc", "pA"], [pk])
        fs = TT("fs", [8, 16])
        dve(lambda e, ps=ps: e.tensor_reduce(out=fs[:, 0:1], in_=ps[0:8, 0:128], axis=AX.X, op=ALU.max), [pk], ["fs"])
        dve(lambda e: e.tensor_scalar_max(out=fs[:, 0:1], in0=fs[:, 0:1], scalar1=0.0), ["fs"], ["fs"])
        dve(lambda e, ps=ps: e.tensor_tensor(out=fs[:, 1:2], in0=fs[:, 0:1], in1=ps[0:8, 128:129], op=ALU.subtract), [pk, "fs"], ["fs"])
        dma("pool", om, fs[:, 1:2], ["fs"], [], "fin")
        act(lambda e: e.activation(out=fs[:, 2:3], in_=fs[:, 1:2], func=AF.Exp, scale=-1.0), ["fs"], ["fs"])
        dve(lambda e: e.tensor_scalar_mul(out=fs[:, 4:8], in0=pA[0:8, OFF["rsel"][0]:OFF["rsel"][1]], scalar1=fs[:, 2:3]), ["fs", "pA"], ["fs"])
        ps, psb, pk = nps()
        mm(ps[:, 0:4], pA[0:8, OFF["lsel"][0]:OFF["lsel"][1]], fs[:, 4:8], True, True, ["pA", "fs"], [pk])
        scb = TT("scb", [128, 4])
        act(lambda e, ps=ps: e.copy(out=scb[:], in_=ps[:, 0:4]), [pk], ["scb"])
        dve(lambda e: e.tensor_tensor(out=Cst[:], in0=Cst[:], in1=scb[:].unsqueeze(2).to_broadcast([128, 4, 65]), op=ALU.mult), ["Cst", "scb"], ["Cst"])
        dma("pool", oC, Cst[:], ["Cst"], [], "fin")
        dma("pool", oS, Sst[:], ["Sst"], [], "fin")
        PP[0].finalize()
        PP[0] = Prog(ctx)

    with contextlib.ExitStack() as es_s:
        cur[0] = es_s
        if do_sample:
            W = {}
            W["xm"] = TT("s_xm", [128, D])
            junk = W["xm"]
            st = TT("s_st", [128, 4])
            mix = TT("s_mix", [128, D], BF16)
            xsb = mix
            lor = TT("s_lor", [128, 256], BF16)
            lorT = TT("s_lorT", [128, 2, 128], BF16)
            W["mixT"] = TT("s_mixT", [128, 8, 128], BF16)
            sx = TT("sx", [NS, D])
            sxT = TT("sxT", [128, 8, NS], BF16)
            spj = TT("spj", [NS, INW])
            spk = TT("spk", [128, NSP])
            sl_t = TT("sl_t", [NS, 3, 256])
            Cs = TT("Cs", [128, 4096])
            Ss = Cs
            sn_t = TT("sn_t", [128, 64])
            sm_t = TT("sm_t", [128, 1])
            scv = TT("scv", [128, 2, 4, 64])
            ssh = TT("ssh", [128, 3, 64])
            dma("sp", sx[:], xs, [], ["sx"], "sin", True)
            dma("sp", spk[:], spk_d, [], ["spk"], "sin", True)
            dma("sp", sl_t[:, 0, :], sshl_d, [], ["sl_t"], "sin", True)
            dma("sp", sl_t[:, 1, :], mul_d, [], ["sl_t"], "sin", True)
            dma("sp", Cs[:], sC_d, [], ["Cs"], "sin", True)
            dma("sp", sn_t[:], sn_d, [], ["sn_t"], "sin", True)
            dma("sp", sm_t[:], sm_d, [], ["sm_t"], "sin", True)
            dma("sp", scv[:, :, 0:3, :], sconv_d, [], ["scv"], "sin", True)
            dma("sp", ssh[:], sshift_d, [], ["ssh"], "sin", True)
            rmsnorm_T(sx, "sx", NS, sxT, "sxT", 0, "nmw", xsb, "xsb", junk, "junk", st, "st")
            for n0 in range(0, INW, 512):
                nn = min(512, INW - n0)
                ps, psb, pk = nps()
                for c in range(8):
                    mm(ps[:NS, 0:nn], sxT[:, c, :], wq[:, c, n0:n0 + nn], c == 0, c == 7, ["sxT", "wq"], [pk])
                act(lambda e, ps=ps, n0=n0, nn=nn: e.copy(out=spj[:, n0:n0 + nn], in_=ps[:NS, 0:nn]), [pk], ["spj"])
            s1v = scr1.rearrange("(b h) a d -> b a h d", b=NS)
            for a_ in range(7):
                c0_ = a_ * 512 if a_ < 4 else MLW + (a_ - 4) * 512
                dma("pool", s1v[:, a_, :, :], spj[:, c0_:c0_ + 512].rearrange("p (h d) -> p h d", h=8), ["spj"], ["scr1"], "scrw1")
            A7 = TT("A7", [128, 8, 64])
            dma("sp", A7[:, 0:7, :], scr1[:, 0:7, :], ["scr1"], ["A7"], "scrr1")
            gif = TT("gif", [128, 2])
            s_if = nc.dram_tensor("scr_if", [2, 128], F32, kind="Internal").ap()
            for g_ in range(2):
                dma("pool", s_if[g_, :].rearrange("(b h) -> b h", b=NS), spj[:, 2048 + 8 * g_:2056 + 8 * g_], ["spj"], ["scr_if"], "scrwif")
            for g_ in range(2):
                dma("sp", gif[:, g_:g_ + 1], s_if[g_, :].rearrange("(p o) -> p o", o=1), ["scr_if"], ["gif"], "scrrif")
            SP_ = lambda n: spk[:, SOFF[n][0]:SOFF[n][1]]
            pl = spj[:, MLW + 1536:MLW + 1792]
            dma("pool", osshl, pl, ["spj"], [], "fin")
            dve(lambda e: e.tensor_tensor(out=sl_t[:, 2, :], in0=sl_t[:, 0, :], in1=pl, op=ALU.subtract), ["sl_t", "spj"], ["sl_t"])
            dve(lambda e: e.tensor_tensor(out=sl_t[:, 2, :], in0=sl_t[:, 2, :], in1=sl_t[:, 1, :], op=ALU.mult), ["sl_t"], ["sl_t"])
            dve(lambda e: e.tensor_tensor(out=sl_t[:, 2, :], in0=sl_t[:, 2, :], in1=pl, op=ALU.add), ["sl_t", "spj"], ["sl_t"])
            act(lambda e: e.activation(out=lor[:NS, 0:64], in_=sl_t[:, 2, 0:64], func=AF.Tanh), ["sl_t"], ["lor"])
            act(lambda e: e.copy(out=lor[:NS, 64:128], in_=sl_t[:, 2, 64:128]), ["sl_t"], ["lor"])
            act(lambda e: e.activation(out=lor[:NS, 128:256], in_=sl_t[:, 2, 128:256], func=AF.Sigmoid), ["sl_t"], ["lor"])
            ps, psb, pk = nps()
            pe(lambda e, psb=psb: e.transpose(psb[:, 0:NS], lor[:NS, 0:128], identb[:NS, :NS]), ["lor", "identb"], [pk])
            pe(lambda e, psb=psb: e.transpose(psb[:, 128:128 + NS], lor[:NS, 128:256], identb[:NS, :NS]), ["lor", "identb"], [pk])
            act(lambda e, psb=psb: e.copy(out=lorT[:, :, 0:NS], in_=psb[:, 0:256].rearrange("p (a t) -> p a t", a=2)[:, :, 0:NS]), [pk], ["lorT"])
            ps, psb, pk = nps()
            mm(ps[:NS, 0:512], lorT[0:64, 0, 0:NS], luw[0:64, :], True, True, ["lorT", "luw"], [pk])
            mm(ps[:NS, 512:1024], lorT[64:128, 0, 0:NS], luw[64:128, :], True, True, ["lorT", "luw"], [pk])
            ps2, _, pk2 = nps()
            mm(ps2[:NS, 0:512], lorT[:, 1, 0:NS], gup[:, :], True, True, ["lorT", "gup"], [pk2])
            lo3 = spj[:, 0:1536].rearrange("p (a n) -> p a n", a=3)
            for a_ in range(2):
                act(lambda e, ps=ps, a_=a_: e.copy(out=lo3[:, a_, :], in_=ps[:NS, a_ * 512:(a_ + 1) * 512]), [pk], ["lo3"])
            act(lambda e, ps2=ps2: e.copy(out=lo3[:, 2, :], in_=ps2[:NS, 0:512]), [pk2], ["lo3"])
            for a_ in range(3):
                dma("pool", scr2.rearrange("(b h) a d -> b a h d", b=NS)[:, a_, :, :], lo3[:, a_, :].rearrange("p (h d) -> p h d", h=8), ["lo3"], ["scr2"], "scrw2")
            L3 = TT("L3", [128, 3, 64])
            dma("sp", L3[:], scr2, ["scr2"], ["L3"], "scrr2")
            big = TT("big", [128, 4096])
            sv = TT("sv", [128, 64])
            pool(lambda e: e.tensor_copy(out=scv[:, :, 3, :], in_=A7[:, 0:2, :]), ["A7"], ["scv"])
            dma("pool", osconv, scv[:, :, 1:4, :], ["scv"], [], "fin")
            qk_s = TT("qk_s", [128, 2, 64])
            cwqk = lambda w_: spk[:, SOFF["cwq"][0] + w_ * 256:SOFF["cwq"][0] + (w_ + 1) * 256].rearrange("p (j d) -> p j d", j=4)
            for w_ in range(2):
                dve(lambda e, w_=w_: e.tensor_tensor(out=big[:, 0:256].rearrange("p (j d) -> p j d", j=4), in0=scv[:, w_, :, :], in1=cwqk(w_), op=ALU.mult), ["scv", "spk"], ["big"])
                dve(lambda e, w_=w_: e.tensor_reduce(out=qk_s[:, w_, :], in_=big[:, 0:256].rearrange("p (j d) -> p d j", j=4), axis=AX.X, op=ALU.add), ["big"], ["qk_s"])
            dve(lambda e: e.tensor_tensor(out=qk_s[:], in0=qk_s[:], in1=spk[:, SOFF["cbq"][0]:SOFF["cbk"][1]].rearrange("p (a d) -> p a d", a=2), op=ALU.add), ["qk_s", "spk"], ["qk_s"])
            act(lambda e: e.activation(out=qk_s[:], in_=qk_s[:], func=AF.Silu), ["qk_s"], ["qk_s"])
            act(lambda e: e.activation(out=qk_s[:, 1, :], in_=qk_s[:, 1, :], func=AF.Copy, scale=0.125), ["qk_s"], ["qk_s"])
            dve(lambda e: e.tensor_tensor(out=sv[:, 0:2], in0=gif[:], in1=spk[:, SOFF["ib"][0]:SOFF["fb"][1]], op=ALU.add), ["gif", "spk"], ["sv"])
            act(lambda e: e.activation(out=sv[:, 9:10], in_=sv[:, 1:2], func=AF.Exp, scale=-1.0), ["sv"], ["sv"])
            act(lambda e: e.activation(out=sv[:, 2:3], in_=sv[:, 9:10], func=AF.Ln, bias=1.0, scale=1.0), ["sv"], ["sv"])
            dve(lambda e: e.tensor_tensor(out=sv[:, 3:4], in0=sm_t[:], in1=sv[:, 2:3], op=ALU.subtract), ["sv", "sm_t"], ["sv"])
            dve(lambda e: e.tensor_tensor(out=sv[:, 4:5], in0=sv[:, 3:4], in1=sv[:, 0:1], op=ALU.max), ["sv"], ["sv"])
            dma("pool", osm, sv[:, 4:5], ["sv"], [], "fin")
            dve(lambda e: e.tensor_tensor(out=sv[:, 9:10], in0=sv[:, 0:1], in1=sv[:, 4:5], op=ALU.subtract), ["sv"], ["sv"])
            act(lambda e: e.activation(out=sv[:, 5:6], in_=sv[:, 9:10], func=AF.Exp), ["sv"], ["sv"])
            dve(lambda e: e.tensor_tensor(out=sv[:, 9:10], in0=sv[:, 3:4], in1=sv[:, 4:5], op=ALU.subtract), ["sv"], ["sv"])
            act(lambda e: e.activation(out=sv[:, 6:7], in_=sv[:, 9:10], func=AF.Exp), ["sv"], ["sv"])
            act(lambda e: e.activation(out=sv[:, 7:8], in_=sv[:, 4:5], func=AF.Exp, scale=-1.0), ["sv"], ["sv"])
            q_ = qk_s[:, 0, :]
            k_ = qk_s[:, 1, :]
            v_ = A7[:, 2, :]
            b3 = lambda t: t[:, :].rearrange("p (a c) -> p a c", a=64)
            pool(lambda e: e.tensor_tensor(out=b3(big), in0=k_.unsqueeze(2).to_broadcast([128, 64, 64]), in1=v_.unsqueeze(1).to_broadcast([128, 64, 64]), op=ALU.mult), ["qk_s", "A7"], ["big"])
            dve(lambda e: e.tensor_scalar_mul(out=Cs[:], in0=Cs[:], scalar1=sv[:, 6:7]), ["Cs", "sv"], ["Cs"])
            dve(lambda e: e.scalar_tensor_tensor(out=Cs[:], in0=big[:], scalar=sv[:, 5:6], in1=Cs[:], op0=ALU.mult, op1=ALU.add), ["big", "sv", "Cs"], ["Cs"])
            dma("pool", osC, Cs[:], ["Cs"], [], "fin")
            dve(lambda e: e.tensor_scalar_mul(out=sn_t[:], in0=sn_t[:], scalar1=sv[:, 6:7]), ["sn_t", "sv"], ["sn_t"])
            dve(lambda e: e.scalar_tensor_tensor(out=sn_t[:], in0=k_, scalar=sv[:, 5:6], in1=sn_t[:], op0=ALU.mult, op1=ALU.add), ["qk_s", "sv", "sn_t"], ["sn_t"])
            dma("pool", osn, sn_t[:], ["sn_t"], [], "fin")
            pool(lambda e: e.tensor_tensor(out=b3(big), in0=Cs[:, :].rearrange("p (k v) -> p v k", k=64), in1=q_.unsqueeze(1).to_broadcast([128, 64, 64]), op=ALU.mult), ["Cs", "qk_s"], ["big"])
            hs = TT("hs", [128, 2, 64])
            dve(lambda e: e.tensor_reduce(out=hs[:, 0, :], in_=b3(big), axis=AX.X, op=ALU.add), ["big"], ["hs"])
            dve(lambda e: e.tensor_tensor(out=sv[:, 16:80 - 16] if False else big[:, 0:64], in0=q_, in1=sn_t[:], op=ALU.mult), ["qk_s", "sn_t"], ["big"])
            dve(lambda e: e.tensor_reduce(out=sv[:, 8:9], in_=big[:, 0:64], axis=AX.X, op=ALU.add), ["big"], ["sv"])
            dve(lambda e: e.scalar_tensor_tensor(out=sv[:, 9:10], in0=sv[:, 8:9], scalar=-1.0, in1=sv[:, 8:9], op0=ALU.mult, op1=ALU.max), ["sv"], ["sv"])
            dve(lambda e: e.tensor_tensor(out=sv[:, 9:10], in0=sv[:, 9:10], in1=sv[:, 7:8], op=ALU.max), ["sv"], ["sv"])
            dve(lambda e: e.reciprocal(out=sv[:, 10:11], in_=sv[:, 9:10]), ["sv"], ["sv"])
            dve(lambda e: e.tensor_scalar_mul(out=hs[:, 0, :], in0=hs[:, 0, :], scalar1=sv[:, 10:11]), ["hs", "sv"], ["hs"])
            dma("pool", osshift, A7[:, 4:7, :], ["A7"], [], "fin")
            rk3 = TT("rk3", [128, 3, 64])
            mu3 = spk[:, SOFF["mu_r"][0]:SOFF["mu_v"][1]].rearrange("p (a d) -> p a d", a=3)
            dve(lambda e: e.tensor_tensor(out=rk3[:], in0=ssh[:], in1=A7[:, 4:7, :], op=ALU.subtract), ["ssh", "A7"], ["rk3"])
            dve(lambda e: e.tensor_tensor(out=rk3[:], in0=rk3[:], in1=mu3, op=ALU.mult), ["rk3", "spk"], ["rk3"])
            dve(lambda e: e.tensor_tensor(out=rk3[:], in0=rk3[:], in1=A7[:, 4:7, :], op=ALU.add), ["rk3", "A7"], ["rk3"])
            w8 = TT("w8", [128, 8, 64])
            dve(lambda e: e.tensor_tensor(out=w8[:, 0, :], in0=L3[:, 0, :], in1=SP_("w0"), op=ALU.add), ["L3", "spk"], ["w8"])
            act(lambda e: e.activation(out=w8[:, 0, :], in_=w8[:, 0, :], func=AF.Sigmoid), ["w8"], ["w8"])
            act(lambda e: e.activation(out=w8[:, 0, :], in_=w8[:, 0, :], func=AF.Exp, scale=-C0), ["w8"], ["w8"])
            dve(lambda e: e.tensor_tensor(out=w8[:, 1, :], in0=L3[:, 1, :], in1=SP_("a0"), op=ALU.add), ["L3", "spk"], ["w8"])
            act(lambda e: e.activation(out=w8[:, 1, :], in_=w8[:, 1, :], func=AF.Sigmoid), ["w8"], ["w8"])
            dve(lambda e: e.tensor_tensor(out=w8[:, 6, :], in0=rk3[:, 1, :], in1=SP_("kk"), op=ALU.mult), ["rk3", "spk"], ["w8"])
            dve(lambda e: e.tensor_tensor(out=w8[:, 7, :], in0=w8[:, 6, :], in1=w8[:, 6, :], op=ALU.mult), ["w8"], ["w8"])
            dve(lambda e: e.tensor_reduce(out=sv[:, 11:12], in_=w8[:, 7, :], axis=AX.X, op=ALU.add), ["w8"], ["sv"])
            dve(lambda e: e.tensor_scalar_max(out=sv[:, 11:12], in0=sv[:, 11:12], scalar1=1e-24), ["sv"], ["sv"])
            act(lambda e: e.activation(out=sv[:, 11:12], in_=sv[:, 11:12], func=AF.Sqrt), ["sv"], ["sv"])
            dve(lambda e: e.reciprocal(out=sv[:, 11:12], in_=sv[:, 11:12]), ["sv"], ["sv"])
            dve(lambda e: e.tensor_scalar_mul(out=w8[:, 3, :], in0=w8[:, 6, :], scalar1=sv[:, 11:12]), ["w8", "sv"], ["w8"])
            dve(lambda e: e.tensor_tensor(out=w8[:, 6, :], in0=w8[:, 1, :], in1=SP_("ka"), op=ALU.mult), ["w8", "spk"], ["w8"])
            dve(lambda e: e.tensor_tensor(out=w8[:, 6, :], in0=w8[:, 6, :], in1=SP_("ka"), op=ALU.subtract), ["w8", "spk"], ["w8"])
            dve(lambda e: e.tensor_scalar_add(out=w8[:, 6, :], in0=w8[:, 6, :], scalar1=1.0), ["w8"], ["w8"])
            dve(lambda e: e.tensor_tensor(out=w8[:, 4, :], in0=rk3[:, 1, :], in1=w8[:, 6, :], op=ALU.mult), ["rk3", "w8"], ["w8"])
            dve(lambda e: e.tensor_tensor(out=w8[:, 5, :], in0=w8[:, 1, :], in1=w8[:, 3, :], op=ALU.mult), ["w8"], ["w8"])
            dma("sp", Ss[:], sS_d, [], ["Ss"], "sin2")
            bk = lambda ap: ap.unsqueeze(1).to_broadcast([128, 64, 64])
            bv = lambda ap: ap.unsqueeze(2).to_broadcast([128, 64, 64])
            pool(lambda e: e.tensor_tensor(out=b3(big), in0=b3(Ss), in1=bk(w8[:, 3, :]), op=ALU.mult), ["Ss", "w8"], ["big"])
            dve(lambda e: e.tensor_reduce(out=w8[:, 7, :], in_=b3(big), axis=AX.X, op=ALU.add), ["big"], ["w8"])
            dve(lambda e: e.tensor_tensor(out=b3(Ss), in0=b3(Ss), in1=bk(w8[:, 0, :]), op=ALU.mult), ["Ss", "w8"], ["Ss"])
            pool(lambda e: e.tensor_tensor(out=b3(big), in0=bv(w8[:, 7, :]), in1=bk(w8[:, 5, :]), op=ALU.mult), ["w8"], ["big"])
            dve(lambda e: e.tensor_tensor(out=Ss[:], in0=Ss[:], in1=big[:], op=ALU.subtract), ["Ss", "big"], ["Ss"])
            pool(lambda e: e.tensor_tensor(out=b3(big), in0=bv(rk3[:, 2, :]), in1=bk(w8[:, 4, :]), op=ALU.mult), ["rk3", "w8"], ["big"])
            dve(lambda e: e.tensor_tensor(out=Ss[:], in0=Ss[:], in1=big[:], op=ALU.add), ["Ss", "big"], ["Ss"])
            dma("pool", osS, Ss[:], ["Ss"], [], "fin")
            pool(lambda e: e.tensor_tensor(out=b3(big), in0=b3(Ss), in1=bk(rk3[:, 0, :]), op=ALU.mult), ["Ss", "rk3"], ["big"])
            dve(lambda e: e.tensor_reduce(out=hs[:, 1, :], in_=b3(big), axis=AX.X, op=ALU.add), ["big"], ["hs"])
            dve(lambda e: e.tensor_tensor(out=w8[:, 6, :], in0=rk3[:, 0, :], in1=w8[:, 4, :], op=ALU.mult), ["rk3", "w8"], ["w8"])
            dve(lambda e: e.tensor_tensor(out=w8[:, 6, :], in0=w8[:, 6, :], in1=SP_("rk"), op=ALU.mult), ["w8", "spk"], ["w8"])
            dve(lambda e: e.tensor_reduce(out=sv[:, 12:13], in_=w8[:, 6, :], axis=AX.X, op=ALU.add), ["w8"], ["sv"])
            dve(lambda e: e.scalar_tensor_tensor(out=hs[:, 1, :], in0=rk3[:, 2, :], scalar=sv[:, 12:13], in1=hs[:, 1, :], op0=ALU.mult, op1=ALU.add), ["rk3", "sv", "hs"], ["hs"])
            act(lambda e: e.activation(out=w8[:, 6, :], in_=A7[:, 3, :], func=AF.Sigmoid), ["A7"], ["w8"])
            dve(lambda e: e.tensor_tensor(out=hs[:, 0, :], in0=hs[:, 0, :], in1=w8[:, 6, :], op=ALU.mult), ["hs", "w8"], ["hs"])
            dve(lambda e: e.tensor_tensor(out=w8[:, 7, :], in0=hs[:, 0, :], in1=hs[:, 0, :], op=ALU.mult), ["hs"], ["w8"])
            dve(lambda e: e.tensor_reduce(out=sv[:, 13:14], in_=w8[:, 7, :], axis=AX.X, op=ALU.add), ["w8"], ["sv"])
            act(lambda e: e.activation(out=sv[:, 13:14], in_=sv[:, 13:14], func=AF.Sqrt, bias=EPS, scale=1.0 / 64), ["sv"], ["sv"])
            dve(lambda e: e.reciprocal(out=sv[:, 13:14], in_=sv[:, 13:14]), ["sv"], ["sv"])
            dve(lambda e: e.scalar_tensor_tensor(out=hs[:, 0, :], in0=hs[:, 0, :], scalar=sv[:, 13:14], in1=SP_("mnw"), op0=ALU.mult, op1=ALU.mult), ["hs", "sv", "spk"], ["hs"])
            dve(lambda e: e.tensor_reduce(out=sv[:, 14:15], in_=hs[:, 1, :], axis=AX.X, op=ALU.add), ["hs"], ["sv"])
            dve(lambda e: e.tensor_scalar_mul(out=sv[:, 14:15], in0=sv[:, 14:15], scalar1=1.0 / 64), ["sv"], ["sv"])
            dve(lambda e: e.tensor_scalar_sub(out=hs[:, 1, :], in0=hs[:, 1, :], scalar1=sv[:, 14:15]), ["hs", "sv"], ["hs"])
            dve(lambda e: e.tensor_tensor(out=w8[:, 7, :], in0=hs[:, 1, :], in1=hs[:, 1, :], op=ALU.mult), ["hs"], ["w8"])
            dve(lambda e: e.tensor_reduce(out=sv[:, 15:16], in_=w8[:, 7, :], axis=AX.X, op=ALU.add), ["w8"], ["sv"])
            act(lambda e: e.activation(out=sv[:, 15:16], in_=sv[:, 15:16], func=AF.Sqrt, bias=GN_EPS, scale=1.0 / 64), ["sv"], ["sv"])
            dve(lambda e: e.reciprocal(out=sv[:, 15:16], in_=sv[:, 15:16]), ["sv"], ["sv"])
            dve(lambda e: e.scalar_tensor_tensor(out=hs[:, 1, :], in0=hs[:, 1, :], scalar=sv[:, 15:16], in1=SP_("lnw"), op0=ALU.mult, op1=ALU.mult), ["hs", "sv", "spk"], ["hs"])
            dve(lambda e: e.tensor_tensor(out=hs[:, 1, :], in0=hs[:, 1, :], in1=SP_("lnb"), op=ALU.add), ["hs", "spk"], ["hs"])
            dve(lambda e: e.tensor_tensor(out=hs[:, 1, :], in0=hs[:, 1, :], in1=L3[:, 2, :], op=ALU.mult), ["hs", "L3"], ["hs"])
            s3v = nc.dram_tensor("scr3b", [128, 2, 64], F32, kind="Internal").ap()
            dma("pool", s3v, hs[:], ["hs"], ["scr3b"], "scrw3")
            smix = spj[:, 2304:3328].rearrange("p (a h d) -> p a h d", a=2, h=8)
            for a_ in range(2):
                dma("sp", smix[:, a_, :, :], s3v.rearrange("(b h) a d -> b a h d", b=NS)[:, a_, :, :], ["scr3b"], ["smix"], "scrr3")
            act(lambda e: e.copy(out=mix[:NS, :], in_=smix[:].rearrange("p a h d -> p (a h d)")), ["smix"], ["mix"])
            out_proj(NS, mix, "mix", sx, "sx", T, W)
        PP[0].finalize()
        PP[0] = Prog(ctx)
    es_res.close()

    with contextlib.ExitStack() as es2:
        cur[0] = es2
        upb = TT("upb", [128, 8, DFF], BF16)
        dnb = TT("dnb", [128, 32, D], BF16)
        pB2 = TT("pB2", [128, 136], F32)
        nfw = TT("nfw", [128, D])
        identb2 = TT("identb2", [128, 128], BF16)
        wout = TT("wout", [128, 8, D], BF16)
        mixin = TT("mixin", [128, D], BF16)
        for c in range(0, 8, 4):
            dma("pool", wout[:, c:c + 4, :], w_out_v[:, c:c + 4, :], [], ["wout"], "wout")
        up_v = mlp_up.rearrange("(c p) n -> p c n", p=128)
        dn_v = mlp_down.rearrange("(c p) n -> p c n", p=128)
        dma("sp", pB2[:, 0:128], packA_d[:, OFF["ident"][0]:OFF["ident"][1]], [], ["pB2"], "init2", True)
        dma("sp", pB2[:, 128:136], packA_d[:, OFF["nmlp"][0]:OFF["nmlp"][1]], [], ["pB2"], "init2", True)
        dma("sp", nfw[:], nfw_d, [], ["nfw"], "init2", True)
        for g8 in range(8):
            dma("pool", upb[:, :, g8 * 512:(g8 + 1) * 512], up_v[:, :, g8 * 512:(g8 + 1) * 512], [], ["upb%d" % g8], "up%d" % g8)
        for g8 in range(8):
            dma("pool", dnb[:, g8 * 4:(g8 + 1) * 4, :], dn_v[:, g8 * 4:(g8 + 1) * 4, :], [], ["dnb%d" % g8], "dn%d" % g8)
        dve(lambda e: e.tensor_copy(out=identb2[:], in_=pB2[:, 0:128]), ["pB2"], ["identb2"])
        NSUB = 2
        NTT = NSUB * 128
        xsb2 = TT("xsb2", [128, D], BF16)
        xb = [TT("xb%d" % i, [128, NSUB, D]) for i in range(2)]
        st2 = TT("st2", [128, 8])
        xn2 = [TT("xn2T%d" % i, [128, 8, NTT], BF16) for i in range(2)]
        hT = TT("hT", [128, 32, NTT], BF16)
        junk2 = TT("junk2", [128, D], BF16)
        rl = [TT("rl%d" % i, [128, 512]) for i in range(2)]
        nmlp = pB2[:, 128:136]
        xm_v = xp.rearrange("(s p) d -> p s d", p=128)
        yp_v = yp.rearrange("(s p) d -> p s d", p=128)
        sbs = [(sb * NSUB, NSUB, 128) for sb in range(NB // NSUB)] + [(NB, 1, NS)]

        def front(i):
            s0, nsub, nt = sbs[i]
            x4 = xb[i % 2]
            xk = "xb%d" % (i % 2)
            xn2T = xn2[i % 2]
            xnk = "xn2T%d" % (i % 2)
            if nsub == NSUB:
                dma("sp", x4[:], xm_v[:, s0:s0 + NSUB, :], [], [xk], xk)
            else:
                dma("sp", x4[:nt, 0, :], xs, [], [xk], xk)
            for si in range(nsub):
                r0_ = (s0 + si) * 128 if nsub == NSUB else T
                dma("sp", mixin[:nt, :], mix_d[r0_:r0_ + nt, :], [], ["mixin"], "mixin")
                ps, psb, pk = nps()
                for c in range(8):
                    pe(lambda e, c=c, psb=psb: e.transpose(psb[:, c * 128:c * 128 + nt], mixin[:nt, c * 128:(c + 1) * 128], identb2[:nt, :nt]), ["mixin", "identb2"], [pk])
                act(lambda e, psb=psb, si=si: e.copy(out=xn2T[:, :, si * 128:si * 128 + nt], in_=psb[:, 0:1024].rearrange("p (c t) -> p c t", c=8)[:, :, 0:nt]), [pk], [xnk])
                yield
                ps, psb, pk = nps()
                for n in range(2):
                    for c in range(8):
                        mm(ps[:nt, n * 512:(n + 1) * 512], xn2T[:, c, si * 128:si * 128 + nt], wout[:, c, n * 512:(n + 1) * 512], c == 0, c == 7, [xnk, "wout"], [pk])
                for a_ in range(2):
                    dve(lambda e, ps=ps, si=si, a_=a_: e.tensor_tensor(out=x4[:nt, si, a_ * 512:(a_ + 1) * 512], in0=ps[:nt, a_ * 512:(a_ + 1) * 512], in1=x4[:nt, si, a_ * 512:(a_ + 1) * 512], op=ALU.add), [pk, xk], [xk])
                act(lambda e, si=si: e.activation(out=junk2[:nt, :], in_=x4[:nt, si, :], func=AF.Square, accum_out=st2[:nt, 0:1]), [xk], ["junk2", "st2"])
                act(lambda e: e.activation(out=st2[:nt, 1:2], in_=st2[:nt, 0:1], func=AF.Sqrt, bias=EPS, scale=1.0 / D), ["st2"], ["st2"])
                dve(lambda e: e.reciprocal(out=st2[:nt, 2:3], in_=st2[:nt, 1:2]), ["st2"], ["st2"])
                dve(lambda e, si=si: e.tensor_scalar_mul(out=xsb2[:nt, :], in0=x4[:nt, si, :], scalar1=st2[:nt, 2:3]), [xk, "st2"], ["xsb2"])
                yield
                yield
                ps, psb, pk = nps()
                for c in range(8):
                    pe(lambda e, c=c, psb=psb: e.transpose(psb[:, c * 128:c * 128 + nt], xsb2[:nt, c * 128:(c + 1) * 128], identb2[:nt, :nt]), ["xsb2", "identb2"], [pk])
                dve(lambda e, psb=psb, si=si: e.tensor_tensor(out=xn2T[:, :, si * 128:si * 128 + nt], in0=psb[:, 0:1024].rearrange("p (c t) -> p c t", c=8)[:, :, 0:nt],
                                                             in1=nmlp.unsqueeze(2).to_broadcast([128, 8, nt]), op=ALU.mult), [pk, "pB2"], [xnk])
                yield

        def up(i):
            s0, nsub, nt = sbs[i]
            ntt = nsub * nt if nsub == NSUB else nt
            xn2T = xn2[i % 2]
            xnk = "xn2T%d" % (i % 2)
            per = 512 // NTT
            for j2 in range(32 // (2 * per)):
                ps, psb, pk = nps()
                for jj in range(2 * per):
                    j = j2 * 2 * per + jj
                    for c in range(8):
                        mm(ps[:, jj * NTT:jj * NTT + ntt], upb[:, c, j * 128:(j + 1) * 128], xn2T[:, c, 0:ntt], c == 0, c == 7, ["upb%d" % (j // 4), xnk], [pk])
                for bk in range(2):
                    r_ = rl[bk]
                    rk_ = "rl%d" % bk
                    j0 = j2 * 2 * per + bk * per
                    psv = ps[:, bk * 512:(bk + 1) * 512].rearrange("p (j t) -> p j t", j=per)[:, :, 0:ntt]
                    rv = r_[:, :].rearrange("p (j t) -> p j t", j=per)[:, :, 0:ntt]
                    act(lambda e, psv=psv, rv=rv: e.activation(out=rv, in_=psv, func=AF.Relu), [pk], [rk_])
                    if bk == 0:
                        dve(lambda e, rv=rv, j0=j0: e.tensor_tensor(out=hT[:, j0:j0 + per, 0:ntt], in0=rv, in1=rv, op=ALU.mult), [rk_], ["hT"])
                    else:
                        pool(lambda e, rv=rv, j0=j0: e.tensor_tensor(out=hT[:, j0:j0 + per, 0:ntt], in0=rv, in1=rv, op=ALU.mult), [rk_], ["hT"])
                yield

        def down(i):
            s0, nsub, nt = sbs[i]
            x4 = xb[i % 2]
            xk = "xb%d" % (i % 2)
            for si in range(nsub):
                ps, psb, pk = nps()
                for n in range(2):
                    for j in range(32):
                        mm(ps[:nt, n * 512:(n + 1) * 512], hT[:, j, si * 128:si * 128 + nt], dnb[:, j, n * 512:(n + 1) * 512], j == 0, j == 31, ["hT", "dnb%d" % (j // 4)], [pk])
                for a_ in range(2):
                    dve(lambda e, ps=ps, si=si, a_=a_: e.tensor_tensor(out=x4[:nt, si, a_ * 512:(a_ + 1) * 512], in0=ps[:nt, a_ * 512:(a_ + 1) * 512], in1=x4[:nt, si, a_ * 512:(a_ + 1) * 512], op=ALU.add), [pk, xk], [xk])
                act(lambda e, si=si: e.activation(out=junk2[:nt, :], in_=x4[:nt, si, :], func=AF.Square, accum_out=st2[:nt, 4:5]), [xk], ["junk2", "st2"])
                act(lambda e: e.activation(out=st2[:nt, 5:6], in_=st2[:nt, 4:5], func=AF.Sqrt, bias=EPS, scale=1.0 / D), ["st2"], ["st2"])
                dve(lambda e: e.reciprocal(out=st2[:nt, 6:7], in_=st2[:nt, 5:6]), ["st2"], ["st2"])
                dve(lambda e, si=si: e.scalar_tensor_tensor(out=x4[:nt, si, :], in0=x4[:nt, si, :], scalar=st2[:nt, 6:7], in1=nfw[:nt, :], op0=ALU.mult, op1=ALU.mult), [xk, "st2", "nfw"], [xk])
            if nsub == NSUB:
                dma("pool", yp_v[:, s0:s0 + NSUB, :], x4[:], [xk], [], "yo%d" % (i % 2))
            else:
                dma("pool", ys, x4[:nt, 0, :], [xk], [], "yo%d" % (i % 2))

        def run2(gl):
            gl = list(gl)
            while gl:
                for g_ in list(gl):
                    try:
                        next(g_)
                    except StopIteration:
                        gl.remove(g_)

        run2([front(0)])
        for i in range(len(sbs)):
            gl = [up(i)]
            if i + 1 < len(sbs):
                gl.append(front(i + 1))
            run2(gl)
            down(i)
        PP[0].finalize()
    es_ps.close()
    ctx.close()
    return nc


_CACHE = {}


def _host_packs(inp, core):
    f = np.float32
    L = 0
    pa = np.zeros((128, NA), f)

    def put(n, arr):
        a, b = OFF[n]
        pa[:, a:b] = arr

    rep = lambda v: np.broadcast_to(np.asarray(v, f).reshape(1, -1), (128, np.asarray(v).size))
    put("mnw", rep(inp["mlstm_norm_w"][L]))
    put("w0", rep(inp["rw_w0"][L]))
    put("a0", rep(inp["rw_a0"][L]))
    put("kk", rep(inp["rw_k_k"][L]))
    put("ka", rep(inp["rw_k_a"][L]))
    put("rk", rep(inp["rw_r_k"][L].reshape(-1)))
    put("lnw", rep(inp["rw_ln_w"][L]))
    put("lnb", rep(inp["rw_ln_b"][L]))
    put("ifb", rep(np.concatenate([inp["mlstm_i_b"][L], inp["mlstm_f_b"][L]])))
    put("nmw", inp["norm_mix_w"][L].reshape(8, 128).T)
    put("nmlp", inp["norm_mlp_w"][L].reshape(8, 128).T)
    cw = inp["mlstm_conv_w"][L]
    put("cw", cw.reshape(4, 8, 128).transpose(2, 1, 0).reshape(128, 32))
    put("cb", inp["mlstm_conv_b"][L].reshape(8, 128).T)
    put("ident", np.eye(128, dtype=f))
    put("mui", np.triu(np.ones((128, 128), f), 0))
    put("mus", np.triu(np.ones((128, 128), f), 1))
    put("mls", np.tril(np.ones((128, 128), f), -1))
    put("ones", np.ones((128, 128), f))
    lsel = np.zeros((128, 128), f)
    rsel = np.zeros((128, 4), f)
    for h in range(8):
        lsel[h, (h % 2) * 64:(h % 2) * 64 + 64] = 1.0
        rsel[h, h // 2] = 1.0
    put("lsel", lsel)
    put("rsel", rsel)
    return pa


def _sample_pack(inp):
    f = np.float32
    L = 0
    sp = np.zeros((128, NSP), f)

    def bh(v512):
        return np.tile(np.asarray(v512, f).reshape(8, 64), (NS, 1))

    def put(n, arr):
        a, b = SOFF[n]
        sp[:, a:b] = arr

    mu = inp["rw_mu"][L]
    put("mu_r", bh(mu[0:512]))
    put("mu_k", bh(mu[512:1024]))
    put("mu_v", bh(mu[1024:1536]))
    cw = inp["mlstm_conv_w"][L]
    put("cwq", np.concatenate([bh(cw[j, 0:512]) for j in range(4)], axis=1))
    put("cwk", np.concatenate([bh(cw[j, 512:1024]) for j in range(4)], axis=1))
    cb = inp["mlstm_conv_b"][L]
    put("cbq", bh(cb[0:512]))
    put("cbk", bh(cb[512:1024]))
    put("mnw", bh(inp["mlstm_norm_w"][L]))
    put("w0", bh(inp["rw_w0"][L]))
    put("a0", bh(inp["rw_a0"][L]))
    put("kk", bh(inp["rw_k_k"][L]))
    put("ka", bh(inp["rw_k_a"][L]))
    put("rk", bh(inp["rw_r_k"][L].reshape(-1)))
    put("lnw", bh(inp["rw_ln_w"][L]))
    put("lnb", bh(inp["rw_ln_b"][L]))
    put("ib", np.tile(inp["mlstm_i_b"][L].reshape(8, 1), (NS, 1)))
    put("fb", np.tile(inp["mlstm_f_b"][L].reshape(8, 1), (NS, 1)))
    return sp


def kernel(**inp):
    f = np.float32
    inp = {k: np.asarray(v) for k, v in inp.items()}
    if "nc" not in _CACHE:
        _CACHE["nc"] = build_program()
    nc = _CACHE["nc"]
    L = 0
    pa = _host_packs(inp, 0)
    sp = _sample_pack(inp)
    mu = inp["rw_mu"][L]
    luw = np.concatenate([inp["rw_w_up"][L], inp["rw_a_up"][L]], axis=0).astype(f)
    common = {
        "w_in": np.ascontiguousarray(inp["w_in"][L], f),
        "w_out": np.ascontiguousarray(inp["w_out"][L], f),
        "mlp_up": np.ascontiguousarray(inp["mlp_up"][L], f),
        "mlp_down": np.ascontiguousarray(inp["mlp_down"][L], f),
        "packA": pa,
        "mu_b": np.ascontiguousarray(np.broadcast_to(mu.reshape(1, -1), (128, RWW)), f),
        "nfw_b": np.ascontiguousarray(np.broadcast_to(inp["norm_f_w"].reshape(1, -1), (128, D)), f),
        "luw": luw,
        "gup": np.ascontiguousarray(inp["rw_g_up"][L], f),
        "spack": sp,
        "mul": np.ascontiguousarray(np.broadcast_to(mu[1536:1792].reshape(1, -1), (NS, 256)), f),
    }
    in_maps = []
    for c in range(8):
        rs = slice(c * NS, (c + 1) * NS)
        m = dict(common)
        m["xp"] = np.ascontiguousarray(inp["x_prompt"][c], f)
        m["xs"] = np.ascontiguousarray(inp["x_sample"][rs, 0, :], f)
        m["sC"] = np.ascontiguousarray(inp["state_mlstm_C"][L, rs].reshape(128, 4096), f)
        m["sn"] = np.ascontiguousarray(inp["state_mlstm_n"][L, rs].reshape(128, 64), f)
        m["sm"] = np.ascontiguousarray(inp["state_mlstm_m"][L, rs].reshape(128, 1), f)
        cv = inp["state_mlstm_conv"][L, rs]
        m["sconv"] = np.ascontiguousarray(cv.reshape(NS, 3, 2, 8, 64).transpose(0, 3, 2, 1, 4).reshape(128, 2, 3, 64), f)
        m["sS"] = np.ascontiguousarray(inp["state_rwkv_S"][L, rs].reshape(128, 4096), f)
        sh = inp["state_rwkv_shift"][L, rs, 0, :]
        m["sshift"] = np.ascontiguousarray(sh[:, 0:1536].reshape(NS, 3, 8, 64).transpose(0, 2, 1, 3).reshape(128, 3, 64), f)
        m["sshl"] = np.ascontiguousarray(sh[:, 1536:1792], f)
        in_maps.append(m)
    res = run_bass_kernel_spmd(nc, in_maps, core_ids=list(range(8)))
    R = res.results
    y_prompt = np.stack([R[c]["yp"] for c in range(8)]).astype(f)
    y_sample = np.concatenate([R[c]["ys"] for c in range(8)], axis=0).reshape(128, 1, D).astype(f)
    pC = np.zeros((1, 8, 8, 64, 64), f)
    pn = np.zeros((1, 8, 8, 64), f)
    pm = np.zeros((1, 8, 8), f)
    pconv = np.zeros((1, 8, 3, 1024), f)
    pS = np.zeros((1, 8, 8, 64, 64), f)
    pshift = np.zeros((1, 8, 1, RWW), f)
    for c in range(8):
        oC = R[c]["oC"].reshape(2, 64, 4, 65)
        Ch = oC.transpose(2, 0, 1, 3).reshape(8, 64, 65)
        pC[0, c] = Ch[:, :, 0:64]
        pn[0, c] = Ch[:, :, 64]
        pm[0, c] = R[c]["om"].reshape(8)
        pconv[0, c] = R[c]["oconv"].transpose(2, 1, 0).reshape(3, 1024)
        oS = R[c]["oS"].reshape(2, 64, 4, 64)
        pS[0, c] = oS.transpose(2, 0, 3, 1).reshape(8, 64, 64)
        pshift[0, c, 0] = R[c]["oshift"].reshape(RWW)
    sC = np.concatenate([R[c]["osC"].reshape(NS, 8, 64, 64) for c in range(8)])[None].astype(f)
    sn = np.concatenate([R[c]["osn"].reshape(NS, 8, 64) for c in range(8)])[None].astype(f)
    sm = np.concatenate([R[c]["osm"].reshape(NS, 8) for c in range(8)])[None].astype(f)
    sconv = np.concatenate([R[c]["osconv"].reshape(NS, 8, 2, 3, 64).transpose(0, 3, 2, 1, 4).reshape(NS, 3, 1024) for c in range(8)])[None].astype(f)
    sS = np.concatenate([R[c]["osS"].reshape(NS, 8, 64, 64) for c in range(8)])[None].astype(f)
    sshift = np.concatenate([
        np.concatenate([R[c]["osshift"].reshape(NS, 8, 3, 64).transpose(0, 2, 1, 3).reshape(NS, 1536), R[c]["osshl"]], axis=1)
        for c in range(8)]).reshape(1, 128, 1, RWW).astype(f)
    return (y_prompt, y_sample, pC, pn, pm, pconv, pS, pshift, sC, sn, sm, sconv, sS, sshift)
```

```python
import contextlib
import numpy as np
import concourse.bass as bass
import concourse.mybir as mybir
from concourse.bass_utils import run_bass_kernel_spmd

F32 = mybir.dt.float32
BF16 = mybir.dt.bfloat16
AF = mybir.ActivationFunctionType
ALU = mybir.AluOpType
AX = mybir.AxisListType

D = 1024
T = 2048
NB = 16
NS = 16
INW = 3856
MLW = 2064
RWW = 1792
DFF = 4096
EPS = 1e-6
GN_EPS = 64e-5
C0 = 0.6065306597126334

OFF = {}
_o = 0
for _n, _w in [("mnw", 512), ("w0", 512), ("a0", 512), ("kk", 512), ("ka", 512), ("rk", 512),
               ("lnw", 512), ("lnb", 512), ("ifb", 16), ("nmw", 8), ("nmlp", 8), ("cw", 32), ("cb", 8),
               ("ident", 128), ("mui", 128), ("mus", 128), ("mls", 128), ("ones", 128),
               ("lsel", 128), ("rsel", 4)]:
    OFF[_n] = (_o, _o + _w)
    _o += _w
NA = _o
SOFF = {}
_o = 0
for _n, _w in [("mu_r", 64), ("mu_k", 64), ("mu_v", 64), ("cwq", 256), ("cwk", 256), ("cbq", 64), ("cbk", 64),
               ("mnw", 64), ("w0", 64), ("a0", 64), ("kk", 64), ("ka", 64), ("rk", 64), ("lnw", 64), ("lnb", 64),
               ("ib", 1), ("fb", 1)]:
    SOFF[_n] = (_o, _o + _w)
    _o += _w
NSP = _o


ALIAS = {"r_sb0": "G0", "kf_sb0": "G1", "vf0": "G2", "osig0": "G13", "r_sb1": "G20", "kf_sb1": "G21", "vf1": "G22", "osig1": "G23",
         "wsig": "G3", "a_sb": "G4", "g_sb": "G5", "kap": "G6", "ktl": "G7",
         "bvec": "G8", "e1": "G9", "e2": "G10", "e3": "G11", "pcw": "G12", "ysb": "G11", "tA": "G9", "tB": "G10", "plast": "G16",
         "hml": "G14", "nlrep": "G15", "Fb": "G15", "cacc": "G16", "qks": "G16", "tAm": "G18", "tBm": "G19",
         "junk": "xm", "ctmp": "xm", "mixT": "TMbA", "TMb0": "TMbA", "TMb1": "TMbA",
         "TMb2": "TMbB", "TMb3": "TMbB", "Ub": "Zb1", "Ss": "Cs", "lo3": "spj", "smix": "spj",
         "xt1": "xt0", "xnT1": "xnT0", "qkx1": "qkx0"}
PARKEYS = {"r_sb", "kf_sb", "vf", "osig", "vaug", "gt", "qpb", "kTb", "ktm", "vrw", "lor"}
CURP = [0]


class SemCtx:
    def __init__(self, nc):
        self.nc = nc
        self.es = contextlib.ExitStack()
        self.engs = ["pe", "act", "dve", "pool", "sp"]
        self.esem = {e: self.es.enter_context(nc.semaphore("s_" + e)) for e in self.engs}
        self.ecnt = {e: 0 for e in self.engs}
        self.bsem = self.es.enter_context(nc.semaphore("s_bar"))
        self.phase = 0
        self.gsem = {}
        self.gbase = {}

    def group_sem(self, g):
        if g not in self.gsem:
            self.gsem[g] = self.es.enter_context(self.nc.semaphore("g_%d" % len(self.gsem)))
            self.gbase[g] = 0
        return self.gsem[g]

    def close(self):
        self.es.close()


class Prog:
    max_ops = None

    def __init__(self, ctx):
        self.ctx = ctx
        self.nc = ctx.nc
        self.ops = []
        self.last_writer = {}
        self.readers = {}
        self.dma_groups = {}

    def op(self, eng, fn, reads=(), writes=(), dma_group=None, wait_total=False):
        if self.max_ops is not None and len(self.ops) >= self.max_ops:
            return None
        reads = [(k + str(CURP[0])) if k in PARKEYS else k for k in reads]
        writes = [(k + str(CURP[0])) if k in PARKEYS else k for k in writes]
        reads = [ALIAS.get(k, k) for k in reads]
        writes = [ALIAS.get(k, k) for k in writes]
        if eng != "pe":
            writes = writes + [k for k in reads if k.startswith("PS") and k not in writes]
        deps = set()
        for b in reads:
            if b in self.last_writer:
                deps.add(self.last_writer[b])
        for b in writes:
            if b in self.last_writer:
                deps.add(self.last_writer[b])
            for r in self.readers.get(b, ()):
                deps.add(r)
        idx = len(self.ops)
        if dma_group is not None:
            deps = {d for d in deps if self.ops[d]["dma"] != dma_group}
        o = dict(eng=eng, fn=fn, deps=sorted(deps), dma=dma_group, idx=idx)
        if dma_group is not None:
            g = self.dma_groups.setdefault(dma_group, dict(total=0, wait_total=wait_total))
            g["total"] += 1
            o["dma_cnt"] = g["total"]
        self.ops.append(o)
        for b in reads:
            self.readers.setdefault(b, []).append(idx)
        for b in writes:
            self.last_writer[b] = idx
            self.readers[b] = []
        return idx

    def finalize(self):
        nc = self.nc
        ctx = self.ctx
        ops = self.ops
        needed = set()
        for o in ops:
            best = {}
            rd = []
            for d in o["deps"]:
                p = ops[d]
                if p["dma"] is not None:
                    rd.append(d)
                else:
                    if p["eng"] == "pe" and o["eng"] == "pe" and o["dma"] is None:
                        continue
                    best[p["eng"]] = max(best.get(p["eng"], -1), d)
            rd.extend(best.values())
            o["deps"] = sorted(rd)
            for d in best.values():
                needed.add(d)
        engs = ctx.engs
        last = {}
        for o in ops:
            if o["dma"] is None:
                last[o["eng"]] = o["idx"]
        needed |= set(last.values())
        cnt = dict(ctx.ecnt)
        for o in ops:
            if o["dma"] is None and o["idx"] in needed:
                cnt[o["eng"]] += 1
                o["sig"] = cnt[o["eng"]]
        for g in self.dma_groups:
            ctx.group_sem(g)
        phase = ctx.phase
        with nc.Block() as block:

            def emit_engine(ename, eng):
                known = {}
                if phase > 0:
                    eng.wait_ge(ctx.bsem, phase)
                for o in ops:
                    if o["eng"] != ename:
                        continue
                    for d in o["deps"]:
                        p = ops[d]
                        if p["dma"] is not None:
                            g = self.dma_groups[p["dma"]]
                            sem = ctx.gsem[p["dma"]]
                            val = ctx.gbase[p["dma"]] + 16 * (g["total"] if g["wait_total"] else p["dma_cnt"])
                            key = ("g", p["dma"])
                        else:
                            if p["eng"] == "pe" and ename == "pe" and o["dma"] is None:
                                continue
                            sem = ctx.esem[p["eng"]]
                            val = p["sig"]
                            key = ("e", p["eng"])
                        if known.get(key, 0) >= val:
                            continue
                        known[key] = val
                        eng.wait_ge(sem, val)
                    ins = o["fn"](eng)
                    if o["dma"] is not None:
                        ins.then_inc(ctx.gsem[o["dma"]], 16)
                    elif "sig" in o:
                        ins.then_inc(ctx.esem[ename], 1)
                if ename == "sp":
                    for e2 in engs:
                        if cnt[e2] > ctx.ecnt[e2]:
                            eng.wait_ge(ctx.esem[e2], cnt[e2])
                    for g, info in self.dma_groups.items():
                        eng.wait_ge(ctx.gsem[g], ctx.gbase[g] + 16 * info["total"])
                    eng.sem_inc(ctx.bsem, 1)

            @block.tensor
            def _(e):
                emit_engine("pe", e)

            @block.scalar
            def _(e):
                emit_engine("act", e)

            @block.vector
            def _(e):
                emit_engine("dve", e)

            @block.gpsimd
            def _(e):
                emit_engine("pool", e)

            @block.sync
            def _(e):
                emit_engine("sp", e)

        ctx.ecnt = cnt
        for g, info in self.dma_groups.items():
            ctx.gbase[g] += 16 * info["total"]
        ctx.phase += 1


STOP_EARLY = True


class _StopBuild(Exception):
    pass


def build_program(do_sample=True, debug=False):
    nc = bass.Bass("TRN2", target_bir_lowering=False)
    try:
        return _build_program(nc, do_sample, debug)
    except _StopBuild:
        return nc


def _build_program(nc, do_sample, debug):
    dbg = nc.dram_tensor("dbg", [128, 16, 512], F32, kind="ExternalOutput").ap() if debug else None
    din = lambda n, s: nc.dram_tensor(n, s, F32, kind="ExternalInput").ap()
    dout = lambda n, s: nc.dram_tensor(n, s, F32, kind="ExternalOutput").ap()
    xp = din("xp", [T, D])
    xs = din("xs", [NS, D])
    w_in = din("w_in", [D, INW])
    w_out = din("w_out", [D, D])
    mlp_up = din("mlp_up", [D, DFF])
    mlp_down = din("mlp_down", [DFF, D])
    packA_d = din("packA", [128, NA])
    mu_d = din("mu_b", [128, RWW])
    nfw_d = din("nfw_b", [128, D])
    wup_d = din("luw", [128, 512])
    gup_d = din("gup", [128, 512])
    spk_d = din("spack", [128, NSP])
    sC_d = din("sC", [128, 4096])
    sn_d = din("sn", [128, 64])
    sm_d = din("sm", [128, 1])
    sconv_d = din("sconv", [128, 2, 3, 64])
    sS_d = din("sS", [128, 4096])
    sshift_d = din("sshift", [128, 3, 64])
    sshl_d = din("sshl", [NS, 256])
    mul_d = din("mul", [NS, 256])

    yp = dout("yp", [T, D])
    ys = dout("ys", [NS, D])
    oC = dout("oC", [128, 4, 65])
    om = dout("om", [8, 1])
    oconv = dout("oconv", [128, 8, 3])
    oS = dout("oS", [128, 4, 64])
    oshift = dout("oshift", [1, RWW])
    osC = dout("osC", [128, 4096])
    osn = dout("osn", [128, 64])
    osm = dout("osm", [128, 1])
    osconv = dout("osconv", [128, 2, 3, 64])
    osS = dout("osS", [128, 4096])
    osshift = dout("osshift", [128, 3, 64])
    osshl = dout("osshl", [NS, 256])

    mix_d = nc.dram_tensor("mix_scr", [T + NS, D], BF16, kind="Internal").ap()
    scr1 = nc.dram_tensor("scr1", [128, 8, 64], F32, kind="Internal").ap()
    scr2 = nc.dram_tensor("scr2", [128, 3, 64], F32, kind="Internal").ap()
    scr3 = nc.dram_tensor("scr3", [NS, 2, 8, 64], F32, kind="Internal").ap()

    ctx = SemCtx(nc)
    PP = [Prog(ctx)]
    es_res = contextlib.ExitStack()
    cur = [es_res]

    def TT(name, shape, dt=F32):
        return cur[0].enter_context(nc.sbuf_tensor("t_" + name, list(shape), dt))

    def dma(q, out, in_, reads, writes, group, wait_total=False):
        group = "%s@%s" % (group, q)
        PP[0].op(q, lambda e: e.dma_start(out=out, in_=in_), reads=reads, writes=writes, dma_group=group, wait_total=wait_total)

    def dve(fn, r, w):
        PP[0].op("dve", fn, reads=r, writes=w)

    def act(fn, r, w):
        PP[0].op("act", fn, reads=r, writes=w)

    def pool(fn, r, w):
        PP[0].op("pool", fn, reads=r, writes=w)

    def pe(fn, r, w):
        PP[0].op("pe", fn, reads=r, writes=w)

    def mm(out, lhsT, rhs, start, stop, r, w):
        pe(lambda e: e.matmul(out, lhsT=lhsT, rhs=rhs, start=start, stop=stop), r, w)

    es_ps = contextlib.ExitStack()
    PS = [es_ps.enter_context(nc.psum_tensor("PS%d" % i, [128, 1024], F32)) for i in range(4)]
    PSB = [p.bitcast(BF16) for p in PS]
    psi = {"A": 0, "F": 0, "B": 0}
    POOLS = {"A": [0, 1, 2, 3], "F": [0, 1], "B": [2, 3]}
    CURPOOL = ["A"]

    def nps():
        pl = CURPOOL[0]
        lst = POOLS[pl]
        i = lst[psi[pl] % len(lst)]
        psi[pl] += 1
        return PS[i], PSB[i], "PS%d" % i

    wq = TT("wq", [128, 8, INW], BF16)
    mub = TT("mub", [128, RWW], F32)
    luw = TT("luw", [128, 512], BF16)
    gup = TT("gup", [128, 512], BF16)
    pA = TT("pA", [128, NA], F32)
    identb = TT("identb", [128, 128], BF16)

    def PA(n):
        a, b = OFF[n]
        return pA[:, a:b]

    w_in_v = w_in.rearrange("(c p) n -> p c n", p=128)
    dma("sp", pA[:], packA_d, [], ["pA"], "init", True)
    for c in range(8):
        dma("pool", wq[:, c, :], w_in_v[:, c, :], [], ["wq"], "init", True)
    dma("pool", luw[:], wup_d, [], ["luw"], "init", True)
    dma("pool", gup[:], gup_d, [], ["gup"], "init", True)
    w_out_v = w_out.rearrange("(c p) n -> p c n", p=128)
    dve(lambda e: e.tensor_copy(out=identb[:], in_=PA("ident")), ["pA"], ["identb"])


    def _dbgdump(tag):
        if debug != tag:
            return
        dstg_ = cur[0].enter_context(nc.sbuf_tensor("t_dbgst%d" % tag, [128, 512], F32))
        def dd(slot, ap, key, n):
            dve(lambda e: e.tensor_copy(out=dstg_[:, 0:n], in_=ap), [key], ["dbgst"])
            dma("sp", dbg[:, slot, 0:n], dstg_[:, 0:n], ["dbgst"], [], "dbg")
        dd(0, PA("ident"), "pA", 128)
        dd(1, PA("mui"), "pA", 128)
        dd(2, PA("mnw"), "pA", 512)
        dd(3, PA("w0"), "pA", 512)
        dd(4, PA("lnb"), "pA", 512)
        PP[0].max_ops = len(PP[0].ops)
        PP[0].finalize()
        raise _StopBuild()
    _dbgdump(3)
    dma("sp", mub[:], mu_d, [], ["mub"], "init", True)
    PP[0].finalize()
    PP[0] = Prog(ctx)

    def rmsnorm_T(xt, xk, nt, dstT, dstk, col0, wname, tmpb, tmpbk, junk, junkk, st, stk):
        act(lambda e: e.activation(out=junk[:nt, :], in_=xt[:nt, :], func=AF.Square, accum_out=st[:nt, 0:1]), [xk], [junkk, stk])
        act(lambda e: e.activation(out=st[:nt, 1:2], in_=st[:nt, 0:1], func=AF.Sqrt, bias=EPS, scale=1.0 / D), [stk], [stk])
        dve(lambda e: e.reciprocal(out=st[:nt, 2:3], in_=st[:nt, 1:2]), [stk], [stk])
        dve(lambda e: e.tensor_scalar_mul(out=tmpb[:nt, :], in0=xt[:nt, :], scalar1=st[:nt, 2:3]), [xk, stk], [tmpbk])
        ps, psb, pk = nps()
        for c in range(8):
            pe(lambda e, c=c: e.transpose(psb[:, c * 128:c * 128 + nt], tmpb[:nt, c * 128:(c + 1) * 128], identb[:nt, :nt]), [tmpbk, "identb"], [pk])
        a, b_ = OFF[wname]
        dve(lambda e: e.tensor_tensor(out=dstT[:, :, col0:col0 + nt],
                                      in0=psb[:, 0:1024].rearrange("p (c t) -> p c t", c=8)[:, :, 0:nt],
                                      in1=pA[:, a:b_].unsqueeze(2).to_broadcast([128, 8, nt]), op=ALU.mult), [pk, "pA"], [dstk])

    def head_ml(nt, hsrc, hk, osig, ok, mix, mixk, W, sfx=""):
        tA, tB, s8 = W["tA"], W["tB"], W["s8"]
        h3 = lambda t: t[:nt, :].rearrange("p (h d) -> p h d", h=8)
        bc = lambda t, c: t[:nt, c:c + 8].unsqueeze(2).to_broadcast([nt, 8, 64])
        dve(lambda e: e.tensor_tensor(out=tA[:nt, :], in0=hsrc[:nt, :], in1=osig[:nt, :], op=ALU.mult), [hk, ok], ["tA" + sfx])
        dve(lambda e: e.tensor_tensor(out=tB[:nt, :], in0=tA[:nt, :], in1=tA[:nt, :], op=ALU.mult), ["tA" + sfx], ["tB" + sfx])
        dve(lambda e: e.tensor_reduce(out=s8[:nt, 0:8], in_=h3(tB), axis=AX.X, op=ALU.add), ["tB" + sfx], ["s8" + sfx])
        act(lambda e: e.activation(out=s8[:nt, 8:16], in_=s8[:nt, 0:8], func=AF.Sqrt, bias=EPS, scale=1.0 / 64), ["s8" + sfx], ["s8" + sfx])
        dve(lambda e: e.reciprocal(out=s8[:nt, 16:24], in_=s8[:nt, 8:16]), ["s8" + sfx], ["s8" + sfx])
        dve(lambda e: e.tensor_tensor(out=h3(tB), in0=h3(tA), in1=bc(s8, 16), op=ALU.mult), ["tA" + sfx, "s8" + sfx], ["tB" + sfx])
        dve(lambda e: e.tensor_tensor(out=mix[:nt, 0:512], in0=tB[:nt, :], in1=PA("mnw")[:nt, :], op=ALU.mult), ["tB" + sfx, "pA"], [mixk])

    def head_rw(nt, ysrc, yk, bon, bonk, vf, vfk, g, gk, mix, mixk, W):
        tA, tB, s8 = W["tA"], W["tB"], W["s8"]
        h3 = lambda t: t[:nt, :].rearrange("p (h d) -> p h d", h=8)
        bc = lambda t, c: t[:nt, c:c + 8].unsqueeze(2).to_broadcast([nt, 8, 64])
        dve(lambda e: e.tensor_tensor(out=h3(tA), in0=h3(vf), in1=bc(bon, 0), op=ALU.mult), [vfk, bonk], ["tA"])
        dve(lambda e: e.tensor_tensor(out=tA[:nt, :], in0=tA[:nt, :], in1=ysrc[:nt, :], op=ALU.add), ["tA", yk], ["tA"])
        dve(lambda e: e.tensor_reduce(out=s8[:nt, 24:32], in_=h3(tA), axis=AX.X, op=ALU.add), ["tA"], ["s8"])
        dve(lambda e: e.tensor_scalar_mul(out=s8[:nt, 24:32], in0=s8[:nt, 24:32], scalar1=1.0 / 64), ["s8"], ["s8"])
        dve(lambda e: e.tensor_tensor(out=h3(tA), in0=h3(tA), in1=bc(s8, 24), op=ALU.subtract), ["tA", "s8"], ["tA"])
        dve(lambda e: e.tensor_tensor(out=tB[:nt, :], in0=tA[:nt, :], in1=tA[:nt, :], op=ALU.mult), ["tA"], ["tB"])
        dve(lambda e: e.tensor_reduce(out=s8[:nt, 32:40], in_=h3(tB), axis=AX.X, op=ALU.add), ["tB"], ["s8"])
        act(lambda e: e.activation(out=s8[:nt, 40:48], in_=s8[:nt, 32:40], func=AF.Sqrt, bias=GN_EPS, scale=1.0 / 64), ["s8"], ["s8"])
        dve(lambda e: e.reciprocal(out=s8[:nt, 48:56], in_=s8[:nt, 40:48]), ["s8"], ["s8"])
        dve(lambda e: e.tensor_tensor(out=h3(tB), in0=h3(tA), in1=bc(s8, 48), op=ALU.mult), ["tA", "s8"], ["tB"])
        dve(lambda e: e.tensor_tensor(out=tB[:nt, :], in0=tB[:nt, :], in1=PA("lnw")[:nt, :], op=ALU.mult), ["tB", "pA"], ["tB"])
        dve(lambda e: e.tensor_tensor(out=tB[:nt, :], in0=tB[:nt, :], in1=PA("lnb")[:nt, :], op=ALU.add), ["tB", "pA"], ["tB"])
        dve(lambda e: e.tensor_tensor(out=mix[:nt, 512:1024], in0=tB[:nt, :], in1=g[:nt, :], op=ALU.mult), ["tB", gk], [mixk])

    def out_proj(nt, mix, mixk, xt, xk, row0, W):
        dma("pool", mix_d[row0:row0 + nt, :], mix[:nt, :], [mixk], [], "mixst")

    with contextlib.ExitStack() as es1:
        cur[0] = es1
        W = {}
        Gbig = TT("Gbig", [128, 24, 512])
        G = [Gbig[:, i, :] for i in range(24)]
        W["s8"] = TT("s8", [128, 64])
        W["xm"] = TT("xm", [128, D])
        xt = [TT("xt0", [128, D])] * 2
        junk = W["xm"]
        st = TT("st", [128, 4])
        mix = TT("mix", [128, D], BF16)
        xsb = TT("xsb", [128, D], BF16)
        xnT = [TT("xnT0", [128, 8, 129], BF16)] * 2
        dxT = TT("dxT", [128, 8, 128], BF16)
        qkx = [TT("qkx0", [128, 8, 131])] * 2
        cacc = Gbig[:, 16:18, :].rearrange("p a (c t) -> p (a c) t", t=128)
        ctmp = W["xm"][:, :].rearrange("p (c t) -> p c t", c=8)
        qks = cacc
        QPB = [TT("qpb%d" % i, [128, 4, 128], BF16) for i in range(2)]
        KTB = [TT("kTb%d" % i, [128, 4, 128], BF16) for i in range(2)]
        KTM = [TT("ktm%d" % i, [128, 8, 64], BF16) for i in range(2)]
        VAUG = [TT("vaug%d" % i, [128, 8, 65], BF16) for i in range(2)]
        GT = [TT("gt%d" % i, [128, 96]) for i in range(2)]
        runmax = TT("runmax", [128, 8])
        nBc = TT("nBc", [128, 8])
        Cst = TT("Cst", [128, 4, 65])
        Cbf = TT("Cbf", [128, 4, 65], BF16)
        Fb = G[15].rearrange("p (j t) -> p j t", j=4)
        hml = G[14]
        nlrep = G[15].rearrange("p (h d) -> p h d", h=8)
        wsig, a_sb, g_sb, kap, ktl, bvec, e1, e2, e3, pcw, ysb = G[3], G[4], G[5], G[6], G[7], G[8], G[9], G[10], G[11], G[12], G[11]
        W["tA"], W["tB"] = G[9], G[10]
        WM = {"tA": G[18], "tB": G[19], "s8": TT("s8m", [128, 64])}
        r8b = TT("r8b", [128, 8])
        VRW = [TT("vrw%d" % i, [128, 8, 64], BF16) for i in range(2)]
        LOR = [TT("lor%d" % i, [128, 256], BF16) for i in range(2)]
        lorT = TT("lorT", [128, 2, 128], BF16)
        r8 = TT("r8", [128, 32])
        bon = TT("bon", [128, 8])
        TMb = TT("TMb", [128, 4, 512], BF16)
        W["mixT"] = TMb[:, 0:2, :].rearrange("p a (c t) -> p (a c) t", t=128)
        Btz = TT("Btz", [128, 8, 128], BF16)
        Ktz = TT("Ktz", [128, 8, 128], BF16)
        FMt = TT("FMt", [128, 4, 4, 128], BF16)
        Am = [TT("Am%d" % i, [128, 8, 128], BF16) for i in range(3)]
        PTb = TT("PTb", [128, 8, 128], BF16)
        Pw = [TT("Pw%d" % i, [128, 8, 128], BF16) for i in range(4)]
        Zb = [TT("Zb%d" % i, [128, 8, 64], BF16) for i in range(2)]
        Ub = Zb[1]
        Sst = TT("Sst", [128, 4, 64])
        Sbf = TT("Sbf", [128, 4, 64], BF16)
        WLfm = TT("WLfm", [128, 4])
        plast = G[17]

        for p_ in range(2):
            pool(lambda e, p_=p_: e.memset(VAUG[p_][:], 1.0), [], ["vaug%d" % p_])
        pool(lambda e: e.memset(Btz[:], 0.0), [], ["Btz"])
        pool(lambda e: e.memset(Ktz[:], 0.0), [], ["Ktz"])
        pool(lambda e: e.memset(Cst[:], 0.0), [], ["Cst"])
        pool(lambda e: e.memset(Cbf[:], 0.0), [], ["Cbf"])
        pool(lambda e: e.memset(Sst[:], 0.0), [], ["Sst"])
        pool(lambda e: e.memset(Sbf[:], 0.0), [], ["Sbf"])
        pool(lambda e: e.memset(runmax[:], -1e30), [], ["runmax"])
        pool(lambda e: e.memset(nBc[:], 0.0), [], ["nBc"])
        pool(lambda e: e.memset(xnT[0][:, :, 0:1], 0.0), [], ["xnT0"])
        pool(lambda e: e.memset(qkx[0][:, :, 0:3], 0.0), [], ["qkx0"])

        if debug == 2:
            dstg = TT("dbgstage", [128, 512]) if False else G[12]
            def ddump0(slot, ap, key, n):
                dve(lambda e: e.tensor_copy(out=dstg[:, 0:n], in_=ap), [key], ["pcw"])
                dma("sp", dbg[:, slot, 0:n], dstg[:, 0:n], ["pcw"], [], "dbg")
            ddump0(0, PA("ident"), "pA", 128)
            ddump0(1, PA("mui"), "pA", 128)
            ddump0(2, luw[:, :], "luw", 512)
            ddump0(3, gup[:, :], "gup", 512)
            ddump0(4, W1[:, 0, 0:512], "W1", 512)
            PP[0].max_ops = len(PP[0].ops)
            if STOP_EARLY:
                PP[0].finalize()
                raise _StopBuild()
        MUI = PA("mui")
        MUS = PA("mus")
        MLS = PA("mls")
        ONES = PA("ones")
        IDF = PA("ident")
        bc8 = lambda ap: ap.unsqueeze(2).to_broadcast([128, 8, 64])
        m8 = lambda m: m.unsqueeze(1).to_broadcast([128, 8, 128])
        v3 = lambda t: t[:].rearrange("p (h d) -> p h d", h=8)
        hoff = lambda h: (h % 2) * 512 + (h // 2) * 128

        def make_block(b):
            p_ = b % 2
            r_sb, kf_sb, vf, osig = G[0 + 20 * p_] if p_ == 0 else G[20], G[1] if p_ == 0 else G[21], G[2] if p_ == 0 else G[22], G[13] if p_ == 0 else G[23]
            vaug, gt, qpb, kTb, ktm, vrw, lor = VAUG[p_], GT[p_], QPB[p_], KTB[p_], KTM[p_], VRW[p_], LOR[p_]

            def front_stage():
                x_ = xt[b % 2]
                xk = "xt%d" % (b % 2)
                xn = xnT[b % 2]
                xnk = "xnT%d" % (b % 2)
                qx = qkx[b % 2]
                qxk = "qkx%d" % (b % 2)
                dma("sp", x_[:], xp[b * 128:(b + 1) * 128, :], [], [xk], xk)
                rmsnorm_T(x_, xk, 128, xn, xnk, 1, "nmw", xsb, "xsb", junk, "junk", st, "st")
                cur_x = xn[:, :, 1:129]
                prv_x = xn[:, :, 0:128]
                dve(lambda e, xn=xn: e.tensor_tensor(out=dxT[:], in0=xn[:, :, 0:128], in1=xn[:, :, 1:129], op=ALU.subtract), [xnk], ["dxT"])

                yield
                ps, psb, pk = nps()
                for j in range(8):
                    for c in range(8):
                        mm(ps[:, j * 128:(j + 1) * 128], wq[:, c, j * 128:(j + 1) * 128], cur_x[:, c, :], c == 0, c == 7, ["wq", xnk], [pk])
                for a_ in range(2):
                    act(lambda e, ps=ps, qx=qx, a_=a_: e.copy(out=qx[:, 4 * a_:4 * a_ + 4, 3:131], in_=ps[:, a_ * 512:(a_ + 1) * 512].rearrange("p (j t) -> p j t", j=4)), [pk], [qxk])
                if b == NB - 1:
                    dma("pool", oconv, qx[:, :, 128:131], [qxk], [], "fin")

                yield
                def tm_plain(col0, ncol, ps_ap, pk):
                    for c in range(8):
                        mm(ps_ap, cur_x[:, c, :], wq[:, c, col0:col0 + ncol], c == 0, c == 7, [xnk, "wq"], [pk])

                def tm_shift(col0, ncol, dst, dk):
                    ps, psb, pk = nps()
                    for c in range(8):
                        mm(ps[:, 0:ncol], cur_x[:, c, :], wq[:, c, MLW + col0:MLW + col0 + ncol], c == 0, c == 7, [xnk, "wq"], [pk])
                    for c in range(8):
                        mm(ps[:, 512:512 + ncol], dxT[:, c, :], wq[:, c, MLW + col0:MLW + col0 + ncol], c == 0, c == 7, ["dxT", "wq"], [pk])
                    dve(lambda e, ps=ps: e.tensor_tensor(out=dst, in0=ps[:, 512:512 + ncol], in1=mub[:, col0:col0 + ncol], op=ALU.mult), [pk, "mub"], [dk])
                    dve(lambda e, ps=ps: e.tensor_tensor(out=dst, in0=dst, in1=ps[:, 0:ncol], op=ALU.add), [pk, dk], [dk])

                ps, psb, pk = nps()
                tm_plain(1024, 512, ps[:, 0:512], pk)
                tm_plain(1536, 512, ps[:, 512:1024], pk)
                act(lambda e, ps=ps: e.copy(out=vaug[:, :, 0:64], in_=ps[:, 0:512].rearrange("p (h d) -> p h d", h=8)), [pk], ["vaug"])
                act(lambda e, ps=ps: e.activation(out=osig[:], in_=ps[:, 512:1024], func=AF.Sigmoid), [pk], ["osig"])
                ps, psb, pk = nps()
                tm_plain(2048, 16, ps[:, 0:16], pk)
                dve(lambda e, ps=ps: e.tensor_tensor(out=gt[:, 0:16], in0=ps[:, 0:16], in1=PA("ifb"), op=ALU.add), [pk, "pA"], ["gt"])
                tm_shift(0, 512, r_sb[:], "r_sb")
                tm_shift(512, 512, kf_sb[:], "kf_sb")
                tm_shift(1024, 512, vf[:], "vf")
                pool(lambda e: e.tensor_copy(out=vrw[:], in_=vf[:].rearrange("p (h d) -> p h d", h=8)), ["vf"], ["vrw"])
                ltmp = G[16]
                tm_shift(1536, 256, ltmp[:, 0:256], "cacc")
                act(lambda e: e.activation(out=lor[:, 0:64], in_=ltmp[:, 0:64], func=AF.Tanh), ["cacc"], ["lor"])
                act(lambda e: e.copy(out=lor[:, 64:128], in_=ltmp[:, 64:128]), ["cacc"], ["lor"])
                act(lambda e: e.activation(out=lor[:, 128:256], in_=ltmp[:, 128:256], func=AF.Sigmoid), ["cacc"], ["lor"])
                if b == NB - 1:
                    lastc = xn[:, :, 128:129]
                    for n0 in range(0, RWW, 512):
                        nn = min(512, RWW - n0)
                        ps2, _, pk2 = nps()
                        for c in range(8):
                            mm(ps2[0:1, 0:nn], lastc[:, c, :], wq[:, c, MLW + n0:MLW + n0 + nn], c == 0, c == 7, [xnk, "wq"], [pk2])
                        act(lambda e, ps2=ps2, n0=n0, nn=nn: e.copy(out=plast[0:1, 0:nn], in_=ps2[0:1, 0:nn]), [pk2], ["plast"])
                        dma("pool", oshift[:, n0:n0 + nn], plast[0:1, 0:nn], ["plast"], [], "fin")

                act(lambda e: e.activation(out=gt[:, 56:64], in_=gt[:, 8:16], func=AF.Exp, scale=-1.0), ["gt"], ["gt"])
                act(lambda e: e.activation(out=gt[:, 16:24], in_=gt[:, 56:64], func=AF.Ln, bias=1.0, scale=1.0), ["gt"], ["gt"])
                dve(lambda e: e.tensor_copy(out=nlrep[:], in_=bc8(gt[:, 16:24])), ["gt"], ["nlrep"])
                ps, psb, pk = nps()
                mm(ps[:, 0:8], MUI, gt[:, 16:24], True, True, ["pA", "gt"], [pk])
                mm(ps[:, 8:16], ONES, gt[:, 16:24], True, True, ["pA", "gt"], [pk])
                for j in range(4):
                    mm(ps[:, 512 + j * 128:512 + (j + 1) * 128], nlrep[:, 2 * j:2 * j + 2, :].rearrange("p a d -> p (a d)"), MUI, True, True, ["nlrep", "pA"], [pk])
                dve(lambda e, ps=ps: e.tensor_tensor(out=gt[:, 24:32], in0=ps[:, 0:8], in1=gt[:, 0:8], op=ALU.add), [pk, "gt"], ["gt"])
                act(lambda e: e.activation(out=gt[:, 32:40], in_=gt[:, 24:32], func=AF.Exp), ["gt"], ["gt"])
                dve(lambda e, ps=ps: e.tensor_tensor(out=gt[:, 56:64], in0=gt[:, 24:32], in1=ps[:, 8:16], op=ALU.subtract), [pk, "gt"], ["gt"])
                act(lambda e: e.activation(out=gt[:, 40:48], in_=gt[:, 56:64], func=AF.Exp), ["gt"], ["gt"])
                act(lambda e, ps=ps: e.activation(out=gt[:, 48:56], in_=ps[:, 8:16], func=AF.Exp, scale=-1.0), [pk], ["gt"])
                act(lambda e, ps=ps: e.activation(out=Fb[:], in_=ps[:, 512:1024].rearrange("p (j t) -> p j t", j=4), func=AF.Exp, scale=-1.0), [pk], ["Fb"])
                dve(lambda e: e.tensor_tensor(out=gt[:, 56:64], in0=gt[:, 24:32], in1=nBc[:], op=ALU.add), ["gt", "nBc"], ["gt"])
                dve(lambda e: e.tensor_tensor(out=runmax[:], in0=runmax[:], in1=gt[:, 56:64], op=ALU.max), ["gt", "runmax"], ["runmax"])
                dve(lambda e, ps=ps: e.tensor_tensor(out=nBc[:], in0=nBc[:], in1=ps[:, 8:16], op=ALU.add), [pk, "nBc"], ["nBc"])

                yield
                cwv = PA("cw").rearrange("p (c j) -> p c j", j=4)
                wbc = lambda j: cwv[:, :, j:j + 1].to_broadcast([128, 8, 128])
                pool(lambda e, qx=qx: e.tensor_tensor(out=cacc[:], in0=qx[:, :, 3:131], in1=wbc(3), op=ALU.mult), [qxk, "pA"], ["cacc"])
                for j in range(3):
                    pool(lambda e, qx=qx, j=j: e.tensor_tensor(out=ctmp[:], in0=qx[:, :, j:j + 128], in1=wbc(j), op=ALU.mult), [qxk, "pA"], ["ctmp"])
                    pool(lambda e: e.tensor_tensor(out=cacc[:], in0=cacc[:], in1=ctmp[:], op=ALU.add), ["cacc", "ctmp"], ["cacc"])
                pool(lambda e: e.tensor_tensor(out=cacc[:], in0=cacc[:], in1=PA("cb").unsqueeze(2).to_broadcast([128, 8, 128]), op=ALU.add), ["cacc", "pA"], ["cacc"])
                act(lambda e: e.activation(out=qks[:], in_=cacc[:], func=AF.Silu), ["cacc"], ["qks"])
                dve(lambda e: e.tensor_tensor(out=qpb[:], in0=qks[:, 0:4, :], in1=Fb[:], op=ALU.mult), ["qks", "Fb"], ["qpb"])
                act(lambda e: e.activation(out=kTb[:], in_=qks[:, 4:8, :], func=AF.Copy, scale=0.125), ["qks"], ["kTb"])

                yield
                ps, psb, pk = nps()
                for j in range(4):
                    pe(lambda e, j=j, psb=psb: e.transpose(psb[:, j * 128:(j + 1) * 128], kTb[:, j, :], identb[:]), ["kTb", "identb"], [pk])
                dve(lambda e, psb=psb: e.tensor_tensor(out=ktm[:], in0=psb[:, 0:512].rearrange("p (h d) -> p h d", h=8), in1=bc8(gt[:, 40:48]), op=ALU.mult), [pk, "gt"], ["ktm"])

                yield
                yield
                if b + 1 < NB:
                    pool(lambda e, xn=xn: e.tensor_copy(out=xn[:, :, 0:1], in_=xn[:, :, 128:129]), [xnk], [xnk])
                    pool(lambda e, qx=qx: e.tensor_copy(out=qx[:, :, 0:3], in_=qx[:, :, 128:131]), [qxk], [qxk])
                yield

            def ml_stage():
                ps, psb, pk = nps()
                for h in range(8):
                    j, hp = h // 2, h % 2
                    sl = slice(hp * 64, hp * 64 + 64)
                    mm(ps[:, hoff(h):hoff(h) + 128], kTb[sl, j, :], qpb[sl, j, :], True, True, ["kTb", "qpb"], [pk])
                for h in range(8):
                    dve(lambda e, h=h, ps=ps: e.scalar_tensor_tensor(out=PTb[:, h, :], in0=ps[:, hoff(h):hoff(h) + 128], scalar=gt[:, 32 + h:33 + h], in1=MUI, op0=ALU.mult, op1=ALU.mult), [pk, "gt", "pA"], ["PTb"])
                yield
                ps, psb, pk = nps()
                psn = lambda ps, h: ps[:, (h // 4) * 512 + (h % 4) * 65:(h // 4) * 512 + (h % 4) * 65 + 65]
                for h in range(8):
                    j, hp = h // 2, h % 2
                    sl = slice(hp * 64, hp * 64 + 64)
                    mm(psn(ps, h), PTb[:, h, :], vaug[:, h, :], True, False, ["PTb", "vaug"], [pk])
                    mm(psn(ps, h), qpb[sl, j, :], Cbf[sl, j, :], False, True, ["qpb", "Cbf"], [pk])
                pn4 = ps[:, :].rearrange("p (a r) -> p a r", a=2)[:, :, 0:260].rearrange("p a (h d) -> p a h d", h=4)
                for a_ in range(2):
                    act(lambda e, pn4=pn4, a_=a_: e.copy(out=r8[:, 4 * a_:4 * a_ + 4], in_=pn4[:, a_, :, 64]), [pk], ["r8"])
                dve(lambda e: e.scalar_tensor_tensor(out=r8[:, 8:16], in0=r8[:, 0:8], scalar=-1.0, in1=r8[:, 0:8], op0=ALU.mult, op1=ALU.max), ["r8"], ["r8"])
                dve(lambda e: e.tensor_scalar_max(out=r8[:, 8:16], in0=r8[:, 8:16], scalar1=1.0), ["r8"], ["r8"])
                dve(lambda e: e.reciprocal(out=r8[:, 16:24], in_=r8[:, 8:16]), ["r8"], ["r8"])
                for a_ in range(2):
                    dve(lambda e, pn4=pn4, a_=a_: e.tensor_tensor(out=hml[:, a_ * 256:(a_ + 1) * 256].rearrange("p (h d) -> p h d", h=4), in0=pn4[:, a_, :, 0:64],
                                                              in1=r8[:, 16 + 4 * a_:20 + 4 * a_].unsqueeze(2).to_broadcast([128, 4, 64]), op=ALU.mult), [pk, "r8"], ["hml"])
                yield
                ps, psb, pk = nps()
                for h in range(8):
                    j = h // 2
                    mm(psn(ps, h), ktm[:, 2 * j:2 * j + 2, :].rearrange("p a d -> p (a d)"), vaug[:, h, :], True, True, ["ktm", "vaug"], [pk])
                pu4 = ps[:, :].rearrange("p (a r) -> p a r", a=2)[:, :, 0:260].rearrange("p a (h d) -> p a h d", h=4)
                for hp in range(2):
                    sl = slice(hp * 64, hp * 64 + 64)
                    decb = gt[sl, 48:56].rearrange("p (j q) -> p j q", q=2)[:, :, hp:hp + 1].to_broadcast([64, 4, 65])
                    dve(lambda e, sl=sl, decb=decb: e.tensor_tensor(out=Cst[sl, :, :], in0=Cst[sl, :, :], in1=decb, op=ALU.mult), ["Cst", "gt"], ["Cst"])
                    for a in range(2):
                        src = pu4[sl, a, hp::2, :]
                        dve(lambda e, sl=sl, a=a, src=src: e.tensor_tensor(out=Cst[sl, 2 * a:2 * a + 2, :], in0=Cst[sl, 2 * a:2 * a + 2, :], in1=src, op=ALU.add), [pk, "Cst"], ["Cst"])
                act(lambda e: e.copy(out=Cbf[:], in_=Cst[:]), ["Cst"], ["Cbf"])
                head_ml(128, hml, "hml", osig, "osig", mix, "mix", WM, "m")


                yield
            def rw_stage():
                ps, psb, pk = nps()
                pe(lambda e, psb=psb: e.transpose(psb[:, 0:128], lor[:, 0:128], identb[:]), ["lor", "identb"], [pk])
                pe(lambda e, psb=psb: e.transpose(psb[:, 128:256], lor[:, 128:256], identb[:]), ["lor", "identb"], [pk])
                act(lambda e, psb=psb: e.copy(out=lorT[:], in_=psb[:, 0:256].rearrange("p (a t) -> p a t", a=2)), [pk], ["lorT"])
                ps, psb, pk = nps()
                mm(ps[:, 0:512], lorT[0:64, 0, :], luw[0:64, :], True, True, ["lorT", "luw"], [pk])
                mm(ps[:, 512:1024], lorT[64:128, 0, :], luw[64:128, :], True, True, ["lorT", "luw"], [pk])
                dve(lambda e, ps=ps: e.tensor_tensor(out=e1[:], in0=ps[:, 0:512], in1=PA("w0"), op=ALU.add), [pk, "pA"], ["e1"])
                act(lambda e: e.activation(out=wsig[:], in_=e1[:], func=AF.Sigmoid), ["e1"], ["wsig"])
                dve(lambda e, ps=ps: e.tensor_tensor(out=e2[:], in0=ps[:, 512:1024], in1=PA("a0"), op=ALU.add), [pk, "pA"], ["e2"])
                act(lambda e: e.activation(out=a_sb[:], in_=e2[:], func=AF.Sigmoid), ["e2"], ["a_sb"])
                ps, psb, pk = nps()
                mm(ps[:, 0:512], lorT[:, 1, :], gup[:, :], True, True, ["lorT", "gup"], [pk])
                act(lambda e, ps=ps: e.copy(out=g_sb[:], in_=ps[:, 0:512]), [pk], ["g_sb"])
                yield
                dve(lambda e: e.tensor_tensor(out=e1[:], in0=kf_sb[:], in1=PA("kk"), op=ALU.mult), ["kf_sb", "pA"], ["e1"])
                dve(lambda e: e.tensor_tensor(out=e2[:], in0=e1[:], in1=e1[:], op=ALU.mult), ["e1"], ["e2"])
                dve(lambda e: e.tensor_reduce(out=r8b[:, 0:8], in_=v3(e2), axis=AX.X, op=ALU.add), ["e2"], ["r8b"])
                dve(lambda e: e.tensor_scalar_max(out=r8b[:, 0:8], in0=r8b[:, 0:8], scalar1=1e-24), ["r8b"], ["r8b"])
                act(lambda e: e.activation(out=r8b[:, 0:8], in_=r8b[:, 0:8], func=AF.Sqrt), ["r8b"], ["r8b"])
                dve(lambda e: e.reciprocal(out=r8b[:, 0:8], in_=r8b[:, 0:8]), ["r8b"], ["r8b"])
                dve(lambda e: e.tensor_tensor(out=v3(kap), in0=v3(e1), in1=bc8(r8b[:, 0:8]), op=ALU.mult), ["e1", "r8b"], ["kap"])
                dve(lambda e: e.tensor_scalar_add(out=e2[:], in0=a_sb[:], scalar1=-1.0), ["a_sb"], ["e2"])
                dve(lambda e: e.tensor_tensor(out=e2[:], in0=e2[:], in1=PA("ka"), op=ALU.mult), ["e2", "pA"], ["e2"])
                dve(lambda e: e.tensor_tensor(out=e2[:], in0=e2[:], in1=kf_sb[:], op=ALU.mult), ["e2", "kf_sb"], ["e2"])
                dve(lambda e: e.tensor_tensor(out=ktl[:], in0=e2[:], in1=kf_sb[:], op=ALU.add), ["e2", "kf_sb"], ["ktl"])
                dve(lambda e: e.tensor_tensor(out=bvec[:], in0=a_sb[:], in1=kap[:], op=ALU.mult), ["a_sb", "kap"], ["bvec"])
                dve(lambda e: e.tensor_tensor(out=e2[:], in0=r_sb[:], in1=ktl[:], op=ALU.mult), ["r_sb", "ktl"], ["e2"])
                dve(lambda e: e.tensor_tensor(out=e2[:], in0=e2[:], in1=PA("rk"), op=ALU.mult), ["e2", "pA"], ["e2"])
                dve(lambda e: e.tensor_reduce(out=bon[:], in_=v3(e2), axis=AX.X, op=ALU.add), ["e2"], ["bon"])
                yield
                ps, psb, pk = nps()
                mm(ps[:, 0:512], MUI, wsig[:], True, True, ["pA", "wsig"], [pk])
                mm(ps[:, 512:1024], ONES, wsig[:], True, True, ["pA", "wsig"], [pk])
                act(lambda e, ps=ps: e.copy(out=pcw[:], in_=ps[:, 0:512]), [pk], ["pcw"])
                dve(lambda e: e.tensor_tensor(out=e1[:], in0=pcw[:], in1=wsig[:], op=ALU.subtract), ["pcw", "wsig"], ["e1"])
                act(lambda e: e.activation(out=e1[:], in_=e1[:], func=AF.Exp, scale=-C0), ["e1"], ["e1"])
                dve(lambda e: e.tensor_tensor(out=TMb[:, 0, :], in0=kap[:], in1=e1[:], op=ALU.mult), ["kap", "e1"], ["TMb0"])
                act(lambda e: e.activation(out=e2[:], in_=pcw[:], func=AF.Exp, scale=-C0), ["pcw"], ["e2"])
                dve(lambda e: e.tensor_tensor(out=TMb[:, 1, :], in0=r_sb[:], in1=e2[:], op=ALU.mult), ["r_sb", "e2"], ["TMb1"])
                act(lambda e: e.activation(out=e3[:], in_=pcw[:], func=AF.Exp, scale=C0), ["pcw"], ["e3"])
                dve(lambda e: e.tensor_tensor(out=TMb[:, 2, :], in0=bvec[:], in1=e3[:], op=ALU.mult), ["bvec", "e3"], ["TMb2"])
                dve(lambda e: e.tensor_tensor(out=TMb[:, 3, :], in0=ktl[:], in1=e3[:], op=ALU.mult), ["ktl", "e3"], ["TMb3"])
                dve(lambda e, ps=ps: e.tensor_tensor(out=e1[:], in0=ps[:, 512:1024], in1=pcw[:], op=ALU.subtract), [pk, "pcw"], ["e1"])
                act(lambda e: e.activation(out=e1[:], in_=e1[:], func=AF.Exp, scale=-C0), ["e1"], ["e1"])
                for hp in range(2):
                    srcb = v3(bvec).rearrange("p (j q) d -> p j q d", q=2)[:, :, hp, :]
                    srck = v3(ktl).rearrange("p (j q) d -> p j q d", q=2)[:, :, hp, :]
                    wl = v3(e1).rearrange("p (j q) d -> p j q d", q=2)[:, :, hp, :]
                    dstb = Btz[:].rearrange("p (j q) c -> p j q c", q=2)[:, :, hp, hp * 64:hp * 64 + 64]
                    dstk = Ktz[:].rearrange("p (j q) c -> p j q c", q=2)[:, :, hp, hp * 64:hp * 64 + 64]
                    dve(lambda e, srcb=srcb, wl=wl, dstb=dstb: e.tensor_tensor(out=dstb, in0=srcb, in1=wl, op=ALU.mult), ["bvec", "e1"], ["Btz"])
                    dve(lambda e, srck=srck, wl=wl, dstk=dstk: e.tensor_tensor(out=dstk, in0=srck, in1=wl, op=ALU.mult), ["ktl", "e1"], ["Ktz"])
                ps2, _, pk2 = nps()
                for j in range(4):
                    mm(ps2[:, j:j + 1], wsig[:, j * 128:(j + 1) * 128], ONES[:, 0:1], True, True, ["wsig", "pA"], [pk2])
                act(lambda e, ps2=ps2: e.activation(out=WLfm[:], in_=ps2[:, 0:4], func=AF.Exp, scale=-C0), [pk2], ["WLfm"])
                yield
                ps, psb, pk = nps()
                for w_ in range(4):
                    for j in range(4):
                        pe(lambda e, w_=w_, j=j, psb=psb: e.transpose(psb[:, (w_ * 4 + j) * 128:(w_ * 4 + j + 1) * 128], TMb[:, w_, j * 128:(j + 1) * 128], identb[:]), ["TMb%d" % w_, "identb"], [pk])
                for w_ in range(4):
                    eng_ = act if w_ % 2 == 0 else dve
                    if w_ % 2 == 0:
                        act(lambda e, psb=psb, w_=w_: e.copy(out=FMt[:, w_, :, :], in_=psb[:, w_ * 512:(w_ + 1) * 512].rearrange("p (j t) -> p j t", j=4)), [pk], ["FMt"])
                    else:
                        dve(lambda e, psb=psb, w_=w_: e.tensor_copy(out=FMt[:, w_, :, :], in_=psb[:, w_ * 512:(w_ + 1) * 512].rearrange("p (j t) -> p j t", j=4)), [pk], ["FMt"])
                KAP, RB, BB, KKB = 0, 1, 2, 3

                def amat(lw, rw_, dst, dk, mask, neg):
                    ps, psb, pk = nps()
                    for h in range(8):
                        j, hp = h // 2, h % 2
                        sl = slice(hp * 64, hp * 64 + 64)
                        mm(ps[:, hoff(h):hoff(h) + 128], FMt[sl, lw, j, :], FMt[sl, rw_, j, :], True, True, ["FMt"], [pk])
                    psv = ps[:, :].rearrange("p (q j t) -> p q j t", q=2, j=4)
                    dstv = dst[:].rearrange("p (j q) t -> p q j t", q=2)
                    mk = mask.unsqueeze(1).unsqueeze(1).to_broadcast([128, 2, 4, 128])
                    if neg:
                        mk3 = mask.unsqueeze(1).to_broadcast([128, 4, 128])
                        for q in range(2):
                            dve(lambda e, q=q: e.scalar_tensor_tensor(out=dstv[:, q], in0=psv[:, q], scalar=-1.0, in1=mk3, op0=ALU.mult, op1=ALU.mult), [pk, "pA"], [dk])
                    else:
                        mk3 = mask.unsqueeze(1).to_broadcast([128, 4, 128])
                        for q in range(2):
                            dve(lambda e, q=q: e.tensor_tensor(out=dstv[:, q], in0=psv[:, q], in1=mk3, op=ALU.mult), [pk, "pA"], [dk])

                amat(BB, KAP, Pw[1], "Pw1", MUS, True)
                amat(KAP, BB, Pw[0], "Pw0", MLS, True)
                amat(KKB, KAP, Am[0], "Am0", MUS, False)
                amat(BB, RB, Am[1], "Am1", MUI, False)
                amat(KKB, RB, Am[2], "Am2", MUI, False)
                yield
                ps, psb, pk = nps()
                for h in range(8):
                    j, hp = h // 2, h % 2
                    sl = slice(hp * 64, hp * 64 + 64)
                    mm(ps[:, h * 64:(h + 1) * 64], FMt[sl, KAP, j, :], Sbf[sl, j, :], True, False, ["FMt", "Sbf"], [pk])
                    mm(ps[:, h * 64:(h + 1) * 64], Am[0][:, h, :], vrw[:, h, :], False, True, ["Am0", "vrw"], [pk])
                act(lambda e, ps=ps: e.copy(out=Zb[0][:], in_=ps[:, 0:512].rearrange("p (h d) -> p h d", h=8)), [pk], ["Zb0"])
                pi = 0
                zi = 0
                for lvl in range(7):
                    yield
                    Pc, PTc = Pw[pi], Pw[pi + 1]
                    Pk, PTk = "Pw%d" % pi, "Pw%d" % (pi + 1)
                    Zc, Zn = Zb[zi], Zb[1 - zi]
                    ps, psb, pk = nps()
                    for h in range(8):
                        mm(ps[:, h * 64:(h + 1) * 64], identb[:], Zc[:, h, :], True, False, ["identb", "Zb%d" % zi], [pk])
                        mm(ps[:, h * 64:(h + 1) * 64], PTc[:, h, :], Zc[:, h, :], False, True, [PTk, "Zb%d" % zi], [pk])
                    if lvl < 6:
                        act(lambda e, ps=ps, Zn=Zn: e.copy(out=Zn[:], in_=ps[:, 0:512].rearrange("p (h d) -> p h d", h=8)), [pk], ["Zb%d" % (1 - zi)])
                        zi = 1 - zi
                        ni = 2 - pi
                        Pn, PTn = Pw[ni], Pw[ni + 1]
                        psA, _, pkA = nps()
                        for h in range(8):
                            mm(psA[:, h * 128:(h + 1) * 128], PTc[:, h, :], Pc[:, h, :], True, True, [PTk, Pk], [pkA])
                        for a_ in range(2):
                            dve(lambda e, psA=psA, Pn=Pn, a_=a_: e.tensor_copy(out=Pn[:, 4 * a_:4 * a_ + 4, :], in_=psA[:, a_ * 512:(a_ + 1) * 512].rearrange("p (h t) -> p h t", h=4)), [pkA], ["Pw%d" % ni])
                        psB, _, pkB = nps()
                        for h in range(8):
                            mm(psB[:, h * 128:(h + 1) * 128], Pc[:, h, :], PTc[:, h, :], True, True, [Pk, PTk], [pkB])
                        for a_ in range(2):
                            act(lambda e, psB=psB, PTn=PTn, a_=a_: e.copy(out=PTn[:, 4 * a_:4 * a_ + 4, :], in_=psB[:, a_ * 512:(a_ + 1) * 512].rearrange("p (h t) -> p h t", h=4)), [pkB], ["Pw%d" % (ni + 1)])
                        pi = ni
                    else:
                        act(lambda e, ps=ps: e.activation(out=Ub[:], in_=ps[:, 0:512].rearrange("p (h d) -> p h d", h=8), func=AF.Copy, scale=-1.0), [pk], ["Ub"])
                yield
                ps, psb, pk = nps()
                for h in range(8):
                    j, hp = h // 2, h % 2
                    sl = slice(hp * 64, hp * 64 + 64)
                    o_ = ps[:, h * 64:(h + 1) * 64]
                    mm(o_, Am[1][:, h, :], Ub[:, h, :], True, False, ["Am1", "Ub"], [pk])
                    mm(o_, Am[2][:, h, :], vrw[:, h, :], False, False, ["Am2", "vrw"], [pk])
                    mm(o_, FMt[sl, RB, j, :], Sbf[sl, j, :], False, True, ["FMt", "Sbf"], [pk])
                act(lambda e, ps=ps: e.copy(out=ysb[:], in_=ps[:, 0:512]), [pk], ["ysb"])
                yield
                ps, psb, pk = nps()
                for j in range(4):
                    o_ = ps[:, j * 64:(j + 1) * 64]
                    mm(o_, Btz[:, 2 * j, :], Ub[:, 2 * j, :], True, False, ["Btz", "Ub"], [pk])
                    mm(o_, Ktz[:, 2 * j, :], vrw[:, 2 * j, :], False, False, ["Ktz", "vrw"], [pk])
                    mm(o_, Btz[:, 2 * j + 1, :], Ub[:, 2 * j + 1, :], False, False, ["Btz", "Ub"], [pk])
                    mm(o_, Ktz[:, 2 * j + 1, :], vrw[:, 2 * j + 1, :], False, True, ["Ktz", "vrw"], [pk])
                dve(lambda e: e.tensor_tensor(out=Sst[:], in0=Sst[:], in1=WLfm[:].unsqueeze(2).to_broadcast([128, 4, 64]), op=ALU.mult), ["Sst", "WLfm"], ["Sst"])
                dve(lambda e, ps=ps: e.tensor_tensor(out=Sst[:], in0=Sst[:], in1=ps[:, 0:256].rearrange("p (j d) -> p j d", j=4), op=ALU.add), [pk, "Sst"], ["Sst"])
                act(lambda e: e.copy(out=Sbf[:], in_=Sst[:]), ["Sst"], ["Sbf"])
                yield
            def tail():
                head_rw(128, ysb, "ysb", bon, "bon", vf, "vf", g_sb, "g_sb", mix, "mix", W)
                out_proj(128, mix, "mix", None, None, b * 128, W)

            return front_stage, ml_stage, rw_stage, tail, p_

        def run_gens(gl):
            gl = list(gl)
            while gl:
                for item in list(gl):
                    CURP[0] = item[1]
                    CURPOOL[0] = item[3] if len(item) > 3 else "A"
                    for _ in range(item[2] if len(item) > 2 else 1):
                        try:
                            next(item[0])
                        except StopIteration:
                            gl.remove(item)
                            break

        blocks = [make_block(b) for b in range(NB)]
        run_gens([(blocks[0][0](), 0, 1, "A")])
        for b in range(NB):
            fr, ml_, rw_, tl, p_ = blocks[b]
            gl = [(rw_(), p_, 1, "A"), (ml_(), p_, 1, "A")]
            if b + 1 < NB:
                gl.append((blocks[b + 1][0](), (b + 1) % 2, 1, "A"))
            run_gens(gl)
            CURP[0] = p_
            CURPOOL[0] = "A"
            tl()
        CURP[0] = 0

        ps, psb, pk = nps()
        mm(ps[0:8, 0:128], runmax[:], IDF, True, True, ["runmax", "pA"], [pk])
        mm(ps[0:8, 128:256], nBc[:], IDF, True, True, ["nBc", "pA"], [pk])
        fs = TT("fs", [8, 16])
        dve(lambda e, ps=ps: e.tensor_reduce(out=fs[:, 0:1], in_=ps[0:8, 0:128], axis=AX.X, op=ALU.max), [pk], ["fs"])
        dve(lambda e: e.tensor_scalar_max(out=fs[:, 0:1], in0=fs[:, 0:1], scalar1=0.0), ["fs"], ["fs"])
        dve(lambda e, ps=ps: e.tensor_tensor(out=fs[:, 1:2], in0=fs[:, 0:1], in1=ps[0:8, 128:129], op=ALU.subtract), [pk, "fs"], ["fs"])
        dma("pool", om, fs[:, 1:2], ["fs"], [], "fin")
        act(lambda e: e.activation(out=fs[:, 2:3], in_=fs[:, 1:2], func=AF.Exp, scale=-1.0), ["fs"], ["fs"])
        dve(lambda e: e.tensor_scalar_mul(out=fs[:, 4:8], in0=pA[0:8, OFF["rsel"][0]:OFF["rsel"][1]], scalar1=fs[:, 2:3]), ["fs", "pA"], ["fs"])
        ps, psb, pk = nps()
        mm(ps[:, 0:4], pA[0:8, OFF["lsel"][0]:OFF["lsel"][1]], fs[:, 4:8], True, True, ["pA", "fs"], [pk])
        scb = TT("scb", [128, 4])
        act(lambda e, ps=ps: e.copy(out=scb[:], in_=ps[:, 0:4]), [pk], ["scb"])
        dve(lambda e: e.tensor_tensor(out=Cst[:], in0=Cst[:], in1=scb[:].unsqueeze(2).to_broadcast([128, 4, 65]), op=ALU.mult), ["Cst", "scb"], ["Cst"])
        dma("pool", oC, Cst[:], ["Cst"], [], "fin")
        dma("pool", oS, Sst[:], ["Sst"], [], "fin")
        PP[0].finalize()
        PP[0] = Prog(ctx)

    with contextlib.ExitStack() as es_s:
        cur[0] = es_s
        if do_sample:
            W = {}
            W["xm"] = TT("s_xm", [128, D])
            junk = W["xm"]
            st = TT("s_st", [128, 4])
            mix = TT("s_mix", [128, D], BF16)
            xsb = mix
            lor = TT("s_lor", [128, 256], BF16)
            lorT = TT("s_lorT", [128, 2, 128], BF16)
            W["mixT"] = TT("s_mixT", [128, 8, 128], BF16)
            sx = TT("sx", [NS, D])
            sxT = TT("sxT", [128, 8, NS], BF16)
            spj = TT("spj", [NS, INW])
            spk = TT("spk", [128, NSP])
            sl_t = TT("sl_t", [NS, 3, 256])
            Cs = TT("Cs", [128, 4096])
            Ss = Cs
            sn_t = TT("sn_t", [128, 64])
            sm_t = TT("sm_t", [128, 1])
            scv = TT("scv", [128, 2, 4, 64])
            ssh = TT("ssh", [128, 3, 64])
            dma("sp", sx[:], xs, [], ["sx"], "sin", True)
            dma("sp", spk[:], spk_d, [], ["spk"], "sin", True)
            dma("sp", sl_t[:, 0, :], sshl_d, [], ["sl_t"], "sin", True)
            dma("sp", sl_t[:, 1, :], mul_d, [], ["sl_t"], "sin", True)
            dma("sp", Cs[:], sC_d, [], ["Cs"], "sin", True)
            dma("sp", sn_t[:], sn_d, [], ["sn_t"], "sin", True)
            dma("sp", sm_t[:], sm_d, [], ["sm_t"], "sin", True)
            dma("sp", scv[:, :, 0:3, :], sconv_d, [], ["scv"], "sin", True)
            dma("sp", ssh[:], sshift_d, [], ["ssh"], "sin", True)
            rmsnorm_T(sx, "sx", NS, sxT, "sxT", 0, "nmw", xsb, "xsb", junk, "junk", st, "st")
            for n0 in range(0, INW, 512):
                nn = min(512, INW - n0)
                ps, psb, pk = nps()
                for c in range(8):
                    mm(ps[:NS, 0:nn], sxT[:, c, :], wq[:, c, n0:n0 + nn], c == 0, c == 7, ["sxT", "wq"], [pk])
                act(lambda e, ps=ps, n0=n0, nn=nn: e.copy(out=spj[:, n0:n0 + nn], in_=ps[:NS, 0:nn]), [pk], ["spj"])
            s1v = scr1.rearrange("(b h) a d -> b a h d", b=NS)
            for a_ in range(7):
                c0_ = a_ * 512 if a_ < 4 else MLW + (a_ - 4) * 512
                dma("pool", s1v[:, a_, :, :], spj[:, c0_:c0_ + 512].rearrange("p (h d) -> p h d", h=8), ["spj"], ["scr1"], "scrw1")
            A7 = TT("A7", [128, 8, 64])
            dma("sp", A7[:, 0:7, :], scr1[:, 0:7, :], ["scr1"], ["A7"], "scrr1")
            gif = TT("gif", [128, 2])
            s_if = nc.dram_tensor("scr_if", [2, 128], F32, kind="Internal").ap()
            for g_ in range(2):
                dma("pool", s_if[g_, :].rearrange("(b h) -> b h", b=NS), spj[:, 2048 + 8 * g_:2056 + 8 * g_], ["spj"], ["scr_if"], "scrwif")
            for g_ in range(2):
                dma("sp", gif[:, g_:g_ + 1], s_if[g_, :].rearrange("(p o) -> p o", o=1), ["scr_if"], ["gif"], "scrrif")
            SP_ = lambda n: spk[:, SOFF[n][0]:SOFF[n][1]]
            pl = spj[:, MLW + 1536:MLW + 1792]
            dma("pool", osshl, pl, ["spj"], [], "fin")
            dve(lambda e: e.tensor_tensor(out=sl_t[:, 2, :], in0=sl_t[:, 0, :], in1=pl, op=ALU.subtract), ["sl_t", "spj"], ["sl_t"])
            dve(lambda e: e.tensor_tensor(out=sl_t[:, 2, :], in0=sl_t[:, 2, :], in1=sl_t[:, 1, :], op=ALU.mult), ["sl_t"], ["sl_t"])
            dve(lambda e: e.tensor_tensor(out=sl_t[:, 2, :], in0=sl_t[:, 2, :], in1=pl, op=ALU.add), ["sl_t", "spj"], ["sl_t"])
            act(lambda e: e.activation(out=lor[:NS, 0:64], in_=sl_t[:, 2, 0:64], func=AF.Tanh), ["sl_t"], ["lor"])
            act(lambda e: e.copy(out=lor[:NS, 64:128], in_=sl_t[:, 2, 64:128]), ["sl_t"], ["lor"])
            act(lambda e: e.activation(out=lor[:NS, 128:256], in_=sl_t[:, 2, 128:256], func=AF.Sigmoid), ["sl_t"], ["lor"])
            ps, psb, pk = nps()
            pe(lambda e, psb=psb: e.transpose(psb[:, 0:NS], lor[:NS, 0:128], identb[:NS, :NS]), ["lor", "identb"], [pk])
            pe(lambda e, psb=psb: e.transpose(psb[:, 128:128 + NS], lor[:NS, 128:256], identb[:NS, :NS]), ["lor", "identb"], [pk])
            act(lambda e, psb=psb: e.copy(out=lorT[:, :, 0:NS], in_=psb[:, 0:256].rearrange("p (a t) -> p a t", a=2)[:, :, 0:NS]), [pk], ["lorT"])
            ps, psb, pk = nps()
            mm(ps[:NS, 0:512], lorT[0:64, 0, 0:NS], luw[0:64, :], True, True, ["lorT", "luw"], [pk])
            mm(ps[:NS, 512:1024], lorT[64:128, 0, 0:NS], luw[64:128, :], True, True, ["lorT", "luw"], [pk])
            ps2, _, pk2 = nps()
            mm(ps2[:NS, 0:512], lorT[:, 1, 0:NS], gup[:, :], True, True, ["lorT", "gup"], [pk2])
            lo3 = spj[:, 0:1536].rearrange("p (a n) -> p a n", a=3)
            for a_ in range(2):
                act(lambda e, ps=ps, a_=a_: e.copy(out=lo3[:, a_, :], in_=ps[:NS, a_ * 512:(a_ + 1) * 512]), [pk], ["lo3"])
            act(lambda e, ps2=ps2: e.copy(out=lo3[:, 2, :], in_=ps2[:NS, 0:512]), [pk2], ["lo3"])
            for a_ in range(3):
                dma("pool", scr2.rearrange("(b h) a d -> b a h d", b=NS)[:, a_, :, :], lo3[:, a_, :].rearrange("p (h d) -> p h d", h=8), ["lo3"], ["scr2"], "scrw2")
            L3 = TT("L3", [128, 3, 64])
            dma("sp", L3[:], scr2, ["scr2"], ["L3"], "scrr2")
            big = TT("big", [128, 4096])
            sv = TT("sv", [128, 64])
            pool(lambda e: e.tensor_copy(out=scv[:, :, 3, :], in_=A7[:, 0:2, :]), ["A7"], ["scv"])
            dma("pool", osconv, scv[:, :, 1:4, :], ["scv"], [], "fin")
            qk_s = TT("qk_s", [128, 2, 64])
            cwqk = lambda w_: spk[:, SOFF["cwq"][0] + w_ * 256:SOFF["cwq"][0] + (w_ + 1) * 256].rearrange("p (j d) -> p j d", j=4)
            for w_ in range(2):
                dve(lambda e, w_=w_: e.tensor_tensor(out=big[:, 0:256].rearrange("p (j d) -> p j d", j=4), in0=scv[:, w_, :, :], in1=cwqk(w_), op=ALU.mult), ["scv", "spk"], ["big"])
                dve(lambda e, w_=w_: e.tensor_reduce(out=qk_s[:, w_, :], in_=big[:, 0:256].rearrange("p (j d) -> p d j", j=4), axis=AX.X, op=ALU.add), ["big"], ["qk_s"])
            dve(lambda e: e.tensor_tensor(out=qk_s[:], in0=qk_s[:], in1=spk[:, SOFF["cbq"][0]:SOFF["cbk"][1]].rearrange("p (a d) -> p a d", a=2), op=ALU.add), ["qk_s", "spk"], ["qk_s"])
            act(lambda e: e.activation(out=qk_s[:], in_=qk_s[:], func=AF.Silu), ["qk_s"], ["qk_s"])
            act(lambda e: e.activation(out=qk_s[:, 1, :], in_=qk_s[:, 1, :], func=AF.Copy, scale=0.125), ["qk_s"], ["qk_s"])
            dve(lambda e: e.tensor_tensor(out=sv[:, 0:2], in0=gif[:], in1=spk[:, SOFF["ib"][0]:SOFF["fb"][1]], op=ALU.add), ["gif", "spk"], ["sv"])
            act(lambda e: e.activation(out=sv[:, 9:10], in_=sv[:, 1:2], func=AF.Exp, scale=-1.0), ["sv"], ["sv"])
            act(lambda e: e.activation(out=sv[:, 2:3], in_=sv[:, 9:10], func=AF.Ln, bias=1.0, scale=1.0), ["sv"], ["sv"])
            dve(lambda e: e.tensor_tensor(out=sv[:, 3:4], in0=sm_t[:], in1=sv[:, 2:3], op=ALU.subtract), ["sv", "sm_t"], ["sv"])
            dve(lambda e: e.tensor_tensor(out=sv[:, 4:5], in0=sv[:, 3:4], in1=sv[:, 0:1], op=ALU.max), ["sv"], ["sv"])
            dma("pool", osm, sv[:, 4:5], ["sv"], [], "fin")
            dve(lambda e: e.tensor_tensor(out=sv[:, 9:10], in0=sv[:, 0:1], in1=sv[:, 4:5], op=ALU.subtract), ["sv"], ["sv"])
            act(lambda e: e.activation(out=sv[:, 5:6], in_=sv[:, 9:10], func=AF.Exp), ["sv"], ["sv"])
            dve(lambda e: e.tensor_tensor(out=sv[:, 9:10], in0=sv[:, 3:4], in1=sv[:, 4:5], op=ALU.subtract), ["sv"], ["sv"])
            act(lambda e: e.activation(out=sv[:, 6:7], in_=sv[:, 9:10], func=AF.Exp), ["sv"], ["sv"])
            act(lambda e: e.activation(out=sv[:, 7:8], in_=sv[:, 4:5], func=AF.Exp, scale=-1.0), ["sv"], ["sv"])
            q_ = qk_s[:, 0, :]
            k_ = qk_s[:, 1, :]
            v_ = A7[:, 2, :]
            b3 = lambda t: t[:, :].rearrange("p (a c) -> p a c", a=64)
            pool(lambda e: e.tensor_tensor(out=b3(big), in0=k_.unsqueeze(2).to_broadcast([128, 64, 64]), in1=v_.unsqueeze(1).to_broadcast([128, 64, 64]), op=ALU.mult), ["qk_s", "A7"], ["big"])
            dve(lambda e: e.tensor_scalar_mul(out=Cs[:], in0=Cs[:], scalar1=sv[:, 6:7]), ["Cs", "sv"], ["Cs"])
            dve(lambda e: e.scalar_tensor_tensor(out=Cs[:], in0=big[:], scalar=sv[:, 5:6], in1=Cs[:], op0=ALU.mult, op1=ALU.add), ["big", "sv", "Cs"], ["Cs"])
            dma("pool", osC, Cs[:], ["Cs"], [], "fin")
            dve(lambda e: e.tensor_scalar_mul(out=sn_t[:], in0=sn_t[:], scalar1=sv[:, 6:7]), ["sn_t", "sv"], ["sn_t"])
            dve(lambda e: e.scalar_tensor_tensor(out=sn_t[:], in0=k_, scalar=sv[:, 5:6], in1=sn_t[:], op0=ALU.mult, op1=ALU.add), ["qk_s", "sv", "sn_t"], ["sn_t"])
            dma("pool", osn, sn_t[:], ["sn_t"], [], "fin")
            pool(lambda e: e.tensor_tensor(out=b3(big), in0=Cs[:, :].rearrange("p (k v) -> p v k", k=64), in1=q_.unsqueeze(1).to_broadcast([128, 64, 64]), op=ALU.mult), ["Cs", "qk_s"], ["big"])
            hs = TT("hs", [128, 2, 64])
            dve(lambda e: e.tensor_reduce(out=hs[:, 0, :], in_=b3(big), axis=AX.X, op=ALU.add), ["big"], ["hs"])
            dve(lambda e: e.tensor_tensor(out=sv[:, 16:80 - 16] if False else big[:, 0:64], in0=q_, in1=sn_t[:], op=ALU.mult), ["qk_s", "sn_t"], ["big"])
            dve(lambda e: e.tensor_reduce(out=sv[:, 8:9], in_=big[:, 0:64], axis=AX.X, op=ALU.add), ["big"], ["sv"])
            dve(lambda e: e.scalar_tensor_tensor(out=sv[:, 9:10], in0=sv[:, 8:9], scalar=-1.0, in1=sv[:, 8:9], op0=ALU.mult, op1=ALU.max), ["sv"], ["sv"])
            dve(lambda e: e.tensor_tensor(out=sv[:, 9:10], in0=sv[:, 9:10], in1=sv[:, 7:8], op=ALU.max), ["sv"], ["sv"])
            dve(lambda e: e.reciprocal(out=sv[:, 10:11], in_=sv[:, 9:10]), ["sv"], ["sv"])
            dve(lambda e: e.tensor_scalar_mul(out=hs[:, 0, :], in0=hs[:, 0, :], scalar1=sv[:, 10:11]), ["hs", "sv"], ["hs"])
            dma("pool", osshift, A7[:, 4:7, :], ["A7"], [], "fin")
            rk3 = TT("rk3", [128, 3, 64])
            mu3 = spk[:, SOFF["mu_r"][0]:SOFF["mu_v"][1]].rearrange("p (a d) -> p a d", a=3)
            dve(lambda e: e.tensor_tensor(out=rk3[:], in0=ssh[:], in1=A7[:, 4:7, :], op=ALU.subtract), ["ssh", "A7"], ["rk3"])
            dve(lambda e: e.tensor_tensor(out=rk3[:], in0=rk3[:], in1=mu3, op=ALU.mult), ["rk3", "spk"], ["rk3"])
            dve(lambda e: e.tensor_tensor(out=rk3[:], in0=rk3[:], in1=A7[:, 4:7, :], op=ALU.add), ["rk3", "A7"], ["rk3"])
            w8 = TT("w8", [128, 8, 64])
            dve(lambda e: e.tensor_tensor(out=w8[:, 0, :], in0=L3[:, 0, :], in1=SP_("w0"), op=ALU.add), ["L3", "spk"], ["w8"])
            act(lambda e: e.activation(out=w8[:, 0, :], in_=w8[:, 0, :], func=AF.Sigmoid), ["w8"], ["w8"])
            act(lambda e: e.activation(out=w8[:, 0, :], in_=w8[:, 0, :], func=AF.Exp, scale=-C0), ["w8"], ["w8"])
            dve(lambda e: e.tensor_tensor(out=w8[:, 1, :], in0=L3[:, 1, :], in1=SP_("a0"), op=ALU.add), ["L3", "spk"], ["w8"])
            act(lambda e: e.activation(out=w8[:, 1, :], in_=w8[:, 1, :], func=AF.Sigmoid), ["w8"], ["w8"])
            dve(lambda e: e.tensor_tensor(out=w8[:, 6, :], in0=rk3[:, 1, :], in1=SP_("kk"), op=ALU.mult), ["rk3", "spk"], ["w8"])
            dve(lambda e: e.tensor_tensor(out=w8[:, 7, :], in0=w8[:, 6, :], in1=w8[:, 6, :], op=ALU.mult), ["w8"], ["w8"])
            dve(lambda e: e.tensor_reduce(out=sv[:, 11:12], in_=w8[:, 7, :], axis=AX.X, op=ALU.add), ["w8"], ["sv"])
            dve(lambda e: e.tensor_scalar_max(out=sv[:, 11:12], in0=sv[:, 11:12], scalar1=1e-24), ["sv"], ["sv"])
            act(lambda e: e.activation(out=sv[:, 11:12], in_=sv[:, 11:12], func=AF.Sqrt), ["sv"], ["sv"])
            dve(lambda e: e.reciprocal(out=sv[:, 11:12], in_=sv[:, 11:12]), ["sv"], ["sv"])
            dve(lambda e: e.tensor_scalar_mul(out=w8[:, 3, :], in0=w8[:, 6, :], scalar1=sv[:, 11:12]), ["w8", "sv"], ["w8"])
            dve(lambda e: e.tensor_tensor(out=w8[:, 6, :], in0=w8[:, 1, :], in1=SP_("ka"), op=ALU.mult), ["w8", "spk"], ["w8"])
            dve(lambda e: e.tensor_tensor(out=w8[:, 6, :], in0=w8[:, 6, :], in1=SP_("ka"), op=ALU.subtract), ["w8", "spk"], ["w8"])
            dve(lambda e: e.tensor_scalar_add(out=w8[:, 6, :], in0=w8[:, 6, :], scalar1=1.0), ["w8"], ["w8"])
            dve(lambda e: e.tensor_tensor(out=w8[:, 4, :], in0=rk3[:, 1, :], in1=w8[:, 6, :], op=ALU.mult), ["rk3", "w8"], ["w8"])
            dve(lambda e: e.tensor_tensor(out=w8[:, 5, :], in0=w8[:, 1, :], in1=w8[:, 3, :], op=ALU.mult), ["w8"], ["w8"])
            dma("sp", Ss[:], sS_d, [], ["Ss"], "sin2")
            bk = lambda ap: ap.unsqueeze(1).to_broadcast([128, 64, 64])
            bv = lambda ap: ap.unsqueeze(2).to_broadcast([128, 64, 64])
            pool(lambda e: e.tensor_tensor(out=b3(big), in0=b3(Ss), in1=bk(w8[:, 3, :]), op=ALU.mult), ["Ss", "w8"], ["big"])
            dve(lambda e: e.tensor_reduce(out=w8[:, 7, :], in_=b3(big), axis=AX.X, op=ALU.add), ["big"], ["w8"])
            dve(lambda e: e.tensor_tensor(out=b3(Ss), in0=b3(Ss), in1=bk(w8[:, 0, :]), op=ALU.mult), ["Ss", "w8"], ["Ss"])
            pool(lambda e: e.tensor_tensor(out=b3(big), in0=bv(w8[:, 7, :]), in1=bk(w8[:, 5, :]), op=ALU.mult), ["w8"], ["big"])
            dve(lambda e: e.tensor_tensor(out=Ss[:], in0=Ss[:], in1=big[:], op=ALU.subtract), ["Ss", "big"], ["Ss"])
            pool(lambda e: e.tensor_tensor(out=b3(big), in0=bv(rk3[:, 2, :]), in1=bk(w8[:, 4, :]), op=ALU.mult), ["rk3", "w8"], ["big"])
            dve(lambda e: e.tensor_tensor(out=Ss[:], in0=Ss[:], in1=big[:], op=ALU.add), ["Ss", "big"], ["Ss"])
            dma("pool", osS, Ss[:], ["Ss"], [], "fin")
            pool(lambda e: e.tensor_tensor(out=b3(big), in0=b3(Ss), in1=bk(rk3[:, 0, :]), op=ALU.mult), ["Ss", "rk3"], ["big"])
            dve(lambda e: e.tensor_reduce(out=hs[:, 1, :], in_=b3(big), axis=AX.X, op=ALU.add), ["big"], ["hs"])
            dve(lambda e: e.tensor_tensor(out=w8[:, 6, :], in0=rk3[:, 0, :], in1=w8[:, 4, :], op=ALU.mult), ["rk3", "w8"], ["w8"])
            dve(lambda e: e.tensor_tensor(out=w8[:, 6, :], in0=w8[:, 6, :], in1=SP_("rk"), op=ALU.mult), ["w8", "spk"], ["w8"])
            dve(lambda e: e.tensor_reduce(out=sv[:, 12:13], in_=w8[:, 6, :], axis=AX.X, op=ALU.add), ["w8"], ["sv"])
            dve(lambda e: e.scalar_tensor_tensor(out=hs[:, 1, :], in0=rk3[:, 2, :], scalar=sv[:, 12:13], in1=hs[:, 1, :], op0=ALU.mult, op1=ALU.add), ["rk3", "sv", "hs"], ["hs"])
            act(lambda e: e.activation(out=w8[:, 6, :], in_=A7[:, 3, :], func=AF.Sigmoid), ["A7"], ["w8"])
            dve(lambda e: e.tensor_tensor(out=hs[:, 0, :], in0=hs[:, 0, :], in1=w8[:, 6, :], op=ALU.mult), ["hs", "w8"], ["hs"])
            dve(lambda e: e.tensor_tensor(out=w8[:, 7, :], in0=hs[:, 0, :], in1=hs[:, 0, :], op=ALU.mult), ["hs"], ["w8"])
            dve(lambda e: e.tensor_reduce(out=sv[:, 13:14], in_=w8[:, 7, :], axis=AX.X, op=ALU.add), ["w8"], ["sv"])
            act(lambda e: e.activation(out=sv[:, 13:14], in_=sv[:, 13:14], func=AF.Sqrt, bias=EPS, scale=1.0 / 64), ["sv"], ["sv"])
            dve(lambda e: e.reciprocal(out=sv[:, 13:14], in_=sv[:, 13:14]), ["sv"], ["sv"])
            dve(lambda e: e.scalar_tensor_tensor(out=hs[:, 0, :], in0=hs[:, 0, :], scalar=sv[:, 13:14], in1=SP_("mnw"), op0=ALU.mult, op1=ALU.mult), ["hs", "sv", "spk"], ["hs"])
            dve(lambda e: e.tensor_reduce(out=sv[:, 14:15], in_=hs[:, 1, :], axis=AX.X, op=ALU.add), ["hs"], ["sv"])
            dve(lambda e: e.tensor_scalar_mul(out=sv[:, 14:15], in0=sv[:, 14:15], scalar1=1.0 / 64), ["sv"], ["sv"])
            dve(lambda e: e.tensor_scalar_sub(out=hs[:, 1, :], in0=hs[:, 1, :], scalar1=sv[:, 14:15]), ["hs", "sv"], ["hs"])
            dve(lambda e: e.tensor_tensor(out=w8[:, 7, :], in0=hs[:, 1, :], in1=hs[:, 1, :], op=ALU.mult), ["hs"], ["w8"])
            dve(lambda e: e.tensor_reduce(out=sv[:, 15:16], in_=w8[:, 7, :], axis=AX.X, op=ALU.add), ["w8"], ["sv"])
            act(lambda e: e.activation(out=sv[:, 15:16], in_=sv[:, 15:16], func=AF.Sqrt, bias=GN_EPS, scale=1.0 / 64), ["sv"], ["sv"])
            dve(lambda e: e.reciprocal(out=sv[:, 15:16], in_=sv[:, 15:16]), ["sv"], ["sv"])
            dve(lambda e: e.scalar_tensor_tensor(out=hs[:, 1, :], in0=hs[:, 1, :], scalar=sv[:, 15:16], in1=SP_("lnw"), op0=ALU.mult, op1=ALU.mult), ["hs", "sv", "spk"], ["hs"])
            dve(lambda e: e.tensor_tensor(out=hs[:, 1, :], in0=hs[:, 1, :], in1=SP_("lnb"), op=ALU.add), ["hs", "spk"], ["hs"])
            dve(lambda e: e.tensor_tensor(out=hs[:, 1, :], in0=hs[:, 1, :], in1=L3[:, 2, :], op=ALU.mult), ["hs", "L3"], ["hs"])
            s3v = nc.dram_tensor("scr3b", [128, 2, 64], F32, kind="Internal").ap()
            dma("pool", s3v, hs[:], ["hs"], ["scr3b"], "scrw3")
            smix = spj[:, 2304:3328].rearrange("p (a h d) -> p a h d", a=2, h=8)
            for a_ in range(2):
                dma("sp", smix[:, a_, :, :], s3v.rearrange("(b h) a d -> b a h d", b=NS)[:, a_, :, :], ["scr3b"], ["smix"], "scrr3")
            act(lambda e: e.copy(out=mix[:NS, :], in_=smix[:].rearrange("p a h d -> p (a h d)")), ["smix"], ["mix"])
            out_proj(NS, mix, "mix", sx, "sx", T, W)
        PP[0].finalize()
        PP[0] = Prog(ctx)
    es_res.close()

    with contextlib.ExitStack() as es2:
        cur[0] = es2
        upb = TT("upb", [128, 8, DFF], BF16)
        dnb = TT("dnb", [128, 32, D], BF16)
        pB2 = TT("pB2", [128, 136], F32)
        nfw = TT("nfw", [128, D])
        identb2 = TT("identb2", [128, 128], BF16)
        wout = TT("wout", [128, 8, D], BF16)
        mixin = TT("mixin", [128, D], BF16)
        for c in range(0, 8, 4):
            dma("pool", wout[:, c:c + 4, :], w_out_v[:, c:c + 4, :], [], ["wout"], "wout")
        up_v = mlp_up.rearrange("(c p) n -> p c n", p=128)
        dn_v = mlp_down.rearrange("(c p) n -> p c n", p=128)
        dma("sp", pB2[:, 0:128], packA_d[:, OFF["ident"][0]:OFF["ident"][1]], [], ["pB2"], "init2", True)
        dma("sp", pB2[:, 128:136], packA_d[:, OFF["nmlp"][0]:OFF["nmlp"][1]], [], ["pB2"], "init2", True)
        dma("sp", nfw[:], nfw_d, [], ["nfw"], "init2", True)
        for g8 in range(8):
            dma("pool", upb[:, :, g8 * 512:(g8 + 1) * 512], up_v[:, :, g8 * 512:(g8 + 1) * 512], [], ["upb%d" % g8], "up%d" % g8)
        for g8 in range(8):
            dma("pool", dnb[:, g8 * 4:(g8 + 1) * 4, :], dn_v[:, g8 * 4:(g8 + 1) * 4, :], [], ["dnb%d" % g8], "dn%d" % g8)
        dve(lambda e: e.tensor_copy(out=identb2[:], in_=pB2[:, 0:128]), ["pB2"], ["identb2"])
        NSUB = 2
        NTT = NSUB * 128
        xsb2 = TT("xsb2", [128, D], BF16)
        xb = [TT("xb%d" % i, [128, NSUB, D]) for i in range(2)]
        st2 = TT("st2", [128, 8])
        xn2 = [TT("xn2T%d" % i, [128, 8, NTT], BF16) for i in range(2)]
        hT = TT("hT", [128, 32, NTT], BF16)
        junk2 = TT("junk2", [128, D], BF16)
        rl = [TT("rl%d" % i, [128, 512]) for i in range(2)]
        nmlp = pB2[:, 128:136]
        xm_v = xp.rearrange("(s p) d -> p s d", p=128)
        yp_v = yp.rearrange("(s p) d -> p s d", p=128)
        sbs = [(sb * NSUB, NSUB, 128) for sb in range(NB // NSUB)] + [(NB, 1, NS)]

        def front(i):
            s0, nsub, nt = sbs[i]
            x4 = xb[i % 2]
            xk = "xb%d" % (i % 2)
            xn2T = xn2[i % 2]
            xnk = "xn2T%d" % (i % 2)
            if nsub == NSUB:
                dma("sp", x4[:], xm_v[:, s0:s0 + NSUB, :], [], [xk], xk)
            else:
                dma("sp", x4[:nt, 0, :], xs, [], [xk], xk)
            for si in range(nsub):
                r0_ = (s0 + si) * 128 if nsub == NSUB else T
                dma("sp", mixin[:nt, :], mix_d[r0_:r0_ + nt, :], [], ["mixin"], "mixin")
                ps, psb, pk = nps()
                for c in range(8):
                    pe(lambda e, c=c, psb=psb: e.transpose(psb[:, c * 128:c * 128 + nt], mixin[:nt, c * 128:(c + 1) * 128], identb2[:nt, :nt]), ["mixin", "identb2"], [pk])
                act(lambda e, psb=psb, si=si: e.copy(out=xn2T[:, :, si * 128:si * 128 + nt], in_=psb[:, 0:1024].rearrange("p (c t) -> p c t", c=8)[:, :, 0:nt]), [pk], [xnk])
                yield
                ps, psb, pk = nps()
                for n in range(2):
                    for c in range(8):
                        mm(ps[:nt, n * 512:(n + 1) * 512], xn2T[:, c, si * 128:si * 128 + nt], wout[:, c, n * 512:(n + 1) * 512], c == 0, c == 7, [xnk, "wout"], [pk])
                for a_ in range(2):
                    dve(lambda e, ps=ps, si=si, a_=a_: e.tensor_tensor(out=x4[:nt, si, a_ * 512:(a_ + 1) * 512], in0=ps[:nt, a_ * 512:(a_ + 1) * 512], in1=x4[:nt, si, a_ * 512:(a_ + 1) * 512], op=ALU.add), [pk, xk], [xk])
                act(lambda e, si=si: e.activation(out=junk2[:nt, :], in_=x4[:nt, si, :], func=AF.Square, accum_out=st2[:nt, 0:1]), [xk], ["junk2", "st2"])
                act(lambda e: e.activation(out=st2[:nt, 1:2], in_=st2[:nt, 0:1], func=AF.Sqrt, bias=EPS, scale=1.0 / D), ["st2"], ["st2"])
                dve(lambda e: e.reciprocal(out=st2[:nt, 2:3], in_=st2[:nt, 1:2]), ["st2"], ["st2"])
                dve(lambda e, si=si: e.tensor_scalar_mul(out=xsb2[:nt, :], in0=x4[:nt, si, :], scalar1=st2[:nt, 2:3]), [xk, "st2"], ["xsb2"])
                yield
                yield
                ps, psb, pk = nps()
                for c in range(8):
                    pe(lambda e, c=c, psb=psb: e.transpose(psb[:, c * 128:c * 128 + nt], xsb2[:nt, c * 128:(c + 1) * 128], identb2[:nt, :nt]), ["xsb2", "identb2"], [pk])
                dve(lambda e, psb=psb, si=si: e.tensor_tensor(out=xn2T[:, :, si * 128:si * 128 + nt], in0=psb[:, 0:1024].rearrange("p (c t) -> p c t", c=8)[:, :, 0:nt],
                                                             in1=nmlp.unsqueeze(2).to_broadcast([128, 8, nt]), op=ALU.mult), [pk, "pB2"], [xnk])
                yield

        def up(i):
            s0, nsub, nt = sbs[i]
            ntt = nsub * nt if nsub == NSUB else nt
            xn2T = xn2[i % 2]
            xnk = "xn2T%d" % (i % 2)
            per = 512 // NTT
            for j2 in range(32 // (2 * per)):
                ps, psb, pk = nps()
                for jj in range(2 * per):
                    j = j2 * 2 * per + jj
                    for c in range(8):
                        mm(ps[:, jj * NTT:jj * NTT + ntt], upb[:, c, j * 128:(j + 1) * 128], xn2T[:, c, 0:ntt], c == 0, c == 7, ["upb%d" % (j // 4), xnk], [pk])
                for bk in range(2):
                    r_ = rl[bk]
                    rk_ = "rl%d" % bk
                    j0 = j2 * 2 * per + bk * per
                    psv = ps[:, bk * 512:(bk + 1) * 512].rearrange("p (j t) -> p j t", j=per)[:, :, 0:ntt]
                    rv = r_[:, :].rearrange("p (j t) -> p j t", j=per)[:, :, 0:ntt]
                    act(lambda e, psv=psv, rv=rv: e.activation(out=rv, in_=psv, func=AF.Relu), [pk], [rk_])
                    if bk == 0:
                        dve(lambda e, rv=rv, j0=j0: e.tensor_tensor(out=hT[:, j0:j0 + per, 0:ntt], in0=rv, in1=rv, op=ALU.mult), [rk_], ["hT"])
                    else:
                        pool(lambda e, rv=rv, j0=j0: e.tensor_tensor(out=hT[:, j0:j0 + per, 0:ntt], in0=rv, in1=rv, op=ALU.mult), [rk_], ["hT"])
                yield

        def down(i):
            s0, nsub, nt = sbs[i]
            x4 = xb[i % 2]
            xk = "xb%d" % (i % 2)
            for si in range(nsub):
                ps, psb, pk = nps()
                for n in range(2):
                    for j in range(32):
                        mm(ps[:nt, n * 512:(n + 1) * 512], hT[:, j, si * 128:si * 128 + nt], dnb[:, j, n * 512:(n + 1) * 512], j == 0, j == 31, ["hT", "dnb%d" % (j // 4)], [pk])
                for a_ in range(2):
                    dve(lambda e, ps=ps, si=si, a_=a_: e.tensor_tensor(out=x4[:nt, si, a_ * 512:(a_ + 1) * 512], in0=ps[:nt, a_ * 512:(a_ + 1) * 512], in1=x4[:nt, si, a_ * 512:(a_ + 1) * 512], op=ALU.add), [pk, xk], [xk])
                act(lambda e, si=si: e.activation(out=junk2[:nt, :], in_=x4[:nt, si, :], func=AF.Square, accum_out=st2[:nt, 4:5]), [xk], ["junk2", "st2"])
                act(lambda e: e.activation(out=st2[:nt, 5:6], in_=st2[:nt, 4:5], func=AF.Sqrt, bias=EPS, scale=1.0 / D), ["st2"], ["st2"])
                dve(lambda e: e.reciprocal(out=st2[:nt, 6:7], in_=st2[:nt, 5:6]), ["st2"], ["st2"])
                dve(lambda e, si=si: e.scalar_tensor_tensor(out=x4[:nt, si, :], in0=x4[:nt, si, :], scalar=st2[:nt, 6:7], in1=nfw[:nt, :], op0=ALU.mult, op1=ALU.mult), [xk, "st2", "nfw"], [xk])
            if nsub == NSUB:
                dma("pool", yp_v[:, s0:s0 + NSUB, :], x4[:], [xk], [], "yo%d" % (i % 2))
            else:
                dma("pool", ys, x4[:nt, 0, :], [xk], [], "yo%d" % (i % 2))

        def run2(gl):
            gl = list(gl)
            while gl:
                for g_ in list(gl):
                    try:
                        next(g_)
                    except StopIteration:
                        gl.remove(g_)

        run2([front(0)])
        for i in range(len(sbs)):
            gl = [up(i)]
            if i + 1 < len(sbs):
                gl.append(front(i + 1))
            run2(gl)
            down(i)
        PP[0].finalize()
    es_ps.close()
    ctx.close()
    return nc


_CACHE = {}


def _host_packs(inp, core):
    f = np.float32
    L = 0
    pa = np.zeros((128, NA), f)

    def put(n, arr):
        a, b = OFF[n]
        pa[:, a:b] = arr

    rep = lambda v: np.broadcast_to(np.asarray(v, f).reshape(1, -1), (128, np.asarray(v).size))
    put("mnw", rep(inp["mlstm_norm_w"][L]))
    put("w0", rep(inp["rw_w0"][L]))
    put("a0", rep(inp["rw_a0"][L]))
    put("kk", rep(inp["rw_k_k"][L]))
    put("ka", rep(inp["rw_k_a"][L]))
    put("rk", rep(inp["rw_r_k"][L].reshape(-1)))
    put("lnw", rep(inp["rw_ln_w"][L]))
    put("lnb", rep(inp["rw_ln_b"][L]))
    put("ifb", rep(np.concatenate([inp["mlstm_i_b"][L], inp["mlstm_f_b"][L]])))
    put("nmw", inp["norm_mix_w"][L].reshape(8, 128).T)
    put("nmlp", inp["norm_mlp_w"][L].reshape(8, 128).T)
    cw = inp["mlstm_conv_w"][L]
    put("cw", cw.reshape(4, 8, 128).transpose(2, 1, 0).reshape(128, 32))
    put("cb", inp["mlstm_conv_b"][L].reshape(8, 128).T)
    put("ident", np.eye(128, dtype=f))
    put("mui", np.triu(np.ones((128, 128), f), 0))
    put("mus", np.triu(np.ones((128, 128), f), 1))
    put("mls", np.tril(np.ones((128, 128), f), -1))
    put("ones", np.ones((128, 128), f))
    lsel = np.zeros((128, 128), f)
    rsel = np.zeros((128, 4), f)
    for h in range(8):
        lsel[h, (h % 2) * 64:(h % 2) * 64 + 64] = 1.0
        rsel[h, h // 2] = 1.0
    put("lsel", lsel)
    put("rsel", rsel)
    return pa


def _sample_pack(inp):
    f = np.float32
    L = 0
    sp = np.zeros((128, NSP), f)

    def bh(v512):
        return np.tile(np.asarray(v512, f).reshape(8, 64), (NS, 1))

    def put(n, arr):
        a, b = SOFF[n]
        sp[:, a:b] = arr

    mu = inp["rw_mu"][L]
    put("mu_r", bh(mu[0:512]))
    put("mu_k", bh(mu[512:1024]))
    put("mu_v", bh(mu[1024:1536]))
    cw = inp["mlstm_conv_w"][L]
    put("cwq", np.concatenate([bh(cw[j, 0:512]) for j in range(4)], axis=1))
    put("cwk", np.concatenate([bh(cw[j, 512:1024]) for j in range(4)], axis=1))
    cb = inp["mlstm_conv_b"][L]
    put("cbq", bh(cb[0:512]))
    put("cbk", bh(cb[512:1024]))
    put("mnw", bh(inp["mlstm_norm_w"][L]))
    put("w0", bh(inp["rw_w0"][L]))
    put("a0", bh(inp["rw_a0"][L]))
    put("kk", bh(inp["rw_k_k"][L]))
    put("ka", bh(inp["rw_k_a"][L]))
    put("rk", bh(inp["rw_r_k"][L].reshape(-1)))
    put("lnw", bh(inp["rw_ln_w"][L]))
    put("lnb", bh(inp["rw_ln_b"][L]))
    put("ib", np.tile(inp["mlstm_i_b"][L].reshape(8, 1), (NS, 1)))
    put("fb", np.tile(inp["mlstm_f_b"][L].reshape(8, 1), (NS, 1)))
    return sp


def kernel(**inp):
    f = np.float32
    inp = {k: np.asarray(v) for k, v in inp.items()}
    if "nc" not in _CACHE:
        _CACHE["nc"] = build_program()
    nc = _CACHE["nc"]
    L = 0
    pa = _host_packs(inp, 0)
    sp = _sample_pack(inp)
    mu = inp["rw_mu"][L]
    luw = np.concatenate([inp["rw_w_up"][L], inp["rw_a_up"][L]], axis=0).astype(f)
    common = {
        "w_in": np.ascontiguousarray(inp["w_in"][L], f),
        "w_out": np.ascontiguousarray(inp["w_out"][L], f),
        "mlp_up": np.ascontiguousarray(inp["mlp_up"][L], f),
        "mlp_down": np.ascontiguousarray(inp["mlp_down"][L], f),
        "packA": pa,
        "mu_b": np.ascontiguousarray(np.broadcast_to(mu.reshape(1, -1), (128, RWW)), f),
        "nfw_b": np.ascontiguousarray(np.broadcast_to(inp["norm_f_w"].reshape(1, -1), (128, D)), f),
        "luw": luw,
        "gup": np.ascontiguousarray(inp["rw_g_up"][L], f),
        "spack": sp,
        "mul": np.ascontiguousarray(np.broadcast_to(mu[1536:1792].reshape(1, -1), (NS, 256)), f),
    }
    in_maps = []
    for c in range(8):
        rs = slice(c * NS, (c + 1) * NS)
        m = dict(common)
        m["xp"] = np.ascontiguousarray(inp["x_prompt"][c], f)
        m["xs"] = np.ascontiguousarray(inp["x_sample"][rs, 0, :], f)
        m["sC"] = np.ascontiguousarray(inp["state_mlstm_C"][L, rs].reshape(128, 4096), f)
        m["sn"] = np.ascontiguousarray(inp["state_mlstm_n"][L, rs].reshape(128, 64), f)
        m["sm"] = np.ascontiguousarray(inp["state_mlstm_m"][L, rs].reshape(128, 1), f)
        cv = inp["state_mlstm_conv"][L, rs]
        m["sconv"] = np.ascontiguousarray(cv.reshape(NS, 3, 2, 8, 64).transpose(0, 3, 2, 1, 4).reshape(128, 2, 3, 64), f)
        m["sS"] = np.ascontiguousarray(inp["state_rwkv_S"][L, rs].reshape(128, 4096), f)
        sh = inp["state_rwkv_shift"][L, rs, 0, :]
        m["sshift"] = np.ascontiguousarray(sh[:, 0:1536].reshape(NS, 3, 8, 64).transpose(0, 2, 1, 3).reshape(128, 3, 64), f)
        m["sshl"] = np.ascontiguousarray(sh[:, 1536:1792], f)
        in_maps.append(m)
    res = run_bass_kernel_spmd(nc, in_maps, core_ids=list(range(8)))
    R = res.results
    y_prompt = np.stack([R[c]["yp"] for c in range(8)]).astype(f)
    y_sample = np.concatenate([R[c]["ys"] for c in range(8)], axis=0).reshape(128, 1, D).astype(f)
    pC = np.zeros((1, 8, 8, 64, 64), f)
    pn = np.zeros((1, 8, 8, 64), f)
    pm = np.zeros((1, 8, 8), f)
    pconv = np.zeros((1, 8, 3, 1024), f)
    pS = np.zeros((1, 8, 8, 64, 64), f)
    pshift = np.zeros((1, 8, 1, RWW), f)
    for c in range(8):
        oC = R[c]["oC"].reshape(2, 64, 4, 65)
        Ch = oC.transpose(2, 0, 1, 3).reshape(8, 64, 65)
        pC[0, c] = Ch[:, :, 0:64]
        pn[0, c] = Ch[:, :, 64]
        pm[0, c] = R[c]["om"].reshape(8)
        pconv[0, c] = R[c]["oconv"].transpose(2, 1, 0).reshape(3, 1024)
        oS = R[c]["oS"].reshape(2, 64, 4, 64)
        pS[0, c] = oS.transpose(2, 0, 3, 1).reshape(8, 64, 64)
        pshift[0, c, 0] = R[c]["oshift"].reshape(RWW)
    sC = np.concatenate([R[c]["osC"].reshape(NS, 8, 64, 64) for c in range(8)])[None].astype(f)
    sn = np.concatenate([R[c]["osn"].reshape(NS, 8, 64) for c in range(8)])[None].astype(f)
    sm = np.concatenate([R[c]["osm"].reshape(NS, 8) for c in range(8)])[None].astype(f)
    sconv = np.concatenate([R[c]["osconv"].reshape(NS, 8, 2, 3, 64).transpose(0, 3, 2, 1, 4).reshape(NS, 3, 1024) for c in range(8)])[None].astype(f)
    sS = np.concatenate([R[c]["osS"].reshape(NS, 8, 64, 64) for c in range(8)])[None].astype(f)
    sshift = np.concatenate([
        np.concatenate([R[c]["osshift"].reshape(NS, 8, 3, 64).transpose(0, 2, 1, 3).reshape(NS, 1536), R[c]["osshl"]], axis=1)
        for c in range(8)]).reshape(1, 128, 1, RWW).astype(f)
    return (y_prompt, y_sample, pC, pn, pm, pconv, pS, pshift, sC, sn, sm, sconv, sS, sshift)
```

```python
import contextlib
import numpy as np
import concourse.bass as bass
import concourse.mybir as mybir
from concourse.bass_utils import run_bass_kernel_spmd

F32 = mybir.dt.float32
BF16 = mybir.dt.bfloat16
AF = mybir.ActivationFunctionType
ALU = mybir.AluOpType
AX = mybir.AxisListType

D = 1024
T = 2048
NB = 16
NS = 16
INW = 3856
MLW = 2064
RWW = 1792
DFF = 4096
EPS = 1e-6
GN_EPS = 64e-5
C0 = 0.6065306597126334

OFF = {}
_o = 0
for _n, _w in [("mnw", 512), ("w0", 512), ("a0", 512), ("kk", 512), ("ka", 512), ("rk", 512),
               ("lnw", 512), ("lnb", 512), ("ifb", 16), ("nmw", 8), ("nmlp", 8), ("cw", 32), ("cb", 8),
               ("ident", 128), ("mui", 128), ("mus", 128), ("mls", 128), ("ones", 128),
               ("lsel", 128), ("rsel", 4)]:
    OFF[_n] = (_o, _o + _w)
    _o += _w
NA = _o
SOFF = {}
_o = 0
for _n, _w in [("mu_r", 64), ("mu_k", 64), ("mu_v", 64), ("cwq", 256), ("cwk", 256), ("cbq", 64), ("cbk", 64),
               ("mnw", 64), ("w0", 64), ("a0", 64), ("kk", 64), ("ka", 64), ("rk", 64), ("lnw", 64), ("lnb", 64),
               ("ib", 1), ("fb", 1)]:
    SOFF[_n] = (_o, _o + _w)
    _o += _w
NSP = _o


ALIAS = {"r_sb0": "G0", "kf_sb0": "G1", "vf0": "G2", "osig0": "G13", "r_sb1": "G20", "kf_sb1": "G21", "vf1": "G22", "osig1": "G23",
         "wsig": "G3", "a_sb": "G4", "g_sb": "G5", "kap": "G6", "ktl": "G7",
         "bvec": "G8", "e1": "G9", "e2": "G10", "e3": "G11", "pcw": "G12", "ysb": "G11", "tA": "G9", "tB": "G10", "plast": "G16",
         "hml": "G14", "nlrep": "G15", "Fb": "G15", "cacc": "G16", "qks": "G16", "tAm": "G18", "tBm": "G19",
         "junk": "xm", "ctmp": "xm", "mixT": "TMbA", "TMb0": "TMbA", "TMb1": "TMbA",
         "TMb2": "TMbB", "TMb3": "TMbB", "Ub": "Zb1", "Ss": "Cs", "lo3": "spj", "smix": "spj",
         "xt1": "xt0", "xnT1": "xnT0", "qkx1": "qkx0"}
PARKEYS = {"r_sb", "kf_sb", "vf", "osig", "vaug", "gt", "qpb", "kTb", "ktm", "vrw", "lor"}
CURP = [0]


class SemCtx:
    def __init__(self, nc):
        self.nc = nc
        self.es = contextlib.ExitStack()
        self.engs = ["pe", "act", "dve", "pool", "sp"]
        self.esem = {e: self.es.enter_context(nc.semaphore("s_" + e)) for e in self.engs}
        self.ecnt = {e: 0 for e in self.engs}
        self.bsem = self.es.enter_context(nc.semaphore("s_bar"))
        self.phase = 0
        self.gsem = {}
        self.gbase = {}

    def group_sem(self, g):
        if g not in self.gsem:
            self.gsem[g] = self.es.enter_context(self.nc.semaphore("g_%d" % len(self.gsem)))
            self.gbase[g] = 0
        return self.gsem[g]

    def close(self):
        self.es.close()


class Prog:
    max_ops = None

    def __init__(self, ctx):
        self.ctx = ctx
        self.nc = ctx.nc
        self.ops = []
        self.last_writer = {}
        self.readers = {}
        self.dma_groups = {}

    def op(self, eng, fn, reads=(), writes=(), dma_group=None, wait_total=False):
        if self.max_ops is not None and len(self.ops) >= self.max_ops:
            return None
        reads = [(k + str(CURP[0])) if k in PARKEYS else k for k in reads]
        writes = [(k + str(CURP[0])) if k in PARKEYS else k for k in writes]
        reads = [ALIAS.get(k, k) for k in reads]
        writes = [ALIAS.get(k, k) for k in writes]
        if eng != "pe":
            writes = writes + [k for k in reads if k.startswith("PS") and k not in writes]
        deps = set()
        for b in reads:
            if b in self.last_writer:
                deps.add(self.last_writer[b])
        for b in writes:
            if b in self.last_writer:
                deps.add(self.last_writer[b])
            for r in self.readers.get(b, ()):
                deps.add(r)
        idx = len(self.ops)
        if dma_group is not None:
            deps = {d for d in deps if self.ops[d]["dma"] != dma_group}
        o = dict(eng=eng, fn=fn, deps=sorted(deps), dma=dma_group, idx=idx)
        if dma_group is not None:
            g = self.dma_groups.setdefault(dma_group, dict(total=0, wait_total=wait_total))
            g["total"] += 1
            o["dma_cnt"] = g["total"]
        self.ops.append(o)
        for b in reads:
            self.readers.setdefault(b, []).append(idx)
        for b in writes:
            self.last_writer[b] = idx
            self.readers[b] = []
        return idx

    def finalize(self):
        nc = self.nc
        ctx = self.ctx
        ops = self.ops
        needed = set()
        for o in ops:
            best = {}
            bestg = {}
            rd = []
            for d in o["deps"]:
                p = ops[d]
                if p["dma"] is not None:
                    bestg[p["dma"]] = max(bestg.get(p["dma"], -1), d)
                else:
                    if p["eng"] == "pe" and o["eng"] == "pe" and o["dma"] is None:
                        continue
                    best[p["eng"]] = max(best.get(p["eng"], -1), d)
            rd.extend(best.values())
            rd.extend(bestg.values())
            o["deps"] = sorted(rd)
            for d in best.values():
                needed.add(d)
        engs = ctx.engs
        last = {}
        for o in ops:
            if o["dma"] is None:
                last[o["eng"]] = o["idx"]
        needed |= set(last.values())
        cnt = dict(ctx.ecnt)
        for o in ops:
            if o["dma"] is None and o["idx"] in needed:
                cnt[o["eng"]] += 1
                o["sig"] = cnt[o["eng"]]
        for g in self.dma_groups:
            ctx.group_sem(g)
        phase = ctx.phase
        with nc.Block() as block:

            def emit_engine(ename, eng):
                known = {}
                if phase > 0:
                    eng.wait_ge(ctx.bsem, phase)
                for o in ops:
                    if o["eng"] != ename:
                        continue
                    for d in o["deps"]:
                        p = ops[d]
                        if p["dma"] is not None:
                            g = self.dma_groups[p["dma"]]
                            sem = ctx.gsem[p["dma"]]
                            val = ctx.gbase[p["dma"]] + 16 * (g["total"] if g["wait_total"] else p["dma_cnt"])
                            key = ("g", p["dma"])
                        else:
                            if p["eng"] == "pe" and ename == "pe" and o["dma"] is None:
                                continue
                            sem = ctx.esem[p["eng"]]
                            val = p["sig"]
                            key = ("e", p["eng"])
                        if known.get(key, 0) >= val:
                            continue
                        known[key] = val
                        eng.wait_ge(sem, val)
                    ins = o["fn"](eng)
                    if o["dma"] is not None:
                        ins.then_inc(ctx.gsem[o["dma"]], 16)
                    elif "sig" in o:
                        ins.then_inc(ctx.esem[ename], 1)
                if ename == "sp":
                    for e2 in engs:
                        if cnt[e2] > ctx.ecnt[e2]:
                            eng.wait_ge(ctx.esem[e2], cnt[e2])
                    for g, info in self.dma_groups.items():
                        eng.wait_ge(ctx.gsem[g], ctx.gbase[g] + 16 * info["total"])
                    eng.sem_inc(ctx.bsem, 1)

            @block.tensor
            def _(e):
                emit_engine("pe", e)

            @block.scalar
            def _(e):
                emit_engine("act", e)

            @block.vector
            def _(e):
                emit_engine("dve", e)

            @block.gpsimd
            def _(e):
                emit_engine("pool", e)

            @block.sync
            def _(e):
                emit_engine("sp", e)

        ctx.ecnt = cnt
        for g, info in self.dma_groups.items():
            ctx.gbase[g] += 16 * info["total"]
        ctx.phase += 1


STOP_EARLY = True


class _StopBuild(Exception):
    pass


def build_program(do_sample=True, debug=False):
    nc = bass.Bass("TRN2", target_bir_lowering=False)
    try:
        return _build_program(nc, do_sample, debug)
    except _StopBuild:
        return nc


def _build_program(nc, do_sample, debug):
    dbg = nc.dram_tensor("dbg", [128, 16, 512], F32, kind="ExternalOutput").ap() if debug else None
    din = lambda n, s: nc.dram_tensor(n, s, F32, kind="ExternalInput").ap()
    dout = lambda n, s: nc.dram_tensor(n, s, F32, kind="ExternalOutput").ap()
    xp = din("xp", [T, D])
    xs = din("xs", [NS, D])
    w_in = din("w_in", [D, INW])
    w_out = din("w_out", [D, D])
    mlp_up = din("mlp_up", [D, DFF])
    mlp_down = din("mlp_down", [DFF, D])
    packA_d = din("packA", [128, NA])
    mu_d = din("mu_b", [128, RWW])
    nfw_d = din("nfw_b", [128, D])
    wup_d = din("luw", [128, 512])
    gup_d = din("gup", [128, 512])
    spk_d = din("spack", [128, NSP])
    sC_d = din("sC", [128, 4096])
    sn_d = din("sn", [128, 64])
    sm_d = din("sm", [128, 1])
    sconv_d = din("sconv", [128, 2, 3, 64])
    sS_d = din("sS", [128, 4096])
    sshift_d = din("sshift", [128, 3, 64])
    sshl_d = din("sshl", [NS, 256])
    mul_d = din("mul", [NS, 256])

    yp = dout("yp", [T, D])
    ys = dout("ys", [NS, D])
    oC = dout("oC", [128, 4, 65])
    om = dout("om", [8, 1])
    oconv = dout("oconv", [128, 8, 3])
    oS = dout("oS", [128, 4, 64])
    oshift = dout("oshift", [1, RWW])
    osC = dout("osC", [128, 4096])
    osn = dout("osn", [128, 64])
    osm = dout("osm", [128, 1])
    osconv = dout("osconv", [128, 2, 3, 64])
    osS = dout("osS", [128, 4096])
    osshift = dout("osshift", [128, 3, 64])
    osshl = dout("osshl", [NS, 256])

    mix_d = nc.dram_tensor("mix_scr", [T + NS, D], BF16, kind="Internal").ap()
    scr1 = nc.dram_tensor("scr1", [128, 8, 64], F32, kind="Internal").ap()
    scr2 = nc.dram_tensor("scr2", [128, 3, 64], F32, kind="Internal").ap()
    scr3 = nc.dram_tensor("scr3", [NS, 2, 8, 64], F32, kind="Internal").ap()

    ctx = SemCtx(nc)
    PP = [Prog(ctx)]
    es_res = contextlib.ExitStack()
    cur = [es_res]

    def TT(name, shape, dt=F32):
        return cur[0].enter_context(nc.sbuf_tensor("t_" + name, list(shape), dt))

    def dma(q, out, in_, reads, writes, group, wait_total=False):
        group = "%s@%s" % (group, q)
        PP[0].op(q, lambda e: e.dma_start(out=out, in_=in_), reads=reads, writes=writes, dma_group=group, wait_total=wait_total)

    def dve(fn, r, w):
        PP[0].op("dve", fn, reads=r, writes=w)

    def act(fn, r, w):
        PP[0].op("act", fn, reads=r, writes=w)

    def pool(fn, r, w):
        PP[0].op("pool", fn, reads=r, writes=w)

    def pe(fn, r, w):
        PP[0].op("pe", fn, reads=r, writes=w)

    def mm(out, lhsT, rhs, start, stop, r, w):
        pe(lambda e: e.matmul(out, lhsT=lhsT, rhs=rhs, start=start, stop=stop), r, w)

    es_ps = contextlib.ExitStack()
    PS = [es_ps.enter_context(nc.psum_tensor("PS%d" % i, [128, 1024], F32)) for i in range(4)]
    PSB = [p.bitcast(BF16) for p in PS]
    psi = {"A": 0, "F": 0, "B": 0}
    POOLS = {"A": [0, 1, 2, 3], "F": [0, 1], "B": [2, 3]}
    CURPOOL = ["A"]

    def nps():
        pl = CURPOOL[0]
        lst = POOLS[pl]
        i = lst[psi[pl] % len(lst)]
        psi[pl] += 1
        return PS[i], PSB[i], "PS%d" % i

    wq = TT("wq", [128, 8, INW], BF16)
    mub = TT("mub", [128, RWW], F32)
    luw = TT("luw", [128, 512], BF16)
    gup = TT("gup", [128, 512], BF16)
    pA = TT("pA", [128, NA], F32)
    identb = TT("identb", [128, 128], BF16)

    def PA(n):
        a, b = OFF[n]
        return pA[:, a:b]

    w_in_v = w_in.rearrange("(c p) n -> p c n", p=128)
    dma("sp", pA[:], packA_d, [], ["pA"], "init", True)
    dma("pool", luw[:], wup_d, [], ["luw"], "init", True)
    dma("pool", gup[:], gup_d, [], ["gup"], "init", True)
    w_out_v = w_out.rearrange("(c p) n -> p c n", p=128)
    dve(lambda e: e.tensor_copy(out=identb[:], in_=PA("ident")), ["pA"], ["identb"])


    def _dbgdump(tag):
        if debug != tag:
            return
        dstg_ = cur[0].enter_context(nc.sbuf_tensor("t_dbgst%d" % tag, [128, 512], F32))
        def dd(slot, ap, key, n):
            dve(lambda e: e.tensor_copy(out=dstg_[:, 0:n], in_=ap), [key], ["dbgst"])
            dma("sp", dbg[:, slot, 0:n], dstg_[:, 0:n], ["dbgst"], [], "dbg")
        dd(0, PA("ident"), "pA", 128)
        dd(1, PA("mui"), "pA", 128)
        dd(2, PA("mnw"), "pA", 512)
        dd(3, PA("w0"), "pA", 512)
        dd(4, PA("lnb"), "pA", 512)
        PP[0].max_ops = len(PP[0].ops)
        PP[0].finalize()
        raise _StopBuild()
    _dbgdump(3)
    dma("sp", mub[:], mu_d, [], ["mub"], "init", True)
    PP[0].finalize()
    PP[0] = Prog(ctx)

    def rmsnorm_T(xt, xk, nt, dstT, dstk, col0, wname, tmpb, tmpbk, junk, junkk, st, stk):
        act(lambda e: e.activation(out=junk[:nt, :], in_=xt[:nt, :], func=AF.Square, accum_out=st[:nt, 0:1]), [xk], [junkk, stk])
        act(lambda e: e.activation(out=st[:nt, 1:2], in_=st[:nt, 0:1], func=AF.Sqrt, bias=EPS, scale=1.0 / D), [stk], [stk])
        dve(lambda e: e.reciprocal(out=st[:nt, 2:3], in_=st[:nt, 1:2]), [stk], [stk])
        dve(lambda e: e.tensor_scalar_mul(out=tmpb[:nt, :], in0=xt[:nt, :], scalar1=st[:nt, 2:3]), [xk, stk], [tmpbk])
        ps, psb, pk = nps()
        for c in range(8):
            pe(lambda e, c=c: e.transpose(psb[:, c * 128:c * 128 + nt], tmpb[:nt, c * 128:(c + 1) * 128], identb[:nt, :nt]), [tmpbk, "identb"], [pk])
        a, b_ = OFF[wname]
        dve(lambda e: e.tensor_tensor(out=dstT[:, :, col0:col0 + nt],
                                      in0=psb[:, 0:1024].rearrange("p (c t) -> p c t", c=8)[:, :, 0:nt],
                                      in1=pA[:, a:b_].unsqueeze(2).to_broadcast([128, 8, nt]), op=ALU.mult), [pk, "pA"], [dstk])

    def head_ml(nt, hsrc, hk, osig, ok, mix, mixk, W, sfx=""):
        tA, tB, s8 = W["tA"], W["tB"], W["s8"]
        h3 = lambda t: t[:nt, :].rearrange("p (h d) -> p h d", h=8)
        bc = lambda t, c: t[:nt, c:c + 8].unsqueeze(2).to_broadcast([nt, 8, 64])
        dve(lambda e: e.tensor_tensor(out=tA[:nt, :], in0=hsrc[:nt, :], in1=osig[:nt, :], op=ALU.mult), [hk, ok], ["tA" + sfx])
        dve(lambda e: e.tensor_tensor(out=tB[:nt, :], in0=tA[:nt, :], in1=tA[:nt, :], op=ALU.mult), ["tA" + sfx], ["tB" + sfx])
        dve(lambda e: e.tensor_reduce(out=s8[:nt, 0:8], in_=h3(tB), axis=AX.X, op=ALU.add), ["tB" + sfx], ["s8" + sfx])
        act(lambda e: e.activation(out=s8[:nt, 8:16], in_=s8[:nt, 0:8], func=AF.Sqrt, bias=EPS, scale=1.0 / 64), ["s8" + sfx], ["s8" + sfx])
        dve(lambda e: e.reciprocal(out=s8[:nt, 16:24], in_=s8[:nt, 8:16]), ["s8" + sfx], ["s8" + sfx])
        dve(lambda e: e.tensor_tensor(out=h3(tB), in0=h3(tA), in1=bc(s8, 16), op=ALU.mult), ["tA" + sfx, "s8" + sfx], ["tB" + sfx])
        dve(lambda e: e.tensor_tensor(out=mix[:nt, 0:512], in0=tB[:nt, :], in1=PA("mnw")[:nt, :], op=ALU.mult), ["tB" + sfx, "pA"], [mixk])

    def head_rw(nt, ysrc, yk, bon, bonk, vf, vfk, g, gk, mix, mixk, W):
        tA, tB, s8 = W["tA"], W["tB"], W["s8"]
        h3 = lambda t: t[:nt, :].rearrange("p (h d) -> p h d", h=8)
        bc = lambda t, c: t[:nt, c:c + 8].unsqueeze(2).to_broadcast([nt, 8, 64])
        dve(lambda e: e.tensor_tensor(out=h3(tA), in0=h3(vf), in1=bc(bon, 0), op=ALU.mult), [vfk, bonk], ["tA"])
        dve(lambda e: e.tensor_tensor(out=tA[:nt, :], in0=tA[:nt, :], in1=ysrc[:nt, :], op=ALU.add), ["tA", yk], ["tA"])
        dve(lambda e: e.tensor_reduce(out=s8[:nt, 24:32], in_=h3(tA), axis=AX.X, op=ALU.add), ["tA"], ["s8"])
        dve(lambda e: e.tensor_scalar_mul(out=s8[:nt, 24:32], in0=s8[:nt, 24:32], scalar1=1.0 / 64), ["s8"], ["s8"])
        dve(lambda e: e.tensor_tensor(out=h3(tA), in0=h3(tA), in1=bc(s8, 24), op=ALU.subtract), ["tA", "s8"], ["tA"])
        dve(lambda e: e.tensor_tensor(out=tB[:nt, :], in0=tA[:nt, :], in1=tA[:nt, :], op=ALU.mult), ["tA"], ["tB"])
        dve(lambda e: e.tensor_reduce(out=s8[:nt, 32:40], in_=h3(tB), axis=AX.X, op=ALU.add), ["tB"], ["s8"])
        act(lambda e: e.activation(out=s8[:nt, 40:48], in_=s8[:nt, 32:40], func=AF.Sqrt, bias=GN_EPS, scale=1.0 / 64), ["s8"], ["s8"])
        dve(lambda e: e.reciprocal(out=s8[:nt, 48:56], in_=s8[:nt, 40:48]), ["s8"], ["s8"])
        dve(lambda e: e.tensor_tensor(out=h3(tB), in0=h3(tA), in1=bc(s8, 48), op=ALU.mult), ["tA", "s8"], ["tB"])
        dve(lambda e: e.tensor_tensor(out=tB[:nt, :], in0=tB[:nt, :], in1=PA("lnw")[:nt, :], op=ALU.mult), ["tB", "pA"], ["tB"])
        dve(lambda e: e.tensor_tensor(out=tB[:nt, :], in0=tB[:nt, :], in1=PA("lnb")[:nt, :], op=ALU.add), ["tB", "pA"], ["tB"])
        dve(lambda e: e.tensor_tensor(out=mix[:nt, 512:1024], in0=tB[:nt, :], in1=g[:nt, :], op=ALU.mult), ["tB", gk], [mixk])

    def out_proj(nt, mix, mixk, xt, xk, row0, W):
        dma("pool", mix_d[row0:row0 + nt, :], mix[:nt, :], [mixk], [], "mixst")

    with contextlib.ExitStack() as es1:
        cur[0] = es1
        W = {}
        Gbig = TT("Gbig", [128, 24, 512])
        G = [Gbig[:, i, :] for i in range(24)]
        W["s8"] = TT("s8", [128, 64])
        W["xm"] = TT("xm", [128, D])
        xt = [TT("xt0", [128, D])] * 2
        junk = W["xm"]
        st = TT("st", [128, 4])
        mix = TT("mix", [128, D], BF16)
        xsb = TT("xsb", [128, D], BF16)
        xnT = [TT("xnT0", [128, 8, 129], BF16)] * 2
        dxT = TT("dxT", [128, 8, 128], BF16)
        qkx = [TT("qkx0", [128, 8, 131])] * 2
        cacc = Gbig[:, 16:18, :].rearrange("p a (c t) -> p (a c) t", t=128)
        ctmp = W["xm"][:, :].rearrange("p (c t) -> p c t", c=8)
        qks = cacc
        QPB = [TT("qpb%d" % i, [128, 4, 128], BF16) for i in range(2)]
        KTB = [TT("kTb%d" % i, [128, 4, 128], BF16) for i in range(2)]
        KTM = [TT("ktm%d" % i, [128, 8, 64], BF16) for i in range(2)]
        VAUG = [TT("vaug%d" % i, [128, 8, 65], BF16) for i in range(2)]
        GT = [TT("gt%d" % i, [128, 96]) for i in range(2)]
        runmax = TT("runmax", [128, 8])
        nBc = TT("nBc", [128, 8])
        Cst = TT("Cst", [128, 4, 65])
        Cbf = TT("Cbf", [128, 4, 65], BF16)
        Fb = G[15].rearrange("p (j t) -> p j t", j=4)
        hml = G[14]
        nlrep = G[15].rearrange("p (h d) -> p h d", h=8)
        wsig, a_sb, g_sb, kap, ktl, bvec, e1, e2, e3, pcw, ysb = G[3], G[4], G[5], G[6], G[7], G[8], G[9], G[10], G[11], G[12], G[11]
        W["tA"], W["tB"] = G[9], G[10]
        WM = {"tA": G[18], "tB": G[19], "s8": TT("s8m", [128, 64])}
        r8b = TT("r8b", [128, 8])
        VRW = [TT("vrw%d" % i, [128, 8, 64], BF16) for i in range(2)]
        LOR = [TT("lor%d" % i, [128, 256], BF16) for i in range(2)]
        lorT = TT("lorT", [128, 2, 128], BF16)
        r8 = TT("r8", [128, 32])
        bon = TT("bon", [128, 8])
        TMb = TT("TMb", [128, 4, 512], BF16)
        W["mixT"] = TMb[:, 0:2, :].rearrange("p a (c t) -> p (a c) t", t=128)
        Btz = TT("Btz", [128, 8, 128], BF16)
        Ktz = TT("Ktz", [128, 8, 128], BF16)
        FMt = TT("FMt", [128, 4, 4, 128], BF16)
        Am = [TT("Am%d" % i, [128, 8, 128], BF16) for i in range(3)]
        PTb = TT("PTb", [128, 8, 128], BF16)
        Pw = [TT("Pw%d" % i, [128, 8, 128], BF16) for i in range(4)]
        Zb = [TT("Zb%d" % i, [128, 8, 64], BF16) for i in range(2)]
        Ub = Zb[1]
        Sst = TT("Sst", [128, 4, 64])
        Sbf = TT("Sbf", [128, 4, 64], BF16)
        WLfm = TT("WLfm", [128, 4])
        plast = G[17]

        def wqk(c0, c1):
            return ["wq%d" % g for g in range(c0 // 512, (c1 - 1) // 512 + 1)]

        for g in range(8):
            c0, c1 = g * 512, min(INW, (g + 1) * 512)
            dma("pool", wq[:, :, c0:c1], w_in_v[:, :, c0:c1], [], ["wq%d" % g], "wq%d" % g)
        for p_ in range(2):
            pool(lambda e, p_=p_: e.memset(VAUG[p_][:], 1.0), [], ["vaug%d" % p_])
        pool(lambda e: e.memset(Btz[:], 0.0), [], ["Btz"])
        pool(lambda e: e.memset(Ktz[:], 0.0), [], ["Ktz"])
        pool(lambda e: e.memset(Cst[:], 0.0), [], ["Cst"])
        pool(lambda e: e.memset(Cbf[:], 0.0), [], ["Cbf"])
        pool(lambda e: e.memset(Sst[:], 0.0), [], ["Sst"])
        pool(lambda e: e.memset(Sbf[:], 0.0), [], ["Sbf"])
        pool(lambda e: e.memset(runmax[:], -1e30), [], ["runmax"])
        pool(lambda e: e.memset(nBc[:], 0.0), [], ["nBc"])
        pool(lambda e: e.memset(xnT[0][:, :, 0:1], 0.0), [], ["xnT0"])
        pool(lambda e: e.memset(qkx[0][:, :, 0:3], 0.0), [], ["qkx0"])

        if debug == 2:
            dstg = TT("dbgstage", [128, 512]) if False else G[12]
            def ddump0(slot, ap, key, n):
                dve(lambda e: e.tensor_copy(out=dstg[:, 0:n], in_=ap), [key], ["pcw"])
                dma("sp", dbg[:, slot, 0:n], dstg[:, 0:n], ["pcw"], [], "dbg")
            ddump0(0, PA("ident"), "pA", 128)
            ddump0(1, PA("mui"), "pA", 128)
            ddump0(2, luw[:, :], "luw", 512)
            ddump0(3, gup[:, :], "gup", 512)
            ddump0(4, W1[:, 0, 0:512], "W1", 512)
            PP[0].max_ops = len(PP[0].ops)
            if STOP_EARLY:
                PP[0].finalize()
                raise _StopBuild()
        MUI = PA("mui")
        MUS = PA("mus")
        MLS = PA("mls")
        ONES = PA("ones")
        IDF = PA("ident")
        bc8 = lambda ap: ap.unsqueeze(2).to_broadcast([128, 8, 64])
        m8 = lambda m: m.unsqueeze(1).to_broadcast([128, 8, 128])
        v3 = lambda t: t[:].rearrange("p (h d) -> p h d", h=8)
        hoff = lambda h: (h % 2) * 512 + (h // 2) * 128

        def make_block(b):
            p_ = b % 2
            r_sb, kf_sb, vf, osig = G[0 + 20 * p_] if p_ == 0 else G[20], G[1] if p_ == 0 else G[21], G[2] if p_ == 0 else G[22], G[13] if p_ == 0 else G[23]
            vaug, gt, qpb, kTb, ktm, vrw, lor = VAUG[p_], GT[p_], QPB[p_], KTB[p_], KTM[p_], VRW[p_], LOR[p_]

            def front_stage():
                x_ = xt[b % 2]
                xk = "xt%d" % (b % 2)
                xn = xnT[b % 2]
                xnk = "xnT%d" % (b % 2)
                qx = qkx[b % 2]
                qxk = "qkx%d" % (b % 2)
                dma("sp", x_[:], xp[b * 128:(b + 1) * 128, :], [], [xk], xk)
                rmsnorm_T(x_, xk, 128, xn, xnk, 1, "nmw", xsb, "xsb", junk, "junk", st, "st")
                cur_x = xn[:, :, 1:129]
                prv_x = xn[:, :, 0:128]
                dve(lambda e, xn=xn: e.tensor_tensor(out=dxT[:], in0=xn[:, :, 0:128], in1=xn[:, :, 1:129], op=ALU.subtract), [xnk], ["dxT"])

                yield
                ps, psb, pk = nps()
                for j in range(8):
                    for c in range(8):
                        mm(ps[:, j * 128:(j + 1) * 128], wq[:, c, j * 128:(j + 1) * 128], cur_x[:, c, :], c == 0, c == 7, wqk(j * 128, (j + 1) * 128) + [xnk], [pk])
                for a_ in range(2):
                    act(lambda e, ps=ps, qx=qx, a_=a_: e.copy(out=qx[:, 4 * a_:4 * a_ + 4, 3:131], in_=ps[:, a_ * 512:(a_ + 1) * 512].rearrange("p (j t) -> p j t", j=4)), [pk], [qxk])
                if b == NB - 1:
                    dma("pool", oconv, qx[:, :, 128:131], [qxk], [], "fin")

                yield
                def tm_plain(col0, ncol, ps_ap, pk):
                    for c in range(8):
                        mm(ps_ap, cur_x[:, c, :], wq[:, c, col0:col0 + ncol], c == 0, c == 7, [xnk] + wqk(col0, col0 + ncol), [pk])

                def tm_shift(col0, ncol, dst, dk):
                    ps, psb, pk = nps()
                    for c in range(8):
                        mm(ps[:, 0:ncol], cur_x[:, c, :], wq[:, c, MLW + col0:MLW + col0 + ncol], c == 0, c == 7, [xnk] + wqk(MLW + col0, MLW + col0 + ncol), [pk])
                    for c in range(8):
                        mm(ps[:, 512:512 + ncol], dxT[:, c, :], wq[:, c, MLW + col0:MLW + col0 + ncol], c == 0, c == 7, ["dxT"] + wqk(MLW + col0, MLW + col0 + ncol), [pk])
                    dve(lambda e, ps=ps: e.tensor_tensor(out=dst, in0=ps[:, 512:512 + ncol], in1=mub[:, col0:col0 + ncol], op=ALU.mult), [pk, "mub"], [dk])
                    dve(lambda e, ps=ps: e.tensor_tensor(out=dst, in0=dst, in1=ps[:, 0:ncol], op=ALU.add), [pk, dk], [dk])

                ps, psb, pk = nps()
                tm_plain(1024, 512, ps[:, 0:512], pk)
                tm_plain(1536, 512, ps[:, 512:1024], pk)
                act(lambda e, ps=ps: e.copy(out=vaug[:, :, 0:64], in_=ps[:, 0:512].rearrange("p (h d) -> p h d", h=8)), [pk], ["vaug"])
                act(lambda e, ps=ps: e.activation(out=osig[:], in_=ps[:, 512:1024], func=AF.Sigmoid), [pk], ["osig"])
                ps, psb, pk = nps()
                tm_plain(2048, 16, ps[:, 0:16], pk)
                dve(lambda e, ps=ps: e.tensor_tensor(out=gt[:, 0:16], in0=ps[:, 0:16], in1=PA("ifb"), op=ALU.add), [pk, "pA"], ["gt"])
                tm_shift(0, 512, r_sb[:], "r_sb")
                tm_shift(512, 512, kf_sb[:], "kf_sb")
                tm_shift(1024, 512, vf[:], "vf")
                pool(lambda e: e.tensor_copy(out=vrw[:], in_=vf[:].rearrange("p (h d) -> p h d", h=8)), ["vf"], ["vrw"])
                ltmp = G[16]
                tm_shift(1536, 256, ltmp[:, 0:256], "cacc")
                act(lambda e: e.activation(out=lor[:, 0:64], in_=ltmp[:, 0:64], func=AF.Tanh), ["cacc"], ["lor"])
                act(lambda e: e.copy(out=lor[:, 64:128], in_=ltmp[:, 64:128]), ["cacc"], ["lor"])
                act(lambda e: e.activation(out=lor[:, 128:256], in_=ltmp[:, 128:256], func=AF.Sigmoid), ["cacc"], ["lor"])
                if b == NB - 1:
                    lastc = xn[:, :, 128:129]
                    for n0 in range(0, RWW, 512):
                        nn = min(512, RWW - n0)
                        ps2, _, pk2 = nps()
                        for c in range(8):
                            mm(ps2[0:1, 0:nn], lastc[:, c, :], wq[:, c, MLW + n0:MLW + n0 + nn], c == 0, c == 7, [xnk] + wqk(MLW + n0, MLW + n0 + nn), [pk2])
                        act(lambda e, ps2=ps2, n0=n0, nn=nn: e.copy(out=plast[0:1, 0:nn], in_=ps2[0:1, 0:nn]), [pk2], ["plast"])
                        dma("pool", oshift[:, n0:n0 + nn], plast[0:1, 0:nn], ["plast"], [], "fin")

                act(lambda e: e.activation(out=gt[:, 56:64], in_=gt[:, 8:16], func=AF.Exp, scale=-1.0), ["gt"], ["gt"])
                act(lambda e: e.activation(out=gt[:, 16:24], in_=gt[:, 56:64], func=AF.Ln, bias=1.0, scale=1.0), ["gt"], ["gt"])
                dve(lambda e: e.tensor_copy(out=nlrep[:], in_=bc8(gt[:, 16:24])), ["gt"], ["nlrep"])
                ps, psb, pk = nps()
                mm(ps[:, 0:8], MUI, gt[:, 16:24], True, True, ["pA", "gt"], [pk])
                mm(ps[:, 8:16], ONES, gt[:, 16:24], True, True, ["pA", "gt"], [pk])
                for j in range(4):
                    mm(ps[:, 512 + j * 128:512 + (j + 1) * 128], nlrep[:, 2 * j:2 * j + 2, :].rearrange("p a d -> p (a d)"), MUI, True, True, ["nlrep", "pA"], [pk])
                dve(lambda e, ps=ps: e.tensor_tensor(out=gt[:, 24:32], in0=ps[:, 0:8], in1=gt[:, 0:8], op=ALU.add), [pk, "gt"], ["gt"])
                act(lambda e: e.activation(out=gt[:, 32:40], in_=gt[:, 24:32], func=AF.Exp), ["gt"], ["gt"])
                dve(lambda e, ps=ps: e.tensor_tensor(out=gt[:, 56:64], in0=gt[:, 24:32], in1=ps[:, 8:16], op=ALU.subtract), [pk, "gt"], ["gt"])
                act(lambda e: e.activation(out=gt[:, 40:48], in_=gt[:, 56:64], func=AF.Exp), ["gt"], ["gt"])
                act(lambda e, ps=ps: e.activation(out=gt[:, 48:56], in_=ps[:, 8:16], func=AF.Exp, scale=-1.0), [pk], ["gt"])
                act(lambda e, ps=ps: e.activation(out=Fb[:], in_=ps[:, 512:1024].rearrange("p (j t) -> p j t", j=4), func=AF.Exp, scale=-1.0), [pk], ["Fb"])
                dve(lambda e: e.tensor_tensor(out=gt[:, 56:64], in0=gt[:, 24:32], in1=nBc[:], op=ALU.add), ["gt", "nBc"], ["gt"])
                dve(lambda e: e.tensor_tensor(out=runmax[:], in0=runmax[:], in1=gt[:, 56:64], op=ALU.max), ["gt", "runmax"], ["runmax"])
                dve(lambda e, ps=ps: e.tensor_tensor(out=nBc[:], in0=nBc[:], in1=ps[:, 8:16], op=ALU.add), [pk, "nBc"], ["nBc"])

                yield
                cwv = PA("cw").rearrange("p (c j) -> p c j", j=4)
                wbc = lambda j: cwv[:, :, j:j + 1].to_broadcast([128, 8, 128])
                pool(lambda e, qx=qx: e.tensor_tensor(out=cacc[:], in0=qx[:, :, 3:131], in1=wbc(3), op=ALU.mult), [qxk, "pA"], ["cacc"])
                for j in range(3):
                    pool(lambda e, qx=qx, j=j: e.tensor_tensor(out=ctmp[:], in0=qx[:, :, j:j + 128], in1=wbc(j), op=ALU.mult), [qxk, "pA"], ["ctmp"])
                    pool(lambda e: e.tensor_tensor(out=cacc[:], in0=cacc[:], in1=ctmp[:], op=ALU.add), ["cacc", "ctmp"], ["cacc"])
                pool(lambda e: e.tensor_tensor(out=cacc[:], in0=cacc[:], in1=PA("cb").unsqueeze(2).to_broadcast([128, 8, 128]), op=ALU.add), ["cacc", "pA"], ["cacc"])
                act(lambda e: e.activation(out=qks[:], in_=cacc[:], func=AF.Silu), ["cacc"], ["qks"])
                dve(lambda e: e.tensor_tensor(out=qpb[:], in0=qks[:, 0:4, :], in1=Fb[:], op=ALU.mult), ["qks", "Fb"], ["qpb"])
                act(lambda e: e.activation(out=kTb[:], in_=qks[:, 4:8, :], func=AF.Copy, scale=0.125), ["qks"], ["kTb"])

                yield
                ps, psb, pk = nps()
                for j in range(4):
                    pe(lambda e, j=j, psb=psb: e.transpose(psb[:, j * 128:(j + 1) * 128], kTb[:, j, :], identb[:]), ["kTb", "identb"], [pk])
                dve(lambda e, psb=psb: e.tensor_tensor(out=ktm[:], in0=psb[:, 0:512].rearrange("p (h d) -> p h d", h=8), in1=bc8(gt[:, 40:48]), op=ALU.mult), [pk, "gt"], ["ktm"])

                yield
                yield
                if b + 1 < NB:
                    pool(lambda e, xn=xn: e.tensor_copy(out=xn[:, :, 0:1], in_=xn[:, :, 128:129]), [xnk], [xnk])
                    pool(lambda e, qx=qx: e.tensor_copy(out=qx[:, :, 0:3], in_=qx[:, :, 128:131]), [qxk], [qxk])
                yield

            def ml_stage():
                ps, psb, pk = nps()
                for h in range(8):
                    j, hp = h // 2, h % 2
                    sl = slice(hp * 64, hp * 64 + 64)
                    mm(ps[:, hoff(h):hoff(h) + 128], kTb[sl, j, :], qpb[sl, j, :], True, True, ["kTb", "qpb"], [pk])
                for h in range(8):
                    dve(lambda e, h=h, ps=ps: e.scalar_tensor_tensor(out=PTb[:, h, :], in0=ps[:, hoff(h):hoff(h) + 128], scalar=gt[:, 32 + h:33 + h], in1=MUI, op0=ALU.mult, op1=ALU.mult), [pk, "gt", "pA"], ["PTb"])
                yield
                ps, psb, pk = nps()
                psn = lambda ps, h: ps[:, (h // 4) * 512 + (h % 4) * 65:(h // 4) * 512 + (h % 4) * 65 + 65]
                for h in range(8):
                    j, hp = h // 2, h % 2
                    sl = slice(hp * 64, hp * 64 + 64)
                    mm(psn(ps, h), PTb[:, h, :], vaug[:, h, :], True, False, ["PTb", "vaug"], [pk])
                    mm(psn(ps, h), qpb[sl, j, :], Cbf[sl, j, :], False, True, ["qpb", "Cbf"], [pk])
                pn4 = ps[:, :].rearrange("p (a r) -> p a r", a=2)[:, :, 0:260].rearrange("p a (h d) -> p a h d", h=4)
                for a_ in range(2):
                    act(lambda e, pn4=pn4, a_=a_: e.copy(out=r8[:, 4 * a_:4 * a_ + 4], in_=pn4[:, a_, :, 64]), [pk], ["r8"])
                dve(lambda e: e.scalar_tensor_tensor(out=r8[:, 8:16], in0=r8[:, 0:8], scalar=-1.0, in1=r8[:, 0:8], op0=ALU.mult, op1=ALU.max), ["r8"], ["r8"])
                dve(lambda e: e.tensor_scalar_max(out=r8[:, 8:16], in0=r8[:, 8:16], scalar1=1.0), ["r8"], ["r8"])
                dve(lambda e: e.reciprocal(out=r8[:, 16:24], in_=r8[:, 8:16]), ["r8"], ["r8"])
                for a_ in range(2):
                    dve(lambda e, pn4=pn4, a_=a_: e.tensor_tensor(out=hml[:, a_ * 256:(a_ + 1) * 256].rearrange("p (h d) -> p h d", h=4), in0=pn4[:, a_, :, 0:64],
                                                              in1=r8[:, 16 + 4 * a_:20 + 4 * a_].unsqueeze(2).to_broadcast([128, 4, 64]), op=ALU.mult), [pk, "r8"], ["hml"])
                yield
                ps, psb, pk = nps()
                for h in range(8):
                    j = h // 2
                    mm(psn(ps, h), ktm[:, 2 * j:2 * j + 2, :].rearrange("p a d -> p (a d)"), vaug[:, h, :], True, True, ["ktm", "vaug"], [pk])
                pu4 = ps[:, :].rearrange("p (a r) -> p a r", a=2)[:, :, 0:260].rearrange("p a (h d) -> p a h d", h=4)
                for hp in range(2):
                    sl = slice(hp * 64, hp * 64 + 64)
                    decb = gt[sl, 48:56].rearrange("p (j q) -> p j q", q=2)[:, :, hp:hp + 1].to_broadcast([64, 4, 65])
                    dve(lambda e, sl=sl, decb=decb: e.tensor_tensor(out=Cst[sl, :, :], in0=Cst[sl, :, :], in1=decb, op=ALU.mult), ["Cst", "gt"], ["Cst"])
                    for a in range(2):
                        src = pu4[sl, a, hp::2, :]
                        dve(lambda e, sl=sl, a=a, src=src: e.tensor_tensor(out=Cst[sl, 2 * a:2 * a + 2, :], in0=Cst[sl, 2 * a:2 * a + 2, :], in1=src, op=ALU.add), [pk, "Cst"], ["Cst"])
                act(lambda e: e.copy(out=Cbf[:], in_=Cst[:]), ["Cst"], ["Cbf"])
                head_ml(128, hml, "hml", osig, "osig", mix, "mix", WM, "m")


                yield
            def rw_stage():
                ps, psb, pk = nps()
                pe(lambda e, psb=psb: e.transpose(psb[:, 0:128], lor[:, 0:128], identb[:]), ["lor", "identb"], [pk])
                pe(lambda e, psb=psb: e.transpose(psb[:, 128:256], lor[:, 128:256], identb[:]), ["lor", "identb"], [pk])
                act(lambda e, psb=psb: e.copy(out=lorT[:], in_=psb[:, 0:256].rearrange("p (a t) -> p a t", a=2)), [pk], ["lorT"])
                ps, psb, pk = nps()
                mm(ps[:, 0:512], lorT[0:64, 0, :], luw[0:64, :], True, True, ["lorT", "luw"], [pk])
                mm(ps[:, 512:1024], lorT[64:128, 0, :], luw[64:128, :], True, True, ["lorT", "luw"], [pk])
                dve(lambda e, ps=ps: e.tensor_tensor(out=e1[:], in0=ps[:, 0:512], in1=PA("w0"), op=ALU.add), [pk, "pA"], ["e1"])
                act(lambda e: e.activation(out=wsig[:], in_=e1[:], func=AF.Sigmoid), ["e1"], ["wsig"])
                dve(lambda e, ps=ps: e.tensor_tensor(out=e2[:], in0=ps[:, 512:1024], in1=PA("a0"), op=ALU.add), [pk, "pA"], ["e2"])
                act(lambda e: e.activation(out=a_sb[:], in_=e2[:], func=AF.Sigmoid), ["e2"], ["a_sb"])
                ps, psb, pk = nps()
                mm(ps[:, 0:512], lorT[:, 1, :], gup[:, :], True, True, ["lorT", "gup"], [pk])
                act(lambda e, ps=ps: e.copy(out=g_sb[:], in_=ps[:, 0:512]), [pk], ["g_sb"])
                yield
                dve(lambda e: e.tensor_tensor(out=e1[:], in0=kf_sb[:], in1=PA("kk"), op=ALU.mult), ["kf_sb", "pA"], ["e1"])
                dve(lambda e: e.tensor_tensor(out=e2[:], in0=e1[:], in1=e1[:], op=ALU.mult), ["e1"], ["e2"])
                dve(lambda e: e.tensor_reduce(out=r8b[:, 0:8], in_=v3(e2), axis=AX.X, op=ALU.add), ["e2"], ["r8b"])
                dve(lambda e: e.tensor_scalar_max(out=r8b[:, 0:8], in0=r8b[:, 0:8], scalar1=1e-24), ["r8b"], ["r8b"])
                act(lambda e: e.activation(out=r8b[:, 0:8], in_=r8b[:, 0:8], func=AF.Sqrt), ["r8b"], ["r8b"])
                dve(lambda e: e.reciprocal(out=r8b[:, 0:8], in_=r8b[:, 0:8]), ["r8b"], ["r8b"])
                dve(lambda e: e.tensor_tensor(out=v3(kap), in0=v3(e1), in1=bc8(r8b[:, 0:8]), op=ALU.mult), ["e1", "r8b"], ["kap"])
                dve(lambda e: e.tensor_scalar_add(out=e2[:], in0=a_sb[:], scalar1=-1.0), ["a_sb"], ["e2"])
                dve(lambda e: e.tensor_tensor(out=e2[:], in0=e2[:], in1=PA("ka"), op=ALU.mult), ["e2", "pA"], ["e2"])
                dve(lambda e: e.tensor_tensor(out=e2[:], in0=e2[:], in1=kf_sb[:], op=ALU.mult), ["e2", "kf_sb"], ["e2"])
                dve(lambda e: e.tensor_tensor(out=ktl[:], in0=e2[:], in1=kf_sb[:], op=ALU.add), ["e2", "kf_sb"], ["ktl"])
                dve(lambda e: e.tensor_tensor(out=bvec[:], in0=a_sb[:], in1=kap[:], op=ALU.mult), ["a_sb", "kap"], ["bvec"])
                dve(lambda e: e.tensor_tensor(out=e2[:], in0=r_sb[:], in1=ktl[:], op=ALU.mult), ["r_sb", "ktl"], ["e2"])
                dve(lambda e: e.tensor_tensor(out=e2[:], in0=e2[:], in1=PA("rk"), op=ALU.mult), ["e2", "pA"], ["e2"])
                dve(lambda e: e.tensor_reduce(out=bon[:], in_=v3(e2), axis=AX.X, op=ALU.add), ["e2"], ["bon"])
                yield
                ps, psb, pk = nps()
                mm(ps[:, 0:512], MUI, wsig[:], True, True, ["pA", "wsig"], [pk])
                mm(ps[:, 512:1024], ONES, wsig[:], True, True, ["pA", "wsig"], [pk])
                act(lambda e, ps=ps: e.copy(out=pcw[:], in_=ps[:, 0:512]), [pk], ["pcw"])
                dve(lambda e: e.tensor_tensor(out=e1[:], in0=pcw[:], in1=wsig[:], op=ALU.subtract), ["pcw", "wsig"], ["e1"])
                act(lambda e: e.activation(out=e1[:], in_=e1[:], func=AF.Exp, scale=-C0), ["e1"], ["e1"])
                dve(lambda e: e.tensor_tensor(out=TMb[:, 0, :], in0=kap[:], in1=e1[:], op=ALU.mult), ["kap", "e1"], ["TMb0"])
                act(lambda e: e.activation(out=e2[:], in_=pcw[:], func=AF.Exp, scale=-C0), ["pcw"], ["e2"])
                dve(lambda e: e.tensor_tensor(out=TMb[:, 1, :], in0=r_sb[:], in1=e2[:], op=ALU.mult), ["r_sb", "e2"], ["TMb1"])
                act(lambda e: e.activation(out=e3[:], in_=pcw[:], func=AF.Exp, scale=C0), ["pcw"], ["e3"])
                dve(lambda e: e.tensor_tensor(out=TMb[:, 2, :], in0=bvec[:], in1=e3[:], op=ALU.mult), ["bvec", "e3"], ["TMb2"])
                dve(lambda e: e.tensor_tensor(out=TMb[:, 3, :], in0=ktl[:], in1=e3[:], op=ALU.mult), ["ktl", "e3"], ["TMb3"])
                dve(lambda e, ps=ps: e.tensor_tensor(out=e1[:], in0=ps[:, 512:1024], in1=pcw[:], op=ALU.subtract), [pk, "pcw"], ["e1"])
                act(lambda e: e.activation(out=e1[:], in_=e1[:], func=AF.Exp, scale=-C0), ["e1"], ["e1"])
                for hp in range(2):
                    srcb = v3(bvec).rearrange("p (j q) d -> p j q d", q=2)[:, :, hp, :]
                    srck = v3(ktl).rearrange("p (j q) d -> p j q d", q=2)[:, :, hp, :]
                    wl = v3(e1).rearrange("p (j q) d -> p j q d", q=2)[:, :, hp, :]
                    dstb = Btz[:].rearrange("p (j q) c -> p j q c", q=2)[:, :, hp, hp * 64:hp * 64 + 64]
                    dstk = Ktz[:].rearrange("p (j q) c -> p j q c", q=2)[:, :, hp, hp * 64:hp * 64 + 64]
                    dve(lambda e, srcb=srcb, wl=wl, dstb=dstb: e.tensor_tensor(out=dstb, in0=srcb, in1=wl, op=ALU.mult), ["bvec", "e1"], ["Btz"])
                    dve(lambda e, srck=srck, wl=wl, dstk=dstk: e.tensor_tensor(out=dstk, in0=srck, in1=wl, op=ALU.mult), ["ktl", "e1"], ["Ktz"])
                ps2, _, pk2 = nps()
                for j in range(4):
                    mm(ps2[:, j:j + 1], wsig[:, j * 128:(j + 1) * 128], ONES[:, 0:1], True, True, ["wsig", "pA"], [pk2])
                act(lambda e, ps2=ps2: e.activation(out=WLfm[:], in_=ps2[:, 0:4], func=AF.Exp, scale=-C0), [pk2], ["WLfm"])
                yield
                ps, psb, pk = nps()
                for w_ in range(4):
                    for j in range(4):
                        pe(lambda e, w_=w_, j=j, psb=psb: e.transpose(psb[:, (w_ * 4 + j) * 128:(w_ * 4 + j + 1) * 128], TMb[:, w_, j * 128:(j + 1) * 128], identb[:]), ["TMb%d" % w_, "identb"], [pk])
                for w_ in range(4):
                    eng_ = act if w_ % 2 == 0 else dve
                    if w_ % 2 == 0:
                        act(lambda e, psb=psb, w_=w_: e.copy(out=FMt[:, w_, :, :], in_=psb[:, w_ * 512:(w_ + 1) * 512].rearrange("p (j t) -> p j t", j=4)), [pk], ["FMt"])
                    else:
                        dve(lambda e, psb=psb, w_=w_: e.tensor_copy(out=FMt[:, w_, :, :], in_=psb[:, w_ * 512:(w_ + 1) * 512].rearrange("p (j t) -> p j t", j=4)), [pk], ["FMt"])
                KAP, RB, BB, KKB = 0, 1, 2, 3

                def amat(lw, rw_, dst, dk, mask, neg):
                    ps, psb, pk = nps()
                    for h in range(8):
                        j, hp = h // 2, h % 2
                        sl = slice(hp * 64, hp * 64 + 64)
                        mm(ps[:, hoff(h):hoff(h) + 128], FMt[sl, lw, j, :], FMt[sl, rw_, j, :], True, True, ["FMt"], [pk])
                    psv = ps[:, :].rearrange("p (q j t) -> p q j t", q=2, j=4)
                    dstv = dst[:].rearrange("p (j q) t -> p q j t", q=2)
                    mk = mask.unsqueeze(1).unsqueeze(1).to_broadcast([128, 2, 4, 128])
                    if neg:
                        mk3 = mask.unsqueeze(1).to_broadcast([128, 4, 128])
                        for q in range(2):
                            dve(lambda e, q=q: e.scalar_tensor_tensor(out=dstv[:, q], in0=psv[:, q], scalar=-1.0, in1=mk3, op0=ALU.mult, op1=ALU.mult), [pk, "pA"], [dk])
                    else:
                        mk3 = mask.unsqueeze(1).to_broadcast([128, 4, 128])
                        for q in range(2):
                            dve(lambda e, q=q: e.tensor_tensor(out=dstv[:, q], in0=psv[:, q], in1=mk3, op=ALU.mult), [pk, "pA"], [dk])

                amat(BB, KAP, Pw[1], "Pw1", MUS, True)
                amat(KAP, BB, Pw[0], "Pw0", MLS, True)
                amat(KKB, KAP, Am[0], "Am0", MUS, False)
                amat(BB, RB, Am[1], "Am1", MUI, False)
                amat(KKB, RB, Am[2], "Am2", MUI, False)
                yield
                ps, psb, pk = nps()
                for h in range(8):
                    j, hp = h // 2, h % 2
                    sl = slice(hp * 64, hp * 64 + 64)
                    mm(ps[:, h * 64:(h + 1) * 64], FMt[sl, KAP, j, :], Sbf[sl, j, :], True, False, ["FMt", "Sbf"], [pk])
                    mm(ps[:, h * 64:(h + 1) * 64], Am[0][:, h, :], vrw[:, h, :], False, True, ["Am0", "vrw"], [pk])
                act(lambda e, ps=ps: e.copy(out=Zb[0][:], in_=ps[:, 0:512].rearrange("p (h d) -> p h d", h=8)), [pk], ["Zb0"])
                pi = 0
                zi = 0
                for lvl in range(7):
                    yield
                    Pc, PTc = Pw[pi], Pw[pi + 1]
                    Pk, PTk = "Pw%d" % pi, "Pw%d" % (pi + 1)
                    Zc, Zn = Zb[zi], Zb[1 - zi]
                    ps, psb, pk = nps()
                    for h in range(8):
                        mm(ps[:, h * 64:(h + 1) * 64], identb[:], Zc[:, h, :], True, False, ["identb", "Zb%d" % zi], [pk])
                        mm(ps[:, h * 64:(h + 1) * 64], PTc[:, h, :], Zc[:, h, :], False, True, [PTk, "Zb%d" % zi], [pk])
                    if lvl < 6:
                        act(lambda e, ps=ps, Zn=Zn: e.copy(out=Zn[:], in_=ps[:, 0:512].rearrange("p (h d) -> p h d", h=8)), [pk], ["Zb%d" % (1 - zi)])
                        zi = 1 - zi
                        ni = 2 - pi
                        Pn, PTn = Pw[ni], Pw[ni + 1]
                        psA, _, pkA = nps()
                        for h in range(8):
                            mm(psA[:, h * 128:(h + 1) * 128], PTc[:, h, :], Pc[:, h, :], True, True, [PTk, Pk], [pkA])
                        for a_ in range(2):
                            dve(lambda e, psA=psA, Pn=Pn, a_=a_: e.tensor_copy(out=Pn[:, 4 * a_:4 * a_ + 4, :], in_=psA[:, a_ * 512:(a_ + 1) * 512].rearrange("p (h t) -> p h t", h=4)), [pkA], ["Pw%d" % ni])
                        psB, _, pkB = nps()
                        for h in range(8):
                            mm(psB[:, h * 128:(h + 1) * 128], Pc[:, h, :], PTc[:, h, :], True, True, [Pk, PTk], [pkB])
                        for a_ in range(2):
                            act(lambda e, psB=psB, PTn=PTn, a_=a_: e.copy(out=PTn[:, 4 * a_:4 * a_ + 4, :], in_=psB[:, a_ * 512:(a_ + 1) * 512].rearrange("p (h t) -> p h t", h=4)), [pkB], ["Pw%d" % (ni + 1)])
                        pi = ni
                    else:
                        act(lambda e, ps=ps: e.activation(out=Ub[:], in_=ps[:, 0:512].rearrange("p (h d) -> p h d", h=8), func=AF.Copy, scale=-1.0), [pk], ["Ub"])
                yield
                ps, psb, pk = nps()
                for h in range(8):
                    j, hp = h // 2, h % 2
                    sl = slice(hp * 64, hp * 64 + 64)
                    o_ = ps[:, h * 64:(h + 1) * 64]
                    mm(o_, Am[1][:, h, :], Ub[:, h, :], True, False, ["Am1", "Ub"], [pk])
                    mm(o_, Am[2][:, h, :], vrw[:, h, :], False, False, ["Am2", "vrw"], [pk])
                    mm(o_, FMt[sl, RB, j, :], Sbf[sl, j, :], False, True, ["FMt", "Sbf"], [pk])
                act(lambda e, ps=ps: e.copy(out=ysb[:], in_=ps[:, 0:512]), [pk], ["ysb"])
                yield
                ps, psb, pk = nps()
                for j in range(4):
                    o_ = ps[:, j * 64:(j + 1) * 64]
                    mm(o_, Btz[:, 2 * j, :], Ub[:, 2 * j, :], True, False, ["Btz", "Ub"], [pk])
                    mm(o_, Ktz[:, 2 * j, :], vrw[:, 2 * j, :], False, False, ["Ktz", "vrw"], [pk])
                    mm(o_, Btz[:, 2 * j + 1, :], Ub[:, 2 * j + 1, :], False, False, ["Btz", "Ub"], [pk])
                    mm(o_, Ktz[:, 2 * j + 1, :], vrw[:, 2 * j + 1, :], False, True, ["Ktz", "vrw"], [pk])
                dve(lambda e: e.tensor_tensor(out=Sst[:], in0=Sst[:], in1=WLfm[:].unsqueeze(2).to_broadcast([128, 4, 64]), op=ALU.mult), ["Sst", "WLfm"], ["Sst"])
                dve(lambda e, ps=ps: e.tensor_tensor(out=Sst[:], in0=Sst[:], in1=ps[:, 0:256].rearrange("p (j d) -> p j d", j=4), op=ALU.add), [pk, "Sst"], ["Sst"])
                act(lambda e: e.copy(out=Sbf[:], in_=Sst[:]), ["Sst"], ["Sbf"])
                yield
            def tail():
                head_rw(128, ysb, "ysb", bon, "bon", vf, "vf", g_sb, "g_sb", mix, "mix", W)
                out_proj(128, mix, "mix", None, None, b * 128, W)

            return front_stage, ml_stage, rw_stage, tail, p_

        def run_gens(gl):
            gl = list(gl)
            while gl:
                for item in list(gl):
                    CURP[0] = item[1]
                    CURPOOL[0] = item[3] if len(item) > 3 else "A"
                    for _ in range(item[2] if len(item) > 2 else 1):
                        try:
                            next(item[0])
                        except StopIteration:
                            gl.remove(item)
                            break

        blocks = [make_block(b) for b in range(NB)]
        run_gens([(blocks[0][0](), 0, 1, "A")])
        for b in range(NB):
            fr, ml_, rw_, tl, p_ = blocks[b]
            gl = [(rw_(), p_, 1, "A"), (ml_(), p_, 1, "A")]
            if b + 1 < NB:
                gl.append((blocks[b + 1][0](), (b + 1) % 2, 1, "A"))
            run_gens(gl)
            CURP[0] = p_
            CURPOOL[0] = "A"
            tl()
        CURP[0] = 0

        ps, psb, pk = nps()
        mm(ps[0:8, 0:128], runmax[:], IDF, True, True, ["runmax", "pA"], [pk])
        mm(ps[0:8, 128:256], nBc[:], IDF, True, True, ["nBc", "pA"], [pk])
        fs = TT("fs", [8, 16])
        dve(lambda e, ps=ps: e.tensor_reduce(out=fs[:, 0:1], in_=ps[0:8, 0:128], axis=AX.X, op=ALU.max), [pk], ["fs"])
        dve(lambda e: e.tensor_scalar_max(out=fs[:, 0:1], in0=fs[:, 0:1], scalar1=0.0), ["fs"], ["fs"])
        dve(lambda e, ps=ps: e.tensor_tensor(out=fs[:, 1:2], in0=fs[:, 0:1], in1=ps[0:8, 128:129], op=ALU.subtract), [pk, "fs"], ["fs"])
        dma("pool", om, fs[:, 1:2], ["fs"], [], "fin")
        act(lambda e: e.activation(out=fs[:, 2:3], in_=fs[:, 1:2], func=AF.Exp, scale=-1.0), ["fs"], ["fs"])
        dve(lambda e: e.tensor_scalar_mul(out=fs[:, 4:8], in0=pA[0:8, OFF["rsel"][0]:OFF["rsel"][1]], scalar1=fs[:, 2:3]), ["fs", "pA"], ["fs"])
        ps, psb, pk = nps()
        mm(ps[:, 0:4], pA[0:8, OFF["lsel"][0]:OFF["lsel"][1]], fs[:, 4:8], True, True, ["pA", "fs"], [pk])
        scb = TT("scb", [128, 4])
        act(lambda e, ps=ps: e.copy(out=scb[:], in_=ps[:, 0:4]), [pk], ["scb"])
        dve(lambda e: e.tensor_tensor(out=Cst[:], in0=Cst[:], in1=scb[:].unsqueeze(2).to_broadcast([128, 4, 65]), op=ALU.mult), ["Cst", "scb"], ["Cst"])
        dma("pool", oC, Cst[:], ["Cst"], [], "fin")
        dma("pool", oS, Sst[:], ["Sst"], [], "fin")
        PP[0].finalize()
        PP[0] = Prog(ctx)

    with contextlib.ExitStack() as es_s:
        cur[0] = es_s
        if do_sample:
            W = {}
            W["xm"] = TT("s_xm", [128, D])
            junk = W["xm"]
            st = TT("s_st", [128, 4])
            mix = TT("s_mix", [128, D], BF16)
            xsb = mix
            lor = TT("s_lor", [128, 256], BF16)
            lorT = TT("s_lorT", [128, 2, 128], BF16)
            W["mixT"] = TT("s_mixT", [128, 8, 128], BF16)
            sx = TT("sx", [NS, D])
            sxT = TT("sxT", [128, 8, NS], BF16)
            spj = TT("spj", [NS, INW])
            spk = TT("spk", [128, NSP])
            sl_t = TT("sl_t", [NS, 3, 256])
            Cs = TT("Cs", [128, 4096])
            Ss = Cs
            sn_t = TT("sn_t", [128, 64])
            sm_t = TT("sm_t", [128, 1])
            scv = TT("scv", [128, 2, 4, 64])
            ssh = TT("ssh", [128, 3, 64])
            dma("sp", sx[:], xs, [], ["sx"], "sin", True)
            dma("sp", spk[:], spk_d, [], ["spk"], "sin", True)
            dma("sp", sl_t[:, 0, :], sshl_d, [], ["sl_t"], "sin", True)
            dma("sp", sl_t[:, 1, :], mul_d, [], ["sl_t"], "sin", True)
            dma("sp", Cs[:], sC_d, [], ["Cs"], "sin", True)
            dma("sp", sn_t[:], sn_d, [], ["sn_t"], "sin", True)
            dma("sp", sm_t[:], sm_d, [], ["sm_t"], "sin", True)
            dma("sp", scv[:, :, 0:3, :], sconv_d, [], ["scv"], "sin", True)
            dma("sp", ssh[:], sshift_d, [], ["ssh"], "sin", True)
            rmsnorm_T(sx, "sx", NS, sxT, "sxT", 0, "nmw", xsb, "xsb", junk, "junk", st, "st")
            for n0 in range(0, INW, 512):
                nn = min(512, INW - n0)
                ps, psb, pk = nps()
                for c in range(8):
                    mm(ps[:NS, 0:nn], sxT[:, c, :], wq[:, c, n0:n0 + nn], c == 0, c == 7, ["sxT", "wq"], [pk])
                act(lambda e, ps=ps, n0=n0, nn=nn: e.copy(out=spj[:, n0:n0 + nn], in_=ps[:NS, 0:nn]), [pk], ["spj"])
            s1v = scr1.rearrange("(b h) a d -> b a h d", b=NS)
            for a_ in range(7):
                c0_ = a_ * 512 if a_ < 4 else MLW + (a_ - 4) * 512
                dma("pool", s1v[:, a_, :, :], spj[:, c0_:c0_ + 512].rearrange("p (h d) -> p h d", h=8), ["spj"], ["scr1"], "scrw1")
            A7 = TT("A7", [128, 8, 64])
            dma("sp", A7[:, 0:7, :], scr1[:, 0:7, :], ["scr1"], ["A7"], "scrr1")
            gif = TT("gif", [128, 2])
            s_if = nc.dram_tensor("scr_if", [2, 128], F32, kind="Internal").ap()
            for g_ in range(2):
                dma("pool", s_if[g_, :].rearrange("(b h) -> b h", b=NS), spj[:, 2048 + 8 * g_:2056 + 8 * g_], ["spj"], ["scr_if"], "scrwif")
            for g_ in range(2):
                dma("sp", gif[:, g_:g_ + 1], s_if[g_, :].rearrange("(p o) -> p o", o=1), ["scr_if"], ["gif"], "scrrif")
            SP_ = lambda n: spk[:, SOFF[n][0]:SOFF[n][1]]
            pl = spj[:, MLW + 1536:MLW + 1792]
            dma("pool", osshl, pl, ["spj"], [], "fin")
            dve(lambda e: e.tensor_tensor(out=sl_t[:, 2, :], in0=sl_t[:, 0, :], in1=pl, op=ALU.subtract), ["sl_t", "spj"], ["sl_t"])
            dve(lambda e: e.tensor_tensor(out=sl_t[:, 2, :], in0=sl_t[:, 2, :], in1=sl_t[:, 1, :], op=ALU.mult), ["sl_t"], ["sl_t"])
            dve(lambda e: e.tensor_tensor(out=sl_t[:, 2, :], in0=sl_t[:, 2, :], in1=pl, op=ALU.add), ["sl_t", "spj"], ["sl_t"])
            act(lambda e: e.activation(out=lor[:NS, 0:64], in_=sl_t[:, 2, 0:64], func=AF.Tanh), ["sl_t"], ["lor"])
            act(lambda e: e.copy(out=lor[:NS, 64:128], in_=sl_t[:, 2, 64:128]), ["sl_t"], ["lor"])
            act(lambda e: e.activation(out=lor[:NS, 128:256], in_=sl_t[:, 2, 128:256], func=AF.Sigmoid), ["sl_t"], ["lor"])
            ps, psb, pk = nps()
            pe(lambda e, psb=psb: e.transpose(psb[:, 0:NS], lor[:NS, 0:128], identb[:NS, :NS]), ["lor", "identb"], [pk])
            pe(lambda e, psb=psb: e.transpose(psb[:, 128:128 + NS], lor[:NS, 128:256], identb[:NS, :NS]), ["lor", "identb"], [pk])
            act(lambda e, psb=psb: e.copy(out=lorT[:, :, 0:NS], in_=psb[:, 0:256].rearrange("p (a t) -> p a t", a=2)[:, :, 0:NS]), [pk], ["lorT"])
            ps, psb, pk = nps()
            mm(ps[:NS, 0:512], lorT[0:64, 0, 0:NS], luw[0:64, :], True, True, ["lorT", "luw"], [pk])
            mm(ps[:NS, 512:1024], lorT[64:128, 0, 0:NS], luw[64:128, :], True, True, ["lorT", "luw"], [pk])
            ps2, _, pk2 = nps()
            mm(ps2[:NS, 0:512], lorT[:, 1, 0:NS], gup[:, :], True, True, ["lorT", "gup"], [pk2])
            lo3 = spj[:, 0:1536].rearrange("p (a n) -> p a n", a=3)
            for a_ in range(2):
                act(lambda e, ps=ps, a_=a_: e.copy(out=lo3[:, a_, :], in_=ps[:NS, a_ * 512:(a_ + 1) * 512]), [pk], ["lo3"])
            act(lambda e, ps2=ps2: e.copy(out=lo3[:, 2, :], in_=ps2[:NS, 0:512]), [pk2], ["lo3"])
            for a_ in range(3):
                dma("pool", scr2.rearrange("(b h) a d -> b a h d", b=NS)[:, a_, :, :], lo3[:, a_, :].rearrange("p (h d) -> p h d", h=8), ["lo3"], ["scr2"], "scrw2")
            L3 = TT("L3", [128, 3, 64])
            dma("sp", L3[:], scr2, ["scr2"], ["L3"], "scrr2")
            big = TT("big", [128, 4096])
            sv = TT("sv", [128, 64])
            pool(lambda e: e.tensor_copy(out=scv[:, :, 3, :], in_=A7[:, 0:2, :]), ["A7"], ["scv"])
            dma("pool", osconv, scv[:, :, 1:4, :], ["scv"], [], "fin")
            qk_s = TT("qk_s", [128, 2, 64])
            cwqk = lambda w_: spk[:, SOFF["cwq"][0] + w_ * 256:SOFF["cwq"][0] + (w_ + 1) * 256].rearrange("p (j d) -> p j d", j=4)
            for w_ in range(2):
                dve(lambda e, w_=w_: e.tensor_tensor(out=big[:, 0:256].rearrange("p (j d) -> p j d", j=4), in0=scv[:, w_, :, :], in1=cwqk(w_), op=ALU.mult), ["scv", "spk"], ["big"])
                dve(lambda e, w_=w_: e.tensor_reduce(out=qk_s[:, w_, :], in_=big[:, 0:256].rearrange("p (j d) -> p d j", j=4), axis=AX.X, op=ALU.add), ["big"], ["qk_s"])
            dve(lambda e: e.tensor_tensor(out=qk_s[:], in0=qk_s[:], in1=spk[:, SOFF["cbq"][0]:SOFF["cbk"][1]].rearrange("p (a d) -> p a d", a=2), op=ALU.add), ["qk_s", "spk"], ["qk_s"])
            act(lambda e: e.activation(out=qk_s[:], in_=qk_s[:], func=AF.Silu), ["qk_s"], ["qk_s"])
            act(lambda e: e.activation(out=qk_s[:, 1, :], in_=qk_s[:, 1, :], func=AF.Copy, scale=0.125), ["qk_s"], ["qk_s"])
            dve(lambda e: e.tensor_tensor(out=sv[:, 0:2], in0=gif[:], in1=spk[:, SOFF["ib"][0]:SOFF["fb"][1]], op=ALU.add), ["gif", "spk"], ["sv"])
            act(lambda e: e.activation(out=sv[:, 9:10], in_=sv[:, 1:2], func=AF.Exp, scale=-1.0), ["sv"], ["sv"])
            act(lambda e: e.activation(out=sv[:, 2:3], in_=sv[:, 9:10], func=AF.Ln, bias=1.0, scale=1.0), ["sv"], ["sv"])
            dve(lambda e: e.tensor_tensor(out=sv[:, 3:4], in0=sm_t[:], in1=sv[:, 2:3], op=ALU.subtract), ["sv", "sm_t"], ["sv"])
            dve(lambda e: e.tensor_tensor(out=sv[:, 4:5], in0=sv[:, 3:4], in1=sv[:, 0:1], op=ALU.max), ["sv"], ["sv"])
            dma("pool", osm, sv[:, 4:5], ["sv"], [], "fin")
            dve(lambda e: e.tensor_tensor(out=sv[:, 9:10], in0=sv[:, 0:1], in1=sv[:, 4:5], op=ALU.subtract), ["sv"], ["sv"])
            act(lambda e: e.activation(out=sv[:, 5:6], in_=sv[:, 9:10], func=AF.Exp), ["sv"], ["sv"])
            dve(lambda e: e.tensor_tensor(out=sv[:, 9:10], in0=sv[:, 3:4], in1=sv[:, 4:5], op=ALU.subtract), ["sv"], ["sv"])
            act(lambda e: e.activation(out=sv[:, 6:7], in_=sv[:, 9:10], func=AF.Exp), ["sv"], ["sv"])
            act(lambda e: e.activation(out=sv[:, 7:8], in_=sv[:, 4:5], func=AF.Exp, scale=-1.0), ["sv"], ["sv"])
            q_ = qk_s[:, 0, :]
            k_ = qk_s[:, 1, :]
            v_ = A7[:, 2, :]
            b3 = lambda t: t[:, :].rearrange("p (a c) -> p a c", a=64)
            pool(lambda e: e.tensor_tensor(out=b3(big), in0=k_.unsqueeze(2).to_broadcast([128, 64, 64]), in1=v_.unsqueeze(1).to_broadcast([128, 64, 64]), op=ALU.mult), ["qk_s", "A7"], ["big"])
            dve(lambda e: e.tensor_scalar_mul(out=Cs[:], in0=Cs[:], scalar1=sv[:, 6:7]), ["Cs", "sv"], ["Cs"])
            dve(lambda e: e.scalar_tensor_tensor(out=Cs[:], in0=big[:], scalar=sv[:, 5:6], in1=Cs[:], op0=ALU.mult, op1=ALU.add), ["big", "sv", "Cs"], ["Cs"])
            dma("pool", osC, Cs[:], ["Cs"], [], "fin")
            dve(lambda e: e.tensor_scalar_mul(out=sn_t[:], in0=sn_t[:], scalar1=sv[:, 6:7]), ["sn_t", "sv"], ["sn_t"])
            dve(lambda e: e.scalar_tensor_tensor(out=sn_t[:], in0=k_, scalar=sv[:, 5:6], in1=sn_t[:], op0=ALU.mult, op1=ALU.add), ["qk_s", "sv", "sn_t"], ["sn_t"])
            dma("pool", osn, sn_t[:], ["sn_t"], [], "fin")
            pool(lambda e: e.tensor_tensor(out=b3(big), in0=Cs[:, :].rearrange("p (k v) -> p v k", k=64), in1=q_.unsqueeze(1).to_broadcast([128, 64, 64]), op=ALU.mult), ["Cs", "qk_s"], ["big"])
            hs = TT("hs", [128, 2, 64])
            dve(lambda e: e.tensor_reduce(out=hs[:, 0, :], in_=b3(big), axis=AX.X, op=ALU.add), ["big"], ["hs"])
            dve(lambda e: e.tensor_tensor(out=sv[:, 16:80 - 16] if False else big[:, 0:64], in0=q_, in1=sn_t[:], op=ALU.mult), ["qk_s", "sn_t"], ["big"])
            dve(lambda e: e.tensor_reduce(out=sv[:, 8:9], in_=big[:, 0:64], axis=AX.X, op=ALU.add), ["big"], ["sv"])
            dve(lambda e: e.scalar_tensor_tensor(out=sv[:, 9:10], in0=sv[:, 8:9], scalar=-1.0, in1=sv[:, 8:9], op0=ALU.mult, op1=ALU.max), ["sv"], ["sv"])
            dve(lambda e: e.tensor_tensor(out=sv[:, 9:10], in0=sv[:, 9:10], in1=sv[:, 7:8], op=ALU.max), ["sv"], ["sv"])
            dve(lambda e: e.reciprocal(out=sv[:, 10:11], in_=sv[:, 9:10]), ["sv"], ["sv"])
            dve(lambda e: e.tensor_scalar_mul(out=hs[:, 0, :], in0=hs[:, 0, :], scalar1=sv[:, 10:11]), ["hs", "sv"], ["hs"])
            dma("pool", osshift, A7[:, 4:7, :], ["A7"], [], "fin")
            rk3 = TT("rk3", [128, 3, 64])
            mu3 = spk[:, SOFF["mu_r"][0]:SOFF["mu_v"][1]].rearrange("p (a d) -> p a d", a=3)
            dve(lambda e: e.tensor_tensor(out=rk3[:], in0=ssh[:], in1=A7[:, 4:7, :], op=ALU.subtract), ["ssh", "A7"], ["rk3"])
            dve(lambda e: e.tensor_tensor(out=rk3[:], in0=rk3[:], in1=mu3, op=ALU.mult), ["rk3", "spk"], ["rk3"])
            dve(lambda e: e.tensor_tensor(out=rk3[:], in0=rk3[:], in1=A7[:, 4:7, :], op=ALU.add), ["rk3", "A7"], ["rk3"])
            w8 = TT("w8", [128, 8, 64])
            dve(lambda e: e.tensor_tensor(out=w8[:, 0, :], in0=L3[:, 0, :], in1=SP_("w0"), op=ALU.add), ["L3", "spk"], ["w8"])
            act(lambda e: e.activation(out=w8[:, 0, :], in_=w8[:, 0, :], func=AF.Sigmoid), ["w8"], ["w8"])
            act(lambda e: e.activation(out=w8[:, 0, :], in_=w8[:, 0, :], func=AF.Exp, scale=-C0), ["w8"], ["w8"])
            dve(lambda e: e.tensor_tensor(out=w8[:, 1, :], in0=L3[:, 1, :], in1=SP_("a0"), op=ALU.add), ["L3", "spk"], ["w8"])
            act(lambda e: e.activation(out=w8[:, 1, :], in_=w8[:, 1, :], func=AF.Sigmoid), ["w8"], ["w8"])
            dve(lambda e: e.tensor_tensor(out=w8[:, 6, :], in0=rk3[:, 1, :], in1=SP_("kk"), op=ALU.mult), ["rk3", "spk"], ["w8"])
            dve(lambda e: e.tensor_tensor(out=w8[:, 7, :], in0=w8[:, 6, :], in1=w8[:, 6, :], op=ALU.mult), ["w8"], ["w8"])
            dve(lambda e: e.tensor_reduce(out=sv[:, 11:12], in_=w8[:, 7, :], axis=AX.X, op=ALU.add), ["w8"], ["sv"])
            dve(lambda e: e.tensor_scalar_max(out=sv[:, 11:12], in0=sv[:, 11:12], scalar1=1e-24), ["sv"], ["sv"])
            act(lambda e: e.activation(out=sv[:, 11:12], in_=sv[:, 11:12], func=AF.Sqrt), ["sv"], ["sv"])
            dve(lambda e: e.reciprocal(out=sv[:, 11:12], in_=sv[:, 11:12]), ["sv"], ["sv"])
            dve(lambda e: e.tensor_scalar_mul(out=w8[:, 3, :], in0=w8[:, 6, :], scalar1=sv[:, 11:12]), ["w8", "sv"], ["w8"])
            dve(lambda e: e.tensor_tensor(out=w8[:, 6, :], in0=w8[:, 1, :], in1=SP_("ka"), op=ALU.mult), ["w8", "spk"], ["w8"])
            dve(lambda e: e.tensor_tensor(out=w8[:, 6, :], in0=w8[:, 6, :], in1=SP_("ka"), op=ALU.subtract), ["w8", "spk"], ["w8"])
            dve(lambda e: e.tensor_scalar_add(out=w8[:, 6, :], in0=w8[:, 6, :], scalar1=1.0), ["w8"], ["w8"])
            dve(lambda e: e.tensor_tensor(out=w8[:, 4, :], in0=rk3[:, 1, :], in1=w8[:, 6, :], op=ALU.mult), ["rk3", "w8"], ["w8"])
            dve(lambda e: e.tensor_tensor(out=w8[:, 5, :], in0=w8[:, 1, :], in1=w8[:, 3, :], op=ALU.mult), ["w8"], ["w8"])
            dma("sp", Ss[:], sS_d, [], ["Ss"], "sin2")
            bk = lambda ap: ap.unsqueeze(1).to_broadcast([128, 64, 64])
            bv = lambda ap: ap.unsqueeze(2).to_broadcast([128, 64, 64])
            pool(lambda e: e.tensor_tensor(out=b3(big), in0=b3(Ss), in1=bk(w8[:, 3, :]), op=ALU.mult), ["Ss", "w8"], ["big"])
            dve(lambda e: e.tensor_reduce(out=w8[:, 7, :], in_=b3(big), axis=AX.X, op=ALU.add), ["big"], ["w8"])
            dve(lambda e: e.tensor_tensor(out=b3(Ss), in0=b3(Ss), in1=bk(w8[:, 0, :]), op=ALU.mult), ["Ss", "w8"], ["Ss"])
            pool(lambda e: e.tensor_tensor(out=b3(big), in0=bv(w8[:, 7, :]), in1=bk(w8[:, 5, :]), op=ALU.mult), ["w8"], ["big"])
            dve(lambda e: e.tensor_tensor(out=Ss[:], in0=Ss[:], in1=big[:], op=ALU.subtract), ["Ss", "big"], ["Ss"])
            pool(lambda e: e.tensor_tensor(out=b3(big), in0=bv(rk3[:, 2, :]), in1=bk(w8[:, 4, :]), op=ALU.mult), ["rk3", "w8"], ["big"])
            dve(lambda e: e.tensor_tensor(out=Ss[:], in0=Ss[:], in1=big[:], op=ALU.add), ["Ss", "big"], ["Ss"])
            dma("pool", osS, Ss[:], ["Ss"], [], "fin")
            pool(lambda e: e.tensor_tensor(out=b3(big), in0=b3(Ss), in1=bk(rk3[:, 0, :]), op=ALU.mult), ["Ss", "rk3"], ["big"])
            dve(lambda e: e.tensor_reduce(out=hs[:, 1, :], in_=b3(big), axis=AX.X, op=ALU.add), ["big"], ["hs"])
            dve(lambda e: e.tensor_tensor(out=w8[:, 6, :], in0=rk3[:, 0, :], in1=w8[:, 4, :], op=ALU.mult), ["rk3", "w8"], ["w8"])
            dve(lambda e: e.tensor_tensor(out=w8[:, 6, :], in0=w8[:, 6, :], in1=SP_("rk"), op=ALU.mult), ["w8", "spk"], ["w8"])
            dve(lambda e: e.tensor_reduce(out=sv[:, 12:13], in_=w8[:, 6, :], axis=AX.X, op=ALU.add), ["w8"], ["sv"])
            dve(lambda e: e.scalar_tensor_tensor(out=hs[:, 1, :], in0=rk3[:, 2, :], scalar=sv[:, 12:13], in1=hs[:, 1, :], op0=ALU.mult, op1=ALU.add), ["rk3", "sv", "hs"], ["hs"])
            act(lambda e: e.activation(out=w8[:, 6, :], in_=A7[:, 3, :], func=AF.Sigmoid), ["A7"], ["w8"])
            dve(lambda e: e.tensor_tensor(out=hs[:, 0, :], in0=hs[:, 0, :], in1=w8[:, 6, :], op=ALU.mult), ["hs", "w8"], ["hs"])
            dve(lambda e: e.tensor_tensor(out=w8[:, 7, :], in0=hs[:, 0, :], in1=hs[:, 0, :], op=ALU.mult), ["hs"], ["w8"])
            dve(lambda e: e.tensor_reduce(out=sv[:, 13:14], in_=w8[:, 7, :], axis=AX.X, op=ALU.add), ["w8"], ["sv"])
            act(lambda e: e.activation(out=sv[:, 13:14], in_=sv[:, 13:14], func=AF.Sqrt, bias=EPS, scale=1.0 / 64), ["sv"], ["sv"])
            dve(lambda e: e.reciprocal(out=sv[:, 13:14], in_=sv[:, 13:14]), ["sv"], ["sv"])
            dve(lambda e: e.scalar_tensor_tensor(out=hs[:, 0, :], in0=hs[:, 0, :], scalar=sv[:, 13:14], in1=SP_("mnw"), op0=ALU.mult, op1=ALU.mult), ["hs", "sv", "spk"], ["hs"])
            dve(lambda e: e.tensor_reduce(out=sv[:, 14:15], in_=hs[:, 1, :], axis=AX.X, op=ALU.add), ["hs"], ["sv"])
            dve(lambda e: e.tensor_scalar_mul(out=sv[:, 14:15], in0=sv[:, 14:15], scalar1=1.0 / 64), ["sv"], ["sv"])
            dve(lambda e: e.tensor_scalar_sub(out=hs[:, 1, :], in0=hs[:, 1, :], scalar1=sv[:, 14:15]), ["hs", "sv"], ["hs"])
            dve(lambda e: e.tensor_tensor(out=w8[:, 7, :], in0=hs[:, 1, :], in1=hs[:, 1, :], op=ALU.mult), ["hs"], ["w8"])
            dve(lambda e: e.tensor_reduce(out=sv[:, 15:16], in_=w8[:, 7, :], axis=AX.X, op=ALU.add), ["w8"], ["sv"])
            act(lambda e: e.activation(out=sv[:, 15:16], in_=sv[:, 15:16], func=AF.Sqrt, bias=GN_EPS, scale=1.0 / 64), ["sv"], ["sv"])
            dve(lambda e: e.reciprocal(out=sv[:, 15:16], in_=sv[:, 15:16]), ["sv"], ["sv"])
            dve(lambda e: e.scalar_tensor_tensor(out=hs[:, 1, :], in0=hs[:, 1, :], scalar=sv[:, 15:16], in1=SP_("lnw"), op0=ALU.mult, op1=ALU.mult), ["hs", "sv", "spk"], ["hs"])
            dve(lambda e: e.tensor_tensor(out=hs[:, 1, :], in0=hs[:, 1, :], in1=SP_("lnb"), op=ALU.add), ["hs", "spk"], ["hs"])
            dve(lambda e: e.tensor_tensor(out=hs[:, 1, :], in0=hs[:, 1, :], in1=L3[:, 2, :], op=ALU.mult), ["hs", "L3"], ["hs"])
            s3v = nc.dram_tensor("scr3b", [128, 2, 64], F32, kind="Internal").ap()
            dma("pool", s3v, hs[:], ["hs"], ["scr3b"], "scrw3")
            smix = spj[:, 2304:3328].rearrange("p (a h d) -> p a h d", a=2, h=8)
            for a_ in range(2):
                dma("sp", smix[:, a_, :, :], s3v.rearrange("(b h) a d -> b a h d", b=NS)[:, a_, :, :], ["scr3b"], ["smix"], "scrr3")
            act(lambda e: e.copy(out=mix[:NS, :], in_=smix[:].rearrange("p a h d -> p (a h d)")), ["smix"], ["mix"])
            out_proj(NS, mix, "mix", sx, "sx", T, W)
        PP[0].finalize()
        PP[0] = Prog(ctx)
    es_res.close()

    with contextlib.ExitStack() as es2:
        cur[0] = es2
        upb = TT("upb", [128, 8, DFF], BF16)
        dnb = TT("dnb", [128, 32, D], BF16)
        pB2 = TT("pB2", [128, 136], F32)
        nfw = TT("nfw", [128, D])
        identb2 = TT("identb2", [128, 128], BF16)
        wout = TT("wout", [128, 8, D], BF16)
        mixin = TT("mixin", [128, D], BF16)
        for c in range(0, 8, 4):
            dma("pool", wout[:, c:c + 4, :], w_out_v[:, c:c + 4, :], [], ["wout"], "wout")
        up_v = mlp_up.rearrange("(c p) n -> p c n", p=128)
        dn_v = mlp_down.rearrange("(c p) n -> p c n", p=128)
        dma("sp", pB2[:, 0:128], packA_d[:, OFF["ident"][0]:OFF["ident"][1]], [], ["pB2"], "init2", True)
        dma("sp", pB2[:, 128:136], packA_d[:, OFF["nmlp"][0]:OFF["nmlp"][1]], [], ["pB2"], "init2", True)
        dma("sp", nfw[:], nfw_d, [], ["nfw"], "init2", True)
        for g8 in range(8):
            dma("pool", upb[:, :, g8 * 512:(g8 + 1) * 512], up_v[:, :, g8 * 512:(g8 + 1) * 512], [], ["upb%d" % g8], "up%d" % g8)
        for g8 in range(8):
            dma("pool", dnb[:, g8 * 4:(g8 + 1) * 4, :], dn_v[:, g8 * 4:(g8 + 1) * 4, :], [], ["dnb%d" % g8], "dn%d" % g8)
        dve(lambda e: e.tensor_copy(out=identb2[:], in_=pB2[:, 0:128]), ["pB2"], ["identb2"])
        NSUB = 2
        NTT = NSUB * 128
        xsb2 = TT("xsb2", [128, D], BF16)
        xb = [TT("xb%d" % i, [128, NSUB, D]) for i in range(2)]
        st2 = TT("st2", [128, 8])
        xn2 = [TT("xn2T%d" % i, [128, 8, NTT], BF16) for i in range(2)]
        hT = TT("hT", [128, 32, NTT], BF16)
        junk2 = TT("junk2", [128, D], BF16)
        rl = [TT("rl%d" % i, [128, 512]) for i in range(2)]
        nmlp = pB2[:, 128:136]
        xm_v = xp.rearrange("(s p) d -> p s d", p=128)
        yp_v = yp.rearrange("(s p) d -> p s d", p=128)
        sbs = [(sb * NSUB, NSUB, 128) for sb in range(NB // NSUB)] + [(NB, 1, NS)]

        def front(i):
            s0, nsub, nt = sbs[i]
            x4 = xb[i % 2]
            xk = "xb%d" % (i % 2)
            xn2T = xn2[i % 2]
            xnk = "xn2T%d" % (i % 2)
            if nsub == NSUB:
                dma("sp", x4[:], xm_v[:, s0:s0 + NSUB, :], [], [xk], xk)
            else:
                dma("sp", x4[:nt, 0, :], xs, [], [xk], xk)
            for si in range(nsub):
                r0_ = (s0 + si) * 128 if nsub == NSUB else T
                dma("sp", mixin[:nt, :], mix_d[r0_:r0_ + nt, :], [], ["mixin"], "mixin")
                ps, psb, pk = nps()
                for c in range(8):
                    pe(lambda e, c=c, psb=psb: e.transpose(psb[:, c * 128:c * 128 + nt], mixin[:nt, c * 128:(c + 1) * 128], identb2[:nt, :nt]), ["mixin", "identb2"], [pk])
                act(lambda e, psb=psb, si=si: e.copy(out=xn2T[:, :, si * 128:si * 128 + nt], in_=psb[:, 0:1024].rearrange("p (c t) -> p c t", c=8)[:, :, 0:nt]), [pk], [xnk])
                yield
                ps, psb, pk = nps()
                for n in range(2):
                    for c in range(8):
                        mm(ps[:nt, n * 512:(n + 1) * 512], xn2T[:, c, si * 128:si * 128 + nt], wout[:, c, n * 512:(n + 1) * 512], c == 0, c == 7, [xnk, "wout"], [pk])
                for a_ in range(2):
                    dve(lambda e, ps=ps, si=si, a_=a_: e.tensor_tensor(out=x4[:nt, si, a_ * 512:(a_ + 1) * 512], in0=ps[:nt, a_ * 512:(a_ + 1) * 512], in1=x4[:nt, si, a_ * 512:(a_ + 1) * 512], op=ALU.add), [pk, xk], [xk])
                act(lambda e, si=si: e.activation(out=junk2[:nt, :], in_=x4[:nt, si, :], func=AF.Square, accum_out=st2[:nt, 0:1]), [xk], ["junk2", "st2"])
                act(lambda e: e.activation(out=st2[:nt, 1:2], in_=st2[:nt, 0:1], func=AF.Sqrt, bias=EPS, scale=1.0 / D), ["st2"], ["st2"])
                dve(lambda e: e.reciprocal(out=st2[:nt, 2:3], in_=st2[:nt, 1:2]), ["st2"], ["st2"])
                dve(lambda e, si=si: e.tensor_scalar_mul(out=xsb2[:nt, :], in0=x4[:nt, si, :], scalar1=st2[:nt, 2:3]), [xk, "st2"], ["xsb2"])
                yield
                yield
                ps, psb, pk = nps()
                for c in range(8):
                    pe(lambda e, c=c, psb=psb: e.transpose(psb[:, c * 128:c * 128 + nt], xsb2[:nt, c * 128:(c + 1) * 128], identb2[:nt, :nt]), ["xsb2", "identb2"], [pk])
                dve(lambda e, psb=psb, si=si: e.tensor_tensor(out=xn2T[:, :, si * 128:si * 128 + nt], in0=psb[:, 0:1024].rearrange("p (c t) -> p c t", c=8)[:, :, 0:nt],
                                                             in1=nmlp.unsqueeze(2).to_broadcast([128, 8, nt]), op=ALU.mult), [pk, "pB2"], [xnk])
                yield

        def up(i):
            s0, nsub, nt = sbs[i]
            ntt = nsub * nt if nsub == NSUB else nt
            xn2T = xn2[i % 2]
            xnk = "xn2T%d" % (i % 2)
            per = 512 // NTT
            for j2 in range(32 // (2 * per)):
                ps, psb, pk = nps()
                for jj in range(2 * per):
                    j = j2 * 2 * per + jj
                    for c in range(8):
                        mm(ps[:, jj * NTT:jj * NTT + ntt], upb[:, c, j * 128:(j + 1) * 128], xn2T[:, c, 0:ntt], c == 0, c == 7, ["upb%d" % (j // 4), xnk], [pk])
                for bk in range(2):
                    r_ = rl[bk]
                    rk_ = "rl%d" % bk
                    j0 = j2 * 2 * per + bk * per
                    psv = ps[:, bk * 512:(bk + 1) * 512].rearrange("p (j t) -> p j t", j=per)[:, :, 0:ntt]
                    rv = r_[:, :].rearrange("p (j t) -> p j t", j=per)[:, :, 0:ntt]
                    act(lambda e, psv=psv, rv=rv: e.activation(out=rv, in_=psv, func=AF.Relu), [pk], [rk_])
                    if bk == 0:
                        dve(lambda e, rv=rv, j0=j0: e.tensor_tensor(out=hT[:, j0:j0 + per, 0:ntt], in0=rv, in1=rv, op=ALU.mult), [rk_], ["hT"])
                    else:
                        pool(lambda e, rv=rv, j0=j0: e.tensor_tensor(out=hT[:, j0:j0 + per, 0:ntt], in0=rv, in1=rv, op=ALU.mult), [rk_], ["hT"])
                yield

        def down(i):
            s0, nsub, nt = sbs[i]
            x4 = xb[i % 2]
            xk = "xb%d" % (i % 2)
            for si in range(nsub):
                ps, psb, pk = nps()
                for n in range(2):
                    for j in range(32):
                        mm(ps[:nt, n * 512:(n + 1) * 512], hT[:, j, si * 128:si * 128 + nt], dnb[:, j, n * 512:(n + 1) * 512], j == 0, j == 31, ["hT", "dnb%d" % (j // 4)], [pk])
                for a_ in range(2):
                    dve(lambda e, ps=ps, si=si, a_=a_: e.tensor_tensor(out=x4[:nt, si, a_ * 512:(a_ + 1) * 512], in0=ps[:nt, a_ * 512:(a_ + 1) * 512], in1=x4[:nt, si, a_ * 512:(a_ + 1) * 512], op=ALU.add), [pk, xk], [xk])
                act(lambda e, si=si: e.activation(out=junk2[:nt, :], in_=x4[:nt, si, :], func=AF.Square, accum_out=st2[:nt, 4:5]), [xk], ["junk2", "st2"])
                act(lambda e: e.activation(out=st2[:nt, 5:6], in_=st2[:nt, 4:5], func=AF.Sqrt, bias=EPS, scale=1.0 / D), ["st2"], ["st2"])
                dve(lambda e: e.reciprocal(out=st2[:nt, 6:7], in_=st2[:nt, 5:6]), ["st2"], ["st2"])
                dve(lambda e, si=si: e.scalar_tensor_tensor(out=x4[:nt, si, :], in0=x4[:nt, si, :], scalar=st2[:nt, 6:7], in1=nfw[:nt, :], op0=ALU.mult, op1=ALU.mult), [xk, "st2", "nfw"], [xk])
            if nsub == NSUB:
                dma("pool", yp_v[:, s0:s0 + NSUB, :], x4[:], [xk], [], "yo%d" % (i % 2))
            else:
                dma("pool", ys, x4[:nt, 0, :], [xk], [], "yo%d" % (i % 2))

        def run2(gl):
            gl = list(gl)
            while gl:
                for g_ in list(gl):
                    try:
                        next(g_)
                    except StopIteration:
                        gl.remove(g_)

        run2([front(0)])
        for i in range(len(sbs)):
            gl = [up(i)]
            if i + 1 < len(sbs):
                gl.append(front(i + 1))
            run2(gl)
            down(i)
        PP[0].finalize()
    es_ps.close()
    ctx.close()
    return nc


_CACHE = {}


def _host_packs(inp, core):
    f = np.float32
    L = 0
    pa = np.zeros((128, NA), f)

    def put(n, arr):
        a, b = OFF[n]
        pa[:, a:b] = arr

    rep = lambda v: np.broadcast_to(np.asarray(v, f).reshape(1, -1), (128, np.asarray(v).size))
    put("mnw", rep(inp["mlstm_norm_w"][L]))
    put("w0", rep(inp["rw_w0"][L]))
    put("a0", rep(inp["rw_a0"][L]))
    put("kk", rep(inp["rw_k_k"][L]))
    put("ka", rep(inp["rw_k_a"][L]))
    put("rk", rep(inp["rw_r_k"][L].reshape(-1)))
    put("lnw", rep(inp["rw_ln_w"][L]))
    put("lnb", rep(inp["rw_ln_b"][L]))
    put("ifb", rep(np.concatenate([inp["mlstm_i_b"][L], inp["mlstm_f_b"][L]])))
    put("nmw", inp["norm_mix_w"][L].reshape(8, 128).T)
    put("nmlp", inp["norm_mlp_w"][L].reshape(8, 128).T)
    cw = inp["mlstm_conv_w"][L]
    put("cw", cw.reshape(4, 8, 128).transpose(2, 1, 0).reshape(128, 32))
    put("cb", inp["mlstm_conv_b"][L].reshape(8, 128).T)
    put("ident", np.eye(128, dtype=f))
    put("mui", np.triu(np.ones((128, 128), f), 0))
    put("mus", np.triu(np.ones((128, 128), f), 1))
    put("mls", np.tril(np.ones((128, 128), f), -1))
    put("ones", np.ones((128, 128), f))
    lsel = np.zeros((128, 128), f)
    rsel = np.zeros((128, 4), f)
    for h in range(8):
        lsel[h, (h % 2) * 64:(h % 2) * 64 + 64] = 1.0
        rsel[h, h // 2] = 1.0
    put("lsel", lsel)
    put("rsel", rsel)
    return pa


def _sample_pack(inp):
    f = np.float32
    L = 0
    sp = np.zeros((128, NSP), f)

    def bh(v512):
        return np.tile(np.asarray(v512, f).reshape(8, 64), (NS, 1))

    def put(n, arr):
        a, b = SOFF[n]
        sp[:, a:b] = arr

    mu = inp["rw_mu"][L]
    put("mu_r", bh(mu[0:512]))
    put("mu_k", bh(mu[512:1024]))
    put("mu_v", bh(mu[1024:1536]))
    cw = inp["mlstm_conv_w"][L]
    put("cwq", np.concatenate([bh(cw[j, 0:512]) for j in range(4)], axis=1))
    put("cwk", np.concatenate([bh(cw[j, 512:1024]) for j in range(4)], axis=1))
    cb = inp["mlstm_conv_b"][L]
    put("cbq", bh(cb[0:512]))
    put("cbk", bh(cb[512:1024]))
    put("mnw", bh(inp["mlstm_norm_w"][L]))
    put("w0", bh(inp["rw_w0"][L]))
    put("a0", bh(inp["rw_a0"][L]))
    put("kk", bh(inp["rw_k_k"][L]))
    put("ka", bh(inp["rw_k_a"][L]))
    put("rk", bh(inp["rw_r_k"][L].reshape(-1)))
    put("lnw", bh(inp["rw_ln_w"][L]))
    put("lnb", bh(inp["rw_ln_b"][L]))
    put("ib", np.tile(inp["mlstm_i_b"][L].reshape(8, 1), (NS, 1)))
    put("fb", np.tile(inp["mlstm_f_b"][L].reshape(8, 1), (NS, 1)))
    return sp


def kernel(**inp):
    f = np.float32
    inp = {k: np.asarray(v) for k, v in inp.items()}
    if "nc" not in _CACHE:
        _CACHE["nc"] = build_program()
    nc = _CACHE["nc"]
    L = 0
    pa = _host_packs(inp, 0)
    sp = _sample_pack(inp)
    mu = inp["rw_mu"][L]
    luw = np.concatenate([inp["rw_w_up"][L], inp["rw_a_up"][L]], axis=0).astype(f)
    common = {
        "w_in": np.ascontiguousarray(inp["w_in"][L], f),
        "w_out": np.ascontiguousarray(inp["w_out"][L], f),
        "mlp_up": np.ascontiguousarray(inp["mlp_up"][L], f),
        "mlp_down": np.ascontiguousarray(inp["mlp_down"][L], f),
        "packA": pa,
        "mu_b": np.ascontiguousarray(np.broadcast_to(mu.reshape(1, -1), (128, RWW)), f),
        "nfw_b": np.ascontiguousarray(np.broadcast_to(inp["norm_f_w"].reshape(1, -1), (128, D)), f),
        "luw": luw,
        "gup": np.ascontiguousarray(inp["rw_g_up"][L], f),
        "spack": sp,
        "mul": np.ascontiguousarray(np.broadcast_to(mu[1536:1792].reshape(1, -1), (NS, 256)), f),
    }
    in_maps = []
    for c in range(8):
        rs = slice(c * NS, (c + 1) * NS)
        m = dict(common)
        m["xp"] = np.ascontiguousarray(inp["x_prompt"][c], f)
        m["xs"] = np.ascontiguousarray(inp["x_sample"][rs, 0, :], f)
        m["sC"] = np.ascontiguousarray(inp["state_mlstm_C"][L, rs].reshape(128, 4096), f)
        m["sn"] = np.ascontiguousarray(inp["state_mlstm_n"][L, rs].reshape(128, 64), f)
        m["sm"] = np.ascontiguousarray(inp["state_mlstm_m"][L, rs].reshape(128, 1), f)
        cv = inp["state_mlstm_conv"][L, rs]
        m["sconv"] = np.ascontiguousarray(cv.reshape(NS, 3, 2, 8, 64).transpose(0, 3, 2, 1, 4).reshape(128, 2, 3, 64), f)
        m["sS"] = np.ascontiguousarray(inp["state_rwkv_S"][L, rs].reshape(128, 4096), f)
        sh = inp["state_rwkv_shift"][L, rs, 0, :]
        m["sshift"] = np.ascontiguousarray(sh[:, 0:1536].reshape(NS, 3, 8, 64).transpose(0, 2, 1, 3).reshape(128, 3, 64), f)
        m["sshl"] = np.ascontiguousarray(sh[:, 1536:1792], f)
        in_maps.append(m)
    res = run_bass_kernel_spmd(nc, in_maps, core_ids=list(range(8)))
    R = res.results
    y_prompt = np.stack([R[c]["yp"] for c in range(8)]).astype(f)
    y_sample = np.concatenate([R[c]["ys"] for c in range(8)], axis=0).reshape(128, 1, D).astype(f)
    pC = np.zeros((1, 8, 8, 64, 64), f)
    pn = np.zeros((1, 8, 8, 64), f)
    pm = np.zeros((1, 8, 8), f)
    pconv = np.zeros((1, 8, 3, 1024), f)
    pS = np.zeros((1, 8, 8, 64, 64), f)
    pshift = np.zeros((1, 8, 1, RWW), f)
    for c in range(8):
        oC = R[c]["oC"].reshape(2, 64, 4, 65)
        Ch = oC.transpose(2, 0, 1, 3).reshape(8, 64, 65)
        pC[0, c] = Ch[:, :, 0:64]
        pn[0, c] = Ch[:, :, 64]
        pm[0, c] = R[c]["om"].reshape(8)
        pconv[0, c] = R[c]["oconv"].transpose(2, 1, 0).reshape(3, 1024)
        oS = R[c]["oS"].reshape(2, 64, 4, 64)
        pS[0, c] = oS.transpose(2, 0, 3, 1).reshape(8, 64, 64)
        pshift[0, c, 0] = R[c]["oshift"].reshape(RWW)
    sC = np.concatenate([R[c]["osC"].reshape(NS, 8, 64, 64) for c in range(8)])[None].astype(f)
    sn = np.concatenate([R[c]["osn"].reshape(NS, 8, 64) for c in range(8)])[None].astype(f)
    sm = np.concatenate([R[c]["osm"].reshape(NS, 8) for c in range(8)])[None].astype(f)
    sconv = np.concatenate([R[c]["osconv"].reshape(NS, 8, 2, 3, 64).transpose(0, 3, 2, 1, 4).reshape(NS, 3, 1024) for c in range(8)])[None].astype(f)
    sS = np.concatenate([R[c]["osS"].reshape(NS, 8, 64, 64) for c in range(8)])[None].astype(f)
    sshift = np.concatenate([
        np.concatenate([R[c]["osshift"].reshape(NS, 8, 3, 64).transpose(0, 2, 1, 3).reshape(NS, 1536), R[c]["osshl"]], axis=1)
        for c in range(8)]).reshape(1, 128, 1, RWW).astype(f)
    return (y_prompt, y_sample, pC, pn, pm, pconv, pS, pshift, sC, sn, sm, sconv, sS, sshift)
```

```python
import contextlib
import numpy as np
import concourse.bass as bass
import concourse.mybir as mybir
from concourse.bass_utils import run_bass_kernel_spmd

F32 = mybir.dt.float32
BF16 = mybir.dt.bfloat16
AF = mybir.ActivationFunctionType
ALU = mybir.AluOpType
AX = mybir.AxisListType

D = 1024
T = 2048
NB = 16
NS = 16
INW = 3856
MLW = 2064
RWW = 1792
DFF = 4096
EPS = 1e-6
GN_EPS = 64e-5
C0 = 0.6065306597126334

OFF = {}
_o = 0
for _n, _w in [("mnw", 512), ("w0", 512), ("a0", 512), ("kk", 512), ("ka", 512), ("rk", 512),
               ("lnw", 512), ("lnb", 512), ("ifb", 16), ("nmw", 8), ("nmlp", 8), ("cw", 32), ("cb", 8),
               ("ident", 128), ("mui", 128), ("mus", 128), ("mls", 128), ("ones", 128),
               ("lsel", 128), ("rsel", 4)]:
    OFF[_n] = (_o, _o + _w)
    _o += _w
NA = _o
SOFF = {}
_o = 0
for _n, _w in [("mu_r", 64), ("mu_k", 64), ("mu_v", 64), ("cwq", 256), ("cwk", 256), ("cbq", 64), ("cbk", 64),
               ("mnw", 64), ("w0", 64), ("a0", 64), ("kk", 64), ("ka", 64), ("rk", 64), ("lnw", 64), ("lnb", 64),
               ("ib", 1), ("fb", 1)]:
    SOFF[_n] = (_o, _o + _w)
    _o += _w
NSP = _o


ALIAS = {"r_sb0": "G0", "kf_sb0": "G1", "vf0": "G2", "osig0": "G13", "r_sb1": "G20", "kf_sb1": "G21", "vf1": "G22", "osig1": "G23",
         "wsig": "G3", "a_sb": "G4", "g_sb0": "G5", "g_sb1": "G24", "kap": "G6", "ktl": "G7",
         "bvec": "G8", "e1": "G9", "e2": "G10", "e3": "G11", "pcw": "G12", "ysb": "G11", "tA": "G9", "tB": "G10", "plast": "G16",
         "hml": "G14", "nlrep": "G15", "Fb": "G15", "cacc": "G16", "qks": "G16", "tAm": "G18", "tBm": "G19",
         "junk": "xm", "ctmp": "xm", "mixT": "TMbA", "TMb0": "TMbA", "TMb1": "TMbA",
         "TMb2": "TMbB", "TMb3": "TMbB", "Ub": "Zb1", "Ss": "Cs", "lo3": "spj", "smix": "spj",
         "xt1": "xt0", "xnT1": "xnT0", "qkx1": "qkx0"}
PARKEYS = {"g_sb", "r_sb", "kf_sb", "vf", "osig", "vaug", "gt", "qpb", "kTb", "ktm", "vrw", "lor"}
CURP = [0]


class SemCtx:
    def __init__(self, nc):
        self.nc = nc
        self.es = contextlib.ExitStack()
        self.engs = ["pe", "act", "dve", "pool", "sp"]
        self.esem = {e: self.es.enter_context(nc.semaphore("s_" + e)) for e in self.engs}
        self.ecnt = {e: 0 for e in self.engs}
        self.bsem = self.es.enter_context(nc.semaphore("s_bar"))
        self.phase = 0
        self.gsem = {}
        self.gbase = {}

    def group_sem(self, g):
        if g not in self.gsem:
            self.gsem[g] = self.es.enter_context(self.nc.semaphore("g_%d" % len(self.gsem)))
            self.gbase[g] = 0
        return self.gsem[g]

    def close(self):
        self.es.close()


class Prog:
    max_ops = None

    def __init__(self, ctx):
        self.ctx = ctx
        self.nc = ctx.nc
        self.ops = []
        self.last_writer = {}
        self.readers = {}
        self.dma_groups = {}

    def op(self, eng, fn, reads=(), writes=(), dma_group=None, wait_total=False):
        if self.max_ops is not None and len(self.ops) >= self.max_ops:
            return None
        reads = [(k + str(CURP[0])) if k in PARKEYS else k for k in reads]
        writes = [(k + str(CURP[0])) if k in PARKEYS else k for k in writes]
        reads = [ALIAS.get(k, k) for k in reads]
        writes = [ALIAS.get(k, k) for k in writes]
        if eng != "pe":
            writes = writes + [k for k in reads if k.startswith("PS") and k not in writes]
        deps = set()
        for b in reads:
            if b in self.last_writer:
                deps.add(self.last_writer[b])
        for b in writes:
            if b in self.last_writer:
                deps.add(self.last_writer[b])
            for r in self.readers.get(b, ()):
                deps.add(r)
        idx = len(self.ops)
        if dma_group is not None:
            deps = {d for d in deps if self.ops[d]["dma"] != dma_group}
        o = dict(eng=eng, fn=fn, deps=sorted(deps), dma=dma_group, idx=idx)
        if dma_group is not None:
            g = self.dma_groups.setdefault(dma_group, dict(total=0, wait_total=wait_total))
            g["total"] += 1
            o["dma_cnt"] = g["total"]
        self.ops.append(o)
        for b in reads:
            self.readers.setdefault(b, []).append(idx)
        for b in writes:
            self.last_writer[b] = idx
            self.readers[b] = []
        return idx

    def finalize(self):
        nc = self.nc
        ctx = self.ctx
        ops = self.ops
        needed = set()
        for o in ops:
            best = {}
            bestg = {}
            rd = []
            for d in o["deps"]:
                p = ops[d]
                if p["dma"] is not None:
                    bestg[p["dma"]] = max(bestg.get(p["dma"], -1), d)
                else:
                    if p["eng"] == "pe" and o["eng"] == "pe" and o["dma"] is None:
                        continue
                    best[p["eng"]] = max(best.get(p["eng"], -1), d)
            rd.extend(best.values())
            rd.extend(bestg.values())
            o["deps"] = sorted(rd)
            for d in best.values():
                needed.add(d)
        engs = ctx.engs
        last = {}
        for o in ops:
            if o["dma"] is None:
                last[o["eng"]] = o["idx"]
        needed |= set(last.values())
        cnt = dict(ctx.ecnt)
        for o in ops:
            if o["dma"] is None and o["idx"] in needed:
                cnt[o["eng"]] += 1
                o["sig"] = cnt[o["eng"]]
        for g in self.dma_groups:
            ctx.group_sem(g)
        phase = ctx.phase
        with nc.Block() as block:

            def emit_engine(ename, eng):
                known = {}
                if phase > 0:
                    eng.wait_ge(ctx.bsem, phase)
                for o in ops:
                    if o["eng"] != ename:
                        continue
                    for d in o["deps"]:
                        p = ops[d]
                        if p["dma"] is not None:
                            g = self.dma_groups[p["dma"]]
                            sem = ctx.gsem[p["dma"]]
                            val = ctx.gbase[p["dma"]] + 16 * (g["total"] if g["wait_total"] else p["dma_cnt"])
                            key = ("g", p["dma"])
                        else:
                            if p["eng"] == "pe" and ename == "pe" and o["dma"] is None:
                                continue
                            sem = ctx.esem[p["eng"]]
                            val = p["sig"]
                            key = ("e", p["eng"])
                        if known.get(key, 0) >= val:
                            continue
                        known[key] = val
                        eng.wait_ge(sem, val)
                    ins = o["fn"](eng)
                    if o["dma"] is not None:
                        ins.then_inc(ctx.gsem[o["dma"]], 16)
                    elif "sig" in o:
                        ins.then_inc(ctx.esem[ename], 1)
                if ename == "sp":
                    for e2 in engs:
                        if cnt[e2] > ctx.ecnt[e2]:
                            eng.wait_ge(ctx.esem[e2], cnt[e2])
                    for g, info in self.dma_groups.items():
                        eng.wait_ge(ctx.gsem[g], ctx.gbase[g] + 16 * info["total"])
                    eng.sem_inc(ctx.bsem, 1)

            @block.tensor
            def _(e):
                emit_engine("pe", e)

            @block.scalar
            def _(e):
                emit_engine("act", e)

            @block.vector
            def _(e):
                emit_engine("dve", e)

            @block.gpsimd
            def _(e):
                emit_engine("pool", e)

            @block.sync
            def _(e):
                emit_engine("sp", e)

        ctx.ecnt = cnt
        for g, info in self.dma_groups.items():
            ctx.gbase[g] += 16 * info["total"]
        ctx.phase += 1


STOP_EARLY = True


class _StopBuild(Exception):
    pass


def build_program(do_sample=True, debug=False):
    nc = bass.Bass("TRN2", target_bir_lowering=False)
    try:
        return _build_program(nc, do_sample, debug)
    except _StopBuild:
        return nc


def _build_program(nc, do_sample, debug):
    dbg = nc.dram_tensor("dbg", [128, 16, 512], F32, kind="ExternalOutput").ap() if debug else None
    din = lambda n, s: nc.dram_tensor(n, s, F32, kind="ExternalInput").ap()
    dout = lambda n, s: nc.dram_tensor(n, s, F32, kind="ExternalOutput").ap()
    xp = din("xp", [T, D])
    xs = din("xs", [NS, D])
    w_in = din("w_in", [D, INW])
    w_out = din("w_out", [D, D])
    mlp_up = din("mlp_up", [D, DFF])
    mlp_down = din("mlp_down", [DFF, D])
    packA_d = din("packA", [128, NA])
    mu_d = din("mu_b", [128, RWW])
    nfw_d = din("nfw_b", [128, D])
    wup_d = din("luw", [128, 512])
    gup_d = din("gup", [128, 512])
    spk_d = din("spack", [128, NSP])
    sC_d = din("sC", [128, 4096])
    sn_d = din("sn", [128, 64])
    sm_d = din("sm", [128, 1])
    sconv_d = din("sconv", [128, 2, 3, 64])
    sS_d = din("sS", [128, 4096])
    sshift_d = din("sshift", [128, 3, 64])
    sshl_d = din("sshl", [NS, 256])
    mul_d = din("mul", [NS, 256])

    yp = dout("yp", [T, D])
    ys = dout("ys", [NS, D])
    oC = dout("oC", [128, 4, 65])
    om = dout("om", [8, 1])
    oconv = dout("oconv", [128, 8, 3])
    oS = dout("oS", [128, 4, 64])
    oshift = dout("oshift", [1, RWW])
    osC = dout("osC", [128, 4096])
    osn = dout("osn", [128, 64])
    osm = dout("osm", [128, 1])
    osconv = dout("osconv", [128, 2, 3, 64])
    osS = dout("osS", [128, 4096])
    osshift = dout("osshift", [128, 3, 64])
    osshl = dout("osshl", [NS, 256])

    mix_d = nc.dram_tensor("mix_scr", [T + NS, D], BF16, kind="Internal").ap()
    scr1 = nc.dram_tensor("scr1", [128, 8, 64], F32, kind="Internal").ap()
    scr2 = nc.dram_tensor("scr2", [128, 3, 64], F32, kind="Internal").ap()
    scr3 = nc.dram_tensor("scr3", [NS, 2, 8, 64], F32, kind="Internal").ap()

    ctx = SemCtx(nc)
    PP = [Prog(ctx)]
    es_res = contextlib.ExitStack()
    cur = [es_res]

    def TT(name, shape, dt=F32):
        return cur[0].enter_context(nc.sbuf_tensor("t_" + name, list(shape), dt))

    def dma(q, out, in_, reads, writes, group, wait_total=False):
        group = "%s@%s" % (group, q)
        PP[0].op(q, lambda e: e.dma_start(out=out, in_=in_), reads=reads, writes=writes, dma_group=group, wait_total=wait_total)

    def dve(fn, r, w):
        PP[0].op("dve", fn, reads=r, writes=w)

    def act(fn, r, w):
        PP[0].op("act", fn, reads=r, writes=w)

    def pool(fn, r, w):
        PP[0].op("pool", fn, reads=r, writes=w)

    def pe(fn, r, w):
        PP[0].op("pe", fn, reads=r, writes=w)

    def mm(out, lhsT, rhs, start, stop, r, w):
        pe(lambda e: e.matmul(out, lhsT=lhsT, rhs=rhs, start=start, stop=stop), r, w)

    es_ps = contextlib.ExitStack()
    PS = [es_ps.enter_context(nc.psum_tensor("PS%d" % i, [128, 1024], F32)) for i in range(4)]
    PSB = [p.bitcast(BF16) for p in PS]
    psi = {"A": 0, "F": 0, "B": 0}
    POOLS = {"A": [0, 1, 2, 3], "F": [0, 1], "B": [2, 3]}
    CURPOOL = ["A"]

    def nps():
        pl = CURPOOL[0]
        lst = POOLS[pl]
        i = lst[psi[pl] % len(lst)]
        psi[pl] += 1
        return PS[i], PSB[i], "PS%d" % i

    wq = TT("wq", [128, 8, INW], BF16)
    mub = TT("mub", [128, RWW], F32)
    luw = TT("luw", [128, 512], BF16)
    gup = TT("gup", [128, 512], BF16)
    pA = TT("pA", [128, NA], F32)
    identb = TT("identb", [128, 128], BF16)

    def PA(n):
        a, b = OFF[n]
        return pA[:, a:b]

    w_in_v = w_in.rearrange("(c p) n -> p c n", p=128)
    dma("sp", pA[:], packA_d, [], ["pA"], "init", True)
    dma("pool", luw[:], wup_d, [], ["luw"], "init", True)
    dma("pool", gup[:], gup_d, [], ["gup"], "init", True)
    w_out_v = w_out.rearrange("(c p) n -> p c n", p=128)
    dve(lambda e: e.tensor_copy(out=identb[:], in_=PA("ident")), ["pA"], ["identb"])


    def _dbgdump(tag):
        if debug != tag:
            return
        dstg_ = cur[0].enter_context(nc.sbuf_tensor("t_dbgst%d" % tag, [128, 512], F32))
        def dd(slot, ap, key, n):
            dve(lambda e: e.tensor_copy(out=dstg_[:, 0:n], in_=ap), [key], ["dbgst"])
            dma("sp", dbg[:, slot, 0:n], dstg_[:, 0:n], ["dbgst"], [], "dbg")
        dd(0, PA("ident"), "pA", 128)
        dd(1, PA("mui"), "pA", 128)
        dd(2, PA("mnw"), "pA", 512)
        dd(3, PA("w0"), "pA", 512)
        dd(4, PA("lnb"), "pA", 512)
        PP[0].max_ops = len(PP[0].ops)
        PP[0].finalize()
        raise _StopBuild()
    _dbgdump(3)
    dma("sp", mub[:], mu_d, [], ["mub"], "init", True)
    PP[0].finalize()
    PP[0] = Prog(ctx)

    def rmsnorm_T(xt, xk, nt, dstT, dstk, col0, wname, tmpb, tmpbk, junk, junkk, st, stk):
        act(lambda e: e.activation(out=junk[:nt, :], in_=xt[:nt, :], func=AF.Square, accum_out=st[:nt, 0:1]), [xk], [junkk, stk])
        act(lambda e: e.activation(out=st[:nt, 1:2], in_=st[:nt, 0:1], func=AF.Sqrt, bias=EPS, scale=1.0 / D), [stk], [stk])
        dve(lambda e: e.reciprocal(out=st[:nt, 2:3], in_=st[:nt, 1:2]), [stk], [stk])
        dve(lambda e: e.tensor_scalar_mul(out=tmpb[:nt, :], in0=xt[:nt, :], scalar1=st[:nt, 2:3]), [xk, stk], [tmpbk])
        ps, psb, pk = nps()
        for c in range(8):
            pe(lambda e, c=c: e.transpose(psb[:, c * 128:c * 128 + nt], tmpb[:nt, c * 128:(c + 1) * 128], identb[:nt, :nt]), [tmpbk, "identb"], [pk])
        a, b_ = OFF[wname]
        dve(lambda e: e.tensor_tensor(out=dstT[:, :, col0:col0 + nt],
                                      in0=psb[:, 0:1024].rearrange("p (c t) -> p c t", c=8)[:, :, 0:nt],
                                      in1=pA[:, a:b_].unsqueeze(2).to_broadcast([128, 8, nt]), op=ALU.mult), [pk, "pA"], [dstk])

    def head_ml(nt, hsrc, hk, osig, ok, mix, mixk, W, sfx=""):
        tA, tB, s8 = W["tA"], W["tB"], W["s8"]
        h3 = lambda t: t[:nt, :].rearrange("p (h d) -> p h d", h=8)
        bc = lambda t, c: t[:nt, c:c + 8].unsqueeze(2).to_broadcast([nt, 8, 64])
        dve(lambda e: e.tensor_tensor(out=tA[:nt, :], in0=hsrc[:nt, :], in1=osig[:nt, :], op=ALU.mult), [hk, ok], ["tA" + sfx])
        dve(lambda e: e.tensor_tensor(out=tB[:nt, :], in0=tA[:nt, :], in1=tA[:nt, :], op=ALU.mult), ["tA" + sfx], ["tB" + sfx])
        dve(lambda e: e.tensor_reduce(out=s8[:nt, 0:8], in_=h3(tB), axis=AX.X, op=ALU.add), ["tB" + sfx], ["s8" + sfx])
        act(lambda e: e.activation(out=s8[:nt, 8:16], in_=s8[:nt, 0:8], func=AF.Sqrt, bias=EPS, scale=1.0 / 64), ["s8" + sfx], ["s8" + sfx])
        dve(lambda e: e.reciprocal(out=s8[:nt, 16:24], in_=s8[:nt, 8:16]), ["s8" + sfx], ["s8" + sfx])
        dve(lambda e: e.tensor_tensor(out=h3(tB), in0=h3(tA), in1=bc(s8, 16), op=ALU.mult), ["tA" + sfx, "s8" + sfx], ["tB" + sfx])
        dve(lambda e: e.tensor_tensor(out=mix[:nt, 0:512], in0=tB[:nt, :], in1=PA("mnw")[:nt, :], op=ALU.mult), ["tB" + sfx, "pA"], [mixk])

    def head_rw(nt, ysrc, yk, bon, bonk, vf, vfk, g, gk, mix, mixk, W):
        tA, tB, s8 = W["tA"], W["tB"], W["s8"]
        h3 = lambda t: t[:nt, :].rearrange("p (h d) -> p h d", h=8)
        bc = lambda t, c: t[:nt, c:c + 8].unsqueeze(2).to_broadcast([nt, 8, 64])
        pool(lambda e: e.tensor_tensor(out=h3(tA), in0=h3(vf), in1=bc(bon, 0), op=ALU.mult), [vfk, bonk], ["tAm"])
        pool(lambda e: e.tensor_tensor(out=tA[:nt, :], in0=tA[:nt, :], in1=ysrc[:nt, :], op=ALU.add), ["tAm", yk], ["tAm"])
        dve(lambda e: e.tensor_reduce(out=s8[:nt, 24:32], in_=h3(tA), axis=AX.X, op=ALU.add), ["tAm"], ["s8"])
        pool(lambda e: e.tensor_scalar_mul(out=s8[:nt, 24:32], in0=s8[:nt, 24:32], scalar1=1.0 / 64), ["s8"], ["s8"])
        pool(lambda e: e.tensor_tensor(out=h3(tA), in0=h3(tA), in1=bc(s8, 24), op=ALU.subtract), ["tAm", "s8"], ["tAm"])
        pool(lambda e: e.tensor_tensor(out=tB[:nt, :], in0=tA[:nt, :], in1=tA[:nt, :], op=ALU.mult), ["tAm"], ["tBm"])
        dve(lambda e: e.tensor_reduce(out=s8[:nt, 32:40], in_=h3(tB), axis=AX.X, op=ALU.add), ["tBm"], ["s8"])
        act(lambda e: e.activation(out=s8[:nt, 40:48], in_=s8[:nt, 32:40], func=AF.Sqrt, bias=GN_EPS, scale=1.0 / 64), ["s8"], ["s8"])
        dve(lambda e: e.reciprocal(out=s8[:nt, 48:56], in_=s8[:nt, 40:48]), ["s8"], ["s8"])
        pool(lambda e: e.tensor_tensor(out=h3(tB), in0=h3(tA), in1=bc(s8, 48), op=ALU.mult), ["tAm", "s8"], ["tBm"])
        pool(lambda e: e.tensor_tensor(out=tB[:nt, :], in0=tB[:nt, :], in1=PA("lnw")[:nt, :], op=ALU.mult), ["tBm", "pA"], ["tBm"])
        pool(lambda e: e.tensor_tensor(out=tB[:nt, :], in0=tB[:nt, :], in1=PA("lnb")[:nt, :], op=ALU.add), ["tBm", "pA"], ["tBm"])
        pool(lambda e: e.tensor_tensor(out=mix[:nt, 512:1024], in0=tB[:nt, :], in1=g[:nt, :], op=ALU.mult), ["tBm", gk], [mixk])

    def out_proj(nt, mix, mixk, xt, xk, row0, W):
        dma("pool", mix_d[row0:row0 + nt, :], mix[:nt, :], [mixk], [], "mixst")

    with contextlib.ExitStack() as es1:
        cur[0] = es1
        W = {}
        Gbig = TT("Gbig", [128, 25, 512])
        G = [Gbig[:, i, :] for i in range(25)]
        W["s8"] = TT("s8", [128, 64])
        W["xm"] = TT("xm", [128, D])
        xt = [TT("xt0", [128, D])] * 2
        junk = W["xm"]
        st = TT("st", [128, 4])
        mix = TT("mix", [128, D], BF16)
        xsb = TT("xsb", [128, D], BF16)
        xnT = [TT("xnT0", [128, 8, 129], BF16)] * 2
        dxT = TT("dxT", [128, 8, 128], BF16)
        qkx = [TT("qkx0", [128, 8, 131])] * 2
        cacc = Gbig[:, 16:18, :].rearrange("p a (c t) -> p (a c) t", t=128)
        ctmp = W["xm"][:, :].rearrange("p (c t) -> p c t", c=8)
        qks = cacc
        QPB = [TT("qpb%d" % i, [128, 4, 128], BF16) for i in range(2)]
        KTB = [TT("kTb%d" % i, [128, 4, 128], BF16) for i in range(2)]
        KTM = [TT("ktm%d" % i, [128, 8, 64], BF16) for i in range(2)]
        VAUG = [TT("vaug%d" % i, [128, 8, 65], BF16) for i in range(2)]
        GT = [TT("gt%d" % i, [128, 96]) for i in range(2)]
        runmax = TT("runmax", [128, 8])
        nBc = TT("nBc", [128, 8])
        Cst = TT("Cst", [128, 4, 65])
        Cbf = TT("Cbf", [128, 4, 65], BF16)
        Fb = G[15].rearrange("p (j t) -> p j t", j=4)
        hml = G[14]
        nlrep = G[15].rearrange("p (h d) -> p h d", h=8)
        wsig, a_sb, g_sb, kap, ktl, bvec, e1, e2, e3, pcw, ysb = G[3], G[4], G[5], G[6], G[7], G[8], G[9], G[10], G[11], G[12], G[11]
        W["tA"], W["tB"] = G[9], G[10]
        WM = {"tA": G[18], "tB": G[19], "s8": TT("s8m", [128, 64])}
        r8b = TT("r8b", [128, 8])
        VRW = [TT("vrw%d" % i, [128, 8, 64], BF16) for i in range(2)]
        LOR = [TT("lor%d" % i, [128, 256], BF16) for i in range(2)]
        lorT = TT("lorT", [128, 2, 128], BF16)
        r8 = TT("r8", [128, 32])
        bon = TT("bon", [128, 8])
        TMb = TT("TMb", [128, 4, 512], BF16)
        W["mixT"] = TMb[:, 0:2, :].rearrange("p a (c t) -> p (a c) t", t=128)
        Btz = TT("Btz", [128, 8, 128], BF16)
        Ktz = TT("Ktz", [128, 8, 128], BF16)
        FMt = TT("FMt", [128, 4, 4, 128], BF16)
        Am = [TT("Am%d" % i, [128, 8, 128], BF16) for i in range(3)]
        PTb = TT("PTb", [128, 8, 128], BF16)
        Pw = [TT("Pw%d" % i, [128, 8, 128], BF16) for i in range(4)]
        Zb = [TT("Zb%d" % i, [128, 8, 64], BF16) for i in range(2)]
        Ub = Zb[1]
        Sst = TT("Sst", [128, 4, 64])
        Sbf = TT("Sbf", [128, 4, 64], BF16)
        WLfm = TT("WLfm", [128, 4])
        plast = G[17]

        def wqk(c0, c1):
            return ["wq%d" % g for g in range(c0 // 512, (c1 - 1) // 512 + 1)]

        for g in range(8):
            c0, c1 = g * 512, min(INW, (g + 1) * 512)
            dma("pool", wq[:, :, c0:c1], w_in_v[:, :, c0:c1], [], ["wq%d" % g], "wq%d" % g)
        for p_ in range(2):
            pool(lambda e, p_=p_: e.memset(VAUG[p_][:], 1.0), [], ["vaug%d" % p_])
        pool(lambda e: e.memset(Btz[:], 0.0), [], ["Btz"])
        pool(lambda e: e.memset(Ktz[:], 0.0), [], ["Ktz"])
        pool(lambda e: e.memset(Cst[:], 0.0), [], ["Cst"])
        pool(lambda e: e.memset(Cbf[:], 0.0), [], ["Cbf"])
        pool(lambda e: e.memset(Sst[:], 0.0), [], ["Sst"])
        pool(lambda e: e.memset(Sbf[:], 0.0), [], ["Sbf"])
        pool(lambda e: e.memset(runmax[:], -1e30), [], ["runmax"])
        pool(lambda e: e.memset(nBc[:], 0.0), [], ["nBc"])
        pool(lambda e: e.memset(xnT[0][:, :, 0:1], 0.0), [], ["xnT0"])
        pool(lambda e: e.memset(qkx[0][:, :, 0:3], 0.0), [], ["qkx0"])

        if debug == 2:
            dstg = TT("dbgstage", [128, 512]) if False else G[12]
            def ddump0(slot, ap, key, n):
                dve(lambda e: e.tensor_copy(out=dstg[:, 0:n], in_=ap), [key], ["pcw"])
                dma("sp", dbg[:, slot, 0:n], dstg[:, 0:n], ["pcw"], [], "dbg")
            ddump0(0, PA("ident"), "pA", 128)
            ddump0(1, PA("mui"), "pA", 128)
            ddump0(2, luw[:, :], "luw", 512)
            ddump0(3, gup[:, :], "gup", 512)
            ddump0(4, W1[:, 0, 0:512], "W1", 512)
            PP[0].max_ops = len(PP[0].ops)
            if STOP_EARLY:
                PP[0].finalize()
                raise _StopBuild()
        MUI = PA("mui")
        MUS = PA("mus")
        MLS = PA("mls")
        ONES = PA("ones")
        IDF = PA("ident")
        bc8 = lambda ap: ap.unsqueeze(2).to_broadcast([128, 8, 64])
        m8 = lambda m: m.unsqueeze(1).to_broadcast([128, 8, 128])
        v3 = lambda t: t[:].rearrange("p (h d) -> p h d", h=8)
        hoff = lambda h: (h % 2) * 512 + (h // 2) * 128

        def make_block(b):
            p_ = b % 2
            r_sb, kf_sb, vf, osig = G[0 + 20 * p_] if p_ == 0 else G[20], G[1] if p_ == 0 else G[21], G[2] if p_ == 0 else G[22], G[13] if p_ == 0 else G[23]
            vaug, gt, qpb, kTb, ktm, vrw, lor = VAUG[p_], GT[p_], QPB[p_], KTB[p_], KTM[p_], VRW[p_], LOR[p_]
            g_sb = G[5] if p_ == 0 else G[24]

            def front_stage():
                x_ = xt[b % 2]
                xk = "xt%d" % (b % 2)
                xn = xnT[b % 2]
                xnk = "xnT%d" % (b % 2)
                qx = qkx[b % 2]
                qxk = "qkx%d" % (b % 2)
                dma("sp", x_[:], xp[b * 128:(b + 1) * 128, :], [], [xk], xk)
                rmsnorm_T(x_, xk, 128, xn, xnk, 1, "nmw", xsb, "xsb", junk, "junk", st, "st")
                cur_x = xn[:, :, 1:129]
                prv_x = xn[:, :, 0:128]
                dve(lambda e, xn=xn: e.tensor_tensor(out=dxT[:], in0=xn[:, :, 0:128], in1=xn[:, :, 1:129], op=ALU.subtract), [xnk], ["dxT"])

                yield
                ps, psb, pk = nps()
                for j in range(8):
                    for c in range(8):
                        mm(ps[:, j * 128:(j + 1) * 128], wq[:, c, j * 128:(j + 1) * 128], cur_x[:, c, :], c == 0, c == 7, wqk(j * 128, (j + 1) * 128) + [xnk], [pk])
                for a_ in range(2):
                    act(lambda e, ps=ps, qx=qx, a_=a_: e.copy(out=qx[:, 4 * a_:4 * a_ + 4, 3:131], in_=ps[:, a_ * 512:(a_ + 1) * 512].rearrange("p (j t) -> p j t", j=4)), [pk], [qxk])
                if b == NB - 1:
                    dma("pool", oconv, qx[:, :, 128:131], [qxk], [], "fin")

                yield
                def tm_plain(col0, ncol, ps_ap, pk):
                    for c in range(8):
                        mm(ps_ap, cur_x[:, c, :], wq[:, c, col0:col0 + ncol], c == 0, c == 7, [xnk] + wqk(col0, col0 + ncol), [pk])

                def tm_shift(col0, ncol, dst, dk):
                    ps, psb, pk = nps()
                    for c in range(8):
                        mm(ps[:, 0:ncol], cur_x[:, c, :], wq[:, c, MLW + col0:MLW + col0 + ncol], c == 0, c == 7, [xnk] + wqk(MLW + col0, MLW + col0 + ncol), [pk])
                    for c in range(8):
                        mm(ps[:, 512:512 + ncol], dxT[:, c, :], wq[:, c, MLW + col0:MLW + col0 + ncol], c == 0, c == 7, ["dxT"] + wqk(MLW + col0, MLW + col0 + ncol), [pk])
                    dve(lambda e, ps=ps: e.tensor_tensor(out=dst, in0=ps[:, 512:512 + ncol], in1=mub[:, col0:col0 + ncol], op=ALU.mult), [pk, "mub"], [dk])
                    dve(lambda e, ps=ps: e.tensor_tensor(out=dst, in0=dst, in1=ps[:, 0:ncol], op=ALU.add), [pk, dk], [dk])

                ps, psb, pk = nps()
                tm_plain(1024, 512, ps[:, 0:512], pk)
                tm_plain(1536, 512, ps[:, 512:1024], pk)
                act(lambda e, ps=ps: e.copy(out=vaug[:, :, 0:64], in_=ps[:, 0:512].rearrange("p (h d) -> p h d", h=8)), [pk], ["vaug"])
                act(lambda e, ps=ps: e.activation(out=osig[:], in_=ps[:, 512:1024], func=AF.Sigmoid), [pk], ["osig"])
                ps, psb, pk = nps()
                tm_plain(2048, 16, ps[:, 0:16], pk)
                dve(lambda e, ps=ps: e.tensor_tensor(out=gt[:, 0:16], in0=ps[:, 0:16], in1=PA("ifb"), op=ALU.add), [pk, "pA"], ["gt"])
                tm_shift(0, 512, r_sb[:], "r_sb")
                tm_shift(512, 512, kf_sb[:], "kf_sb")
                tm_shift(1024, 512, vf[:], "vf")
                pool(lambda e: e.tensor_copy(out=vrw[:], in_=vf[:].rearrange("p (h d) -> p h d", h=8)), ["vf"], ["vrw"])
                ltmp = G[16]
                tm_shift(1536, 256, ltmp[:, 0:256], "cacc")
                act(lambda e: e.activation(out=lor[:, 0:64], in_=ltmp[:, 0:64], func=AF.Tanh), ["cacc"], ["lor"])
                act(lambda e: e.copy(out=lor[:, 64:128], in_=ltmp[:, 64:128]), ["cacc"], ["lor"])
                act(lambda e: e.activation(out=lor[:, 128:256], in_=ltmp[:, 128:256], func=AF.Sigmoid), ["cacc"], ["lor"])
                if b == NB - 1:
                    lastc = xn[:, :, 128:129]
                    for n0 in range(0, RWW, 512):
                        nn = min(512, RWW - n0)
                        ps2, _, pk2 = nps()
                        for c in range(8):
                            mm(ps2[0:1, 0:nn], lastc[:, c, :], wq[:, c, MLW + n0:MLW + n0 + nn], c == 0, c == 7, [xnk] + wqk(MLW + n0, MLW + n0 + nn), [pk2])
                        act(lambda e, ps2=ps2, n0=n0, nn=nn: e.copy(out=plast[0:1, 0:nn], in_=ps2[0:1, 0:nn]), [pk2], ["plast"])
                        dma("pool", oshift[:, n0:n0 + nn], plast[0:1, 0:nn], ["plast"], [], "fin")

                act(lambda e: e.activation(out=gt[:, 56:64], in_=gt[:, 8:16], func=AF.Exp, scale=-1.0), ["gt"], ["gt"])
                act(lambda e: e.activation(out=gt[:, 16:24], in_=gt[:, 56:64], func=AF.Ln, bias=1.0, scale=1.0), ["gt"], ["gt"])
                dve(lambda e: e.tensor_copy(out=nlrep[:], in_=bc8(gt[:, 16:24])), ["gt"], ["nlrep"])
                ps, psb, pk = nps()
                mm(ps[:, 0:8], MUI, gt[:, 16:24], True, True, ["pA", "gt"], [pk])
                mm(ps[:, 8:16], ONES, gt[:, 16:24], True, True, ["pA", "gt"], [pk])
                for j in range(4):
                    mm(ps[:, 512 + j * 128:512 + (j + 1) * 128], nlrep[:, 2 * j:2 * j + 2, :].rearrange("p a d -> p (a d)"), MUI, True, True, ["nlrep", "pA"], [pk])
                dve(lambda e, ps=ps: e.tensor_tensor(out=gt[:, 24:32], in0=ps[:, 0:8], in1=gt[:, 0:8], op=ALU.add), [pk, "gt"], ["gt"])
                act(lambda e: e.activation(out=gt[:, 32:40], in_=gt[:, 24:32], func=AF.Exp), ["gt"], ["gt"])
                dve(lambda e, ps=ps: e.tensor_tensor(out=gt[:, 56:64], in0=gt[:, 24:32], in1=ps[:, 8:16], op=ALU.subtract), [pk, "gt"], ["gt"])
                act(lambda e: e.activation(out=gt[:, 40:48], in_=gt[:, 56:64], func=AF.Exp), ["gt"], ["gt"])
                act(lambda e, ps=ps: e.activation(out=gt[:, 48:56], in_=ps[:, 8:16], func=AF.Exp, scale=-1.0), [pk], ["gt"])
                act(lambda e, ps=ps: e.activation(out=Fb[:], in_=ps[:, 512:1024].rearrange("p (j t) -> p j t", j=4), func=AF.Exp, scale=-1.0), [pk], ["Fb"])
                dve(lambda e: e.tensor_tensor(out=gt[:, 56:64], in0=gt[:, 24:32], in1=nBc[:], op=ALU.add), ["gt", "nBc"], ["gt"])
                dve(lambda e: e.tensor_tensor(out=runmax[:], in0=runmax[:], in1=gt[:, 56:64], op=ALU.max), ["gt", "runmax"], ["runmax"])
                dve(lambda e, ps=ps: e.tensor_tensor(out=nBc[:], in0=nBc[:], in1=ps[:, 8:16], op=ALU.add), [pk, "nBc"], ["nBc"])

                yield
                cwv = PA("cw").rearrange("p (c j) -> p c j", j=4)
                wbc = lambda j: cwv[:, :, j:j + 1].to_broadcast([128, 8, 128])
                pool(lambda e, qx=qx: e.tensor_tensor(out=cacc[:], in0=qx[:, :, 3:131], in1=wbc(3), op=ALU.mult), [qxk, "pA"], ["cacc"])
                for j in range(3):
                    pool(lambda e, qx=qx, j=j: e.tensor_tensor(out=ctmp[:], in0=qx[:, :, j:j + 128], in1=wbc(j), op=ALU.mult), [qxk, "pA"], ["ctmp"])
                    pool(lambda e: e.tensor_tensor(out=cacc[:], in0=cacc[:], in1=ctmp[:], op=ALU.add), ["cacc", "ctmp"], ["cacc"])
                pool(lambda e: e.tensor_tensor(out=cacc[:], in0=cacc[:], in1=PA("cb").unsqueeze(2).to_broadcast([128, 8, 128]), op=ALU.add), ["cacc", "pA"], ["cacc"])
                act(lambda e: e.activation(out=qks[:], in_=cacc[:], func=AF.Silu), ["cacc"], ["qks"])
                dve(lambda e: e.tensor_tensor(out=qpb[:], in0=qks[:, 0:4, :], in1=Fb[:], op=ALU.mult), ["qks", "Fb"], ["qpb"])
                act(lambda e: e.activation(out=kTb[:], in_=qks[:, 4:8, :], func=AF.Copy, scale=0.125), ["qks"], ["kTb"])

                yield
                ps, psb, pk = nps()
                for j in range(4):
                    pe(lambda e, j=j, psb=psb: e.transpose(psb[:, j * 128:(j + 1) * 128], kTb[:, j, :], identb[:]), ["kTb", "identb"], [pk])
                dve(lambda e, psb=psb: e.tensor_tensor(out=ktm[:], in0=psb[:, 0:512].rearrange("p (h d) -> p h d", h=8), in1=bc8(gt[:, 40:48]), op=ALU.mult), [pk, "gt"], ["ktm"])

                yield
                yield
                if b + 1 < NB:
                    pool(lambda e, xn=xn: e.tensor_copy(out=xn[:, :, 0:1], in_=xn[:, :, 128:129]), [xnk], [xnk])
                    pool(lambda e, qx=qx: e.tensor_copy(out=qx[:, :, 0:3], in_=qx[:, :, 128:131]), [qxk], [qxk])
                yield

            def ml_stage():
                ps, psb, pk = nps()
                for h in range(8):
                    j, hp = h // 2, h % 2
                    sl = slice(hp * 64, hp * 64 + 64)
                    mm(ps[:, hoff(h):hoff(h) + 128], kTb[sl, j, :], qpb[sl, j, :], True, True, ["kTb", "qpb"], [pk])
                for h in range(8):
                    dve(lambda e, h=h, ps=ps: e.scalar_tensor_tensor(out=PTb[:, h, :], in0=ps[:, hoff(h):hoff(h) + 128], scalar=gt[:, 32 + h:33 + h], in1=MUI, op0=ALU.mult, op1=ALU.mult), [pk, "gt", "pA"], ["PTb"])
                yield
                ps, psb, pk = nps()
                psn = lambda ps, h: ps[:, (h // 4) * 512 + (h % 4) * 65:(h // 4) * 512 + (h % 4) * 65 + 65]
                for h in range(8):
                    j, hp = h // 2, h % 2
                    sl = slice(hp * 64, hp * 64 + 64)
                    mm(psn(ps, h), PTb[:, h, :], vaug[:, h, :], True, False, ["PTb", "vaug"], [pk])
                    mm(psn(ps, h), qpb[sl, j, :], Cbf[sl, j, :], False, True, ["qpb", "Cbf"], [pk])
                pn4 = ps[:, :].rearrange("p (a r) -> p a r", a=2)[:, :, 0:260].rearrange("p a (h d) -> p a h d", h=4)
                for a_ in range(2):
                    act(lambda e, pn4=pn4, a_=a_: e.copy(out=r8[:, 4 * a_:4 * a_ + 4], in_=pn4[:, a_, :, 64]), [pk], ["r8"])
                dve(lambda e: e.scalar_tensor_tensor(out=r8[:, 8:16], in0=r8[:, 0:8], scalar=-1.0, in1=r8[:, 0:8], op0=ALU.mult, op1=ALU.max), ["r8"], ["r8"])
                dve(lambda e: e.tensor_scalar_max(out=r8[:, 8:16], in0=r8[:, 8:16], scalar1=1.0), ["r8"], ["r8"])
                dve(lambda e: e.reciprocal(out=r8[:, 16:24], in_=r8[:, 8:16]), ["r8"], ["r8"])
                for a_ in range(2):
                    dve(lambda e, pn4=pn4, a_=a_: e.tensor_tensor(out=hml[:, a_ * 256:(a_ + 1) * 256].rearrange("p (h d) -> p h d", h=4), in0=pn4[:, a_, :, 0:64],
                                                              in1=r8[:, 16 + 4 * a_:20 + 4 * a_].unsqueeze(2).to_broadcast([128, 4, 64]), op=ALU.mult), [pk, "r8"], ["hml"])
                yield
                ps, psb, pk = nps()
                for h in range(8):
                    j = h // 2
                    mm(psn(ps, h), ktm[:, 2 * j:2 * j + 2, :].rearrange("p a d -> p (a d)"), vaug[:, h, :], True, True, ["ktm", "vaug"], [pk])
                pu4 = ps[:, :].rearrange("p (a r) -> p a r", a=2)[:, :, 0:260].rearrange("p a (h d) -> p a h d", h=4)
                for hp in range(2):
                    sl = slice(hp * 64, hp * 64 + 64)
                    decb = gt[sl, 48:56].rearrange("p (j q) -> p j q", q=2)[:, :, hp:hp + 1].to_broadcast([64, 4, 65])
                    dve(lambda e, sl=sl, decb=decb: e.tensor_tensor(out=Cst[sl, :, :], in0=Cst[sl, :, :], in1=decb, op=ALU.mult), ["Cst", "gt"], ["Cst"])
                    for a in range(2):
                        src = pu4[sl, a, hp::2, :]
                        dve(lambda e, sl=sl, a=a, src=src: e.tensor_tensor(out=Cst[sl, 2 * a:2 * a + 2, :], in0=Cst[sl, 2 * a:2 * a + 2, :], in1=src, op=ALU.add), [pk, "Cst"], ["Cst"])
                act(lambda e: e.copy(out=Cbf[:], in_=Cst[:]), ["Cst"], ["Cbf"])
                head_ml(128, hml, "hml", osig, "osig", mix, "mix", WM, "m")


                yield
            def rw_stage():
                ps, psb, pk = nps()
                pe(lambda e, psb=psb: e.transpose(psb[:, 0:128], lor[:, 0:128], identb[:]), ["lor", "identb"], [pk])
                pe(lambda e, psb=psb: e.transpose(psb[:, 128:256], lor[:, 128:256], identb[:]), ["lor", "identb"], [pk])
                act(lambda e, psb=psb: e.copy(out=lorT[:], in_=psb[:, 0:256].rearrange("p (a t) -> p a t", a=2)), [pk], ["lorT"])
                ps, psb, pk = nps()
                mm(ps[:, 0:512], lorT[0:64, 0, :], luw[0:64, :], True, True, ["lorT", "luw"], [pk])
                mm(ps[:, 512:1024], lorT[64:128, 0, :], luw[64:128, :], True, True, ["lorT", "luw"], [pk])
                dve(lambda e, ps=ps: e.tensor_tensor(out=e1[:], in0=ps[:, 0:512], in1=PA("w0"), op=ALU.add), [pk, "pA"], ["e1"])
                act(lambda e: e.activation(out=wsig[:], in_=e1[:], func=AF.Sigmoid), ["e1"], ["wsig"])
                dve(lambda e, ps=ps: e.tensor_tensor(out=e2[:], in0=ps[:, 512:1024], in1=PA("a0"), op=ALU.add), [pk, "pA"], ["e2"])
                act(lambda e: e.activation(out=a_sb[:], in_=e2[:], func=AF.Sigmoid), ["e2"], ["a_sb"])
                ps, psb, pk = nps()
                mm(ps[:, 0:512], lorT[:, 1, :], gup[:, :], True, True, ["lorT", "gup"], [pk])
                act(lambda e, ps=ps: e.copy(out=g_sb[:], in_=ps[:, 0:512]), [pk], ["g_sb"])
                yield
                dve(lambda e: e.tensor_tensor(out=e1[:], in0=kf_sb[:], in1=PA("kk"), op=ALU.mult), ["kf_sb", "pA"], ["e1"])
                dve(lambda e: e.tensor_tensor(out=e2[:], in0=e1[:], in1=e1[:], op=ALU.mult), ["e1"], ["e2"])
                dve(lambda e: e.tensor_reduce(out=r8b[:, 0:8], in_=v3(e2), axis=AX.X, op=ALU.add), ["e2"], ["r8b"])
                dve(lambda e: e.tensor_scalar_max(out=r8b[:, 0:8], in0=r8b[:, 0:8], scalar1=1e-24), ["r8b"], ["r8b"])
                act(lambda e: e.activation(out=r8b[:, 0:8], in_=r8b[:, 0:8], func=AF.Sqrt), ["r8b"], ["r8b"])
                dve(lambda e: e.reciprocal(out=r8b[:, 0:8], in_=r8b[:, 0:8]), ["r8b"], ["r8b"])
                dve(lambda e: e.tensor_tensor(out=v3(kap), in0=v3(e1), in1=bc8(r8b[:, 0:8]), op=ALU.mult), ["e1", "r8b"], ["kap"])
                dve(lambda e: e.tensor_scalar_add(out=e2[:], in0=a_sb[:], scalar1=-1.0), ["a_sb"], ["e2"])
                dve(lambda e: e.tensor_tensor(out=e2[:], in0=e2[:], in1=PA("ka"), op=ALU.mult), ["e2", "pA"], ["e2"])
                dve(lambda e: e.tensor_tensor(out=e2[:], in0=e2[:], in1=kf_sb[:], op=ALU.mult), ["e2", "kf_sb"], ["e2"])
                dve(lambda e: e.tensor_tensor(out=ktl[:], in0=e2[:], in1=kf_sb[:], op=ALU.add), ["e2", "kf_sb"], ["ktl"])
                dve(lambda e: e.tensor_tensor(out=bvec[:], in0=a_sb[:], in1=kap[:], op=ALU.mult), ["a_sb", "kap"], ["bvec"])
                dve(lambda e: e.tensor_tensor(out=e2[:], in0=r_sb[:], in1=ktl[:], op=ALU.mult), ["r_sb", "ktl"], ["e2"])
                dve(lambda e: e.tensor_tensor(out=e2[:], in0=e2[:], in1=PA("rk"), op=ALU.mult), ["e2", "pA"], ["e2"])
                dve(lambda e: e.tensor_reduce(out=bon[:], in_=v3(e2), axis=AX.X, op=ALU.add), ["e2"], ["bon"])
                yield
                ps, psb, pk = nps()
                mm(ps[:, 0:512], MUI, wsig[:], True, True, ["pA", "wsig"], [pk])
                mm(ps[:, 512:1024], ONES, wsig[:], True, True, ["pA", "wsig"], [pk])
                act(lambda e, ps=ps: e.copy(out=pcw[:], in_=ps[:, 0:512]), [pk], ["pcw"])
                dve(lambda e: e.tensor_tensor(out=e1[:], in0=pcw[:], in1=wsig[:], op=ALU.subtract), ["pcw", "wsig"], ["e1"])
                act(lambda e: e.activation(out=e1[:], in_=e1[:], func=AF.Exp, scale=-C0), ["e1"], ["e1"])
                dve(lambda e: e.tensor_tensor(out=TMb[:, 0, :], in0=kap[:], in1=e1[:], op=ALU.mult), ["kap", "e1"], ["TMb0"])
                act(lambda e: e.activation(out=e2[:], in_=pcw[:], func=AF.Exp, scale=-C0), ["pcw"], ["e2"])
                dve(lambda e: e.tensor_tensor(out=TMb[:, 1, :], in0=r_sb[:], in1=e2[:], op=ALU.mult), ["r_sb", "e2"], ["TMb1"])
                act(lambda e: e.activation(out=e3[:], in_=pcw[:], func=AF.Exp, scale=C0), ["pcw"], ["e3"])
                dve(lambda e: e.tensor_tensor(out=TMb[:, 2, :], in0=bvec[:], in1=e3[:], op=ALU.mult), ["bvec", "e3"], ["TMb2"])
                dve(lambda e: e.tensor_tensor(out=TMb[:, 3, :], in0=ktl[:], in1=e3[:], op=ALU.mult), ["ktl", "e3"], ["TMb3"])
                dve(lambda e, ps=ps: e.tensor_tensor(out=e1[:], in0=ps[:, 512:1024], in1=pcw[:], op=ALU.subtract), [pk, "pcw"], ["e1"])
                act(lambda e: e.activation(out=e1[:], in_=e1[:], func=AF.Exp, scale=-C0), ["e1"], ["e1"])
                for hp in range(2):
                    srcb = v3(bvec).rearrange("p (j q) d -> p j q d", q=2)[:, :, hp, :]
                    srck = v3(ktl).rearrange("p (j q) d -> p j q d", q=2)[:, :, hp, :]
                    wl = v3(e1).rearrange("p (j q) d -> p j q d", q=2)[:, :, hp, :]
                    dstb = Btz[:].rearrange("p (j q) c -> p j q c", q=2)[:, :, hp, hp * 64:hp * 64 + 64]
                    dstk = Ktz[:].rearrange("p (j q) c -> p j q c", q=2)[:, :, hp, hp * 64:hp * 64 + 64]
                    dve(lambda e, srcb=srcb, wl=wl, dstb=dstb: e.tensor_tensor(out=dstb, in0=srcb, in1=wl, op=ALU.mult), ["bvec", "e1"], ["Btz"])
                    dve(lambda e, srck=srck, wl=wl, dstk=dstk: e.tensor_tensor(out=dstk, in0=srck, in1=wl, op=ALU.mult), ["ktl", "e1"], ["Ktz"])
                ps2, _, pk2 = nps()
                for j in range(4):
                    mm(ps2[:, j:j + 1], wsig[:, j * 128:(j + 1) * 128], ONES[:, 0:1], True, True, ["wsig", "pA"], [pk2])
                act(lambda e, ps2=ps2: e.activation(out=WLfm[:], in_=ps2[:, 0:4], func=AF.Exp, scale=-C0), [pk2], ["WLfm"])
                yield
                ps, psb, pk = nps()
                for w_ in range(4):
                    for j in range(4):
                        pe(lambda e, w_=w_, j=j, psb=psb: e.transpose(psb[:, (w_ * 4 + j) * 128:(w_ * 4 + j + 1) * 128], TMb[:, w_, j * 128:(j + 1) * 128], identb[:]), ["TMb%d" % w_, "identb"], [pk])
                for w_ in range(4):
                    eng_ = act if w_ % 2 == 0 else dve
                    if w_ % 2 == 0:
                        act(lambda e, psb=psb, w_=w_: e.copy(out=FMt[:, w_, :, :], in_=psb[:, w_ * 512:(w_ + 1) * 512].rearrange("p (j t) -> p j t", j=4)), [pk], ["FMt"])
                    else:
                        dve(lambda e, psb=psb, w_=w_: e.tensor_copy(out=FMt[:, w_, :, :], in_=psb[:, w_ * 512:(w_ + 1) * 512].rearrange("p (j t) -> p j t", j=4)), [pk], ["FMt"])
                KAP, RB, BB, KKB = 0, 1, 2, 3

                def amat(lw, rw_, dst, dk, mask, neg):
                    ps, psb, pk = nps()
                    for h in range(8):
                        j, hp = h // 2, h % 2
                        sl = slice(hp * 64, hp * 64 + 64)
                        mm(ps[:, hoff(h):hoff(h) + 128], FMt[sl, lw, j, :], FMt[sl, rw_, j, :], True, True, ["FMt"], [pk])
                    psv = ps[:, :].rearrange("p (q j t) -> p q j t", q=2, j=4)
                    dstv = dst[:].rearrange("p (j q) t -> p q j t", q=2)
                    mk = mask.unsqueeze(1).unsqueeze(1).to_broadcast([128, 2, 4, 128])
                    if neg:
                        mk3 = mask.unsqueeze(1).to_broadcast([128, 4, 128])
                        for q in range(2):
                            dve(lambda e, q=q: e.scalar_tensor_tensor(out=dstv[:, q], in0=psv[:, q], scalar=-1.0, in1=mk3, op0=ALU.mult, op1=ALU.mult), [pk, "pA"], [dk])
                    else:
                        mk3 = mask.unsqueeze(1).to_broadcast([128, 4, 128])
                        for q in range(2):
                            dve(lambda e, q=q: e.tensor_tensor(out=dstv[:, q], in0=psv[:, q], in1=mk3, op=ALU.mult), [pk, "pA"], [dk])

                amat(BB, KAP, Pw[1], "Pw1", MUS, True)
                amat(KAP, BB, Pw[0], "Pw0", MLS, True)
                amat(KKB, KAP, Am[0], "Am0", MUS, False)
                amat(BB, RB, Am[1], "Am1", MUI, False)
                amat(KKB, RB, Am[2], "Am2", MUI, False)
                yield
                ps, psb, pk = nps()
                for h in range(8):
                    j, hp = h // 2, h % 2
                    sl = slice(hp * 64, hp * 64 + 64)
                    mm(ps[:, h * 64:(h + 1) * 64], FMt[sl, KAP, j, :], Sbf[sl, j, :], True, False, ["FMt", "Sbf"], [pk])
                    mm(ps[:, h * 64:(h + 1) * 64], Am[0][:, h, :], vrw[:, h, :], False, True, ["Am0", "vrw"], [pk])
                act(lambda e, ps=ps: e.copy(out=Zb[0][:], in_=ps[:, 0:512].rearrange("p (h d) -> p h d", h=8)), [pk], ["Zb0"])
                pi = 0
                zi = 0
                for lvl in range(7):
                    yield
                    Pc, PTc = Pw[pi], Pw[pi + 1]
                    Pk, PTk = "Pw%d" % pi, "Pw%d" % (pi + 1)
                    Zc, Zn = Zb[zi], Zb[1 - zi]
                    ps, psb, pk = nps()
                    for h in range(8):
                        mm(ps[:, h * 64:(h + 1) * 64], identb[:], Zc[:, h, :], True, False, ["identb", "Zb%d" % zi], [pk])
                        mm(ps[:, h * 64:(h + 1) * 64], PTc[:, h, :], Zc[:, h, :], False, True, [PTk, "Zb%d" % zi], [pk])
                    if lvl < 6:
                        act(lambda e, ps=ps, Zn=Zn: e.copy(out=Zn[:], in_=ps[:, 0:512].rearrange("p (h d) -> p h d", h=8)), [pk], ["Zb%d" % (1 - zi)])
                        zi = 1 - zi
                        ni = 2 - pi
                        Pn, PTn = Pw[ni], Pw[ni + 1]
                        psA, _, pkA = nps()
                        for h in range(8):
                            mm(psA[:, h * 128:(h + 1) * 128], PTc[:, h, :], Pc[:, h, :], True, True, [PTk, Pk], [pkA])
                        for a_ in range(2):
                            dve(lambda e, psA=psA, Pn=Pn, a_=a_: e.tensor_copy(out=Pn[:, 4 * a_:4 * a_ + 4, :], in_=psA[:, a_ * 512:(a_ + 1) * 512].rearrange("p (h t) -> p h t", h=4)), [pkA], ["Pw%d" % ni])
                        psB, _, pkB = nps()
                        for h in range(8):
                            mm(psB[:, h * 128:(h + 1) * 128], Pc[:, h, :], PTc[:, h, :], True, True, [Pk, PTk], [pkB])
                        for a_ in range(2):
                            act(lambda e, psB=psB, PTn=PTn, a_=a_: e.copy(out=PTn[:, 4 * a_:4 * a_ + 4, :], in_=psB[:, a_ * 512:(a_ + 1) * 512].rearrange("p (h t) -> p h t", h=4)), [pkB], ["Pw%d" % (ni + 1)])
                        pi = ni
                    else:
                        act(lambda e, ps=ps: e.activation(out=Ub[:], in_=ps[:, 0:512].rearrange("p (h d) -> p h d", h=8), func=AF.Copy, scale=-1.0), [pk], ["Ub"])
                yield
                ps, psb, pk = nps()
                for h in range(8):
                    j, hp = h // 2, h % 2
                    sl = slice(hp * 64, hp * 64 + 64)
                    o_ = ps[:, h * 64:(h + 1) * 64]
                    mm(o_, Am[1][:, h, :], Ub[:, h, :], True, False, ["Am1", "Ub"], [pk])
                    mm(o_, Am[2][:, h, :], vrw[:, h, :], False, False, ["Am2", "vrw"], [pk])
                    mm(o_, FMt[sl, RB, j, :], Sbf[sl, j, :], False, True, ["FMt", "Sbf"], [pk])
                act(lambda e, ps=ps: e.copy(out=ysb[:], in_=ps[:, 0:512]), [pk], ["ysb"])
                yield
                ps, psb, pk = nps()
                for j in range(4):
                    o_ = ps[:, j * 64:(j + 1) * 64]
                    mm(o_, Btz[:, 2 * j, :], Ub[:, 2 * j, :], True, False, ["Btz", "Ub"], [pk])
                    mm(o_, Ktz[:, 2 * j, :], vrw[:, 2 * j, :], False, False, ["Ktz", "vrw"], [pk])
                    mm(o_, Btz[:, 2 * j + 1, :], Ub[:, 2 * j + 1, :], False, False, ["Btz", "Ub"], [pk])
                    mm(o_, Ktz[:, 2 * j + 1, :], vrw[:, 2 * j + 1, :], False, True, ["Ktz", "vrw"], [pk])
                dve(lambda e: e.tensor_tensor(out=Sst[:], in0=Sst[:], in1=WLfm[:].unsqueeze(2).to_broadcast([128, 4, 64]), op=ALU.mult), ["Sst", "WLfm"], ["Sst"])
                dve(lambda e, ps=ps: e.tensor_tensor(out=Sst[:], in0=Sst[:], in1=ps[:, 0:256].rearrange("p (j d) -> p j d", j=4), op=ALU.add), [pk, "Sst"], ["Sst"])
                act(lambda e: e.copy(out=Sbf[:], in_=Sst[:]), ["Sst"], ["Sbf"])
                yield
            def tail():
                head_rw(128, ysb, "ysb", bon, "bon", vf, "vf", g_sb, "g_sb", mix, "mix", {"tA": G[18], "tB": G[19], "s8": W["s8"]})
                out_proj(128, mix, "mix", None, None, b * 128, W)

            return front_stage, ml_stage, rw_stage, tail, p_

        def run_gens(gl):
            gl = list(gl)
            while gl:
                for item in list(gl):
                    CURP[0] = item[1]
                    CURPOOL[0] = item[3] if len(item) > 3 else "A"
                    for _ in range(item[2] if len(item) > 2 else 1):
                        try:
                            next(item[0])
                        except StopIteration:
                            gl.remove(item)
                            break

        blocks = [make_block(b) for b in range(NB)]
        run_gens([(blocks[0][0](), 0, 1, "A")])
        for b in range(NB):
            fr, ml_, rw_, tl, p_ = blocks[b]
            gl = [(rw_(), p_, 1, "A"), (ml_(), p_, 1, "A")]
            if b + 1 < NB:
                gl.append((blocks[b + 1][0](), (b + 1) % 2, 1, "A"))
            run_gens(gl)
            CURP[0] = p_
            CURPOOL[0] = "A"
            tl()
        CURP[0] = 0

        ps, psb, pk = nps()
        mm(ps[0:8, 0:128], runmax[:], IDF, True, True, ["runmax", "pA"], [pk])
        mm(ps[0:8, 128:256], nBc[:], IDF, True, True, ["nBc", "pA"], [pk])
        fs = TT("fs", [8, 16])
        dve(lambda e, ps=ps: e.tensor_reduce(out=fs[:, 0:1], in_=ps[0:8, 0:128], axis=AX.X, op=ALU.max), [pk], ["fs"])
        dve(lambda e: e.tensor_scalar_max(out=fs[:, 0:1], in0=fs[:, 0:1], scalar1=0.0), ["fs"], ["fs"])
        dve(lambda e, ps=ps: e.tensor_tensor(out=fs[:, 1:2], in0=fs[:, 0:1], in1=ps[0:8, 128:129], op=ALU.subtract), [pk, "fs"], ["fs"])
        dma("pool", om, fs[:, 1:2], ["fs"], [], "fin")
        act(lambda e: e.activation(out=fs[:, 2:3], in_=fs[:, 1:2], func=AF.Exp, scale=-1.0), ["fs"], ["fs"])
        dve(lambda e: e.tensor_scalar_mul(out=fs[:, 4:8], in0=pA[0:8, OFF["rsel"][0]:OFF["rsel"][1]], scalar1=fs[:, 2:3]), ["fs", "pA"], ["fs"])
        ps, psb, pk = nps()
        mm(ps[:, 0:4], pA[0:8, OFF["lsel"][0]:OFF["lsel"][1]], fs[:, 4:8], True, True, ["pA", "fs"], [pk])
        scb = TT("scb", [128, 4])
        act(lambda e, ps=ps: e.copy(out=scb[:], in_=ps[:, 0:4]), [pk], ["scb"])
        dve(lambda e: e.tensor_tensor(out=Cst[:], in0=Cst[:], in1=scb[:].unsqueeze(2).to_broadcast([128, 4, 65]), op=ALU.mult), ["Cst", "scb"], ["Cst"])
        dma("pool", oC, Cst[:], ["Cst"], [], "fin")
        dma("pool", oS, Sst[:], ["Sst"], [], "fin")
        PP[0].finalize()
        PP[0] = Prog(ctx)

    with contextlib.ExitStack() as es_s:
        cur[0] = es_s
        if do_sample:
            W = {}
            W["xm"] = TT("s_xm", [128, D])
            junk = W["xm"]
            st = TT("s_st", [128, 4])
            mix = TT("s_mix", [128, D], BF16)
            xsb = mix
            lor = TT("s_lor", [128, 256], BF16)
            lorT = TT("s_lorT", [128, 2, 128], BF16)
            W["mixT"] = TT("s_mixT", [128, 8, 128], BF16)
            sx = TT("sx", [NS, D])
            sxT = TT("sxT", [128, 8, NS], BF16)
            spj = TT("spj", [NS, INW])
            spk = TT("spk", [128, NSP])
            sl_t = TT("sl_t", [NS, 3, 256])
            Cs = TT("Cs", [128, 4096])
            Ss = Cs
            sn_t = TT("sn_t", [128, 64])
            sm_t = TT("sm_t", [128, 1])
            scv = TT("scv", [128, 2, 4, 64])
            ssh = TT("ssh", [128, 3, 64])
            dma("sp", sx[:], xs, [], ["sx"], "sin", True)
            dma("sp", spk[:], spk_d, [], ["spk"], "sin", True)
            dma("sp", sl_t[:, 0, :], sshl_d, [], ["sl_t"], "sin", True)
            dma("sp", sl_t[:, 1, :], mul_d, [], ["sl_t"], "sin", True)
            dma("sp", Cs[:], sC_d, [], ["Cs"], "sin", True)
            dma("sp", sn_t[:], sn_d, [], ["sn_t"], "sin", True)
            dma("sp", sm_t[:], sm_d, [], ["sm_t"], "sin", True)
            dma("sp", scv[:, :, 0:3, :], sconv_d, [], ["scv"], "sin", True)
            dma("sp", ssh[:], sshift_d, [], ["ssh"], "sin", True)
            rmsnorm_T(sx, "sx", NS, sxT, "sxT", 0, "nmw", xsb, "xsb", junk, "junk", st, "st")
            for n0 in range(0, INW, 512):
                nn = min(512, INW - n0)
                ps, psb, pk = nps()
                for c in range(8):
                    mm(ps[:NS, 0:nn], sxT[:, c, :], wq[:, c, n0:n0 + nn], c == 0, c == 7, ["sxT", "wq"], [pk])
                act(lambda e, ps=ps, n0=n0, nn=nn: e.copy(out=spj[:, n0:n0 + nn], in_=ps[:NS, 0:nn]), [pk], ["spj"])
            s1v = scr1.rearrange("(b h) a d -> b a h d", b=NS)
            for a_ in range(7):
                c0_ = a_ * 512 if a_ < 4 else MLW + (a_ - 4) * 512
                dma("pool", s1v[:, a_, :, :], spj[:, c0_:c0_ + 512].rearrange("p (h d) -> p h d", h=8), ["spj"], ["scr1"], "scrw1")
            A7 = TT("A7", [128, 8, 64])
            dma("sp", A7[:, 0:7, :], scr1[:, 0:7, :], ["scr1"], ["A7"], "scrr1")
            gif = TT("gif", [128, 2])
            s_if = nc.dram_tensor("scr_if", [2, 128], F32, kind="Internal").ap()
            for g_ in range(2):
                dma("pool", s_if[g_, :].rearrange("(b h) -> b h", b=NS), spj[:, 2048 + 8 * g_:2056 + 8 * g_], ["spj"], ["scr_if"], "scrwif")
            for g_ in range(2):
                dma("sp", gif[:, g_:g_ + 1], s_if[g_, :].rearrange("(p o) -> p o", o=1), ["scr_if"], ["gif"], "scrrif")
            SP_ = lambda n: spk[:, SOFF[n][0]:SOFF[n][1]]
            pl = spj[:, MLW + 1536:MLW + 1792]
            dma("pool", osshl, pl, ["spj"], [], "fin")
            dve(lambda e: e.tensor_tensor(out=sl_t[:, 2, :], in0=sl_t[:, 0, :], in1=pl, op=ALU.subtract), ["sl_t", "spj"], ["sl_t"])
            dve(lambda e: e.tensor_tensor(out=sl_t[:, 2, :], in0=sl_t[:, 2, :], in1=sl_t[:, 1, :], op=ALU.mult), ["sl_t"], ["sl_t"])
            dve(lambda e: e.tensor_tensor(out=sl_t[:, 2, :], in0=sl_t[:, 2, :], in1=pl, op=ALU.add), ["sl_t", "spj"], ["sl_t"])
            act(lambda e: e.activation(out=lor[:NS, 0:64], in_=sl_t[:, 2, 0:64], func=AF.Tanh), ["sl_t"], ["lor"])
            act(lambda e: e.copy(out=lor[:NS, 64:128], in_=sl_t[:, 2, 64:128]), ["sl_t"], ["lor"])
            act(lambda e: e.activation(out=lor[:NS, 128:256], in_=sl_t[:, 2, 128:256], func=AF.Sigmoid), ["sl_t"], ["lor"])
            ps, psb, pk = nps()
            pe(lambda e, psb=psb: e.transpose(psb[:, 0:NS], lor[:NS, 0:128], identb[:NS, :NS]), ["lor", "identb"], [pk])
            pe(lambda e, psb=psb: e.transpose(psb[:, 128:128 + NS], lor[:NS, 128:256], identb[:NS, :NS]), ["lor", "identb"], [pk])
            act(lambda e, psb=psb: e.copy(out=lorT[:, :, 0:NS], in_=psb[:, 0:256].rearrange("p (a t) -> p a t", a=2)[:, :, 0:NS]), [pk], ["lorT"])
            ps, psb, pk = nps()
            mm(ps[:NS, 0:512], lorT[0:64, 0, 0:NS], luw[0:64, :], True, True, ["lorT", "luw"], [pk])
            mm(ps[:NS, 512:1024], lorT[64:128, 0, 0:NS], luw[64:128, :], True, True, ["lorT", "luw"], [pk])
            ps2, _, pk2 = nps()
            mm(ps2[:NS, 0:512], lorT[:, 1, 0:NS], gup[:, :], True, True, ["lorT", "gup"], [pk2])
            lo3 = spj[:, 0:1536].rearrange("p (a n) -> p a n", a=3)
            for a_ in range(2):
                act(lambda e, ps=ps, a_=a_: e.copy(out=lo3[:, a_, :], in_=ps[:NS, a_ * 512:(a_ + 1) * 512]), [pk], ["lo3"])
            act(lambda e, ps2=ps2: e.copy(out=lo3[:, 2, :], in_=ps2[:NS, 0:512]), [pk2], ["lo3"])
            for a_ in range(3):
                dma("pool", scr2.rearrange("(b h) a d -> b a h d", b=NS)[:, a_, :, :], lo3[:, a_, :].rearrange("p (h d) -> p h d", h=8), ["lo3"], ["scr2"], "scrw2")
            L3 = TT("L3", [128, 3, 64])
            dma("sp", L3[:], scr2, ["scr2"], ["L3"], "scrr2")
            big = TT("big", [128, 4096])
            sv = TT("sv", [128, 64])
            pool(lambda e: e.tensor_copy(out=scv[:, :, 3, :], in_=A7[:, 0:2, :]), ["A7"], ["scv"])
            dma("pool", osconv, scv[:, :, 1:4, :], ["scv"], [], "fin")
            qk_s = TT("qk_s", [128, 2, 64])
            cwqk = lambda w_: spk[:, SOFF["cwq"][0] + w_ * 256:SOFF["cwq"][0] + (w_ + 1) * 256].rearrange("p (j d) -> p j d", j=4)
            for w_ in range(2):
                dve(lambda e, w_=w_: e.tensor_tensor(out=big[:, 0:256].rearrange("p (j d) -> p j d", j=4), in0=scv[:, w_, :, :], in1=cwqk(w_), op=ALU.mult), ["scv", "spk"], ["big"])
                dve(lambda e, w_=w_: e.tensor_reduce(out=qk_s[:, w_, :], in_=big[:, 0:256].rearrange("p (j d) -> p d j", j=4), axis=AX.X, op=ALU.add), ["big"], ["qk_s"])
            dve(lambda e: e.tensor_tensor(out=qk_s[:], in0=qk_s[:], in1=spk[:, SOFF["cbq"][0]:SOFF["cbk"][1]].rearrange("p (a d) -> p a d", a=2), op=ALU.add), ["qk_s", "spk"], ["qk_s"])
            act(lambda e: e.activation(out=qk_s[:], in_=qk_s[:], func=AF.Silu), ["qk_s"], ["qk_s"])
            act(lambda e: e.activation(out=qk_s[:, 1, :], in_=qk_s[:, 1, :], func=AF.Copy, scale=0.125), ["qk_s"], ["qk_s"])
            dve(lambda e: e.tensor_tensor(out=sv[:, 0:2], in0=gif[:], in1=spk[:, SOFF["ib"][0]:SOFF["fb"][1]], op=ALU.add), ["gif", "spk"], ["sv"])
            act(lambda e: e.activation(out=sv[:, 9:10], in_=sv[:, 1:2], func=AF.Exp, scale=-1.0), ["sv"], ["sv"])
            act(lambda e: e.activation(out=sv[:, 2:3], in_=sv[:, 9:10], func=AF.Ln, bias=1.0, scale=1.0), ["sv"], ["sv"])
            dve(lambda e: e.tensor_tensor(out=sv[:, 3:4], in0=sm_t[:], in1=sv[:, 2:3], op=ALU.subtract), ["sv", "sm_t"], ["sv"])
            dve(lambda e: e.tensor_tensor(out=sv[:, 4:5], in0=sv[:, 3:4], in1=sv[:, 0:1], op=ALU.max), ["sv"], ["sv"])
            dma("pool", osm, sv[:, 4:5], ["sv"], [], "fin")
            dve(lambda e: e.tensor_tensor(out=sv[:, 9:10], in0=sv[:, 0:1], in1=sv[:, 4:5], op=ALU.subtract), ["sv"], ["sv"])
            act(lambda e: e.activation(out=sv[:, 5:6], in_=sv[:, 9:10], func=AF.Exp), ["sv"], ["sv"])
            dve(lambda e: e.tensor_tensor(out=sv[:, 9:10], in0=sv[:, 3:4], in1=sv[:, 4:5], op=ALU.subtract), ["sv"], ["sv"])
            act(lambda e: e.activation(out=sv[:, 6:7], in_=sv[:, 9:10], func=AF.Exp), ["sv"], ["sv"])
            act(lambda e: e.activation(out=sv[:, 7:8], in_=sv[:, 4:5], func=AF.Exp, scale=-1.0), ["sv"], ["sv"])
            q_ = qk_s[:, 0, :]
            k_ = qk_s[:, 1, :]
            v_ = A7[:, 2, :]
            b3 = lambda t: t[:, :].rearrange("p (a c) -> p a c", a=64)
            pool(lambda e: e.tensor_tensor(out=b3(big), in0=k_.unsqueeze(2).to_broadcast([128, 64, 64]), in1=v_.unsqueeze(1).to_broadcast([128, 64, 64]), op=ALU.mult), ["qk_s", "A7"], ["big"])
            dve(lambda e: e.tensor_scalar_mul(out=Cs[:], in0=Cs[:], scalar1=sv[:, 6:7]), ["Cs", "sv"], ["Cs"])
            dve(lambda e: e.scalar_tensor_tensor(out=Cs[:], in0=big[:], scalar=sv[:, 5:6], in1=Cs[:], op0=ALU.mult, op1=ALU.add), ["big", "sv", "Cs"], ["Cs"])
            dma("pool", osC, Cs[:], ["Cs"], [], "fin")
            dve(lambda e: e.tensor_scalar_mul(out=sn_t[:], in0=sn_t[:], scalar1=sv[:, 6:7]), ["sn_t", "sv"], ["sn_t"])
            dve(lambda e: e.scalar_tensor_tensor(out=sn_t[:], in0=k_, scalar=sv[:, 5:6], in1=sn_t[:], op0=ALU.mult, op1=ALU.add), ["qk_s", "sv", "sn_t"], ["sn_t"])
            dma("pool", osn, sn_t[:], ["sn_t"], [], "fin")
            pool(lambda e: e.tensor_tensor(out=b3(big), in0=Cs[:, :].rearrange("p (k v) -> p v k", k=64), in1=q_.unsqueeze(1).to_broadcast([128, 64, 64]), op=ALU.mult), ["Cs", "qk_s"], ["big"])
            hs = TT("hs", [128, 2, 64])
            dve(lambda e: e.tensor_reduce(out=hs[:, 0, :], in_=b3(big), axis=AX.X, op=ALU.add), ["big"], ["hs"])
            dve(lambda e: e.tensor_tensor(out=sv[:, 16:80 - 16] if False else big[:, 0:64], in0=q_, in1=sn_t[:], op=ALU.mult), ["qk_s", "sn_t"], ["big"])
            dve(lambda e: e.tensor_reduce(out=sv[:, 8:9], in_=big[:, 0:64], axis=AX.X, op=ALU.add), ["big"], ["sv"])
            dve(lambda e: e.scalar_tensor_tensor(out=sv[:, 9:10], in0=sv[:, 8:9], scalar=-1.0, in1=sv[:, 8:9], op0=ALU.mult, op1=ALU.max), ["sv"], ["sv"])
            dve(lambda e: e.tensor_tensor(out=sv[:, 9:10], in0=sv[:, 9:10], in1=sv[:, 7:8], op=ALU.max), ["sv"], ["sv"])
            dve(lambda e: e.reciprocal(out=sv[:, 10:11], in_=sv[:, 9:10]), ["sv"], ["sv"])
            dve(lambda e: e.tensor_scalar_mul(out=hs[:, 0, :], in0=hs[:, 0, :], scalar1=sv[:, 10:11]), ["hs", "sv"], ["hs"])
            dma("pool", osshift, A7[:, 4:7, :], ["A7"], [], "fin")
            rk3 = TT("rk3", [128, 3, 64])
            mu3 = spk[:, SOFF["mu_r"][0]:SOFF["mu_v"][1]].rearrange("p (a d) -> p a d", a=3)
            dve(lambda e: e.tensor_tensor(out=rk3[:], in0=ssh[:], in1=A7[:, 4:7, :], op=ALU.subtract), ["ssh", "A7"], ["rk3"])
            dve(lambda e: e.tensor_tensor(out=rk3[:], in0=rk3[:], in1=mu3, op=ALU.mult), ["rk3", "spk"], ["rk3"])
            dve(lambda e: e.tensor_tensor(out=rk3[:], in0=rk3[:], in1=A7[:, 4:7, :], op=ALU.add), ["rk3", "A7"], ["rk3"])
            w8 = TT("w8", [128, 8, 64])
            dve(lambda e: e.tensor_tensor(out=w8[:, 0, :], in0=L3[:, 0, :], in1=SP_("w0"), op=ALU.add), ["L3", "spk"], ["w8"])
            act(lambda e: e.activation(out=w8[:, 0, :], in_=w8[:, 0, :], func=AF.Sigmoid), ["w8"], ["w8"])
            act(lambda e: e.activation(out=w8[:, 0, :], in_=w8[:, 0, :], func=AF.Exp, scale=-C0), ["w8"], ["w8"])
            dve(lambda e: e.tensor_tensor(out=w8[:, 1, :], in0=L3[:, 1, :], in1=SP_("a0"), op=ALU.add), ["L3", "spk"], ["w8"])
            act(lambda e: e.activation(out=w8[:, 1, :], in_=w8[:, 1, :], func=AF.Sigmoid), ["w8"], ["w8"])
            dve(lambda e: e.tensor_tensor(out=w8[:, 6, :], in0=rk3[:, 1, :], in1=SP_("kk"), op=ALU.mult), ["rk3", "spk"], ["w8"])
            dve(lambda e: e.tensor_tensor(out=w8[:, 7, :], in0=w8[:, 6, :], in1=w8[:, 6, :], op=ALU.mult), ["w8"], ["w8"])
            dve(lambda e: e.tensor_reduce(out=sv[:, 11:12], in_=w8[:, 7, :], axis=AX.X, op=ALU.add), ["w8"], ["sv"])
            dve(lambda e: e.tensor_scalar_max(out=sv[:, 11:12], in0=sv[:, 11:12], scalar1=1e-24), ["sv"], ["sv"])
            act(lambda e: e.activation(out=sv[:, 11:12], in_=sv[:, 11:12], func=AF.Sqrt), ["sv"], ["sv"])
            dve(lambda e: e.reciprocal(out=sv[:, 11:12], in_=sv[:, 11:12]), ["sv"], ["sv"])
            dve(lambda e: e.tensor_scalar_mul(out=w8[:, 3, :], in0=w8[:, 6, :], scalar1=sv[:, 11:12]), ["w8", "sv"], ["w8"])
            dve(lambda e: e.tensor_tensor(out=w8[:, 6, :], in0=w8[:, 1, :], in1=SP_("ka"), op=ALU.mult), ["w8", "spk"], ["w8"])
            dve(lambda e: e.tensor_tensor(out=w8[:, 6, :], in0=w8[:, 6, :], in1=SP_("ka"), op=ALU.subtract), ["w8", "spk"], ["w8"])
            dve(lambda e: e.tensor_scalar_add(out=w8[:, 6, :], in0=w8[:, 6, :], scalar1=1.0), ["w8"], ["w8"])
            dve(lambda e: e.tensor_tensor(out=w8[:, 4, :], in0=rk3[:, 1, :], in1=w8[:, 6, :], op=ALU.mult), ["rk3", "w8"], ["w8"])
            dve(lambda e: e.tensor_tensor(out=w8[:, 5, :], in0=w8[:, 1, :], in1=w8[:, 3, :], op=ALU.mult), ["w8"], ["w8"])
            dma("sp", Ss[:], sS_d, [], ["Ss"], "sin2")
            bk = lambda ap: ap.unsqueeze(1).to_broadcast([128, 64, 64])
            bv = lambda ap: ap.unsqueeze(2).to_broadcast([128, 64, 64])
            pool(lambda e: e.tensor_tensor(out=b3(big), in0=b3(Ss), in1=bk(w8[:, 3, :]), op=ALU.mult), ["Ss", "w8"], ["big"])
            dve(lambda e: e.tensor_reduce(out=w8[:, 7, :], in_=b3(big), axis=AX.X, op=ALU.add), ["big"], ["w8"])
            dve(lambda e: e.tensor_tensor(out=b3(Ss), in0=b3(Ss), in1=bk(w8[:, 0, :]), op=ALU.mult), ["Ss", "w8"], ["Ss"])
            pool(lambda e: e.tensor_tensor(out=b3(big), in0=bv(w8[:, 7, :]), in1=bk(w8[:, 5, :]), op=ALU.mult), ["w8"], ["big"])
            dve(lambda e: e.tensor_tensor(out=Ss[:], in0=Ss[:], in1=big[:], op=ALU.subtract), ["Ss", "big"], ["Ss"])
            pool(lambda e: e.tensor_tensor(out=b3(big), in0=bv(rk3[:, 2, :]), in1=bk(w8[:, 4, :]), op=ALU.mult), ["rk3", "w8"], ["big"])
            dve(lambda e: e.tensor_tensor(out=Ss[:], in0=Ss[:], in1=big[:], op=ALU.add), ["Ss", "big"], ["Ss"])
            dma("pool", osS, Ss[:], ["Ss"], [], "fin")
            pool(lambda e: e.tensor_tensor(out=b3(big), in0=b3(Ss), in1=bk(rk3[:, 0, :]), op=ALU.mult), ["Ss", "rk3"], ["big"])
            dve(lambda e: e.tensor_reduce(out=hs[:, 1, :], in_=b3(big), axis=AX.X, op=ALU.add), ["big"], ["hs"])
            dve(lambda e: e.tensor_tensor(out=w8[:, 6, :], in0=rk3[:, 0, :], in1=w8[:, 4, :], op=ALU.mult), ["rk3", "w8"], ["w8"])
            dve(lambda e: e.tensor_tensor(out=w8[:, 6, :], in0=w8[:, 6, :], in1=SP_("rk"), op=ALU.mult), ["w8", "spk"], ["w8"])
            dve(lambda e: e.tensor_reduce(out=sv[:, 12:13], in_=w8[:, 6, :], axis=AX.X, op=ALU.add), ["w8"], ["sv"])
            dve(lambda e: e.scalar_tensor_tensor(out=hs[:, 1, :], in0=rk3[:, 2, :], scalar=sv[:, 12:13], in1=hs[:, 1, :], op0=ALU.mult, op1=ALU.add), ["rk3", "sv", "hs"], ["hs"])
            act(lambda e: e.activation(out=w8[:, 6, :], in_=A7[:, 3, :], func=AF.Sigmoid), ["A7"], ["w8"])
            dve(lambda e: e.tensor_tensor(out=hs[:, 0, :], in0=hs[:, 0, :], in1=w8[:, 6, :], op=ALU.mult), ["hs", "w8"], ["hs"])
            dve(lambda e: e.tensor_tensor(out=w8[:, 7, :], in0=hs[:, 0, :], in1=hs[:, 0, :], op=ALU.mult), ["hs"], ["w8"])
            dve(lambda e: e.tensor_reduce(out=sv[:, 13:14], in_=w8[:, 7, :], axis=AX.X, op=ALU.add), ["w8"], ["sv"])
            act(lambda e: e.activation(out=sv[:, 13:14], in_=sv[:, 13:14], func=AF.Sqrt, bias=EPS, scale=1.0 / 64), ["sv"], ["sv"])
            dve(lambda e: e.reciprocal(out=sv[:, 13:14], in_=sv[:, 13:14]), ["sv"], ["sv"])
            dve(lambda e: e.scalar_tensor_tensor(out=hs[:, 0, :], in0=hs[:, 0, :], scalar=sv[:, 13:14], in1=SP_("mnw"), op0=ALU.mult, op1=ALU.mult), ["hs", "sv", "spk"], ["hs"])
            dve(lambda e: e.tensor_reduce(out=sv[:, 14:15], in_=hs[:, 1, :], axis=AX.X, op=ALU.add), ["hs"], ["sv"])
            dve(lambda e: e.tensor_scalar_mul(out=sv[:, 14:15], in0=sv[:, 14:15], scalar1=1.0 / 64), ["sv"], ["sv"])
            dve(lambda e: e.tensor_scalar_sub(out=hs[:, 1, :], in0=hs[:, 1, :], scalar1=sv[:, 14:15]), ["hs", "sv"], ["hs"])
            dve(lambda e: e.tensor_tensor(out=w8[:, 7, :], in0=hs[:, 1, :], in1=hs[:, 1, :], op=ALU.mult), ["hs"], ["w8"])
            dve(lambda e: e.tensor_reduce(out=sv[:, 15:16], in_=w8[:, 7, :], axis=AX.X, op=ALU.add), ["w8"], ["sv"])
            act(lambda e: e.activation(out=sv[:, 15:16], in_=sv[:, 15:16], func=AF.Sqrt, bias=GN_EPS, scale=1.0 / 64), ["sv"], ["sv"])
            dve(lambda e: e.reciprocal(out=sv[:, 15:16], in_=sv[:, 15:16]), ["sv"], ["sv"])
            dve(lambda e: e.scalar_tensor_tensor(out=hs[:, 1, :], in0=hs[:, 1, :], scalar=sv[:, 15:16], in1=SP_("lnw"), op0=ALU.mult, op1=ALU.mult), ["hs", "sv", "spk"], ["hs"])
            dve(lambda e: e.tensor_tensor(out=hs[:, 1, :], in0=hs[:, 1, :], in1=SP_("lnb"), op=ALU.add), ["hs", "spk"], ["hs"])
            dve(lambda e: e.tensor_tensor(out=hs[:, 1, :], in0=hs[:, 1, :], in1=L3[:, 2, :], op=ALU.mult), ["hs", "L3"], ["hs"])
            s3v = nc.dram_tensor("scr3b", [128, 2, 64], F32, kind="Internal").ap()
            dma("pool", s3v, hs[:], ["hs"], ["scr3b"], "scrw3")
            smix = spj[:, 2304:3328].rearrange("p (a h d) -> p a h d", a=2, h=8)
            for a_ in range(2):
                dma("sp", smix[:, a_, :, :], s3v.rearrange("(b h) a d -> b a h d", b=NS)[:, a_, :, :], ["scr3b"], ["smix"], "scrr3")
            act(lambda e: e.copy(out=mix[:NS, :], in_=smix[:].rearrange("p a h d -> p (a h d)")), ["smix"], ["mix"])
            out_proj(NS, mix, "mix", sx, "sx", T, W)
        PP[0].finalize()
        PP[0] = Prog(ctx)
    es_res.close()

    with contextlib.ExitStack() as es2:
        cur[0] = es2
        upb = TT("upb", [128, 8, DFF], BF16)
        dnb = TT("dnb", [128, 32, D], BF16)
        pB2 = TT("pB2", [128, 136], F32)
        nfw = TT("nfw", [128, D])
        identb2 = TT("identb2", [128, 128], BF16)
        wout = TT("wout", [128, 8, D], BF16)
        mixin = TT("mixin", [128, D], BF16)
        for c in range(0, 8, 4):
            dma("pool", wout[:, c:c + 4, :], w_out_v[:, c:c + 4, :], [], ["wout"], "wout")
        up_v = mlp_up.rearrange("(c p) n -> p c n", p=128)
        dn_v = mlp_down.rearrange("(c p) n -> p c n", p=128)
        dma("sp", pB2[:, 0:128], packA_d[:, OFF["ident"][0]:OFF["ident"][1]], [], ["pB2"], "init2", True)
        dma("sp", pB2[:, 128:136], packA_d[:, OFF["nmlp"][0]:OFF["nmlp"][1]], [], ["pB2"], "init2", True)
        dma("sp", nfw[:], nfw_d, [], ["nfw"], "init2", True)
        for g8 in range(8):
            dma("pool", upb[:, :, g8 * 512:(g8 + 1) * 512], up_v[:, :, g8 * 512:(g8 + 1) * 512], [], ["upb%d" % g8], "up%d" % g8)
        for g8 in range(8):
            dma("pool", dnb[:, g8 * 4:(g8 + 1) * 4, :], dn_v[:, g8 * 4:(g8 + 1) * 4, :], [], ["dnb%d" % g8], "dn%d" % g8)
        dve(lambda e: e.tensor_copy(out=identb2[:], in_=pB2[:, 0:128]), ["pB2"], ["identb2"])
        NSUB = 2
        NTT = NSUB * 128
        xsb2 = TT("xsb2", [128, D], BF16)
        xb = [TT("xb%d" % i, [128, NSUB, D]) for i in range(2)]
        st2 = TT("st2", [128, 8])
        xn2 = [TT("xn2T%d" % i, [128, 8, NTT], BF16) for i in range(2)]
        hT = TT("hT", [128, 32, NTT], BF16)
        junk2 = TT("junk2", [128, D], BF16)
        rl = [TT("rl%d" % i, [128, 512]) for i in range(2)]
        nmlp = pB2[:, 128:136]
        xm_v = xp.rearrange("(s p) d -> p s d", p=128)
        yp_v = yp.rearrange("(s p) d -> p s d", p=128)
        sbs = [(sb * NSUB, NSUB, 128) for sb in range(NB // NSUB)] + [(NB, 1, NS)]

        def front(i):
            s0, nsub, nt = sbs[i]
            x4 = xb[i % 2]
            xk = "xb%d" % (i % 2)
            xn2T = xn2[i % 2]
            xnk = "xn2T%d" % (i % 2)
            if nsub == NSUB:
                dma("sp", x4[:], xm_v[:, s0:s0 + NSUB, :], [], [xk], xk)
            else:
                dma("sp", x4[:nt, 0, :], xs, [], [xk], xk)
            for si in range(nsub):
                r0_ = (s0 + si) * 128 if nsub == NSUB else T
                dma("sp", mixin[:nt, :], mix_d[r0_:r0_ + nt, :], [], ["mixin"], "mixin")
                ps, psb, pk = nps()
                for c in range(8):
                    pe(lambda e, c=c, psb=psb: e.transpose(psb[:, c * 128:c * 128 + nt], mixin[:nt, c * 128:(c + 1) * 128], identb2[:nt, :nt]), ["mixin", "identb2"], [pk])
                act(lambda e, psb=psb, si=si: e.copy(out=xn2T[:, :, si * 128:si * 128 + nt], in_=psb[:, 0:1024].rearrange("p (c t) -> p c t", c=8)[:, :, 0:nt]), [pk], [xnk])
                yield
                ps, psb, pk = nps()
                for n in range(2):
                    for c in range(8):
                        mm(ps[:nt, n * 512:(n + 1) * 512], xn2T[:, c, si * 128:si * 128 + nt], wout[:, c, n * 512:(n + 1) * 512], c == 0, c == 7, [xnk, "wout"], [pk])
                for a_ in range(2):
                    dve(lambda e, ps=ps, si=si, a_=a_: e.tensor_tensor(out=x4[:nt, si, a_ * 512:(a_ + 1) * 512], in0=ps[:nt, a_ * 512:(a_ + 1) * 512], in1=x4[:nt, si, a_ * 512:(a_ + 1) * 512], op=ALU.add), [pk, xk], [xk])
                act(lambda e, si=si: e.activation(out=junk2[:nt, :], in_=x4[:nt, si, :], func=AF.Square, accum_out=st2[:nt, 0:1]), [xk], ["junk2", "st2"])
                act(lambda e: e.activation(out=st2[:nt, 1:2], in_=st2[:nt, 0:1], func=AF.Sqrt, bias=EPS, scale=1.0 / D), ["st2"], ["st2"])
                dve(lambda e: e.reciprocal(out=st2[:nt, 2:3], in_=st2[:nt, 1:2]), ["st2"], ["st2"])
                dve(lambda e, si=si: e.tensor_scalar_mul(out=xsb2[:nt, :], in0=x4[:nt, si, :], scalar1=st2[:nt, 2:3]), [xk, "st2"], ["xsb2"])
                yield
                yield
                ps, psb, pk = nps()
                for c in range(8):
                    pe(lambda e, c=c, psb=psb: e.transpose(psb[:, c * 128:c * 128 + nt], xsb2[:nt, c * 128:(c + 1) * 128], identb2[:nt, :nt]), ["xsb2", "identb2"], [pk])
                dve(lambda e, psb=psb, si=si: e.tensor_tensor(out=xn2T[:, :, si * 128:si * 128 + nt], in0=psb[:, 0:1024].rearrange("p (c t) -> p c t", c=8)[:, :, 0:nt],
                                                             in1=nmlp.unsqueeze(2).to_broadcast([128, 8, nt]), op=ALU.mult), [pk, "pB2"], [xnk])
                yield

        def up(i):
            s0, nsub, nt = sbs[i]
            ntt = nsub * nt if nsub == NSUB else nt
            xn2T = xn2[i % 2]
            xnk = "xn2T%d" % (i % 2)
            per = 512 // NTT
            for j2 in range(32 // (2 * per)):
                ps, psb, pk = nps()
                for jj in range(2 * per):
                    j = j2 * 2 * per + jj
                    for c in range(8):
                        mm(ps[:, jj * NTT:jj * NTT + ntt], upb[:, c, j * 128:(j + 1) * 128], xn2T[:, c, 0:ntt], c == 0, c == 7, ["upb%d" % (j // 4), xnk], [pk])
                for bk in range(2):
                    r_ = rl[bk]
                    rk_ = "rl%d" % bk
                    j0 = j2 * 2 * per + bk * per
                    psv = ps[:, bk * 512:(bk + 1) * 512].rearrange("p (j t) -> p j t", j=per)[:, :, 0:ntt]
                    rv = r_[:, :].rearrange("p (j t) -> p j t", j=per)[:, :, 0:ntt]
                    act(lambda e, psv=psv, rv=rv: e.activation(out=rv, in_=psv, func=AF.Relu), [pk], [rk_])
                    if bk == 0:
                        dve(lambda e, rv=rv, j0=j0: e.tensor_tensor(out=hT[:, j0:j0 + per, 0:ntt], in0=rv, in1=rv, op=ALU.mult), [rk_], ["hT"])
                    else:
                        pool(lambda e, rv=rv, j0=j0: e.tensor_tensor(out=hT[:, j0:j0 + per, 0:ntt], in0=rv, in1=rv, op=ALU.mult), [rk_], ["hT"])
                yield

        def down(i):
            s0, nsub, nt = sbs[i]
            x4 = xb[i % 2]
            xk = "xb%d" % (i % 2)
            for si in range(nsub):
                ps, psb, pk = nps()
                for n in range(2):
                    for j in range(32):
                        mm(ps[:nt, n * 512:(n + 1) * 512], hT[:, j, si * 128:si * 128 + nt], dnb[:, j, n * 512:(n + 1) * 512], j == 0, j == 31, ["hT", "dnb%d" % (j // 4)], [pk])
                for a_ in range(2):
                    dve(lambda e, ps=ps, si=si, a_=a_: e.tensor_tensor(out=x4[:nt, si, a_ * 512:(a_ + 1) * 512], in0=ps[:nt, a_ * 512:(a_ + 1) * 512], in1=x4[:nt, si, a_ * 512:(a_ + 1) * 512], op=ALU.add), [pk, xk], [xk])
                act(lambda e, si=si: e.activation(out=junk2[:nt, :], in_=x4[:nt, si, :], func=AF.Square, accum_out=st2[:nt, 4:5]), [xk], ["junk2", "st2"])
                act(lambda e: e.activation(out=st2[:nt, 5:6], in_=st2[:nt, 4:5], func=AF.Sqrt, bias=EPS, scale=1.0 / D), ["st2"], ["st2"])
                dve(lambda e: e.reciprocal(out=st2[:nt, 6:7], in_=st2[:nt, 5:6]), ["st2"], ["st2"])
                dve(lambda e, si=si: e.scalar_tensor_tensor(out=x4[:nt, si, :], in0=x4[:nt, si, :], scalar=st2[:nt, 6:7], in1=nfw[:nt, :], op0=ALU.mult, op1=ALU.mult), [xk, "st2", "nfw"], [xk])
            if nsub == NSUB:
                dma("pool", yp_v[:, s0:s0 + NSUB, :], x4[:], [xk], [], "yo%d" % (i % 2))
            else:
                dma("pool", ys, x4[:nt, 0, :], [xk], [], "yo%d" % (i % 2))

        def run2(gl):
            gl = list(gl)
            while gl:
                for g_ in list(gl):
                    try:
                        next(g_)
                    except StopIteration:
                        gl.remove(g_)

        run2([front(0)])
        for i in range(len(sbs)):
            gl = [up(i)]
            if i + 1 < len(sbs):
                gl.append(front(i + 1))
            run2(gl)
            down(i)
        PP[0].finalize()
    es_ps.close()
    ctx.close()
    return nc


_CACHE = {}


def _host_packs(inp, core):
    f = np.float32
    L = 0
    pa = np.zeros((128, NA), f)

    def put(n, arr):
        a, b = OFF[n]
        pa[:, a:b] = arr

    rep = lambda v: np.broadcast_to(np.asarray(v, f).reshape(1, -1), (128, np.asarray(v).size))
    put("mnw", rep(inp["mlstm_norm_w"][L]))
    put("w0", rep(inp["rw_w0"][L]))
    put("a0", rep(inp["rw_a0"][L]))
    put("kk", rep(inp["rw_k_k"][L]))
    put("ka", rep(inp["rw_k_a"][L]))
    put("rk", rep(inp["rw_r_k"][L].reshape(-1)))
    put("lnw", rep(inp["rw_ln_w"][L]))
    put("lnb", rep(inp["rw_ln_b"][L]))
    put("ifb", rep(np.concatenate([inp["mlstm_i_b"][L], inp["mlstm_f_b"][L]])))
    put("nmw", inp["norm_mix_w"][L].reshape(8, 128).T)
    put("nmlp", inp["norm_mlp_w"][L].reshape(8, 128).T)
    cw = inp["mlstm_conv_w"][L]
    put("cw", cw.reshape(4, 8, 128).transpose(2, 1, 0).reshape(128, 32))
    put("cb", inp["mlstm_conv_b"][L].reshape(8, 128).T)
    put("ident", np.eye(128, dtype=f))
    put("mui", np.triu(np.ones((128, 128), f), 0))
    put("mus", np.triu(np.ones((128, 128), f), 1))
    put("mls", np.tril(np.ones((128, 128), f), -1))
    put("ones", np.ones((128, 128), f))
    lsel = np.zeros((128, 128), f)
    rsel = np.zeros((128, 4), f)
    for h in range(8):
        lsel[h, (h % 2) * 64:(h % 2) * 64 + 64] = 1.0
        rsel[h, h // 2] = 1.0
    put("lsel", lsel)
    put("rsel", rsel)
    return pa


def _sample_pack(inp):
    f = np.float32
    L = 0
    sp = np.zeros((128, NSP), f)

    def bh(v512):
        return np.tile(np.asarray(v512, f).reshape(8, 64), (NS, 1))

    def put(n, arr):
        a, b = SOFF[n]
        sp[:, a:b] = arr

    mu = inp["rw_mu"][L]
    put("mu_r", bh(mu[0:512]))
    put("mu_k", bh(mu[512:1024]))
    put("mu_v", bh(mu[1024:1536]))
    cw = inp["mlstm_conv_w"][L]
    put("cwq", np.concatenate([bh(cw[j, 0:512]) for j in range(4)], axis=1))
    put("cwk", np.concatenate([bh(cw[j, 512:1024]) for j in range(4)], axis=1))
    cb = inp["mlstm_conv_b"][L]
    put("cbq", bh(cb[0:512]))
    put("cbk", bh(cb[512:1024]))
    put("mnw", bh(inp["mlstm_norm_w"][L]))
    put("w0", bh(inp["rw_w0"][L]))
    put("a0", bh(inp["rw_a0"][L]))
    put("kk", bh(inp["rw_k_k"][L]))
    put("ka", bh(inp["rw_k_a"][L]))
    put("rk", bh(inp["rw_r_k"][L].reshape(-1)))
    put("lnw", bh(inp["rw_ln_w"][L]))
    put("lnb", bh(inp["rw_ln_b"][L]))
    put("ib", np.tile(inp["mlstm_i_b"][L].reshape(8, 1), (NS, 1)))
    put("fb", np.tile(inp["mlstm_f_b"][L].reshape(8, 1), (NS, 1)))
    return sp


def kernel(**inp):
    f = np.float32
    inp = {k: np.asarray(v) for k, v in inp.items()}
    if "nc" not in _CACHE:
        _CACHE["nc"] = build_program()
    nc = _CACHE["nc"]
    L = 0
    pa = _host_packs(inp, 0)
    sp = _sample_pack(inp)
    mu = inp["rw_mu"][L]
    luw = np.concatenate([inp["rw_w_up"][L], inp["rw_a_up"][L]], axis=0).astype(f)
    common = {
        "w_in": np.ascontiguousarray(inp["w_in"][L], f),
        "w_out": np.ascontiguousarray(inp["w_out"][L], f),
        "mlp_up": np.ascontiguousarray(inp["mlp_up"][L], f),
        "mlp_down": np.ascontiguousarray(inp["mlp_down"][L], f),
        "packA": pa,
        "mu_b": np.ascontiguousarray(np.broadcast_to(mu.reshape(1, -1), (128, RWW)), f),
        "nfw_b": np.ascontiguousarray(np.broadcast_to(inp["norm_f_w"].reshape(1, -1), (128, D)), f),
        "luw": luw,
        "gup": np.ascontiguousarray(inp["rw_g_up"][L], f),
        "spack": sp,
        "mul": np.ascontiguousarray(np.broadcast_to(mu[1536:1792].reshape(1, -1), (NS, 256)), f),
    }
    in_maps = []
    for c in range(8):
        rs = slice(c * NS, (c + 1) * NS)
        m = dict(common)
        m["xp"] = np.ascontiguousarray(inp["x_prompt"][c], f)
        m["xs"] = np.ascontiguousarray(inp["x_sample"][rs, 0, :], f)
        m["sC"] = np.ascontiguousarray(inp["state_mlstm_C"][L, rs].reshape(128, 4096), f)
        m["sn"] = np.ascontiguousarray(inp["state_mlstm_n"][L, rs].reshape(128, 64), f)
        m["sm"] = np.ascontiguousarray(inp["state_mlstm_m"][L, rs].reshape(128, 1), f)
        cv = inp["state_mlstm_conv"][L, rs]
        m["sconv"] = np.ascontiguousarray(cv.reshape(NS, 3, 2, 8, 64).transpose(0, 3, 2, 1, 4).reshape(128, 2, 3, 64), f)
        m["sS"] = np.ascontiguousarray(inp["state_rwkv_S"][L, rs].reshape(128, 4096), f)
        sh = inp["state_rwkv_shift"][L, rs, 0, :]
        m["sshift"] = np.ascontiguousarray(sh[:, 0:1536].reshape(NS, 3, 8, 64).transpose(0, 2, 1, 3).reshape(128, 3, 64), f)
        m["sshl"] = np.ascontiguousarray(sh[:, 1536:1792], f)
        in_maps.append(m)
    res = run_bass_kernel_spmd(nc, in_maps, core_ids=list(range(8)))
    R = res.results
    y_prompt = np.stack([R[c]["yp"] for c in range(8)]).astype(f)
    y_sample = np.concatenate([R[c]["ys"] for c in range(8)], axis=0).reshape(128, 1, D).astype(f)
    pC = np.zeros((1, 8, 8, 64, 64), f)
    pn = np.zeros((1, 8, 8, 64), f)
    pm = np.zeros((1, 8, 8), f)
    pconv = np.zeros((1, 8, 3, 1024), f)
    pS = np.zeros((1, 8, 8, 64, 64), f)
    pshift = np.zeros((1, 8, 1, RWW), f)
    for c in range(8):
        oC = R[c]["oC"].reshape(2, 64, 4, 65)
        Ch = oC.transpose(2, 0, 1, 3).reshape(8, 64, 65)
        pC[0, c] = Ch[:, :, 0:64]
        pn[0, c] = Ch[:, :, 64]
        pm[0, c] = R[c]["om"].reshape(8)
        pconv[0, c] = R[c]["oconv"].transpose(2, 1, 0).reshape(3, 1024)
        oS = R[c]["oS"].reshape(2, 64, 4, 64)
        pS[0, c] = oS.transpose(2, 0, 3, 1).reshape(8, 64, 64)
        pshift[0, c, 0] = R[c]["oshift"].reshape(RWW)
    sC = np.concatenate([R[c]["osC"].reshape(NS, 8, 64, 64) for c in range(8)])[None].astype(f)
    sn = np.concatenate([R[c]["osn"].reshape(NS, 8, 64) for c in range(8)])[None].astype(f)
    sm = np.concatenate([R[c]["osm"].reshape(NS, 8) for c in range(8)])[None].astype(f)
    sconv = np.concatenate([R[c]["osconv"].reshape(NS, 8, 2, 3, 64).transpose(0, 3, 2, 1, 4).reshape(NS, 3, 1024) for c in range(8)])[None].astype(f)
    sS = np.concatenate([R[c]["osS"].reshape(NS, 8, 64, 64) for c in range(8)])[None].astype(f)
    sshift = np.concatenate([
        np.concatenate([R[c]["osshift"].reshape(NS, 8, 3, 64).transpose(0, 2, 1, 3).reshape(NS, 1536), R[c]["osshl"]], axis=1)
        for c in range(8)]).reshape(1, 128, 1, RWW).astype(f)
    return (y_prompt, y_sample, pC, pn, pm, pconv, pS, pshift, sC, sn, sm, sconv, sS, sshift)
```

```python
import contextlib
import numpy as np
import concourse.bass as bass
import concourse.mybir as mybir
from concourse.bass_utils import run_bass_kernel_spmd

F32 = mybir.dt.float32
BF16 = mybir.dt.bfloat16
AF = mybir.ActivationFunctionType
ALU = mybir.AluOpType
AX = mybir.AxisListType

D = 1024
T = 2048
NB = 16
NS = 16
INW = 3856
MLW = 2064
RWW = 1792
DFF = 4096
EPS = 1e-6
GN_EPS = 64e-5
C0 = 0.6065306597126334

OFF = {}
_o = 0
for _n, _w in [("mnw", 512), ("w0", 512), ("a0", 512), ("kk", 512), ("ka", 512), ("rk", 512),
               ("lnw", 512), ("lnb", 512), ("ifb", 16), ("nmw", 8), ("nmlp", 8), ("cw", 32), ("cb", 8),
               ("ident", 128), ("mui", 128), ("mus", 128), ("mls", 128), ("ones", 128),
               ("lsel", 128), ("rsel", 4)]:
    OFF[_n] = (_o, _o + _w)
    _o += _w
NA = _o
SOFF = {}
_o = 0
for _n, _w in [("mu_r", 64), ("mu_k", 64), ("mu_v", 64), ("cwq", 256), ("cwk", 256), ("cbq", 64), ("cbk", 64),
               ("mnw", 64), ("w0", 64), ("a0", 64), ("kk", 64), ("ka", 64), ("rk", 64), ("lnw", 64), ("lnb", 64),
               ("ib", 1), ("fb", 1)]:
    SOFF[_n] = (_o, _o + _w)
    _o += _w
NSP = _o


ALIAS = {"r_sb0": "G0", "kf_sb0": "G1", "vf0": "G2", "osig0": "G13", "r_sb1": "G20", "kf_sb1": "G21", "vf1": "G22", "osig1": "G23",
         "wsig": "G3", "a_sb": "G4", "g_sb0": "G5", "g_sb1": "G24", "kap": "G6", "ktl": "G7",
         "bvec": "G8", "e1": "G9", "e2": "G10", "e3": "G11", "pcw": "G12", "ysb": "G11", "tA": "G9", "tB": "G10", "plast": "G16",
         "hml": "G14", "nlrep": "G15", "Fb": "G15", "cacc": "G16", "qks": "G16", "tAm": "G18", "tBm": "G19",
         "junk": "xm", "ctmp": "xm", "mixT": "TMbA", "TMb0": "TMbA", "TMb1": "TMbA",
         "TMb2": "TMbB", "TMb3": "TMbB", "Ub": "Zb1", "Ss": "Cs", "lo3": "spj", "smix": "spj",
         "xt1": "xt0", "xnT1": "xnT0", "qkx1": "qkx0"}
PARKEYS = {"g_sb", "r_sb", "kf_sb", "vf", "osig", "vaug", "gt", "qpb", "kTb", "ktm", "vrw", "lor"}
CURP = [0]


class SemCtx:
    def __init__(self, nc):
        self.nc = nc
        self.es = contextlib.ExitStack()
        self.engs = ["pe", "act", "dve", "pool", "sp"]
        self.esem = {e: self.es.enter_context(nc.semaphore("s_" + e)) for e in self.engs}
        self.ecnt = {e: 0 for e in self.engs}
        self.bsem = self.es.enter_context(nc.semaphore("s_bar"))
        self.phase = 0
        self.gsem = {}
        self.gbase = {}

    def group_sem(self, g):
        if g not in self.gsem:
            self.gsem[g] = self.es.enter_context(self.nc.semaphore("g_%d" % len(self.gsem)))
            self.gbase[g] = 0
        return self.gsem[g]

    def close(self):
        self.es.close()


class Prog:
    max_ops = None

    def __init__(self, ctx):
        self.ctx = ctx
        self.nc = ctx.nc
        self.ops = []
        self.last_writer = {}
        self.readers = {}
        self.dma_groups = {}

    def op(self, eng, fn, reads=(), writes=(), dma_group=None, wait_total=False):
        if self.max_ops is not None and len(self.ops) >= self.max_ops:
            return None
        reads = [(k + str(CURP[0])) if k in PARKEYS else k for k in reads]
        writes = [(k + str(CURP[0])) if k in PARKEYS else k for k in writes]
        reads = [ALIAS.get(k, k) for k in reads]
        writes = [ALIAS.get(k, k) for k in writes]
        if eng != "pe":
            writes = writes + [k for k in reads if k.startswith("PS") and k not in writes]
        deps = set()
        for b in reads:
            if b in self.last_writer:
                deps.add(self.last_writer[b])
        for b in writes:
            if b in self.last_writer:
                deps.add(self.last_writer[b])
            for r in self.readers.get(b, ()):
                deps.add(r)
        idx = len(self.ops)
        if dma_group is not None:
            deps = {d for d in deps if self.ops[d]["dma"] != dma_group}
        o = dict(eng=eng, fn=fn, deps=sorted(deps), dma=dma_group, idx=idx)
        if dma_group is not None:
            g = self.dma_groups.setdefault(dma_group, dict(total=0, wait_total=wait_total))
            g["total"] += 1
            o["dma_cnt"] = g["total"]
        self.ops.append(o)
        for b in reads:
            self.readers.setdefault(b, []).append(idx)
        for b in writes:
            self.last_writer[b] = idx
            self.readers[b] = []
        return idx

    def finalize(self):
        nc = self.nc
        ctx = self.ctx
        ops = self.ops
        needed = set()
        for o in ops:
            best = {}
            bestg = {}
            rd = []
            for d in o["deps"]:
                p = ops[d]
                if p["dma"] is not None:
                    bestg[p["dma"]] = max(bestg.get(p["dma"], -1), d)
                else:
                    if p["eng"] == "pe" and o["eng"] == "pe" and o["dma"] is None:
                        continue
                    best[p["eng"]] = max(best.get(p["eng"], -1), d)
            rd.extend(best.values())
            rd.extend(bestg.values())
            o["deps"] = sorted(rd)
            for d in best.values():
                needed.add(d)
        engs = ctx.engs
        last = {}
        for o in ops:
            if o["dma"] is None:
                last[o["eng"]] = o["idx"]
        needed |= set(last.values())
        cnt = dict(ctx.ecnt)
        for o in ops:
            if o["dma"] is None and o["idx"] in needed:
                cnt[o["eng"]] += 1
                o["sig"] = cnt[o["eng"]]
        for g in self.dma_groups:
            ctx.group_sem(g)
        phase = ctx.phase
        with nc.Block() as block:

            def emit_engine(ename, eng):
                known = {}
                if phase > 0:
                    eng.wait_ge(ctx.bsem, phase)
                for o in ops:
                    if o["eng"] != ename:
                        continue
                    for d in o["deps"]:
                        p = ops[d]
                        if p["dma"] is not None:
                            g = self.dma_groups[p["dma"]]
                            sem = ctx.gsem[p["dma"]]
                            val = ctx.gbase[p["dma"]] + 16 * (g["total"] if g["wait_total"] else p["dma_cnt"])
                            key = ("g", p["dma"])
                        else:
                            if p["eng"] == "pe" and ename == "pe" and o["dma"] is None:
                                continue
                            sem = ctx.esem[p["eng"]]
                            val = p["sig"]
                            key = ("e", p["eng"])
                        if known.get(key, 0) >= val:
                            continue
                        known[key] = val
                        eng.wait_ge(sem, val)
                    ins = o["fn"](eng)
                    if o["dma"] is not None:
                        ins.then_inc(ctx.gsem[o["dma"]], 16)
                    elif "sig" in o:
                        ins.then_inc(ctx.esem[ename], 1)
                if ename == "sp":
                    for e2 in engs:
                        if cnt[e2] > ctx.ecnt[e2]:
                            eng.wait_ge(ctx.esem[e2], cnt[e2])
                    for g, info in self.dma_groups.items():
                        eng.wait_ge(ctx.gsem[g], ctx.gbase[g] + 16 * info["total"])
                    eng.sem_inc(ctx.bsem, 1)

            @block.tensor
            def _(e):
                emit_engine("pe", e)

            @block.scalar
            def _(e):
                emit_engine("act", e)

            @block.vector
            def _(e):
                emit_engine("dve", e)

            @block.gpsimd
            def _(e):
                emit_engine("pool", e)

            @block.sync
            def _(e):
                emit_engine("sp", e)

        ctx.ecnt = cnt
        for g, info in self.dma_groups.items():
            ctx.gbase[g] += 16 * info["total"]
        ctx.phase += 1


STOP_EARLY = True


class _StopBuild(Exception):
    pass


def build_program(do_sample=True, debug=False):
    nc = bass.Bass("TRN2", target_bir_lowering=False)
    try:
        return _build_program(nc, do_sample, debug)
    except _StopBuild:
        return nc


def _build_program(nc, do_sample, debug):
    dbg = nc.dram_tensor("dbg", [128, 16, 512], F32, kind="ExternalOutput").ap() if debug else None
    din = lambda n, s: nc.dram_tensor(n, s, F32, kind="ExternalInput").ap()
    dout = lambda n, s: nc.dram_tensor(n, s, F32, kind="ExternalOutput").ap()
    xp = din("xp", [T, D])
    xs = din("xs", [NS, D])
    w_in = din("w_in", [D, INW])
    w_out = din("w_out", [D, D])
    mlp_up = din("mlp_up", [D, DFF])
    mlp_down = din("mlp_down", [DFF, D])
    packA_d = din("packA", [128, NA])
    mu_d = din("mu_b", [128, RWW])
    nfw_d = din("nfw_b", [128, D])
    wup_d = din("luw", [128, 512])
    gup_d = din("gup", [128, 512])
    spk_d = din("spack", [128, NSP])
    sC_d = din("sC", [128, 4096])
    sn_d = din("sn", [128, 64])
    sm_d = din("sm", [128, 1])
    sconv_d = din("sconv", [128, 2, 3, 64])
    sS_d = din("sS", [128, 4096])
    sshift_d = din("sshift", [128, 3, 64])
    sshl_d = din("sshl", [NS, 256])
    mul_d = din("mul", [NS, 256])

    yp = dout("yp", [T, D])
    ys = dout("ys", [NS, D])
    oC = dout("oC", [128, 4, 65])
    om = dout("om", [8, 1])
    oconv = dout("oconv", [128, 8, 3])
    oS = dout("oS", [128, 4, 64])
    oshift = dout("oshift", [1, RWW])
    osC = dout("osC", [128, 4096])
    osn = dout("osn", [128, 64])
    osm = dout("osm", [128, 1])
    osconv = dout("osconv", [128, 2, 3, 64])
    osS = dout("osS", [128, 4096])
    osshift = dout("osshift", [128, 3, 64])
    osshl = dout("osshl", [NS, 256])

    mix_d = nc.dram_tensor("mix_scr", [T + NS, D], BF16, kind="Internal").ap()
    scr1 = nc.dram_tensor("scr1", [128, 8, 64], F32, kind="Internal").ap()
    scr2 = nc.dram_tensor("scr2", [128, 3, 64], F32, kind="Internal").ap()
    scr3 = nc.dram_tensor("scr3", [NS, 2, 8, 64], F32, kind="Internal").ap()

    ctx = SemCtx(nc)
    PP = [Prog(ctx)]
    es_res = contextlib.ExitStack()
    cur = [es_res]

    def TT(name, shape, dt=F32):
        return cur[0].enter_context(nc.sbuf_tensor("t_" + name, list(shape), dt))

    def dma(q, out, in_, reads, writes, group, wait_total=False):
        group = "%s@%s" % (group, q)
        PP[0].op(q, lambda e: e.dma_start(out=out, in_=in_), reads=reads, writes=writes, dma_group=group, wait_total=wait_total)

    def dve(fn, r, w):
        PP[0].op("dve", fn, reads=r, writes=w)

    def act(fn, r, w):
        PP[0].op("act", fn, reads=r, writes=w)

    def pool(fn, r, w):
        PP[0].op("pool", fn, reads=r, writes=w)

    def pe(fn, r, w):
        PP[0].op("pe", fn, reads=r, writes=w)

    def mm(out, lhsT, rhs, start, stop, r, w):
        pe(lambda e: e.matmul(out, lhsT=lhsT, rhs=rhs, start=start, stop=stop), r, w)

    es_ps = contextlib.ExitStack()
    PS = [es_ps.enter_context(nc.psum_tensor("PS%d" % i, [128, 1024], F32)) for i in range(4)]
    PSB = [p.bitcast(BF16) for p in PS]
    psi = {"A": 0, "F": 0, "B": 0}
    POOLS = {"A": [0, 1, 2, 3], "F": [0, 1], "B": [2, 3]}
    CURPOOL = ["A"]

    def nps():
        pl = CURPOOL[0]
        lst = POOLS[pl]
        i = lst[psi[pl] % len(lst)]
        psi[pl] += 1
        return PS[i], PSB[i], "PS%d" % i

    wq = TT("wq", [128, 8, INW], BF16)
    mub = TT("mub", [128, RWW], F32)
    luw = TT("luw", [128, 512], BF16)
    gup = TT("gup", [128, 512], BF16)
    pA = TT("pA", [128, NA], F32)
    identb = TT("identb", [128, 128], BF16)

    def PA(n):
        a, b = OFF[n]
        return pA[:, a:b]

    w_in_v = w_in.rearrange("(c p) n -> p c n", p=128)
    dma("sp", pA[:], packA_d, [], ["pA"], "init", True)
    dma("pool", luw[:], wup_d, [], ["luw"], "init", True)
    dma("pool", gup[:], gup_d, [], ["gup"], "init", True)
    w_out_v = w_out.rearrange("(c p) n -> p c n", p=128)
    dve(lambda e: e.tensor_copy(out=identb[:], in_=PA("ident")), ["pA"], ["identb"])


    def _dbgdump(tag):
        if debug != tag:
            return
        dstg_ = cur[0].enter_context(nc.sbuf_tensor("t_dbgst%d" % tag, [128, 512], F32))
        def dd(slot, ap, key, n):
            dve(lambda e: e.tensor_copy(out=dstg_[:, 0:n], in_=ap), [key], ["dbgst"])
            dma("sp", dbg[:, slot, 0:n], dstg_[:, 0:n], ["dbgst"], [], "dbg")
        dd(0, PA("ident"), "pA", 128)
        dd(1, PA("mui"), "pA", 128)
        dd(2, PA("mnw"), "pA", 512)
        dd(3, PA("w0"), "pA", 512)
        dd(4, PA("lnb"), "pA", 512)
        PP[0].max_ops = len(PP[0].ops)
        PP[0].finalize()
        raise _StopBuild()
    _dbgdump(3)
    dma("sp", mub[:], mu_d, [], ["mub"], "init", True)
    PP[0].finalize()
    PP[0] = Prog(ctx)

    def rmsnorm_T(xt, xk, nt, dstT, dstk, col0, wname, tmpb, tmpbk, junk, junkk, st, stk):
        act(lambda e: e.activation(out=junk[:nt, :], in_=xt[:nt, :], func=AF.Square, accum_out=st[:nt, 0:1]), [xk], [junkk, stk])
        act(lambda e: e.activation(out=st[:nt, 1:2], in_=st[:nt, 0:1], func=AF.Ln, bias=EPS, scale=1.0 / D), [stk], [stk])
        act(lambda e: e.activation(out=st[:nt, 2:3], in_=st[:nt, 1:2], func=AF.Exp, scale=-0.5), [stk], [stk])
        dve(lambda e: e.tensor_scalar_mul(out=tmpb[:nt, :], in0=xt[:nt, :], scalar1=st[:nt, 2:3]), [xk, stk], [tmpbk])
        ps, psb, pk = nps()
        for c in range(8):
            pe(lambda e, c=c: e.transpose(psb[:, c * 128:c * 128 + nt], tmpb[:nt, c * 128:(c + 1) * 128], identb[:nt, :nt]), [tmpbk, "identb"], [pk])
        a, b_ = OFF[wname]
        dve(lambda e: e.tensor_tensor(out=dstT[:, :, col0:col0 + nt],
                                      in0=psb[:, 0:1024].rearrange("p (c t) -> p c t", c=8)[:, :, 0:nt],
                                      in1=pA[:, a:b_].unsqueeze(2).to_broadcast([128, 8, nt]), op=ALU.mult), [pk, "pA"], [dstk])

    def head_ml(nt, hsrc, hk, osig, ok, mix, mixk, W, sfx=""):
        tA, tB, s8 = W["tA"], W["tB"], W["s8"]
        h3 = lambda t: t[:nt, :].rearrange("p (h d) -> p h d", h=8)
        bc = lambda t, c: t[:nt, c:c + 8].unsqueeze(2).to_broadcast([nt, 8, 64])
        dve(lambda e: e.tensor_tensor(out=tA[:nt, :], in0=hsrc[:nt, :], in1=osig[:nt, :], op=ALU.mult), [hk, ok], ["tA" + sfx])
        dve(lambda e: e.tensor_tensor(out=tB[:nt, :], in0=tA[:nt, :], in1=tA[:nt, :], op=ALU.mult), ["tA" + sfx], ["tB" + sfx])
        dve(lambda e: e.tensor_reduce(out=s8[:nt, 0:8], in_=h3(tB), axis=AX.X, op=ALU.add), ["tB" + sfx], ["s8" + sfx])
        act(lambda e: e.activation(out=s8[:nt, 8:16], in_=s8[:nt, 0:8], func=AF.Ln, bias=EPS, scale=1.0 / 64), ["s8" + sfx], ["s8" + sfx])
        act(lambda e: e.activation(out=s8[:nt, 16:24], in_=s8[:nt, 8:16], func=AF.Exp, scale=-0.5), ["s8" + sfx], ["s8" + sfx])
        dve(lambda e: e.tensor_tensor(out=h3(tB), in0=h3(tA), in1=bc(s8, 16), op=ALU.mult), ["tA" + sfx, "s8" + sfx], ["tB" + sfx])
        dve(lambda e: e.tensor_tensor(out=mix[:nt, 0:512], in0=tB[:nt, :], in1=PA("mnw")[:nt, :], op=ALU.mult), ["tB" + sfx, "pA"], [mixk])

    def head_rw(nt, ysrc, yk, bon, bonk, vf, vfk, g, gk, mix, mixk, W):
        tA, tB, s8 = W["tA"], W["tB"], W["s8"]
        h3 = lambda t: t[:nt, :].rearrange("p (h d) -> p h d", h=8)
        bc = lambda t, c: t[:nt, c:c + 8].unsqueeze(2).to_broadcast([nt, 8, 64])
        pool(lambda e: e.tensor_tensor(out=h3(tA), in0=h3(vf), in1=bc(bon, 0), op=ALU.mult), [vfk, bonk], ["tAm"])
        pool(lambda e: e.tensor_tensor(out=tA[:nt, :], in0=tA[:nt, :], in1=ysrc[:nt, :], op=ALU.add), ["tAm", yk], ["tAm"])
        dve(lambda e: e.tensor_reduce(out=s8[:nt, 24:32], in_=h3(tA), axis=AX.X, op=ALU.add), ["tAm"], ["s8"])
        pool(lambda e: e.tensor_scalar_mul(out=s8[:nt, 24:32], in0=s8[:nt, 24:32], scalar1=1.0 / 64), ["s8"], ["s8"])
        pool(lambda e: e.tensor_tensor(out=h3(tA), in0=h3(tA), in1=bc(s8, 24), op=ALU.subtract), ["tAm", "s8"], ["tAm"])
        pool(lambda e: e.tensor_tensor(out=tB[:nt, :], in0=tA[:nt, :], in1=tA[:nt, :], op=ALU.mult), ["tAm"], ["tBm"])
        dve(lambda e: e.tensor_reduce(out=s8[:nt, 32:40], in_=h3(tB), axis=AX.X, op=ALU.add), ["tBm"], ["s8"])
        act(lambda e: e.activation(out=s8[:nt, 40:48], in_=s8[:nt, 32:40], func=AF.Ln, bias=GN_EPS, scale=1.0 / 64), ["s8"], ["s8"])
        act(lambda e: e.activation(out=s8[:nt, 48:56], in_=s8[:nt, 40:48], func=AF.Exp, scale=-0.5), ["s8"], ["s8"])
        pool(lambda e: e.tensor_tensor(out=h3(tB), in0=h3(tA), in1=bc(s8, 48), op=ALU.mult), ["tAm", "s8"], ["tBm"])
        pool(lambda e: e.tensor_tensor(out=tB[:nt, :], in0=tB[:nt, :], in1=PA("lnw")[:nt, :], op=ALU.mult), ["tBm", "pA"], ["tBm"])
        pool(lambda e: e.tensor_tensor(out=tB[:nt, :], in0=tB[:nt, :], in1=PA("lnb")[:nt, :], op=ALU.add), ["tBm", "pA"], ["tBm"])
        pool(lambda e: e.tensor_tensor(out=mix[:nt, 512:1024], in0=tB[:nt, :], in1=g[:nt, :], op=ALU.mult), ["tBm", gk], [mixk])

    def out_proj(nt, mix, mixk, xt, xk, row0, W):
        dma("pool", mix_d[row0:row0 + nt, :], mix[:nt, :], [mixk], [], "mixst")

    with contextlib.ExitStack() as es1:
        cur[0] = es1
        W = {}
        Gbig = TT("Gbig", [128, 25, 512])
        G = [Gbig[:, i, :] for i in range(25)]
        W["s8"] = TT("s8", [128, 64])
        W["xm"] = TT("xm", [128, D])
        xt = [TT("xt0", [128, D])] * 2
        junk = W["xm"]
        st = TT("st", [128, 4])
        mix = TT("mix", [128, D], BF16)
        xsb = TT("xsb", [128, D], BF16)
        xnT = [TT("xnT0", [128, 8, 129], BF16)] * 2
        dxT = TT("dxT", [128, 8, 128], BF16)
        qkx = [TT("qkx0", [128, 8, 131])] * 2
        cacc = Gbig[:, 16:18, :].rearrange("p a (c t) -> p (a c) t", t=128)
        ctmp = W["xm"][:, :].rearrange("p (c t) -> p c t", c=8)
        qks = cacc
        QPB = [TT("qpb%d" % i, [128, 4, 128], BF16) for i in range(2)]
        KTB = [TT("kTb%d" % i, [128, 4, 128], BF16) for i in range(2)]
        KTM = [TT("ktm%d" % i, [128, 8, 64], BF16) for i in range(2)]
        VAUG = [TT("vaug%d" % i, [128, 8, 65], BF16) for i in range(2)]
        GT = [TT("gt%d" % i, [128, 96]) for i in range(2)]
        runmax = TT("runmax", [128, 8])
        nBc = TT("nBc", [128, 8])
        Cst = TT("Cst", [128, 4, 65])
        Cbf = TT("Cbf", [128, 4, 65], BF16)
        Fb = G[15].rearrange("p (j t) -> p j t", j=4)
        hml = G[14]
        nlrep = G[15].rearrange("p (h d) -> p h d", h=8)
        wsig, a_sb, g_sb, kap, ktl, bvec, e1, e2, e3, pcw, ysb = G[3], G[4], G[5], G[6], G[7], G[8], G[9], G[10], G[11], G[12], G[11]
        W["tA"], W["tB"] = G[9], G[10]
        WM = {"tA": G[18], "tB": G[19], "s8": TT("s8m", [128, 64])}
        r8b = TT("r8b", [128, 8])
        VRW = [TT("vrw%d" % i, [128, 8, 64], BF16) for i in range(2)]
        LOR = [TT("lor%d" % i, [128, 256], BF16) for i in range(2)]
        lorT = TT("lorT", [128, 2, 128], BF16)
        r8 = TT("r8", [128, 32])
        bon = TT("bon", [128, 8])
        TMb = TT("TMb", [128, 4, 512], BF16)
        W["mixT"] = TMb[:, 0:2, :].rearrange("p a (c t) -> p (a c) t", t=128)
        Btz = TT("Btz", [128, 8, 128], BF16)
        Ktz = TT("Ktz", [128, 8, 128], BF16)
        FMt = TT("FMt", [128, 4, 4, 128], BF16)
        Am = [TT("Am%d" % i, [128, 8, 128], BF16) for i in range(3)]
        PTb = TT("PTb", [128, 8, 128], BF16)
        Pw = [TT("Pw%d" % i, [128, 8, 128], BF16) for i in range(4)]
        Zb = [TT("Zb%d" % i, [128, 8, 64], BF16) for i in range(2)]
        Ub = Zb[1]
        Sst = TT("Sst", [128, 4, 64])
        Sbf = TT("Sbf", [128, 4, 64], BF16)
        WLfm = TT("WLfm", [128, 4])
        plast = G[17]

        def wqk(c0, c1):
            return ["wq%d" % g for g in range(c0 // 512, (c1 - 1) // 512 + 1)]

        for g in range(8):
            c0, c1 = g * 512, min(INW, (g + 1) * 512)
            dma("pool", wq[:, :, c0:c1], w_in_v[:, :, c0:c1], [], ["wq%d" % g], "wq%d" % g)
        for p_ in range(2):
            pool(lambda e, p_=p_: e.memset(VAUG[p_][:], 1.0), [], ["vaug%d" % p_])
        pool(lambda e: e.memset(Btz[:], 0.0), [], ["Btz"])
        pool(lambda e: e.memset(Ktz[:], 0.0), [], ["Ktz"])
        pool(lambda e: e.memset(Cst[:], 0.0), [], ["Cst"])
        pool(lambda e: e.memset(Cbf[:], 0.0), [], ["Cbf"])
        pool(lambda e: e.memset(Sst[:], 0.0), [], ["Sst"])
        pool(lambda e: e.memset(Sbf[:], 0.0), [], ["Sbf"])
        pool(lambda e: e.memset(runmax[:], -1e30), [], ["runmax"])
        pool(lambda e: e.memset(nBc[:], 0.0), [], ["nBc"])
        pool(lambda e: e.memset(xnT[0][:, :, 0:1], 0.0), [], ["xnT0"])
        pool(lambda e: e.memset(qkx[0][:, :, 0:3], 0.0), [], ["qkx0"])

        if debug == 2:
            dstg = TT("dbgstage", [128, 512]) if False else G[12]
            def ddump0(slot, ap, key, n):
                dve(lambda e: e.tensor_copy(out=dstg[:, 0:n], in_=ap), [key], ["pcw"])
                dma("sp", dbg[:, slot, 0:n], dstg[:, 0:n], ["pcw"], [], "dbg")
            ddump0(0, PA("ident"), "pA", 128)
            ddump0(1, PA("mui"), "pA", 128)
            ddump0(2, luw[:, :], "luw", 512)
            ddump0(3, gup[:, :], "gup", 512)
            ddump0(4, W1[:, 0, 0:512], "W1", 512)
            PP[0].max_ops = len(PP[0].ops)
            if STOP_EARLY:
                PP[0].finalize()
                raise _StopBuild()
        MUI = PA("mui")
        MUS = PA("mus")
        MLS = PA("mls")
        ONES = PA("ones")
        IDF = PA("ident")
        bc8 = lambda ap: ap.unsqueeze(2).to_broadcast([128, 8, 64])
        m8 = lambda m: m.unsqueeze(1).to_broadcast([128, 8, 128])
        v3 = lambda t: t[:].rearrange("p (h d) -> p h d", h=8)
        hoff = lambda h: (h % 2) * 512 + (h // 2) * 128

        def make_block(b):
            p_ = b % 2
            r_sb, kf_sb, vf, osig = G[0 + 20 * p_] if p_ == 0 else G[20], G[1] if p_ == 0 else G[21], G[2] if p_ == 0 else G[22], G[13] if p_ == 0 else G[23]
            vaug, gt, qpb, kTb, ktm, vrw, lor = VAUG[p_], GT[p_], QPB[p_], KTB[p_], KTM[p_], VRW[p_], LOR[p_]
            g_sb = G[5] if p_ == 0 else G[24]

            def front_stage():
                x_ = xt[b % 2]
                xk = "xt%d" % (b % 2)
                xn = xnT[b % 2]
                xnk = "xnT%d" % (b % 2)
                qx = qkx[b % 2]
                qxk = "qkx%d" % (b % 2)
                dma("sp", x_[:], xp[b * 128:(b + 1) * 128, :], [], [xk], xk)
                rmsnorm_T(x_, xk, 128, xn, xnk, 1, "nmw", xsb, "xsb", junk, "junk", st, "st")
                cur_x = xn[:, :, 1:129]
                prv_x = xn[:, :, 0:128]
                dve(lambda e, xn=xn: e.tensor_tensor(out=dxT[:], in0=xn[:, :, 0:128], in1=xn[:, :, 1:129], op=ALU.subtract), [xnk], ["dxT"])

                yield
                ps, psb, pk = nps()
                for j in range(8):
                    for c in range(8):
                        mm(ps[:, j * 128:(j + 1) * 128], wq[:, c, j * 128:(j + 1) * 128], cur_x[:, c, :], c == 0, c == 7, wqk(j * 128, (j + 1) * 128) + [xnk], [pk])
                for a_ in range(2):
                    act(lambda e, ps=ps, qx=qx, a_=a_: e.copy(out=qx[:, 4 * a_:4 * a_ + 4, 3:131], in_=ps[:, a_ * 512:(a_ + 1) * 512].rearrange("p (j t) -> p j t", j=4)), [pk], [qxk])
                if b == NB - 1:
                    dma("pool", oconv, qx[:, :, 128:131], [qxk], [], "fin")

                yield
                def tm_plain(col0, ncol, ps_ap, pk):
                    for c in range(8):
                        mm(ps_ap, cur_x[:, c, :], wq[:, c, col0:col0 + ncol], c == 0, c == 7, [xnk] + wqk(col0, col0 + ncol), [pk])

                def tm_shift(col0, ncol, dst, dk):
                    ps, psb, pk = nps()
                    for c in range(8):
                        mm(ps[:, 0:ncol], cur_x[:, c, :], wq[:, c, MLW + col0:MLW + col0 + ncol], c == 0, c == 7, [xnk] + wqk(MLW + col0, MLW + col0 + ncol), [pk])
                    for c in range(8):
                        mm(ps[:, 512:512 + ncol], dxT[:, c, :], wq[:, c, MLW + col0:MLW + col0 + ncol], c == 0, c == 7, ["dxT"] + wqk(MLW + col0, MLW + col0 + ncol), [pk])
                    dve(lambda e, ps=ps: e.tensor_tensor(out=dst, in0=ps[:, 512:512 + ncol], in1=mub[:, col0:col0 + ncol], op=ALU.mult), [pk, "mub"], [dk])
                    dve(lambda e, ps=ps: e.tensor_tensor(out=dst, in0=dst, in1=ps[:, 0:ncol], op=ALU.add), [pk, dk], [dk])

                ps, psb, pk = nps()
                tm_plain(1024, 512, ps[:, 0:512], pk)
                tm_plain(1536, 512, ps[:, 512:1024], pk)
                act(lambda e, ps=ps: e.copy(out=vaug[:, :, 0:64], in_=ps[:, 0:512].rearrange("p (h d) -> p h d", h=8)), [pk], ["vaug"])
                act(lambda e, ps=ps: e.activation(out=osig[:], in_=ps[:, 512:1024], func=AF.Sigmoid), [pk], ["osig"])
                ps, psb, pk = nps()
                tm_plain(2048, 16, ps[:, 0:16], pk)
                dve(lambda e, ps=ps: e.tensor_tensor(out=gt[:, 0:16], in0=ps[:, 0:16], in1=PA("ifb"), op=ALU.add), [pk, "pA"], ["gt"])
                tm_shift(0, 512, r_sb[:], "r_sb")
                tm_shift(512, 512, kf_sb[:], "kf_sb")
                tm_shift(1024, 512, vf[:], "vf")
                pool(lambda e: e.tensor_copy(out=vrw[:], in_=vf[:].rearrange("p (h d) -> p h d", h=8)), ["vf"], ["vrw"])
                ltmp = G[16]
                tm_shift(1536, 256, ltmp[:, 0:256], "cacc")
                act(lambda e: e.activation(out=lor[:, 0:64], in_=ltmp[:, 0:64], func=AF.Tanh), ["cacc"], ["lor"])
                act(lambda e: e.copy(out=lor[:, 64:128], in_=ltmp[:, 64:128]), ["cacc"], ["lor"])
                act(lambda e: e.activation(out=lor[:, 128:256], in_=ltmp[:, 128:256], func=AF.Sigmoid), ["cacc"], ["lor"])
                if b == NB - 1:
                    lastc = xn[:, :, 128:129]
                    for n0 in range(0, RWW, 512):
                        nn = min(512, RWW - n0)
                        ps2, _, pk2 = nps()
                        for c in range(8):
                            mm(ps2[0:1, 0:nn], lastc[:, c, :], wq[:, c, MLW + n0:MLW + n0 + nn], c == 0, c == 7, [xnk] + wqk(MLW + n0, MLW + n0 + nn), [pk2])
                        act(lambda e, ps2=ps2, n0=n0, nn=nn: e.copy(out=plast[0:1, 0:nn], in_=ps2[0:1, 0:nn]), [pk2], ["plast"])
                        dma("pool", oshift[:, n0:n0 + nn], plast[0:1, 0:nn], ["plast"], [], "fin")

                act(lambda e: e.activation(out=gt[:, 56:64], in_=gt[:, 8:16], func=AF.Exp, scale=-1.0), ["gt"], ["gt"])
                act(lambda e: e.activation(out=gt[:, 16:24], in_=gt[:, 56:64], func=AF.Ln, bias=1.0, scale=1.0), ["gt"], ["gt"])
                dve(lambda e: e.tensor_copy(out=nlrep[:], in_=bc8(gt[:, 16:24])), ["gt"], ["nlrep"])
                ps, psb, pk = nps()
                mm(ps[:, 0:8], MUI, gt[:, 16:24], True, True, ["pA", "gt"], [pk])
                mm(ps[:, 8:16], ONES, gt[:, 16:24], True, True, ["pA", "gt"], [pk])
                for j in range(4):
                    mm(ps[:, 512 + j * 128:512 + (j + 1) * 128], nlrep[:, 2 * j:2 * j + 2, :].rearrange("p a d -> p (a d)"), MUI, True, True, ["nlrep", "pA"], [pk])
                dve(lambda e, ps=ps: e.tensor_tensor(out=gt[:, 24:32], in0=ps[:, 0:8], in1=gt[:, 0:8], op=ALU.add), [pk, "gt"], ["gt"])
                act(lambda e: e.activation(out=gt[:, 32:40], in_=gt[:, 24:32], func=AF.Exp), ["gt"], ["gt"])
                dve(lambda e, ps=ps: e.tensor_tensor(out=gt[:, 56:64], in0=gt[:, 24:32], in1=ps[:, 8:16], op=ALU.subtract), [pk, "gt"], ["gt"])
                act(lambda e: e.activation(out=gt[:, 40:48], in_=gt[:, 56:64], func=AF.Exp), ["gt"], ["gt"])
                act(lambda e, ps=ps: e.activation(out=gt[:, 48:56], in_=ps[:, 8:16], func=AF.Exp, scale=-1.0), [pk], ["gt"])
                act(lambda e, ps=ps: e.activation(out=Fb[:], in_=ps[:, 512:1024].rearrange("p (j t) -> p j t", j=4), func=AF.Exp, scale=-1.0), [pk], ["Fb"])
                dve(lambda e: e.tensor_tensor(out=gt[:, 56:64], in0=gt[:, 24:32], in1=nBc[:], op=ALU.add), ["gt", "nBc"], ["gt"])
                dve(lambda e: e.tensor_tensor(out=runmax[:], in0=runmax[:], in1=gt[:, 56:64], op=ALU.max), ["gt", "runmax"], ["runmax"])
                dve(lambda e, ps=ps: e.tensor_tensor(out=nBc[:], in0=nBc[:], in1=ps[:, 8:16], op=ALU.add), [pk, "nBc"], ["nBc"])

                yield
                cwv = PA("cw").rearrange("p (c j) -> p c j", j=4)
                wbc = lambda j: cwv[:, :, j:j + 1].to_broadcast([128, 8, 128])
                pool(lambda e, qx=qx: e.tensor_tensor(out=cacc[:], in0=qx[:, :, 3:131], in1=wbc(3), op=ALU.mult), [qxk, "pA"], ["cacc"])
                for j in range(3):
                    pool(lambda e, qx=qx, j=j: e.tensor_tensor(out=ctmp[:], in0=qx[:, :, j:j + 128], in1=wbc(j), op=ALU.mult), [qxk, "pA"], ["ctmp"])
                    pool(lambda e: e.tensor_tensor(out=cacc[:], in0=cacc[:], in1=ctmp[:], op=ALU.add), ["cacc", "ctmp"], ["cacc"])
                pool(lambda e: e.tensor_tensor(out=cacc[:], in0=cacc[:], in1=PA("cb").unsqueeze(2).to_broadcast([128, 8, 128]), op=ALU.add), ["cacc", "pA"], ["cacc"])
                act(lambda e: e.activation(out=qks[:], in_=cacc[:], func=AF.Silu), ["cacc"], ["qks"])
                dve(lambda e: e.tensor_tensor(out=qpb[:], in0=qks[:, 0:4, :], in1=Fb[:], op=ALU.mult), ["qks", "Fb"], ["qpb"])
                act(lambda e: e.activation(out=kTb[:], in_=qks[:, 4:8, :], func=AF.Copy, scale=0.125), ["qks"], ["kTb"])

                yield
                ps, psb, pk = nps()
                for j in range(4):
                    pe(lambda e, j=j, psb=psb: e.transpose(psb[:, j * 128:(j + 1) * 128], kTb[:, j, :], identb[:]), ["kTb", "identb"], [pk])
                dve(lambda e, psb=psb: e.tensor_tensor(out=ktm[:], in0=psb[:, 0:512].rearrange("p (h d) -> p h d", h=8), in1=bc8(gt[:, 40:48]), op=ALU.mult), [pk, "gt"], ["ktm"])

                yield
                yield
                if b + 1 < NB:
                    pool(lambda e, xn=xn: e.tensor_copy(out=xn[:, :, 0:1], in_=xn[:, :, 128:129]), [xnk], [xnk])
                    pool(lambda e, qx=qx: e.tensor_copy(out=qx[:, :, 0:3], in_=qx[:, :, 128:131]), [qxk], [qxk])
                yield

            def ml_stage():
                ps, psb, pk = nps()
                for h in range(8):
                    j, hp = h // 2, h % 2
                    sl = slice(hp * 64, hp * 64 + 64)
                    mm(ps[:, hoff(h):hoff(h) + 128], kTb[sl, j, :], qpb[sl, j, :], True, True, ["kTb", "qpb"], [pk])
                for h in range(8):
                    dve(lambda e, h=h, ps=ps: e.scalar_tensor_tensor(out=PTb[:, h, :], in0=ps[:, hoff(h):hoff(h) + 128], scalar=gt[:, 32 + h:33 + h], in1=MUI, op0=ALU.mult, op1=ALU.mult), [pk, "gt", "pA"], ["PTb"])
                yield
                ps, psb, pk = nps()
                psn = lambda ps, h: ps[:, (h // 4) * 512 + (h % 4) * 65:(h // 4) * 512 + (h % 4) * 65 + 65]
                for h in range(8):
                    j, hp = h // 2, h % 2
                    sl = slice(hp * 64, hp * 64 + 64)
                    mm(psn(ps, h), PTb[:, h, :], vaug[:, h, :], True, False, ["PTb", "vaug"], [pk])
                    mm(psn(ps, h), qpb[sl, j, :], Cbf[sl, j, :], False, True, ["qpb", "Cbf"], [pk])
                pn4 = ps[:, :].rearrange("p (a r) -> p a r", a=2)[:, :, 0:260].rearrange("p a (h d) -> p a h d", h=4)
                for a_ in range(2):
                    act(lambda e, pn4=pn4, a_=a_: e.copy(out=r8[:, 4 * a_:4 * a_ + 4], in_=pn4[:, a_, :, 64]), [pk], ["r8"])
                dve(lambda e: e.scalar_tensor_tensor(out=r8[:, 8:16], in0=r8[:, 0:8], scalar=-1.0, in1=r8[:, 0:8], op0=ALU.mult, op1=ALU.max), ["r8"], ["r8"])
                dve(lambda e: e.tensor_scalar_max(out=r8[:, 8:16], in0=r8[:, 8:16], scalar1=1.0), ["r8"], ["r8"])
                dve(lambda e: e.reciprocal(out=r8[:, 16:24], in_=r8[:, 8:16]), ["r8"], ["r8"])
                for a_ in range(2):
                    dve(lambda e, pn4=pn4, a_=a_: e.tensor_tensor(out=hml[:, a_ * 256:(a_ + 1) * 256].rearrange("p (h d) -> p h d", h=4), in0=pn4[:, a_, :, 0:64],
                                                              in1=r8[:, 16 + 4 * a_:20 + 4 * a_].unsqueeze(2).to_broadcast([128, 4, 64]), op=ALU.mult), [pk, "r8"], ["hml"])
                yield
                ps, psb, pk = nps()
                for h in range(8):
                    j = h // 2
                    mm(psn(ps, h), ktm[:, 2 * j:2 * j + 2, :].rearrange("p a d -> p (a d)"), vaug[:, h, :], True, True, ["ktm", "vaug"], [pk])
                pu4 = ps[:, :].rearrange("p (a r) -> p a r", a=2)[:, :, 0:260].rearrange("p a (h d) -> p a h d", h=4)
                for hp in range(2):
                    sl = slice(hp * 64, hp * 64 + 64)
                    decb = gt[sl, 48:56].rearrange("p (j q) -> p j q", q=2)[:, :, hp:hp + 1].to_broadcast([64, 4, 65])
                    dve(lambda e, sl=sl, decb=decb: e.tensor_tensor(out=Cst[sl, :, :], in0=Cst[sl, :, :], in1=decb, op=ALU.mult), ["Cst", "gt"], ["Cst"])
                    for a in range(2):
                        src = pu4[sl, a, hp::2, :]
                        dve(lambda e, sl=sl, a=a, src=src: e.tensor_tensor(out=Cst[sl, 2 * a:2 * a + 2, :], in0=Cst[sl, 2 * a:2 * a + 2, :], in1=src, op=ALU.add), [pk, "Cst"], ["Cst"])
                act(lambda e: e.copy(out=Cbf[:], in_=Cst[:]), ["Cst"], ["Cbf"])
                head_ml(128, hml, "hml", osig, "osig", mix, "mix", WM, "m")


                yield
            def rw_stage():
                ps, psb, pk = nps()
                pe(lambda e, psb=psb: e.transpose(psb[:, 0:128], lor[:, 0:128], identb[:]), ["lor", "identb"], [pk])
                pe(lambda e, psb=psb: e.transpose(psb[:, 128:256], lor[:, 128:256], identb[:]), ["lor", "identb"], [pk])
                act(lambda e, psb=psb: e.copy(out=lorT[:], in_=psb[:, 0:256].rearrange("p (a t) -> p a t", a=2)), [pk], ["lorT"])
                ps, psb, pk = nps()
                mm(ps[:, 0:512], lorT[0:64, 0, :], luw[0:64, :], True, True, ["lorT", "luw"], [pk])
                mm(ps[:, 512:1024], lorT[64:128, 0, :], luw[64:128, :], True, True, ["lorT", "luw"], [pk])
                dve(lambda e, ps=ps: e.tensor_tensor(out=e1[:], in0=ps[:, 0:512], in1=PA("w0"), op=ALU.add), [pk, "pA"], ["e1"])
                act(lambda e: e.activation(out=wsig[:], in_=e1[:], func=AF.Sigmoid), ["e1"], ["wsig"])
                dve(lambda e, ps=ps: e.tensor_tensor(out=e2[:], in0=ps[:, 512:1024], in1=PA("a0"), op=ALU.add), [pk, "pA"], ["e2"])
                act(lambda e: e.activation(out=a_sb[:], in_=e2[:], func=AF.Sigmoid), ["e2"], ["a_sb"])
                ps, psb, pk = nps()
                mm(ps[:, 0:512], lorT[:, 1, :], gup[:, :], True, True, ["lorT", "gup"], [pk])
                act(lambda e, ps=ps: e.copy(out=g_sb[:], in_=ps[:, 0:512]), [pk], ["g_sb"])
                yield
                dve(lambda e: e.tensor_tensor(out=e1[:], in0=kf_sb[:], in1=PA("kk"), op=ALU.mult), ["kf_sb", "pA"], ["e1"])
                dve(lambda e: e.tensor_tensor(out=e2[:], in0=e1[:], in1=e1[:], op=ALU.mult), ["e1"], ["e2"])
                dve(lambda e: e.tensor_reduce(out=r8b[:, 0:8], in_=v3(e2), axis=AX.X, op=ALU.add), ["e2"], ["r8b"])
                dve(lambda e: e.tensor_scalar_max(out=r8b[:, 0:8], in0=r8b[:, 0:8], scalar1=1e-24), ["r8b"], ["r8b"])
                act(lambda e: e.activation(out=r8b[:, 0:8], in_=r8b[:, 0:8], func=AF.Ln), ["r8b"], ["r8b"])
                act(lambda e: e.activation(out=r8b[:, 0:8], in_=r8b[:, 0:8], func=AF.Exp, scale=-0.5), ["r8b"], ["r8b"])
                dve(lambda e: e.tensor_tensor(out=v3(kap), in0=v3(e1), in1=bc8(r8b[:, 0:8]), op=ALU.mult), ["e1", "r8b"], ["kap"])
                dve(lambda e: e.tensor_scalar_add(out=e2[:], in0=a_sb[:], scalar1=-1.0), ["a_sb"], ["e2"])
                dve(lambda e: e.tensor_tensor(out=e2[:], in0=e2[:], in1=PA("ka"), op=ALU.mult), ["e2", "pA"], ["e2"])
                dve(lambda e: e.tensor_tensor(out=e2[:], in0=e2[:], in1=kf_sb[:], op=ALU.mult), ["e2", "kf_sb"], ["e2"])
                dve(lambda e: e.tensor_tensor(out=ktl[:], in0=e2[:], in1=kf_sb[:], op=ALU.add), ["e2", "kf_sb"], ["ktl"])
                dve(lambda e: e.tensor_tensor(out=bvec[:], in0=a_sb[:], in1=kap[:], op=ALU.mult), ["a_sb", "kap"], ["bvec"])
                dve(lambda e: e.tensor_tensor(out=e2[:], in0=r_sb[:], in1=ktl[:], op=ALU.mult), ["r_sb", "ktl"], ["e2"])
                dve(lambda e: e.tensor_tensor(out=e2[:], in0=e2[:], in1=PA("rk"), op=ALU.mult), ["e2", "pA"], ["e2"])
                dve(lambda e: e.tensor_reduce(out=bon[:], in_=v3(e2), axis=AX.X, op=ALU.add), ["e2"], ["bon"])
                yield
                ps, psb, pk = nps()
                mm(ps[:, 0:512], MUI, wsig[:], True, True, ["pA", "wsig"], [pk])
                mm(ps[:, 512:1024], ONES, wsig[:], True, True, ["pA", "wsig"], [pk])
                act(lambda e, ps=ps: e.copy(out=pcw[:], in_=ps[:, 0:512]), [pk], ["pcw"])
                dve(lambda e: e.tensor_tensor(out=e1[:], in0=pcw[:], in1=wsig[:], op=ALU.subtract), ["pcw", "wsig"], ["e1"])
                act(lambda e: e.activation(out=e1[:], in_=e1[:], func=AF.Exp, scale=-C0), ["e1"], ["e1"])
                dve(lambda e: e.tensor_tensor(out=TMb[:, 0, :], in0=kap[:], in1=e1[:], op=ALU.mult), ["kap", "e1"], ["TMb0"])
                act(lambda e: e.activation(out=e2[:], in_=pcw[:], func=AF.Exp, scale=-C0), ["pcw"], ["e2"])
                dve(lambda e: e.tensor_tensor(out=TMb[:, 1, :], in0=r_sb[:], in1=e2[:], op=ALU.mult), ["r_sb", "e2"], ["TMb1"])
                act(lambda e: e.activation(out=e3[:], in_=pcw[:], func=AF.Exp, scale=C0), ["pcw"], ["e3"])
                dve(lambda e: e.tensor_tensor(out=TMb[:, 2, :], in0=bvec[:], in1=e3[:], op=ALU.mult), ["bvec", "e3"], ["TMb2"])
                dve(lambda e: e.tensor_tensor(out=TMb[:, 3, :], in0=ktl[:], in1=e3[:], op=ALU.mult), ["ktl", "e3"], ["TMb3"])
                dve(lambda e, ps=ps: e.tensor_tensor(out=e1[:], in0=ps[:, 512:1024], in1=pcw[:], op=ALU.subtract), [pk, "pcw"], ["e1"])
                act(lambda e: e.activation(out=e1[:], in_=e1[:], func=AF.Exp, scale=-C0), ["e1"], ["e1"])
                for hp in range(2):
                    srcb = v3(bvec).rearrange("p (j q) d -> p j q d", q=2)[:, :, hp, :]
                    srck = v3(ktl).rearrange("p (j q) d -> p j q d", q=2)[:, :, hp, :]
                    wl = v3(e1).rearrange("p (j q) d -> p j q d", q=2)[:, :, hp, :]
                    dstb = Btz[:].rearrange("p (j q) c -> p j q c", q=2)[:, :, hp, hp * 64:hp * 64 + 64]
                    dstk = Ktz[:].rearrange("p (j q) c -> p j q c", q=2)[:, :, hp, hp * 64:hp * 64 + 64]
                    dve(lambda e, srcb=srcb, wl=wl, dstb=dstb: e.tensor_tensor(out=dstb, in0=srcb, in1=wl, op=ALU.mult), ["bvec", "e1"], ["Btz"])
                    dve(lambda e, srck=srck, wl=wl, dstk=dstk: e.tensor_tensor(out=dstk, in0=srck, in1=wl, op=ALU.mult), ["ktl", "e1"], ["Ktz"])
                ps2, _, pk2 = nps()
                for j in range(4):
                    mm(ps2[:, j:j + 1], wsig[:, j * 128:(j + 1) * 128], ONES[:, 0:1], True, True, ["wsig", "pA"], [pk2])
                act(lambda e, ps2=ps2: e.activation(out=WLfm[:], in_=ps2[:, 0:4], func=AF.Exp, scale=-C0), [pk2], ["WLfm"])
                yield
                ps, psb, pk = nps()
                for w_ in range(4):
                    for j in range(4):
                        pe(lambda e, w_=w_, j=j, psb=psb: e.transpose(psb[:, (w_ * 4 + j) * 128:(w_ * 4 + j + 1) * 128], TMb[:, w_, j * 128:(j + 1) * 128], identb[:]), ["TMb%d" % w_, "identb"], [pk])
                for w_ in range(4):
                    eng_ = act if w_ % 2 == 0 else dve
                    if w_ % 2 == 0:
                        act(lambda e, psb=psb, w_=w_: e.copy(out=FMt[:, w_, :, :], in_=psb[:, w_ * 512:(w_ + 1) * 512].rearrange("p (j t) -> p j t", j=4)), [pk], ["FMt"])
                    else:
                        dve(lambda e, psb=psb, w_=w_: e.tensor_copy(out=FMt[:, w_, :, :], in_=psb[:, w_ * 512:(w_ + 1) * 512].rearrange("p (j t) -> p j t", j=4)), [pk], ["FMt"])
                KAP, RB, BB, KKB = 0, 1, 2, 3

                def amat(lw, rw_, dst, dk, mask, neg):
                    ps, psb, pk = nps()
                    for h in range(8):
                        j, hp = h // 2, h % 2
                        sl = slice(hp * 64, hp * 64 + 64)
                        mm(ps[:, hoff(h):hoff(h) + 128], FMt[sl, lw, j, :], FMt[sl, rw_, j, :], True, True, ["FMt"], [pk])
                    psv = ps[:, :].rearrange("p (q j t) -> p q j t", q=2, j=4)
                    dstv = dst[:].rearrange("p (j q) t -> p q j t", q=2)
                    mk = mask.unsqueeze(1).unsqueeze(1).to_broadcast([128, 2, 4, 128])
                    if neg:
                        mk3 = mask.unsqueeze(1).to_broadcast([128, 4, 128])
                        for q in range(2):
                            dve(lambda e, q=q: e.scalar_tensor_tensor(out=dstv[:, q], in0=psv[:, q], scalar=-1.0, in1=mk3, op0=ALU.mult, op1=ALU.mult), [pk, "pA"], [dk])
                    else:
                        mk3 = mask.unsqueeze(1).to_broadcast([128, 4, 128])
                        for q in range(2):
                            dve(lambda e, q=q: e.tensor_tensor(out=dstv[:, q], in0=psv[:, q], in1=mk3, op=ALU.mult), [pk, "pA"], [dk])

                amat(BB, KAP, Pw[1], "Pw1", MUS, True)
                amat(KAP, BB, Pw[0], "Pw0", MLS, True)
                amat(KKB, KAP, Am[0], "Am0", MUS, False)
                amat(BB, RB, Am[1], "Am1", MUI, False)
                amat(KKB, RB, Am[2], "Am2", MUI, False)
                yield
                ps, psb, pk = nps()
                for h in range(8):
                    j, hp = h // 2, h % 2
                    sl = slice(hp * 64, hp * 64 + 64)
                    mm(ps[:, h * 64:(h + 1) * 64], FMt[sl, KAP, j, :], Sbf[sl, j, :], True, False, ["FMt", "Sbf"], [pk])
                    mm(ps[:, h * 64:(h + 1) * 64], Am[0][:, h, :], vrw[:, h, :], False, True, ["Am0", "vrw"], [pk])
                act(lambda e, ps=ps: e.copy(out=Zb[0][:], in_=ps[:, 0:512].rearrange("p (h d) -> p h d", h=8)), [pk], ["Zb0"])
                pi = 0
                zi = 0
                for lvl in range(7):
                    yield
                    Pc, PTc = Pw[pi], Pw[pi + 1]
                    Pk, PTk = "Pw%d" % pi, "Pw%d" % (pi + 1)
                    Zc, Zn = Zb[zi], Zb[1 - zi]
                    ps, psb, pk = nps()
                    for h in range(8):
                        mm(ps[:, h * 64:(h + 1) * 64], identb[:], Zc[:, h, :], True, False, ["identb", "Zb%d" % zi], [pk])
                        mm(ps[:, h * 64:(h + 1) * 64], PTc[:, h, :], Zc[:, h, :], False, True, [PTk, "Zb%d" % zi], [pk])
                    if lvl < 6:
                        act(lambda e, ps=ps, Zn=Zn: e.copy(out=Zn[:], in_=ps[:, 0:512].rearrange("p (h d) -> p h d", h=8)), [pk], ["Zb%d" % (1 - zi)])
                        zi = 1 - zi
                        ni = 2 - pi
                        Pn, PTn = Pw[ni], Pw[ni + 1]
                        psA, _, pkA = nps()
                        for h in range(8):
                            mm(psA[:, h * 128:(h + 1) * 128], PTc[:, h, :], Pc[:, h, :], True, True, [PTk, Pk], [pkA])
                        for a_ in range(2):
                            dve(lambda e, psA=psA, Pn=Pn, a_=a_: e.tensor_copy(out=Pn[:, 4 * a_:4 * a_ + 4, :], in_=psA[:, a_ * 512:(a_ + 1) * 512].rearrange("p (h t) -> p h t", h=4)), [pkA], ["Pw%d" % ni])
                        psB, _, pkB = nps()
                        for h in range(8):
                            mm(psB[:, h * 128:(h + 1) * 128], Pc[:, h, :], PTc[:, h, :], True, True, [Pk, PTk], [pkB])
                        for a_ in range(2):
                            act(lambda e, psB=psB, PTn=PTn, a_=a_: e.copy(out=PTn[:, 4 * a_:4 * a_ + 4, :], in_=psB[:, a_ * 512:(a_ + 1) * 512].rearrange("p (h t) -> p h t", h=4)), [pkB], ["Pw%d" % (ni + 1)])
                        pi = ni
                    else:
                        act(lambda e, ps=ps: e.activation(out=Ub[:], in_=ps[:, 0:512].rearrange("p (h d) -> p h d", h=8), func=AF.Copy, scale=-1.0), [pk], ["Ub"])
                yield
                ps, psb, pk = nps()
                for h in range(8):
                    j, hp = h // 2, h % 2
                    sl = slice(hp * 64, hp * 64 + 64)
                    o_ = ps[:, h * 64:(h + 1) * 64]
                    mm(o_, Am[1][:, h, :], Ub[:, h, :], True, False, ["Am1", "Ub"], [pk])
                    mm(o_, Am[2][:, h, :], vrw[:, h, :], False, False, ["Am2", "vrw"], [pk])
                    mm(o_, FMt[sl, RB, j, :], Sbf[sl, j, :], False, True, ["FMt", "Sbf"], [pk])
                act(lambda e, ps=ps: e.copy(out=ysb[:], in_=ps[:, 0:512]), [pk], ["ysb"])
                yield
                ps, psb, pk = nps()
                for j in range(4):
                    o_ = ps[:, j * 64:(j + 1) * 64]
                    mm(o_, Btz[:, 2 * j, :], Ub[:, 2 * j, :], True, False, ["Btz", "Ub"], [pk])
                    mm(o_, Ktz[:, 2 * j, :], vrw[:, 2 * j, :], False, False, ["Ktz", "vrw"], [pk])
                    mm(o_, Btz[:, 2 * j + 1, :], Ub[:, 2 * j + 1, :], False, False, ["Btz", "Ub"], [pk])
                    mm(o_, Ktz[:, 2 * j + 1, :], vrw[:, 2 * j + 1, :], False, True, ["Ktz", "vrw"], [pk])
                dve(lambda e: e.tensor_tensor(out=Sst[:], in0=Sst[:], in1=WLfm[:].unsqueeze(2).to_broadcast([128, 4, 64]), op=ALU.mult), ["Sst", "WLfm"], ["Sst"])
                dve(lambda e, ps=ps: e.tensor_tensor(out=Sst[:], in0=Sst[:], in1=ps[:, 0:256].rearrange("p (j d) -> p j d", j=4), op=ALU.add), [pk, "Sst"], ["Sst"])
                act(lambda e: e.copy(out=Sbf[:], in_=Sst[:]), ["Sst"], ["Sbf"])
                yield
            def tail():
                head_rw(128, ysb, "ysb", bon, "bon", vf, "vf", g_sb, "g_sb", mix, "mix", {"tA": G[18], "tB": G[19], "s8": W["s8"]})
                out_proj(128, mix, "mix", None, None, b * 128, W)

            return front_stage, ml_stage, rw_stage, tail, p_

        def run_gens(gl):
            gl = list(gl)
            while gl:
                for item in list(gl):
                    CURP[0] = item[1]
                    CURPOOL[0] = item[3] if len(item) > 3 else "A"
                    for _ in range(item[2] if len(item) > 2 else 1):
                        try:
                            next(item[0])
                        except StopIteration:
                            gl.remove(item)
                            break

        blocks = [make_block(b) for b in range(NB)]
        run_gens([(blocks[0][0](), 0, 1, "A")])
        for b in range(NB):
            fr, ml_, rw_, tl, p_ = blocks[b]
            gl = [(rw_(), p_, 1, "A"), (ml_(), p_, 1, "A")]
            if b + 1 < NB:
                gl.append((blocks[b + 1][0](), (b + 1) % 2, 1, "A"))
            run_gens(gl)
            CURP[0] = p_
            CURPOOL[0] = "A"
            tl()
        CURP[0] = 0

        ps, psb, pk = nps()
        mm(ps[0:8, 0:128], runmax[:], IDF, True, True, ["runmax", "pA"], [pk])
        mm(ps[0:8, 128:256], nBc[:], IDF, True, True, ["nBc", "pA"], [pk])
        fs = TT("fs", [8, 16])
        dve(lambda e, ps=ps: e.tensor_reduce(out=fs[:, 0:1], in_=ps[0:8, 0:128], axis=AX.X, op=ALU.max), [pk], ["fs"])
        dve(lambda e: e.tensor_scalar_max(out=fs[:, 0:1], in0=fs[:, 0:1], scalar1=0.0), ["fs"], ["fs"])
        dve(lambda e, ps=ps: e.tensor_tensor(out=fs[:, 1:2], in0=fs[:, 0:1], in1=ps[0:8, 128:129], op=ALU.subtract), [pk, "fs"], ["fs"])
        dma("pool", om, fs[:, 1:2], ["fs"], [], "fin")
        act(lambda e: e.activation(out=fs[:, 2:3], in_=fs[:, 1:2], func=AF.Exp, scale=-1.0), ["fs"], ["fs"])
        dve(lambda e: e.tensor_scalar_mul(out=fs[:, 4:8], in0=pA[0:8, OFF["rsel"][0]:OFF["rsel"][1]], scalar1=fs[:, 2:3]), ["fs", "pA"], ["fs"])
        ps, psb, pk = nps()
        mm(ps[:, 0:4], pA[0:8, OFF["lsel"][0]:OFF["lsel"][1]], fs[:, 4:8], True, True, ["pA", "fs"], [pk])
        scb = TT("scb", [128, 4])
        act(lambda e, ps=ps: e.copy(out=scb[:], in_=ps[:, 0:4]), [pk], ["scb"])
        dve(lambda e: e.tensor_tensor(out=Cst[:], in0=Cst[:], in1=scb[:].unsqueeze(2).to_broadcast([128, 4, 65]), op=ALU.mult), ["Cst", "scb"], ["Cst"])
        dma("pool", oC, Cst[:], ["Cst"], [], "fin")
        dma("pool", oS, Sst[:], ["Sst"], [], "fin")
        PP[0].finalize()
        PP[0] = Prog(ctx)

    with contextlib.ExitStack() as es_s:
        cur[0] = es_s
        if do_sample:
            W = {}
            W["xm"] = TT("s_xm", [128, D])
            junk = W["xm"]
            st = TT("s_st", [128, 4])
            mix = TT("s_mix", [128, D], BF16)
            xsb = mix
            lor = TT("s_lor", [128, 256], BF16)
            lorT = TT("s_lorT", [128, 2, 128], BF16)
            W["mixT"] = TT("s_mixT", [128, 8, 128], BF16)
            sx = TT("sx", [NS, D])
            sxT = TT("sxT", [128, 8, NS], BF16)
            spj = TT("spj", [NS, INW])
            spk = TT("spk", [128, NSP])
            sl_t = TT("sl_t", [NS, 3, 256])
            Cs = TT("Cs", [128, 4096])
            Ss = Cs
            sn_t = TT("sn_t", [128, 64])
            sm_t = TT("sm_t", [128, 1])
            scv = TT("scv", [128, 2, 4, 64])
            ssh = TT("ssh", [128, 3, 64])
            dma("sp", sx[:], xs, [], ["sx"], "sin", True)
            dma("sp", spk[:], spk_d, [], ["spk"], "sin", True)
            dma("sp", sl_t[:, 0, :], sshl_d, [], ["sl_t"], "sin", True)
            dma("sp", sl_t[:, 1, :], mul_d, [], ["sl_t"], "sin", True)
            dma("sp", Cs[:], sC_d, [], ["Cs"], "sin", True)
            dma("sp", sn_t[:], sn_d, [], ["sn_t"], "sin", True)
            dma("sp", sm_t[:], sm_d, [], ["sm_t"], "sin", True)
            dma("sp", scv[:, :, 0:3, :], sconv_d, [], ["scv"], "sin", True)
            dma("sp", ssh[:], sshift_d, [], ["ssh"], "sin", True)
            rmsnorm_T(sx, "sx", NS, sxT, "sxT", 0, "nmw", xsb, "xsb", junk, "junk", st, "st")
            for n0 in range(0, INW, 512):
                nn = min(512, INW - n0)
                ps, psb, pk = nps()
                for c in range(8):
                    mm(ps[:NS, 0:nn], sxT[:, c, :], wq[:, c, n0:n0 + nn], c == 0, c == 7, ["sxT", "wq"], [pk])
                act(lambda e, ps=ps, n0=n0, nn=nn: e.copy(out=spj[:, n0:n0 + nn], in_=ps[:NS, 0:nn]), [pk], ["spj"])
            s1v = scr1.rearrange("(b h) a d -> b a h d", b=NS)
            for a_ in range(7):
                c0_ = a_ * 512 if a_ < 4 else MLW + (a_ - 4) * 512
                dma("pool", s1v[:, a_, :, :], spj[:, c0_:c0_ + 512].rearrange("p (h d) -> p h d", h=8), ["spj"], ["scr1"], "scrw1")
            A7 = TT("A7", [128, 8, 64])
            dma("sp", A7[:, 0:7, :], scr1[:, 0:7, :], ["scr1"], ["A7"], "scrr1")
            gif = TT("gif", [128, 2])
            s_if = nc.dram_tensor("scr_if", [2, 128], F32, kind="Internal").ap()
            for g_ in range(2):
                dma("pool", s_if[g_, :].rearrange("(b h) -> b h", b=NS), spj[:, 2048 + 8 * g_:2056 + 8 * g_], ["spj"], ["scr_if"], "scrwif")
            for g_ in range(2):
                dma("sp", gif[:, g_:g_ + 1], s_if[g_, :].rearrange("(p o) -> p o", o=1), ["scr_if"], ["gif"], "scrrif")
            SP_ = lambda n: spk[:, SOFF[n][0]:SOFF[n][1]]
            pl = spj[:, MLW + 1536:MLW + 1792]
            dma("pool", osshl, pl, ["spj"], [], "fin")
            dve(lambda e: e.tensor_tensor(out=sl_t[:, 2, :], in0=sl_t[:, 0, :], in1=pl, op=ALU.subtract), ["sl_t", "spj"], ["sl_t"])
            dve(lambda e: e.tensor_tensor(out=sl_t[:, 2, :], in0=sl_t[:, 2, :], in1=sl_t[:, 1, :], op=ALU.mult), ["sl_t"], ["sl_t"])
            dve(lambda e: e.tensor_tensor(out=sl_t[:, 2, :], in0=sl_t[:, 2, :], in1=pl, op=ALU.add), ["sl_t", "spj"], ["sl_t"])
            act(lambda e: e.activation(out=lor[:NS, 0:64], in_=sl_t[:, 2, 0:64], func=AF.Tanh), ["sl_t"], ["lor"])
            act(lambda e: e.copy(out=lor[:NS, 64:128], in_=sl_t[:, 2, 64:128]), ["sl_t"], ["lor"])
            act(lambda e: e.activation(out=lor[:NS, 128:256], in_=sl_t[:, 2, 128:256], func=AF.Sigmoid), ["sl_t"], ["lor"])
            ps, psb, pk = nps()
            pe(lambda e, psb=psb: e.transpose(psb[:, 0:NS], lor[:NS, 0:128], identb[:NS, :NS]), ["lor", "identb"], [pk])
            pe(lambda e, psb=psb: e.transpose(psb[:, 128:128 + NS], lor[:NS, 128:256], identb[:NS, :NS]), ["lor", "identb"], [pk])
            act(lambda e, psb=psb: e.copy(out=lorT[:, :, 0:NS], in_=psb[:, 0:256].rearrange("p (a t) -> p a t", a=2)[:, :, 0:NS]), [pk], ["lorT"])
            ps, psb, pk = nps()
            mm(ps[:NS, 0:512], lorT[0:64, 0, 0:NS], luw[0:64, :], True, True, ["lorT", "luw"], [pk])
            mm(ps[:NS, 512:1024], lorT[64:128, 0, 0:NS], luw[64:128, :], True, True, ["lorT", "luw"], [pk])
            ps2, _, pk2 = nps()
            mm(ps2[:NS, 0:512], lorT[:, 1, 0:NS], gup[:, :], True, True, ["lorT", "gup"], [pk2])
            lo3 = spj[:, 0:1536].rearrange("p (a n) -> p a n", a=3)
            for a_ in range(2):
                act(lambda e, ps=ps, a_=a_: e.copy(out=lo3[:, a_, :], in_=ps[:NS, a_ * 512:(a_ + 1) * 512]), [pk], ["lo3"])
            act(lambda e, ps2=ps2: e.copy(out=lo3[:, 2, :], in_=ps2[:NS, 0:512]), [pk2], ["lo3"])
            for a_ in range(3):
                dma("pool", scr2.rearrange("(b h) a d -> b a h d", b=NS)[:, a_, :, :], lo3[:, a_, :].rearrange("p (h d) -> p h d", h=8), ["lo3"], ["scr2"], "scrw2")
            L3 = TT("L3", [128, 3, 64])
            dma("sp", L3[:], scr2, ["scr2"], ["L3"], "scrr2")
            big = TT("big", [128, 4096])
            sv = TT("sv", [128, 64])
            pool(lambda e: e.tensor_copy(out=scv[:, :, 3, :], in_=A7[:, 0:2, :]), ["A7"], ["scv"])
            dma("pool", osconv, scv[:, :, 1:4, :], ["scv"], [], "fin")
            qk_s = TT("qk_s", [128, 2, 64])
            cwqk = lambda w_: spk[:, SOFF["cwq"][0] + w_ * 256:SOFF["cwq"][0] + (w_ + 1) * 256].rearrange("p (j d) -> p j d", j=4)
            for w_ in range(2):
                dve(lambda e, w_=w_: e.tensor_tensor(out=big[:, 0:256].rearrange("p (j d) -> p j d", j=4), in0=scv[:, w_, :, :], in1=cwqk(w_), op=ALU.mult), ["scv", "spk"], ["big"])
                dve(lambda e, w_=w_: e.tensor_reduce(out=qk_s[:, w_, :], in_=big[:, 0:256].rearrange("p (j d) -> p d j", j=4), axis=AX.X, op=ALU.add), ["big"], ["qk_s"])
            dve(lambda e: e.tensor_tensor(out=qk_s[:], in0=qk_s[:], in1=spk[:, SOFF["cbq"][0]:SOFF["cbk"][1]].rearrange("p (a d) -> p a d", a=2), op=ALU.add), ["qk_s", "spk"], ["qk_s"])
            act(lambda e: e.activation(out=qk_s[:], in_=qk_s[:], func=AF.Silu), ["qk_s"], ["qk_s"])
            act(lambda e: e.activation(out=qk_s[:, 1, :], in_=qk_s[:, 1, :], func=AF.Copy, scale=0.125), ["qk_s"], ["qk_s"])
            dve(lambda e: e.tensor_tensor(out=sv[:, 0:2], in0=gif[:], in1=spk[:, SOFF["ib"][0]:SOFF["fb"][1]], op=ALU.add), ["gif", "spk"], ["sv"])
            act(lambda e: e.activation(out=sv[:, 9:10], in_=sv[:, 1:2], func=AF.Exp, scale=-1.0), ["sv"], ["sv"])
            act(lambda e: e.activation(out=sv[:, 2:3], in_=sv[:, 9:10], func=AF.Ln, bias=1.0, scale=1.0), ["sv"], ["sv"])
            dve(lambda e: e.tensor_tensor(out=sv[:, 3:4], in0=sm_t[:], in1=sv[:, 2:3], op=ALU.subtract), ["sv", "sm_t"], ["sv"])
            dve(lambda e: e.tensor_tensor(out=sv[:, 4:5], in0=sv[:, 3:4], in1=sv[:, 0:1], op=ALU.max), ["sv"], ["sv"])
            dma("pool", osm, sv[:, 4:5], ["sv"], [], "fin")
            dve(lambda e: e.tensor_tensor(out=sv[:, 9:10], in0=sv[:, 0:1], in1=sv[:, 4:5], op=ALU.subtract), ["sv"], ["sv"])
            act(lambda e: e.activation(out=sv[:, 5:6], in_=sv[:, 9:10], func=AF.Exp), ["sv"], ["sv"])
            dve(lambda e: e.tensor_tensor(out=sv[:, 9:10], in0=sv[:, 3:4], in1=sv[:, 4:5], op=ALU.subtract), ["sv"], ["sv"])
            act(lambda e: e.activation(out=sv[:, 6:7], in_=sv[:, 9:10], func=AF.Exp), ["sv"], ["sv"])
            act(lambda e: e.activation(out=sv[:, 7:8], in_=sv[:, 4:5], func=AF.Exp, scale=-1.0), ["sv"], ["sv"])
            q_ = qk_s[:, 0, :]
            k_ = qk_s[:, 1, :]
            v_ = A7[:, 2, :]
            b3 = lambda t: t[:, :].rearrange("p (a c) -> p a c", a=64)
            pool(lambda e: e.tensor_tensor(out=b3(big), in0=k_.unsqueeze(2).to_broadcast([128, 64, 64]), in1=v_.unsqueeze(1).to_broadcast([128, 64, 64]), op=ALU.mult), ["qk_s", "A7"], ["big"])
            dve(lambda e: e.tensor_scalar_mul(out=Cs[:], in0=Cs[:], scalar1=sv[:, 6:7]), ["Cs", "sv"], ["Cs"])
            dve(lambda e: e.scalar_tensor_tensor(out=Cs[:], in0=big[:], scalar=sv[:, 5:6], in1=Cs[:], op0=ALU.mult, op1=ALU.add), ["big", "sv", "Cs"], ["Cs"])
            dma("pool", osC, Cs[:], ["Cs"], [], "fin")
            dve(lambda e: e.tensor_scalar_mul(out=sn_t[:], in0=sn_t[:], scalar1=sv[:, 6:7]), ["sn_t", "sv"], ["sn_t"])
            dve(lambda e: e.scalar_tensor_tensor(out=sn_t[:], in0=k_, scalar=sv[:, 5:6], in1=sn_t[:], op0=ALU.mult, op1=ALU.add), ["qk_s", "sv", "sn_t"], ["sn_t"])
            dma("pool", osn, sn_t[:], ["sn_t"], [], "fin")
            pool(lambda e: e.tensor_tensor(out=b3(big), in0=Cs[:, :].rearrange("p (k v) -> p v k", k=64), in1=q_.unsqueeze(1).to_broadcast([128, 64, 64]), op=ALU.mult), ["Cs", "qk_s"], ["big"])
            hs = TT("hs", [128, 2, 64])
            dve(lambda e: e.tensor_reduce(out=hs[:, 0, :], in_=b3(big), axis=AX.X, op=ALU.add), ["big"], ["hs"])
            dve(lambda e: e.tensor_tensor(out=sv[:, 16:80 - 16] if False else big[:, 0:64], in0=q_, in1=sn_t[:], op=ALU.mult), ["qk_s", "sn_t"], ["big"])
            dve(lambda e: e.tensor_reduce(out=sv[:, 8:9], in_=big[:, 0:64], axis=AX.X, op=ALU.add), ["big"], ["sv"])
            dve(lambda e: e.scalar_tensor_tensor(out=sv[:, 9:10], in0=sv[:, 8:9], scalar=-1.0, in1=sv[:, 8:9], op0=ALU.mult, op1=ALU.max), ["sv"], ["sv"])
            dve(lambda e: e.tensor_tensor(out=sv[:, 9:10], in0=sv[:, 9:10], in1=sv[:, 7:8], op=ALU.max), ["sv"], ["sv"])
            dve(lambda e: e.reciprocal(out=sv[:, 10:11], in_=sv[:, 9:10]), ["sv"], ["sv"])
            dve(lambda e: e.tensor_scalar_mul(out=hs[:, 0, :], in0=hs[:, 0, :], scalar1=sv[:, 10:11]), ["hs", "sv"], ["hs"])
            dma("pool", osshift, A7[:, 4:7, :], ["A7"], [], "fin")
            rk3 = TT("rk3", [128, 3, 64])
            mu3 = spk[:, SOFF["mu_r"][0]:SOFF["mu_v"][1]].rearrange("p (a d) -> p a d", a=3)
            dve(lambda e: e.tensor_tensor(out=rk3[:], in0=ssh[:], in1=A7[:, 4:7, :], op=ALU.subtract), ["ssh", "A7"], ["rk3"])
            dve(lambda e: e.tensor_tensor(out=rk3[:], in0=rk3[:], in1=mu3, op=ALU.mult), ["rk3", "spk"], ["rk3"])
            dve(lambda e: e.tensor_tensor(out=rk3[:], in0=rk3[:], in1=A7[:, 4:7, :], op=ALU.add), ["rk3", "A7"], ["rk3"])
            w8 = TT("w8", [128, 8, 64])
            dve(lambda e: e.tensor_tensor(out=w8[:, 0, :], in0=L3[:, 0, :], in1=SP_("w0"), op=ALU.add), ["L3", "spk"], ["w8"])
            act(lambda e: e.activation(out=w8[:, 0, :], in_=w8[:, 0, :], func=AF.Sigmoid), ["w8"], ["w8"])
            act(lambda e: e.activation(out=w8[:, 0, :], in_=w8[:, 0, :], func=AF.Exp, scale=-C0), ["w8"], ["w8"])
            dve(lambda e: e.tensor_tensor(out=w8[:, 1, :], in0=L3[:, 1, :], in1=SP_("a0"), op=ALU.add), ["L3", "spk"], ["w8"])
            act(lambda e: e.activation(out=w8[:, 1, :], in_=w8[:, 1, :], func=AF.Sigmoid), ["w8"], ["w8"])
            dve(lambda e: e.tensor_tensor(out=w8[:, 6, :], in0=rk3[:, 1, :], in1=SP_("kk"), op=ALU.mult), ["rk3", "spk"], ["w8"])
            dve(lambda e: e.tensor_tensor(out=w8[:, 7, :], in0=w8[:, 6, :], in1=w8[:, 6, :], op=ALU.mult), ["w8"], ["w8"])
            dve(lambda e: e.tensor_reduce(out=sv[:, 11:12], in_=w8[:, 7, :], axis=AX.X, op=ALU.add), ["w8"], ["sv"])
            dve(lambda e: e.tensor_scalar_max(out=sv[:, 11:12], in0=sv[:, 11:12], scalar1=1e-24), ["sv"], ["sv"])
            act(lambda e: e.activation(out=sv[:, 11:12], in_=sv[:, 11:12], func=AF.Sqrt), ["sv"], ["sv"])
            dve(lambda e: e.reciprocal(out=sv[:, 11:12], in_=sv[:, 11:12]), ["sv"], ["sv"])
            dve(lambda e: e.tensor_scalar_mul(out=w8[:, 3, :], in0=w8[:, 6, :], scalar1=sv[:, 11:12]), ["w8", "sv"], ["w8"])
            dve(lambda e: e.tensor_tensor(out=w8[:, 6, :], in0=w8[:, 1, :], in1=SP_("ka"), op=ALU.mult), ["w8", "spk"], ["w8"])
            dve(lambda e: e.tensor_tensor(out=w8[:, 6, :], in0=w8[:, 6, :], in1=SP_("ka"), op=ALU.subtract), ["w8", "spk"], ["w8"])
            dve(lambda e: e.tensor_scalar_add(out=w8[:, 6, :], in0=w8[:, 6, :], scalar1=1.0), ["w8"], ["w8"])
            dve(lambda e: e.tensor_tensor(out=w8[:, 4, :], in0=rk3[:, 1, :], in1=w8[:, 6, :], op=ALU.mult), ["rk3", "w8"], ["w8"])
            dve(lambda e: e.tensor_tensor(out=w8[:, 5, :], in0=w8[:, 1, :], in1=w8[:, 3, :], op=ALU.mult), ["w8"], ["w8"])
            dma("sp", Ss[:], sS_d, [], ["Ss"], "sin2")
            bk = lambda ap: ap.unsqueeze(1).to_broadcast([128, 64, 64])
            bv = lambda ap: ap.unsqueeze(2).to_broadcast([128, 64, 64])
            pool(lambda e: e.tensor_tensor(out=b3(big), in0=b3(Ss), in1=bk(w8[:, 3, :]), op=ALU.mult), ["Ss", "w8"], ["big"])
            dve(lambda e: e.tensor_reduce(out=w8[:, 7, :], in_=b3(big), axis=AX.X, op=ALU.add), ["big"], ["w8"])
            dve(lambda e: e.tensor_tensor(out=b3(Ss), in0=b3(Ss), in1=bk(w8[:, 0, :]), op=ALU.mult), ["Ss", "w8"], ["Ss"])
            pool(lambda e: e.tensor_tensor(out=b3(big), in0=bv(w8[:, 7, :]), in1=bk(w8[:, 5, :]), op=ALU.mult), ["w8"], ["big"])
            dve(lambda e: e.tensor_tensor(out=Ss[:], in0=Ss[:], in1=big[:], op=ALU.subtract), ["Ss", "big"], ["Ss"])
            pool(lambda e: e.tensor_tensor(out=b3(big), in0=bv(rk3[:, 2, :]), in1=bk(w8[:, 4, :]), op=ALU.mult), ["rk3", "w8"], ["big"])
            dve(lambda e: e.tensor_tensor(out=Ss[:], in0=Ss[:], in1=big[:], op=ALU.add), ["Ss", "big"], ["Ss"])
            dma("pool", osS, Ss[:], ["Ss"], [], "fin")
            pool(lambda e: e.tensor_tensor(out=b3(big), in0=b3(Ss), in1=bk(rk3[:, 0, :]), op=ALU.mult), ["Ss", "rk3"], ["big"])
            dve(lambda e: e.tensor_reduce(out=hs[:, 1, :], in_=b3(big), axis=AX.X, op=ALU.add), ["big"], ["hs"])
            dve(lambda e: e.tensor_tensor(out=w8[:, 6, :], in0=rk3[:, 0, :], in1=w8[:, 4, :], op=ALU.mult), ["rk3", "w8"], ["w8"])
            dve(lambda e: e.tensor_tensor(out=w8[:, 6, :], in0=w8[:, 6, :], in1=SP_("rk"), op=ALU.mult), ["w8", "spk"], ["w8"])
            dve(lambda e: e.tensor_reduce(out=sv[:, 12:13], in_=w8[:, 6, :], axis=AX.X, op=ALU.add), ["w8"], ["sv"])
            dve(lambda e: e.scalar_tensor_tensor(out=hs[:, 1, :], in0=rk3[:, 2, :], scalar=sv[:, 12:13], in1=hs[:, 1, :], op0=ALU.mult, op1=ALU.add), ["rk3", "sv", "hs"], ["hs"])
            act(lambda e: e.activation(out=w8[:, 6, :], in_=A7[:, 3, :], func=AF.Sigmoid), ["A7"], ["w8"])
            dve(lambda e: e.tensor_tensor(out=hs[:, 0, :], in0=hs[:, 0, :], in1=w8[:, 6, :], op=ALU.mult), ["hs", "w8"], ["hs"])
            dve(lambda e: e.tensor_tensor(out=w8[:, 7, :], in0=hs[:, 0, :], in1=hs[:, 0, :], op=ALU.mult), ["hs"], ["w8"])
            dve(lambda e: e.tensor_reduce(out=sv[:, 13:14], in_=w8[:, 7, :], axis=AX.X, op=ALU.add), ["w8"], ["sv"])
            act(lambda e: e.activation(out=sv[:, 13:14], in_=sv[:, 13:14], func=AF.Sqrt, bias=EPS, scale=1.0 / 64), ["sv"], ["sv"])
            dve(lambda e: e.reciprocal(out=sv[:, 13:14], in_=sv[:, 13:14]), ["sv"], ["sv"])
            dve(lambda e: e.scalar_tensor_tensor(out=hs[:, 0, :], in0=hs[:, 0, :], scalar=sv[:, 13:14], in1=SP_("mnw"), op0=ALU.mult, op1=ALU.mult), ["hs", "sv", "spk"], ["hs"])
            dve(lambda e: e.tensor_reduce(out=sv[:, 14:15], in_=hs[:, 1, :], axis=AX.X, op=ALU.add), ["hs"], ["sv"])
            dve(lambda e: e.tensor_scalar_mul(out=sv[:, 14:15], in0=sv[:, 14:15], scalar1=1.0 / 64), ["sv"], ["sv"])
            dve(lambda e: e.tensor_scalar_sub(out=hs[:, 1, :], in0=hs[:, 1, :], scalar1=sv[:, 14:15]), ["hs", "sv"], ["hs"])
            dve(lambda e: e.tensor_tensor(out=w8[:, 7, :], in0=hs[:, 1, :], in1=hs[:, 1, :], op=ALU.mult), ["hs"], ["w8"])
            dve(lambda e: e.tensor_reduce(out=sv[:, 15:16], in_=w8[:, 7, :], axis=AX.X, op=ALU.add), ["w8"], ["sv"])
            act(lambda e: e.activation(out=sv[:, 15:16], in_=sv[:, 15:16], func=AF.Sqrt, bias=GN_EPS, scale=1.0 / 64), ["sv"], ["sv"])
            dve(lambda e: e.reciprocal(out=sv[:, 15:16], in_=sv[:, 15:16]), ["sv"], ["sv"])
            dve(lambda e: e.scalar_tensor_tensor(out=hs[:, 1, :], in0=hs[:, 1, :], scalar=sv[:, 15:16], in1=SP_("lnw"), op0=ALU.mult, op1=ALU.mult), ["hs", "sv", "spk"], ["hs"])
            dve(lambda e: e.tensor_tensor(out=hs[:, 1, :], in0=hs[:, 1, :], in1=SP_("lnb"), op=ALU.add), ["hs", "spk"], ["hs"])
            dve(lambda e: e.tensor_tensor(out=hs[:, 1, :], in0=hs[:, 1, :], in1=L3[:, 2, :], op=ALU.mult), ["hs", "L3"], ["hs"])
            s3v = nc.dram_tensor("scr3b", [128, 2, 64], F32, kind="Internal").ap()
            dma("pool", s3v, hs[:], ["hs"], ["scr3b"], "scrw3")
            smix = spj[:, 2304:3328].rearrange("p (a h d) -> p a h d", a=2, h=8)
            for a_ in range(2):
                dma("sp", smix[:, a_, :, :], s3v.rearrange("(b h) a d -> b a h d", b=NS)[:, a_, :, :], ["scr3b"], ["smix"], "scrr3")
            act(lambda e: e.copy(out=mix[:NS, :], in_=smix[:].rearrange("p a h d -> p (a h d)")), ["smix"], ["mix"])
            out_proj(NS, mix, "mix", sx, "sx", T, W)
        PP[0].finalize()
        PP[0] = Prog(ctx)
    es_res.close()

    with contextlib.ExitStack() as es2:
        cur[0] = es2
        upb = TT("upb", [128, 8, DFF], BF16)
        dnb = TT("dnb", [128, 32, D], BF16)
        pB2 = TT("pB2", [128, 136], F32)
        nfw = TT("nfw", [128, D])
        identb2 = TT("identb2", [128, 128], BF16)
        wout = TT("wout", [128, 8, D], BF16)
        mixin = TT("mixin", [128, D], BF16)
        for c in range(0, 8, 4):
            dma("pool", wout[:, c:c + 4, :], w_out_v[:, c:c + 4, :], [], ["wout"], "wout")
        up_v = mlp_up.rearrange("(c p) n -> p c n", p=128)
        dn_v = mlp_down.rearrange("(c p) n -> p c n", p=128)
        dma("sp", pB2[:, 0:128], packA_d[:, OFF["ident"][0]:OFF["ident"][1]], [], ["pB2"], "init2", True)
        dma("sp", pB2[:, 128:136], packA_d[:, OFF["nmlp"][0]:OFF["nmlp"][1]], [], ["pB2"], "init2", True)
        dma("sp", nfw[:], nfw_d, [], ["nfw"], "init2", True)
        for g8 in range(8):
            dma("pool", upb[:, :, g8 * 512:(g8 + 1) * 512], up_v[:, :, g8 * 512:(g8 + 1) * 512], [], ["upb%d" % g8], "up%d" % g8)
        for g8 in range(8):
            dma("pool", dnb[:, g8 * 4:(g8 + 1) * 4, :], dn_v[:, g8 * 4:(g8 + 1) * 4, :], [], ["dnb%d" % g8], "dn%d" % g8)
        dve(lambda e: e.tensor_copy(out=identb2[:], in_=pB2[:, 0:128]), ["pB2"], ["identb2"])
        NSUB = 2
        NTT = NSUB * 128
        xsb2 = TT("xsb2", [128, D], BF16)
        xb = [TT("xb%d" % i, [128, NSUB, D]) for i in range(2)]
        st2 = TT("st2", [128, 8])
        xn2 = [TT("xn2T%d" % i, [128, 8, NTT], BF16) for i in range(2)]
        hT = TT("hT", [128, 32, NTT], BF16)
        junk2 = TT("junk2", [128, D], BF16)
        rl = [TT("rl%d" % i, [128, 512]) for i in range(2)]
        nmlp = pB2[:, 128:136]
        xm_v = xp.rearrange("(s p) d -> p s d", p=128)
        yp_v = yp.rearrange("(s p) d -> p s d", p=128)
        sbs = [(sb * NSUB, NSUB, 128) for sb in range(NB // NSUB)] + [(NB, 1, NS)]

        def front(i):
            s0, nsub, nt = sbs[i]
            x4 = xb[i % 2]
            xk = "xb%d" % (i % 2)
            xn2T = xn2[i % 2]
            xnk = "xn2T%d" % (i % 2)
            if nsub == NSUB:
                dma("sp", x4[:], xm_v[:, s0:s0 + NSUB, :], [], [xk], xk)
            else:
                dma("sp", x4[:nt, 0, :], xs, [], [xk], xk)
            for si in range(nsub):
                r0_ = (s0 + si) * 128 if nsub == NSUB else T
                dma("sp", mixin[:nt, :], mix_d[r0_:r0_ + nt, :], [], ["mixin"], "mixin")
                ps, psb, pk = nps()
                for c in range(8):
                    pe(lambda e, c=c, psb=psb: e.transpose(psb[:, c * 128:c * 128 + nt], mixin[:nt, c * 128:(c + 1) * 128], identb2[:nt, :nt]), ["mixin", "identb2"], [pk])
                act(lambda e, psb=psb, si=si: e.copy(out=xn2T[:, :, si * 128:si * 128 + nt], in_=psb[:, 0:1024].rearrange("p (c t) -> p c t", c=8)[:, :, 0:nt]), [pk], [xnk])
                yield
                ps, psb, pk = nps()
                for n in range(2):
                    for c in range(8):
                        mm(ps[:nt, n * 512:(n + 1) * 512], xn2T[:, c, si * 128:si * 128 + nt], wout[:, c, n * 512:(n + 1) * 512], c == 0, c == 7, [xnk, "wout"], [pk])
                for a_ in range(2):
                    dve(lambda e, ps=ps, si=si, a_=a_: e.tensor_tensor(out=x4[:nt, si, a_ * 512:(a_ + 1) * 512], in0=ps[:nt, a_ * 512:(a_ + 1) * 512], in1=x4[:nt, si, a_ * 512:(a_ + 1) * 512], op=ALU.add), [pk, xk], [xk])
                act(lambda e, si=si: e.activation(out=junk2[:nt, :], in_=x4[:nt, si, :], func=AF.Square, accum_out=st2[:nt, 0:1]), [xk], ["junk2", "st2"])
                act(lambda e: e.activation(out=st2[:nt, 1:2], in_=st2[:nt, 0:1], func=AF.Sqrt, bias=EPS, scale=1.0 / D), ["st2"], ["st2"])
                dve(lambda e: e.reciprocal(out=st2[:nt, 2:3], in_=st2[:nt, 1:2]), ["st2"], ["st2"])
                dve(lambda e, si=si: e.tensor_scalar_mul(out=xsb2[:nt, :], in0=x4[:nt, si, :], scalar1=st2[:nt, 2:3]), [xk, "st2"], ["xsb2"])
                yield
                yield
                ps, psb, pk = nps()
                for c in range(8):
                    pe(lambda e, c=c, psb=psb: e.transpose(psb[:, c * 128:c * 128 + nt], xsb2[:nt, c * 128:(c + 1) * 128], identb2[:nt, :nt]), ["xsb2", "identb2"], [pk])
                dve(lambda e, psb=psb, si=si: e.tensor_tensor(out=xn2T[:, :, si * 128:si * 128 + nt], in0=psb[:, 0:1024].rearrange("p (c t) -> p c t", c=8)[:, :, 0:nt],
                                                             in1=nmlp.unsqueeze(2).to_broadcast([128, 8, nt]), op=ALU.mult), [pk, "pB2"], [xnk])
                yield

        def up(i):
            s0, nsub, nt = sbs[i]
            ntt = nsub * nt if nsub == NSUB else nt
            xn2T = xn2[i % 2]
            xnk = "xn2T%d" % (i % 2)
            per = 512 // NTT
            for j2 in range(32 // (2 * per)):
                ps, psb, pk = nps()
                for jj in range(2 * per):
                    j = j2 * 2 * per + jj
                    for c in range(8):
                        mm(ps[:, jj * NTT:jj * NTT + ntt], upb[:, c, j * 128:(j + 1) * 128], xn2T[:, c, 0:ntt], c == 0, c == 7, ["upb%d" % (j // 4), xnk], [pk])
                for bk in range(2):
                    r_ = rl[bk]
                    rk_ = "rl%d" % bk
                    j0 = j2 * 2 * per + bk * per
                    psv = ps[:, bk * 512:(bk + 1) * 512].rearrange("p (j t) -> p j t", j=per)[:, :, 0:ntt]
                    rv = r_[:, :].rearrange("p (j t) -> p j t", j=per)[:, :, 0:ntt]
                    act(lambda e, psv=psv, rv=rv: e.activation(out=rv, in_=psv, func=AF.Relu), [pk], [rk_])
                    if bk == 0:
                        dve(lambda e, rv=rv, j0=j0: e.tensor_tensor(out=hT[:, j0:j0 + per, 0:ntt], in0=rv, in1=rv, op=ALU.mult), [rk_], ["hT"])
                    else:
                        pool(lambda e, rv=rv, j0=j0: e.tensor_tensor(out=hT[:, j0:j0 + per, 0:ntt], in0=rv, in1=rv, op=ALU.mult), [rk_], ["hT"])
                yield

        def down(i):
            s0, nsub, nt = sbs[i]
            x4 = xb[i % 2]
            xk = "xb%d" % (i % 2)
            for si in range(nsub):
                ps, psb, pk = nps()
                for n in range(2):
                    for j in range(32):
                        mm(ps[:nt, n * 512:(n + 1) * 512], hT[:, j, si * 128:si * 128 + nt], dnb[:, j, n * 512:(n + 1) * 512], j == 0, j == 31, ["hT", "dnb%d" % (j // 4)], [pk])
                for a_ in range(2):
                    dve(lambda e, ps=ps, si=si, a_=a_: e.tensor_tensor(out=x4[:nt, si, a_ * 512:(a_ + 1) * 512], in0=ps[:nt, a_ * 512:(a_ + 1) * 512], in1=x4[:nt, si, a_ * 512:(a_ + 1) * 512], op=ALU.add), [pk, xk], [xk])
                act(lambda e, si=si: e.activation(out=junk2[:nt, :], in_=x4[:nt, si, :], func=AF.Square, accum_out=st2[:nt, 4:5]), [xk], ["junk2", "st2"])
                act(lambda e: e.activation(out=st2[:nt, 5:6], in_=st2[:nt, 4:5], func=AF.Sqrt, bias=EPS, scale=1.0 / D), ["st2"], ["st2"])
                dve(lambda e: e.reciprocal(out=st2[:nt, 6:7], in_=st2[:nt, 5:6]), ["st2"], ["st2"])
                dve(lambda e, si=si: e.scalar_tensor_tensor(out=x4[:nt, si, :], in0=x4[:nt, si, :], scalar=st2[:nt, 6:7], in1=nfw[:nt, :], op0=ALU.mult, op1=ALU.mult), [xk, "st2", "nfw"], [xk])
            if nsub == NSUB:
                dma("pool", yp_v[:, s0:s0 + NSUB, :], x4[:], [xk], [], "yo%d" % (i % 2))
            else:
                dma("pool", ys, x4[:nt, 0, :], [xk], [], "yo%d" % (i % 2))

        def run2(gl):
            gl = list(gl)
            while gl:
                for g_ in list(gl):
                    try:
                        next(g_)
                    except StopIteration:
                        gl.remove(g_)

        run2([front(0)])
        for i in range(len(sbs)):
            gl = [up(i)]
            if i + 1 < len(sbs):
                gl.append(front(i + 1))
            run2(gl)
            down(i)
        PP[0].finalize()
    es_ps.close()
    ctx.close()
    return nc


_CACHE = {}


def _host_packs(inp, core):
    f = np.float32
    L = 0
    pa = np.zeros((128, NA), f)

    def put(n, arr):
        a, b = OFF[n]
        pa[:, a:b] = arr

    rep = lambda v: np.broadcast_to(np.asarray(v, f).reshape(1, -1), (128, np.asarray(v).size))
    put("mnw", rep(inp["mlstm_norm_w"][L]))
    put("w0", rep(inp["rw_w0"][L]))
    put("a0", rep(inp["rw_a0"][L]))
    put("kk", rep(inp["rw_k_k"][L]))
    put("ka", rep(inp["rw_k_a"][L]))
    put("rk", rep(inp["rw_r_k"][L].reshape(-1)))
    put("lnw", rep(inp["rw_ln_w"][L]))
    put("lnb", rep(inp["rw_ln_b"][L]))
    put("ifb", rep(np.concatenate([inp["mlstm_i_b"][L], inp["mlstm_f_b"][L]])))
    put("nmw", inp["norm_mix_w"][L].reshape(8, 128).T)
    put("nmlp", inp["norm_mlp_w"][L].reshape(8, 128).T)
    cw = inp["mlstm_conv_w"][L]
    put("cw", cw.reshape(4, 8, 128).transpose(2, 1, 0).reshape(128, 32))
    put("cb", inp["mlstm_conv_b"][L].reshape(8, 128).T)
    put("ident", np.eye(128, dtype=f))
    put("mui", np.triu(np.ones((128, 128), f), 0))
    put("mus", np.triu(np.ones((128, 128), f), 1))
    put("mls", np.tril(np.ones((128, 128), f), -1))
    put("ones", np.ones((128, 128), f))
    lsel = np.zeros((128, 128), f)
    rsel = np.zeros((128, 4), f)
    for h in range(8):
        lsel[h, (h % 2) * 64:(h % 2) * 64 + 64] = 1.0
        rsel[h, h // 2] = 1.0
    put("lsel", lsel)
    put("rsel", rsel)
    return pa


def _sample_pack(inp):
    f = np.float32
    L = 0
    sp = np.zeros((128, NSP), f)

    def bh(v512):
        return np.tile(np.asarray(v512, f).reshape(8, 64), (NS, 1))

    def put(n, arr):
        a, b = SOFF[n]
        sp[:, a:b] = arr

    mu = inp["rw_mu"][L]
    put("mu_r", bh(mu[0:512]))
    put("mu_k", bh(mu[512:1024]))
    put("mu_v", bh(mu[1024:1536]))
    cw = inp["mlstm_conv_w"][L]
    put("cwq", np.concatenate([bh(cw[j, 0:512]) for j in range(4)], axis=1))
    put("cwk", np.concatenate([bh(cw[j, 512:1024]) for j in range(4)], axis=1))
    cb = inp["mlstm_conv_b"][L]
    put("cbq", bh(cb[0:512]))
    put("cbk", bh(cb[512:1024]))
    put("mnw", bh(inp["mlstm_norm_w"][L]))
    put("w0", bh(inp["rw_w0"][L]))
    put("a0", bh(inp["rw_a0"][L]))
    put("kk", bh(inp["rw_k_k"][L]))
    put("ka", bh(inp["rw_k_a"][L]))
    put("rk", bh(inp["rw_r_k"][L].reshape(-1)))
    put("lnw", bh(inp["rw_ln_w"][L]))
    put("lnb", bh(inp["rw_ln_b"][L]))
    put("ib", np.tile(inp["mlstm_i_b"][L].reshape(8, 1), (NS, 1)))
    put("fb", np.tile(inp["mlstm_f_b"][L].reshape(8, 1), (NS, 1)))
    return sp


def kernel(**inp):
    f = np.float32
    inp = {k: np.asarray(v) for k, v in inp.items()}
    if "nc" not in _CACHE:
        _CACHE["nc"] = build_program()
    nc = _CACHE["nc"]
    L = 0
    pa = _host_packs(inp, 0)
    sp = _sample_pack(inp)
    mu = inp["rw_mu"][L]
    luw = np.concatenate([inp["rw_w_up"][L], inp["rw_a_up"][L]], axis=0).astype(f)
    common = {
        "w_in": np.ascontiguousarray(inp["w_in"][L], f),
        "w_out": np.ascontiguousarray(inp["w_out"][L], f),
        "mlp_up": np.ascontiguousarray(inp["mlp_up"][L], f),
        "mlp_down": np.ascontiguousarray(inp["mlp_down"][L], f),
        "packA": pa,
        "mu_b": np.ascontiguousarray(np.broadcast_to(mu.reshape(1, -1), (128, RWW)), f),
        "nfw_b": np.ascontiguousarray(np.broadcast_to(inp["norm_f_w"].reshape(1, -1), (128, D)), f),
        "luw": luw,
        "gup": np.ascontiguousarray(inp["rw_g_up"][L], f),
        "spack": sp,
        "mul": np.ascontiguousarray(np.broadcast_to(mu[1536:1792].reshape(1, -1), (NS, 256)), f),
    }
    in_maps = []
    for c in range(8):
        rs = slice(c * NS, (c + 1) * NS)
        m = dict(common)
        m["xp"] = np.ascontiguousarray(inp["x_prompt"][c], f)
        m["xs"] = np.ascontiguousarray(inp["x_sample"][rs, 0, :], f)
        m["sC"] = np.ascontiguousarray(inp["state_mlstm_C"][L, rs].reshape(128, 4096), f)
        m["sn"] = np.ascontiguousarray(inp["state_mlstm_n"][L, rs].reshape(128, 64), f)
        m["sm"] = np.ascontiguousarray(inp["state_mlstm_m"][L, rs].reshape(128, 1), f)
        cv = inp["state_mlstm_conv"][L, rs]
        m["sconv"] = np.ascontiguousarray(cv.reshape(NS, 3, 2, 8, 64).transpose(0, 3, 2, 1, 4).reshape(128, 2, 3, 64), f)
        m["sS"] = np.ascontiguousarray(inp["state_rwkv_S"][L, rs].reshape(128, 4096), f)
        sh = inp["state_rwkv_shift"][L, rs, 0, :]
        m["sshift"] = np.ascontiguousarray(sh[:, 0:1536].reshape(NS, 3, 8, 64).transpose(0, 2, 1, 3).reshape(128, 3, 64), f)
        m["sshl"] = np.ascontiguousarray(sh[:, 1536:1792], f)
        in_maps.append(m)
    res = run_bass_kernel_spmd(nc, in_maps, core_ids=list(range(8)))
    R = res.results
    y_prompt = np.stack([R[c]["yp"] for c in range(8)]).astype(f)
    y_sample = np.concatenate([R[c]["ys"] for c in range(8)], axis=0).reshape(128, 1, D).astype(f)
    pC = np.zeros((1, 8, 8, 64, 64), f)
    pn = np.zeros((1, 8, 8, 64), f)
    pm = np.zeros((1, 8, 8), f)
    pconv = np.zeros((1, 8, 3, 1024), f)
    pS = np.zeros((1, 8, 8, 64, 64), f)
    pshift = np.zeros((1, 8, 1, RWW), f)
    for c in range(8):
        oC = R[c]["oC"].reshape(2, 64, 4, 65)
        Ch = oC.transpose(2, 0, 1, 3).reshape(8, 64, 65)
        pC[0, c] = Ch[:, :, 0:64]
        pn[0, c] = Ch[:, :, 64]
        pm[0, c] = R[c]["om"].reshape(8)
        pconv[0, c] = R[c]["oconv"].transpose(2, 1, 0).reshape(3, 1024)
        oS = R[c]["oS"].reshape(2, 64, 4, 64)
        pS[0, c] = oS.transpose(2, 0, 3, 1).reshape(8, 64, 64)
        pshift[0, c, 0] = R[c]["oshift"].reshape(RWW)
    sC = np.concatenate([R[c]["osC"].reshape(NS, 8, 64, 64) for c in range(8)])[None].astype(f)
    sn = np.concatenate([R[c]["osn"].reshape(NS, 8, 64) for c in range(8)])[None].astype(f)
    sm = np.concatenate([R[c]["osm"].reshape(NS, 8) for c in range(8)])[None].astype(f)
    sconv = np.concatenate([R[c]["osconv"].reshape(NS, 8, 2, 3, 64).transpose(0, 3, 2, 1, 4).reshape(NS, 3, 1024) for c in range(8)])[None].astype(f)
    sS = np.concatenate([R[c]["osS"].reshape(NS, 8, 64, 64) for c in range(8)])[None].astype(f)
    sshift = np.concatenate([
        np.concatenate([R[c]["osshift"].reshape(NS, 8, 3, 64).transpose(0, 2, 1, 3).reshape(NS, 1536), R[c]["osshl"]], axis=1)
        for c in range(8)]).reshape(1, 128, 1, RWW).astype(f)
    return (y_prompt, y_sample, pC, pn, pm, pconv, pS, pshift, sC, sn, sm, sconv, sS, sshift)
```

```python
import contextlib
import numpy as np
import concourse.bass as bass
import concourse.mybir as mybir
from concourse.bass_utils import run_bass_kernel_spmd

F32 = mybir.dt.float32
BF16 = mybir.dt.bfloat16
AF = mybir.ActivationFunctionType
ALU = mybir.AluOpType
AX = mybir.AxisListType

D = 1024
T = 2048
NB = 16
NS = 16
INW = 3856
MLW = 2064
RWW = 1792
DFF = 4096
EPS = 1e-6
GN_EPS = 64e-5
C0 = 0.6065306597126334

OFF = {}
_o = 0
for _n, _w in [("mnw", 512), ("w0", 512), ("a0", 512), ("kk", 512), ("ka", 512), ("rk", 512),
               ("lnw", 512), ("lnb", 512), ("ifb", 16), ("nmw", 8), ("nmlp", 8), ("cw", 32), ("cb", 8),
               ("ident", 128), ("mui", 128), ("mus", 128), ("mls", 128), ("ones", 128),
               ("lsel", 128), ("rsel", 4)]:
    OFF[_n] = (_o, _o + _w)
    _o += _w
NA = _o
SOFF = {}
_o = 0
for _n, _w in [("mu_r", 64), ("mu_k", 64), ("mu_v", 64), ("cwq", 256), ("cwk", 256), ("cbq", 64), ("cbk", 64),
               ("mnw", 64), ("w0", 64), ("a0", 64), ("kk", 64), ("ka", 64), ("rk", 64), ("lnw", 64), ("lnb", 64),
               ("ib", 1), ("fb", 1)]:
    SOFF[_n] = (_o, _o + _w)
    _o += _w
NSP = _o


ALIAS = {"r_sb0": "G0", "kf_sb0": "G1", "vf0": "G2", "osig0": "G13", "r_sb1": "G20", "kf_sb1": "G21", "vf1": "G22", "osig1": "G23",
         "wsig": "G3", "a_sb": "G4", "g_sb0": "G5", "g_sb1": "G24", "kap": "G6", "ktl": "G7",
         "bvec": "G8", "e1": "G9", "e2": "G10", "e3": "G11", "pcw": "G12", "ysb": "G11", "tA": "G9", "tB": "G10", "plast": "G16",
         "hml": "G14", "nlrep": "G15", "Fb": "G15", "cacc": "G16", "qks": "G16", "tAm": "G18", "tBm": "G19",
         "junk": "xm", "ctmp": "xm", "mixT": "TMbA", "TMb0": "TMbA", "TMb1": "TMbA",
         "TMb2": "TMbB", "TMb3": "TMbB", "Ub": "Zb1", "Ss": "Cs", "lo3": "spj", "smix": "spj",
         "xt1": "xt0", "xnT1": "xnT0", "qkx1": "qkx0"}
PARKEYS = {"g_sb", "r_sb", "kf_sb", "vf", "osig", "vaug", "gt", "qpb", "kTb", "ktm", "vrw", "lor"}
CURP = [0]


class SemCtx:
    def __init__(self, nc):
        self.nc = nc
        self.es = contextlib.ExitStack()
        self.engs = ["pe", "act", "dve", "pool", "sp"]
        self.esem = {e: self.es.enter_context(nc.semaphore("s_" + e)) for e in self.engs}
        self.ecnt = {e: 0 for e in self.engs}
        self.bsem = self.es.enter_context(nc.semaphore("s_bar"))
        self.phase = 0
        self.gsem = {}
        self.gbase = {}

    def group_sem(self, g):
        if g not in self.gsem:
            self.gsem[g] = self.es.enter_context(self.nc.semaphore("g_%d" % len(self.gsem)))
            self.gbase[g] = 0
        return self.gsem[g]

    def close(self):
        self.es.close()


class Prog:
    max_ops = None

    def __init__(self, ctx):
        self.ctx = ctx
        self.nc = ctx.nc
        self.ops = []
        self.last_writer = {}
        self.readers = {}
        self.dma_groups = {}

    def op(self, eng, fn, reads=(), writes=(), dma_group=None, wait_total=False):
        if self.max_ops is not None and len(self.ops) >= self.max_ops:
            return None
        reads = [(k + str(CURP[0])) if k in PARKEYS else k for k in reads]
        writes = [(k + str(CURP[0])) if k in PARKEYS else k for k in writes]
        reads = [ALIAS.get(k, k) for k in reads]
        writes = [ALIAS.get(k, k) for k in writes]
        if eng != "pe":
            writes = writes + [k for k in reads if k.startswith("PS") and k not in writes]
        deps = set()
        raw = set()
        for b in reads:
            if b in self.last_writer:
                deps.add(self.last_writer[b])
                raw.add(self.last_writer[b])
        for b in writes:
            if b in self.last_writer:
                deps.add(self.last_writer[b])
            for r in self.readers.get(b, ()):
                deps.add(r)
        idx = len(self.ops)
        if dma_group is not None:
            deps = {d for d in deps if self.ops[d]["dma"] != dma_group}
        o = dict(eng=eng, fn=fn, deps=sorted(deps), dma=dma_group, idx=idx, raw=raw)
        if dma_group is not None:
            g = self.dma_groups.setdefault(dma_group, dict(total=0, wait_total=wait_total))
            g["total"] += 1
            o["dma_cnt"] = g["total"]
        self.ops.append(o)
        for b in reads:
            self.readers.setdefault(b, []).append(idx)
        for b in writes:
            self.last_writer[b] = idx
            self.readers[b] = []
        return idx

    def finalize(self):
        nc = self.nc
        ctx = self.ctx
        ops = self.ops
        needed = set()
        for o in ops:
            best = {}
            bestg = {}
            rd = []
            for d in o["deps"]:
                p = ops[d]
                if p["dma"] is not None:
                    bestg[p["dma"]] = max(bestg.get(p["dma"], -1), d)
                else:
                    if p["eng"] == "pe" and o["eng"] == "pe" and o["dma"] is None:
                        continue
                    if p["eng"] == o["eng"] and o["dma"] is None and p["eng"] in ("act", "dve") and d not in o["raw"]:
                        continue
                    best[p["eng"]] = max(best.get(p["eng"], -1), d)
            rd.extend(best.values())
            rd.extend(bestg.values())
            o["deps"] = sorted(rd)
            for d in best.values():
                needed.add(d)
        engs = ctx.engs
        last = {}
        for o in ops:
            if o["dma"] is None:
                last[o["eng"]] = o["idx"]
        needed |= set(last.values())
        cnt = dict(ctx.ecnt)
        for o in ops:
            if o["dma"] is None and o["idx"] in needed:
                cnt[o["eng"]] += 1
                o["sig"] = cnt[o["eng"]]
        for g in self.dma_groups:
            ctx.group_sem(g)
        phase = ctx.phase
        with nc.Block() as block:

            def emit_engine(ename, eng):
                known = {}
                if phase > 0:
                    eng.wait_ge(ctx.bsem, phase)
                for o in ops:
                    if o["eng"] != ename:
                        continue
                    for d in o["deps"]:
                        p = ops[d]
                        if p["dma"] is not None:
                            g = self.dma_groups[p["dma"]]
                            sem = ctx.gsem[p["dma"]]
                            val = ctx.gbase[p["dma"]] + 16 * (g["total"] if g["wait_total"] else p["dma_cnt"])
                            key = ("g", p["dma"])
                        else:
                            if p["eng"] == "pe" and ename == "pe" and o["dma"] is None:
                                continue
                            sem = ctx.esem[p["eng"]]
                            val = p["sig"]
                            key = ("e", p["eng"])
                        if known.get(key, 0) >= val:
                            continue
                        known[key] = val
                        eng.wait_ge(sem, val)
                    ins = o["fn"](eng)
                    if o["dma"] is not None:
                        ins.then_inc(ctx.gsem[o["dma"]], 16)
                    elif "sig" in o:
                        ins.then_inc(ctx.esem[ename], 1)
                if ename == "sp":
                    for e2 in engs:
                        if cnt[e2] > ctx.ecnt[e2]:
                            eng.wait_ge(ctx.esem[e2], cnt[e2])
                    for g, info in self.dma_groups.items():
                        eng.wait_ge(ctx.gsem[g], ctx.gbase[g] + 16 * info["total"])
                    eng.sem_inc(ctx.bsem, 1)

            @block.tensor
            def _(e):
                emit_engine("pe", e)

            @block.scalar
            def _(e):
                emit_engine("act", e)

            @block.vector
            def _(e):
                emit_engine("dve", e)

            @block.gpsimd
            def _(e):
                emit_engine("pool", e)

            @block.sync
            def _(e):
                emit_engine("sp", e)

        ctx.ecnt = cnt
        for g, info in self.dma_groups.items():
            ctx.gbase[g] += 16 * info["total"]
        ctx.phase += 1


STOP_EARLY = True


class _StopBuild(Exception):
    pass


def build_program(do_sample=True, debug=False):
    nc = bass.Bass("TRN2", target_bir_lowering=False)
    try:
        return _build_program(nc, do_sample, debug)
    except _StopBuild:
        return nc


def _build_program(nc, do_sample, debug):
    dbg = nc.dram_tensor("dbg", [128, 16, 512], F32, kind="ExternalOutput").ap() if debug else None
    din = lambda n, s: nc.dram_tensor(n, s, F32, kind="ExternalInput").ap()
    dout = lambda n, s: nc.dram_tensor(n, s, F32, kind="ExternalOutput").ap()
    xp = din("xp", [T, D])
    xs = din("xs", [NS, D])
    w_in = din("w_in", [D, INW])
    w_out = din("w_out", [D, D])
    mlp_up = din("mlp_up", [D, DFF])
    mlp_down = din("mlp_down", [DFF, D])
    packA_d = din("packA", [128, NA])
    mu_d = din("mu_b", [128, RWW])
    nfw_d = din("nfw_b", [128, D])
    wup_d = din("luw", [128, 512])
    gup_d = din("gup", [128, 512])
    spk_d = din("spack", [128, NSP])
    sC_d = din("sC", [128, 4096])
    sn_d = din("sn", [128, 64])
    sm_d = din("sm", [128, 1])
    sconv_d = din("sconv", [128, 2, 3, 64])
    sS_d = din("sS", [128, 4096])
    sshift_d = din("sshift", [128, 3, 64])
    sshl_d = din("sshl", [NS, 256])
    mul_d = din("mul", [NS, 256])

    yp = dout("yp", [T, D])
    ys = dout("ys", [NS, D])
    oC = dout("oC", [128, 4, 65])
    om = dout("om", [8, 1])
    oconv = dout("oconv", [128, 8, 3])
    oS = dout("oS", [128, 4, 64])
    oshift = dout("oshift", [1, RWW])
    osC = dout("osC", [128, 4096])
    osn = dout("osn", [128, 64])
    osm = dout("osm", [128, 1])
    osconv = dout("osconv", [128, 2, 3, 64])
    osS = dout("osS", [128, 4096])
    osshift = dout("osshift", [128, 3, 64])
    osshl = dout("osshl", [NS, 256])

    mix_d = nc.dram_tensor("mix_scr", [T + NS, D], BF16, kind="Internal").ap()
    scr1 = nc.dram_tensor("scr1", [128, 8, 64], F32, kind="Internal").ap()
    scr2 = nc.dram_tensor("scr2", [128, 3, 64], F32, kind="Internal").ap()
    scr3 = nc.dram_tensor("scr3", [NS, 2, 8, 64], F32, kind="Internal").ap()

    ctx = SemCtx(nc)
    PP = [Prog(ctx)]
    es_res = contextlib.ExitStack()
    cur = [es_res]

    def TT(name, shape, dt=F32):
        return cur[0].enter_context(nc.sbuf_tensor("t_" + name, list(shape), dt))

    def dma(q, out, in_, reads, writes, group, wait_total=False):
        group = "%s@%s" % (group, q)
        PP[0].op(q, lambda e: e.dma_start(out=out, in_=in_), reads=reads, writes=writes, dma_group=group, wait_total=wait_total)

    def dve(fn, r, w):
        PP[0].op("dve", fn, reads=r, writes=w)

    def act(fn, r, w):
        PP[0].op("act", fn, reads=r, writes=w)

    def pool(fn, r, w):
        PP[0].op("pool", fn, reads=r, writes=w)

    def pe(fn, r, w):
        PP[0].op("pe", fn, reads=r, writes=w)

    def mm(out, lhsT, rhs, start, stop, r, w):
        pe(lambda e: e.matmul(out, lhsT=lhsT, rhs=rhs, start=start, stop=stop), r, w)

    es_ps = contextlib.ExitStack()
    PS = [es_ps.enter_context(nc.psum_tensor("PS%d" % i, [128, 1024], F32)) for i in range(4)]
    PSB = [p.bitcast(BF16) for p in PS]
    psi = {"A": 0, "F": 0, "B": 0}
    POOLS = {"A": [0, 1, 2, 3], "F": [0, 1], "B": [2, 3]}
    CURPOOL = ["A"]

    def nps():
        pl = CURPOOL[0]
        lst = POOLS[pl]
        i = lst[psi[pl] % len(lst)]
        psi[pl] += 1
        return PS[i], PSB[i], "PS%d" % i

    wq = TT("wq", [128, 8, INW], BF16)
    mub = TT("mub", [128, RWW], F32)
    luw = TT("luw", [128, 512], BF16)
    gup = TT("gup", [128, 512], BF16)
    pA = TT("pA", [128, NA], F32)
    identb = TT("identb", [128, 128], BF16)

    def PA(n):
        a, b = OFF[n]
        return pA[:, a:b]

    w_in_v = w_in.rearrange("(c p) n -> p c n", p=128)
    dma("sp", pA[:], packA_d, [], ["pA"], "init", True)
    dma("pool", luw[:], wup_d, [], ["luw"], "init", True)
    dma("pool", gup[:], gup_d, [], ["gup"], "init", True)
    w_out_v = w_out.rearrange("(c p) n -> p c n", p=128)
    dve(lambda e: e.tensor_copy(out=identb[:], in_=PA("ident")), ["pA"], ["identb"])


    def _dbgdump(tag):
        if debug != tag:
            return
        dstg_ = cur[0].enter_context(nc.sbuf_tensor("t_dbgst%d" % tag, [128, 512], F32))
        def dd(slot, ap, key, n):
            dve(lambda e: e.tensor_copy(out=dstg_[:, 0:n], in_=ap), [key], ["dbgst"])
            dma("sp", dbg[:, slot, 0:n], dstg_[:, 0:n], ["dbgst"], [], "dbg")
        dd(0, PA("ident"), "pA", 128)
        dd(1, PA("mui"), "pA", 128)
        dd(2, PA("mnw"), "pA", 512)
        dd(3, PA("w0"), "pA", 512)
        dd(4, PA("lnb"), "pA", 512)
        PP[0].max_ops = len(PP[0].ops)
        PP[0].finalize()
        raise _StopBuild()
    _dbgdump(3)
    dma("sp", mub[:], mu_d, [], ["mub"], "init", True)
    PP[0].finalize()
    PP[0] = Prog(ctx)

    def rmsnorm_T(xt, xk, nt, dstT, dstk, col0, wname, tmpb, tmpbk, junk, junkk, st, stk):
        act(lambda e: e.activation(out=junk[:nt, :], in_=xt[:nt, :], func=AF.Square, accum_out=st[:nt, 0:1]), [xk], [junkk, stk])
        act(lambda e: e.activation(out=st[:nt, 1:2], in_=st[:nt, 0:1], func=AF.Ln, bias=EPS, scale=1.0 / D), [stk], [stk])
        act(lambda e: e.activation(out=st[:nt, 2:3], in_=st[:nt, 1:2], func=AF.Exp, scale=-0.5), [stk], [stk])
        dve(lambda e: e.tensor_scalar_mul(out=tmpb[:nt, :], in0=xt[:nt, :], scalar1=st[:nt, 2:3]), [xk, stk], [tmpbk])
        ps, psb, pk = nps()
        for c in range(8):
            pe(lambda e, c=c: e.transpose(psb[:, c * 128:c * 128 + nt], tmpb[:nt, c * 128:(c + 1) * 128], identb[:nt, :nt]), [tmpbk, "identb"], [pk])
        a, b_ = OFF[wname]
        dve(lambda e: e.tensor_tensor(out=dstT[:, :, col0:col0 + nt],
                                      in0=psb[:, 0:1024].rearrange("p (c t) -> p c t", c=8)[:, :, 0:nt],
                                      in1=pA[:, a:b_].unsqueeze(2).to_broadcast([128, 8, nt]), op=ALU.mult), [pk, "pA"], [dstk])

    def head_ml(nt, hsrc, hk, osig, ok, mix, mixk, W, sfx=""):
        tA, tB, s8 = W["tA"], W["tB"], W["s8"]
        h3 = lambda t: t[:nt, :].rearrange("p (h d) -> p h d", h=8)
        bc = lambda t, c: t[:nt, c:c + 8].unsqueeze(2).to_broadcast([nt, 8, 64])
        dve(lambda e: e.tensor_tensor(out=tA[:nt, :], in0=hsrc[:nt, :], in1=osig[:nt, :], op=ALU.mult), [hk, ok], ["tA" + sfx])
        dve(lambda e: e.tensor_tensor(out=tB[:nt, :], in0=tA[:nt, :], in1=tA[:nt, :], op=ALU.mult), ["tA" + sfx], ["tB" + sfx])
        dve(lambda e: e.tensor_reduce(out=s8[:nt, 0:8], in_=h3(tB), axis=AX.X, op=ALU.add), ["tB" + sfx], ["s8" + sfx])
        act(lambda e: e.activation(out=s8[:nt, 8:16], in_=s8[:nt, 0:8], func=AF.Ln, bias=EPS, scale=1.0 / 64), ["s8" + sfx], ["s8" + sfx])
        act(lambda e: e.activation(out=s8[:nt, 16:24], in_=s8[:nt, 8:16], func=AF.Exp, scale=-0.5), ["s8" + sfx], ["s8" + sfx])
        dve(lambda e: e.tensor_tensor(out=h3(tB), in0=h3(tA), in1=bc(s8, 16), op=ALU.mult), ["tA" + sfx, "s8" + sfx], ["tB" + sfx])
        dve(lambda e: e.tensor_tensor(out=mix[:nt, 0:512], in0=tB[:nt, :], in1=PA("mnw")[:nt, :], op=ALU.mult), ["tB" + sfx, "pA"], [mixk])

    def head_rw(nt, ysrc, yk, bon, bonk, vf, vfk, g, gk, mix, mixk, W):
        tA, tB, s8 = W["tA"], W["tB"], W["s8"]
        h3 = lambda t: t[:nt, :].rearrange("p (h d) -> p h d", h=8)
        bc = lambda t, c: t[:nt, c:c + 8].unsqueeze(2).to_broadcast([nt, 8, 64])
        pool(lambda e: e.tensor_tensor(out=h3(tA), in0=h3(vf), in1=bc(bon, 0), op=ALU.mult), [vfk, bonk], ["tAm"])
        pool(lambda e: e.tensor_tensor(out=tA[:nt, :], in0=tA[:nt, :], in1=ysrc[:nt, :], op=ALU.add), ["tAm", yk], ["tAm"])
        dve(lambda e: e.tensor_reduce(out=s8[:nt, 24:32], in_=h3(tA), axis=AX.X, op=ALU.add), ["tAm"], ["s8"])
        pool(lambda e: e.tensor_scalar_mul(out=s8[:nt, 24:32], in0=s8[:nt, 24:32], scalar1=1.0 / 64), ["s8"], ["s8"])
        pool(lambda e: e.tensor_tensor(out=h3(tA), in0=h3(tA), in1=bc(s8, 24), op=ALU.subtract), ["tAm", "s8"], ["tAm"])
        pool(lambda e: e.tensor_tensor(out=tB[:nt, :], in0=tA[:nt, :], in1=tA[:nt, :], op=ALU.mult), ["tAm"], ["tBm"])
        dve(lambda e: e.tensor_reduce(out=s8[:nt, 32:40], in_=h3(tB), axis=AX.X, op=ALU.add), ["tBm"], ["s8"])
        act(lambda e: e.activation(out=s8[:nt, 40:48], in_=s8[:nt, 32:40], func=AF.Ln, bias=GN_EPS, scale=1.0 / 64), ["s8"], ["s8"])
        act(lambda e: e.activation(out=s8[:nt, 48:56], in_=s8[:nt, 40:48], func=AF.Exp, scale=-0.5), ["s8"], ["s8"])
        pool(lambda e: e.tensor_tensor(out=h3(tB), in0=h3(tA), in1=bc(s8, 48), op=ALU.mult), ["tAm", "s8"], ["tBm"])
        pool(lambda e: e.tensor_tensor(out=tB[:nt, :], in0=tB[:nt, :], in1=PA("lnw")[:nt, :], op=ALU.mult), ["tBm", "pA"], ["tBm"])
        pool(lambda e: e.tensor_tensor(out=tB[:nt, :], in0=tB[:nt, :], in1=PA("lnb")[:nt, :], op=ALU.add), ["tBm", "pA"], ["tBm"])
        pool(lambda e: e.tensor_tensor(out=mix[:nt, 512:1024], in0=tB[:nt, :], in1=g[:nt, :], op=ALU.mult), ["tBm", gk], [mixk])

    def out_proj(nt, mix, mixk, xt, xk, row0, W):
        dma("pool", mix_d[row0:row0 + nt, :], mix[:nt, :], [mixk], [], "mixst")

    with contextlib.ExitStack() as es1:
        cur[0] = es1
        W = {}
        Gbig = TT("Gbig", [128, 25, 512])
        G = [Gbig[:, i, :] for i in range(25)]
        W["s8"] = TT("s8", [128, 64])
        W["xm"] = TT("xm", [128, D])
        xt = [TT("xt0", [128, D])] * 2
        junk = W["xm"]
        st = TT("st", [128, 4])
        mix = TT("mix", [128, D], BF16)
        xsb = TT("xsb", [128, D], BF16)
        xnT = [TT("xnT0", [128, 8, 129], BF16)] * 2
        dxT = TT("dxT", [128, 8, 128], BF16)
        qkx = [TT("qkx0", [128, 8, 131])] * 2
        cacc = Gbig[:, 16:18, :].rearrange("p a (c t) -> p (a c) t", t=128)
        ctmp = W["xm"][:, :].rearrange("p (c t) -> p c t", c=8)
        qks = cacc
        QPB = [TT("qpb%d" % i, [128, 4, 128], BF16) for i in range(2)]
        KTB = [TT("kTb%d" % i, [128, 4, 128], BF16) for i in range(2)]
        KTM = [TT("ktm%d" % i, [128, 8, 64], BF16) for i in range(2)]
        VAUG = [TT("vaug%d" % i, [128, 8, 65], BF16) for i in range(2)]
        GT = [TT("gt%d" % i, [128, 96]) for i in range(2)]
        runmax = TT("runmax", [128, 8])
        nBc = TT("nBc", [128, 8])
        Cst = TT("Cst", [128, 4, 65])
        Cbf = TT("Cbf", [128, 4, 65], BF16)
        Fb = G[15].rearrange("p (j t) -> p j t", j=4)
        hml = G[14]
        nlrep = G[15].rearrange("p (h d) -> p h d", h=8)
        wsig, a_sb, g_sb, kap, ktl, bvec, e1, e2, e3, pcw, ysb = G[3], G[4], G[5], G[6], G[7], G[8], G[9], G[10], G[11], G[12], G[11]
        W["tA"], W["tB"] = G[9], G[10]
        WM = {"tA": G[18], "tB": G[19], "s8": TT("s8m", [128, 64])}
        r8b = TT("r8b", [128, 8])
        VRW = [TT("vrw%d" % i, [128, 8, 64], BF16) for i in range(2)]
        LOR = [TT("lor%d" % i, [128, 256], BF16) for i in range(2)]
        lorT = TT("lorT", [128, 2, 128], BF16)
        r8 = TT("r8", [128, 32])
        bon = TT("bon", [128, 8])
        TMb = TT("TMb", [128, 4, 512], BF16)
        W["mixT"] = TMb[:, 0:2, :].rearrange("p a (c t) -> p (a c) t", t=128)
        Btz = TT("Btz", [128, 8, 128], BF16)
        Ktz = TT("Ktz", [128, 8, 128], BF16)
        FMt = TT("FMt", [128, 4, 4, 128], BF16)
        Am = [TT("Am%d" % i, [128, 8, 128], BF16) for i in range(3)]
        PTb = TT("PTb", [128, 8, 128], BF16)
        Pw = [TT("Pw%d" % i, [128, 8, 128], BF16) for i in range(4)]
        Zb = [TT("Zb%d" % i, [128, 8, 64], BF16) for i in range(2)]
        Ub = Zb[1]
        Sst = TT("Sst", [128, 4, 64])
        Sbf = TT("Sbf", [128, 4, 64], BF16)
        WLfm = TT("WLfm", [128, 4])
        plast = G[17]

        def wqk(c0, c1):
            return ["wq%d" % g for g in range(c0 // 512, (c1 - 1) // 512 + 1)]

        for g in range(8):
            c0, c1 = g * 512, min(INW, (g + 1) * 512)
            dma("pool", wq[:, :, c0:c1], w_in_v[:, :, c0:c1], [], ["wq%d" % g], "wq%d" % g)
        for p_ in range(2):
            pool(lambda e, p_=p_: e.memset(VAUG[p_][:], 1.0), [], ["vaug%d" % p_])
        pool(lambda e: e.memset(Btz[:], 0.0), [], ["Btz"])
        pool(lambda e: e.memset(Ktz[:], 0.0), [], ["Ktz"])
        pool(lambda e: e.memset(Cst[:], 0.0), [], ["Cst"])
        pool(lambda e: e.memset(Cbf[:], 0.0), [], ["Cbf"])
        pool(lambda e: e.memset(Sst[:], 0.0), [], ["Sst"])
        pool(lambda e: e.memset(Sbf[:], 0.0), [], ["Sbf"])
        pool(lambda e: e.memset(runmax[:], -1e30), [], ["runmax"])
        pool(lambda e: e.memset(nBc[:], 0.0), [], ["nBc"])
        pool(lambda e: e.memset(xnT[0][:, :, 0:1], 0.0), [], ["xnT0"])
        pool(lambda e: e.memset(qkx[0][:, :, 0:3], 0.0), [], ["qkx0"])

        if debug == 2:
            dstg = TT("dbgstage", [128, 512]) if False else G[12]
            def ddump0(slot, ap, key, n):
                dve(lambda e: e.tensor_copy(out=dstg[:, 0:n], in_=ap), [key], ["pcw"])
                dma("sp", dbg[:, slot, 0:n], dstg[:, 0:n], ["pcw"], [], "dbg")
            ddump0(0, PA("ident"), "pA", 128)
            ddump0(1, PA("mui"), "pA", 128)
            ddump0(2, luw[:, :], "luw", 512)
            ddump0(3, gup[:, :], "gup", 512)
            ddump0(4, W1[:, 0, 0:512], "W1", 512)
            PP[0].max_ops = len(PP[0].ops)
            if STOP_EARLY:
                PP[0].finalize()
                raise _StopBuild()
        MUI = PA("mui")
        MUS = PA("mus")
        MLS = PA("mls")
        ONES = PA("ones")
        IDF = PA("ident")
        bc8 = lambda ap: ap.unsqueeze(2).to_broadcast([128, 8, 64])
        m8 = lambda m: m.unsqueeze(1).to_broadcast([128, 8, 128])
        v3 = lambda t: t[:].rearrange("p (h d) -> p h d", h=8)
        hoff = lambda h: (h % 2) * 512 + (h // 2) * 128

        def make_block(b):
            p_ = b % 2
            r_sb, kf_sb, vf, osig = G[0 + 20 * p_] if p_ == 0 else G[20], G[1] if p_ == 0 else G[21], G[2] if p_ == 0 else G[22], G[13] if p_ == 0 else G[23]
            vaug, gt, qpb, kTb, ktm, vrw, lor = VAUG[p_], GT[p_], QPB[p_], KTB[p_], KTM[p_], VRW[p_], LOR[p_]
            g_sb = G[5] if p_ == 0 else G[24]

            def front_stage():
                x_ = xt[b % 2]
                xk = "xt%d" % (b % 2)
                xn = xnT[b % 2]
                xnk = "xnT%d" % (b % 2)
                qx = qkx[b % 2]
                qxk = "qkx%d" % (b % 2)
                dma("sp", x_[:], xp[b * 128:(b + 1) * 128, :], [], [xk], xk)
                rmsnorm_T(x_, xk, 128, xn, xnk, 1, "nmw", xsb, "xsb", junk, "junk", st, "st")
                cur_x = xn[:, :, 1:129]
                prv_x = xn[:, :, 0:128]
                dve(lambda e, xn=xn: e.tensor_tensor(out=dxT[:], in0=xn[:, :, 0:128], in1=xn[:, :, 1:129], op=ALU.subtract), [xnk], ["dxT"])

                yield
                ps, psb, pk = nps()
                for j in range(8):
                    for c in range(8):
                        mm(ps[:, j * 128:(j + 1) * 128], wq[:, c, j * 128:(j + 1) * 128], cur_x[:, c, :], c == 0, c == 7, wqk(j * 128, (j + 1) * 128) + [xnk], [pk])
                for a_ in range(2):
                    act(lambda e, ps=ps, qx=qx, a_=a_: e.copy(out=qx[:, 4 * a_:4 * a_ + 4, 3:131], in_=ps[:, a_ * 512:(a_ + 1) * 512].rearrange("p (j t) -> p j t", j=4)), [pk], [qxk])
                if b == NB - 1:
                    dma("pool", oconv, qx[:, :, 128:131], [qxk], [], "fin")

                yield
                def tm_plain(col0, ncol, ps_ap, pk):
                    for c in range(8):
                        mm(ps_ap, cur_x[:, c, :], wq[:, c, col0:col0 + ncol], c == 0, c == 7, [xnk] + wqk(col0, col0 + ncol), [pk])

                def tm_shift(col0, ncol, dst, dk):
                    ps, psb, pk = nps()
                    for c in range(8):
                        mm(ps[:, 0:ncol], cur_x[:, c, :], wq[:, c, MLW + col0:MLW + col0 + ncol], c == 0, c == 7, [xnk] + wqk(MLW + col0, MLW + col0 + ncol), [pk])
                    for c in range(8):
                        mm(ps[:, 512:512 + ncol], dxT[:, c, :], wq[:, c, MLW + col0:MLW + col0 + ncol], c == 0, c == 7, ["dxT"] + wqk(MLW + col0, MLW + col0 + ncol), [pk])
                    dve(lambda e, ps=ps: e.tensor_tensor(out=dst, in0=ps[:, 512:512 + ncol], in1=mub[:, col0:col0 + ncol], op=ALU.mult), [pk, "mub"], [dk])
                    dve(lambda e, ps=ps: e.tensor_tensor(out=dst, in0=dst, in1=ps[:, 0:ncol], op=ALU.add), [pk, dk], [dk])

                ps, psb, pk = nps()
                tm_plain(1024, 512, ps[:, 0:512], pk)
                tm_plain(1536, 512, ps[:, 512:1024], pk)
                act(lambda e, ps=ps: e.copy(out=vaug[:, :, 0:64], in_=ps[:, 0:512].rearrange("p (h d) -> p h d", h=8)), [pk], ["vaug"])
                act(lambda e, ps=ps: e.activation(out=osig[:], in_=ps[:, 512:1024], func=AF.Sigmoid), [pk], ["osig"])
                ps, psb, pk = nps()
                tm_plain(2048, 16, ps[:, 0:16], pk)
                dve(lambda e, ps=ps: e.tensor_tensor(out=gt[:, 0:16], in0=ps[:, 0:16], in1=PA("ifb"), op=ALU.add), [pk, "pA"], ["gt"])
                tm_shift(0, 512, r_sb[:], "r_sb")
                tm_shift(512, 512, kf_sb[:], "kf_sb")
                tm_shift(1024, 512, vf[:], "vf")
                pool(lambda e: e.tensor_copy(out=vrw[:], in_=vf[:].rearrange("p (h d) -> p h d", h=8)), ["vf"], ["vrw"])
                ltmp = G[16]
                tm_shift(1536, 256, ltmp[:, 0:256], "cacc")
                act(lambda e: e.activation(out=lor[:, 0:64], in_=ltmp[:, 0:64], func=AF.Tanh), ["cacc"], ["lor"])
                act(lambda e: e.copy(out=lor[:, 64:128], in_=ltmp[:, 64:128]), ["cacc"], ["lor"])
                act(lambda e: e.activation(out=lor[:, 128:256], in_=ltmp[:, 128:256], func=AF.Sigmoid), ["cacc"], ["lor"])
                if b == NB - 1:
                    lastc = xn[:, :, 128:129]
                    for n0 in range(0, RWW, 512):
                        nn = min(512, RWW - n0)
                        ps2, _, pk2 = nps()
                        for c in range(8):
                            mm(ps2[0:1, 0:nn], lastc[:, c, :], wq[:, c, MLW + n0:MLW + n0 + nn], c == 0, c == 7, [xnk] + wqk(MLW + n0, MLW + n0 + nn), [pk2])
                        act(lambda e, ps2=ps2, n0=n0, nn=nn: e.copy(out=plast[0:1, 0:nn], in_=ps2[0:1, 0:nn]), [pk2], ["plast"])
                        dma("pool", oshift[:, n0:n0 + nn], plast[0:1, 0:nn], ["plast"], [], "fin")

                act(lambda e: e.activation(out=gt[:, 56:64], in_=gt[:, 8:16], func=AF.Exp, scale=-1.0), ["gt"], ["gt"])
                act(lambda e: e.activation(out=gt[:, 16:24], in_=gt[:, 56:64], func=AF.Ln, bias=1.0, scale=1.0), ["gt"], ["gt"])
                dve(lambda e: e.tensor_copy(out=nlrep[:], in_=bc8(gt[:, 16:24])), ["gt"], ["nlrep"])
                ps, psb, pk = nps()
                mm(ps[:, 0:8], MUI, gt[:, 16:24], True, True, ["pA", "gt"], [pk])
                mm(ps[:, 8:16], ONES, gt[:, 16:24], True, True, ["pA", "gt"], [pk])
                for j in range(4):
                    mm(ps[:, 512 + j * 128:512 + (j + 1) * 128], nlrep[:, 2 * j:2 * j + 2, :].rearrange("p a d -> p (a d)"), MUI, True, True, ["nlrep", "pA"], [pk])
                dve(lambda e, ps=ps: e.tensor_tensor(out=gt[:, 24:32], in0=ps[:, 0:8], in1=gt[:, 0:8], op=ALU.add), [pk, "gt"], ["gt"])
                act(lambda e: e.activation(out=gt[:, 32:40], in_=gt[:, 24:32], func=AF.Exp), ["gt"], ["gt"])
                dve(lambda e, ps=ps: e.tensor_tensor(out=gt[:, 56:64], in0=gt[:, 24:32], in1=ps[:, 8:16], op=ALU.subtract), [pk, "gt"], ["gt"])
                act(lambda e: e.activation(out=gt[:, 40:48], in_=gt[:, 56:64], func=AF.Exp), ["gt"], ["gt"])
                act(lambda e, ps=ps: e.activation(out=gt[:, 48:56], in_=ps[:, 8:16], func=AF.Exp, scale=-1.0), [pk], ["gt"])
                act(lambda e, ps=ps: e.activation(out=Fb[:], in_=ps[:, 512:1024].rearrange("p (j t) -> p j t", j=4), func=AF.Exp, scale=-1.0), [pk], ["Fb"])
                dve(lambda e: e.tensor_tensor(out=gt[:, 56:64], in0=gt[:, 24:32], in1=nBc[:], op=ALU.add), ["gt", "nBc"], ["gt"])
                dve(lambda e: e.tensor_tensor(out=runmax[:], in0=runmax[:], in1=gt[:, 56:64], op=ALU.max), ["gt", "runmax"], ["runmax"])
                dve(lambda e, ps=ps: e.tensor_tensor(out=nBc[:], in0=nBc[:], in1=ps[:, 8:16], op=ALU.add), [pk, "nBc"], ["nBc"])

                yield
                cwv = PA("cw").rearrange("p (c j) -> p c j", j=4)
                wbc = lambda j: cwv[:, :, j:j + 1].to_broadcast([128, 8, 128])
                pool(lambda e, qx=qx: e.tensor_tensor(out=cacc[:], in0=qx[:, :, 3:131], in1=wbc(3), op=ALU.mult), [qxk, "pA"], ["cacc"])
                for j in range(3):
                    pool(lambda e, qx=qx, j=j: e.tensor_tensor(out=ctmp[:], in0=qx[:, :, j:j + 128], in1=wbc(j), op=ALU.mult), [qxk, "pA"], ["ctmp"])
                    pool(lambda e: e.tensor_tensor(out=cacc[:], in0=cacc[:], in1=ctmp[:], op=ALU.add), ["cacc", "ctmp"], ["cacc"])
                pool(lambda e: e.tensor_tensor(out=cacc[:], in0=cacc[:], in1=PA("cb").unsqueeze(2).to_broadcast([128, 8, 128]), op=ALU.add), ["cacc", "pA"], ["cacc"])
                act(lambda e: e.activation(out=qks[:], in_=cacc[:], func=AF.Silu), ["cacc"], ["qks"])
                dve(lambda e: e.tensor_tensor(out=qpb[:], in0=qks[:, 0:4, :], in1=Fb[:], op=ALU.mult), ["qks", "Fb"], ["qpb"])
                act(lambda e: e.activation(out=kTb[:], in_=qks[:, 4:8, :], func=AF.Copy, scale=0.125), ["qks"], ["kTb"])

                yield
                ps, psb, pk = nps()
                for j in range(4):
                    pe(lambda e, j=j, psb=psb: e.transpose(psb[:, j * 128:(j + 1) * 128], kTb[:, j, :], identb[:]), ["kTb", "identb"], [pk])
                dve(lambda e, psb=psb: e.tensor_tensor(out=ktm[:], in0=psb[:, 0:512].rearrange("p (h d) -> p h d", h=8), in1=bc8(gt[:, 40:48]), op=ALU.mult), [pk, "gt"], ["ktm"])

                yield
                yield
                if b + 1 < NB:
                    pool(lambda e, xn=xn: e.tensor_copy(out=xn[:, :, 0:1], in_=xn[:, :, 128:129]), [xnk], [xnk])
                    pool(lambda e, qx=qx: e.tensor_copy(out=qx[:, :, 0:3], in_=qx[:, :, 128:131]), [qxk], [qxk])
                yield

            def ml_stage():
                ps, psb, pk = nps()
                for h in range(8):
                    j, hp = h // 2, h % 2
                    sl = slice(hp * 64, hp * 64 + 64)
                    mm(ps[:, hoff(h):hoff(h) + 128], kTb[sl, j, :], qpb[sl, j, :], True, True, ["kTb", "qpb"], [pk])
                for h in range(8):
                    dve(lambda e, h=h, ps=ps: e.scalar_tensor_tensor(out=PTb[:, h, :], in0=ps[:, hoff(h):hoff(h) + 128], scalar=gt[:, 32 + h:33 + h], in1=MUI, op0=ALU.mult, op1=ALU.mult), [pk, "gt", "pA"], ["PTb"])
                yield
                ps, psb, pk = nps()
                psn = lambda ps, h: ps[:, (h // 4) * 512 + (h % 4) * 65:(h // 4) * 512 + (h % 4) * 65 + 65]
                for h in range(8):
                    j, hp = h // 2, h % 2
                    sl = slice(hp * 64, hp * 64 + 64)
                    mm(psn(ps, h), PTb[:, h, :], vaug[:, h, :], True, False, ["PTb", "vaug"], [pk])
                    mm(psn(ps, h), qpb[sl, j, :], Cbf[sl, j, :], False, True, ["qpb", "Cbf"], [pk])
                pn4 = ps[:, :].rearrange("p (a r) -> p a r", a=2)[:, :, 0:260].rearrange("p a (h d) -> p a h d", h=4)
                for a_ in range(2):
                    act(lambda e, pn4=pn4, a_=a_: e.copy(out=r8[:, 4 * a_:4 * a_ + 4], in_=pn4[:, a_, :, 64]), [pk], ["r8"])
                dve(lambda e: e.scalar_tensor_tensor(out=r8[:, 8:16], in0=r8[:, 0:8], scalar=-1.0, in1=r8[:, 0:8], op0=ALU.mult, op1=ALU.max), ["r8"], ["r8"])
                dve(lambda e: e.tensor_scalar_max(out=r8[:, 8:16], in0=r8[:, 8:16], scalar1=1.0), ["r8"], ["r8"])
                dve(lambda e: e.reciprocal(out=r8[:, 16:24], in_=r8[:, 8:16]), ["r8"], ["r8"])
                for a_ in range(2):
                    dve(lambda e, pn4=pn4, a_=a_: e.tensor_tensor(out=hml[:, a_ * 256:(a_ + 1) * 256].rearrange("p (h d) -> p h d", h=4), in0=pn4[:, a_, :, 0:64],
                                                              in1=r8[:, 16 + 4 * a_:20 + 4 * a_].unsqueeze(2).to_broadcast([128, 4, 64]), op=ALU.mult), [pk, "r8"], ["hml"])
                yield
                ps, psb, pk = nps()
                for h in range(8):
                    j = h // 2
                    mm(psn(ps, h), ktm[:, 2 * j:2 * j + 2, :].rearrange("p a d -> p (a d)"), vaug[:, h, :], True, True, ["ktm", "vaug"], [pk])
                pu4 = ps[:, :].rearrange("p (a r) -> p a r", a=2)[:, :, 0:260].rearrange("p a (h d) -> p a h d", h=4)
                for hp in range(2):
                    sl = slice(hp * 64, hp * 64 + 64)
                    decb = gt[sl, 48:56].rearrange("p (j q) -> p j q", q=2)[:, :, hp:hp + 1].to_broadcast([64, 4, 65])
                    dve(lambda e, sl=sl, decb=decb: e.tensor_tensor(out=Cst[sl, :, :], in0=Cst[sl, :, :], in1=decb, op=ALU.mult), ["Cst", "gt"], ["Cst"])
                    for a in range(2):
                        src = pu4[sl, a, hp::2, :]
                        dve(lambda e, sl=sl, a=a, src=src: e.tensor_tensor(out=Cst[sl, 2 * a:2 * a + 2, :], in0=Cst[sl, 2 * a:2 * a + 2, :], in1=src, op=ALU.add), [pk, "Cst"], ["Cst"])
                act(lambda e: e.copy(out=Cbf[:], in_=Cst[:]), ["Cst"], ["Cbf"])
                head_ml(128, hml, "hml", osig, "osig", mix, "mix", WM, "m")


                yield
            def rw_stage():
                ps, psb, pk = nps()
                pe(lambda e, psb=psb: e.transpose(psb[:, 0:128], lor[:, 0:128], identb[:]), ["lor", "identb"], [pk])
                pe(lambda e, psb=psb: e.transpose(psb[:, 128:256], lor[:, 128:256], identb[:]), ["lor", "identb"], [pk])
                act(lambda e, psb=psb: e.copy(out=lorT[:], in_=psb[:, 0:256].rearrange("p (a t) -> p a t", a=2)), [pk], ["lorT"])
                ps, psb, pk = nps()
                mm(ps[:, 0:512], lorT[0:64, 0, :], luw[0:64, :], True, True, ["lorT", "luw"], [pk])
                mm(ps[:, 512:1024], lorT[64:128, 0, :], luw[64:128, :], True, True, ["lorT", "luw"], [pk])
                dve(lambda e, ps=ps: e.tensor_tensor(out=e1[:], in0=ps[:, 0:512], in1=PA("w0"), op=ALU.add), [pk, "pA"], ["e1"])
                act(lambda e: e.activation(out=wsig[:], in_=e1[:], func=AF.Sigmoid), ["e1"], ["wsig"])
                dve(lambda e, ps=ps: e.tensor_tensor(out=e2[:], in0=ps[:, 512:1024], in1=PA("a0"), op=ALU.add), [pk, "pA"], ["e2"])
                act(lambda e: e.activation(out=a_sb[:], in_=e2[:], func=AF.Sigmoid), ["e2"], ["a_sb"])
                ps, psb, pk = nps()
                mm(ps[:, 0:512], lorT[:, 1, :], gup[:, :], True, True, ["lorT", "gup"], [pk])
                act(lambda e, ps=ps: e.copy(out=g_sb[:], in_=ps[:, 0:512]), [pk], ["g_sb"])
                yield
                dve(lambda e: e.tensor_tensor(out=e1[:], in0=kf_sb[:], in1=PA("kk"), op=ALU.mult), ["kf_sb", "pA"], ["e1"])
                dve(lambda e: e.tensor_tensor(out=e2[:], in0=e1[:], in1=e1[:], op=ALU.mult), ["e1"], ["e2"])
                dve(lambda e: e.tensor_reduce(out=r8b[:, 0:8], in_=v3(e2), axis=AX.X, op=ALU.add), ["e2"], ["r8b"])
                dve(lambda e: e.tensor_scalar_max(out=r8b[:, 0:8], in0=r8b[:, 0:8], scalar1=1e-24), ["r8b"], ["r8b"])
                act(lambda e: e.activation(out=r8b[:, 0:8], in_=r8b[:, 0:8], func=AF.Ln), ["r8b"], ["r8b"])
                act(lambda e: e.activation(out=r8b[:, 0:8], in_=r8b[:, 0:8], func=AF.Exp, scale=-0.5), ["r8b"], ["r8b"])
                dve(lambda e: e.tensor_tensor(out=v3(kap), in0=v3(e1), in1=bc8(r8b[:, 0:8]), op=ALU.mult), ["e1", "r8b"], ["kap"])
                dve(lambda e: e.tensor_scalar_add(out=e2[:], in0=a_sb[:], scalar1=-1.0), ["a_sb"], ["e2"])
                dve(lambda e: e.tensor_tensor(out=e2[:], in0=e2[:], in1=PA("ka"), op=ALU.mult), ["e2", "pA"], ["e2"])
                dve(lambda e: e.tensor_tensor(out=e2[:], in0=e2[:], in1=kf_sb[:], op=ALU.mult), ["e2", "kf_sb"], ["e2"])
                dve(lambda e: e.tensor_tensor(out=ktl[:], in0=e2[:], in1=kf_sb[:], op=ALU.add), ["e2", "kf_sb"], ["ktl"])
                dve(lambda e: e.tensor_tensor(out=bvec[:], in0=a_sb[:], in1=kap[:], op=ALU.mult), ["a_sb", "kap"], ["bvec"])
                dve(lambda e: e.tensor_tensor(out=e2[:], in0=r_sb[:], in1=ktl[:], op=ALU.mult), ["r_sb", "ktl"], ["e2"])
                dve(lambda e: e.tensor_tensor(out=e2[:], in0=e2[:], in1=PA("rk"), op=ALU.mult), ["e2", "pA"], ["e2"])
                dve(lambda e: e.tensor_reduce(out=bon[:], in_=v3(e2), axis=AX.X, op=ALU.add), ["e2"], ["bon"])
                yield
                ps, psb, pk = nps()
                mm(ps[:, 0:512], MUI, wsig[:], True, True, ["pA", "wsig"], [pk])
                mm(ps[:, 512:1024], ONES, wsig[:], True, True, ["pA", "wsig"], [pk])
                act(lambda e, ps=ps: e.copy(out=pcw[:], in_=ps[:, 0:512]), [pk], ["pcw"])
                dve(lambda e: e.tensor_tensor(out=e1[:], in0=pcw[:], in1=wsig[:], op=ALU.subtract), ["pcw", "wsig"], ["e1"])
                act(lambda e: e.activation(out=e1[:], in_=e1[:], func=AF.Exp, scale=-C0), ["e1"], ["e1"])
                dve(lambda e: e.tensor_tensor(out=TMb[:, 0, :], in0=kap[:], in1=e1[:], op=ALU.mult), ["kap", "e1"], ["TMb0"])
                act(lambda e: e.activation(out=e2[:], in_=pcw[:], func=AF.Exp, scale=-C0), ["pcw"], ["e2"])
                dve(lambda e: e.tensor_tensor(out=TMb[:, 1, :], in0=r_sb[:], in1=e2[:], op=ALU.mult), ["r_sb", "e2"], ["TMb1"])
                act(lambda e: e.activation(out=e3[:], in_=pcw[:], func=AF.Exp, scale=C0), ["pcw"], ["e3"])
                dve(lambda e: e.tensor_tensor(out=TMb[:, 2, :], in0=bvec[:], in1=e3[:], op=ALU.mult), ["bvec", "e3"], ["TMb2"])
                dve(lambda e: e.tensor_tensor(out=TMb[:, 3, :], in0=ktl[:], in1=e3[:], op=ALU.mult), ["ktl", "e3"], ["TMb3"])
                dve(lambda e, ps=ps: e.tensor_tensor(out=e1[:], in0=ps[:, 512:1024], in1=pcw[:], op=ALU.subtract), [pk, "pcw"], ["e1"])
                act(lambda e: e.activation(out=e1[:], in_=e1[:], func=AF.Exp, scale=-C0), ["e1"], ["e1"])
                for hp in range(2):
                    srcb = v3(bvec).rearrange("p (j q) d -> p j q d", q=2)[:, :, hp, :]
                    srck = v3(ktl).rearrange("p (j q) d -> p j q d", q=2)[:, :, hp, :]
                    wl = v3(e1).rearrange("p (j q) d -> p j q d", q=2)[:, :, hp, :]
                    dstb = Btz[:].rearrange("p (j q) c -> p j q c", q=2)[:, :, hp, hp * 64:hp * 64 + 64]
                    dstk = Ktz[:].rearrange("p (j q) c -> p j q c", q=2)[:, :, hp, hp * 64:hp * 64 + 64]
                    dve(lambda e, srcb=srcb, wl=wl, dstb=dstb: e.tensor_tensor(out=dstb, in0=srcb, in1=wl, op=ALU.mult), ["bvec", "e1"], ["Btz"])
                    dve(lambda e, srck=srck, wl=wl, dstk=dstk: e.tensor_tensor(out=dstk, in0=srck, in1=wl, op=ALU.mult), ["ktl", "e1"], ["Ktz"])
                ps2, _, pk2 = nps()
                for j in range(4):
                    mm(ps2[:, j:j + 1], wsig[:, j * 128:(j + 1) * 128], ONES[:, 0:1], True, True, ["wsig", "pA"], [pk2])
                act(lambda e, ps2=ps2: e.activation(out=WLfm[:], in_=ps2[:, 0:4], func=AF.Exp, scale=-C0), [pk2], ["WLfm"])
                yield
                ps, psb, pk = nps()
                for w_ in range(4):
                    for j in range(4):
                        pe(lambda e, w_=w_, j=j, psb=psb: e.transpose(psb[:, (w_ * 4 + j) * 128:(w_ * 4 + j + 1) * 128], TMb[:, w_, j * 128:(j + 1) * 128], identb[:]), ["TMb%d" % w_, "identb"], [pk])
                for w_ in range(4):
                    eng_ = act if w_ % 2 == 0 else dve
                    if w_ % 2 == 0:
                        act(lambda e, psb=psb, w_=w_: e.copy(out=FMt[:, w_, :, :], in_=psb[:, w_ * 512:(w_ + 1) * 512].rearrange("p (j t) -> p j t", j=4)), [pk], ["FMt"])
                    else:
                        dve(lambda e, psb=psb, w_=w_: e.tensor_copy(out=FMt[:, w_, :, :], in_=psb[:, w_ * 512:(w_ + 1) * 512].rearrange("p (j t) -> p j t", j=4)), [pk], ["FMt"])
                KAP, RB, BB, KKB = 0, 1, 2, 3

                def amat(lw, rw_, dst, dk, mask, neg):
                    ps, psb, pk = nps()
                    for h in range(8):
                        j, hp = h // 2, h % 2
                        sl = slice(hp * 64, hp * 64 + 64)
                        mm(ps[:, hoff(h):hoff(h) + 128], FMt[sl, lw, j, :], FMt[sl, rw_, j, :], True, True, ["FMt"], [pk])
                    psv = ps[:, :].rearrange("p (q j t) -> p q j t", q=2, j=4)
                    dstv = dst[:].rearrange("p (j q) t -> p q j t", q=2)
                    mk = mask.unsqueeze(1).unsqueeze(1).to_broadcast([128, 2, 4, 128])
                    if neg:
                        mk3 = mask.unsqueeze(1).to_broadcast([128, 4, 128])
                        for q in range(2):
                            dve(lambda e, q=q: e.scalar_tensor_tensor(out=dstv[:, q], in0=psv[:, q], scalar=-1.0, in1=mk3, op0=ALU.mult, op1=ALU.mult), [pk, "pA"], [dk])
                    else:
                        mk3 = mask.unsqueeze(1).to_broadcast([128, 4, 128])
                        for q in range(2):
                            dve(lambda e, q=q: e.tensor_tensor(out=dstv[:, q], in0=psv[:, q], in1=mk3, op=ALU.mult), [pk, "pA"], [dk])

                amat(BB, KAP, Pw[1], "Pw1", MUS, True)
                amat(KAP, BB, Pw[0], "Pw0", MLS, True)
                amat(KKB, KAP, Am[0], "Am0", MUS, False)
                amat(BB, RB, Am[1], "Am1", MUI, False)
                amat(KKB, RB, Am[2], "Am2", MUI, False)
                yield
                ps, psb, pk = nps()
                for h in range(8):
                    j, hp = h // 2, h % 2
                    sl = slice(hp * 64, hp * 64 + 64)
                    mm(ps[:, h * 64:(h + 1) * 64], FMt[sl, KAP, j, :], Sbf[sl, j, :], True, False, ["FMt", "Sbf"], [pk])
                    mm(ps[:, h * 64:(h + 1) * 64], Am[0][:, h, :], vrw[:, h, :], False, True, ["Am0", "vrw"], [pk])
                act(lambda e, ps=ps: e.copy(out=Zb[0][:], in_=ps[:, 0:512].rearrange("p (h d) -> p h d", h=8)), [pk], ["Zb0"])
                pi = 0
                zi = 0
                for lvl in range(7):
                    yield
                    Pc, PTc = Pw[pi], Pw[pi + 1]
                    Pk, PTk = "Pw%d" % pi, "Pw%d" % (pi + 1)
                    Zc, Zn = Zb[zi], Zb[1 - zi]
                    ps, psb, pk = nps()
                    for h in range(8):
                        mm(ps[:, h * 64:(h + 1) * 64], identb[:], Zc[:, h, :], True, False, ["identb", "Zb%d" % zi], [pk])
                        mm(ps[:, h * 64:(h + 1) * 64], PTc[:, h, :], Zc[:, h, :], False, True, [PTk, "Zb%d" % zi], [pk])
                    if lvl < 6:
                        act(lambda e, ps=ps, Zn=Zn: e.copy(out=Zn[:], in_=ps[:, 0:512].rearrange("p (h d) -> p h d", h=8)), [pk], ["Zb%d" % (1 - zi)])
                        zi = 1 - zi
                        ni = 2 - pi
                        Pn, PTn = Pw[ni], Pw[ni + 1]
                        psA, _, pkA = nps()
                        for h in range(8):
                            mm(psA[:, h * 128:(h + 1) * 128], PTc[:, h, :], Pc[:, h, :], True, True, [PTk, Pk], [pkA])
                        for a_ in range(2):
                            dve(lambda e, psA=psA, Pn=Pn, a_=a_: e.tensor_copy(out=Pn[:, 4 * a_:4 * a_ + 4, :], in_=psA[:, a_ * 512:(a_ + 1) * 512].rearrange("p (h t) -> p h t", h=4)), [pkA], ["Pw%d" % ni])
                        psB, _, pkB = nps()
                        for h in range(8):
                            mm(psB[:, h * 128:(h + 1) * 128], Pc[:, h, :], PTc[:, h, :], True, True, [Pk, PTk], [pkB])
                        for a_ in range(2):
                            act(lambda e, psB=psB, PTn=PTn, a_=a_: e.copy(out=PTn[:, 4 * a_:4 * a_ + 4, :], in_=psB[:, a_ * 512:(a_ + 1) * 512].rearrange("p (h t) -> p h t", h=4)), [pkB], ["Pw%d" % (ni + 1)])
                        pi = ni
                    else:
                        act(lambda e, ps=ps: e.activation(out=Ub[:], in_=ps[:, 0:512].rearrange("p (h d) -> p h d", h=8), func=AF.Copy, scale=-1.0), [pk], ["Ub"])
                yield
                ps, psb, pk = nps()
                for h in range(8):
                    j, hp = h // 2, h % 2
                    sl = slice(hp * 64, hp * 64 + 64)
                    o_ = ps[:, h * 64:(h + 1) * 64]
                    mm(o_, Am[1][:, h, :], Ub[:, h, :], True, False, ["Am1", "Ub"], [pk])
                    mm(o_, Am[2][:, h, :], vrw[:, h, :], False, False, ["Am2", "vrw"], [pk])
                    mm(o_, FMt[sl, RB, j, :], Sbf[sl, j, :], False, True, ["FMt", "Sbf"], [pk])
                act(lambda e, ps=ps: e.copy(out=ysb[:], in_=ps[:, 0:512]), [pk], ["ysb"])
                yield
                ps, psb, pk = nps()
                for j in range(4):
                    o_ = ps[:, j * 64:(j + 1) * 64]
                    mm(o_, Btz[:, 2 * j, :], Ub[:, 2 * j, :], True, False, ["Btz", "Ub"], [pk])
                    mm(o_, Ktz[:, 2 * j, :], vrw[:, 2 * j, :], False, False, ["Ktz", "vrw"], [pk])
                    mm(o_, Btz[:, 2 * j + 1, :], Ub[:, 2 * j + 1, :], False, False, ["Btz", "Ub"], [pk])
                    mm(o_, Ktz[:, 2 * j + 1, :], vrw[:, 2 * j + 1, :], False, True, ["Ktz", "vrw"], [pk])
                dve(lambda e: e.tensor_tensor(out=Sst[:], in0=Sst[:], in1=WLfm[:].unsqueeze(2).to_broadcast([128, 4, 64]), op=ALU.mult), ["Sst", "WLfm"], ["Sst"])
                dve(lambda e, ps=ps: e.tensor_tensor(out=Sst[:], in0=Sst[:], in1=ps[:, 0:256].rearrange("p (j d) -> p j d", j=4), op=ALU.add), [pk, "Sst"], ["Sst"])
                act(lambda e: e.copy(out=Sbf[:], in_=Sst[:]), ["Sst"], ["Sbf"])
                yield
            def tail():
                head_rw(128, ysb, "ysb", bon, "bon", vf, "vf", g_sb, "g_sb", mix, "mix", {"tA": G[18], "tB": G[19], "s8": W["s8"]})
                out_proj(128, mix, "mix", None, None, b * 128, W)

            return front_stage, ml_stage, rw_stage, tail, p_

        def run_gens(gl):
            gl = list(gl)
            while gl:
                for item in list(gl):
                    CURP[0] = item[1]
                    CURPOOL[0] = item[3] if len(item) > 3 else "A"
                    for _ in range(item[2] if len(item) > 2 else 1):
                        try:
                            next(item[0])
                        except StopIteration:
                            gl.remove(item)
                            break

        blocks = [make_block(b) for b in range(NB)]
        run_gens([(blocks[0][0](), 0, 1, "A")])
        for b in range(NB):
            fr, ml_, rw_, tl, p_ = blocks[b]
            gl = [(rw_(), p_, 1, "A"), (ml_(), p_, 1, "A")]
            if b + 1 < NB:
                gl.append((blocks[b + 1][0](), (b + 1) % 2, 1, "A"))
            run_gens(gl)
            CURP[0] = p_
            CURPOOL[0] = "A"
            tl()
        CURP[0] = 0

        ps, psb, pk = nps()
        mm(ps[0:8, 0:128], runmax[:], IDF, True, True, ["runmax", "pA"], [pk])
        mm(ps[0:8, 128:256], nBc[:], IDF, True, True, ["nBc", "pA"], [pk])
        fs = TT("fs", [8, 16])
        dve(lambda e, ps=ps: e.tensor_reduce(out=fs[:, 0:1], in_=ps[0:8, 0:128], axis=AX.X, op=ALU.max), [pk], ["fs"])
        dve(lambda e: e.tensor_scalar_max(out=fs[:, 0:1], in0=fs[:, 0:1], scalar1=0.0), ["fs"], ["fs"])
        dve(lambda e, ps=ps: e.tensor_tensor(out=fs[:, 1:2], in0=fs[:, 0:1], in1=ps[0:8, 128:129], op=ALU.subtract), [pk, "fs"], ["fs"])
        dma("pool", om, fs[:, 1:2], ["fs"], [], "fin")
        act(lambda e: e.activation(out=fs[:, 2:3], in_=fs[:, 1:2], func=AF.Exp, scale=-1.0), ["fs"], ["fs"])
        dve(lambda e: e.tensor_scalar_mul(out=fs[:, 4:8], in0=pA[0:8, OFF["rsel"][0]:OFF["rsel"][1]], scalar1=fs[:, 2:3]), ["fs", "pA"], ["fs"])
        ps, psb, pk = nps()
        mm(ps[:, 0:4], pA[0:8, OFF["lsel"][0]:OFF["lsel"][1]], fs[:, 4:8], True, True, ["pA", "fs"], [pk])
        scb = TT("scb", [128, 4])
        act(lambda e, ps=ps: e.copy(out=scb[:], in_=ps[:, 0:4]), [pk], ["scb"])
        dve(lambda e: e.tensor_tensor(out=Cst[:], in0=Cst[:], in1=scb[:].unsqueeze(2).to_broadcast([128, 4, 65]), op=ALU.mult), ["Cst", "scb"], ["Cst"])
        dma("pool", oC, Cst[:], ["Cst"], [], "fin")
        dma("pool", oS, Sst[:], ["Sst"], [], "fin")
        PP[0].finalize()
        PP[0] = Prog(ctx)

    with contextlib.ExitStack() as es_s:
        cur[0] = es_s
        if do_sample:
            W = {}
            W["xm"] = TT("s_xm", [128, D])
            junk = W["xm"]
            st = TT("s_st", [128, 4])
            mix = TT("s_mix", [128, D], BF16)
            xsb = mix
            lor = TT("s_lor", [128, 256], BF16)
            lorT = TT("s_lorT", [128, 2, 128], BF16)
            W["mixT"] = TT("s_mixT", [128, 8, 128], BF16)
            sx = TT("sx", [NS, D])
            sxT = TT("sxT", [128, 8, NS], BF16)
            spj = TT("spj", [NS, INW])
            spk = TT("spk", [128, NSP])
            sl_t = TT("sl_t", [NS, 3, 256])
            Cs = TT("Cs", [128, 4096])
            Ss = Cs
            sn_t = TT("sn_t", [128, 64])
            sm_t = TT("sm_t", [128, 1])
            scv = TT("scv", [128, 2, 4, 64])
            ssh = TT("ssh", [128, 3, 64])
            dma("sp", sx[:], xs, [], ["sx"], "sin", True)
            dma("sp", spk[:], spk_d, [], ["spk"], "sin", True)
            dma("sp", sl_t[:, 0, :], sshl_d, [], ["sl_t"], "sin", True)
            dma("sp", sl_t[:, 1, :], mul_d, [], ["sl_t"], "sin", True)
            dma("sp", Cs[:], sC_d, [], ["Cs"], "sin", True)
            dma("sp", sn_t[:], sn_d, [], ["sn_t"], "sin", True)
            dma("sp", sm_t[:], sm_d, [], ["sm_t"], "sin", True)
            dma("sp", scv[:, :, 0:3, :], sconv_d, [], ["scv"], "sin", True)
            dma("sp", ssh[:], sshift_d, [], ["ssh"], "sin", True)
            rmsnorm_T(sx, "sx", NS, sxT, "sxT", 0, "nmw", xsb, "xsb", junk, "junk", st, "st")
            for n0 in range(0, INW, 512):
                nn = min(512, INW - n0)
                ps, psb, pk = nps()
                for c in range(8):
                    mm(ps[:NS, 0:nn], sxT[:, c, :], wq[:, c, n0:n0 + nn], c == 0, c == 7, ["sxT", "wq"], [pk])
                act(lambda e, ps=ps, n0=n0, nn=nn: e.copy(out=spj[:, n0:n0 + nn], in_=ps[:NS, 0:nn]), [pk], ["spj"])
            s1v = scr1.rearrange("(b h) a d -> b a h d", b=NS)
            for a_ in range(7):
                c0_ = a_ * 512 if a_ < 4 else MLW + (a_ - 4) * 512
                dma("pool", s1v[:, a_, :, :], spj[:, c0_:c0_ + 512].rearrange("p (h d) -> p h d", h=8), ["spj"], ["scr1"], "scrw1")
            A7 = TT("A7", [128, 8, 64])
            dma("sp", A7[:, 0:7, :], scr1[:, 0:7, :], ["scr1"], ["A7"], "scrr1")
            gif = TT("gif", [128, 2])
            s_if = nc.dram_tensor("scr_if", [2, 128], F32, kind="Internal").ap()
            for g_ in range(2):
                dma("pool", s_if[g_, :].rearrange("(b h) -> b h", b=NS), spj[:, 2048 + 8 * g_:2056 + 8 * g_], ["spj"], ["scr_if"], "scrwif")
            for g_ in range(2):
                dma("sp", gif[:, g_:g_ + 1], s_if[g_, :].rearrange("(p o) -> p o", o=1), ["scr_if"], ["gif"], "scrrif")
            SP_ = lambda n: spk[:, SOFF[n][0]:SOFF[n][1]]
            pl = spj[:, MLW + 1536:MLW + 1792]
            dma("pool", osshl, pl, ["spj"], [], "fin")
            dve(lambda e: e.tensor_tensor(out=sl_t[:, 2, :], in0=sl_t[:, 0, :], in1=pl, op=ALU.subtract), ["sl_t", "spj"], ["sl_t"])
            dve(lambda e: e.tensor_tensor(out=sl_t[:, 2, :], in0=sl_t[:, 2, :], in1=sl_t[:, 1, :], op=ALU.mult), ["sl_t"], ["sl_t"])
            dve(lambda e: e.tensor_tensor(out=sl_t[:, 2, :], in0=sl_t[:, 2, :], in1=pl, op=ALU.add), ["sl_t", "spj"], ["sl_t"])
            act(lambda e: e.activation(out=lor[:NS, 0:64], in_=sl_t[:, 2, 0:64], func=AF.Tanh), ["sl_t"], ["lor"])
            act(lambda e: e.copy(out=lor[:NS, 64:128], in_=sl_t[:, 2, 64:128]), ["sl_t"], ["lor"])
            act(lambda e: e.activation(out=lor[:NS, 128:256], in_=sl_t[:, 2, 128:256], func=AF.Sigmoid), ["sl_t"], ["lor"])
            ps, psb, pk = nps()
            pe(lambda e, psb=psb: e.transpose(psb[:, 0:NS], lor[:NS, 0:128], identb[:NS, :NS]), ["lor", "identb"], [pk])
            pe(lambda e, psb=psb: e.transpose(psb[:, 128:128 + NS], lor[:NS, 128:256], identb[:NS, :NS]), ["lor", "identb"], [pk])
            act(lambda e, psb=psb: e.copy(out=lorT[:, :, 0:NS], in_=psb[:, 0:256].rearrange("p (a t) -> p a t", a=2)[:, :, 0:NS]), [pk], ["lorT"])
            ps, psb, pk = nps()
            mm(ps[:NS, 0:512], lorT[0:64, 0, 0:NS], luw[0:64, :], True, True, ["lorT", "luw"], [pk])
            mm(ps[:NS, 512:1024], lorT[64:128, 0, 0:NS], luw[64:128, :], True, True, ["lorT", "luw"], [pk])
            ps2, _, pk2 = nps()
            mm(ps2[:NS, 0:512], lorT[:, 1, 0:NS], gup[:, :], True, True, ["lorT", "gup"], [pk2])
            lo3 = spj[:, 0:1536].rearrange("p (a n) -> p a n", a=3)
            for a_ in range(2):
                act(lambda e, ps=ps, a_=a_: e.copy(out=lo3[:, a_, :], in_=ps[:NS, a_ * 512:(a_ + 1) * 512]), [pk], ["lo3"])
            act(lambda e, ps2=ps2: e.copy(out=lo3[:, 2, :], in_=ps2[:NS, 0:512]), [pk2], ["lo3"])
            for a_ in range(3):
                dma("pool", scr2.rearrange("(b h) a d -> b a h d", b=NS)[:, a_, :, :], lo3[:, a_, :].rearrange("p (h d) -> p h d", h=8), ["lo3"], ["scr2"], "scrw2")
            L3 = TT("L3", [128, 3, 64])
            dma("sp", L3[:], scr2, ["scr2"], ["L3"], "scrr2")
            big = TT("big", [128, 4096])
            sv = TT("sv", [128, 64])
            pool(lambda e: e.tensor_copy(out=scv[:, :, 3, :], in_=A7[:, 0:2, :]), ["A7"], ["scv"])
            dma("pool", osconv, scv[:, :, 1:4, :], ["scv"], [], "fin")
            qk_s = TT("qk_s", [128, 2, 64])
            cwqk = lambda w_: spk[:, SOFF["cwq"][0] + w_ * 256:SOFF["cwq"][0] + (w_ + 1) * 256].rearrange("p (j d) -> p j d", j=4)
            for w_ in range(2):
                dve(lambda e, w_=w_: e.tensor_tensor(out=big[:, 0:256].rearrange("p (j d) -> p j d", j=4), in0=scv[:, w_, :, :], in1=cwqk(w_), op=ALU.mult), ["scv", "spk"], ["big"])
                dve(lambda e, w_=w_: e.tensor_reduce(out=qk_s[:, w_, :], in_=big[:, 0:256].rearrange("p (j d) -> p d j", j=4), axis=AX.X, op=ALU.add), ["big"], ["qk_s"])
            dve(lambda e: e.tensor_tensor(out=qk_s[:], in0=qk_s[:], in1=spk[:, SOFF["cbq"][0]:SOFF["cbk"][1]].rearrange("p (a d) -> p a d", a=2), op=ALU.add), ["qk_s", "spk"], ["qk_s"])
            act(lambda e: e.activation(out=qk_s[:], in_=qk_s[:], func=AF.Silu), ["qk_s"], ["qk_s"])
            act(lambda e: e.activation(out=qk_s[:, 1, :], in_=qk_s[:, 1, :], func=AF.Copy, scale=0.125), ["qk_s"], ["qk_s"])
            dve(lambda e: e.tensor_tensor(out=sv[:, 0:2], in0=gif[:], in1=spk[:, SOFF["ib"][0]:SOFF["fb"][1]], op=ALU.add), ["gif", "spk"], ["sv"])
            act(lambda e: e.activation(out=sv[:, 9:10], in_=sv[:, 1:2], func=AF.Exp, scale=-1.0), ["sv"], ["sv"])
            act(lambda e: e.activation(out=sv[:, 2:3], in_=sv[:, 9:10], func=AF.Ln, bias=1.0, scale=1.0), ["sv"], ["sv"])
            dve(lambda e: e.tensor_tensor(out=sv[:, 3:4], in0=sm_t[:], in1=sv[:, 2:3], op=ALU.subtract), ["sv", "sm_t"], ["sv"])
            dve(lambda e: e.tensor_tensor(out=sv[:, 4:5], in0=sv[:, 3:4], in1=sv[:, 0:1], op=ALU.max), ["sv"], ["sv"])
            dma("pool", osm, sv[:, 4:5], ["sv"], [], "fin")
            dve(lambda e: e.tensor_tensor(out=sv[:, 9:10], in0=sv[:, 0:1], in1=sv[:, 4:5], op=ALU.subtract), ["sv"], ["sv"])
            act(lambda e: e.activation(out=sv[:, 5:6], in_=sv[:, 9:10], func=AF.Exp), ["sv"], ["sv"])
            dve(lambda e: e.tensor_tensor(out=sv[:, 9:10], in0=sv[:, 3:4], in1=sv[:, 4:5], op=ALU.subtract), ["sv"], ["sv"])
            act(lambda e: e.activation(out=sv[:, 6:7], in_=sv[:, 9:10], func=AF.Exp), ["sv"], ["sv"])
            act(lambda e: e.activation(out=sv[:, 7:8], in_=sv[:, 4:5], func=AF.Exp, scale=-1.0), ["sv"], ["sv"])
            q_ = qk_s[:, 0, :]
            k_ = qk_s[:, 1, :]
            v_ = A7[:, 2, :]
            b3 = lambda t: t[:, :].rearrange("p (a c) -> p a c", a=64)
            pool(lambda e: e.tensor_tensor(out=b3(big), in0=k_.unsqueeze(2).to_broadcast([128, 64, 64]), in1=v_.unsqueeze(1).to_broadcast([128, 64, 64]), op=ALU.mult), ["qk_s", "A7"], ["big"])
            dve(lambda e: e.tensor_scalar_mul(out=Cs[:], in0=Cs[:], scalar1=sv[:, 6:7]), ["Cs", "sv"], ["Cs"])
            dve(lambda e: e.scalar_tensor_tensor(out=Cs[:], in0=big[:], scalar=sv[:, 5:6], in1=Cs[:], op0=ALU.mult, op1=ALU.add), ["big", "sv", "Cs"], ["Cs"])
            dma("pool", osC, Cs[:], ["Cs"], [], "fin")
            dve(lambda e: e.tensor_scalar_mul(out=sn_t[:], in0=sn_t[:], scalar1=sv[:, 6:7]), ["sn_t", "sv"], ["sn_t"])
            dve(lambda e: e.scalar_tensor_tensor(out=sn_t[:], in0=k_, scalar=sv[:, 5:6], in1=sn_t[:], op0=ALU.mult, op1=ALU.add), ["qk_s", "sv", "sn_t"], ["sn_t"])
            dma("pool", osn, sn_t[:], ["sn_t"], [], "fin")
            pool(lambda e: e.tensor_tensor(out=b3(big), in0=Cs[:, :].rearrange("p (k v) -> p v k", k=64), in1=q_.unsqueeze(1).to_broadcast([128, 64, 64]), op=ALU.mult), ["Cs", "qk_s"], ["big"])
            hs = TT("hs", [128, 2, 64])
            dve(lambda e: e.tensor_reduce(out=hs[:, 0, :], in_=b3(big), axis=AX.X, op=ALU.add), ["big"], ["hs"])
            dve(lambda e: e.tensor_tensor(out=sv[:, 16:80 - 16] if False else big[:, 0:64], in0=q_, in1=sn_t[:], op=ALU.mult), ["qk_s", "sn_t"], ["big"])
            dve(lambda e: e.tensor_reduce(out=sv[:, 8:9], in_=big[:, 0:64], axis=AX.X, op=ALU.add), ["big"], ["sv"])
            dve(lambda e: e.scalar_tensor_tensor(out=sv[:, 9:10], in0=sv[:, 8:9], scalar=-1.0, in1=sv[:, 8:9], op0=ALU.mult, op1=ALU.max), ["sv"], ["sv"])
            dve(lambda e: e.tensor_tensor(out=sv[:, 9:10], in0=sv[:, 9:10], in1=sv[:, 7:8], op=ALU.max), ["sv"], ["sv"])
            dve(lambda e: e.reciprocal(out=sv[:, 10:11], in_=sv[:, 9:10]), ["sv"], ["sv"])
            dve(lambda e: e.tensor_scalar_mul(out=hs[:, 0, :], in0=hs[:, 0, :], scalar1=sv[:, 10:11]), ["hs", "sv"], ["hs"])
            dma("pool", osshift, A7[:, 4:7, :], ["A7"], [], "fin")
            rk3 = TT("rk3", [128, 3, 64])
            mu3 = spk[:, SOFF["mu_r"][0]:SOFF["mu_v"][1]].rearrange("p (a d) -> p a d", a=3)
            dve(lambda e: e.tensor_tensor(out=rk3[:], in0=ssh[:], in1=A7[:, 4:7, :], op=ALU.subtract), ["ssh", "A7"], ["rk3"])
            dve(lambda e: e.tensor_tensor(out=rk3[:], in0=rk3[:], in1=mu3, op=ALU.mult), ["rk3", "spk"], ["rk3"])
            dve(lambda e: e.tensor_tensor(out=rk3[:], in0=rk3[:], in1=A7[:, 4:7, :], op=ALU.add), ["rk3", "A7"], ["rk3"])
            w8 = TT("w8", [128, 8, 64])
            dve(lambda e: e.tensor_tensor(out=w8[:, 0, :], in0=L3[:, 0, :], in1=SP_("w0"), op=ALU.add), ["L3", "spk"], ["w8"])
            act(lambda e: e.activation(out=w8[:, 0, :], in_=w8[:, 0, :], func=AF.Sigmoid), ["w8"], ["w8"])
            act(lambda e: e.activation(out=w8[:, 0, :], in_=w8[:, 0, :], func=AF.Exp, scale=-C0), ["w8"], ["w8"])
            dve(lambda e: e.tensor_tensor(out=w8[:, 1, :], in0=L3[:, 1, :], in1=SP_("a0"), op=ALU.add), ["L3", "spk"], ["w8"])
            act(lambda e: e.activation(out=w8[:, 1, :], in_=w8[:, 1, :], func=AF.Sigmoid), ["w8"], ["w8"])
            dve(lambda e: e.tensor_tensor(out=w8[:, 6, :], in0=rk3[:, 1, :], in1=SP_("kk"), op=ALU.mult), ["rk3", "spk"], ["w8"])
            dve(lambda e: e.tensor_tensor(out=w8[:, 7, :], in0=w8[:, 6, :], in1=w8[:, 6, :], op=ALU.mult), ["w8"], ["w8"])
            dve(lambda e: e.tensor_reduce(out=sv[:, 11:12], in_=w8[:, 7, :], axis=AX.X, op=ALU.add), ["w8"], ["sv"])
            dve(lambda e: e.tensor_scalar_max(out=sv[:, 11:12], in0=sv[:, 11:12], scalar1=1e-24), ["sv"], ["sv"])
            act(lambda e: e.activation(out=sv[:, 11:12], in_=sv[:, 11:12], func=AF.Sqrt), ["sv"], ["sv"])
            dve(lambda e: e.reciprocal(out=sv[:, 11:12], in_=sv[:, 11:12]), ["sv"], ["sv"])
            dve(lambda e: e.tensor_scalar_mul(out=w8[:, 3, :], in0=w8[:, 6, :], scalar1=sv[:, 11:12]), ["w8", "sv"], ["w8"])
            dve(lambda e: e.tensor_tensor(out=w8[:, 6, :], in0=w8[:, 1, :], in1=SP_("ka"), op=ALU.mult), ["w8", "spk"], ["w8"])
            dve(lambda e: e.tensor_tensor(out=w8[:, 6, :], in0=w8[:, 6, :], in1=SP_("ka"), op=ALU.subtract), ["w8", "spk"], ["w8"])
            dve(lambda e: e.tensor_scalar_add(out=w8[:, 6, :], in0=w8[:, 6, :], scalar1=1.0), ["w8"], ["w8"])
            dve(lambda e: e.tensor_tensor(out=w8[:, 4, :], in0=rk3[:, 1, :], in1=w8[:, 6, :], op=ALU.mult), ["rk3", "w8"], ["w8"])
            dve(lambda e: e.tensor_tensor(out=w8[:, 5, :], in0=w8[:, 1, :], in1=w8[:, 3, :], op=ALU.mult), ["w8"], ["w8"])
            dma("sp", Ss[:], sS_d, [], ["Ss"], "sin2")
            bk = lambda ap: ap.unsqueeze(1).to_broadcast([128, 64, 64])
            bv = lambda ap: ap.unsqueeze(2).to_broadcast([128, 64, 64])
            pool(lambda e: e.tensor_tensor(out=b3(big), in0=b3(Ss), in1=bk(w8[:, 3, :]), op=ALU.mult), ["Ss", "w8"], ["big"])
            dve(lambda e: e.tensor_reduce(out=w8[:, 7, :], in_=b3(big), axis=AX.X, op=ALU.add), ["big"], ["w8"])
            dve(lambda e: e.tensor_tensor(out=b3(Ss), in0=b3(Ss), in1=bk(w8[:, 0, :]), op=ALU.mult), ["Ss", "w8"], ["Ss"])
            pool(lambda e: e.tensor_tensor(out=b3(big), in0=bv(w8[:, 7, :]), in1=bk(w8[:, 5, :]), op=ALU.mult), ["w8"], ["big"])
            dve(lambda e: e.tensor_tensor(out=Ss[:], in0=Ss[:], in1=big[:], op=ALU.subtract), ["Ss", "big"], ["Ss"])
            pool(lambda e: e.tensor_tensor(out=b3(big), in0=bv(rk3[:, 2, :]), in1=bk(w8[:, 4, :]), op=ALU.mult), ["rk3", "w8"], ["big"])
            dve(lambda e: e.tensor_tensor(out=Ss[:], in0=Ss[:], in1=big[:], op=ALU.add), ["Ss", "big"], ["Ss"])
            dma("pool", osS, Ss[:], ["Ss"], [], "fin")
            pool(lambda e: e.tensor_tensor(out=b3(big), in0=b3(Ss), in1=bk(rk3[:, 0, :]), op=ALU.mult), ["Ss", "rk3"], ["big"])
            dve(lambda e: e.tensor_reduce(out=hs[:, 1, :], in_=b3(big), axis=AX.X, op=ALU.add), ["big"], ["hs"])
            dve(lambda e: e.tensor_tensor(out=w8[:, 6, :], in0=rk3[:, 0, :], in1=w8[:, 4, :], op=ALU.mult), ["rk3", "w8"], ["w8"])
            dve(lambda e: e.tensor_tensor(out=w8[:, 6, :], in0=w8[:, 6, :], in1=SP_("rk"), op=ALU.mult), ["w8", "spk"], ["w8"])
            dve(lambda e: e.tensor_reduce(out=sv[:, 12:13], in_=w8[:, 6, :], axis=AX.X, op=ALU.add), ["w8"], ["sv"])
            dve(lambda e: e.scalar_tensor_tensor(out=hs[:, 1, :], in0=rk3[:, 2, :], scalar=sv[:, 12:13], in1=hs[:, 1, :], op0=ALU.mult, op1=ALU.add), ["rk3", "sv", "hs"], ["hs"])
            act(lambda e: e.activation(out=w8[:, 6, :], in_=A7[:, 3, :], func=AF.Sigmoid), ["A7"], ["w8"])
            dve(lambda e: e.tensor_tensor(out=hs[:, 0, :], in0=hs[:, 0, :], in1=w8[:, 6, :], op=ALU.mult), ["hs", "w8"], ["hs"])
            dve(lambda e: e.tensor_tensor(out=w8[:, 7, :], in0=hs[:, 0, :], in1=hs[:, 0, :], op=ALU.mult), ["hs"], ["w8"])
            dve(lambda e: e.tensor_reduce(out=sv[:, 13:14], in_=w8[:, 7, :], axis=AX.X, op=ALU.add), ["w8"], ["sv"])
            act(lambda e: e.activation(out=sv[:, 13:14], in_=sv[:, 13:14], func=AF.Sqrt, bias=EPS, scale=1.0 / 64), ["sv"], ["sv"])
            dve(lambda e: e.reciprocal(out=sv[:, 13:14], in_=sv[:, 13:14]), ["sv"], ["sv"])
            dve(lambda e: e.scalar_tensor_tensor(out=hs[:, 0, :], in0=hs[:, 0, :], scalar=sv[:, 13:14], in1=SP_("mnw"), op0=ALU.mult, op1=ALU.mult), ["hs", "sv", "spk"], ["hs"])
            dve(lambda e: e.tensor_reduce(out=sv[:, 14:15], in_=hs[:, 1, :], axis=AX.X, op=ALU.add), ["hs"], ["sv"])
            dve(lambda e: e.tensor_scalar_mul(out=sv[:, 14:15], in0=sv[:, 14:15], scalar1=1.0 / 64), ["sv"], ["sv"])
            dve(lambda e: e.tensor_scalar_sub(out=hs[:, 1, :], in0=hs[:, 1, :], scalar1=sv[:, 14:15]), ["hs", "sv"], ["hs"])
            dve(lambda e: e.tensor_tensor(out=w8[:, 7, :], in0=hs[:, 1, :], in1=hs[:, 1, :], op=ALU.mult), ["hs"], ["w8"])
            dve(lambda e: e.tensor_reduce(out=sv[:, 15:16], in_=w8[:, 7, :], axis=AX.X, op=ALU.add), ["w8"], ["sv"])
            act(lambda e: e.activation(out=sv[:, 15:16], in_=sv[:, 15:16], func=AF.Sqrt, bias=GN_EPS, scale=1.0 / 64), ["sv"], ["sv"])
            dve(lambda e: e.reciprocal(out=sv[:, 15:16], in_=sv[:, 15:16]), ["sv"], ["sv"])
            dve(lambda e: e.scalar_tensor_tensor(out=hs[:, 1, :], in0=hs[:, 1, :], scalar=sv[:, 15:16], in1=SP_("lnw"), op0=ALU.mult, op1=ALU.mult), ["hs", "sv", "spk"], ["hs"])
            dve(lambda e: e.tensor_tensor(out=hs[:, 1, :], in0=hs[:, 1, :], in1=SP_("lnb"), op=ALU.add), ["hs", "spk"], ["hs"])
            dve(lambda e: e.tensor_tensor(out=hs[:, 1, :], in0=hs[:, 1, :], in1=L3[:, 2, :], op=ALU.mult), ["hs", "L3"], ["hs"])
            s3v = nc.dram_tensor("scr3b", [128, 2, 64], F32, kind="Internal").ap()
            dma("pool", s3v, hs[:], ["hs"], ["scr3b"], "scrw3")
            smix = spj[:, 2304:3328].rearrange("p (a h d) -> p a h d", a=2, h=8)
            for a_ in range(2):
                dma("sp", smix[:, a_, :, :], s3v.rearrange("(b h) a d -> b a h d", b=NS)[:, a_, :, :], ["scr3b"], ["smix"], "scrr3")
            act(lambda e: e.copy(out=mix[:NS, :], in_=smix[:].rearrange("p a h d -> p (a h d)")), ["smix"], ["mix"])
            out_proj(NS, mix, "mix", sx, "sx", T, W)
        PP[0].finalize()
        PP[0] = Prog(ctx)
    es_res.close()

    with contextlib.ExitStack() as es2:
        cur[0] = es2
        upb = TT("upb", [128, 8, DFF], BF16)
        dnb = TT("dnb", [128, 32, D], BF16)
        pB2 = TT("pB2", [128, 136], F32)
        nfw = TT("nfw", [128, D])
        identb2 = TT("identb2", [128, 128], BF16)
        wout = TT("wout", [128, 8, D], BF16)
        mixin = TT("mixin", [128, D], BF16)
        for c in range(0, 8, 4):
            dma("pool", wout[:, c:c + 4, :], w_out_v[:, c:c + 4, :], [], ["wout"], "wout")
        up_v = mlp_up.rearrange("(c p) n -> p c n", p=128)
        dn_v = mlp_down.rearrange("(c p) n -> p c n", p=128)
        dma("sp", pB2[:, 0:128], packA_d[:, OFF["ident"][0]:OFF["ident"][1]], [], ["pB2"], "init2", True)
        dma("sp", pB2[:, 128:136], packA_d[:, OFF["nmlp"][0]:OFF["nmlp"][1]], [], ["pB2"], "init2", True)
        dma("sp", nfw[:], nfw_d, [], ["nfw"], "init2", True)
        for g8 in range(8):
            dma("pool", upb[:, :, g8 * 512:(g8 + 1) * 512], up_v[:, :, g8 * 512:(g8 + 1) * 512], [], ["upb%d" % g8], "up%d" % g8)
        for g8 in range(8):
            dma("pool", dnb[:, g8 * 4:(g8 + 1) * 4, :], dn_v[:, g8 * 4:(g8 + 1) * 4, :], [], ["dnb%d" % g8], "dn%d" % g8)
        dve(lambda e: e.tensor_copy(out=identb2[:], in_=pB2[:, 0:128]), ["pB2"], ["identb2"])
        NSUB = 2
        NTT = NSUB * 128
        xsb2 = TT("xsb2", [128, D], BF16)
        xb = [TT("xb%d" % i, [128, NSUB, D]) for i in range(2)]
        st2 = TT("st2", [128, 8])
        xn2 = [TT("xn2T%d" % i, [128, 8, NTT], BF16) for i in range(2)]
        hT = TT("hT", [128, 32, NTT], BF16)
        junk2 = TT("junk2", [128, D], BF16)
        rl = [TT("rl%d" % i, [128, 512]) for i in range(2)]
        nmlp = pB2[:, 128:136]
        xm_v = xp.rearrange("(s p) d -> p s d", p=128)
        yp_v = yp.rearrange("(s p) d -> p s d", p=128)
        sbs = [(sb * NSUB, NSUB, 128) for sb in range(NB // NSUB)] + [(NB, 1, NS)]

        def front(i):
            s0, nsub, nt = sbs[i]
            x4 = xb[i % 2]
            xk = "xb%d" % (i % 2)
            xn2T = xn2[i % 2]
            xnk = "xn2T%d" % (i % 2)
            if nsub == NSUB:
                dma("sp", x4[:], xm_v[:, s0:s0 + NSUB, :], [], [xk], xk)
            else:
                dma("sp", x4[:nt, 0, :], xs, [], [xk], xk)
            for si in range(nsub):
                r0_ = (s0 + si) * 128 if nsub == NSUB else T
                dma("sp", mixin[:nt, :], mix_d[r0_:r0_ + nt, :], [], ["mixin"], "mixin")
                ps, psb, pk = nps()
                for c in range(8):
                    pe(lambda e, c=c, psb=psb: e.transpose(psb[:, c * 128:c * 128 + nt], mixin[:nt, c * 128:(c + 1) * 128], identb2[:nt, :nt]), ["mixin", "identb2"], [pk])
                act(lambda e, psb=psb, si=si: e.copy(out=xn2T[:, :, si * 128:si * 128 + nt], in_=psb[:, 0:1024].rearrange("p (c t) -> p c t", c=8)[:, :, 0:nt]), [pk], [xnk])
                yield
                ps, psb, pk = nps()
                for n in range(2):
                    for c in range(8):
                        mm(ps[:nt, n * 512:(n + 1) * 512], xn2T[:, c, si * 128:si * 128 + nt], wout[:, c, n * 512:(n + 1) * 512], c == 0, c == 7, [xnk, "wout"], [pk])
                for a_ in range(2):
                    dve(lambda e, ps=ps, si=si, a_=a_: e.tensor_tensor(out=x4[:nt, si, a_ * 512:(a_ + 1) * 512], in0=ps[:nt, a_ * 512:(a_ + 1) * 512], in1=x4[:nt, si, a_ * 512:(a_ + 1) * 512], op=ALU.add), [pk, xk], [xk])
                act(lambda e, si=si: e.activation(out=junk2[:nt, :], in_=x4[:nt, si, :], func=AF.Square, accum_out=st2[:nt, 0:1]), [xk], ["junk2", "st2"])
                act(lambda e: e.activation(out=st2[:nt, 1:2], in_=st2[:nt, 0:1], func=AF.Sqrt, bias=EPS, scale=1.0 / D), ["st2"], ["st2"])
                dve(lambda e: e.reciprocal(out=st2[:nt, 2:3], in_=st2[:nt, 1:2]), ["st2"], ["st2"])
                dve(lambda e, si=si: e.tensor_scalar_mul(out=xsb2[:nt, :], in0=x4[:nt, si, :], scalar1=st2[:nt, 2:3]), [xk, "st2"], ["xsb2"])
                yield
                yield
                ps, psb, pk = nps()
                for c in range(8):
                    pe(lambda e, c=c, psb=psb: e.transpose(psb[:, c * 128:c * 128 + nt], xsb2[:nt, c * 128:(c + 1) * 128], identb2[:nt, :nt]), ["xsb2", "identb2"], [pk])
                dve(lambda e, psb=psb, si=si: e.tensor_tensor(out=xn2T[:, :, si * 128:si * 128 + nt], in0=psb[:, 0:1024].rearrange("p (c t) -> p c t", c=8)[:, :, 0:nt],
                                                             in1=nmlp.unsqueeze(2).to_broadcast([128, 8, nt]), op=ALU.mult), [pk, "pB2"], [xnk])
                yield

        def up(i):
            s0, nsub, nt = sbs[i]
            ntt = nsub * nt if nsub == NSUB else nt
            xn2T = xn2[i % 2]
            xnk = "xn2T%d" % (i % 2)
            per = 512 // NTT
            for j2 in range(32 // (2 * per)):
                ps, psb, pk = nps()
                for jj in range(2 * per):
                    j = j2 * 2 * per + jj
                    for c in range(8):
                        mm(ps[:, jj * NTT:jj * NTT + ntt], upb[:, c, j * 128:(j + 1) * 128], xn2T[:, c, 0:ntt], c == 0, c == 7, ["upb%d" % (j // 4), xnk], [pk])
                for bk in range(2):
                    r_ = rl[bk]
                    rk_ = "rl%d" % bk
                    j0 = j2 * 2 * per + bk * per
                    psv = ps[:, bk * 512:(bk + 1) * 512].rearrange("p (j t) -> p j t", j=per)[:, :, 0:ntt]
                    rv = r_[:, :].rearrange("p (j t) -> p j t", j=per)[:, :, 0:ntt]
                    act(lambda e, psv=psv, rv=rv: e.activation(out=rv, in_=psv, func=AF.Relu), [pk], [rk_])
                    if bk == 0:
                        dve(lambda e, rv=rv, j0=j0: e.tensor_tensor(out=hT[:, j0:j0 + per, 0:ntt], in0=rv, in1=rv, op=ALU.mult), [rk_], ["hT"])
                    else:
                        pool(lambda e, rv=rv, j0=j0: e.tensor_tensor(out=hT[:, j0:j0 + per, 0:ntt], in0=rv, in1=rv, op=ALU.mult), [rk_], ["hT"])
                yield

        def down(i):
            s0, nsub, nt = sbs[i]
            x4 = xb[i % 2]
            xk = "xb%d" % (i % 2)
            for si in range(nsub):
                ps, psb, pk = nps()
                for n in range(2):
                    for j in range(32):
                        mm(ps[:nt, n * 512:(n + 1) * 512], hT[:, j, si * 128:si * 128 + nt], dnb[:, j, n * 512:(n + 1) * 512], j == 0, j == 31, ["hT", "dnb%d" % (j // 4)], [pk])
                for a_ in range(2):
                    dve(lambda e, ps=ps, si=si, a_=a_: e.tensor_tensor(out=x4[:nt, si, a_ * 512:(a_ + 1) * 512], in0=ps[:nt, a_ * 512:(a_ + 1) * 512], in1=x4[:nt, si, a_ * 512:(a_ + 1) * 512], op=ALU.add), [pk, xk], [xk])
                act(lambda e, si=si: e.activation(out=junk2[:nt, :], in_=x4[:nt, si, :], func=AF.Square, accum_out=st2[:nt, 4:5]), [xk], ["junk2", "st2"])
                act(lambda e: e.activation(out=st2[:nt, 5:6], in_=st2[:nt, 4:5], func=AF.Sqrt, bias=EPS, scale=1.0 / D), ["st2"], ["st2"])
                dve(lambda e: e.reciprocal(out=st2[:nt, 6:7], in_=st2[:nt, 5:6]), ["st2"], ["st2"])
                dve(lambda e, si=si: e.scalar_tensor_tensor(out=x4[:nt, si, :], in0=x4[:nt, si, :], scalar=st2[:nt, 6:7], in1=nfw[:nt, :], op0=ALU.mult, op1=ALU.mult), [xk, "st2", "nfw"], [xk])
            if nsub == NSUB:
                dma("pool", yp_v[:, s0:s0 + NSUB, :], x4[:], [xk], [], "yo%d" % (i % 2))
            else:
                dma("pool", ys, x4[:nt, 0, :], [xk], [], "yo%d" % (i % 2))

        def run2(gl):
            gl = list(gl)
            while gl:
                for g_ in list(gl):
                    try:
                        next(g_)
                    except StopIteration:
                        gl.remove(g_)

        run2([front(0)])
        for i in range(len(sbs)):
            gl = [up(i)]
            if i + 1 < len(sbs):
                gl.append(front(i + 1))
            run2(gl)
            down(i)
        PP[0].finalize()
    es_ps.close()
    ctx.close()
    return nc


_CACHE = {}


def _host_packs(inp, core):
    f = np.float32
    L = 0
    pa = np.zeros((128, NA), f)

    def put(n, arr):
        a, b = OFF[n]
        pa[:, a:b] = arr

    rep = lambda v: np.broadcast_to(np.asarray(v, f).reshape(1, -1), (128, np.asarray(v).size))
    put("mnw", rep(inp["mlstm_norm_w"][L]))
    put("w0", rep(inp["rw_w0"][L]))
    put("a0", rep(inp["rw_a0"][L]))
    put("kk", rep(inp["rw_k_k"][L]))
    put("ka", rep(inp["rw_k_a"][L]))
    put("rk", rep(inp["rw_r_k"][L].reshape(-1)))
    put("lnw", rep(inp["rw_ln_w"][L]))
    put("lnb", rep(inp["rw_ln_b"][L]))
    put("ifb", rep(np.concatenate([inp["mlstm_i_b"][L], inp["mlstm_f_b"][L]])))
    put("nmw", inp["norm_mix_w"][L].reshape(8, 128).T)
    put("nmlp", inp["norm_mlp_w"][L].reshape(8, 128).T)
    cw = inp["mlstm_conv_w"][L]
    put("cw", cw.reshape(4, 8, 128).transpose(2, 1, 0).reshape(128, 32))
    put("cb", inp["mlstm_conv_b"][L].reshape(8, 128).T)
    put("ident", np.eye(128, dtype=f))
    put("mui", np.triu(np.ones((128, 128), f), 0))
    put("mus", np.triu(np.ones((128, 128), f), 1))
    put("mls", np.tril(np.ones((128, 128), f), -1))
    put("ones", np.ones((128, 128), f))
    lsel = np.zeros((128, 128), f)
    rsel = np.zeros((128, 4), f)
    for h in range(8):
        lsel[h, (h % 2) * 64:(h % 2) * 64 + 64] = 1.0
        rsel[h, h // 2] = 1.0
    put("lsel", lsel)
    put("rsel", rsel)
    return pa


def _sample_pack(inp):
    f = np.float32
    L = 0
    sp = np.zeros((128, NSP), f)

    def bh(v512):
        return np.tile(np.asarray(v512, f).reshape(8, 64), (NS, 1))

    def put(n, arr):
        a, b = SOFF[n]
        sp[:, a:b] = arr

    mu = inp["rw_mu"][L]
    put("mu_r", bh(mu[0:512]))
    put("mu_k", bh(mu[512:1024]))
    put("mu_v", bh(mu[1024:1536]))
    cw = inp["mlstm_conv_w"][L]
    put("cwq", np.concatenate([bh(cw[j, 0:512]) for j in range(4)], axis=1))
    put("cwk", np.concatenate([bh(cw[j, 512:1024]) for j in range(4)], axis=1))
    cb = inp["mlstm_conv_b"][L]
    put("cbq", bh(cb[0:512]))
    put("cbk", bh(cb[512:1024]))
    put("mnw", bh(inp["mlstm_norm_w"][L]))
    put("w0", bh(inp["rw_w0"][L]))
    put("a0", bh(inp["rw_a0"][L]))
    put("kk", bh(inp["rw_k_k"][L]))
    put("ka", bh(inp["rw_k_a"][L]))
    put("rk", bh(inp["rw_r_k"][L].reshape(-1)))
    put("lnw", bh(inp["rw_ln_w"][L]))
    put("lnb", bh(inp["rw_ln_b"][L]))
    put("ib", np.tile(inp["mlstm_i_b"][L].reshape(8, 1), (NS, 1)))
    put("fb", np.tile(inp["mlstm_f_b"][L].reshape(8, 1), (NS, 1)))
    return sp


def kernel(**inp):
    f = np.float32
    inp = {k: np.asarray(v) for k, v in inp.items()}
    if "nc" not in _CACHE:
        _CACHE["nc"] = build_program()
    nc = _CACHE["nc"]
    L = 0
    pa = _host_packs(inp, 0)
    sp = _sample_pack(inp)
    mu = inp["rw_mu"][L]
    luw = np.concatenate([inp["rw_w_up"][L], inp["rw_a_up"][L]], axis=0).astype(f)
    common = {
        "w_in": np.ascontiguousarray(inp["w_in"][L], f),
        "w_out": np.ascontiguousarray(inp["w_out"][L], f),
        "mlp_up": np.ascontiguousarray(inp["mlp_up"][L], f),
        "mlp_down": np.ascontiguousarray(inp["mlp_down"][L], f),
        "packA": pa,
        "mu_b": np.ascontiguousarray(np.broadcast_to(mu.reshape(1, -1), (128, RWW)), f),
        "nfw_b": np.ascontiguousarray(np.broadcast_to(inp["norm_f_w"].reshape(1, -1), (128, D)), f),
        "luw": luw,
        "gup": np.ascontiguousarray(inp["rw_g_up"][L], f),
        "spack": sp,
        "mul": np.ascontiguousarray(np.broadcast_to(mu[1536:1792].reshape(1, -1), (NS, 256)), f),
    }
    in_maps = []
    for c in range(8):
        rs = slice(c * NS, (c + 1) * NS)
        m = dict(common)
        m["xp"] = np.ascontiguousarray(inp["x_prompt"][c], f)
        m["xs"] = np.ascontiguousarray(inp["x_sample"][rs, 0, :], f)
        m["sC"] = np.ascontiguousarray(inp["state_mlstm_C"][L, rs].reshape(128, 4096), f)
        m["sn"] = np.ascontiguousarray(inp["state_mlstm_n"][L, rs].reshape(128, 64), f)
        m["sm"] = np.ascontiguousarray(inp["state_mlstm_m"][L, rs].reshape(128, 1), f)
        cv = inp["state_mlstm_conv"][L, rs]
        m["sconv"] = np.ascontiguousarray(cv.reshape(NS, 3, 2, 8, 64).transpose(0, 3, 2, 1, 4).reshape(128, 2, 3, 64), f)
        m["sS"] = np.ascontiguousarray(inp["state_rwkv_S"][L, rs].reshape(128, 4096), f)
        sh = inp["state_rwkv_shift"][L, rs, 0, :]
        m["sshift"] = np.ascontiguousarray(sh[:, 0:1536].reshape(NS, 3, 8, 64).transpose(0, 2, 1, 3).reshape(128, 3, 64), f)
        m["sshl"] = np.ascontiguousarray(sh[:, 1536:1792], f)
        in_maps.append(m)
    res = run_bass_kernel_spmd(nc, in_maps, core_ids=list(range(8)))
    R = res.results
    y_prompt = np.stack([R[c]["yp"] for c in range(8)]).astype(f)
    y_sample = np.concatenate([R[c]["ys"] for c in range(8)], axis=0).reshape(128, 1, D).astype(f)
    pC = np.zeros((1, 8, 8, 64, 64), f)
    pn = np.zeros((1, 8, 8, 64), f)
    pm = np.zeros((1, 8, 8), f)
    pconv = np.zeros((1, 8, 3, 1024), f)
    pS = np.zeros((1, 8, 8, 64, 64), f)
    pshift = np.zeros((1, 8, 1, RWW), f)
    for c in range(8):
        oC = R[c]["oC"].reshape(2, 64, 4, 65)
        Ch = oC.transpose(2, 0, 1, 3).reshape(8, 64, 65)
        pC[0, c] = Ch[:, :, 0:64]
        pn[0, c] = Ch[:, :, 64]
        pm[0, c] = R[c]["om"].reshape(8)
        pconv[0, c] = R[c]["oconv"].transpose(2, 1, 0).reshape(3, 1024)
        oS = R[c]["oS"].reshape(2, 64, 4, 64)
        pS[0, c] = oS.transpose(2, 0, 3, 1).reshape(8, 64, 64)
        pshift[0, c, 0] = R[c]["oshift"].reshape(RWW)
    sC = np.concatenate([R[c]["osC"].reshape(NS, 8, 64, 64) for c in range(8)])[None].astype(f)
    sn = np.concatenate([R[c]["osn"].reshape(NS, 8, 64) for c in range(8)])[None].astype(f)
    sm = np.concatenate([R[c]["osm"].reshape(NS, 8) for c in range(8)])[None].astype(f)
    sconv = np.concatenate([R[c]["osconv"].reshape(NS, 8, 2, 3, 64).transpose(0, 3, 2, 1, 4).reshape(NS, 3, 1024) for c in range(8)])[None].astype(f)
    sS = np.concatenate([R[c]["osS"].reshape(NS, 8, 64, 64) for c in range(8)])[None].astype(f)
    sshift = np.concatenate([
        np.concatenate([R[c]["osshift"].reshape(NS, 8, 3, 64).transpose(0, 2, 1, 3).reshape(NS, 1536), R[c]["osshl"]], axis=1)
        for c in range(8)]).reshape(1, 128, 1, RWW).astype(f)
    return (y_prompt, y_sample, pC, pn, pm, pconv, pS, pshift, sC, sn, sm, sconv, sS, sshift)
```

```python
import contextlib
import numpy as np
import concourse.bass as bass
import concourse.mybir as mybir
from concourse.bass_utils import run_bass_kernel_spmd

F32 = mybir.dt.float32
BF16 = mybir.dt.bfloat16
AF = mybir.ActivationFunctionType
ALU = mybir.AluOpType
AX = mybir.AxisListType

D = 1024
T = 2048
NB = 16
NS = 16
INW = 3856
MLW = 2064
RWW = 1792
DFF = 4096
EPS = 1e-6
GN_EPS = 64e-5
C0 = 0.6065306597126334

OFF = {}
_o = 0
for _n, _w in [("mnw", 512), ("w0", 512), ("a0", 512), ("kk", 512), ("ka", 512), ("rk", 512),
               ("lnw", 512), ("lnb", 512), ("ifb", 16), ("nmw", 8), ("nmlp", 8), ("cw", 32), ("cb", 8),
               ("ident", 128), ("mui", 128), ("mus", 128), ("mls", 128), ("ones", 128),
               ("lsel", 128), ("rsel", 4)]:
    OFF[_n] = (_o, _o + _w)
    _o += _w
NA = _o
SOFF = {}
_o = 0
for _n, _w in [("mu_r", 64), ("mu_k", 64), ("mu_v", 64), ("cwq", 256), ("cwk", 256), ("cbq", 64), ("cbk", 64),
               ("mnw", 64), ("w0", 64), ("a0", 64), ("kk", 64), ("ka", 64), ("rk", 64), ("lnw", 64), ("lnb", 64),
               ("ib", 1), ("fb", 1)]:
    SOFF[_n] = (_o, _o + _w)
    _o += _w
NSP = _o


ALIAS = {"r_sb0": "G0", "kf_sb0": "G1", "vf0": "G2", "osig0": "G13", "r_sb1": "G20", "kf_sb1": "G21", "vf1": "G22", "osig1": "G23",
         "wsig": "G3", "a_sb": "G4", "g_sb0": "G5", "g_sb1": "G24", "kap": "G6", "ktl": "G7",
         "bvec": "G8", "e1": "G9", "e2": "G10", "e3": "G11", "pcw": "G12", "ysb": "G11", "tA": "G9", "tB": "G10", "plast": "G16",
         "hml": "G14", "nlrep": "G15", "Fb": "G15", "cacc": "G16", "qks": "G16", "tAm": "G18", "tBm": "G19",
         "junk": "xm", "ctmp": "xm", "mixT": "TMbA", "TMb0": "TMbA", "TMb1": "TMbA",
         "TMb2": "TMbB", "TMb3": "TMbB", "Ub": "Zb1", "Ss": "Cs", "lo3": "spj", "smix": "spj",
         "xt1": "xt0", "xnT1": "xnT0", "qkx1": "qkx0"}
PARKEYS = {"g_sb", "r_sb", "kf_sb", "vf", "osig", "vaug", "gt", "qpb", "kTb", "ktm", "vrw", "lor"}
CURP = [0]


class SemCtx:
    def __init__(self, nc):
        self.nc = nc
        self.es = contextlib.ExitStack()
        self.engs = ["pe", "act", "dve", "pool", "sp"]
        self.esem = {e: self.es.enter_context(nc.semaphore("s_" + e)) for e in self.engs}
        self.ecnt = {e: 0 for e in self.engs}
        self.bsem = self.es.enter_context(nc.semaphore("s_bar"))
        self.phase = 0
        self.gsem = {}
        self.gbase = {}

    def group_sem(self, g):
        if g not in self.gsem:
            self.gsem[g] = self.es.enter_context(self.nc.semaphore("g_%d" % len(self.gsem)))
            self.gbase[g] = 0
        return self.gsem[g]

    def close(self):
        self.es.close()


class Prog:
    max_ops = None

    def __init__(self, ctx):
        self.ctx = ctx
        self.nc = ctx.nc
        self.ops = []
        self.last_writer = {}
        self.readers = {}
        self.dma_groups = {}

    def op(self, eng, fn, reads=(), writes=(), dma_group=None, wait_total=False):
        if self.max_ops is not None and len(self.ops) >= self.max_ops:
            return None
        reads = [(k + str(CURP[0])) if k in PARKEYS else k for k in reads]
        writes = [(k + str(CURP[0])) if k in PARKEYS else k for k in writes]
        reads = [ALIAS.get(k, k) for k in reads]
        writes = [ALIAS.get(k, k) for k in writes]
        _x = lambda ks: [kk for k in ks for kk in ((k + "a", k + "b") if (len(k) == 3 and k.startswith("PS")) else (k,))]
        reads, writes = _x(reads), _x(writes)
        if eng != "pe":
            writes = writes + [k for k in reads if k.startswith("PS") and k not in writes]
        deps = set()
        raw = set()
        for b in reads:
            if b in self.last_writer:
                deps.add(self.last_writer[b])
                raw.add(self.last_writer[b])
        for b in writes:
            if b in self.last_writer:
                deps.add(self.last_writer[b])
            for r in self.readers.get(b, ()):
                deps.add(r)
        idx = len(self.ops)
        if dma_group is not None:
            deps = {d for d in deps if self.ops[d]["dma"] != dma_group}
        o = dict(eng=eng, fn=fn, deps=sorted(deps), dma=dma_group, idx=idx, raw=raw)
        if dma_group is not None:
            g = self.dma_groups.setdefault(dma_group, dict(total=0, wait_total=wait_total))
            g["total"] += 1
            o["dma_cnt"] = g["total"]
        self.ops.append(o)
        for b in reads:
            self.readers.setdefault(b, []).append(idx)
        for b in writes:
            self.last_writer[b] = idx
            self.readers[b] = []
        return idx

    def finalize(self):
        nc = self.nc
        ctx = self.ctx
        ops = self.ops
        needed = set()
        for o in ops:
            best = {}
            bestg = {}
            rd = []
            for d in o["deps"]:
                p = ops[d]
                if p["dma"] is not None:
                    bestg[p["dma"]] = max(bestg.get(p["dma"], -1), d)
                else:
                    if p["eng"] == "pe" and o["eng"] == "pe" and o["dma"] is None:
                        continue
                    if p["eng"] == o["eng"] and o["dma"] is None and p["eng"] in ("act", "dve") and d not in o["raw"]:
                        continue
                    best[p["eng"]] = max(best.get(p["eng"], -1), d)
            rd.extend(best.values())
            rd.extend(bestg.values())
            o["deps"] = sorted(rd)
            for d in best.values():
                needed.add(d)
        engs = ctx.engs
        last = {}
        for o in ops:
            if o["dma"] is None:
                last[o["eng"]] = o["idx"]
        needed |= set(last.values())
        cnt = dict(ctx.ecnt)
        for o in ops:
            if o["dma"] is None and o["idx"] in needed:
                cnt[o["eng"]] += 1
                o["sig"] = cnt[o["eng"]]
        for g in self.dma_groups:
            ctx.group_sem(g)
        phase = ctx.phase
        with nc.Block() as block:

            def emit_engine(ename, eng):
                known = {}
                if phase > 0:
                    eng.wait_ge(ctx.bsem, phase)
                for o in ops:
                    if o["eng"] != ename:
                        continue
                    for d in o["deps"]:
                        p = ops[d]
                        if p["dma"] is not None:
                            g = self.dma_groups[p["dma"]]
                            sem = ctx.gsem[p["dma"]]
                            val = ctx.gbase[p["dma"]] + 16 * (g["total"] if g["wait_total"] else p["dma_cnt"])
                            key = ("g", p["dma"])
                        else:
                            if p["eng"] == "pe" and ename == "pe" and o["dma"] is None:
                                continue
                            sem = ctx.esem[p["eng"]]
                            val = p["sig"]
                            key = ("e", p["eng"])
                        if known.get(key, 0) >= val:
                            continue
                        known[key] = val
                        eng.wait_ge(sem, val)
                    ins = o["fn"](eng)
                    if o["dma"] is not None:
                        ins.then_inc(ctx.gsem[o["dma"]], 16)
                    elif "sig" in o:
                        ins.then_inc(ctx.esem[ename], 1)
                if ename == "sp":
                    for e2 in engs:
                        if cnt[e2] > ctx.ecnt[e2]:
                            eng.wait_ge(ctx.esem[e2], cnt[e2])
                    for g, info in self.dma_groups.items():
                        eng.wait_ge(ctx.gsem[g], ctx.gbase[g] + 16 * info["total"])
                    eng.sem_inc(ctx.bsem, 1)

            @block.tensor
            def _(e):
                emit_engine("pe", e)

            @block.scalar
            def _(e):
                emit_engine("act", e)

            @block.vector
            def _(e):
                emit_engine("dve", e)

            @block.gpsimd
            def _(e):
                emit_engine("pool", e)

            @block.sync
            def _(e):
                emit_engine("sp", e)

        ctx.ecnt = cnt
        for g, info in self.dma_groups.items():
            ctx.gbase[g] += 16 * info["total"]
        ctx.phase += 1


STOP_EARLY = True


class _StopBuild(Exception):
    pass


def build_program(do_sample=True, debug=False):
    nc = bass.Bass("TRN2", target_bir_lowering=False)
    try:
        return _build_program(nc, do_sample, debug)
    except _StopBuild:
        return nc


def _build_program(nc, do_sample, debug):
    dbg = nc.dram_tensor("dbg", [128, 16, 512], F32, kind="ExternalOutput").ap() if debug else None
    din = lambda n, s: nc.dram_tensor(n, s, F32, kind="ExternalInput").ap()
    dout = lambda n, s: nc.dram_tensor(n, s, F32, kind="ExternalOutput").ap()
    xp = din("xp", [T, D])
    xs = din("xs", [NS, D])
    w_in = din("w_in", [D, INW])
    w_out = din("w_out", [D, D])
    mlp_up = din("mlp_up", [D, DFF])
    mlp_down = din("mlp_down", [DFF, D])
    packA_d = din("packA", [128, NA])
    mu_d = din("mu_b", [128, RWW])
    nfw_d = din("nfw_b", [128, D])
    wup_d = din("luw", [128, 512])
    gup_d = din("gup", [128, 512])
    spk_d = din("spack", [128, NSP])
    sC_d = din("sC", [128, 4096])
    sn_d = din("sn", [128, 64])
    sm_d = din("sm", [128, 1])
    sconv_d = din("sconv", [128, 2, 3, 64])
    sS_d = din("sS", [128, 4096])
    sshift_d = din("sshift", [128, 3, 64])
    sshl_d = din("sshl", [NS, 256])
    mul_d = din("mul", [NS, 256])

    yp = dout("yp", [T, D])
    ys = dout("ys", [NS, D])
    oC = dout("oC", [128, 4, 65])
    om = dout("om", [8, 1])
    oconv = dout("oconv", [128, 8, 3])
    oS = dout("oS", [128, 4, 64])
    oshift = dout("oshift", [1, RWW])
    osC = dout("osC", [128, 4096])
    osn = dout("osn", [128, 64])
    osm = dout("osm", [128, 1])
    osconv = dout("osconv", [128, 2, 3, 64])
    osS = dout("osS", [128, 4096])
    osshift = dout("osshift", [128, 3, 64])
    osshl = dout("osshl", [NS, 256])

    mix_d = nc.dram_tensor("mix_scr", [T + NS, D], BF16, kind="Internal").ap()
    scr1 = nc.dram_tensor("scr1", [128, 8, 64], F32, kind="Internal").ap()
    scr2 = nc.dram_tensor("scr2", [128, 3, 64], F32, kind="Internal").ap()
    scr3 = nc.dram_tensor("scr3", [NS, 2, 8, 64], F32, kind="Internal").ap()

    ctx = SemCtx(nc)
    PP = [Prog(ctx)]
    es_res = contextlib.ExitStack()
    cur = [es_res]

    def TT(name, shape, dt=F32):
        return cur[0].enter_context(nc.sbuf_tensor("t_" + name, list(shape), dt))

    def dma(q, out, in_, reads, writes, group, wait_total=False):
        group = "%s@%s" % (group, q)
        PP[0].op(q, lambda e: e.dma_start(out=out, in_=in_), reads=reads, writes=writes, dma_group=group, wait_total=wait_total)

    def dve(fn, r, w):
        PP[0].op("dve", fn, reads=r, writes=w)

    def act(fn, r, w):
        PP[0].op("act", fn, reads=r, writes=w)

    def pool(fn, r, w):
        PP[0].op("pool", fn, reads=r, writes=w)

    def pe(fn, r, w):
        PP[0].op("pe", fn, reads=r, writes=w)

    def mm(out, lhsT, rhs, start, stop, r, w):
        pe(lambda e: e.matmul(out, lhsT=lhsT, rhs=rhs, start=start, stop=stop), r, w)

    es_ps = contextlib.ExitStack()
    PS = [es_ps.enter_context(nc.psum_tensor("PS%d" % i, [128, 1024], F32)) for i in range(4)]
    PSB = [p.bitcast(BF16) for p in PS]
    bank = [0]

    def nps():
        if bank[0] % 2:
            bank[0] += 1
        i = (bank[0] // 2) % 4
        bank[0] += 2
        return PS[i], PSB[i], "PS%d" % i

    def nps1():
        i = (bank[0] // 2) % 4
        h = bank[0] % 2
        bank[0] += 1
        return PS[i][:, h * 512:(h + 1) * 512], PSB[i][:, h * 1024:(h + 1) * 1024], "PS%d%s" % (i, "ab"[h])

    wq = TT("wq", [128, 8, INW], BF16)
    mub = TT("mub", [128, RWW], F32)
    luw = TT("luw", [128, 512], BF16)
    gup = TT("gup", [128, 512], BF16)
    pA = TT("pA", [128, NA], F32)
    identb = TT("identb", [128, 128], BF16)

    def PA(n):
        a, b = OFF[n]
        return pA[:, a:b]

    w_in_v = w_in.rearrange("(c p) n -> p c n", p=128)
    dma("sp", pA[:], packA_d, [], ["pA"], "init", True)
    dma("pool", luw[:], wup_d, [], ["luw"], "init", True)
    dma("pool", gup[:], gup_d, [], ["gup"], "init", True)
    w_out_v = w_out.rearrange("(c p) n -> p c n", p=128)
    dve(lambda e: e.tensor_copy(out=identb[:], in_=PA("ident")), ["pA"], ["identb"])


    def _dbgdump(tag):
        if debug != tag:
            return
        dstg_ = cur[0].enter_context(nc.sbuf_tensor("t_dbgst%d" % tag, [128, 512], F32))
        def dd(slot, ap, key, n):
            dve(lambda e: e.tensor_copy(out=dstg_[:, 0:n], in_=ap), [key], ["dbgst"])
            dma("sp", dbg[:, slot, 0:n], dstg_[:, 0:n], ["dbgst"], [], "dbg")
        dd(0, PA("ident"), "pA", 128)
        dd(1, PA("mui"), "pA", 128)
        dd(2, PA("mnw"), "pA", 512)
        dd(3, PA("w0"), "pA", 512)
        dd(4, PA("lnb"), "pA", 512)
        PP[0].max_ops = len(PP[0].ops)
        PP[0].finalize()
        raise _StopBuild()
    _dbgdump(3)
    dma("sp", mub[:], mu_d, [], ["mub"], "init", True)
    PP[0].finalize()
    PP[0] = Prog(ctx)

    def rmsnorm_T(xt, xk, nt, dstT, dstk, col0, wname, tmpb, tmpbk, junk, junkk, st, stk):
        act(lambda e: e.activation(out=junk[:nt, :], in_=xt[:nt, :], func=AF.Square, accum_out=st[:nt, 0:1]), [xk], [junkk, stk])
        act(lambda e: e.activation(out=st[:nt, 1:2], in_=st[:nt, 0:1], func=AF.Ln, bias=EPS, scale=1.0 / D), [stk], [stk])
        act(lambda e: e.activation(out=st[:nt, 2:3], in_=st[:nt, 1:2], func=AF.Exp, scale=-0.5), [stk], [stk])
        dve(lambda e: e.tensor_scalar_mul(out=tmpb[:nt, :], in0=xt[:nt, :], scalar1=st[:nt, 2:3]), [xk, stk], [tmpbk])
        ps, psb, pk = nps1()
        for c in range(8):
            pe(lambda e, c=c: e.transpose(psb[:, c * 128:c * 128 + nt], tmpb[:nt, c * 128:(c + 1) * 128], identb[:nt, :nt]), [tmpbk, "identb"], [pk])
        a, b_ = OFF[wname]
        dve(lambda e: e.tensor_tensor(out=dstT[:, :, col0:col0 + nt],
                                      in0=psb[:, 0:1024].rearrange("p (c t) -> p c t", c=8)[:, :, 0:nt],
                                      in1=pA[:, a:b_].unsqueeze(2).to_broadcast([128, 8, nt]), op=ALU.mult), [pk, "pA"], [dstk])

    def head_ml(nt, hsrc, hk, osig, ok, mix, mixk, W, sfx=""):
        tA, tB, s8 = W["tA"], W["tB"], W["s8"]
        h3 = lambda t: t[:nt, :].rearrange("p (h d) -> p h d", h=8)
        bc = lambda t, c: t[:nt, c:c + 8].unsqueeze(2).to_broadcast([nt, 8, 64])
        dve(lambda e: e.tensor_tensor(out=tA[:nt, :], in0=hsrc[:nt, :], in1=osig[:nt, :], op=ALU.mult), [hk, ok], ["tA" + sfx])
        dve(lambda e: e.tensor_tensor(out=tB[:nt, :], in0=tA[:nt, :], in1=tA[:nt, :], op=ALU.mult), ["tA" + sfx], ["tB" + sfx])
        dve(lambda e: e.tensor_reduce(out=s8[:nt, 0:8], in_=h3(tB), axis=AX.X, op=ALU.add), ["tB" + sfx], ["s8" + sfx])
        act(lambda e: e.activation(out=s8[:nt, 8:16], in_=s8[:nt, 0:8], func=AF.Ln, bias=EPS, scale=1.0 / 64), ["s8" + sfx], ["s8" + sfx])
        act(lambda e: e.activation(out=s8[:nt, 16:24], in_=s8[:nt, 8:16], func=AF.Exp, scale=-0.5), ["s8" + sfx], ["s8" + sfx])
        dve(lambda e: e.tensor_tensor(out=h3(tB), in0=h3(tA), in1=bc(s8, 16), op=ALU.mult), ["tA" + sfx, "s8" + sfx], ["tB" + sfx])
        dve(lambda e: e.tensor_tensor(out=mix[:nt, 0:512], in0=tB[:nt, :], in1=PA("mnw")[:nt, :], op=ALU.mult), ["tB" + sfx, "pA"], [mixk])

    def head_rw(nt, ysrc, yk, bon, bonk, vf, vfk, g, gk, mix, mixk, W):
        tA, tB, s8 = W["tA"], W["tB"], W["s8"]
        h3 = lambda t: t[:nt, :].rearrange("p (h d) -> p h d", h=8)
        bc = lambda t, c: t[:nt, c:c + 8].unsqueeze(2).to_broadcast([nt, 8, 64])
        pool(lambda e: e.tensor_tensor(out=h3(tA), in0=h3(vf), in1=bc(bon, 0), op=ALU.mult), [vfk, bonk], ["tAm"])
        pool(lambda e: e.tensor_tensor(out=tA[:nt, :], in0=tA[:nt, :], in1=ysrc[:nt, :], op=ALU.add), ["tAm", yk], ["tAm"])
        dve(lambda e: e.tensor_reduce(out=s8[:nt, 24:32], in_=h3(tA), axis=AX.X, op=ALU.add), ["tAm"], ["s8"])
        pool(lambda e: e.tensor_scalar_mul(out=s8[:nt, 24:32], in0=s8[:nt, 24:32], scalar1=1.0 / 64), ["s8"], ["s8"])
        pool(lambda e: e.tensor_tensor(out=h3(tA), in0=h3(tA), in1=bc(s8, 24), op=ALU.subtract), ["tAm", "s8"], ["tAm"])
        pool(lambda e: e.tensor_tensor(out=tB[:nt, :], in0=tA[:nt, :], in1=tA[:nt, :], op=ALU.mult), ["tAm"], ["tBm"])
        dve(lambda e: e.tensor_reduce(out=s8[:nt, 32:40], in_=h3(tB), axis=AX.X, op=ALU.add), ["tBm"], ["s8"])
        act(lambda e: e.activation(out=s8[:nt, 40:48], in_=s8[:nt, 32:40], func=AF.Ln, bias=GN_EPS, scale=1.0 / 64), ["s8"], ["s8"])
        act(lambda e: e.activation(out=s8[:nt, 48:56], in_=s8[:nt, 40:48], func=AF.Exp, scale=-0.5), ["s8"], ["s8"])
        pool(lambda e: e.tensor_tensor(out=h3(tB), in0=h3(tA), in1=bc(s8, 48), op=ALU.mult), ["tAm", "s8"], ["tBm"])
        pool(lambda e: e.tensor_tensor(out=tB[:nt, :], in0=tB[:nt, :], in1=PA("lnw")[:nt, :], op=ALU.mult), ["tBm", "pA"], ["tBm"])
        pool(lambda e: e.tensor_tensor(out=tB[:nt, :], in0=tB[:nt, :], in1=PA("lnb")[:nt, :], op=ALU.add), ["tBm", "pA"], ["tBm"])
        pool(lambda e: e.tensor_tensor(out=mix[:nt, 512:1024], in0=tB[:nt, :], in1=g[:nt, :], op=ALU.mult), ["tBm", gk], [mixk])

    def out_proj(nt, mix, mixk, xt, xk, row0, W):
        dma("pool", mix_d[row0:row0 + nt, :], mix[:nt, :], [mixk], [], "mixst")

    with contextlib.ExitStack() as es1:
        cur[0] = es1
        W = {}
        Gbig = TT("Gbig", [128, 25, 512])
        G = [Gbig[:, i, :] for i in range(25)]
        W["s8"] = TT("s8", [128, 64])
        W["xm"] = TT("xm", [128, D])
        xt = [TT("xt0", [128, D])] * 2
        junk = W["xm"]
        st = TT("st", [128, 4])
        mix = TT("mix", [128, D], BF16)
        xsb = TT("xsb", [128, D], BF16)
        xnT = [TT("xnT0", [128, 8, 129], BF16)] * 2
        dxT = TT("dxT", [128, 8, 128], BF16)
        qkx = [TT("qkx0", [128, 8, 131])] * 2
        cacc = Gbig[:, 16:18, :].rearrange("p a (c t) -> p (a c) t", t=128)
        ctmp = W["xm"][:, :].rearrange("p (c t) -> p c t", c=8)
        qks = cacc
        QPB = [TT("qpb%d" % i, [128, 4, 128], BF16) for i in range(2)]
        KTB = [TT("kTb%d" % i, [128, 4, 128], BF16) for i in range(2)]
        KTM = [TT("ktm%d" % i, [128, 8, 64], BF16) for i in range(2)]
        VAUG = [TT("vaug%d" % i, [128, 8, 65], BF16) for i in range(2)]
        GT = [TT("gt%d" % i, [128, 96]) for i in range(2)]
        runmax = TT("runmax", [128, 8])
        nBc = TT("nBc", [128, 8])
        Cst = TT("Cst", [128, 4, 65])
        Cbf = TT("Cbf", [128, 4, 65], BF16)
        Fb = G[15].rearrange("p (j t) -> p j t", j=4)
        hml = G[14]
        nlrep = G[15].rearrange("p (h d) -> p h d", h=8)
        wsig, a_sb, g_sb, kap, ktl, bvec, e1, e2, e3, pcw, ysb = G[3], G[4], G[5], G[6], G[7], G[8], G[9], G[10], G[11], G[12], G[11]
        W["tA"], W["tB"] = G[9], G[10]
        WM = {"tA": G[18], "tB": G[19], "s8": TT("s8m", [128, 64])}
        r8b = TT("r8b", [128, 8])
        VRW = [TT("vrw%d" % i, [128, 8, 64], BF16) for i in range(2)]
        LOR = [TT("lor%d" % i, [128, 256], BF16) for i in range(2)]
        lorT = TT("lorT", [128, 2, 128], BF16)
        r8 = TT("r8", [128, 32])
        bon = TT("bon", [128, 8])
        TMb = TT("TMb", [128, 4, 512], BF16)
        W["mixT"] = TMb[:, 0:2, :].rearrange("p a (c t) -> p (a c) t", t=128)
        Btz = TT("Btz", [128, 8, 128], BF16)
        Ktz = TT("Ktz", [128, 8, 128], BF16)
        FMt = TT("FMt", [128, 4, 4, 128], BF16)
        Am = [TT("Am%d" % i, [128, 8, 128], BF16) for i in range(3)]
        PTb = TT("PTb", [128, 8, 128], BF16)
        Pw = [TT("Pw%d" % i, [128, 8, 128], BF16) for i in range(4)]
        Zb = [TT("Zb%d" % i, [128, 8, 64], BF16) for i in range(2)]
        Ub = Zb[1]
        Sst = TT("Sst", [128, 4, 64])
        Sbf = TT("Sbf", [128, 4, 64], BF16)
        WLfm = TT("WLfm", [128, 4])
        plast = G[17]

        def wqk(c0, c1):
            return ["wq%d" % g for g in range(c0 // 512, (c1 - 1) // 512 + 1)]

        for g in range(8):
            c0, c1 = g * 512, min(INW, (g + 1) * 512)
            dma("pool", wq[:, :, c0:c1], w_in_v[:, :, c0:c1], [], ["wq%d" % g], "wq%d" % g)
        for p_ in range(2):
            pool(lambda e, p_=p_: e.memset(VAUG[p_][:], 1.0), [], ["vaug%d" % p_])
        pool(lambda e: e.memset(Btz[:], 0.0), [], ["Btz"])
        pool(lambda e: e.memset(Ktz[:], 0.0), [], ["Ktz"])
        pool(lambda e: e.memset(Cst[:], 0.0), [], ["Cst"])
        pool(lambda e: e.memset(Cbf[:], 0.0), [], ["Cbf"])
        pool(lambda e: e.memset(Sst[:], 0.0), [], ["Sst"])
        pool(lambda e: e.memset(Sbf[:], 0.0), [], ["Sbf"])
        pool(lambda e: e.memset(runmax[:], -1e30), [], ["runmax"])
        pool(lambda e: e.memset(nBc[:], 0.0), [], ["nBc"])
        pool(lambda e: e.memset(xnT[0][:, :, 0:1], 0.0), [], ["xnT0"])
        pool(lambda e: e.memset(qkx[0][:, :, 0:3], 0.0), [], ["qkx0"])

        if debug == 2:
            dstg = TT("dbgstage", [128, 512]) if False else G[12]
            def ddump0(slot, ap, key, n):
                dve(lambda e: e.tensor_copy(out=dstg[:, 0:n], in_=ap), [key], ["pcw"])
                dma("sp", dbg[:, slot, 0:n], dstg[:, 0:n], ["pcw"], [], "dbg")
            ddump0(0, PA("ident"), "pA", 128)
            ddump0(1, PA("mui"), "pA", 128)
            ddump0(2, luw[:, :], "luw", 512)
            ddump0(3, gup[:, :], "gup", 512)
            ddump0(4, W1[:, 0, 0:512], "W1", 512)
            PP[0].max_ops = len(PP[0].ops)
            if STOP_EARLY:
                PP[0].finalize()
                raise _StopBuild()
        MUI = PA("mui")
        MUS = PA("mus")
        MLS = PA("mls")
        ONES = PA("ones")
        IDF = PA("ident")
        bc8 = lambda ap: ap.unsqueeze(2).to_broadcast([128, 8, 64])
        m8 = lambda m: m.unsqueeze(1).to_broadcast([128, 8, 128])
        v3 = lambda t: t[:].rearrange("p (h d) -> p h d", h=8)
        hoff = lambda h: (h % 2) * 512 + (h // 2) * 128

        def make_block(b):
            p_ = b % 2
            r_sb, kf_sb, vf, osig = G[0 + 20 * p_] if p_ == 0 else G[20], G[1] if p_ == 0 else G[21], G[2] if p_ == 0 else G[22], G[13] if p_ == 0 else G[23]
            vaug, gt, qpb, kTb, ktm, vrw, lor = VAUG[p_], GT[p_], QPB[p_], KTB[p_], KTM[p_], VRW[p_], LOR[p_]
            g_sb = G[5] if p_ == 0 else G[24]

            def front_stage():
                x_ = xt[b % 2]
                xk = "xt%d" % (b % 2)
                xn = xnT[b % 2]
                xnk = "xnT%d" % (b % 2)
                qx = qkx[b % 2]
                qxk = "qkx%d" % (b % 2)
                dma("sp", x_[:], xp[b * 128:(b + 1) * 128, :], [], [xk], xk)
                rmsnorm_T(x_, xk, 128, xn, xnk, 1, "nmw", xsb, "xsb", junk, "junk", st, "st")
                cur_x = xn[:, :, 1:129]
                prv_x = xn[:, :, 0:128]
                dve(lambda e, xn=xn: e.tensor_tensor(out=dxT[:], in0=xn[:, :, 0:128], in1=xn[:, :, 1:129], op=ALU.subtract), [xnk], ["dxT"])

                yield
                ps, psb, pk = nps()
                for j in range(8):
                    for c in range(8):
                        mm(ps[:, j * 128:(j + 1) * 128], wq[:, c, j * 128:(j + 1) * 128], cur_x[:, c, :], c == 0, c == 7, wqk(j * 128, (j + 1) * 128) + [xnk], [pk])
                for a_ in range(2):
                    act(lambda e, ps=ps, qx=qx, a_=a_: e.copy(out=qx[:, 4 * a_:4 * a_ + 4, 3:131], in_=ps[:, a_ * 512:(a_ + 1) * 512].rearrange("p (j t) -> p j t", j=4)), [pk], [qxk])
                if b == NB - 1:
                    dma("pool", oconv, qx[:, :, 128:131], [qxk], [], "fin")

                yield
                def tm_plain(col0, ncol, ps_ap, pk):
                    for c in range(8):
                        mm(ps_ap, cur_x[:, c, :], wq[:, c, col0:col0 + ncol], c == 0, c == 7, [xnk] + wqk(col0, col0 + ncol), [pk])

                def tm_shift(col0, ncol, dst, dk):
                    ps, psb, pk = nps()
                    for c in range(8):
                        mm(ps[:, 0:ncol], cur_x[:, c, :], wq[:, c, MLW + col0:MLW + col0 + ncol], c == 0, c == 7, [xnk] + wqk(MLW + col0, MLW + col0 + ncol), [pk])
                    for c in range(8):
                        mm(ps[:, 512:512 + ncol], dxT[:, c, :], wq[:, c, MLW + col0:MLW + col0 + ncol], c == 0, c == 7, ["dxT"] + wqk(MLW + col0, MLW + col0 + ncol), [pk])
                    dve(lambda e, ps=ps: e.tensor_tensor(out=dst, in0=ps[:, 512:512 + ncol], in1=mub[:, col0:col0 + ncol], op=ALU.mult), [pk, "mub"], [dk])
                    dve(lambda e, ps=ps: e.tensor_tensor(out=dst, in0=dst, in1=ps[:, 0:ncol], op=ALU.add), [pk, dk], [dk])

                ps, psb, pk = nps()
                tm_plain(1024, 512, ps[:, 0:512], pk)
                tm_plain(1536, 512, ps[:, 512:1024], pk)
                act(lambda e, ps=ps: e.copy(out=vaug[:, :, 0:64], in_=ps[:, 0:512].rearrange("p (h d) -> p h d", h=8)), [pk], ["vaug"])
                act(lambda e, ps=ps: e.activation(out=osig[:], in_=ps[:, 512:1024], func=AF.Sigmoid), [pk], ["osig"])
                ps, psb, pk = nps()
                tm_plain(2048, 16, ps[:, 0:16], pk)
                dve(lambda e, ps=ps: e.tensor_tensor(out=gt[:, 0:16], in0=ps[:, 0:16], in1=PA("ifb"), op=ALU.add), [pk, "pA"], ["gt"])
                tm_shift(0, 512, r_sb[:], "r_sb")
                tm_shift(512, 512, kf_sb[:], "kf_sb")
                tm_shift(1024, 512, vf[:], "vf")
                pool(lambda e: e.tensor_copy(out=vrw[:], in_=vf[:].rearrange("p (h d) -> p h d", h=8)), ["vf"], ["vrw"])
                ltmp = G[16]
                tm_shift(1536, 256, ltmp[:, 0:256], "cacc")
                act(lambda e: e.activation(out=lor[:, 0:64], in_=ltmp[:, 0:64], func=AF.Tanh), ["cacc"], ["lor"])
                act(lambda e: e.copy(out=lor[:, 64:128], in_=ltmp[:, 64:128]), ["cacc"], ["lor"])
                act(lambda e: e.activation(out=lor[:, 128:256], in_=ltmp[:, 128:256], func=AF.Sigmoid), ["cacc"], ["lor"])
                if b == NB - 1:
                    lastc = xn[:, :, 128:129]
                    for n0 in range(0, RWW, 512):
                        nn = min(512, RWW - n0)
                        ps2, _, pk2 = nps()
                        for c in range(8):
                            mm(ps2[0:1, 0:nn], lastc[:, c, :], wq[:, c, MLW + n0:MLW + n0 + nn], c == 0, c == 7, [xnk] + wqk(MLW + n0, MLW + n0 + nn), [pk2])
                        act(lambda e, ps2=ps2, n0=n0, nn=nn: e.copy(out=plast[0:1, 0:nn], in_=ps2[0:1, 0:nn]), [pk2], ["plast"])
                        dma("pool", oshift[:, n0:n0 + nn], plast[0:1, 0:nn], ["plast"], [], "fin")

                act(lambda e: e.activation(out=gt[:, 56:64], in_=gt[:, 8:16], func=AF.Exp, scale=-1.0), ["gt"], ["gt"])
                act(lambda e: e.activation(out=gt[:, 16:24], in_=gt[:, 56:64], func=AF.Ln, bias=1.0, scale=1.0), ["gt"], ["gt"])
                dve(lambda e: e.tensor_copy(out=nlrep[:], in_=bc8(gt[:, 16:24])), ["gt"], ["nlrep"])
                ps, psb, pk = nps()
                mm(ps[:, 0:8], MUI, gt[:, 16:24], True, True, ["pA", "gt"], [pk])
                mm(ps[:, 8:16], ONES, gt[:, 16:24], True, True, ["pA", "gt"], [pk])
                for j in range(4):
                    mm(ps[:, 512 + j * 128:512 + (j + 1) * 128], nlrep[:, 2 * j:2 * j + 2, :].rearrange("p a d -> p (a d)"), MUI, True, True, ["nlrep", "pA"], [pk])
                dve(lambda e, ps=ps: e.tensor_tensor(out=gt[:, 24:32], in0=ps[:, 0:8], in1=gt[:, 0:8], op=ALU.add), [pk, "gt"], ["gt"])
                act(lambda e: e.activation(out=gt[:, 32:40], in_=gt[:, 24:32], func=AF.Exp), ["gt"], ["gt"])
                dve(lambda e, ps=ps: e.tensor_tensor(out=gt[:, 56:64], in0=gt[:, 24:32], in1=ps[:, 8:16], op=ALU.subtract), [pk, "gt"], ["gt"])
                act(lambda e: e.activation(out=gt[:, 40:48], in_=gt[:, 56:64], func=AF.Exp), ["gt"], ["gt"])
                act(lambda e, ps=ps: e.activation(out=gt[:, 48:56], in_=ps[:, 8:16], func=AF.Exp, scale=-1.0), [pk], ["gt"])
                act(lambda e, ps=ps: e.activation(out=Fb[:], in_=ps[:, 512:1024].rearrange("p (j t) -> p j t", j=4), func=AF.Exp, scale=-1.0), [pk], ["Fb"])
                dve(lambda e: e.tensor_tensor(out=gt[:, 56:64], in0=gt[:, 24:32], in1=nBc[:], op=ALU.add), ["gt", "nBc"], ["gt"])
                dve(lambda e: e.tensor_tensor(out=runmax[:], in0=runmax[:], in1=gt[:, 56:64], op=ALU.max), ["gt", "runmax"], ["runmax"])
                dve(lambda e, ps=ps: e.tensor_tensor(out=nBc[:], in0=nBc[:], in1=ps[:, 8:16], op=ALU.add), [pk, "nBc"], ["nBc"])

                yield
                cwv = PA("cw").rearrange("p (c j) -> p c j", j=4)
                wbc = lambda j: cwv[:, :, j:j + 1].to_broadcast([128, 8, 128])
                pool(lambda e, qx=qx: e.tensor_tensor(out=cacc[:], in0=qx[:, :, 3:131], in1=wbc(3), op=ALU.mult), [qxk, "pA"], ["cacc"])
                for j in range(3):
                    pool(lambda e, qx=qx, j=j: e.tensor_tensor(out=ctmp[:], in0=qx[:, :, j:j + 128], in1=wbc(j), op=ALU.mult), [qxk, "pA"], ["ctmp"])
                    pool(lambda e: e.tensor_tensor(out=cacc[:], in0=cacc[:], in1=ctmp[:], op=ALU.add), ["cacc", "ctmp"], ["cacc"])
                pool(lambda e: e.tensor_tensor(out=cacc[:], in0=cacc[:], in1=PA("cb").unsqueeze(2).to_broadcast([128, 8, 128]), op=ALU.add), ["cacc", "pA"], ["cacc"])
                act(lambda e: e.activation(out=qks[:], in_=cacc[:], func=AF.Silu), ["cacc"], ["qks"])
                dve(lambda e: e.tensor_tensor(out=qpb[:], in0=qks[:, 0:4, :], in1=Fb[:], op=ALU.mult), ["qks", "Fb"], ["qpb"])
                act(lambda e: e.activation(out=kTb[:], in_=qks[:, 4:8, :], func=AF.Copy, scale=0.125), ["qks"], ["kTb"])

                yield
                ps, psb, pk = nps1()
                for j in range(4):
                    pe(lambda e, j=j, psb=psb: e.transpose(psb[:, j * 128:(j + 1) * 128], kTb[:, j, :], identb[:]), ["kTb", "identb"], [pk])
                dve(lambda e, psb=psb: e.tensor_tensor(out=ktm[:], in0=psb[:, 0:512].rearrange("p (h d) -> p h d", h=8), in1=bc8(gt[:, 40:48]), op=ALU.mult), [pk, "gt"], ["ktm"])

                yield
                yield
                if b + 1 < NB:
                    pool(lambda e, xn=xn: e.tensor_copy(out=xn[:, :, 0:1], in_=xn[:, :, 128:129]), [xnk], [xnk])
                    pool(lambda e, qx=qx: e.tensor_copy(out=qx[:, :, 0:3], in_=qx[:, :, 128:131]), [qxk], [qxk])
                yield

            def ml_stage():
                ps, psb, pk = nps()
                for h in range(8):
                    j, hp = h // 2, h % 2
                    sl = slice(hp * 64, hp * 64 + 64)
                    mm(ps[:, hoff(h):hoff(h) + 128], kTb[sl, j, :], qpb[sl, j, :], True, True, ["kTb", "qpb"], [pk])
                for h in range(8):
                    dve(lambda e, h=h, ps=ps: e.scalar_tensor_tensor(out=PTb[:, h, :], in0=ps[:, hoff(h):hoff(h) + 128], scalar=gt[:, 32 + h:33 + h], in1=MUI, op0=ALU.mult, op1=ALU.mult), [pk, "gt", "pA"], ["PTb"])
                yield
                ps, psb, pk = nps()
                psn = lambda ps, h: ps[:, (h // 4) * 512 + (h % 4) * 65:(h // 4) * 512 + (h % 4) * 65 + 65]
                for h in range(8):
                    j, hp = h // 2, h % 2
                    sl = slice(hp * 64, hp * 64 + 64)
                    mm(psn(ps, h), PTb[:, h, :], vaug[:, h, :], True, False, ["PTb", "vaug"], [pk])
                    mm(psn(ps, h), qpb[sl, j, :], Cbf[sl, j, :], False, True, ["qpb", "Cbf"], [pk])
                pn4 = ps[:, :].rearrange("p (a r) -> p a r", a=2)[:, :, 0:260].rearrange("p a (h d) -> p a h d", h=4)
                for a_ in range(2):
                    act(lambda e, pn4=pn4, a_=a_: e.copy(out=r8[:, 4 * a_:4 * a_ + 4], in_=pn4[:, a_, :, 64]), [pk], ["r8"])
                dve(lambda e: e.scalar_tensor_tensor(out=r8[:, 8:16], in0=r8[:, 0:8], scalar=-1.0, in1=r8[:, 0:8], op0=ALU.mult, op1=ALU.max), ["r8"], ["r8"])
                dve(lambda e: e.tensor_scalar_max(out=r8[:, 8:16], in0=r8[:, 8:16], scalar1=1.0), ["r8"], ["r8"])
                dve(lambda e: e.reciprocal(out=r8[:, 16:24], in_=r8[:, 8:16]), ["r8"], ["r8"])
                for a_ in range(2):
                    dve(lambda e, pn4=pn4, a_=a_: e.tensor_tensor(out=hml[:, a_ * 256:(a_ + 1) * 256].rearrange("p (h d) -> p h d", h=4), in0=pn4[:, a_, :, 0:64],
                                                              in1=r8[:, 16 + 4 * a_:20 + 4 * a_].unsqueeze(2).to_broadcast([128, 4, 64]), op=ALU.mult), [pk, "r8"], ["hml"])
                yield
                ps, psb, pk = nps()
                for h in range(8):
                    j = h // 2
                    mm(psn(ps, h), ktm[:, 2 * j:2 * j + 2, :].rearrange("p a d -> p (a d)"), vaug[:, h, :], True, True, ["ktm", "vaug"], [pk])
                pu4 = ps[:, :].rearrange("p (a r) -> p a r", a=2)[:, :, 0:260].rearrange("p a (h d) -> p a h d", h=4)
                for hp in range(2):
                    sl = slice(hp * 64, hp * 64 + 64)
                    decb = gt[sl, 48:56].rearrange("p (j q) -> p j q", q=2)[:, :, hp:hp + 1].to_broadcast([64, 4, 65])
                    dve(lambda e, sl=sl, decb=decb: e.tensor_tensor(out=Cst[sl, :, :], in0=Cst[sl, :, :], in1=decb, op=ALU.mult), ["Cst", "gt"], ["Cst"])
                    for a in range(2):
                        src = pu4[sl, a, hp::2, :]
                        dve(lambda e, sl=sl, a=a, src=src: e.tensor_tensor(out=Cst[sl, 2 * a:2 * a + 2, :], in0=Cst[sl, 2 * a:2 * a + 2, :], in1=src, op=ALU.add), [pk, "Cst"], ["Cst"])
                act(lambda e: e.copy(out=Cbf[:], in_=Cst[:]), ["Cst"], ["Cbf"])
                head_ml(128, hml, "hml", osig, "osig", mix, "mix", WM, "m")


                yield
            def rw_stage():
                ps, psb, pk = nps1()
                pe(lambda e, psb=psb: e.transpose(psb[:, 0:128], lor[:, 0:128], identb[:]), ["lor", "identb"], [pk])
                pe(lambda e, psb=psb: e.transpose(psb[:, 128:256], lor[:, 128:256], identb[:]), ["lor", "identb"], [pk])
                act(lambda e, psb=psb: e.copy(out=lorT[:], in_=psb[:, 0:256].rearrange("p (a t) -> p a t", a=2)), [pk], ["lorT"])
                ps, psb, pk = nps()
                mm(ps[:, 0:512], lorT[0:64, 0, :], luw[0:64, :], True, True, ["lorT", "luw"], [pk])
                mm(ps[:, 512:1024], lorT[64:128, 0, :], luw[64:128, :], True, True, ["lorT", "luw"], [pk])
                dve(lambda e, ps=ps: e.tensor_tensor(out=e1[:], in0=ps[:, 0:512], in1=PA("w0"), op=ALU.add), [pk, "pA"], ["e1"])
                act(lambda e: e.activation(out=wsig[:], in_=e1[:], func=AF.Sigmoid), ["e1"], ["wsig"])
                dve(lambda e, ps=ps: e.tensor_tensor(out=e2[:], in0=ps[:, 512:1024], in1=PA("a0"), op=ALU.add), [pk, "pA"], ["e2"])
                act(lambda e: e.activation(out=a_sb[:], in_=e2[:], func=AF.Sigmoid), ["e2"], ["a_sb"])
                ps, psb, pk = nps1()
                mm(ps[:, 0:512], lorT[:, 1, :], gup[:, :], True, True, ["lorT", "gup"], [pk])
                act(lambda e, ps=ps: e.copy(out=g_sb[:], in_=ps[:, 0:512]), [pk], ["g_sb"])
                yield
                dve(lambda e: e.tensor_tensor(out=e1[:], in0=kf_sb[:], in1=PA("kk"), op=ALU.mult), ["kf_sb", "pA"], ["e1"])
                dve(lambda e: e.tensor_tensor(out=e2[:], in0=e1[:], in1=e1[:], op=ALU.mult), ["e1"], ["e2"])
                dve(lambda e: e.tensor_reduce(out=r8b[:, 0:8], in_=v3(e2), axis=AX.X, op=ALU.add), ["e2"], ["r8b"])
                dve(lambda e: e.tensor_scalar_max(out=r8b[:, 0:8], in0=r8b[:, 0:8], scalar1=1e-24), ["r8b"], ["r8b"])
                act(lambda e: e.activation(out=r8b[:, 0:8], in_=r8b[:, 0:8], func=AF.Ln), ["r8b"], ["r8b"])
                act(lambda e: e.activation(out=r8b[:, 0:8], in_=r8b[:, 0:8], func=AF.Exp, scale=-0.5), ["r8b"], ["r8b"])
                dve(lambda e: e.tensor_tensor(out=v3(kap), in0=v3(e1), in1=bc8(r8b[:, 0:8]), op=ALU.mult), ["e1", "r8b"], ["kap"])
                dve(lambda e: e.tensor_scalar_add(out=e2[:], in0=a_sb[:], scalar1=-1.0), ["a_sb"], ["e2"])
                dve(lambda e: e.tensor_tensor(out=e2[:], in0=e2[:], in1=PA("ka"), op=ALU.mult), ["e2", "pA"], ["e2"])
                dve(lambda e: e.tensor_tensor(out=e2[:], in0=e2[:], in1=kf_sb[:], op=ALU.mult), ["e2", "kf_sb"], ["e2"])
                dve(lambda e: e.tensor_tensor(out=ktl[:], in0=e2[:], in1=kf_sb[:], op=ALU.add), ["e2", "kf_sb"], ["ktl"])
                dve(lambda e: e.tensor_tensor(out=bvec[:], in0=a_sb[:], in1=kap[:], op=ALU.mult), ["a_sb", "kap"], ["bvec"])
                dve(lambda e: e.tensor_tensor(out=e2[:], in0=r_sb[:], in1=ktl[:], op=ALU.mult), ["r_sb", "ktl"], ["e2"])
                dve(lambda e: e.tensor_tensor(out=e2[:], in0=e2[:], in1=PA("rk"), op=ALU.mult), ["e2", "pA"], ["e2"])
                dve(lambda e: e.tensor_reduce(out=bon[:], in_=v3(e2), axis=AX.X, op=ALU.add), ["e2"], ["bon"])
                yield
                ps, psb, pk = nps()
                mm(ps[:, 0:512], MUI, wsig[:], True, True, ["pA", "wsig"], [pk])
                mm(ps[:, 512:1024], ONES, wsig[:], True, True, ["pA", "wsig"], [pk])
                act(lambda e, ps=ps: e.copy(out=pcw[:], in_=ps[:, 0:512]), [pk], ["pcw"])
                dve(lambda e: e.tensor_tensor(out=e1[:], in0=pcw[:], in1=wsig[:], op=ALU.subtract), ["pcw", "wsig"], ["e1"])
                act(lambda e: e.activation(out=e1[:], in_=e1[:], func=AF.Exp, scale=-C0), ["e1"], ["e1"])
                dve(lambda e: e.tensor_tensor(out=TMb[:, 0, :], in0=kap[:], in1=e1[:], op=ALU.mult), ["kap", "e1"], ["TMb0"])
                act(lambda e: e.activation(out=e2[:], in_=pcw[:], func=AF.Exp, scale=-C0), ["pcw"], ["e2"])
                dve(lambda e: e.tensor_tensor(out=TMb[:, 1, :], in0=r_sb[:], in1=e2[:], op=ALU.mult), ["r_sb", "e2"], ["TMb1"])
                act(lambda e: e.activation(out=e3[:], in_=pcw[:], func=AF.Exp, scale=C0), ["pcw"], ["e3"])
                dve(lambda e: e.tensor_tensor(out=TMb[:, 2, :], in0=bvec[:], in1=e3[:], op=ALU.mult), ["bvec", "e3"], ["TMb2"])
                dve(lambda e: e.tensor_tensor(out=TMb[:, 3, :], in0=ktl[:], in1=e3[:], op=ALU.mult), ["ktl", "e3"], ["TMb3"])
                dve(lambda e, ps=ps: e.tensor_tensor(out=e1[:], in0=ps[:, 512:1024], in1=pcw[:], op=ALU.subtract), [pk, "pcw"], ["e1"])
                act(lambda e: e.activation(out=e1[:], in_=e1[:], func=AF.Exp, scale=-C0), ["e1"], ["e1"])
                for hp in range(2):
                    srcb = v3(bvec).rearrange("p (j q) d -> p j q d", q=2)[:, :, hp, :]
                    srck = v3(ktl).rearrange("p (j q) d -> p j q d", q=2)[:, :, hp, :]
                    wl = v3(e1).rearrange("p (j q) d -> p j q d", q=2)[:, :, hp, :]
                    dstb = Btz[:].rearrange("p (j q) c -> p j q c", q=2)[:, :, hp, hp * 64:hp * 64 + 64]
                    dstk = Ktz[:].rearrange("p (j q) c -> p j q c", q=2)[:, :, hp, hp * 64:hp * 64 + 64]
                    dve(lambda e, srcb=srcb, wl=wl, dstb=dstb: e.tensor_tensor(out=dstb, in0=srcb, in1=wl, op=ALU.mult), ["bvec", "e1"], ["Btz"])
                    dve(lambda e, srck=srck, wl=wl, dstk=dstk: e.tensor_tensor(out=dstk, in0=srck, in1=wl, op=ALU.mult), ["ktl", "e1"], ["Ktz"])
                ps2, _, pk2 = nps()
                for j in range(4):
                    mm(ps2[:, j:j + 1], wsig[:, j * 128:(j + 1) * 128], ONES[:, 0:1], True, True, ["wsig", "pA"], [pk2])
                act(lambda e, ps2=ps2: e.activation(out=WLfm[:], in_=ps2[:, 0:4], func=AF.Exp, scale=-C0), [pk2], ["WLfm"])
                yield
                ps, psb, pk = nps()
                for w_ in range(4):
                    for j in range(4):
                        pe(lambda e, w_=w_, j=j, psb=psb: e.transpose(psb[:, (w_ * 4 + j) * 128:(w_ * 4 + j + 1) * 128], TMb[:, w_, j * 128:(j + 1) * 128], identb[:]), ["TMb%d" % w_, "identb"], [pk])
                for w_ in range(4):
                    eng_ = act if w_ % 2 == 0 else dve
                    if w_ % 2 == 0:
                        act(lambda e, psb=psb, w_=w_: e.copy(out=FMt[:, w_, :, :], in_=psb[:, w_ * 512:(w_ + 1) * 512].rearrange("p (j t) -> p j t", j=4)), [pk], ["FMt"])
                    else:
                        dve(lambda e, psb=psb, w_=w_: e.tensor_copy(out=FMt[:, w_, :, :], in_=psb[:, w_ * 512:(w_ + 1) * 512].rearrange("p (j t) -> p j t", j=4)), [pk], ["FMt"])
                KAP, RB, BB, KKB = 0, 1, 2, 3

                def amat(lw, rw_, dst, dk, mask, neg):
                    ps, psb, pk = nps()
                    for h in range(8):
                        j, hp = h // 2, h % 2
                        sl = slice(hp * 64, hp * 64 + 64)
                        mm(ps[:, hoff(h):hoff(h) + 128], FMt[sl, lw, j, :], FMt[sl, rw_, j, :], True, True, ["FMt"], [pk])
                    psv = ps[:, :].rearrange("p (q j t) -> p q j t", q=2, j=4)
                    dstv = dst[:].rearrange("p (j q) t -> p q j t", q=2)
                    mk = mask.unsqueeze(1).unsqueeze(1).to_broadcast([128, 2, 4, 128])
                    if neg:
                        mk3 = mask.unsqueeze(1).to_broadcast([128, 4, 128])
                        for q in range(2):
                            dve(lambda e, q=q: e.scalar_tensor_tensor(out=dstv[:, q], in0=psv[:, q], scalar=-1.0, in1=mk3, op0=ALU.mult, op1=ALU.mult), [pk, "pA"], [dk])
                    else:
                        mk3 = mask.unsqueeze(1).to_broadcast([128, 4, 128])
                        for q in range(2):
                            dve(lambda e, q=q: e.tensor_tensor(out=dstv[:, q], in0=psv[:, q], in1=mk3, op=ALU.mult), [pk, "pA"], [dk])

                amat(BB, KAP, Pw[1], "Pw1", MUS, True)
                amat(KAP, BB, Pw[0], "Pw0", MLS, True)
                amat(KKB, KAP, Am[0], "Am0", MUS, False)
                amat(BB, RB, Am[1], "Am1", MUI, False)
                amat(KKB, RB, Am[2], "Am2", MUI, False)
                yield
                ps, psb, pk = nps1()
                for h in range(8):
                    j, hp = h // 2, h % 2
                    sl = slice(hp * 64, hp * 64 + 64)
                    mm(ps[:, h * 64:(h + 1) * 64], FMt[sl, KAP, j, :], Sbf[sl, j, :], True, False, ["FMt", "Sbf"], [pk])
                    mm(ps[:, h * 64:(h + 1) * 64], Am[0][:, h, :], vrw[:, h, :], False, True, ["Am0", "vrw"], [pk])
                act(lambda e, ps=ps: e.copy(out=Zb[0][:], in_=ps[:, 0:512].rearrange("p (h d) -> p h d", h=8)), [pk], ["Zb0"])
                pi = 0
                zi = 0
                for lvl in range(7):
                    yield
                    Pc, PTc = Pw[pi], Pw[pi + 1]
                    Pk, PTk = "Pw%d" % pi, "Pw%d" % (pi + 1)
                    Zc, Zn = Zb[zi], Zb[1 - zi]
                    ps, psb, pk = nps1()
                    for h in range(8):
                        mm(ps[:, h * 64:(h + 1) * 64], identb[:], Zc[:, h, :], True, False, ["identb", "Zb%d" % zi], [pk])
                        mm(ps[:, h * 64:(h + 1) * 64], PTc[:, h, :], Zc[:, h, :], False, True, [PTk, "Zb%d" % zi], [pk])
                    if lvl < 6:
                        act(lambda e, ps=ps, Zn=Zn: e.copy(out=Zn[:], in_=ps[:, 0:512].rearrange("p (h d) -> p h d", h=8)), [pk], ["Zb%d" % (1 - zi)])
                        zi = 1 - zi
                        ni = 2 - pi
                        Pn, PTn = Pw[ni], Pw[ni + 1]
                        psA, _, pkA = nps()
                        for h in range(8):
                            mm(psA[:, h * 128:(h + 1) * 128], PTc[:, h, :], Pc[:, h, :], True, True, [PTk, Pk], [pkA])
                        for a_ in range(2):
                            dve(lambda e, psA=psA, Pn=Pn, a_=a_: e.tensor_copy(out=Pn[:, 4 * a_:4 * a_ + 4, :], in_=psA[:, a_ * 512:(a_ + 1) * 512].rearrange("p (h t) -> p h t", h=4)), [pkA], ["Pw%d" % ni])
                        psB, _, pkB = nps()
                        for h in range(8):
                            mm(psB[:, h * 128:(h + 1) * 128], Pc[:, h, :], PTc[:, h, :], True, True, [Pk, PTk], [pkB])
                        for a_ in range(2):
                            act(lambda e, psB=psB, PTn=PTn, a_=a_: e.copy(out=PTn[:, 4 * a_:4 * a_ + 4, :], in_=psB[:, a_ * 512:(a_ + 1) * 512].rearrange("p (h t) -> p h t", h=4)), [pkB], ["Pw%d" % (ni + 1)])
                        pi = ni
                    else:
                        act(lambda e, ps=ps: e.activation(out=Ub[:], in_=ps[:, 0:512].rearrange("p (h d) -> p h d", h=8), func=AF.Copy, scale=-1.0), [pk], ["Ub"])
                yield
                ps, psb, pk = nps1()
                for h in range(8):
                    j, hp = h // 2, h % 2
                    sl = slice(hp * 64, hp * 64 + 64)
                    o_ = ps[:, h * 64:(h + 1) * 64]
                    mm(o_, Am[1][:, h, :], Ub[:, h, :], True, False, ["Am1", "Ub"], [pk])
                    mm(o_, Am[2][:, h, :], vrw[:, h, :], False, False, ["Am2", "vrw"], [pk])
                    mm(o_, FMt[sl, RB, j, :], Sbf[sl, j, :], False, True, ["FMt", "Sbf"], [pk])
                act(lambda e, ps=ps: e.copy(out=ysb[:], in_=ps[:, 0:512]), [pk], ["ysb"])
                yield
                ps, psb, pk = nps1()
                for j in range(4):
                    o_ = ps[:, j * 64:(j + 1) * 64]
                    mm(o_, Btz[:, 2 * j, :], Ub[:, 2 * j, :], True, False, ["Btz", "Ub"], [pk])
                    mm(o_, Ktz[:, 2 * j, :], vrw[:, 2 * j, :], False, False, ["Ktz", "vrw"], [pk])
                    mm(o_, Btz[:, 2 * j + 1, :], Ub[:, 2 * j + 1, :], False, False, ["Btz", "Ub"], [pk])
                    mm(o_, Ktz[:, 2 * j + 1, :], vrw[:, 2 * j + 1, :], False, True, ["Ktz", "vrw"], [pk])
                dve(lambda e: e.tensor_tensor(out=Sst[:], in0=Sst[:], in1=WLfm[:].unsqueeze(2).to_broadcast([128, 4, 64]), op=ALU.mult), ["Sst", "WLfm"], ["Sst"])
                dve(lambda e, ps=ps: e.tensor_tensor(out=Sst[:], in0=Sst[:], in1=ps[:, 0:256].rearrange("p (j d) -> p j d", j=4), op=ALU.add), [pk, "Sst"], ["Sst"])
                act(lambda e: e.copy(out=Sbf[:], in_=Sst[:]), ["Sst"], ["Sbf"])
                yield
            def tail():
                head_rw(128, ysb, "ysb", bon, "bon", vf, "vf", g_sb, "g_sb", mix, "mix", {"tA": G[18], "tB": G[19], "s8": W["s8"]})
                out_proj(128, mix, "mix", None, None, b * 128, W)

            return front_stage, ml_stage, rw_stage, tail, p_

        def run_gens(gl):
            gl = list(gl)
            while gl:
                for item in list(gl):
                    CURP[0] = item[1]
                    for _ in range(item[2] if len(item) > 2 else 1):
                        try:
                            next(item[0])
                        except StopIteration:
                            gl.remove(item)
                            break

        blocks = [make_block(b) for b in range(NB)]
        run_gens([(blocks[0][0](), 0, 1, "A")])
        for b in range(NB):
            fr, ml_, rw_, tl, p_ = blocks[b]
            gl = [(rw_(), p_, 1, "A"), (ml_(), p_, 1, "A")]
            if b + 1 < NB:
                gl.append((blocks[b + 1][0](), (b + 1) % 2, 1, "A"))
            run_gens(gl)
            CURP[0] = p_
            tl()
        CURP[0] = 0

        ps, psb, pk = nps()
        mm(ps[0:8, 0:128], runmax[:], IDF, True, True, ["runmax", "pA"], [pk])
        mm(ps[0:8, 128:256], nBc[:], IDF, True, True, ["nBc", "pA"], [pk])
        fs = TT("fs", [8, 16])
        dve(lambda e, ps=ps: e.tensor_reduce(out=fs[:, 0:1], in_=ps[0:8, 0:128], axis=AX.X, op=ALU.max), [pk], ["fs"])
        dve(lambda e: e.tensor_scalar_max(out=fs[:, 0:1], in0=fs[:, 0:1], scalar1=0.0), ["fs"], ["fs"])
        dve(lambda e, ps=ps: e.tensor_tensor(out=fs[:, 1:2], in0=fs[:, 0:1], in1=ps[0:8, 128:129], op=ALU.subtract), [pk, "fs"], ["fs"])
        dma("pool", om, fs[:, 1:2], ["fs"], [], "fin")
        act(lambda e: e.activation(out=fs[:, 2:3], in_=fs[:, 1:2], func=AF.Exp, scale=-1.0), ["fs"], ["fs"])
        dve(lambda e: e.tensor_scalar_mul(out=fs[:, 4:8], in0=pA[0:8, OFF["rsel"][0]:OFF["rsel"][1]], scalar1=fs[:, 2:3]), ["fs", "pA"], ["fs"])
        ps, psb, pk = nps()
        mm(ps[:, 0:4], pA[0:8, OFF["lsel"][0]:OFF["lsel"][1]], fs[:, 4:8], True, True, ["pA", "fs"], [pk])
        scb = TT("scb", [128, 4])
        act(lambda e, ps=ps: e.copy(out=scb[:], in_=ps[:, 0:4]), [pk], ["scb"])
        dve(lambda e: e.tensor_tensor(out=Cst[:], in0=Cst[:], in1=scb[:].unsqueeze(2).to_broadcast([128, 4, 65]), op=ALU.mult), ["Cst", "scb"], ["Cst"])
        dma("pool", oC, Cst[:], ["Cst"], [], "fin")
        dma("pool", oS, Sst[:], ["Sst"], [], "fin")
        PP[0].finalize()
        PP[0] = Prog(ctx)

    with contextlib.ExitStack() as es_s:
        cur[0] = es_s
        if do_sample:
            W = {}
            W["xm"] = TT("s_xm", [128, D])
            junk = W["xm"]
            st = TT("s_st", [128, 4])
            mix = TT("s_mix", [128, D], BF16)
            xsb = mix
            lor = TT("s_lor", [128, 256], BF16)
            lorT = TT("s_lorT", [128, 2, 128], BF16)
            W["mixT"] = TT("s_mixT", [128, 8, 128], BF16)
            sx = TT("sx", [NS, D])
            sxT = TT("sxT", [128, 8, NS], BF16)
            spj = TT("spj", [NS, INW])
            spk = TT("spk", [128, NSP])
            sl_t = TT("sl_t", [NS, 3, 256])
            Cs = TT("Cs", [128, 4096])
            Ss = Cs
            sn_t = TT("sn_t", [128, 64])
            sm_t = TT("sm_t", [128, 1])
            scv = TT("scv", [128, 2, 4, 64])
            ssh = TT("ssh", [128, 3, 64])
            dma("sp", sx[:], xs, [], ["sx"], "sin", True)
            dma("sp", spk[:], spk_d, [], ["spk"], "sin", True)
            dma("sp", sl_t[:, 0, :], sshl_d, [], ["sl_t"], "sin", True)
            dma("sp", sl_t[:, 1, :], mul_d, [], ["sl_t"], "sin", True)
            dma("sp", Cs[:], sC_d, [], ["Cs"], "sin", True)
            dma("sp", sn_t[:], sn_d, [], ["sn_t"], "sin", True)
            dma("sp", sm_t[:], sm_d, [], ["sm_t"], "sin", True)
            dma("sp", scv[:, :, 0:3, :], sconv_d, [], ["scv"], "sin", True)
            dma("sp", ssh[:], sshift_d, [], ["ssh"], "sin", True)
            rmsnorm_T(sx, "sx", NS, sxT, "sxT", 0, "nmw", xsb, "xsb", junk, "junk", st, "st")
            for n0 in range(0, INW, 512):
                nn = min(512, INW - n0)
                ps, psb, pk = nps()
                for c in range(8):
                    mm(ps[:NS, 0:nn], sxT[:, c, :], wq[:, c, n0:n0 + nn], c == 0, c == 7, ["sxT", "wq"], [pk])
                act(lambda e, ps=ps, n0=n0, nn=nn: e.copy(out=spj[:, n0:n0 + nn], in_=ps[:NS, 0:nn]), [pk], ["spj"])
            s1v = scr1.rearrange("(b h) a d -> b a h d", b=NS)
            for a_ in range(7):
                c0_ = a_ * 512 if a_ < 4 else MLW + (a_ - 4) * 512
                dma("pool", s1v[:, a_, :, :], spj[:, c0_:c0_ + 512].rearrange("p (h d) -> p h d", h=8), ["spj"], ["scr1"], "scrw1")
            A7 = TT("A7", [128, 8, 64])
            dma("sp", A7[:, 0:7, :], scr1[:, 0:7, :], ["scr1"], ["A7"], "scrr1")
            gif = TT("gif", [128, 2])
            s_if = nc.dram_tensor("scr_if", [2, 128], F32, kind="Internal").ap()
            for g_ in range(2):
                dma("pool", s_if[g_, :].rearrange("(b h) -> b h", b=NS), spj[:, 2048 + 8 * g_:2056 + 8 * g_], ["spj"], ["scr_if"], "scrwif")
            for g_ in range(2):
                dma("sp", gif[:, g_:g_ + 1], s_if[g_, :].rearrange("(p o) -> p o", o=1), ["scr_if"], ["gif"], "scrrif")
            SP_ = lambda n: spk[:, SOFF[n][0]:SOFF[n][1]]
            pl = spj[:, MLW + 1536:MLW + 1792]
            dma("pool", osshl, pl, ["spj"], [], "fin")
            dve(lambda e: e.tensor_tensor(out=sl_t[:, 2, :], in0=sl_t[:, 0, :], in1=pl, op=ALU.subtract), ["sl_t", "spj"], ["sl_t"])
            dve(lambda e: e.tensor_tensor(out=sl_t[:, 2, :], in0=sl_t[:, 2, :], in1=sl_t[:, 1, :], op=ALU.mult), ["sl_t"], ["sl_t"])
            dve(lambda e: e.tensor_tensor(out=sl_t[:, 2, :], in0=sl_t[:, 2, :], in1=pl, op=ALU.add), ["sl_t", "spj"], ["sl_t"])
            act(lambda e: e.activation(out=lor[:NS, 0:64], in_=sl_t[:, 2, 0:64], func=AF.Tanh), ["sl_t"], ["lor"])
            act(lambda e: e.copy(out=lor[:NS, 64:128], in_=sl_t[:, 2, 64:128]), ["sl_t"], ["lor"])
            act(lambda e: e.activation(out=lor[:NS, 128:256], in_=sl_t[:, 2, 128:256], func=AF.Sigmoid), ["sl_t"], ["lor"])
            ps, psb, pk = nps()
            pe(lambda e, psb=psb: e.transpose(psb[:, 0:NS], lor[:NS, 0:128], identb[:NS, :NS]), ["lor", "identb"], [pk])
            pe(lambda e, psb=psb: e.transpose(psb[:, 128:128 + NS], lor[:NS, 128:256], identb[:NS, :NS]), ["lor", "identb"], [pk])
            act(lambda e, psb=psb: e.copy(out=lorT[:, :, 0:NS], in_=psb[:, 0:256].rearrange("p (a t) -> p a t", a=2)[:, :, 0:NS]), [pk], ["lorT"])
            ps, psb, pk = nps()
            mm(ps[:NS, 0:512], lorT[0:64, 0, 0:NS], luw[0:64, :], True, True, ["lorT", "luw"], [pk])
            mm(ps[:NS, 512:1024], lorT[64:128, 0, 0:NS], luw[64:128, :], True, True, ["lorT", "luw"], [pk])
            ps2, _, pk2 = nps()
            mm(ps2[:NS, 0:512], lorT[:, 1, 0:NS], gup[:, :], True, True, ["lorT", "gup"], [pk2])
            lo3 = spj[:, 0:1536].rearrange("p (a n) -> p a n", a=3)
            for a_ in range(2):
                act(lambda e, ps=ps, a_=a_: e.copy(out=lo3[:, a_, :], in_=ps[:NS, a_ * 512:(a_ + 1) * 512]), [pk], ["lo3"])
            act(lambda e, ps2=ps2: e.copy(out=lo3[:, 2, :], in_=ps2[:NS, 0:512]), [pk2], ["lo3"])
            for a_ in range(3):
                dma("pool", scr2.rearrange("(b h) a d -> b a h d", b=NS)[:, a_, :, :], lo3[:, a_, :].rearrange("p (h d) -> p h d", h=8), ["lo3"], ["scr2"], "scrw2")
            L3 = TT("L3", [128, 3, 64])
            dma("sp", L3[:], scr2, ["scr2"], ["L3"], "scrr2")
            big = TT("big", [128, 4096])
            sv = TT("sv", [128, 64])
            pool(lambda e: e.tensor_copy(out=scv[:, :, 3, :], in_=A7[:, 0:2, :]), ["A7"], ["scv"])
            dma("pool", osconv, scv[:, :, 1:4, :], ["scv"], [], "fin")
            qk_s = TT("qk_s", [128, 2, 64])
            cwqk = lambda w_: spk[:, SOFF["cwq"][0] + w_ * 256:SOFF["cwq"][0] + (w_ + 1) * 256].rearrange("p (j d) -> p j d", j=4)
            for w_ in range(2):
                dve(lambda e, w_=w_: e.tensor_tensor(out=big[:, 0:256].rearrange("p (j d) -> p j d", j=4), in0=scv[:, w_, :, :], in1=cwqk(w_), op=ALU.mult), ["scv", "spk"], ["big"])
                dve(lambda e, w_=w_: e.tensor_reduce(out=qk_s[:, w_, :], in_=big[:, 0:256].rearrange("p (j d) -> p d j", j=4), axis=AX.X, op=ALU.add), ["big"], ["qk_s"])
            dve(lambda e: e.tensor_tensor(out=qk_s[:], in0=qk_s[:], in1=spk[:, SOFF["cbq"][0]:SOFF["cbk"][1]].rearrange("p (a d) -> p a d", a=2), op=ALU.add), ["qk_s", "spk"], ["qk_s"])
            act(lambda e: e.activation(out=qk_s[:], in_=qk_s[:], func=AF.Silu), ["qk_s"], ["qk_s"])
            act(lambda e: e.activation(out=qk_s[:, 1, :], in_=qk_s[:, 1, :], func=AF.Copy, scale=0.125), ["qk_s"], ["qk_s"])
            dve(lambda e: e.tensor_tensor(out=sv[:, 0:2], in0=gif[:], in1=spk[:, SOFF["ib"][0]:SOFF["fb"][1]], op=ALU.add), ["gif", "spk"], ["sv"])
            act(lambda e: e.activation(out=sv[:, 9:10], in_=sv[:, 1:2], func=AF.Exp, scale=-1.0), ["sv"], ["sv"])
            act(lambda e: e.activation(out=sv[:, 2:3], in_=sv[:, 9:10], func=AF.Ln, bias=1.0, scale=1.0), ["sv"], ["sv"])
            dve(lambda e: e.tensor_tensor(out=sv[:, 3:4], in0=sm_t[:], in1=sv[:, 2:3], op=ALU.subtract), ["sv", "sm_t"], ["sv"])
            dve(lambda e: e.tensor_tensor(out=sv[:, 4:5], in0=sv[:, 3:4], in1=sv[:, 0:1], op=ALU.max), ["sv"], ["sv"])
            dma("pool", osm, sv[:, 4:5], ["sv"], [], "fin")
            dve(lambda e: e.tensor_tensor(out=sv[:, 9:10], in0=sv[:, 0:1], in1=sv[:, 4:5], op=ALU.subtract), ["sv"], ["sv"])
            act(lambda e: e.activation(out=sv[:, 5:6], in_=sv[:, 9:10], func=AF.Exp), ["sv"], ["sv"])
            dve(lambda e: e.tensor_tensor(out=sv[:, 9:10], in0=sv[:, 3:4], in1=sv[:, 4:5], op=ALU.subtract), ["sv"], ["sv"])
            act(lambda e: e.activation(out=sv[:, 6:7], in_=sv[:, 9:10], func=AF.Exp), ["sv"], ["sv"])
            act(lambda e: e.activation(out=sv[:, 7:8], in_=sv[:, 4:5], func=AF.Exp, scale=-1.0), ["sv"], ["sv"])
            q_ = qk_s[:, 0, :]
            k_ = qk_s[:, 1, :]
            v_ = A7[:, 2, :]
            b3 = lambda t: t[:, :].rearrange("p (a c) -> p a c", a=64)
            pool(lambda e: e.tensor_tensor(out=b3(big), in0=k_.unsqueeze(2).to_broadcast([128, 64, 64]), in1=v_.unsqueeze(1).to_broadcast([128, 64, 64]), op=ALU.mult), ["qk_s", "A7"], ["big"])
            dve(lambda e: e.tensor_scalar_mul(out=Cs[:], in0=Cs[:], scalar1=sv[:, 6:7]), ["Cs", "sv"], ["Cs"])
            dve(lambda e: e.scalar_tensor_tensor(out=Cs[:], in0=big[:], scalar=sv[:, 5:6], in1=Cs[:], op0=ALU.mult, op1=ALU.add), ["big", "sv", "Cs"], ["Cs"])
            dma("pool", osC, Cs[:], ["Cs"], [], "fin")
            dve(lambda e: e.tensor_scalar_mul(out=sn_t[:], in0=sn_t[:], scalar1=sv[:, 6:7]), ["sn_t", "sv"], ["sn_t"])
            dve(lambda e: e.scalar_tensor_tensor(out=sn_t[:], in0=k_, scalar=sv[:, 5:6], in1=sn_t[:], op0=ALU.mult, op1=ALU.add), ["qk_s", "sv", "sn_t"], ["sn_t"])
            dma("pool", osn, sn_t[:], ["sn_t"], [], "fin")
            pool(lambda e: e.tensor_tensor(out=b3(big), in0=Cs[:, :].rearrange("p (k v) -> p v k", k=64), in1=q_.unsqueeze(1).to_broadcast([128, 64, 64]), op=ALU.mult), ["Cs", "qk_s"], ["big"])
            hs = TT("hs", [128, 2, 64])
            dve(lambda e: e.tensor_reduce(out=hs[:, 0, :], in_=b3(big), axis=AX.X, op=ALU.add), ["big"], ["hs"])
            dve(lambda e: e.tensor_tensor(out=sv[:, 16:80 - 16] if False else big[:, 0:64], in0=q_, in1=sn_t[:], op=ALU.mult), ["qk_s", "sn_t"], ["big"])
            dve(lambda e: e.tensor_reduce(out=sv[:, 8:9], in_=big[:, 0:64], axis=AX.X, op=ALU.add), ["big"], ["sv"])
            dve(lambda e: e.scalar_tensor_tensor(out=sv[:, 9:10], in0=sv[:, 8:9], scalar=-1.0, in1=sv[:, 8:9], op0=ALU.mult, op1=ALU.max), ["sv"], ["sv"])
            dve(lambda e: e.tensor_tensor(out=sv[:, 9:10], in0=sv[:, 9:10], in1=sv[:, 7:8], op=ALU.max), ["sv"], ["sv"])
            dve(lambda e: e.reciprocal(out=sv[:, 10:11], in_=sv[:, 9:10]), ["sv"], ["sv"])
            dve(lambda e: e.tensor_scalar_mul(out=hs[:, 0, :], in0=hs[:, 0, :], scalar1=sv[:, 10:11]), ["hs", "sv"], ["hs"])
            dma("pool", osshift, A7[:, 4:7, :], ["A7"], [], "fin")
            rk3 = TT("rk3", [128, 3, 64])
            mu3 = spk[:, SOFF["mu_r"][0]:SOFF["mu_v"][1]].rearrange("p (a d) -> p a d", a=3)
            dve(lambda e: e.tensor_tensor(out=rk3[:], in0=ssh[:], in1=A7[:, 4:7, :], op=ALU.subtract), ["ssh", "A7"], ["rk3"])
            dve(lambda e: e.tensor_tensor(out=rk3[:], in0=rk3[:], in1=mu3, op=ALU.mult), ["rk3", "spk"], ["rk3"])
            dve(lambda e: e.tensor_tensor(out=rk3[:], in0=rk3[:], in1=A7[:, 4:7, :], op=ALU.add), ["rk3", "A7"], ["rk3"])
            w8 = TT("w8", [128, 8, 64])
            dve(lambda e: e.tensor_tensor(out=w8[:, 0, :], in0=L3[:, 0, :], in1=SP_("w0"), op=ALU.add), ["L3", "spk"], ["w8"])
            act(lambda e: e.activation(out=w8[:, 0, :], in_=w8[:, 0, :], func=AF.Sigmoid), ["w8"], ["w8"])
            act(lambda e: e.activation(out=w8[:, 0, :], in_=w8[:, 0, :], func=AF.Exp, scale=-C0), ["w8"], ["w8"])
            dve(lambda e: e.tensor_tensor(out=w8[:, 1, :], in0=L3[:, 1, :], in1=SP_("a0"), op=ALU.add), ["L3", "spk"], ["w8"])
            act(lambda e: e.activation(out=w8[:, 1, :], in_=w8[:, 1, :], func=AF.Sigmoid), ["w8"], ["w8"])
            dve(lambda e: e.tensor_tensor(out=w8[:, 6, :], in0=rk3[:, 1, :], in1=SP_("kk"), op=ALU.mult), ["rk3", "spk"], ["w8"])
            dve(lambda e: e.tensor_tensor(out=w8[:, 7, :], in0=w8[:, 6, :], in1=w8[:, 6, :], op=ALU.mult), ["w8"], ["w8"])
            dve(lambda e: e.tensor_reduce(out=sv[:, 11:12], in_=w8[:, 7, :], axis=AX.X, op=ALU.add), ["w8"], ["sv"])
            dve(lambda e: e.tensor_scalar_max(out=sv[:, 11:12], in0=sv[:, 11:12], scalar1=1e-24), ["sv"], ["sv"])
            act(lambda e: e.activation(out=sv[:, 11:12], in_=sv[:, 11:12], func=AF.Sqrt), ["sv"], ["sv"])
            dve(lambda e: e.reciprocal(out=sv[:, 11:12], in_=sv[:, 11:12]), ["sv"], ["sv"])
            dve(lambda e: e.tensor_scalar_mul(out=w8[:, 3, :], in0=w8[:, 6, :], scalar1=sv[:, 11:12]), ["w8", "sv"], ["w8"])
            dve(lambda e: e.tensor_tensor(out=w8[:, 6, :], in0=w8[:, 1, :], in1=SP_("ka"), op=ALU.mult), ["w8", "spk"], ["w8"])
            dve(lambda e: e.tensor_tensor(out=w8[:, 6, :], in0=w8[:, 6, :], in1=SP_("ka"), op=ALU.subtract), ["w8", "spk"], ["w8"])
            dve(lambda e: e.tensor_scalar_add(out=w8[:, 6, :], in0=w8[:, 6, :], scalar1=1.0), ["w8"], ["w8"])
            dve(lambda e: e.tensor_tensor(out=w8[:, 4, :], in0=rk3[:, 1, :], in1=w8[:, 6, :], op=ALU.mult), ["rk3", "w8"], ["w8"])
            dve(lambda e: e.tensor_tensor(out=w8[:, 5, :], in0=w8[:, 1, :], in1=w8[:, 3, :], op=ALU.mult), ["w8"], ["w8"])
            dma("sp", Ss[:], sS_d, [], ["Ss"], "sin2")
            bk = lambda ap: ap.unsqueeze(1).to_broadcast([128, 64, 64])
            bv = lambda ap: ap.unsqueeze(2).to_broadcast([128, 64, 64])
            pool(lambda e: e.tensor_tensor(out=b3(big), in0=b3(Ss), in1=bk(w8[:, 3, :]), op=ALU.mult), ["Ss", "w8"], ["big"])
            dve(lambda e: e.tensor_reduce(out=w8[:, 7, :], in_=b3(big), axis=AX.X, op=ALU.add), ["big"], ["w8"])
            dve(lambda e: e.tensor_tensor(out=b3(Ss), in0=b3(Ss), in1=bk(w8[:, 0, :]), op=ALU.mult), ["Ss", "w8"], ["Ss"])
            pool(lambda e: e.tensor_tensor(out=b3(big), in0=bv(w8[:, 7, :]), in1=bk(w8[:, 5, :]), op=ALU.mult), ["w8"], ["big"])
            dve(lambda e: e.tensor_tensor(out=Ss[:], in0=Ss[:], in1=big[:], op=ALU.subtract), ["Ss", "big"], ["Ss"])
            pool(lambda e: e.tensor_tensor(out=b3(big), in0=bv(rk3[:, 2, :]), in1=bk(w8[:, 4, :]), op=ALU.mult), ["rk3", "w8"], ["big"])
            dve(lambda e: e.tensor_tensor(out=Ss[:], in0=Ss[:], in1=big[:], op=ALU.add), ["Ss", "big"], ["Ss"])
            dma("pool", osS, Ss[:], ["Ss"], [], "fin")
            pool(lambda e: e.tensor_tensor(out=b3(big), in0=b3(Ss), in1=bk(rk3[:, 0, :]), op=ALU.mult), ["Ss", "rk3"], ["big"])
            dve(lambda e: e.tensor_reduce(out=hs[:, 1, :], in_=b3(big), axis=AX.X, op=ALU.add), ["big"], ["hs"])
            dve(lambda e: e.tensor_tensor(out=w8[:, 6, :], in0=rk3[:, 0, :], in1=w8[:, 4, :], op=ALU.mult), ["rk3", "w8"], ["w8"])
            dve(lambda e: e.tensor_tensor(out=w8[:, 6, :], in0=w8[:, 6, :], in1=SP_("rk"), op=ALU.mult), ["w8", "spk"], ["w8"])
            dve(lambda e: e.tensor_reduce(out=sv[:, 12:13], in_=w8[:, 6, :], axis=AX.X, op=ALU.add), ["w8"], ["sv"])
            dve(lambda e: e.scalar_tensor_tensor(out=hs[:, 1, :], in0=rk3[:, 2, :], scalar=sv[:, 12:13], in1=hs[:, 1, :], op0=ALU.mult, op1=ALU.add), ["rk3", "sv", "hs"], ["hs"])
            act(lambda e: e.activation(out=w8[:, 6, :], in_=A7[:, 3, :], func=AF.Sigmoid), ["A7"], ["w8"])
            dve(lambda e: e.tensor_tensor(out=hs[:, 0, :], in0=hs[:, 0, :], in1=w8[:, 6, :], op=ALU.mult), ["hs", "w8"], ["hs"])
            dve(lambda e: e.tensor_tensor(out=w8[:, 7, :], in0=hs[:, 0, :], in1=hs[:, 0, :], op=ALU.mult), ["hs"], ["w8"])
            dve(lambda e: e.tensor_reduce(out=sv[:, 13:14], in_=w8[:, 7, :], axis=AX.X, op=ALU.add), ["w8"], ["sv"])
            act(lambda e: e.activation(out=sv[:, 13:14], in_=sv[:, 13:14], func=AF.Sqrt, bias=EPS, scale=1.0 / 64), ["sv"], ["sv"])
            dve(lambda e: e.reciprocal(out=sv[:, 13:14], in_=sv[:, 13:14]), ["sv"], ["sv"])
            dve(lambda e: e.scalar_tensor_tensor(out=hs[:, 0, :], in0=hs[:, 0, :], scalar=sv[:, 13:14], in1=SP_("mnw"), op0=ALU.mult, op1=ALU.mult), ["hs", "sv", "spk"], ["hs"])
            dve(lambda e: e.tensor_reduce(out=sv[:, 14:15], in_=hs[:, 1, :], axis=AX.X, op=ALU.add), ["hs"], ["sv"])
            dve(lambda e: e.tensor_scalar_mul(out=sv[:, 14:15], in0=sv[:, 14:15], scalar1=1.0 / 64), ["sv"], ["sv"])
            dve(lambda e: e.tensor_scalar_sub(out=hs[:, 1, :], in0=hs[:, 1, :], scalar1=sv[:, 14:15]), ["hs", "sv"], ["hs"])
            dve(lambda e: e.tensor_tensor(out=w8[:, 7, :], in0=hs[:, 1, :], in1=hs[:, 1, :], op=ALU.mult), ["hs"], ["w8"])
            dve(lambda e: e.tensor_reduce(out=sv[:, 15:16], in_=w8[:, 7, :], axis=AX.X, op=ALU.add), ["w8"], ["sv"])
            act(lambda e: e.activation(out=sv[:, 15:16], in_=sv[:, 15:16], func=AF.Sqrt, bias=GN_EPS, scale=1.0 / 64), ["sv"], ["sv"])
            dve(lambda e: e.reciprocal(out=sv[:, 15:16], in_=sv[:, 15:16]), ["sv"], ["sv"])
            dve(lambda e: e.scalar_tensor_tensor(out=hs[:, 1, :], in0=hs[:, 1, :], scalar=sv[:, 15:16], in1=SP_("lnw"), op0=ALU.mult, op1=ALU.mult), ["hs", "sv", "spk"], ["hs"])
            dve(lambda e: e.tensor_tensor(out=hs[:, 1, :], in0=hs[:, 1, :], in1=SP_("lnb"), op=ALU.add), ["hs", "spk"], ["hs"])
            dve(lambda e: e.tensor_tensor(out=hs[:, 1, :], in0=hs[:, 1, :], in1=L3[:, 2, :], op=ALU.mult), ["hs", "L3"], ["hs"])
            s3v = nc.dram_tensor("scr3b", [128, 2, 64], F32, kind="Internal").ap()
            dma("pool", s3v, hs[:], ["hs"], ["scr3b"], "scrw3")
            smix = spj[:, 2304:3328].rearrange("p (a h d) -> p a h d", a=2, h=8)
            for a_ in range(2):
                dma("sp", smix[:, a_, :, :], s3v.rearrange("(b h) a d -> b a h d", b=NS)[:, a_, :, :], ["scr3b"], ["smix"], "scrr3")
            act(lambda e: e.copy(out=mix[:NS, :], in_=smix[:].rearrange("p a h d -> p (a h d)")), ["smix"], ["mix"])
            out_proj(NS, mix, "mix", sx, "sx", T, W)
        PP[0].finalize()
        PP[0] = Prog(ctx)
    es_res.close()

    with contextlib.ExitStack() as es2:
        cur[0] = es2
        upb = TT("upb", [128, 8, DFF], BF16)
        dnb = TT("dnb", [128, 32, D], BF16)
        pB2 = TT("pB2", [128, 136], F32)
        nfw = TT("nfw", [128, D])
        identb2 = TT("identb2", [128, 128], BF16)
        wout = TT("wout", [128, 8, D], BF16)
        mixin = TT("mixin", [128, D], BF16)
        for c in range(0, 8, 4):
            dma("pool", wout[:, c:c + 4, :], w_out_v[:, c:c + 4, :], [], ["wout"], "wout")
        up_v = mlp_up.rearrange("(c p) n -> p c n", p=128)
        dn_v = mlp_down.rearrange("(c p) n -> p c n", p=128)
        dma("sp", pB2[:, 0:128], packA_d[:, OFF["ident"][0]:OFF["ident"][1]], [], ["pB2"], "init2", True)
        dma("sp", pB2[:, 128:136], packA_d[:, OFF["nmlp"][0]:OFF["nmlp"][1]], [], ["pB2"], "init2", True)
        dma("sp", nfw[:], nfw_d, [], ["nfw"], "init2", True)
        for g8 in range(8):
            dma("pool", upb[:, :, g8 * 512:(g8 + 1) * 512], up_v[:, :, g8 * 512:(g8 + 1) * 512], [], ["upb%d" % g8], "up%d" % g8)
        for g8 in range(8):
            dma("pool", dnb[:, g8 * 4:(g8 + 1) * 4, :], dn_v[:, g8 * 4:(g8 + 1) * 4, :], [], ["dnb%d" % g8], "dn%d" % g8)
        dve(lambda e: e.tensor_copy(out=identb2[:], in_=pB2[:, 0:128]), ["pB2"], ["identb2"])
        NSUB = 2
        NTT = NSUB * 128
        xsb2 = TT("xsb2", [128, D], BF16)
        xb = [TT("xb%d" % i, [128, NSUB, D]) for i in range(2)]
        st2 = TT("st2", [128, 8])
        xn2 = [TT("xn2T%d" % i, [128, 8, NTT], BF16) for i in range(2)]
        hT = TT("hT", [128, 32, NTT], BF16)
        junk2 = TT("junk2", [128, D], BF16)
        rl = [TT("rl%d" % i, [128, 512]) for i in range(2)]
        nmlp = pB2[:, 128:136]
        xm_v = xp.rearrange("(s p) d -> p s d", p=128)
        yp_v = yp.rearrange("(s p) d -> p s d", p=128)
        sbs = [(sb * NSUB, NSUB, 128) for sb in range(NB // NSUB)] + [(NB, 1, NS)]

        def front(i):
            s0, nsub, nt = sbs[i]
            x4 = xb[i % 2]
            xk = "xb%d" % (i % 2)
            xn2T = xn2[i % 2]
            xnk = "xn2T%d" % (i % 2)
            if nsub == NSUB:
                dma("sp", x4[:], xm_v[:, s0:s0 + NSUB, :], [], [xk], xk)
            else:
                dma("sp", x4[:nt, 0, :], xs, [], [xk], xk)
            for si in range(nsub):
                r0_ = (s0 + si) * 128 if nsub == NSUB else T
                dma("sp", mixin[:nt, :], mix_d[r0_:r0_ + nt, :], [], ["mixin"], "mixin")
                ps, psb, pk = nps()
                for c in range(8):
                    pe(lambda e, c=c, psb=psb: e.transpose(psb[:, c * 128:c * 128 + nt], mixin[:nt, c * 128:(c + 1) * 128], identb2[:nt, :nt]), ["mixin", "identb2"], [pk])
                act(lambda e, psb=psb, si=si: e.copy(out=xn2T[:, :, si * 128:si * 128 + nt], in_=psb[:, 0:1024].rearrange("p (c t) -> p c t", c=8)[:, :, 0:nt]), [pk], [xnk])
                yield
                ps, psb, pk = nps()
                for n in range(2):
                    for c in range(8):
                        mm(ps[:nt, n * 512:(n + 1) * 512], xn2T[:, c, si * 128:si * 128 + nt], wout[:, c, n * 512:(n + 1) * 512], c == 0, c == 7, [xnk, "wout"], [pk])
                for a_ in range(2):
                    dve(lambda e, ps=ps, si=si, a_=a_: e.tensor_tensor(out=x4[:nt, si, a_ * 512:(a_ + 1) * 512], in0=ps[:nt, a_ * 512:(a_ + 1) * 512], in1=x4[:nt, si, a_ * 512:(a_ + 1) * 512], op=ALU.add), [pk, xk], [xk])
                act(lambda e, si=si: e.activation(out=junk2[:nt, :], in_=x4[:nt, si, :], func=AF.Square, accum_out=st2[:nt, 0:1]), [xk], ["junk2", "st2"])
                act(lambda e: e.activation(out=st2[:nt, 1:2], in_=st2[:nt, 0:1], func=AF.Sqrt, bias=EPS, scale=1.0 / D), ["st2"], ["st2"])
                dve(lambda e: e.reciprocal(out=st2[:nt, 2:3], in_=st2[:nt, 1:2]), ["st2"], ["st2"])
                dve(lambda e, si=si: e.tensor_scalar_mul(out=xsb2[:nt, :], in0=x4[:nt, si, :], scalar1=st2[:nt, 2:3]), [xk, "st2"], ["xsb2"])
                yield
                yield
                ps, psb, pk = nps()
                for c in range(8):
                    pe(lambda e, c=c, psb=psb: e.transpose(psb[:, c * 128:c * 128 + nt], xsb2[:nt, c * 128:(c + 1) * 128], identb2[:nt, :nt]), ["xsb2", "identb2"], [pk])
                dve(lambda e, psb=psb, si=si: e.tensor_tensor(out=xn2T[:, :, si * 128:si * 128 + nt], in0=psb[:, 0:1024].rearrange("p (c t) -> p c t", c=8)[:, :, 0:nt],
                                                             in1=nmlp.unsqueeze(2).to_broadcast([128, 8, nt]), op=ALU.mult), [pk, "pB2"], [xnk])
                yield

        def up(i):
            s0, nsub, nt = sbs[i]
            ntt = nsub * nt if nsub == NSUB else nt
            xn2T = xn2[i % 2]
            xnk = "xn2T%d" % (i % 2)
            per = 512 // NTT
            for j2 in range(32 // (2 * per)):
                ps, psb, pk = nps()
                for jj in range(2 * per):
                    j = j2 * 2 * per + jj
                    for c in range(8):
                        mm(ps[:, jj * NTT:jj * NTT + ntt], upb[:, c, j * 128:(j + 1) * 128], xn2T[:, c, 0:ntt], c == 0, c == 7, ["upb%d" % (j // 4), xnk], [pk])
                for bk in range(2):
                    r_ = rl[bk]
                    rk_ = "rl%d" % bk
                    j0 = j2 * 2 * per + bk * per
                    psv = ps[:, bk * 512:(bk + 1) * 512].rearrange("p (j t) -> p j t", j=per)[:, :, 0:ntt]
                    rv = r_[:, :].rearrange("p (j t) -> p j t", j=per)[:, :, 0:ntt]
                    act(lambda e, psv=psv, rv=rv: e.activation(out=rv, in_=psv, func=AF.Relu), [pk], [rk_])
                    if bk == 0:
                        dve(lambda e, rv=rv, j0=j0: e.tensor_tensor(out=hT[:, j0:j0 + per, 0:ntt], in0=rv, in1=rv, op=ALU.mult), [rk_], ["hT"])
                    else:
                        pool(lambda e, rv=rv, j0=j0: e.tensor_tensor(out=hT[:, j0:j0 + per, 0:ntt], in0=rv, in1=rv, op=ALU.mult), [rk_], ["hT"])
                yield

        def down(i):
            s0, nsub, nt = sbs[i]
            x4 = xb[i % 2]
            xk = "xb%d" % (i % 2)
            for si in range(nsub):
                ps, psb, pk = nps()
                for n in range(2):
                    for j in range(32):
                        mm(ps[:nt, n * 512:(n + 1) * 512], hT[:, j, si * 128:si * 128 + nt], dnb[:, j, n * 512:(n + 1) * 512], j == 0, j == 31, ["hT", "dnb%d" % (j // 4)], [pk])
                for a_ in range(2):
                    dve(lambda e, ps=ps, si=si, a_=a_: e.tensor_tensor(out=x4[:nt, si, a_ * 512:(a_ + 1) * 512], in0=ps[:nt, a_ * 512:(a_ + 1) * 512], in1=x4[:nt, si, a_ * 512:(a_ + 1) * 512], op=ALU.add), [pk, xk], [xk])
                act(lambda e, si=si: e.activation(out=junk2[:nt, :], in_=x4[:nt, si, :], func=AF.Square, accum_out=st2[:nt, 4:5]), [xk], ["junk2", "st2"])
                act(lambda e: e.activation(out=st2[:nt, 5:6], in_=st2[:nt, 4:5], func=AF.Sqrt, bias=EPS, scale=1.0 / D), ["st2"], ["st2"])
                dve(lambda e: e.reciprocal(out=st2[:nt, 6:7], in_=st2[:nt, 5:6]), ["st2"], ["st2"])
                dve(lambda e, si=si: e.scalar_tensor_tensor(out=x4[:nt, si, :], in0=x4[:nt, si, :], scalar=st2[:nt, 6:7], in1=nfw[:nt, :], op0=ALU.mult, op1=ALU.mult), [xk, "st2", "nfw"], [xk])
            if nsub == NSUB:
                dma("pool", yp_v[:, s0:s0 + NSUB, :], x4[:], [xk], [], "yo%d" % (i % 2))
            else:
                dma("pool", ys, x4[:nt, 0, :], [xk], [], "yo%d" % (i % 2))

        def run2(gl):
            gl = list(gl)
            while gl:
                for g_ in list(gl):
                    try:
                        next(g_)
                    except StopIteration:
                        gl.remove(g_)

        run2([front(0)])
        for i in range(len(sbs)):
            gl = [up(i)]
            if i + 1 < len(sbs):
                gl.append(front(i + 1))
            run2(gl)
            down(i)
        PP[0].finalize()
    es_ps.close()
    ctx.close()
    return nc


_CACHE = {}


def _host_packs(inp, core):
    f = np.float32
    L = 0
    pa = np.zeros((128, NA), f)

    def put(n, arr):
        a, b = OFF[n]
        pa[:, a:b] = arr

    rep = lambda v: np.broadcast_to(np.asarray(v, f).reshape(1, -1), (128, np.asarray(v).size))
    put("mnw", rep(inp["mlstm_norm_w"][L]))
    put("w0", rep(inp["rw_w0"][L]))
    put("a0", rep(inp["rw_a0"][L]))
    put("kk", rep(inp["rw_k_k"][L]))
    put("ka", rep(inp["rw_k_a"][L]))
    put("rk", rep(inp["rw_r_k"][L].reshape(-1)))
    put("lnw", rep(inp["rw_ln_w"][L]))
    put("lnb", rep(inp["rw_ln_b"][L]))
    put("ifb", rep(np.concatenate([inp["mlstm_i_b"][L], inp["mlstm_f_b"][L]])))
    put("nmw", inp["norm_mix_w"][L].reshape(8, 128).T)
    put("nmlp", inp["norm_mlp_w"][L].reshape(8, 128).T)
    cw = inp["mlstm_conv_w"][L]
    put("cw", cw.reshape(4, 8, 128).transpose(2, 1, 0).reshape(128, 32))
    put("cb", inp["mlstm_conv_b"][L].reshape(8, 128).T)
    put("ident", np.eye(128, dtype=f))
    put("mui", np.triu(np.ones((128, 128), f), 0))
    put("mus", np.triu(np.ones((128, 128), f), 1))
    put("mls", np.tril(np.ones((128, 128), f), -1))
    put("ones", np.ones((128, 128), f))
    lsel = np.zeros((128, 128), f)
    rsel = np.zeros((128, 4), f)
    for h in range(8):
        lsel[h, (h % 2) * 64:(h % 2) * 64 + 64] = 1.0
        rsel[h, h // 2] = 1.0
    put("lsel", lsel)
    put("rsel", rsel)
    return pa


def _sample_pack(inp):
    f = np.float32
    L = 0
    sp = np.zeros((128, NSP), f)

    def bh(v512):
        return np.tile(np.asarray(v512, f).reshape(8, 64), (NS, 1))

    def put(n, arr):
        a, b = SOFF[n]
        sp[:, a:b] = arr

    mu = inp["rw_mu"][L]
    put("mu_r", bh(mu[0:512]))
    put("mu_k", bh(mu[512:1024]))
    put("mu_v", bh(mu[1024:1536]))
    cw = inp["mlstm_conv_w"][L]
    put("cwq", np.concatenate([bh(cw[j, 0:512]) for j in range(4)], axis=1))
    put("cwk", np.concatenate([bh(cw[j, 512:1024]) for j in range(4)], axis=1))
    cb = inp["mlstm_conv_b"][L]
    put("cbq", bh(cb[0:512]))
    put("cbk", bh(cb[512:1024]))
    put("mnw", bh(inp["mlstm_norm_w"][L]))
    put("w0", bh(inp["rw_w0"][L]))
    put("a0", bh(inp["rw_a0"][L]))
    put("kk", bh(inp["rw_k_k"][L]))
    put("ka", bh(inp["rw_k_a"][L]))
    put("rk", bh(inp["rw_r_k"][L].reshape(-1)))
    put("lnw", bh(inp["rw_ln_w"][L]))
    put("lnb", bh(inp["rw_ln_b"][L]))
    put("ib", np.tile(inp["mlstm_i_b"][L].reshape(8, 1), (NS, 1)))
    put("fb", np.tile(inp["mlstm_f_b"][L].reshape(8, 1), (NS, 1)))
    return sp


def kernel(**inp):
    f = np.float32
    inp = {k: np.asarray(v) for k, v in inp.items()}
    if "nc" not in _CACHE:
        _CACHE["nc"] = build_program()
    nc = _CACHE["nc"]
    L = 0
    pa = _host_packs(inp, 0)
    sp = _sample_pack(inp)
    mu = inp["rw_mu"][L]
    luw = np.concatenate([inp["rw_w_up"][L], inp["rw_a_up"][L]], axis=0).astype(f)
    common = {
        "w_in": np.ascontiguousarray(inp["w_in"][L], f),
        "w_out": np.ascontiguousarray(inp["w_out"][L], f),
        "mlp_up": np.ascontiguousarray(inp["mlp_up"][L], f),
        "mlp_down": np.ascontiguousarray(inp["mlp_down"][L], f),
        "packA": pa,
        "mu_b": np.ascontiguousarray(np.broadcast_to(mu.reshape(1, -1), (128, RWW)), f),
        "nfw_b": np.ascontiguousarray(np.broadcast_to(inp["norm_f_w"].reshape(1, -1), (128, D)), f),
        "luw": luw,
        "gup": np.ascontiguousarray(inp["rw_g_up"][L], f),
        "spack": sp,
        "mul": np.ascontiguousarray(np.broadcast_to(mu[1536:1792].reshape(1, -1), (NS, 256)), f),
    }
    in_maps = []
    for c in range(8):
        rs = slice(c * NS, (c + 1) * NS)
        m = dict(common)
        m["xp"] = np.ascontiguousarray(inp["x_prompt"][c], f)
        m["xs"] = np.ascontiguousarray(inp["x_sample"][rs, 0, :], f)
        m["sC"] = np.ascontiguousarray(inp["state_mlstm_C"][L, rs].reshape(128, 4096), f)
        m["sn"] = np.ascontiguousarray(inp["state_mlstm_n"][L, rs].reshape(128, 64), f)
        m["sm"] = np.ascontiguousarray(inp["state_mlstm_m"][L, rs].reshape(128, 1), f)
        cv = inp["state_mlstm_conv"][L, rs]
        m["sconv"] = np.ascontiguousarray(cv.reshape(NS, 3, 2, 8, 64).transpose(0, 3, 2, 1, 4).reshape(128, 2, 3, 64), f)
        m["sS"] = np.ascontiguousarray(inp["state_rwkv_S"][L, rs].reshape(128, 4096), f)
        sh = inp["state_rwkv_shift"][L, rs, 0, :]
        m["sshift"] = np.ascontiguousarray(sh[:, 0:1536].reshape(NS, 3, 8, 64).transpose(0, 2, 1, 3).reshape(128, 3, 64), f)
        m["sshl"] = np.ascontiguousarray(sh[:, 1536:1792], f)
        in_maps.append(m)
    res = run_bass_kernel_spmd(nc, in_maps, core_ids=list(range(8)))
    R = res.results
    y_prompt = np.stack([R[c]["yp"] for c in range(8)]).astype(f)
    y_sample = np.concatenate([R[c]["ys"] for c in range(8)], axis=0).reshape(128, 1, D).astype(f)
    pC = np.zeros((1, 8, 8, 64, 64), f)
    pn = np.zeros((1, 8, 8, 64), f)
    pm = np.zeros((1, 8, 8), f)
    pconv = np.zeros((1, 8, 3, 1024), f)
    pS = np.zeros((1, 8, 8, 64, 64), f)
    pshift = np.zeros((1, 8, 1, RWW), f)
    for c in range(8):
        oC = R[c]["oC"].reshape(2, 64, 4, 65)
        Ch = oC.transpose(2, 0, 1, 3).reshape(8, 64, 65)
        pC[0, c] = Ch[:, :, 0:64]
        pn[0, c] = Ch[:, :, 64]
        pm[0, c] = R[c]["om"].reshape(8)
        pconv[0, c] = R[c]["oconv"].transpose(2, 1, 0).reshape(3, 1024)
        oS = R[c]["oS"].reshape(2, 64, 4, 64)
        pS[0, c] = oS.transpose(2, 0, 3, 1).reshape(8, 64, 64)
        pshift[0, c, 0] = R[c]["oshift"].reshape(RWW)
    sC = np.concatenate([R[c]["osC"].reshape(NS, 8, 64, 64) for c in range(8)])[None].astype(f)
    sn = np.concatenate([R[c]["osn"].reshape(NS, 8, 64) for c in range(8)])[None].astype(f)
    sm = np.concatenate([R[c]["osm"].reshape(NS, 8) for c in range(8)])[None].astype(f)
    sconv = np.concatenate([R[c]["osconv"].reshape(NS, 8, 2, 3, 64).transpose(0, 3, 2, 1, 4).reshape(NS, 3, 1024) for c in range(8)])[None].astype(f)
    sS = np.concatenate([R[c]["osS"].reshape(NS, 8, 64, 64) for c in range(8)])[None].astype(f)
    sshift = np.concatenate([
        np.concatenate([R[c]["osshift"].reshape(NS, 8, 3, 64).transpose(0, 2, 1, 3).reshape(NS, 1536), R[c]["osshl"]], axis=1)
        for c in range(8)]).reshape(1, 128, 1, RWW).astype(f)
    return (y_prompt, y_sample, pC, pn, pm, pconv, pS, pshift, sC, sn, sm, sconv, sS, sshift)
```
